# Optimizing a Trainium2 kernel written in Bass

```python
import math
import jax, jax.numpy as jnp
from jax import lax
import numpy as np

D_MODEL = 1024
BATCH = 8
SEQ = 2048
DEPTH = 1

GDN_HEADS = 4
GDN_HEAD_DIM = 128
GDN_WIDTH = GDN_HEADS * GDN_HEAD_DIM
GDN_CONV = 4
GDN_CHUNK = 64
DIL_HEADS = 8
DIL_HEAD_DIM = 64
DIL_WIDTH = DIL_HEADS * DIL_HEAD_DIM
DIL_PATTERNS = ((128, 1), (512, 4), (2048, 16))
BAND_BLOCK = 128
MIX_WIDTH = GDN_WIDTH + DIL_WIDTH
IN_COLS = 3 * GDN_WIDTH + GDN_WIDTH + 2 * GDN_HEADS + 3 * DIL_WIDTH
D_FF = 2816
FFN_CONV = 3
EPS = 1e-6

kernel_name = "hymba_gdn_dilated_convffn"


def rmsnorm(x, w):
    xf = x.astype(jnp.float32)
    y = xf * lax.rsqrt(jnp.mean(xf * xf, axis=-1, keepdims=True) + EPS)
    return (y * w.astype(jnp.float32)).astype(x.dtype)


def l2norm(x):
    return x * lax.rsqrt(jnp.sum(x * x, axis=-1, keepdims=True) + EPS)


def causal_dwconv(x, w):
    K = w.shape[0]
    S = x.shape[1]
    xp = jnp.pad(x, ((0, 0), (K - 1, 0), (0, 0)))
    out = xp[:, 0:S, :] * w[0]
    for i in range(1, K):
        out = out + xp[:, i:i + S, :] * w[i]
    return out


def gated_delta_rule(q, k, v, g, beta):
    B, S, H, Dk = q.shape
    Dv = v.shape[-1]
    C = GDN_CHUNK
    nc = S // C

    def chunk_vec(t):
        return t.reshape(B, nc, C, H, t.shape[-1]).transpose(1, 0, 3, 2, 4)

    def chunk_sc(t):
        return t.reshape(B, nc, C, H).transpose(1, 0, 3, 2)

    qc, kc, vc = chunk_vec(q), chunk_vec(k), chunk_vec(v)
    bc = chunk_sc(beta)
    gc = jnp.cumsum(chunk_sc(g), axis=-1)
    idx = jnp.arange(C)
    incl = idx[:, None] >= idx[None, :]
    strict = idx[:, None] > idx[None, :]
    diff = gc[..., :, None] - gc[..., None, :]
    dec_incl = jnp.where(incl, jnp.exp(jnp.where(incl, diff, 0.0)), 0.0)
    dec_strict = jnp.where(strict, dec_incl, 0.0)
    kk = jnp.einsum('nbhtd,nbhjd->nbhtj', kc, kc)
    lmat = dec_strict * kk * bc[..., None, :]
    gam = jnp.exp(gc)[..., None]
    rhs = jnp.concatenate([vc, gam * kc], axis=-1)
    sol = lax.linalg.triangular_solve(lmat, rhs, left_side=True, lower=True, unit_diagonal=True)
    u_v, w_k = sol[..., :Dv], sol[..., Dv:]
    attn = dec_incl * jnp.einsum('nbhtd,nbhjd->nbhtj', qc, kc) * bc[..., None, :]
    q_dec = gam * qc
    k_end = kc * (jnp.exp(gc[..., -1:] - gc) * bc)[..., None]
    g_end = jnp.exp(gc[..., -1])

    def step(state, xs):
        uv, wk, qd, at, ke, ge = xs
        u = uv - jnp.einsum('bhck,bhkv->bhcv', wk, state)
        o = jnp.einsum('bhck,bhkv->bhcv', qd, state) + jnp.einsum('bhcj,bhjv->bhcv', at, u)
        state = ge[..., None, None] * state + jnp.einsum('bhck,bhcv->bhkv', ke, u)
        return state, o

    s0 = jnp.zeros((B, H, Dk, Dv), jnp.float32)
    _, o = lax.scan(step, s0, (u_v, w_k, q_dec, attn, k_end, g_end))
    return o.transpose(1, 0, 3, 2, 4).reshape(B, S, H, Dv)


def band_attention(q, k, v, n_back):
    L, Dh = q.shape[-2], q.shape[-1]
    nb = -(-L // BAND_BLOCK)
    pad = nb * BAND_BLOCK - L
    padcfg = [(0, 0)] * (q.ndim - 2) + [(0, pad), (0, 0)]
    lead = q.shape[:-2]

    def blocks(t):
        return jnp.pad(t, padcfg).reshape(*lead, nb, BAND_BLOCK, Dh)

    qb, kb, vb = blocks(q), blocks(k), blocks(v)

    def with_prev(t):
        prev = jnp.concatenate([jnp.zeros_like(t[..., :1, :, :]), t[..., :-1, :, :]], axis=-3)
        return jnp.concatenate([prev, t], axis=-2)

    kk, vv = with_prev(kb), with_prev(vb)
    s = jnp.einsum('...nqd,...nkd->...nqk', qb, kk)
    blk = jnp.arange(nb)[:, None, None] * BAND_BLOCK
    qpos = blk + jnp.arange(BAND_BLOCK)[None, :, None]
    kpos = blk - BAND_BLOCK + jnp.arange(2 * BAND_BLOCK)[None, None, :]
    dist = qpos - kpos
    valid = (dist >= 0) & (dist <= n_back) & (kpos >= 0)
    s = jnp.where(valid, s, -jnp.inf)
    m = jnp.max(s, axis=-1)
    p = jnp.exp(s - m[..., None])
    den = jnp.sum(p, axis=-1)
    num = jnp.einsum('...nqk,...nkd->...nqd', p, vv)
    num = num.reshape(*lead, nb * BAND_BLOCK, Dh)[..., :L, :]
    den = den.reshape(*lead, nb * BAND_BLOCK)[..., :L]
    m = m.reshape(*lead, nb * BAND_BLOCK)[..., :L]
    return num, den, m


def dilated_attention(q, k, v):
    B, S, H, Dh = q.shape
    qf = q.astype(jnp.float32).transpose(0, 2, 1, 3) * (Dh ** -0.5)
    kf = k.astype(jnp.float32).transpose(0, 2, 1, 3)
    vf = v.astype(jnp.float32).transpose(0, 2, 1, 3)
    nums, dens, ms = [], [], []
    for window, dil in DIL_PATTERNS:
        L = S // dil

        def stride(t):
            return t.reshape(B, H, L, dil, Dh).transpose(0, 1, 3, 2, 4)

        num, den, m = band_attention(stride(qf), stride(kf), stride(vf), window // dil)
        nums.append(num.transpose(0, 1, 3, 2, 4).reshape(B, H, S, Dh))
        dens.append(den.transpose(0, 1, 3, 2).reshape(B, H, S))
        ms.append(m.transpose(0, 1, 3, 2).reshape(B, H, S))
    m_all = jnp.maximum(jnp.maximum(ms[0], ms[1]), ms[2])
    wts = [jnp.exp(mi - m_all) for mi in ms]
    num_tot = wts[0][..., None] * nums[0] + wts[1][..., None] * nums[1] + wts[2][..., None] * nums[2]
    den_tot = wts[0] * dens[0] + wts[1] * dens[1] + wts[2] * dens[2]
    out = num_tot / den_tot[..., None]
    return out.transpose(0, 2, 1, 3).reshape(B, S, H * Dh)


def hybrid_layer(x, norm1_w, w_in, conv_qkv_w, a_log, dt_bias, gdn_norm_w, w_out,
                 norm2_w, w_up, ffn_conv_w, w_down):
    B, S, _ = x.shape
    h = rmsnorm(x, norm1_w)
    proj = h @ w_in
    o1 = 3 * GDN_WIDTH
    o2 = o1 + GDN_WIDTH
    o3 = o2 + GDN_HEADS
    o4 = o3 + GDN_HEADS
    qkv_a, z_a, b_a, a_a, qkv_b = (proj[..., :o1], proj[..., o1:o2], proj[..., o2:o3],
                                   proj[..., o3:o4], proj[..., o4:])
    qkv_a = jax.nn.silu(causal_dwconv(qkv_a, conv_qkv_w)).astype(jnp.float32)
    qa = qkv_a[..., :GDN_WIDTH].reshape(B, S, GDN_HEADS, GDN_HEAD_DIM)
    ka = qkv_a[..., GDN_WIDTH:2 * GDN_WIDTH].reshape(B, S, GDN_HEADS, GDN_HEAD_DIM)
    va = qkv_a[..., 2 * GDN_WIDTH:].reshape(B, S, GDN_HEADS, GDN_HEAD_DIM)
    qa = l2norm(qa) * (GDN_HEAD_DIM ** -0.5)
    ka = l2norm(ka)
    beta = jax.nn.sigmoid(b_a.astype(jnp.float32))
    g = -jnp.exp(a_log.astype(jnp.float32)) * jax.nn.softplus(
        a_a.astype(jnp.float32) + dt_bias.astype(jnp.float32))
    o_a = gated_delta_rule(qa, ka, va, g, beta)
    z = z_a.astype(jnp.float32).reshape(B, S, GDN_HEADS, GDN_HEAD_DIM)
    o_a = (rmsnorm(o_a, gdn_norm_w) * jax.nn.silu(z)).reshape(B, S, GDN_WIDTH).astype(x.dtype)
    qb = qkv_b[..., :DIL_WIDTH].reshape(B, S, DIL_HEADS, DIL_HEAD_DIM)
    kb = qkv_b[..., DIL_WIDTH:2 * DIL_WIDTH].reshape(B, S, DIL_HEADS, DIL_HEAD_DIM)
    vb = qkv_b[..., 2 * DIL_WIDTH:].reshape(B, S, DIL_HEADS, DIL_HEAD_DIM)
    o_b = dilated_attention(qb, kb, vb).astype(x.dtype)
    x = x + jnp.concatenate([o_a, o_b], axis=-1) @ w_out
    h = rmsnorm(x, norm2_w)
    u = causal_dwconv(h @ w_up, ffn_conv_w)
    gate, up = u[..., :D_FF], u[..., D_FF:]
    x = x + (jax.nn.silu(gate) * up) @ w_down
    return x


def setup_inputs(seed: int = 0) -> dict:
    key = jax.random.key(seed)
    ks = jax.random.split(key, 14)
    f32 = jnp.float32
    x = jax.random.normal(ks[0], (BATCH, SEQ, D_MODEL), f32)
    norm1_w = 1.0 + 0.02 * jax.random.normal(ks[1], (DEPTH, D_MODEL), f32)
    w_in = jax.random.normal(ks[2], (DEPTH, D_MODEL, IN_COLS), f32) * D_MODEL ** -0.5
    conv_qkv_w = jax.random.normal(ks[3], (DEPTH, GDN_CONV, 3 * GDN_WIDTH), f32) * GDN_CONV ** -0.5
    a_log = jnp.log(jax.random.uniform(ks[4], (DEPTH, GDN_HEADS), f32, 1.0, 16.0))
    dt = jnp.exp(jax.random.uniform(ks[5], (DEPTH, GDN_HEADS), f32, math.log(1e-3), math.log(1e-1)))
    dt_bias = dt + jnp.log(-jnp.expm1(-dt))
    gdn_norm_w = 1.0 + 0.02 * jax.random.normal(ks[6], (DEPTH, GDN_HEAD_DIM), f32)
    w_out = jax.random.normal(ks[7], (DEPTH, MIX_WIDTH, D_MODEL), f32) * MIX_WIDTH ** -0.5
    norm2_w = 1.0 + 0.02 * jax.random.normal(ks[8], (DEPTH, D_MODEL), f32)
    w_up = jax.random.normal(ks[9], (DEPTH, D_MODEL, 2 * D_FF), f32) * D_MODEL ** -0.5
    ffn_conv_w = jax.random.normal(ks[10], (DEPTH, FFN_CONV, 2 * D_FF), f32) * FFN_CONV ** -0.5
    w_down = jax.random.normal(ks[11], (DEPTH, D_FF, D_MODEL), f32) * D_FF ** -0.5
    final_norm_w = 1.0 + 0.02 * jax.random.normal(ks[12], (D_MODEL,), f32)
    return {"x": x, "norm1_w": norm1_w, "w_in": w_in, "conv_qkv_w": conv_qkv_w,
            "a_log": a_log, "dt_bias": dt_bias, "gdn_norm_w": gdn_norm_w, "w_out": w_out,
            "norm2_w": norm2_w, "w_up": w_up, "ffn_conv_w": ffn_conv_w, "w_down": w_down,
            "final_norm_w": final_norm_w}


def reference(x, norm1_w, w_in, conv_qkv_w, a_log, dt_bias, gdn_norm_w, w_out,
              norm2_w, w_up, ffn_conv_w, w_down, final_norm_w):
    for l in range(DEPTH):
        x = hybrid_layer(x, norm1_w[l], w_in[l], conv_qkv_w[l], a_log[l], dt_bias[l],
                         gdn_norm_w[l], w_out[l], norm2_w[l], w_up[l], ffn_conv_w[l], w_down[l])
    return rmsnorm(x, final_norm_w)
```

```python
import contextlib
import numpy as np
import concourse.bass as bass
import concourse.mybir as mybir
from concourse.bass_utils import run_bass_kernel_spmd

F32 = mybir.dt.float32
BF16 = mybir.dt.bfloat16
AF = mybir.ActivationFunctionType
ALU = mybir.AluOpType
AX = mybir.AxisListType

S = 2048
D = 1024
NT = 16
EPS = 1e-6
NEG = -30000.0
ENGS = ("pe", "act", "dve", "pool", "sp")


def _rect(ap):
    t = ap.tensor
    if str(ap.space) == "PSUM":
        return (t.name, 0, 128, 0, 2048)
    esz = mybir.dt.size(ap.dtype)
    pstride = 1
    for s in tuple(t.shape)[1:]:
        pstride *= s
    p0 = ap.start_partition()
    p1 = p0 + ap.partition_size()
    f0 = ap.offset - p0 * pstride
    ext = 0
    for (st, cnt) in tuple(ap.ap)[1:]:
        ext += abs(st) * (cnt - 1)
    return (t.name, p0, p1, f0 * esz, (f0 + ext + 1) * esz)


class _Op:
    __slots__ = ("eng", "fn", "idx", "is_dma", "tag", "waits", "signal", "sigval")


class Prog:
    def __init__(self, nc):
        self.nc = nc
        self.ops = {e: [] for e in ENGS}
        self.track = {}
        self.waited = {e: {} for e in ENGS}
        self.tagcount = {}

    def _add(self, eng, fn, reads, writes, is_dma=False, tag=None):
        op = _Op()
        op.eng = eng; op.fn = fn; op.is_dma = is_dma; op.tag = tag
        op.signal = False; op.sigval = None
        op.idx = len(self.ops[eng])
        op.waits = []
        deps = {}
        rrects = [_rect(a) for a in reads if a is not None and str(a.space) != "DRAM"]
        wrects = [_rect(a) for a in writes if a is not None and str(a.space) != "DRAM"]
        prects = [r for r in rrects if r[4] == 2048 and r[0].startswith("pb") and r not in wrects]
        for (nm, p0, p1, f0, f1) in rrects:
            for rec in self.track.get(nm, ()):
                if rec[5] == 1 and rec[0] < p1 and p0 < rec[1] and rec[2] < f1 and f0 < rec[3]:
                    deps[id(rec[4])] = (rec[4], True)
        for (nm, p0, p1, f0, f1) in wrects + prects:
            for rec in self.track.get(nm, ()):
                if rec[0] < p1 and p0 < rec[1] and rec[2] < f1 and f0 < rec[3]:
                    k = id(rec[4])
                    if k not in deps:
                        deps[k] = (rec[4], False)
        need = {}
        for (d, raw) in deps.values():
            if d.is_dma:
                key = ("dma", d.tag)
                val = self.tagcount[d.tag]
                if need.get(key, 0) < val:
                    need[key] = val
            else:
                if d.eng == eng and eng == "pe":
                    continue
                key = ("eng", d.eng)
                cur = need.get(key)
                if cur is None or cur.idx < d.idx:
                    need[key] = d
        w = self.waited[eng]
        for key, v in need.items():
            if key[0] == "dma":
                if w.get(key, 0) >= v:
                    continue
                w[key] = v
                op.waits.append((key, v))
            else:
                if w.get(key, -1) >= v.idx:
                    continue
                w[key] = v.idx
                v.signal = True
                op.waits.append((key, v))
        if is_dma:
            self.tagcount[tag] = self.tagcount.get(tag, 0) + 16
        for (nm, p0, p1, f0, f1) in wrects:
            lst = self.track.setdefault(nm, [])
            lst[:] = [r for r in lst if not (p0 <= r[0] and r[1] <= p1 and f0 <= r[2] and r[3] <= f1)]
            lst.append([p0, p1, f0, f1, op, 1])
        for (nm, p0, p1, f0, f1) in prects:
            self.track[nm] = [[p0, p1, f0, f1, op, 2]]
        for (nm, p0, p1, f0, f1) in rrects:
            if nm.startswith("pb"):
                continue
            lst = self.track.setdefault(nm, [])
            done = False
            for r in lst:
                if r[5] == 0 and r[4].eng == eng and (not r[4].is_dma) and (not is_dma) \
                        and r[0] == p0 and r[1] == p1 and r[2] == f0 and r[3] == f1:
                    r[4] = op
                    done = True
                    break
            if not done:
                lst.append([p0, p1, f0, f1, op, 0])
        self.ops[eng].append(op)
        return op

    def dma(self, q, out, in_, tag):
        return self._add(q, lambda e: e.dma_start(out=out, in_=in_), [in_], [out], is_dma=True, tag=tag)

    def mm(self, out, lhsT, rhs, start=True, stop=True, **kw):
        rd = [lhsT, rhs] + ([] if start else [out])
        return self._add("pe", lambda e: e.matmul(out, lhsT, rhs, start=start, stop=stop, **kw), rd, [out])

    def tr(self, out, in_, ident):
        return self._add("pe", lambda e: e.transpose(out, in_, ident), [in_, ident], [out])

    def act(self, out, in_, func, bias=None, scale=None, accum_out=None):
        kw = {}
        rd = [in_]
        if bias is not None:
            kw["bias"] = bias
            if not isinstance(bias, (int, float)):
                rd.append(bias)
        if scale is not None:
            kw["scale"] = scale
            if not isinstance(scale, (int, float)):
                rd.append(scale)
        wr = [out]
        if accum_out is not None:
            kw["accum_out"] = accum_out
            wr.append(accum_out)
        return self._add("act", lambda e: e.activation(out, in_, func, **kw), rd, wr)

    def tt(self, eng, out, in0, in1, op):
        return self._add(eng, lambda e: e.tensor_tensor(out, in0, in1, op), [in0, in1], [out])

    def ts(self, eng, out, in0, s1, s2, op0, op1=None):
        rd = [in0] + [s for s in (s1, s2) if s is not None and not isinstance(s, (int, float))]
        kw = {}
        if op1 is not None:
            kw["op1"] = op1
        return self._add(eng, lambda e: e.tensor_scalar(out, in0, s1, s2, op0, **kw), rd, [out])

    def stt(self, out, in0, scalar, in1, op0, op1):
        rd = [in0, in1] + ([] if isinstance(scalar, (int, float)) else [scalar])
        return self._add("dve", lambda e: e.scalar_tensor_tensor(out, in0, scalar, in1, op0, op1), rd, [out])

    def copy(self, eng, out, in_):
        if eng == "act":
            return self._add("act", lambda e: e.copy(out, in_), [in_], [out])
        return self._add(eng, lambda e: e.tensor_copy(out, in_), [in_], [out])

    def memset(self, eng, ap, val):
        return self._add(eng, lambda e: e.memset(ap, val), [], [ap])

    def recip(self, out, in_):
        return self._add("dve", lambda e: e.reciprocal(out, in_), [in_], [out])

    def rsum(self, out, in_):
        return self._add("dve", lambda e: e.tensor_reduce(out, in_, AX.X, ALU.add), [in_], [out])

    def emit(self, final_dma_tags=()):
        nc = self.nc
        for e in ENGS:
            c = 0
            for op in self.ops[e]:
                if op.signal and not op.is_dma:
                    c += 1
                    op.sigval = c
        with contextlib.ExitStack() as st:
            esem = {e: st.enter_context(nc.semaphore("s_" + e)) for e in ENGS}
            dsem = {t: st.enter_context(nc.semaphore("d_%d" % i)) for i, t in enumerate(self.tagcount)}
            block = st.enter_context(nc.Block())
            engobj = {"pe": block.tensor, "act": block.scalar, "dve": block.vector,
                      "pool": block.gpsimd, "sp": block.sync}

            def make(ename):
                def body(eng):
                    for op in self.ops[ename]:
                        for (key, v) in op.waits:
                            if key[0] == "dma":
                                eng.wait_ge(dsem[key[1]], v)
                            else:
                                eng.wait_ge(esem[key[1]], v.sigval)
                        ins = op.fn(eng)
                        if op.is_dma:
                            ins.then_inc(dsem[op.tag], 16)
                        elif op.signal:
                            ins.then_inc(esem[ename], 1)
                    if ename == "sp":
                        for t in final_dma_tags:
                            eng.wait_ge(dsem[t], self.tagcount[t])
                return body

            for e in ENGS:
                engobj[e](make(e))
        return nc


def _prod(shape):
    n = 1
    for s in shape:
        n *= s
    return n


class Arena:
    def __init__(self, nc, nbytes):
        self.t = nc.alloc_sbuf_tensor("arena", [128, nbytes // 4], F32)
        self.nbytes = nbytes

    def view(self, off, shape, dt):
        n = _prod(shape)
        esz = 4 if dt == F32 else 2
        assert off % 4 == 0 and (n * esz) % 4 == 0 and off + n * esz <= self.nbytes, (off, shape)
        ap = self.t[:, off // 4:(off + n * esz) // 4]
        if dt != F32:
            ap = ap.bitcast(dt)
        if len(shape) > 1:
            names = " ".join("d%d" % i for i in range(len(shape)))
            kw = {"d%d" % i: shape[i] for i in range(1, len(shape))}
            ap = ap.rearrange("p (%s) -> p %s" % (names, names), **kw)
        return ap


class Bump:
    def __init__(self, arena, lo, hi):
        self.a = arena; self.lo = lo; self.hi = hi; self.cur = lo

    def alloc(self, shape, dt=F32):
        esz = 4 if dt == F32 else 2
        n = (_prod(shape) * esz + 31) // 32 * 32
        off = self.cur
        self.cur += n
        assert self.cur <= self.hi, ("arena overflow", self.cur, self.hi)
        return self.a.view(off, shape, dt)


class Ring:
    def __init__(self, items):
        self.items = list(items); self.i = 0

    def next(self):
        x = self.items[self.i % len(self.items)]
        self.i += 1
        return x


_CNAMES = ["IDENT", "M1", "M2", "TRI", "SUF", "IND0", "IND1", "S32", "OFF", "ONES"]
_CB = {"MB64": 512, "MBA4": 512, "MBB4": 512}
NCF = 128 * len(_CNAMES)
NCONST = NCF + 512 * 3


def make_consts():
    i = np.arange(128)[:, None]
    j = np.arange(128)[None, :]
    same = (i // 64) == (j // 64)
    c = {}
    c["IDENT"] = (i == j)
    c["M1"] = (i > j)
    c["M2"] = (i <= j)
    c["TRI"] = (i <= j) & same
    c["SUF"] = (i > j) & same
    c["IND0"] = (i < 64) & (j >= 0)
    c["IND1"] = (i >= 64) & (j >= 0)
    c["S32"] = (i < j) & ((i // 32) == (j // 32))
    c["OFF"] = same & ((i % 64) < 32) & ((j % 64) >= 32)
    c["ONES"] = np.ones((128, 128), bool)
    cols = [c[n].astype(np.float32) for n in _CNAMES]
    mb64 = np.where((i <= j) & same, 0.0, NEG).astype(np.float32)
    mba = np.where(i <= j, 0.0, NEG).astype(np.float32)
    mbb = np.where(i >= j, 0.0, NEG).astype(np.float32)
    cols += [np.tile(mb64, (1, 4)), np.tile(mba, (1, 4)), np.tile(mbb, (1, 4))]
    return np.ascontiguousarray(np.concatenate(cols, 1))


def build(stage="full", dumps=()):
    nc = bass.Bass("TRN2", target_bir_lowering=False)
    P = Prog(nc)
    dumps = set(dumps)
    dump_tags = []

    def din(name, shape):
        return nc.dram_tensor(name, list(shape), F32, kind="ExternalInput").ap()

    x_d = din("x", [S, D])
    win_d = din("w_in", [D, 3592])
    wout_d = din("w_out", [D, D])
    wup_d = din("w_up", [D, 5632])
    wdn_d = din("w_down", [2816, D])
    n1_d = din("n1rep", [128, D])
    n2_d = din("n2rep", [128, D])
    nf_d = din("nfrep", [128, D])
    cwa_d = din("cwA", [128, 48])
    cwf_d = din("cwF", [128, 132])
    gnw_d = din("gnwrep", [128, 128])
    dtb_d = din("dtbrep", [128, 64])
    alog_d = din("alogrep", [128, 64])
    cst_d = din("consts", [128, NCONST])
    out_d = nc.dram_tensor("out", [S, D], F32, kind="ExternalOutput").ap()

    def dump(name, sb_ap, shape):
        if name not in dumps:
            return
        d = nc.dram_tensor("dbg_" + name, list(shape), sb_ap.dtype, kind="ExternalOutput").ap()
        tg = "dbg_" + name
        P.dma("sp", d, sb_ap, tg)
        dump_tags.append(tg)

    win_v = win_d.rearrange("(k p) c -> p k c", p=128)
    wout_v = wout_d.rearrange("(k p) c -> p k c", p=128)
    wup_v = wup_d.rearrange("(k p) c -> p k c", p=128)
    wdn_v = wdn_d.rearrange("(k p) c -> p k c", p=128)

    ARENA_BYTES = 207872
    A = Arena(nc, ARENA_BYTES)
    pb = [nc.alloc_psum_tensor("pb%d" % i, [128, 512], F32) for i in range(8)]

    def pbf(i):
        return pb[i][:]

    def pbb(i):
        return pb[i][:].bitcast(BF16)

    CT = A.view(0, [8, S], BF16)
    hT = A.view(32768, [8, S], BF16)
    pers = Bump(A, 65536, 86016)
    CF = pers.alloc([NCF])
    CBm = pers.alloc([1536 + 256], BF16)
    cwA = pers.alloc([12, 4])
    cwF = pers.alloc([44, 3])
    HALOA = pers.alloc([12, 3])
    HALOF = pers.alloc([44, 2])
    GNW = pers.alloc([128])
    NW1 = pers.alloc([D])
    NW2 = pers.alloc([D])
    SCR_LO = 86016
    SCR_HI = ARENA_BYTES

    def cf(name):
        k = _CNAMES.index(name)
        return CF[:, 128 * k:128 * (k + 1)]

    IDENT = cf("IDENT"); M1 = cf("M1"); M2 = cf("M2"); TRI = cf("TRI"); SUF = cf("SUF")
    IND0 = cf("IND0"); IND1 = cf("IND1"); S32 = cf("S32"); OFFM = cf("OFF"); ONES = cf("ONES")
    MB64b = CBm[:, 0:512]; MBA4b = CBm[:, 512:1024]; MBB4b = CBm[:, 1024:1536]
    IDENTb = CBm[:, 1536:1664]; ONESb = CBm[:, 1664:1792]

    P.dma("sp", CF, cst_d[:, 0:NCF], "c_cf")
    ctmp = A.view(SCR_LO, [1536], F32)
    P.dma("sp", ctmp, cst_d[:, NCF:NCONST], "c_tmp")
    P.copy("dve", CBm[:, 0:1536], ctmp)
    P.copy("dve", IDENTb, IDENT)
    P.copy("dve", ONESb, ONES)
    P.dma("sp", cwA.rearrange("p a b -> p (a b)"), cwa_d, "c_cwa")
    P.dma("sp", cwF.rearrange("p a b -> p (a b)"), cwf_d, "c_cwf")
    P.dma("sp", GNW, gnw_d, "c_gnw")
    P.dma("sp", NW1, n1_d, "c_nw1")
    P.memset("pool", HALOA.rearrange("p a b -> p (a b)"), 0.0)
    P.memset("pool", HALOF.rearrange("p a b -> p (a b)"), 0.0)

    def bc_last(ap, n):
        return ap.unsqueeze(2).broadcast_to([128, ap.shape[1], n])

    def bc_mid(ap, n):
        return ap.unsqueeze(1).broadcast_to([128, n, ap.shape[1]])

    def rmsnorm_stage1(src_tile, wtile, scr):
        junk, ssv, rst, hb = scr
        P.act(junk, src_tile, AF.Square, accum_out=ssv)
        P.act(rst, ssv, AF.Ln, bias=EPS, scale=1.0 / D)
        P.act(rst, rst, AF.Exp, scale=-0.5)
        P.stt(hb, src_tile, rst, wtile, ALU.mult, ALU.mult)

    def rmsnorm_stage2(dstT, col0, scr, bank):
        hb = scr[3]
        psT = pbb(bank)
        for kc in range(8):
            P.tr(psT[:, 128 * kc:128 * (kc + 1)], hb[:, 128 * kc:128 * (kc + 1)], IDENTb)
        P.copy("act", dstT[:, :, col0:col0 + 128], psT.rearrange("p (k t) -> p k t", k=8))

    bA = Bump(A, SCR_LO + 8192, SCR_HI)
    xbuf = [bA.alloc([D]) for _ in range(2)]
    hbuf = [bA.alloc([D], BF16) for _ in range(2)]
    junkA = bA.alloc([D], BF16)
    ssA = bA.alloc([NT])
    rsA = bA.alloc([NT])
    scrA = [(junkA, ssA[:, i:i + 1], rsA[:, i:i + 1], hbuf[i % 2]) for i in range(NT)]
    for i in range(NT + 1):
        if i < NT:
            xt = xbuf[i % 2]
            P.dma("sp", xt, x_d[128 * i:128 * (i + 1), :], "xa%d" % (i % 2))
            rmsnorm_stage1(xt, NW1, scrA[i])
        if i >= 1:
            rmsnorm_stage2(hT, 128 * (i - 1), scrA[i - 1], 6 + ((i - 1) % 2))
    dump("hT", hT.rearrange("p k t -> p (k t)"), [128, 8 * S])
    if stage == "A":
        return finish(nc, P, out_d, dump_tags)

    bG = Bump(A, SCR_LO, SCR_HI)
    bC = Bump(A, 16384, 32768)
    wqkva = bG.alloc([3, 8, 512], BF16)
    wba = bG.alloc([8, 8], BF16)
    wz = bC.alloc([8, 512], BF16)
    for c3 in range(3):
        P.dma("pool", wqkva[:, c3, :, :], win_v[:, :, 512 * c3:512 * (c3 + 1)], "wqkva%d" % c3)
    P.dma("pool", wba, win_v[:, :, 2048:2056], "wba")
    P.dma("pool", wz, win_v[:, :, 1536:2048], "wz")

    rawb = [bG.alloc([520]) for _ in range(2)]
    accb = [bG.alloc([512]) for _ in range(2)]
    sqb = bG.alloc([512], BF16)
    rtb = bG.alloc([512])
    qkvT = [dict(q=bG.alloc([4, 512], BF16), k=bG.alloc([4, 512], BF16), v=bG.alloc([4, 512], BF16))
            for _ in range(2)]
    BA = bG.alloc([NT, 8])
    sc_x = bG.alloc([64]); sc_mx = bG.alloc([64]); sc_mn = bG.alloc([64])
    dtb = bG.alloc([64]); negA = bG.alloc([64])
    gS = bG.alloc([64]); betaS = bG.alloc([64]); gamS = bG.alloc([64]); kesS = bG.alloc([64])
    gendS = bG.alloc([2, 64])
    gb1 = [bG.alloc([4, 128]) for _ in range(6)]
    gb2 = [bG.alloc([2, 4, 128]) for _ in range(3)]
    TTb = bG.alloc([4, 128], BF16); vb = bG.alloc([4, 128], BF16); gk = bG.alloc([4, 128], BF16)
    scanop = []
    for _ in range(2):
        scanop.append(dict(KLO=bG.alloc([4, 128], BF16), KHI=bG.alloc([4, 128], BF16), ATT=bG.alloc([4, 128], BF16),
                           QD=bG.alloc([4, 128], BF16), WKT=bG.alloc([4, 128], BF16),
                           UV=bG.alloc([4, 128])))
    Sst = bG.alloc([4, 128]); Sb = bG.alloc([4, 128], BF16); ub = bG.alloc([4, 128], BF16)
    tmpS = bG.alloc([4, 128])
    Obuf = [bC.alloc([4, 128]) for _ in range(2)]
    sqO = bG.alloc([4, 128]); oab = bG.alloc([4, 128], BF16)
    szgb = [bG.alloc([4, 512], BF16), bC.alloc([4, 512], BF16)]
    kesLo = bG.alloc([64]); kesHi = bG.alloc([64])
    ssq = bG.alloc([4]); rsq = bG.alloc([4])

    ringG = Ring([0, 1, 2, 3, 4, 5, 6, 7])

    P.dma("sp", dtb, dtb_d, "c_dtb")
    P.dma("sp", negA, alog_d, "c_alog")
    P.act(negA, negA, AF.Exp)
    P.ts("dve", negA, negA, -1.0, None, ALU.mult)
    bk = ringG.next()
    psBA = pbf(bk)[:, 0:128].rearrange("p (n c) -> p n c", c=8)
    for i in range(NT):
        for kc in range(8):
            P.mm(psBA[:, i, :], hT[:, kc, 128 * i:128 * (i + 1)], wba[:, kc, :], start=(kc == 0), stop=(kc == 7))
    P.copy("act", BA, psBA)
    x3 = sc_x.rearrange("p (n h) -> p n h", h=4)
    P.tt("dve", x3, BA[:, :, 4:8], dtb.rearrange("p (n h) -> p n h", h=4), ALU.add)
    P.ts("dve", sc_mx, sc_x, 0.0, None, ALU.max)
    P.ts("dve", sc_mn, sc_x, 0.0, None, ALU.min)
    P.tt("dve", sc_mn, sc_mn, sc_mx, ALU.subtract)
    P.act(sc_mn, sc_mn, AF.Exp)
    P.act(sc_mn, sc_mn, AF.Ln, bias=1.0)
    P.tt("dve", sc_mx, sc_mx, sc_mn, ALU.add)
    P.tt("dve", gS, sc_mx, negA, ALU.mult)
    P.act(betaS.rearrange("p (n h) -> p n h", h=4), BA[:, :, 0:4], AF.Sigmoid)
    bk = ringG.next()
    psg = pbf(bk)
    P.mm(psg[:, 0:64], TRI, gS)
    P.mm(psg[:, 64:128], SUF, gS)
    P.mm(psg[:, 128:192], IND0, gS)
    P.mm(psg[:, 192:256], IND1, gS)
    P.act(gamS, psg[:, 0:64], AF.Exp)
    P.act(kesS, psg[:, 64:128], AF.Exp)
    P.tt("dve", kesS, kesS, betaS, ALU.mult)
    P.act(gendS.rearrange("p a b -> p (a b)"), psg[:, 128:256], AF.Exp)
    dump("gS", gS, [128, 64]); dump("betaS", betaS, [128, 64]); dump("gamS", gamS, [128, 64])
    dump("kesS", kesS, [128, 64]); dump("gendS", gendS.rearrange("p a b -> p (a b)"), [128, 128])
    g3 = gS.rearrange("p (n h) -> p n h", h=4)
    beta3 = betaS.rearrange("p (n h) -> p n h", h=4)
    gam3 = gamS.rearrange("p (n h) -> p n h", h=4)
    P.ts("dve", kesLo, kesS, IND0[:, 0:1], None, ALU.mult)
    P.ts("dve", kesHi, kesS, IND1[:, 0:1], None, ALU.mult)
    kesLo3 = kesLo.rearrange("p (n h) -> p n h", h=4)
    kesHi3 = kesHi.rearrange("p (n h) -> p n h", h=4)
    gend4 = gendS.rearrange("p a (n h) -> p a n h", h=4)

    def G1(m):
        t0 = 512 * m
        o = qkvT[m % 2]
        for c in range(12):
            ps = pbf(ringG.next())
            for kc in range(8):
                P.mm(ps, wqkva[:, c // 4, kc, 128 * (c % 4):128 * (c % 4 + 1)], hT[:, kc, t0:t0 + 512], start=(kc == 0), stop=(kc == 7))
            raw = rawb[c % 2]; acc = accb[c % 2]
            P.copy("pool", raw[:, 0:3], HALOA[:, c, :])
            P.copy("act", raw[:, 3:515], ps)
            P.copy("pool", HALOA[:, c, :], raw[:, 512:515])
            P.act(acc, ps, AF.Identity, scale=cwA[:, c, 3:4])
            P.stt(acc, raw[:, 2:514], cwA[:, c, 2:3], acc, ALU.mult, ALU.add)
            P.stt(acc, raw[:, 1:513], cwA[:, c, 1:2], acc, ALU.mult, ALU.add)
            P.stt(acc, raw[:, 0:512], cwA[:, c, 0:1], acc, ALU.mult, ALU.add)
            dst = (o["q"], o["k"], o["v"])[c // 4][:, c % 4, :]
            P.act(dst, acc, AF.Silu)
            if c % 3 == 2:
                nz = 4 * m + c // 3
                psZ = pbf(ringG.next())
                for kc in range(8):
                    P.mm(psZ, hT[:, kc, 128 * nz:128 * (nz + 1)], wz[:, kc, :], start=(kc == 0), stop=(kc == 7))
                szg = szgb[m % 2][:, c // 3, :]
                P.act(szg, psZ, AF.Silu)
                P.tt("pool", szg.rearrange("p (h d) -> p h d", h=4), szg.rearrange("p (h d) -> p h d", h=4),
                     bc_mid(GNW, 4), ALU.mult)
                yield
        for c in range(8):
            dst = (o["q"], o["k"])[c // 4][:, c % 4, :]
            sc = 128.0 if c < 4 else 1.0
            P.act(sqb, dst, AF.Square)
            psn = pbf(ringG.next())
            P.mm(psn, ONESb, sqb)
            P.act(rtb, psn, AF.Ln, bias=EPS * sc, scale=sc)
            P.act(rtb, rtb, AF.Exp, scale=-0.5)
            P.tt("dve", dst, dst, rtb, ALU.mult)
            yield
        if m == 0:
            dump("qnT0", o["q"].rearrange("p h t -> p (h t)"), [128, 2048])
            dump("knT0", o["k"].rearrange("p h t -> p (h t)"), [128, 2048])
            dump("vsT0", o["v"].rearrange("p h t -> p (h t)"), [128, 2048])

    def G2(n, h0, nh):
        tl = 128 * (n % 4)
        hs = slice(h0, h0 + nh)
        hr = range(h0, h0 + nh)
        so = scanop[n % 2]
        qnT = qkvT[(n // 4) % 2]["q"]; knT = qkvT[(n // 4) % 2]["k"]; vsT = qkvT[(n // 4) % 2]["v"]
        G1m, DECT, Uall, Eu, PTa, PTb = [b_[:, hs, :] for b_ in gb1]
        UL0, PWA, PWB = [b_[:, :, hs, :] for b_ in gb2]
        Du, Dl = UL0[:, 0], UL0[:, 1]
        gam_b = bc_last(gam3[:, n, hs], 128)
        keslo_b = bc_last(kesLo3[:, n, hs], 128)
        keshi_b = bc_last(kesHi3[:, n, hs], 128)
        beta_b = bc_last(beta3[:, n, hs], 128)
        g_b = bc_last(g3[:, n, hs], 128)

        def bank3():
            return pbf(ringG.next())[:, 0:128 * nh].rearrange("p (h d) -> p h d", h=nh)

        def bank4():
            return pbf(ringG.next())[:, 0:256 * nh].rearrange("p (a h d) -> p a h d", a=2, h=nh)

        psKV = pbb(ringG.next())[:, 0:256 * nh].rearrange("p (a h d) -> p a h d", a=2, h=nh)
        psK = psKV[:, 0]; psV = psKV[:, 1]
        for i, h in enumerate(hr):
            P.tr(psK[:, i, :], knT[:, h, tl:tl + 128], IDENTb)
            P.tr(psV[:, i, :], vsT[:, h, tl:tl + 128], IDENTb)
        P.tt("dve", gk[:, hs, :], psK, gam_b, ALU.mult)
        P.tt("dve", so["KLO"][:, hs, :], psK, keslo_b, ALU.mult)
        P.tt("dve", so["KHI"][:, hs, :], psK, keshi_b, ALU.mult)
        P.copy("act", vb[:, hs, :], psV)
        yield
        P.tt("pool", G1m, bc_mid(M1, nh), g_b, ALU.mult)
        psD = bank3()
        P.mm(psD, IDENTb, MB64b[:, 0:128 * nh], start=True, stop=False)
        for i, h in enumerate(hr):
            P.mm(psD[:, i, :], G1m[:, i, :], M2, start=False, stop=(i == nh - 1))
        P.act(DECT, psD, AF.Exp)
        P.tt("pool", DECT, DECT, beta_b, ALU.mult)
        yield
        dg = Uall
        P.tt("pool", dg, bc_mid(IDENT, nh), gam_b, ALU.mult)
        psG = bank3()
        for i, h in enumerate(hr):
            P.mm(psG[:, i, :], ONES, dg[:, i, :])
        P.tt("dve", so["QD"][:, hs, :], qnT[:, hs, tl:tl + 128], psG, ALU.mult)
        yield
        psKK = bank3(); psQK = bank3()
        for i, h in enumerate(hr):
            P.mm(psKK[:, i, :], knT[:, h, tl:tl + 128], knT[:, h, tl:tl + 128])
        for i, h in enumerate(hr):
            P.mm(psQK[:, i, :], knT[:, h, tl:tl + 128], qnT[:, h, tl:tl + 128])
        P.tt("dve", Uall, psKK, DECT, ALU.mult)
        P.tt("dve", so["ATT"][:, hs, :], psQK, DECT, ALU.mult)
        P.tt("pool", Du, Uall, bc_mid(S32, nh), ALU.mult)
        P.tt("pool", Eu, Uall, bc_mid(OFFM, nh), ALU.mult)
        yield
        psT = bank3()
        for i in range(nh):
            P.tr(psT[:, i, :], Du[:, i, :], IDENT)
        P.copy("act", Dl, psT)
        P.tt("pool", PTa, bc_mid(IDENT, nh), Du, ALU.subtract)
        yield
        pw = [UL0, PWA, PWB, PWA, PWB]
        PT, PTn = PTa, PTb
        for k in range(1, 5):
            cur = pw[k - 1]; nxt = pw[k]
            if k <= 4:
                ps4 = bank4()
                if k < 4:
                    for i in range(nh):
                        P.mm(ps4[:, 0, i, :], cur[:, 1, i, :], cur[:, 0, i, :])
                for i in range(nh):
                    P.mm(ps4[:, 1, i, :], cur[:, 0, i, :], cur[:, 1, i, :])
            if k > 1:
                ps3 = bank3()
                for i in range(nh):
                    P.mm(ps3[:, i, :], cur[:, 1, i, :], PT[:, i, :])
            if k < 4:
                P.copy("act", nxt, ps4)
            else:
                P.copy("act", nxt[:, 1], ps4[:, 1])
            if k > 1:
                P.tt("dve", PTn, PT, ps3, ALU.add)
                PT, PTn = PTn, PT
            yield
        ps3 = bank3()
        for i in range(nh):
            P.mm(ps3[:, i, :], PWB[:, 1, i, :], PT[:, i, :])
        P.tt("dve", PTn, PT, ps3, ALU.add)
        PT, PTn = PTn, PT
        yield
        Pm, XT = G1m, DECT
        p1 = bank3()
        for i in range(nh):
            P.tr(p1[:, i, :], PT[:, i, :], IDENT)
        P.copy("act", Pm, p1)
        yield
        p2 = bank3()
        for i in range(nh):
            P.mm(p2[:, i, :], Eu[:, i, :], Pm[:, i, :])
        P.copy("act", XT, p2)
        yield
        p3 = bank3()
        for i in range(nh):
            P.mm(p3[:, i, :], XT[:, i, :], PT[:, i, :])
        P.tt("dve", TTb[:, hs, :], PT, p3, ALU.subtract)
        yield
        p1 = bank3(); p2 = bank3()
        for i, h in enumerate(hr):
            P.mm(p1[:, i, :], TTb[:, h, :], vb[:, h, :])
        for i, h in enumerate(hr):
            P.mm(p2[:, i, :], gk[:, h, :], TTb[:, h, :])
        P.copy("act", so["UV"][:, hs, :], p1)
        P.copy("dve", so["WKT"][:, hs, :], p2)
        yield

    def SCAN(n):
        so = scanop[n % 2]
        O = Obuf[n % 2]
        for half in range(2):
            r0 = 64 * half
            rs = slice(r0, r0 + 64)
            kend = so["KLO"] if half == 0 else so["KHI"]
            psA = pbf(ringG.next()).rearrange("p (h d) -> p h d", h=4)
            for h in range(4):
                P.mm(psA[:, h, :], so["WKT"][:, h, :], Sb[:, h, :])
            P.tt("dve", ub[rs], so["UV"][rs], psA[rs], ALU.subtract)
            yield
            psS = pbf(ringG.next()).rearrange("p (h d) -> p h d", h=4)
            psO = pbf(ringG.next()).rearrange("p (h d) -> p h d", h=4)
            for h in range(4):
                P.mm(psS[:, h, :], kend[:, h, :], ub[:, h, :])
            for h in range(4):
                P.mm(psO[:, h, :], so["QD"][:, h, :], Sb[:, h, :], start=True, stop=False)
                P.mm(psO[:, h, :], so["ATT"][:, h, :], ub[:, h, :], start=False, stop=True)
            P.tt("dve", tmpS, Sst, bc_last(gend4[:, half, n, :], 128), ALU.mult)
            P.tt("dve", Sst, tmpS, psS, ALU.add)
            P.copy("act", Sb, Sst)
            P.copy("act", O[rs], psO[rs])
            yield
        if n == 0:
            dump("O0", O.rearrange("p h d -> p (h d)"), [128, 512])
        if n == 15:
            dump("O15", O.rearrange("p h d -> p (h d)"), [128, 512])
        P.act(sqO, O, AF.Square)
        P.rsum(ssq, sqO)
        P.act(rsq, ssq, AF.Ln, bias=EPS, scale=1.0 / 128)
        P.act(rsq, rsq, AF.Exp, scale=-0.5)
        szg = szgb[(n // 4) % 2][:, n % 4, :].rearrange("p (h d) -> p h d", h=4)
        P.tt("dve", sqO, O, bc_last(rsq, 128), ALU.mult)
        P.tt("dve", oab, sqO, szg, ALU.mult)
        yield
        psT = pbb(ringG.next())[:, 0:512].rearrange("p (h d) -> p h d", h=4)
        for h in range(4):
            P.tr(psT[:, h, :], oab[:, h, :], IDENTb)
        P.copy("act", CT[:, 0:4, 128 * n:128 * (n + 1)], psT)
        yield

    def interleave(gens, bg=None, bg_quota=0):
        gens = list(gens)
        live = [True] * len(gens)
        used = 0
        while any(live):
            for gi, g in enumerate(gens):
                if not live[gi]:
                    continue
                try:
                    next(g)
                except StopIteration:
                    live[gi] = False
            if bg is not None and used < bg_quota:
                used += 1
                try:
                    next(bg)
                except StopIteration:
                    bg = None
        while bg is not None and used < bg_quota:
            used += 1
            try:
                next(bg)
            except StopIteration:
                bg = None
        return bg

    P.memset("dve", Sst.rearrange("p h d -> p (h d)"), 0.0)
    P.memset("pool", Sb.rearrange("p h d -> p (h d)"), 0.0)
    P.memset("pool", ub.rearrange("p h d -> p (h d)"), 0.0)
    interleave([G1(0)])
    bgG1 = G1(1)
    interleave([G2(0, 0, 2), G2(0, 2, 2)], bgG1, 4)
    for n in range(NT):
        m = n // 4
        if n % 4 == 0 and n > 0 and m + 1 < 4:
            bgG1 = G1(m + 1)
        fg = [SCAN(n)]
        if n + 1 < NT:
            fg = [G2(n + 1, 0, 2), G2(n + 1, 2, 2)] + fg
        quota = 1000 if n % 4 == 2 else 4
        bgG1 = interleave(fg, bgG1, quota)
    dump("CTa", CT[:, 0:4, :].rearrange("p k t -> p (k t)"), [128, 4 * S])
    if stage == "G":
        return finish(nc, P, out_d, dump_tags)

    bB = Bump(A, SCR_LO, SCR_HI)
    wqkvb = bB.alloc([3, 8, 512], BF16)
    for c3 in range(3):
        P.dma("pool", wqkvb[:, c3, :, :],
              win_v[:, :, 2056 + 512 * c3:2056 + 512 * (c3 + 1)], "wqkvb%d" % c3)
    QT0 = bB.alloc([S], BF16); KT0 = bB.alloc([S], BF16)
    QTb = [QT0, QT0]; KTb = [KT0, KT0]
    VAll = bB.alloc([48, 4, 192], BF16)
    PTbuf = Ring([bB.alloc([1024], BF16) for _ in range(3)])
    PT3 = bB.alloc([16, 128], BF16)
    rdenb = [bB.alloc([512]) for _ in range(2)]
    P.memset("pool", VAll[:, :, :, 64:128], 1.0)
    ringS = Ring([2, 3, 4, 5])
    ringP = Ring([6, 7])
    accR = Ring([0, 1])

    def tok_slices():
        sl = []
        for n in range(16):
            sl.append(slice(128 * n, 128 * (n + 1), 1))
        for r in range(4):
            for c in range(4):
                sl.append(slice(512 * c + r, 512 * (c + 1), 4))
        for r in range(16):
            sl.append(slice(r, S, 16))
        return sl

    TOK = tok_slices()

    def PROJ_V():
        for t in range(48):
            bank = ringP.next()
            ps = pbf(bank)
            sl = TOK[t]
            for kc in range(8):
                P.mm(ps, hT[:, kc, sl], wqkvb[:, 2, kc, :], start=(kc == 0), stop=(kc == 7))
            ps4 = ps.rearrange("p (j e d) -> p j e d", j=4, e=2)
            P.copy("act", VAll[:, t, :, 0:64], ps4[:, :, 0, :])
            P.copy("dve", VAll[:, t, :, 128:192], ps4[:, :, 1, :])

    def PROJ_B(j):
        k = j % 2
        QT = QTb[k]; KT = KTb[k]
        for (dst, cbase) in ((QT, 128 * j), (KT, 512 + 128 * j)):
            for tb in range(4):
                bank = ringP.next()
                ps = pbf(bank)
                for kc in range(8):
                    P.mm(ps, wqkvb[:, cbase // 512, kc, cbase % 512:cbase % 512 + 128], hT[:, kc, 512 * tb:512 * (tb + 1)],
                         start=(kc == 0), stop=(kc == 7))
                P.copy("dve" if tb % 2 else "act", dst[:, 512 * tb:512 * (tb + 1)], ps)

    def ATTN(j):
        k = j % 2
        QT = QTb[k]; KT = KTb[k]

        class _VA:
            def __init__(self, h):
                self.h = h

            def __getitem__(self, idx):
                e_ = self.h % 2
                return VAll[:, idx[1], self.h // 2, 64 * e_:64 * e_ + 128]

        for e in range(2):
            hp = slice(64 * e, 64 * e + 64)
            VA = _VA(2 * j + e)
            for g in range(4):
                bank = ringS.next()
                ps = pbf(bank)
                P.mm(ps, IDENTb, MBA4b, start=True, stop=False)
                for r4 in range(4):
                    r = 4 * g + r4
                    P.mm(ps[:, 128 * r4:128 * (r4 + 1)], KT[hp, r:S:16], QT[hp, r:S:16], start=False, stop=(r4 == 3))
                P.act(PT3[:, 4 * g:4 * g + 4, :], ps.rearrange("p (a d) -> p a d", a=4), AF.Exp, scale=0.125)
            for c in range(4):
                acc = pbf(accR.next())
                first = [True]

                def pv(out, lhsT, rhs, last=False):
                    P.mm(out, lhsT, rhs, start=first[0], stop=last, skip_group_check=True)
                    first[0] = False
                pt = PTbuf.next()
                bank = ringS.next(); ps = pbf(bank)
                P.mm(ps, IDENTb, MBA4b, start=True, stop=False)
                for i in range(4):
                    n = 4 * c + i
                    P.mm(ps[:, 128 * i:128 * (i + 1)], KT[hp, 128 * n:128 * (n + 1)], QT[hp, 128 * n:128 * (n + 1)],
                         start=False, stop=(i == 3))
                P.act(pt[:, 0:512], ps, AF.Exp, scale=0.125)
                bank = ringS.next(); ps = pbf(bank)
                P.mm(ps, IDENTb, MBB4b, start=True, stop=False)
                for i in range(4):
                    n = 4 * c + i
                    if n == 0:
                        continue
                    P.mm(ps[:, 128 * i:128 * (i + 1)], KT[hp, 128 * (n - 1):128 * n], QT[hp, 128 * n:128 * (n + 1)],
                         start=False, stop=(i == 3))
                P.act(pt[:, 512:1024], ps, AF.Exp, scale=0.125)
                for i in range(4):
                    n = 4 * c + i
                    pv(acc[:, 128 * i:128 * (i + 1)], VA[:, n, e, :], pt[:, 128 * i:128 * (i + 1)])
                    if n > 0:
                        pv(acc[:, 128 * i:128 * (i + 1)], VA[:, n - 1, e, :], pt[:, 512 + 128 * i:512 + 128 * (i + 1)])
                pt = PTbuf.next()
                bank = ringS.next(); ps = pbf(bank)
                P.mm(ps, IDENTb, MBA4b, start=True, stop=False)
                for r in range(4):
                    sl = slice(512 * c + r, 512 * (c + 1), 4)
                    P.mm(ps[:, 128 * r:128 * (r + 1)], KT[hp, sl], QT[hp, sl], start=False, stop=(r == 3))
                P.act(pt[:, 0:512], ps, AF.Exp, scale=0.125)
                if c > 0:
                    bank = ringS.next(); ps = pbf(bank)
                    P.mm(ps, IDENTb, MBB4b, start=True, stop=False)
                    for r in range(4):
                        sl = slice(512 * c + r, 512 * (c + 1), 4)
                        slk = slice(512 * (c - 1) + r, 512 * c, 4)
                        P.mm(ps[:, 128 * r:128 * (r + 1)], KT[hp, slk], QT[hp, sl], start=False, stop=(r == 3))
                    P.act(pt[:, 512:1024], ps, AF.Exp, scale=0.125)
                for r in range(4):
                    pv(acc[:, r:512:4], VA[:, 16 + 4 * r + c, e, :], pt[:, 128 * r:128 * (r + 1)])
                    if c > 0:
                        pv(acc[:, r:512:4], VA[:, 16 + 4 * r + c - 1, e, :], pt[:, 512 + 128 * r:512 + 128 * (r + 1)])
                for r in range(16):
                    pv(acc[:, r:512:16], VA[:, 32 + r, e, :], PT3[:, r, 32 * c:32 * (c + 1)], last=(r == 15))
                num = slice(64 * e, 64 * e + 64)
                den = slice(64 * (1 - e), 64 * (1 - e) + 64)
                rd = rdenb[c % 2]
                P.recip(rd[den, :], acc[den, :])
                P.tt("dve", CT[num, 4 + j, 512 * c:512 * (c + 1)], acc[num, :], rd[den, :], ALU.mult)

    PROJ_V()
    for j in range(4):
        PROJ_B(j)
        ATTN(j)
    dump("CTb", CT[:, 4:8, :].rearrange("p k t -> p (k t)"), [128, 4 * S])
    if stage == "B":
        return finish(nc, P, out_d, dump_tags)

    bF = Bump(A, SCR_LO, SCR_HI)
    h2T = A.view(32768, [8, 1024], BF16)
    woutb = A.view(32768 + 16384, [2, 8, 512], BF16)
    X1 = bF.alloc([8, D])
    wu = [bF.alloc([2, 8, 512], BF16) for _ in range(2)]
    wd = [bF.alloc([4, D], BF16) for _ in range(2)]
    aTb = [bF.alloc([4, 1024], BF16) for _ in range(2)]
    rawF = [[bF.alloc([520]) for _ in range(2)] for _ in range(2)]
    accF = [[bF.alloc([512]) for _ in range(2)] for _ in range(2)]
    sgF = [bF.alloc([512]) for _ in range(2)]
    hbF = [bF.alloc([D], BF16), accF[1][1].bitcast(BF16)]
    junkF = sgF[0].bitcast(BF16)
    ssF = bF.alloc([8]); rsF = bF.alloc([8]); ssO = bF.alloc([8]); rsO = bF.alloc([8])
    for c2 in range(2):
        P.dma("pool", woutb[:, c2, :, :], wout_v[:, :, 512 * c2:512 * (c2 + 1)], "wout%d" % c2)
    P.dma("sp", NW1, n2_d, "c_nw1")
    P.dma("sp", NW2, nf_d, "c_nw2")
    groups = [list(range(g0, min(g0 + 4, 22))) for g0 in range(0, 22, 4)]
    out_tags = ["out%d" % i for i in range(8)]
    items = [(H, gi) for H in range(2) for gi in range(len(groups))]

    def load_wu(k):
        H, gi = items[k]
        grp = groups[gi]; g0 = grp[0]; npair = len(grp); slot = k % 2
        P.dma("pool", wu[slot][:, 0, :, 0:128 * npair], wup_v[:, :, 128 * g0:128 * (g0 + npair)], "wug%d" % slot)
        P.dma("pool", wu[slot][:, 1, :, 0:128 * npair],
              wup_v[:, :, 2816 + 128 * g0:2816 + 128 * (g0 + npair)], "wuu%d" % slot)

    def load_wd(k):
        H, gi = items[k]
        grp = groups[gi]; g0 = grp[0]; npair = len(grp); slot = k % 2
        P.dma("pool", wd[slot][:, 0:npair, :], wdn_v[:, g0:g0 + npair, :], "wd%d" % slot)

    def PRO(H):
        scr = [(junkF, ssF[:, i8:i8 + 1], rsF[:, i8:i8 + 1], hbF[i8 % 2]) for i8 in range(8)]

        def st_a(i8):
            i = 8 * H + i8
            P.dma("sp", X1[:, i8, :], x_d[128 * i:128 * (i + 1), :], "xf%d" % i8)
            b0 = 2 * (i8 % 2)
            for h2 in range(2):
                for kc in range(8):
                    P.mm(pbf(b0 + h2), CT[:, kc, 128 * i:128 * (i + 1)], woutb[:, h2, kc, :],
                         start=(kc == 0), stop=(kc == 7))
            for h2 in range(2):
                P.tt("dve", X1[:, i8, 512 * h2:512 * (h2 + 1)], X1[:, i8, 512 * h2:512 * (h2 + 1)], pbf(b0 + h2), ALU.add)
            if i == 0:
                dump("X1", X1[:, 0, :], [128, D])
            rmsnorm_stage1(X1[:, i8, :], NW1, scr[i8])

        for i8 in range(9):
            if i8 < 8:
                st_a(i8)
            if i8 >= 1:
                rmsnorm_stage2(h2T, 128 * (i8 - 1), scr[i8 - 1], 4 + ((i8 - 1) % 2))

    upar = [0]

    def UGEN(k):
        H, gi = items[k]
        grp = groups[gi]; slot = k % 2; aT = aTb[k % 2]
        for p, g in enumerate(grp):
            for tb in range(2):
                par = upar[0]
                upar[0] ^= 1
                cols = slice(512 * tb, 512 * (tb + 1))
                banks = (4, 5) if par == 0 else (6, 7)
                accs = []
                for gu in range(2):
                    ps = pbf(banks[gu])
                    for kc in range(8):
                        P.mm(ps, wu[slot][:, gu, kc, 128 * p:128 * (p + 1)], h2T[:, kc, cols],
                             start=(kc == 0), stop=(kc == 7))
                    cc = g + 22 * gu
                    raw = rawF[par][gu]; acc = accF[par][gu]
                    P.copy("pool", raw[:, 0:2], HALOF[:, cc, :])
                    P.copy("act", raw[:, 2:514], ps)
                    P.copy("pool", HALOF[:, cc, :], raw[:, 512:514])
                    P.act(acc, ps, AF.Identity, scale=cwF[:, cc, 2:3])
                    P.stt(acc, raw[:, 1:513], cwF[:, cc, 1:2], acc, ALU.mult, ALU.add)
                    P.stt(acc, raw[:, 0:512], cwF[:, cc, 0:1], acc, ALU.mult, ALU.add)
                    accs.append(acc)
                sg = sgF[par]
                P.act(sg, accs[0], AF.Silu)
                P.tt("pool", aT[:, p, cols], sg, accs[1], ALU.mult)
                yield

    def DGEN(k):
        H, gi = items[k]
        grp = groups[gi]; slot = k % 2; aT = aTb[k % 2]; npair = len(grp)
        last = (gi == len(groups) - 1)
        for i8 in range(8):
            i = 8 * H + i8
            b0 = 2 * (i8 % 2)
            for h2 in range(2):
                for p in range(npair):
                    P.mm(pbf(b0 + h2), aT[:, p, 128 * i8:128 * (i8 + 1)], wd[slot][:, p, 512 * h2:512 * (h2 + 1)],
                         start=(p == 0), stop=(p == npair - 1))
            for h2 in range(2):
                P.tt("dve", X1[:, i8, 512 * h2:512 * (h2 + 1)], X1[:, i8, 512 * h2:512 * (h2 + 1)],
                     pbf(b0 + h2), ALU.add)
            if last:
                P.act(junkF, X1[:, i8, :], AF.Square, accum_out=ssO[:, i8:i8 + 1])
                P.act(rsO[:, i8:i8 + 1], ssO[:, i8:i8 + 1], AF.Ln, bias=EPS, scale=1.0 / D)
                P.act(rsO[:, i8:i8 + 1], rsO[:, i8:i8 + 1], AF.Exp, scale=-0.5)
                P.stt(X1[:, i8, :], X1[:, i8, :], rsO[:, i8:i8 + 1], NW2, ALU.mult, ALU.mult)
                P.dma("sp", out_d[128 * i:128 * (i + 1), :], X1[:, i8, :], "out%d" % i8)
            yield

    def rr(gens):
        gens = list(gens)
        live = [True] * len(gens)
        while any(live):
            for gi_, g_ in enumerate(gens):
                if live[gi_]:
                    try:
                        next(g_)
                    except StopIteration:
                        live[gi_] = False

    load_wu(0)
    prevD = None
    for k in range(len(items)):
        H, gi = items[k]
        load_wd(k)
        if k + 1 < len(items):
            load_wu(k + 1)
        if gi == 0:
            if prevD is not None:
                rr([prevD])
                prevD = None
            PRO(H)
        u = UGEN(k)
        rr([u] if prevD is None else [u, prevD])
        prevD = DGEN(k)
    rr([prevD])
    return finish(nc, P, out_d, dump_tags + out_tags)


def finish(nc, P, out_d, dump_tags):
    tags = list(dump_tags)
    P.emit(final_dma_tags=tags)
    return nc


def prep_inputs(inp):
    f = lambda a: np.ascontiguousarray(np.asarray(a, dtype=np.float32))
    x = f(inp["x"])
    rep = lambda v: np.ascontiguousarray(np.broadcast_to(f(v).reshape(1, -1), (128, f(v).size)))
    cwa = f(inp["conv_qkv_w"])[0]
    cwA = np.ascontiguousarray(cwa.T.reshape(12, 128, 4).transpose(1, 0, 2).reshape(128, 48))
    cwf = f(inp["ffn_conv_w"])[0]
    cwF = np.ascontiguousarray(cwf.T.reshape(44, 128, 3).transpose(1, 0, 2).reshape(128, 132))
    shared = {
        "w_in": f(inp["w_in"])[0], "w_out": f(inp["w_out"])[0], "w_up": f(inp["w_up"])[0],
        "w_down": f(inp["w_down"])[0],
        "n1rep": rep(inp["norm1_w"]), "n2rep": rep(inp["norm2_w"]), "nfrep": rep(inp["final_norm_w"]),
        "cwA": cwA, "cwF": cwF, "gnwrep": rep(inp["gdn_norm_w"]),
        "dtbrep": np.ascontiguousarray(np.tile(rep(inp["dt_bias"]), (1, 16))),
        "alogrep": np.ascontiguousarray(np.tile(rep(inp["a_log"]), (1, 16))),
        "consts": make_consts(),
    }
    maps = []
    for b in range(x.shape[0]):
        m = dict(shared)
        m["x"] = np.ascontiguousarray(x[b])
        maps.append(m)
    return maps


def kernel(**inputs):
    maps = prep_inputs(inputs)
    nc = build("full")
    res = run_bass_kernel_spmd(nc, maps, core_ids=list(range(8)))
    out = np.stack([np.asarray(r["out"], dtype=np.float32) for r in res.results], 0)
    return out
```

```python
import contextlib
import numpy as np
import concourse.bass as bass
import concourse.mybir as mybir
from concourse.bass_utils import run_bass_kernel_spmd

F32 = mybir.dt.float32
BF16 = mybir.dt.bfloat16
AF = mybir.ActivationFunctionType
ALU = mybir.AluOpType
AX = mybir.AxisListType

S = 2048
D = 1024
NT = 16
EPS = 1e-6
NEG = -30000.0
ENGS = ("pe", "act", "dve", "pool", "sp")


def _rect(ap):
    t = ap.tensor
    if str(ap.space) == "PSUM":
        return (t.name, 0, 128, 0, 2048)
    esz = mybir.dt.size(ap.dtype)
    pstride = 1
    for s in tuple(t.shape)[1:]:
        pstride *= s
    p0 = ap.start_partition()
    p1 = p0 + ap.partition_size()
    f0 = ap.offset - p0 * pstride
    ext = 0
    for (st, cnt) in tuple(ap.ap)[1:]:
        ext += abs(st) * (cnt - 1)
    return (t.name, p0, p1, f0 * esz, (f0 + ext + 1) * esz)


class _Op:
    __slots__ = ("eng", "fn", "idx", "is_dma", "tag", "waits", "signal", "sigval")


class Prog:
    def __init__(self, nc):
        self.nc = nc
        self.ops = {e: [] for e in ENGS}
        self.track = {}
        self.waited = {e: {} for e in ENGS}
        self.tagcount = {}

    def _add(self, eng, fn, reads, writes, is_dma=False, tag=None):
        op = _Op()
        op.eng = eng; op.fn = fn; op.is_dma = is_dma; op.tag = tag
        op.signal = False; op.sigval = None
        op.idx = len(self.ops[eng])
        op.waits = []
        deps = {}
        rrects = [_rect(a) for a in reads if a is not None and str(a.space) != "DRAM"]
        wrects = [_rect(a) for a in writes if a is not None and str(a.space) != "DRAM"]
        prects = [r for r in rrects if r[4] == 2048 and r[0].startswith("pb") and r not in wrects]
        for (nm, p0, p1, f0, f1) in rrects:
            for rec in self.track.get(nm, ()):
                if rec[5] == 1 and rec[0] < p1 and p0 < rec[1] and rec[2] < f1 and f0 < rec[3]:
                    deps[id(rec[4])] = (rec[4], True)
        for (nm, p0, p1, f0, f1) in wrects + prects:
            for rec in self.track.get(nm, ()):
                if rec[0] < p1 and p0 < rec[1] and rec[2] < f1 and f0 < rec[3]:
                    k = id(rec[4])
                    if k not in deps:
                        deps[k] = (rec[4], False)
        need = {}
        for (d, raw) in deps.values():
            if d.is_dma:
                key = ("dma", d.tag)
                val = self.tagcount[d.tag]
                if need.get(key, 0) < val:
                    need[key] = val
            else:
                if d.eng == eng and eng == "pe":
                    continue
                key = ("eng", d.eng)
                cur = need.get(key)
                if cur is None or cur.idx < d.idx:
                    need[key] = d
        w = self.waited[eng]
        for key, v in need.items():
            if key[0] == "dma":
                if w.get(key, 0) >= v:
                    continue
                w[key] = v
                op.waits.append((key, v))
            else:
                if w.get(key, -1) >= v.idx:
                    continue
                w[key] = v.idx
                v.signal = True
                op.waits.append((key, v))
        if is_dma:
            self.tagcount[tag] = self.tagcount.get(tag, 0) + 16
        for (nm, p0, p1, f0, f1) in wrects:
            lst = self.track.setdefault(nm, [])
            lst[:] = [r for r in lst if not (p0 <= r[0] and r[1] <= p1 and f0 <= r[2] and r[3] <= f1)]
            lst.append([p0, p1, f0, f1, op, 1])
        for (nm, p0, p1, f0, f1) in prects:
            self.track[nm] = [[p0, p1, f0, f1, op, 2]]
        for (nm, p0, p1, f0, f1) in rrects:
            if nm.startswith("pb"):
                continue
            lst = self.track.setdefault(nm, [])
            done = False
            for r in lst:
                if r[5] == 0 and r[4].eng == eng and (not r[4].is_dma) and (not is_dma) \
                        and r[0] == p0 and r[1] == p1 and r[2] == f0 and r[3] == f1:
                    r[4] = op
                    done = True
                    break
            if not done:
                lst.append([p0, p1, f0, f1, op, 0])
        self.ops[eng].append(op)
        return op

    def dma(self, q, out, in_, tag):
        return self._add(q, lambda e: e.dma_start(out=out, in_=in_), [in_], [out], is_dma=True, tag=tag)

    def mm(self, out, lhsT, rhs, start=True, stop=True, **kw):
        rd = [lhsT, rhs] + ([] if start else [out])
        return self._add("pe", lambda e: e.matmul(out, lhsT, rhs, start=start, stop=stop, **kw), rd, [out])

    def tr(self, out, in_, ident):
        return self._add("pe", lambda e: e.transpose(out, in_, ident), [in_, ident], [out])

    def act(self, out, in_, func, bias=None, scale=None, accum_out=None):
        kw = {}
        rd = [in_]
        if bias is not None:
            kw["bias"] = bias
            if not isinstance(bias, (int, float)):
                rd.append(bias)
        if scale is not None:
            kw["scale"] = scale
            if not isinstance(scale, (int, float)):
                rd.append(scale)
        wr = [out]
        if accum_out is not None:
            kw["accum_out"] = accum_out
            wr.append(accum_out)
        return self._add("act", lambda e: e.activation(out, in_, func, **kw), rd, wr)

    def tt(self, eng, out, in0, in1, op):
        return self._add(eng, lambda e: e.tensor_tensor(out, in0, in1, op), [in0, in1], [out])

    def ts(self, eng, out, in0, s1, s2, op0, op1=None):
        rd = [in0] + [s for s in (s1, s2) if s is not None and not isinstance(s, (int, float))]
        kw = {}
        if op1 is not None:
            kw["op1"] = op1
        return self._add(eng, lambda e: e.tensor_scalar(out, in0, s1, s2, op0, **kw), rd, [out])

    def stt(self, out, in0, scalar, in1, op0, op1):
        rd = [in0, in1] + ([] if isinstance(scalar, (int, float)) else [scalar])
        return self._add("dve", lambda e: e.scalar_tensor_tensor(out, in0, scalar, in1, op0, op1), rd, [out])

    def copy(self, eng, out, in_):
        if eng == "act":
            return self._add("act", lambda e: e.copy(out, in_), [in_], [out])
        return self._add(eng, lambda e: e.tensor_copy(out, in_), [in_], [out])

    def memset(self, eng, ap, val):
        return self._add(eng, lambda e: e.memset(ap, val), [], [ap])

    def recip(self, out, in_):
        return self._add("dve", lambda e: e.reciprocal(out, in_), [in_], [out])

    def rsum(self, out, in_):
        return self._add("dve", lambda e: e.tensor_reduce(out, in_, AX.X, ALU.add), [in_], [out])

    def emit(self, final_dma_tags=()):
        nc = self.nc
        for e in ENGS:
            c = 0
            for op in self.ops[e]:
                if op.signal and not op.is_dma:
                    c += 1
                    op.sigval = c
        with contextlib.ExitStack() as st:
            esem = {e: st.enter_context(nc.semaphore("s_" + e)) for e in ENGS}
            dsem = {t: st.enter_context(nc.semaphore("d_%d" % i)) for i, t in enumerate(self.tagcount)}
            block = st.enter_context(nc.Block())
            engobj = {"pe": block.tensor, "act": block.scalar, "dve": block.vector,
                      "pool": block.gpsimd, "sp": block.sync}

            def make(ename):
                def body(eng):
                    for op in self.ops[ename]:
                        for (key, v) in op.waits:
                            if key[0] == "dma":
                                eng.wait_ge(dsem[key[1]], v)
                            else:
                                eng.wait_ge(esem[key[1]], v.sigval)
                        ins = op.fn(eng)
                        if op.is_dma:
                            ins.then_inc(dsem[op.tag], 16)
                        elif op.signal:
                            ins.then_inc(esem[ename], 1)
                    if ename == "sp":
                        for t in final_dma_tags:
                            eng.wait_ge(dsem[t], self.tagcount[t])
                return body

            for e in ENGS:
                engobj[e](make(e))
        return nc


def _prod(shape):
    n = 1
    for s in shape:
        n *= s
    return n


class Arena:
    def __init__(self, nc, nbytes):
        self.t = nc.alloc_sbuf_tensor("arena", [128, nbytes // 4], F32)
        self.nbytes = nbytes

    def view(self, off, shape, dt):
        n = _prod(shape)
        esz = 4 if dt == F32 else 2
        assert off % 4 == 0 and (n * esz) % 4 == 0 and off + n * esz <= self.nbytes, (off, shape)
        ap = self.t[:, off // 4:(off + n * esz) // 4]
        if dt != F32:
            ap = ap.bitcast(dt)
        if len(shape) > 1:
            names = " ".join("d%d" % i for i in range(len(shape)))
            kw = {"d%d" % i: shape[i] for i in range(1, len(shape))}
            ap = ap.rearrange("p (%s) -> p %s" % (names, names), **kw)
        return ap


class Bump:
    def __init__(self, arena, lo, hi):
        self.a = arena; self.lo = lo; self.hi = hi; self.cur = lo

    def alloc(self, shape, dt=F32):
        esz = 4 if dt == F32 else 2
        n = (_prod(shape) * esz + 31) // 32 * 32
        off = self.cur
        self.cur += n
        assert self.cur <= self.hi, ("arena overflow", self.cur, self.hi)
        return self.a.view(off, shape, dt)


class Ring:
    def __init__(self, items):
        self.items = list(items); self.i = 0

    def next(self):
        x = self.items[self.i % len(self.items)]
        self.i += 1
        return x


_CNAMES = ["IDENT", "M1", "M2", "TRI", "SUF", "IND0", "IND1", "S32", "OFF", "ONES"]
_CB = {"MB64": 512, "MBA4": 512, "MBB4": 512}
NCF = 128 * len(_CNAMES)
NCONST = NCF + 512 * 3


def make_consts():
    i = np.arange(128)[:, None]
    j = np.arange(128)[None, :]
    same = (i // 64) == (j // 64)
    c = {}
    c["IDENT"] = (i == j)
    c["M1"] = (i > j)
    c["M2"] = (i <= j)
    c["TRI"] = (i <= j) & same
    c["SUF"] = (i > j) & same
    c["IND0"] = (i < 64) & (j >= 0)
    c["IND1"] = (i >= 64) & (j >= 0)
    c["S32"] = (i < j) & ((i // 32) == (j // 32))
    c["OFF"] = same & ((i % 64) < 32) & ((j % 64) >= 32)
    c["ONES"] = np.ones((128, 128), bool)
    cols = [c[n].astype(np.float32) for n in _CNAMES]
    mb64 = np.where((i <= j) & same, 0.0, NEG).astype(np.float32)
    mba = np.where(i <= j, 0.0, NEG).astype(np.float32)
    mbb = np.where(i >= j, 0.0, NEG).astype(np.float32)
    cols += [np.tile(mb64, (1, 4)), np.tile(mba, (1, 4)), np.tile(mbb, (1, 4))]
    return np.ascontiguousarray(np.concatenate(cols, 1))


def build(stage="full", dumps=()):
    nc = bass.Bass("TRN2", target_bir_lowering=False)
    P = Prog(nc)
    dumps = set(dumps)
    dump_tags = []

    def din(name, shape):
        return nc.dram_tensor(name, list(shape), F32, kind="ExternalInput").ap()

    x_d = din("x", [S, D])
    win_d = din("w_in", [D, 3592])
    wout_d = din("w_out", [D, D])
    wup_d = din("w_up", [D, 5632])
    wdn_d = din("w_down", [2816, D])
    n1_d = din("n1rep", [128, D])
    n2_d = din("n2rep", [128, D])
    nf_d = din("nfrep", [128, D])
    cwa_d = din("cwA", [128, 48])
    cwf_d = din("cwF", [128, 132])
    gnw_d = din("gnwrep", [128, 128])
    dtb_d = din("dtbrep", [128, 64])
    alog_d = din("alogrep", [128, 64])
    cst_d = din("consts", [128, NCONST])
    out_d = nc.dram_tensor("out", [S, D], F32, kind="ExternalOutput").ap()

    def dump(name, sb_ap, shape):
        if name not in dumps:
            return
        d = nc.dram_tensor("dbg_" + name, list(shape), sb_ap.dtype, kind="ExternalOutput").ap()
        tg = "dbg_" + name
        P.dma("sp", d, sb_ap, tg)
        dump_tags.append(tg)

    win_v = win_d.rearrange("(k p) c -> p k c", p=128)
    wout_v = wout_d.rearrange("(k p) c -> p k c", p=128)
    wup_v = wup_d.rearrange("(k p) c -> p k c", p=128)
    wdn_v = wdn_d.rearrange("(k p) c -> p k c", p=128)

    ARENA_BYTES = 207872
    A = Arena(nc, ARENA_BYTES)
    pb = [nc.alloc_psum_tensor("pb%d" % i, [128, 512], F32) for i in range(8)]

    def pbf(i):
        return pb[i][:]

    def pbb(i):
        return pb[i][:].bitcast(BF16)

    CT = A.view(0, [8, S], BF16)
    hT = A.view(32768, [8, S], BF16)
    pers = Bump(A, 65536, 86016)
    CF = pers.alloc([NCF])
    CBm = pers.alloc([1536 + 256], BF16)
    cwA = pers.alloc([12, 4])
    cwF = pers.alloc([44, 3])
    HALOA = pers.alloc([12, 3])
    HALOF = pers.alloc([44, 2])
    GNW = pers.alloc([128])
    NW1 = pers.alloc([D])
    NW2 = pers.alloc([D])
    SCR_LO = 86016
    SCR_HI = ARENA_BYTES

    def cf(name):
        k = _CNAMES.index(name)
        return CF[:, 128 * k:128 * (k + 1)]

    IDENT = cf("IDENT"); M1 = cf("M1"); M2 = cf("M2"); TRI = cf("TRI"); SUF = cf("SUF")
    IND0 = cf("IND0"); IND1 = cf("IND1"); S32 = cf("S32"); OFFM = cf("OFF"); ONES = cf("ONES")
    MB64b = CBm[:, 0:512]; MBA4b = CBm[:, 512:1024]; MBB4b = CBm[:, 1024:1536]
    IDENTb = CBm[:, 1536:1664]; ONESb = CBm[:, 1664:1792]

    P.dma("sp", CF, cst_d[:, 0:NCF], "c_cf")
    ctmp = A.view(SCR_LO, [1536], F32)
    P.dma("sp", ctmp, cst_d[:, NCF:NCONST], "c_tmp")
    P.copy("dve", CBm[:, 0:1536], ctmp)
    P.copy("dve", IDENTb, IDENT)
    P.copy("dve", ONESb, ONES)
    P.dma("sp", cwA.rearrange("p a b -> p (a b)"), cwa_d, "c_cwa")
    P.dma("sp", cwF.rearrange("p a b -> p (a b)"), cwf_d, "c_cwf")
    P.dma("sp", GNW, gnw_d, "c_gnw")
    P.dma("sp", NW1, n1_d, "c_nw1")
    P.memset("pool", HALOA.rearrange("p a b -> p (a b)"), 0.0)
    P.memset("pool", HALOF.rearrange("p a b -> p (a b)"), 0.0)

    def bc_last(ap, n):
        return ap.unsqueeze(2).broadcast_to([128, ap.shape[1], n])

    def bc_mid(ap, n):
        return ap.unsqueeze(1).broadcast_to([128, n, ap.shape[1]])

    def rmsnorm_stage1(src_tile, wtile, scr):
        junk, ssv, rst, hb = scr
        P.act(junk, src_tile, AF.Square, accum_out=ssv)
        P.act(rst, ssv, AF.Ln, bias=EPS, scale=1.0 / D)
        P.act(rst, rst, AF.Exp, scale=-0.5)
        P.stt(hb, src_tile, rst, wtile, ALU.mult, ALU.mult)

    def rmsnorm_stage2(dstT, col0, scr, bank):
        hb = scr[3]
        psT = pbb(bank)
        for kc in range(8):
            P.tr(psT[:, 128 * kc:128 * (kc + 1)], hb[:, 128 * kc:128 * (kc + 1)], IDENTb)
        P.copy("act", dstT[:, :, col0:col0 + 128], psT.rearrange("p (k t) -> p k t", k=8))

    bA = Bump(A, SCR_LO + 8192, SCR_HI)
    xbuf = [bA.alloc([D]) for _ in range(2)]
    hbuf = [bA.alloc([D], BF16) for _ in range(2)]
    junkA = bA.alloc([D], BF16)
    ssA = bA.alloc([NT])
    rsA = bA.alloc([NT])
    scrA = [(junkA, ssA[:, i:i + 1], rsA[:, i:i + 1], hbuf[i % 2]) for i in range(NT)]
    for i in range(NT + 1):
        if i < NT:
            xt = xbuf[i % 2]
            P.dma("sp", xt, x_d[128 * i:128 * (i + 1), :], "xa%d" % (i % 2))
            rmsnorm_stage1(xt, NW1, scrA[i])
        if i >= 1:
            rmsnorm_stage2(hT, 128 * (i - 1), scrA[i - 1], 6 + ((i - 1) % 2))
    dump("hT", hT.rearrange("p k t -> p (k t)"), [128, 8 * S])
    if stage == "A":
        return finish(nc, P, out_d, dump_tags)

    bG = Bump(A, SCR_LO, SCR_HI)
    bC = Bump(A, 16384, 32768)
    wqkva = bG.alloc([3, 8, 512], BF16)
    wba = bG.alloc([8, 8], BF16)
    wz = bC.alloc([8, 512], BF16)
    for c3 in range(3):
        P.dma("pool", wqkva[:, c3, :, :], win_v[:, :, 512 * c3:512 * (c3 + 1)], "wqkva%d" % c3)
    P.dma("pool", wba, win_v[:, :, 2048:2056], "wba")
    P.dma("pool", wz, win_v[:, :, 1536:2048], "wz")

    raw0 = bG.alloc([520]); acc0 = bG.alloc([512])
    rawb = [raw0, raw0]; accb = [acc0, acc0]
    sqb = bG.alloc([512], BF16)
    rtb = acc0
    qkvT = [dict(q=bG.alloc([4, 512], BF16), k=bG.alloc([4, 512], BF16), v=bG.alloc([4, 512], BF16))
            for _ in range(2)]
    BA = bG.alloc([NT, 8])
    sc_x = bG.alloc([64]); sc_mx = bG.alloc([64]); sc_mn = bG.alloc([64])
    dtb = bG.alloc([64]); negA = bG.alloc([64])
    gS = bG.alloc([64]); betaS = bG.alloc([64]); gamS = bG.alloc([64]); kesS = bG.alloc([64])
    gendS = bG.alloc([2, 64])
    gset = []
    for _ in range(2):
        gset.append(dict(
            G1m=bG.alloc([4, 128]),
            DECT=bG.alloc([4, 128], BF16), Uall=bG.alloc([4, 128], BF16), Eu=bG.alloc([4, 128], BF16),
            PTa=bG.alloc([4, 128], BF16), PTb=bG.alloc([4, 128], BF16),
            UL0=bG.alloc([2, 4, 128], BF16), PWA=bG.alloc([2, 4, 128], BF16), PWB=bG.alloc([2, 4, 128], BF16),
            TTb=bG.alloc([4, 128], BF16), vb=bG.alloc([4, 128], BF16), gk=bG.alloc([4, 128], BF16)))
    scanop = []
    for _ in range(3):
        scanop.append(dict(KLO=bG.alloc([4, 128], BF16), KHI=bG.alloc([4, 128], BF16), ATT=bG.alloc([4, 128], BF16),
                           QD=bG.alloc([4, 128], BF16), WKT=bG.alloc([4, 128], BF16),
                           UV=bG.alloc([4, 128], BF16)))
    Sst = bG.alloc([4, 128]); Sb = bG.alloc([4, 128], BF16); ub = bG.alloc([4, 128], BF16)
    Obuf = [bC.alloc([4, 128]) for _ in range(2)]
    sqO = bG.alloc([4, 128], BF16); oab = bG.alloc([4, 128], BF16)
    szgb = [bG.alloc([4, 512], BF16), bC.alloc([4, 512], BF16)]
    kesLo = bG.alloc([64]); kesHi = bG.alloc([64])
    ssq = bG.alloc([4]); rsq = bG.alloc([4])

    ringG = Ring([0, 1, 2, 3, 4, 5, 6, 7])

    P.dma("sp", dtb, dtb_d, "c_dtb")
    P.dma("sp", negA, alog_d, "c_alog")
    P.act(negA, negA, AF.Exp)
    P.ts("dve", negA, negA, -1.0, None, ALU.mult)
    bk = ringG.next()
    psBA = pbf(bk)[:, 0:128].rearrange("p (n c) -> p n c", c=8)
    for i in range(NT):
        for kc in range(8):
            P.mm(psBA[:, i, :], hT[:, kc, 128 * i:128 * (i + 1)], wba[:, kc, :], start=(kc == 0), stop=(kc == 7))
    P.copy("act", BA, psBA)
    x3 = sc_x.rearrange("p (n h) -> p n h", h=4)
    P.tt("dve", x3, BA[:, :, 4:8], dtb.rearrange("p (n h) -> p n h", h=4), ALU.add)
    P.ts("dve", sc_mx, sc_x, 0.0, None, ALU.max)
    P.ts("dve", sc_mn, sc_x, 0.0, None, ALU.min)
    P.tt("dve", sc_mn, sc_mn, sc_mx, ALU.subtract)
    P.act(sc_mn, sc_mn, AF.Exp)
    P.act(sc_mn, sc_mn, AF.Ln, bias=1.0)
    P.tt("dve", sc_mx, sc_mx, sc_mn, ALU.add)
    P.tt("dve", gS, sc_mx, negA, ALU.mult)
    P.act(betaS.rearrange("p (n h) -> p n h", h=4), BA[:, :, 0:4], AF.Sigmoid)
    bk = ringG.next()
    psg = pbf(bk)
    P.mm(psg[:, 0:64], TRI, gS)
    P.mm(psg[:, 64:128], SUF, gS)
    P.mm(psg[:, 128:192], IND0, gS)
    P.mm(psg[:, 192:256], IND1, gS)
    P.act(gamS, psg[:, 0:64], AF.Exp)
    P.act(kesS, psg[:, 64:128], AF.Exp)
    P.tt("dve", kesS, kesS, betaS, ALU.mult)
    P.act(gendS.rearrange("p a b -> p (a b)"), psg[:, 128:256], AF.Exp)
    dump("gS", gS, [128, 64]); dump("betaS", betaS, [128, 64]); dump("gamS", gamS, [128, 64])
    dump("kesS", kesS, [128, 64]); dump("gendS", gendS.rearrange("p a b -> p (a b)"), [128, 128])
    g3 = gS.rearrange("p (n h) -> p n h", h=4)
    beta3 = betaS.rearrange("p (n h) -> p n h", h=4)
    gam3 = gamS.rearrange("p (n h) -> p n h", h=4)
    P.ts("dve", kesLo, kesS, IND0[:, 0:1], None, ALU.mult)
    P.ts("dve", kesHi, kesS, IND1[:, 0:1], None, ALU.mult)
    kesLo3 = kesLo.rearrange("p (n h) -> p n h", h=4)
    kesHi3 = kesHi.rearrange("p (n h) -> p n h", h=4)
    gend4 = gendS.rearrange("p a (n h) -> p a n h", h=4)

    def G1(m):
        t0 = 512 * m
        o = qkvT[m % 2]
        for c in range(12):
            ps = pbf(ringG.next())
            for kc in range(8):
                P.mm(ps, wqkva[:, c // 4, kc, 128 * (c % 4):128 * (c % 4 + 1)], hT[:, kc, t0:t0 + 512], start=(kc == 0), stop=(kc == 7))
            raw = rawb[c % 2]; acc = accb[c % 2]
            P.copy("pool", raw[:, 0:3], HALOA[:, c, :])
            P.copy("act", raw[:, 3:515], ps)
            P.copy("pool", HALOA[:, c, :], raw[:, 512:515])
            P.act(acc, ps, AF.Identity, scale=cwA[:, c, 3:4])
            P.stt(acc, raw[:, 2:514], cwA[:, c, 2:3], acc, ALU.mult, ALU.add)
            P.stt(acc, raw[:, 1:513], cwA[:, c, 1:2], acc, ALU.mult, ALU.add)
            P.stt(acc, raw[:, 0:512], cwA[:, c, 0:1], acc, ALU.mult, ALU.add)
            dst = (o["q"], o["k"], o["v"])[c // 4][:, c % 4, :]
            P.act(dst, acc, AF.Silu)
            if c % 3 == 2:
                nz = 4 * m + c // 3
                psZ = pbf(ringG.next())
                for kc in range(8):
                    P.mm(psZ, hT[:, kc, 128 * nz:128 * (nz + 1)], wz[:, kc, :], start=(kc == 0), stop=(kc == 7))
                szg = szgb[m % 2][:, c // 3, :]
                P.act(szg, psZ, AF.Silu)
                P.tt("pool", szg.rearrange("p (h d) -> p h d", h=4), szg.rearrange("p (h d) -> p h d", h=4),
                     bc_mid(GNW, 4), ALU.mult)
                yield
        for c in range(8):
            dst = (o["q"], o["k"])[c // 4][:, c % 4, :]
            sc = 128.0 if c < 4 else 1.0
            P.act(sqb, dst, AF.Square)
            psn = pbf(ringG.next())
            P.mm(psn, ONESb, sqb)
            P.act(rtb, psn, AF.Ln, bias=EPS * sc, scale=sc)
            P.act(rtb, rtb, AF.Exp, scale=-0.5)
            P.tt("dve", dst, dst, rtb, ALU.mult)
            yield
        if m == 0:
            dump("qnT0", o["q"].rearrange("p h t -> p (h t)"), [128, 2048])
            dump("knT0", o["k"].rearrange("p h t -> p (h t)"), [128, 2048])
            dump("vsT0", o["v"].rearrange("p h t -> p (h t)"), [128, 2048])

    def G2(n):
        tl = 128 * (n % 4)
        so = scanop[n % 3]
        st = gset[n % 2]
        qnT = qkvT[(n // 4) % 2]["q"]; knT = qkvT[(n // 4) % 2]["k"]; vsT = qkvT[(n // 4) % 2]["v"]
        G1m = st["G1m"]; DECT = st["DECT"]; Uall = st["Uall"]; Eu = st["Eu"]
        UL0 = st["UL0"]; PWA = st["PWA"]; PWB = st["PWB"]
        TTb = st["TTb"]; vb = st["vb"]; gk = st["gk"]
        Du, Dl = UL0[:, 0], UL0[:, 1]
        gam_b = bc_last(gam3[:, n, :], 128)
        keslo_b = bc_last(kesLo3[:, n, :], 128)
        keshi_b = bc_last(kesHi3[:, n, :], 128)
        beta_b = bc_last(beta3[:, n, :], 128)
        g_b = bc_last(g3[:, n, :], 128)

        def bankf():
            return pbf(ringG.next()).rearrange("p (h d) -> p h d", h=4)

        def bankb():
            return pbb(ringG.next())[:, 0:512].rearrange("p (h d) -> p h d", h=4)

        psKV = pbb(ringG.next()).rearrange("p (a h d) -> p a h d", a=2, h=4)
        psK = psKV[:, 0]; psV = psKV[:, 1]
        for h in range(4):
            P.tr(psK[:, h, :], knT[:, h, tl:tl + 128], IDENTb)
            P.tr(psV[:, h, :], vsT[:, h, tl:tl + 128], IDENTb)
        P.tt("dve", gk, psK, gam_b, ALU.mult)
        P.tt("dve", so["KLO"], psK, keslo_b, ALU.mult)
        P.tt("dve", so["KHI"], psK, keshi_b, ALU.mult)
        P.copy("act", vb, psV)
        yield
        P.tt("pool", G1m, bc_mid(M1, 4), g_b, ALU.mult)
        psD = bankf()
        P.mm(psD, IDENTb, MB64b, start=True, stop=False)
        for h in range(4):
            P.mm(psD[:, h, :], G1m[:, h, :], M2, start=False, stop=(h == 3))
        P.act(DECT, psD, AF.Exp)
        P.tt("pool", DECT, DECT, beta_b, ALU.mult)
        yield
        dg = Uall
        P.tt("pool", dg, bc_mid(IDENT, 4), gam_b, ALU.mult)
        psG = bankf()
        for h in range(4):
            P.mm(psG[:, h, :], ONESb, dg[:, h, :])
        P.tt("dve", so["QD"], qnT[:, :, tl:tl + 128], psG, ALU.mult)
        yield
        psKK = bankf(); psQK = bankf()
        for h in range(4):
            P.mm(psKK[:, h, :], knT[:, h, tl:tl + 128], knT[:, h, tl:tl + 128])
        for h in range(4):
            P.mm(psQK[:, h, :], knT[:, h, tl:tl + 128], qnT[:, h, tl:tl + 128])
        P.tt("dve", Uall, psKK, DECT, ALU.mult)
        P.tt("dve", so["ATT"], psQK, DECT, ALU.mult)
        P.tt("pool", Du, Uall, bc_mid(S32, 4), ALU.mult)
        P.tt("pool", Eu, Uall, bc_mid(OFFM, 4), ALU.mult)
        yield
        psT = bankb()
        for h in range(4):
            P.tr(psT[:, h, :], Du[:, h, :], IDENTb)
        P.copy("act", Dl, psT)
        P.tt("pool", st["PTa"], bc_mid(IDENT, 4), Du, ALU.subtract)
        yield
        pw = [UL0, PWA, PWB, PWA, PWB]
        PT, PTn = st["PTa"], st["PTb"]
        for k in range(1, 5):
            cur = pw[k - 1]; nxt = pw[k]
            if k < 4:
                psU = bankf()
                for h in range(4):
                    P.mm(psU[:, h, :], cur[:, 1, h, :], cur[:, 0, h, :])
            psL = bankf()
            for h in range(4):
                P.mm(psL[:, h, :], cur[:, 0, h, :], cur[:, 1, h, :])
            if k > 1:
                ps3 = bankf()
                for h in range(4):
                    P.mm(ps3[:, h, :], cur[:, 1, h, :], PT[:, h, :])
            if k < 4:
                P.copy("act", nxt[:, 0], psU)
            P.copy("act", nxt[:, 1], psL)
            if k > 1:
                P.tt("dve", PTn, PT, ps3, ALU.add)
                PT, PTn = PTn, PT
            yield
        ps3 = bankf()
        for h in range(4):
            P.mm(ps3[:, h, :], PWB[:, 1, h, :], PT[:, h, :])
        P.tt("dve", PTn, PT, ps3, ALU.add)
        PT, PTn = PTn, PT
        yield
        Pm = PWA[:, 0]; XT = DECT
        p1 = bankb()
        for h in range(4):
            P.tr(p1[:, h, :], PT[:, h, :], IDENTb)
        P.copy("act", Pm, p1)
        yield
        p2 = bankf()
        for h in range(4):
            P.mm(p2[:, h, :], Eu[:, h, :], Pm[:, h, :])
        P.copy("act", XT, p2)
        yield
        p3 = bankf()
        for h in range(4):
            P.mm(p3[:, h, :], XT[:, h, :], PT[:, h, :])
        P.tt("dve", TTb, PT, p3, ALU.subtract)
        yield
        p1 = bankf(); p2 = bankf()
        for h in range(4):
            P.mm(p1[:, h, :], TTb[:, h, :], vb[:, h, :])
        for h in range(4):
            P.mm(p2[:, h, :], gk[:, h, :], TTb[:, h, :])
        P.copy("act", so["UV"], p1)
        P.copy("dve", so["WKT"], p2)
        yield

    def SCAN(n):
        so = scanop[n % 3]
        O = Obuf[n % 2]
        for half in range(2):
            r0 = 64 * half
            rs = slice(r0, r0 + 64)
            kend = so["KLO"] if half == 0 else so["KHI"]
            psA = pbf(ringG.next()).rearrange("p (h d) -> p h d", h=4)
            for h in range(4):
                P.mm(psA[:, h, :], so["WKT"][:, h, :], Sb[:, h, :])
            P.tt("dve", ub[rs], so["UV"][rs], psA[rs], ALU.subtract)
            yield
            psS = pbf(ringG.next()).rearrange("p (h d) -> p h d", h=4)
            psO = pbf(ringG.next()).rearrange("p (h d) -> p h d", h=4)
            for h in range(4):
                P.mm(psS[:, h, :], kend[:, h, :], ub[:, h, :])
            for h in range(4):
                P.mm(psO[:, h, :], so["QD"][:, h, :], Sb[:, h, :], start=True, stop=False)
                P.mm(psO[:, h, :], so["ATT"][:, h, :], ub[:, h, :], start=False, stop=True)
            for h in range(4):
                P.stt(Sst[:, h, :], Sst[:, h, :], gend4[:, half, n, h:h + 1], psS[:, h, :], ALU.mult, ALU.add)
            P.copy("act", Sb, Sst)
            P.copy("act", O[rs], psO[rs])
            yield
        if n == 0:
            dump("O0", O.rearrange("p h d -> p (h d)"), [128, 512])
        if n == 15:
            dump("O15", O.rearrange("p h d -> p (h d)"), [128, 512])
        P.act(sqO, O, AF.Square)
        P.rsum(ssq, sqO)
        P.act(rsq, ssq, AF.Ln, bias=EPS, scale=1.0 / 128)
        P.act(rsq, rsq, AF.Exp, scale=-0.5)
        szg = szgb[(n // 4) % 2][:, n % 4, :].rearrange("p (h d) -> p h d", h=4)
        P.tt("dve", sqO, O, bc_last(rsq, 128), ALU.mult)
        P.tt("dve", oab, sqO, szg, ALU.mult)
        yield
        psT = pbb(ringG.next())[:, 0:512].rearrange("p (h d) -> p h d", h=4)
        for h in range(4):
            P.tr(psT[:, h, :], oab[:, h, :], IDENTb)
        P.copy("act", CT[:, 0:4, 128 * n:128 * (n + 1)], psT)
        yield

    def advance(must, opt):
        live = [True] * len(must)
        while any(live) or any(q[1] > 0 for q in opt):
            for gi, g in enumerate(must):
                if live[gi]:
                    try:
                        next(g)
                    except StopIteration:
                        live[gi] = False
            for q in opt:
                if q[1] > 0:
                    q[1] -= 1
                    try:
                        next(q[0])
                    except StopIteration:
                        q[1] = 0
                        q[2] = True

    P.memset("dve", Sst.rearrange("p h d -> p (h d)"), 0.0)
    P.memset("pool", Sb.rearrange("p h d -> p (h d)"), 0.0)
    P.memset("pool", ub.rearrange("p h d -> p (h d)"), 0.0)
    advance([G1(0)], [])
    g2 = {0: G2(0), 1: G2(1)}
    g1bg = [G1(1), 0, False]
    advance([g2[0]], [[g2[1], 7, False]])
    for n in range(NT):
        must = [SCAN(n)]
        if n + 1 < NT:
            must.append(g2[n + 1])
        opt = []
        if n + 2 < NT:
            g2[n + 2] = G2(n + 2)
            opt.append([g2[n + 2], 7, False])
        if n % 4 == 0 and n // 4 + 1 < 4:
            if n > 0:
                g1bg = [G1(n // 4 + 1), 0, False]
            g1bg[1] = 6
            opt.append(g1bg)
        elif n % 4 == 1 and n // 4 + 1 < 4:
            g1bg[1] = 1000
            opt.append(g1bg)
        advance(must, opt)
    dump("CTa", CT[:, 0:4, :].rearrange("p k t -> p (k t)"), [128, 4 * S])
    if stage == "G":
        return finish(nc, P, out_d, dump_tags)

    bB = Bump(A, SCR_LO, SCR_HI)
    wqkvb = bB.alloc([3, 8, 512], BF16)
    for c3 in range(3):
        P.dma("pool", wqkvb[:, c3, :, :],
              win_v[:, :, 2056 + 512 * c3:2056 + 512 * (c3 + 1)], "wqkvb%d" % c3)
    QT0 = bB.alloc([S], BF16); KT0 = bB.alloc([S], BF16)
    QTb = [QT0, QT0]; KTb = [KT0, KT0]
    VAll = bB.alloc([48, 4, 192], BF16)
    PTbuf = Ring([bB.alloc([1024], BF16) for _ in range(3)])
    PT3 = bB.alloc([16, 128], BF16)
    rdenb = [bB.alloc([512]) for _ in range(2)]
    P.memset("pool", VAll[:, :, :, 64:128], 1.0)
    ringS = Ring([2, 3, 4, 5])
    ringP = Ring([6, 7])
    accR = Ring([0, 1])

    def tok_slices():
        sl = []
        for n in range(16):
            sl.append(slice(128 * n, 128 * (n + 1), 1))
        for r in range(4):
            for c in range(4):
                sl.append(slice(512 * c + r, 512 * (c + 1), 4))
        for r in range(16):
            sl.append(slice(r, S, 16))
        return sl

    TOK = tok_slices()

    def PROJ_V():
        for t in range(48):
            bank = ringP.next()
            ps = pbf(bank)
            sl = TOK[t]
            for kc in range(8):
                P.mm(ps, hT[:, kc, sl], wqkvb[:, 2, kc, :], start=(kc == 0), stop=(kc == 7))
            ps4 = ps.rearrange("p (j e d) -> p j e d", j=4, e=2)
            P.copy("act", VAll[:, t, :, 0:64], ps4[:, :, 0, :])
            P.copy("dve", VAll[:, t, :, 128:192], ps4[:, :, 1, :])

    def PROJ_B(j):
        k = j % 2
        QT = QTb[k]; KT = KTb[k]
        for (dst, cbase) in ((QT, 128 * j), (KT, 512 + 128 * j)):
            for tb in range(4):
                bank = ringP.next()
                ps = pbf(bank)
                for kc in range(8):
                    P.mm(ps, wqkvb[:, cbase // 512, kc, cbase % 512:cbase % 512 + 128], hT[:, kc, 512 * tb:512 * (tb + 1)],
                         start=(kc == 0), stop=(kc == 7))
                P.copy("dve" if tb % 2 else "act", dst[:, 512 * tb:512 * (tb + 1)], ps)

    def ATTN(j):
        k = j % 2
        QT = QTb[k]; KT = KTb[k]

        class _VA:
            def __init__(self, h):
                self.h = h

            def __getitem__(self, idx):
                e_ = self.h % 2
                return VAll[:, idx[1], self.h // 2, 64 * e_:64 * e_ + 128]

        for e in range(2):
            hp = slice(64 * e, 64 * e + 64)
            VA = _VA(2 * j + e)
            for g in range(4):
                bank = ringS.next()
                ps = pbf(bank)
                P.mm(ps, IDENTb, MBA4b, start=True, stop=False)
                for r4 in range(4):
                    r = 4 * g + r4
                    P.mm(ps[:, 128 * r4:128 * (r4 + 1)], KT[hp, r:S:16], QT[hp, r:S:16], start=False, stop=(r4 == 3))
                P.act(PT3[:, 4 * g:4 * g + 4, :], ps.rearrange("p (a d) -> p a d", a=4), AF.Exp, scale=0.125)
            for c in range(4):
                acc = pbf(accR.next())
                first = [True]

                def pv(out, lhsT, rhs, last=False):
                    P.mm(out, lhsT, rhs, start=first[0], stop=last, skip_group_check=True)
                    first[0] = False
                pt = PTbuf.next()
                bank = ringS.next(); ps = pbf(bank)
                P.mm(ps, IDENTb, MBA4b, start=True, stop=False)
                for i in range(4):
                    n = 4 * c + i
                    P.mm(ps[:, 128 * i:128 * (i + 1)], KT[hp, 128 * n:128 * (n + 1)], QT[hp, 128 * n:128 * (n + 1)],
                         start=False, stop=(i == 3))
                P.act(pt[:, 0:512], ps, AF.Exp, scale=0.125)
                bank = ringS.next(); ps = pbf(bank)
                P.mm(ps, IDENTb, MBB4b, start=True, stop=False)
                for i in range(4):
                    n = 4 * c + i
                    if n == 0:
                        continue
                    P.mm(ps[:, 128 * i:128 * (i + 1)], KT[hp, 128 * (n - 1):128 * n], QT[hp, 128 * n:128 * (n + 1)],
                         start=False, stop=(i == 3))
                P.act(pt[:, 512:1024], ps, AF.Exp, scale=0.125)
                for i in range(4):
                    n = 4 * c + i
                    pv(acc[:, 128 * i:128 * (i + 1)], VA[:, n, e, :], pt[:, 128 * i:128 * (i + 1)])
                    if n > 0:
                        pv(acc[:, 128 * i:128 * (i + 1)], VA[:, n - 1, e, :], pt[:, 512 + 128 * i:512 + 128 * (i + 1)])
                pt = PTbuf.next()
                bank = ringS.next(); ps = pbf(bank)
                P.mm(ps, IDENTb, MBA4b, start=True, stop=False)
                for r in range(4):
                    sl = slice(512 * c + r, 512 * (c + 1), 4)
                    P.mm(ps[:, 128 * r:128 * (r + 1)], KT[hp, sl], QT[hp, sl], start=False, stop=(r == 3))
                P.act(pt[:, 0:512], ps, AF.Exp, scale=0.125)
                if c > 0:
                    bank = ringS.next(); ps = pbf(bank)
                    P.mm(ps, IDENTb, MBB4b, start=True, stop=False)
                    for r in range(4):
                        sl = slice(512 * c + r, 512 * (c + 1), 4)
                        slk = slice(512 * (c - 1) + r, 512 * c, 4)
                        P.mm(ps[:, 128 * r:128 * (r + 1)], KT[hp, slk], QT[hp, sl], start=False, stop=(r == 3))
                    P.act(pt[:, 512:1024], ps, AF.Exp, scale=0.125)
                for r in range(4):
                    pv(acc[:, r:512:4], VA[:, 16 + 4 * r + c, e, :], pt[:, 128 * r:128 * (r + 1)])
                    if c > 0:
                        pv(acc[:, r:512:4], VA[:, 16 + 4 * r + c - 1, e, :], pt[:, 512 + 128 * r:512 + 128 * (r + 1)])
                for r in range(16):
                    pv(acc[:, r:512:16], VA[:, 32 + r, e, :], PT3[:, r, 32 * c:32 * (c + 1)], last=(r == 15))
                num = slice(64 * e, 64 * e + 64)
                den = slice(64 * (1 - e), 64 * (1 - e) + 64)
                rd = rdenb[c % 2]
                P.recip(rd[den, :], acc[den, :])
                P.tt("dve", CT[num, 4 + j, 512 * c:512 * (c + 1)], acc[num, :], rd[den, :], ALU.mult)

    PROJ_V()
    for j in range(4):
        PROJ_B(j)
        ATTN(j)
    dump("CTb", CT[:, 4:8, :].rearrange("p k t -> p (k t)"), [128, 4 * S])
    if stage == "B":
        return finish(nc, P, out_d, dump_tags)

    bF = Bump(A, SCR_LO, SCR_HI)
    h2T = A.view(32768, [8, 1024], BF16)
    woutb = A.view(32768 + 16384, [2, 8, 512], BF16)
    X1 = bF.alloc([8, D])
    wu = [bF.alloc([2, 8, 512], BF16) for _ in range(2)]
    wd = [bF.alloc([4, D], BF16) for _ in range(2)]
    aTb = [bF.alloc([4, 1024], BF16) for _ in range(2)]
    rawF = [[bF.alloc([520]) for _ in range(2)] for _ in range(2)]
    accF = [[bF.alloc([512]) for _ in range(2)] for _ in range(2)]
    sgF = [bF.alloc([512]) for _ in range(2)]
    hbF = [bF.alloc([D], BF16), accF[1][1].bitcast(BF16)]
    junkF = sgF[0].bitcast(BF16)
    ssF = bF.alloc([8]); rsF = bF.alloc([8]); ssO = bF.alloc([8]); rsO = bF.alloc([8])
    for c2 in range(2):
        P.dma("pool", woutb[:, c2, :, :], wout_v[:, :, 512 * c2:512 * (c2 + 1)], "wout%d" % c2)
    P.dma("sp", NW1, n2_d, "c_nw1")
    P.dma("sp", NW2, nf_d, "c_nw2")
    groups = [list(range(g0, min(g0 + 4, 22))) for g0 in range(0, 22, 4)]
    out_tags = ["out%d" % i for i in range(8)]
    items = [(H, gi) for H in range(2) for gi in range(len(groups))]

    def load_wu(k):
        H, gi = items[k]
        grp = groups[gi]; g0 = grp[0]; npair = len(grp); slot = k % 2
        P.dma("pool", wu[slot][:, 0, :, 0:128 * npair], wup_v[:, :, 128 * g0:128 * (g0 + npair)], "wug%d" % slot)
        P.dma("pool", wu[slot][:, 1, :, 0:128 * npair],
              wup_v[:, :, 2816 + 128 * g0:2816 + 128 * (g0 + npair)], "wuu%d" % slot)

    def load_wd(k):
        H, gi = items[k]
        grp = groups[gi]; g0 = grp[0]; npair = len(grp); slot = k % 2
        P.dma("pool", wd[slot][:, 0:npair, :], wdn_v[:, g0:g0 + npair, :], "wd%d" % slot)

    def PRO(H):
        scr = [(junkF, ssF[:, i8:i8 + 1], rsF[:, i8:i8 + 1], hbF[i8 % 2]) for i8 in range(8)]

        def st_a(i8):
            i = 8 * H + i8
            P.dma("sp", X1[:, i8, :], x_d[128 * i:128 * (i + 1), :], "xf%d" % i8)
            b0 = 2 * (i8 % 2)
            for h2 in range(2):
                for kc in range(8):
                    P.mm(pbf(b0 + h2), CT[:, kc, 128 * i:128 * (i + 1)], woutb[:, h2, kc, :],
                         start=(kc == 0), stop=(kc == 7))
            for h2 in range(2):
                P.tt("dve", X1[:, i8, 512 * h2:512 * (h2 + 1)], X1[:, i8, 512 * h2:512 * (h2 + 1)], pbf(b0 + h2), ALU.add)
            if i == 0:
                dump("X1", X1[:, 0, :], [128, D])
            rmsnorm_stage1(X1[:, i8, :], NW1, scr[i8])

        for i8 in range(9):
            if i8 < 8:
                st_a(i8)
            if i8 >= 1:
                rmsnorm_stage2(h2T, 128 * (i8 - 1), scr[i8 - 1], 4 + ((i8 - 1) % 2))

    upar = [0]

    def UGEN(k):
        H, gi = items[k]
        grp = groups[gi]; slot = k % 2; aT = aTb[k % 2]
        for p, g in enumerate(grp):
            for tb in range(2):
                par = upar[0]
                upar[0] ^= 1
                cols = slice(512 * tb, 512 * (tb + 1))
                banks = (4, 5) if par == 0 else (6, 7)
                accs = []
                for gu in range(2):
                    ps = pbf(banks[gu])
                    for kc in range(8):
                        P.mm(ps, wu[slot][:, gu, kc, 128 * p:128 * (p + 1)], h2T[:, kc, cols],
                             start=(kc == 0), stop=(kc == 7))
                    cc = g + 22 * gu
                    raw = rawF[par][gu]; acc = accF[par][gu]
                    P.copy("pool", raw[:, 0:2], HALOF[:, cc, :])
                    P.copy("act", raw[:, 2:514], ps)
                    P.copy("pool", HALOF[:, cc, :], raw[:, 512:514])
                    P.act(acc, ps, AF.Identity, scale=cwF[:, cc, 2:3])
                    P.stt(acc, raw[:, 1:513], cwF[:, cc, 1:2], acc, ALU.mult, ALU.add)
                    P.stt(acc, raw[:, 0:512], cwF[:, cc, 0:1], acc, ALU.mult, ALU.add)
                    accs.append(acc)
                sg = sgF[par]
                P.act(sg, accs[0], AF.Silu)
                P.tt("pool", aT[:, p, cols], sg, accs[1], ALU.mult)
                yield

    def DGEN(k):
        H, gi = items[k]
        grp = groups[gi]; slot = k % 2; aT = aTb[k % 2]; npair = len(grp)
        last = (gi == len(groups) - 1)
        for i8 in range(8):
            i = 8 * H + i8
            b0 = 2 * (i8 % 2)
            for h2 in range(2):
                for p in range(npair):
                    P.mm(pbf(b0 + h2), aT[:, p, 128 * i8:128 * (i8 + 1)], wd[slot][:, p, 512 * h2:512 * (h2 + 1)],
                         start=(p == 0), stop=(p == npair - 1))
            for h2 in range(2):
                P.tt("dve", X1[:, i8, 512 * h2:512 * (h2 + 1)], X1[:, i8, 512 * h2:512 * (h2 + 1)],
                     pbf(b0 + h2), ALU.add)
            if last:
                P.act(junkF, X1[:, i8, :], AF.Square, accum_out=ssO[:, i8:i8 + 1])
                P.act(rsO[:, i8:i8 + 1], ssO[:, i8:i8 + 1], AF.Ln, bias=EPS, scale=1.0 / D)
                P.act(rsO[:, i8:i8 + 1], rsO[:, i8:i8 + 1], AF.Exp, scale=-0.5)
                P.stt(X1[:, i8, :], X1[:, i8, :], rsO[:, i8:i8 + 1], NW2, ALU.mult, ALU.mult)
                P.dma("sp", out_d[128 * i:128 * (i + 1), :], X1[:, i8, :], "out%d" % i8)
            yield

    def rr(gens):
        gens = list(gens)
        live = [True] * len(gens)
        while any(live):
            for gi_, g_ in enumerate(gens):
                if live[gi_]:
                    try:
                        next(g_)
                    except StopIteration:
                        live[gi_] = False

    load_wu(0)
    prevD = None
    for k in range(len(items)):
        H, gi = items[k]
        load_wd(k)
        if k + 1 < len(items):
            load_wu(k + 1)
        if gi == 0:
            if prevD is not None:
                rr([prevD])
                prevD = None
            PRO(H)
        u = UGEN(k)
        rr([u] if prevD is None else [u, prevD])
        prevD = DGEN(k)
    rr([prevD])
    return finish(nc, P, out_d, dump_tags + out_tags)


def finish(nc, P, out_d, dump_tags):
    tags = list(dump_tags)
    P.emit(final_dma_tags=tags)
    return nc


def prep_inputs(inp):
    f = lambda a: np.ascontiguousarray(np.asarray(a, dtype=np.float32))
    x = f(inp["x"])
    rep = lambda v: np.ascontiguousarray(np.broadcast_to(f(v).reshape(1, -1), (128, f(v).size)))
    cwa = f(inp["conv_qkv_w"])[0]
    cwA = np.ascontiguousarray(cwa.T.reshape(12, 128, 4).transpose(1, 0, 2).reshape(128, 48))
    cwf = f(inp["ffn_conv_w"])[0]
    cwF = np.ascontiguousarray(cwf.T.reshape(44, 128, 3).transpose(1, 0, 2).reshape(128, 132))
    shared = {
        "w_in": f(inp["w_in"])[0], "w_out": f(inp["w_out"])[0], "w_up": f(inp["w_up"])[0],
        "w_down": f(inp["w_down"])[0],
        "n1rep": rep(inp["norm1_w"]), "n2rep": rep(inp["norm2_w"]), "nfrep": rep(inp["final_norm_w"]),
        "cwA": cwA, "cwF": cwF, "gnwrep": rep(inp["gdn_norm_w"]),
        "dtbrep": np.ascontiguousarray(np.tile(rep(inp["dt_bias"]), (1, 16))),
        "alogrep": np.ascontiguousarray(np.tile(rep(inp["a_log"]), (1, 16))),
        "consts": make_consts(),
    }
    maps = []
    for b in range(x.shape[0]):
        m = dict(shared)
        m["x"] = np.ascontiguousarray(x[b])
        maps.append(m)
    return maps


def kernel(**inputs):
    maps = prep_inputs(inputs)
    nc = build("full")
    res = run_bass_kernel_spmd(nc, maps, core_ids=list(range(8)))
    out = np.stack([np.asarray(r["out"], dtype=np.float32) for r in res.results], 0)
    return out
```

```python
import contextlib
import numpy as np
import concourse.bass as bass
import concourse.mybir as mybir
from concourse.bass_utils import run_bass_kernel_spmd

F32 = mybir.dt.float32
BF16 = mybir.dt.bfloat16
AF = mybir.ActivationFunctionType
ALU = mybir.AluOpType
AX = mybir.AxisListType

S = 2048
D = 1024
NT = 16
EPS = 1e-6
NEG = -30000.0
ENGS = ("pe", "act", "dve", "pool", "sp")


def _rect(ap):
    t = ap.tensor
    if str(ap.space) == "PSUM":
        return (t.name, 0, 128, 0, 2048)
    esz = mybir.dt.size(ap.dtype)
    pstride = 1
    for s in tuple(t.shape)[1:]:
        pstride *= s
    p0 = ap.start_partition()
    p1 = p0 + ap.partition_size()
    f0 = ap.offset - p0 * pstride
    ext = 0
    for (st, cnt) in tuple(ap.ap)[1:]:
        ext += abs(st) * (cnt - 1)
    return (t.name, p0, p1, f0 * esz, (f0 + ext + 1) * esz)


class _Op:
    __slots__ = ("eng", "fn", "idx", "is_dma", "tag", "waits", "signal", "sigval")


class Prog:
    def __init__(self, nc):
        self.nc = nc
        self.ops = {e: [] for e in ENGS}
        self.track = {}
        self.waited = {e: {} for e in ENGS}
        self.tagcount = {}

    def _add(self, eng, fn, reads, writes, is_dma=False, tag=None):
        op = _Op()
        op.eng = eng; op.fn = fn; op.is_dma = is_dma; op.tag = tag
        op.signal = False; op.sigval = None
        op.idx = len(self.ops[eng])
        op.waits = []
        deps = {}
        rrects = [_rect(a) for a in reads if a is not None and str(a.space) != "DRAM"]
        wrects = [_rect(a) for a in writes if a is not None and str(a.space) != "DRAM"]
        prects = [r for r in rrects if r[4] == 2048 and r[0].startswith("pb") and r not in wrects]
        for (nm, p0, p1, f0, f1) in rrects:
            for rec in self.track.get(nm, ()):
                if rec[5] == 1 and rec[0] < p1 and p0 < rec[1] and rec[2] < f1 and f0 < rec[3]:
                    deps[id(rec[4])] = (rec[4], True)
        for (nm, p0, p1, f0, f1) in wrects + prects:
            for rec in self.track.get(nm, ()):
                if rec[0] < p1 and p0 < rec[1] and rec[2] < f1 and f0 < rec[3]:
                    k = id(rec[4])
                    if k not in deps:
                        deps[k] = (rec[4], False)
        need = {}
        for (d, raw) in deps.values():
            if d.is_dma:
                key = ("dma", d.tag)
                val = self.tagcount[d.tag]
                if need.get(key, 0) < val:
                    need[key] = val
            else:
                if d.eng == eng and eng == "pe":
                    continue
                key = ("eng", d.eng)
                cur = need.get(key)
                if cur is None or cur.idx < d.idx:
                    need[key] = d
        w = self.waited[eng]
        for key, v in need.items():
            if key[0] == "dma":
                if w.get(key, 0) >= v:
                    continue
                w[key] = v
                op.waits.append((key, v))
            else:
                if w.get(key, -1) >= v.idx:
                    continue
                w[key] = v.idx
                v.signal = True
                op.waits.append((key, v))
        if is_dma:
            self.tagcount[tag] = self.tagcount.get(tag, 0) + 16
        for (nm, p0, p1, f0, f1) in wrects:
            lst = self.track.setdefault(nm, [])
            lst[:] = [r for r in lst if not (p0 <= r[0] and r[1] <= p1 and f0 <= r[2] and r[3] <= f1)]
            lst.append([p0, p1, f0, f1, op, 1])
        for (nm, p0, p1, f0, f1) in prects:
            self.track[nm] = [[p0, p1, f0, f1, op, 2]]
        for (nm, p0, p1, f0, f1) in rrects:
            if nm.startswith("pb"):
                continue
            lst = self.track.setdefault(nm, [])
            done = False
            for r in lst:
                if r[5] == 0 and r[4].eng == eng and (not r[4].is_dma) and (not is_dma) \
                        and r[0] == p0 and r[1] == p1 and r[2] == f0 and r[3] == f1:
                    r[4] = op
                    done = True
                    break
            if not done:
                lst.append([p0, p1, f0, f1, op, 0])
        self.ops[eng].append(op)
        return op

    def dma(self, q, out, in_, tag):
        return self._add(q, lambda e: e.dma_start(out=out, in_=in_), [in_], [out], is_dma=True, tag=tag)

    def mm(self, out, lhsT, rhs, start=True, stop=True, **kw):
        rd = [lhsT, rhs] + ([] if start else [out])
        return self._add("pe", lambda e: e.matmul(out, lhsT, rhs, start=start, stop=stop, **kw), rd, [out])

    def tr(self, out, in_, ident):
        return self._add("pe", lambda e: e.transpose(out, in_, ident), [in_, ident], [out])

    def act(self, out, in_, func, bias=None, scale=None, accum_out=None):
        kw = {}
        rd = [in_]
        if bias is not None:
            kw["bias"] = bias
            if not isinstance(bias, (int, float)):
                rd.append(bias)
        if scale is not None:
            kw["scale"] = scale
            if not isinstance(scale, (int, float)):
                rd.append(scale)
        wr = [out]
        if accum_out is not None:
            kw["accum_out"] = accum_out
            wr.append(accum_out)
        return self._add("act", lambda e: e.activation(out, in_, func, **kw), rd, wr)

    def tt(self, eng, out, in0, in1, op):
        return self._add(eng, lambda e: e.tensor_tensor(out, in0, in1, op), [in0, in1], [out])

    def ts(self, eng, out, in0, s1, s2, op0, op1=None):
        rd = [in0] + [s for s in (s1, s2) if s is not None and not isinstance(s, (int, float))]
        kw = {}
        if op1 is not None:
            kw["op1"] = op1
        return self._add(eng, lambda e: e.tensor_scalar(out, in0, s1, s2, op0, **kw), rd, [out])

    def stt(self, out, in0, scalar, in1, op0, op1):
        rd = [in0, in1] + ([] if isinstance(scalar, (int, float)) else [scalar])
        return self._add("dve", lambda e: e.scalar_tensor_tensor(out, in0, scalar, in1, op0, op1), rd, [out])

    def copy(self, eng, out, in_):
        if eng == "act":
            return self._add("act", lambda e: e.copy(out, in_), [in_], [out])
        return self._add(eng, lambda e: e.tensor_copy(out, in_), [in_], [out])

    def memset(self, eng, ap, val):
        return self._add(eng, lambda e: e.memset(ap, val), [], [ap])

    def recip(self, out, in_):
        return self._add("dve", lambda e: e.reciprocal(out, in_), [in_], [out])

    def rsum(self, out, in_):
        return self._add("dve", lambda e: e.tensor_reduce(out, in_, AX.X, ALU.add), [in_], [out])

    def emit(self, final_dma_tags=()):
        nc = self.nc
        for e in ENGS:
            c = 0
            for op in self.ops[e]:
                if op.signal and not op.is_dma:
                    c += 1
                    op.sigval = c
        with contextlib.ExitStack() as st:
            esem = {e: st.enter_context(nc.semaphore("s_" + e)) for e in ENGS}
            dsem = {t: st.enter_context(nc.semaphore("d_%d" % i)) for i, t in enumerate(self.tagcount)}
            block = st.enter_context(nc.Block())
            engobj = {"pe": block.tensor, "act": block.scalar, "dve": block.vector,
                      "pool": block.gpsimd, "sp": block.sync}

            def make(ename):
                def body(eng):
                    for op in self.ops[ename]:
                        for (key, v) in op.waits:
                            if key[0] == "dma":
                                eng.wait_ge(dsem[key[1]], v)
                            else:
                                eng.wait_ge(esem[key[1]], v.sigval)
                        ins = op.fn(eng)
                        if op.is_dma:
                            ins.then_inc(dsem[op.tag], 16)
                        elif op.signal:
                            ins.then_inc(esem[ename], 1)
                    if ename == "sp":
                        for t in final_dma_tags:
                            eng.wait_ge(dsem[t], self.tagcount[t])
                return body

            for e in ENGS:
                engobj[e](make(e))
        return nc


def _prod(shape):
    n = 1
    for s in shape:
        n *= s
    return n


class Arena:
    def __init__(self, nc, nbytes):
        self.t = nc.alloc_sbuf_tensor("arena", [128, nbytes // 4], F32)
        self.nbytes = nbytes

    def view(self, off, shape, dt):
        n = _prod(shape)
        esz = 4 if dt == F32 else 2
        assert off % 4 == 0 and (n * esz) % 4 == 0 and off + n * esz <= self.nbytes, (off, shape)
        ap = self.t[:, off // 4:(off + n * esz) // 4]
        if dt != F32:
            ap = ap.bitcast(dt)
        if len(shape) > 1:
            names = " ".join("d%d" % i for i in range(len(shape)))
            kw = {"d%d" % i: shape[i] for i in range(1, len(shape))}
            ap = ap.rearrange("p (%s) -> p %s" % (names, names), **kw)
        return ap


class Bump:
    def __init__(self, arena, lo, hi):
        self.a = arena; self.lo = lo; self.hi = hi; self.cur = lo

    def alloc(self, shape, dt=F32):
        esz = 4 if dt == F32 else 2
        n = (_prod(shape) * esz + 31) // 32 * 32
        off = self.cur
        self.cur += n
        assert self.cur <= self.hi, ("arena overflow", self.cur, self.hi)
        return self.a.view(off, shape, dt)


class Ring:
    def __init__(self, items):
        self.items = list(items); self.i = 0

    def next(self):
        x = self.items[self.i % len(self.items)]
        self.i += 1
        return x


_CNAMES = ["IDENT", "M1", "M2", "TRI", "SUF", "IND0", "IND1", "S32", "OFF", "ONES"]
_CB = {"MB64": 512, "MBA4": 512, "MBB4": 512}
NCF = 128 * len(_CNAMES)
NCONST = NCF + 512 * 3


def make_consts():
    i = np.arange(128)[:, None]
    j = np.arange(128)[None, :]
    same = (i // 64) == (j // 64)
    c = {}
    c["IDENT"] = (i == j)
    c["M1"] = (i > j)
    c["M2"] = (i <= j)
    c["TRI"] = (i <= j) & same
    c["SUF"] = (i > j) & same
    c["IND0"] = (i < 64) & (j >= 0)
    c["IND1"] = (i >= 64) & (j >= 0)
    c["S32"] = (i < j) & ((i // 32) == (j // 32))
    c["OFF"] = same & ((i % 64) < 32) & ((j % 64) >= 32)
    c["ONES"] = np.ones((128, 128), bool)
    cols = [c[n].astype(np.float32) for n in _CNAMES]
    mb64 = np.where((i <= j) & same, 0.0, NEG).astype(np.float32)
    mba = np.where(i <= j, 0.0, NEG).astype(np.float32)
    mbb = np.where(i >= j, 0.0, NEG).astype(np.float32)
    cols += [np.tile(mb64, (1, 4)), np.tile(mba, (1, 4)), np.tile(mbb, (1, 4))]
    return np.ascontiguousarray(np.concatenate(cols, 1))


def build(stage="full", dumps=()):
    nc = bass.Bass("TRN2", target_bir_lowering=False)
    P = Prog(nc)
    dumps = set(dumps)
    dump_tags = []

    def din(name, shape):
        return nc.dram_tensor(name, list(shape), F32, kind="ExternalInput").ap()

    x_d = din("x", [S, D])
    win_d = din("w_in", [D, 3592])
    wout_d = din("w_out", [D, D])
    wup_d = din("w_up", [D, 5632])
    wdn_d = din("w_down", [2816, D])
    n1_d = din("n1rep", [128, D])
    n2_d = din("n2rep", [128, D])
    nf_d = din("nfrep", [128, D])
    cwa_d = din("cwA", [128, 48])
    cwf_d = din("cwF", [128, 132])
    gnw_d = din("gnwrep", [128, 128])
    dtb_d = din("dtbrep", [128, 64])
    alog_d = din("alogrep", [128, 64])
    cst_d = din("consts", [128, NCONST])
    out_d = nc.dram_tensor("out", [S, D], F32, kind="ExternalOutput").ap()

    def dump(name, sb_ap, shape):
        if name not in dumps:
            return
        d = nc.dram_tensor("dbg_" + name, list(shape), sb_ap.dtype, kind="ExternalOutput").ap()
        tg = "dbg_" + name
        P.dma("sp", d, sb_ap, tg)
        dump_tags.append(tg)

    win_v = win_d.rearrange("(k p) c -> p k c", p=128)
    wout_v = wout_d.rearrange("(k p) c -> p k c", p=128)
    wup_v = wup_d.rearrange("(k p) c -> p k c", p=128)
    wdn_v = wdn_d.rearrange("(k p) c -> p k c", p=128)

    ARENA_BYTES = 207872
    A = Arena(nc, ARENA_BYTES)
    pb = [nc.alloc_psum_tensor("pb%d" % i, [128, 512], F32) for i in range(8)]

    def pbf(i):
        return pb[i][:]

    def pbb(i):
        return pb[i][:].bitcast(BF16)

    CT = A.view(0, [8, S], BF16)
    hT = A.view(32768, [8, S], BF16)
    pers = Bump(A, 65536, 86016)
    CF = pers.alloc([NCF])
    CBm = pers.alloc([1536 + 256], BF16)
    cwA = pers.alloc([12, 4])
    cwF = pers.alloc([44, 3])
    HALOA = pers.alloc([12, 3])
    HALOF = pers.alloc([44, 2])
    GNW = pers.alloc([128])
    NW1 = pers.alloc([D])
    NW2 = pers.alloc([D])
    SCR_LO = 86016
    SCR_HI = ARENA_BYTES

    def cf(name):
        k = _CNAMES.index(name)
        return CF[:, 128 * k:128 * (k + 1)]

    IDENT = cf("IDENT"); M1 = cf("M1"); M2 = cf("M2"); TRI = cf("TRI"); SUF = cf("SUF")
    IND0 = cf("IND0"); IND1 = cf("IND1"); S32 = cf("S32"); OFFM = cf("OFF"); ONES = cf("ONES")
    MB64b = CBm[:, 0:512]; MBA4b = CBm[:, 512:1024]; MBB4b = CBm[:, 1024:1536]
    IDENTb = CBm[:, 1536:1664]; ONESb = CBm[:, 1664:1792]

    P.dma("sp", CF, cst_d[:, 0:NCF], "c_cf")
    ctmp = A.view(SCR_LO, [1536], F32)
    P.dma("sp", ctmp, cst_d[:, NCF:NCONST], "c_tmp")
    P.copy("dve", CBm[:, 0:1536], ctmp)
    P.copy("dve", IDENTb, IDENT)
    P.copy("dve", ONESb, ONES)
    P.dma("sp", cwA.rearrange("p a b -> p (a b)"), cwa_d, "c_cwa")
    P.dma("sp", cwF.rearrange("p a b -> p (a b)"), cwf_d, "c_cwf")
    P.dma("sp", GNW, gnw_d, "c_gnw")
    P.dma("sp", NW1, n1_d, "c_nw1")
    P.memset("pool", HALOA.rearrange("p a b -> p (a b)"), 0.0)
    P.memset("pool", HALOF.rearrange("p a b -> p (a b)"), 0.0)

    def bc_last(ap, n):
        return ap.unsqueeze(2).broadcast_to([128, ap.shape[1], n])

    def bc_mid(ap, n):
        return ap.unsqueeze(1).broadcast_to([128, n, ap.shape[1]])

    def rmsnorm_stage1(src_tile, wtile, scr):
        junk, ssv, rst, hb = scr
        P.act(junk, src_tile, AF.Square, accum_out=ssv)
        P.act(rst, ssv, AF.Ln, bias=EPS, scale=1.0 / D)
        P.act(rst, rst, AF.Exp, scale=-0.5)
        P.stt(hb, src_tile, rst, wtile, ALU.mult, ALU.mult)

    def rmsnorm_stage1a(src_tile, scr):
        junk, ssv, rst, hb = scr
        P.act(junk, src_tile, AF.Square, accum_out=ssv)
        P.act(rst, ssv, AF.Ln, bias=EPS, scale=1.0 / D)
        P.act(rst, rst, AF.Exp, scale=-0.5)

    def rmsnorm_stage1b(src_tile, wtile, scr):
        junk, ssv, rst, hb = scr
        P.stt(hb, src_tile, rst, wtile, ALU.mult, ALU.mult)

    def rmsnorm_stage2(dstT, col0, scr, bank):
        hb = scr[3]
        psT = pbb(bank)
        for kc in range(8):
            P.tr(psT[:, 128 * kc:128 * (kc + 1)], hb[:, 128 * kc:128 * (kc + 1)], IDENTb)
        P.copy("act", dstT[:, :, col0:col0 + 128], psT.rearrange("p (k t) -> p k t", k=8))

    bA = Bump(A, SCR_LO + 8192, SCR_HI)
    xbuf = [bA.alloc([D]) for _ in range(3)]
    hbuf = [bA.alloc([D], BF16) for _ in range(2)]
    junkA = bA.alloc([D], BF16)
    ssA = bA.alloc([NT])
    rsA = bA.alloc([NT])
    scrA = [(junkA, ssA[:, i:i + 1], rsA[:, i:i + 1], hbuf[i % 2]) for i in range(NT)]
    for i in range(NT + 2):
        if i < NT:
            xt = xbuf[i % 3]
            P.dma("sp", xt, x_d[128 * i:128 * (i + 1), :], "xa%d" % (i % 3))
            rmsnorm_stage1a(xt, scrA[i])
        if 1 <= i <= NT:
            rmsnorm_stage1b(xbuf[(i - 1) % 3], NW1, scrA[i - 1])
        if i >= 2:
            rmsnorm_stage2(hT, 128 * (i - 2), scrA[i - 2], 6 + (i % 2))
    dump("hT", hT.rearrange("p k t -> p (k t)"), [128, 8 * S])
    if stage == "A":
        return finish(nc, P, out_d, dump_tags)

    bG = Bump(A, SCR_LO, SCR_HI)
    bC = Bump(A, 16384, 32768)
    wqkva = bG.alloc([3, 8, 512], BF16)
    wba = bG.alloc([8, 8], BF16)
    wz = bC.alloc([8, 512], BF16)
    for c3 in range(3):
        P.dma("pool", wqkva[:, c3, :, :], win_v[:, :, 512 * c3:512 * (c3 + 1)], "wqkva%d" % c3)
    P.dma("pool", wba, win_v[:, :, 2048:2056], "wba")
    P.dma("pool", wz, win_v[:, :, 1536:2048], "wz")

    raw0 = bG.alloc([520]); acc0 = bG.alloc([512])
    rawb = [raw0, raw0]; accb = [acc0, acc0]
    sqb = bG.alloc([512], BF16)
    rtb = acc0
    qkvT = [dict(q=bG.alloc([4, 512], BF16), k=bG.alloc([4, 512], BF16), v=bG.alloc([4, 512], BF16))
            for _ in range(2)]
    BA = bG.alloc([NT, 8])
    sc_x = bG.alloc([64]); sc_mx = bG.alloc([64]); sc_mn = bG.alloc([64])
    dtb = bG.alloc([64]); negA = bG.alloc([64])
    gS = bG.alloc([64]); betaS = bG.alloc([64]); gamS = bG.alloc([64]); kesS = bG.alloc([64])
    gendS = bG.alloc([2, 64])
    gset = []
    for _ in range(2):
        gset.append(dict(
            G1m=bG.alloc([4, 128]),
            DECT=bG.alloc([4, 128], BF16), Uall=bG.alloc([4, 128], BF16), Eu=bG.alloc([4, 128], BF16),
            PTa=bG.alloc([4, 128], BF16), PTb=bG.alloc([4, 128], BF16),
            UL0=bG.alloc([2, 4, 128], BF16), PWA=bG.alloc([2, 4, 128], BF16), PWB=bG.alloc([2, 4, 128], BF16),
            TTb=bG.alloc([4, 128], BF16), vb=bG.alloc([4, 128], BF16), gk=bG.alloc([4, 128], BF16)))
    scanop = []
    for _ in range(3):
        scanop.append(dict(KLO=bG.alloc([4, 128], BF16), KHI=bG.alloc([4, 128], BF16), ATT=bG.alloc([4, 128], BF16),
                           QD=bG.alloc([4, 128], BF16), WKT=bG.alloc([4, 128], BF16),
                           UV=bG.alloc([4, 128], BF16)))
    Sst = bG.alloc([4, 128]); Sb = bG.alloc([4, 128], BF16); ub = bG.alloc([4, 128], BF16)
    Obuf = [bC.alloc([4, 128]) for _ in range(2)]
    sqO = bG.alloc([4, 128], BF16); oab = bG.alloc([4, 128], BF16)
    szgb = [bG.alloc([4, 512], BF16), bC.alloc([4, 512], BF16)]
    kesLo = bG.alloc([64]); kesHi = bG.alloc([64])
    ssq = bG.alloc([4]); rsq = bG.alloc([4])

    ringG = Ring([0, 1, 2, 3, 4, 5, 6, 7])

    P.dma("sp", dtb, dtb_d, "c_dtb")
    P.dma("sp", negA, alog_d, "c_alog")
    P.act(negA, negA, AF.Exp)
    P.ts("dve", negA, negA, -1.0, None, ALU.mult)
    bk = ringG.next()
    psBA = pbf(bk)[:, 0:128].rearrange("p (n c) -> p n c", c=8)
    for i in range(NT):
        for kc in range(8):
            P.mm(psBA[:, i, :], hT[:, kc, 128 * i:128 * (i + 1)], wba[:, kc, :], start=(kc == 0), stop=(kc == 7))
    P.copy("act", BA, psBA)
    x3 = sc_x.rearrange("p (n h) -> p n h", h=4)
    P.tt("dve", x3, BA[:, :, 4:8], dtb.rearrange("p (n h) -> p n h", h=4), ALU.add)
    P.ts("dve", sc_mx, sc_x, 0.0, None, ALU.max)
    P.ts("dve", sc_mn, sc_x, 0.0, None, ALU.min)
    P.tt("dve", sc_mn, sc_mn, sc_mx, ALU.subtract)
    P.act(sc_mn, sc_mn, AF.Exp)
    P.act(sc_mn, sc_mn, AF.Ln, bias=1.0)
    P.tt("dve", sc_mx, sc_mx, sc_mn, ALU.add)
    P.tt("dve", gS, sc_mx, negA, ALU.mult)
    P.act(betaS.rearrange("p (n h) -> p n h", h=4), BA[:, :, 0:4], AF.Sigmoid)
    bk = ringG.next()
    psg = pbf(bk)
    P.mm(psg[:, 0:64], TRI, gS)
    P.mm(psg[:, 64:128], SUF, gS)
    P.mm(psg[:, 128:192], IND0, gS)
    P.mm(psg[:, 192:256], IND1, gS)
    P.act(gamS, psg[:, 0:64], AF.Exp)
    P.act(kesS, psg[:, 64:128], AF.Exp)
    P.tt("dve", kesS, kesS, betaS, ALU.mult)
    P.act(gendS.rearrange("p a b -> p (a b)"), psg[:, 128:256], AF.Exp)
    dump("gS", gS, [128, 64]); dump("betaS", betaS, [128, 64]); dump("gamS", gamS, [128, 64])
    dump("kesS", kesS, [128, 64]); dump("gendS", gendS.rearrange("p a b -> p (a b)"), [128, 128])
    g3 = gS.rearrange("p (n h) -> p n h", h=4)
    beta3 = betaS.rearrange("p (n h) -> p n h", h=4)
    gam3 = gamS.rearrange("p (n h) -> p n h", h=4)
    P.ts("dve", kesLo, kesS, IND0[:, 0:1], None, ALU.mult)
    P.ts("dve", kesHi, kesS, IND1[:, 0:1], None, ALU.mult)
    kesLo3 = kesLo.rearrange("p (n h) -> p n h", h=4)
    kesHi3 = kesHi.rearrange("p (n h) -> p n h", h=4)
    gend4 = gendS.rearrange("p a (n h) -> p a n h", h=4)

    def G1(m):
        t0 = 512 * m
        o = qkvT[m % 2]
        for c in range(12):
            ps = pbf(ringG.next())
            for kc in range(8):
                P.mm(ps, wqkva[:, c // 4, kc, 128 * (c % 4):128 * (c % 4 + 1)], hT[:, kc, t0:t0 + 512], start=(kc == 0), stop=(kc == 7))
            raw = rawb[c % 2]; acc = accb[c % 2]
            P.copy("pool", raw[:, 0:3], HALOA[:, c, :])
            P.copy("act", raw[:, 3:515], ps)
            P.copy("pool", HALOA[:, c, :], raw[:, 512:515])
            P.act(acc, ps, AF.Identity, scale=cwA[:, c, 3:4])
            P.stt(acc, raw[:, 2:514], cwA[:, c, 2:3], acc, ALU.mult, ALU.add)
            P.stt(acc, raw[:, 1:513], cwA[:, c, 1:2], acc, ALU.mult, ALU.add)
            P.stt(acc, raw[:, 0:512], cwA[:, c, 0:1], acc, ALU.mult, ALU.add)
            dst = (o["q"], o["k"], o["v"])[c // 4][:, c % 4, :]
            P.act(dst, acc, AF.Silu)
            if c % 3 == 2:
                nz = 4 * m + c // 3
                psZ = pbf(ringG.next())
                for kc in range(8):
                    P.mm(psZ, hT[:, kc, 128 * nz:128 * (nz + 1)], wz[:, kc, :], start=(kc == 0), stop=(kc == 7))
                szg = szgb[m % 2][:, c // 3, :]
                P.act(szg, psZ, AF.Silu)
                P.tt("pool", szg.rearrange("p (h d) -> p h d", h=4), szg.rearrange("p (h d) -> p h d", h=4),
                     bc_mid(GNW, 4), ALU.mult)
                yield
        for c in range(8):
            dst = (o["q"], o["k"])[c // 4][:, c % 4, :]
            sc = 128.0 if c < 4 else 1.0
            P.act(sqb, dst, AF.Square)
            psn = pbf(ringG.next())
            P.mm(psn, ONESb, sqb)
            P.act(rtb, psn, AF.Ln, bias=EPS * sc, scale=sc)
            P.act(rtb, rtb, AF.Exp, scale=-0.5)
            P.tt("dve", dst, dst, rtb, ALU.mult)
            yield
        if m == 0:
            dump("qnT0", o["q"].rearrange("p h t -> p (h t)"), [128, 2048])
            dump("knT0", o["k"].rearrange("p h t -> p (h t)"), [128, 2048])
            dump("vsT0", o["v"].rearrange("p h t -> p (h t)"), [128, 2048])

    def G2(n):
        tl = 128 * (n % 4)
        so = scanop[n % 3]
        st = gset[n % 2]
        qnT = qkvT[(n // 4) % 2]["q"]; knT = qkvT[(n // 4) % 2]["k"]; vsT = qkvT[(n // 4) % 2]["v"]
        G1m = st["G1m"]; DECT = st["DECT"]; Uall = st["Uall"]; Eu = st["Eu"]
        UL0 = st["UL0"]; PWA = st["PWA"]; PWB = st["PWB"]
        TTb = st["TTb"]; vb = st["vb"]; gk = st["gk"]
        Du, Dl = UL0[:, 0], UL0[:, 1]
        gam_b = bc_last(gam3[:, n, :], 128)
        keslo_b = bc_last(kesLo3[:, n, :], 128)
        keshi_b = bc_last(kesHi3[:, n, :], 128)
        beta_b = bc_last(beta3[:, n, :], 128)
        g_b = bc_last(g3[:, n, :], 128)

        def bankf():
            return pbf(ringG.next()).rearrange("p (h d) -> p h d", h=4)

        def bankb():
            return pbb(ringG.next())[:, 0:512].rearrange("p (h d) -> p h d", h=4)

        psKV = pbb(ringG.next()).rearrange("p (a h d) -> p a h d", a=2, h=4)
        psK = psKV[:, 0]; psV = psKV[:, 1]
        for h in range(4):
            P.tr(psK[:, h, :], knT[:, h, tl:tl + 128], IDENTb)
            P.tr(psV[:, h, :], vsT[:, h, tl:tl + 128], IDENTb)
        P.tt("dve", gk, psK, gam_b, ALU.mult)
        P.tt("dve", so["KLO"], psK, keslo_b, ALU.mult)
        P.tt("dve", so["KHI"], psK, keshi_b, ALU.mult)
        P.copy("act", vb, psV)
        yield
        P.tt("pool", G1m, bc_mid(M1, 4), g_b, ALU.mult)
        psD = bankf()
        P.mm(psD, IDENTb, MB64b, start=True, stop=False)
        for h in range(4):
            P.mm(psD[:, h, :], G1m[:, h, :], M2, start=False, stop=(h == 3))
        P.act(DECT, psD, AF.Exp)
        P.tt("pool", DECT, DECT, beta_b, ALU.mult)
        yield
        dg = Uall
        P.tt("pool", dg, bc_mid(IDENT, 4), gam_b, ALU.mult)
        psG = bankf()
        for h in range(4):
            P.mm(psG[:, h, :], ONESb, dg[:, h, :])
        P.tt("dve", so["QD"], qnT[:, :, tl:tl + 128], psG, ALU.mult)
        yield
        psKK = bankf(); psQK = bankf()
        for h in range(4):
            P.mm(psKK[:, h, :], knT[:, h, tl:tl + 128], knT[:, h, tl:tl + 128])
        for h in range(4):
            P.mm(psQK[:, h, :], knT[:, h, tl:tl + 128], qnT[:, h, tl:tl + 128])
        P.tt("dve", Uall, psKK, DECT, ALU.mult)
        P.tt("dve", so["ATT"], psQK, DECT, ALU.mult)
        P.tt("pool", Du, Uall, bc_mid(S32, 4), ALU.mult)
        P.tt("pool", Eu, Uall, bc_mid(OFFM, 4), ALU.mult)
        yield
        psT = bankb()
        for h in range(4):
            P.tr(psT[:, h, :], Du[:, h, :], IDENTb)
        P.copy("act", Dl, psT)
        P.tt("pool", st["PTa"], bc_mid(IDENT, 4), Du, ALU.subtract)
        yield
        pw = [UL0, PWA, PWB, PWA, PWB]
        PT, PTn = st["PTa"], st["PTb"]
        for k in range(1, 5):
            cur = pw[k - 1]; nxt = pw[k]
            if k < 4:
                psU = bankf()
                for h in range(4):
                    P.mm(psU[:, h, :], cur[:, 1, h, :], cur[:, 0, h, :])
            psL = bankf()
            for h in range(4):
                P.mm(psL[:, h, :], cur[:, 0, h, :], cur[:, 1, h, :])
            if k > 1:
                ps3 = bankf()
                for h in range(4):
                    P.mm(ps3[:, h, :], cur[:, 1, h, :], PT[:, h, :])
            if k < 4:
                P.copy("act", nxt[:, 0], psU)
            P.copy("act", nxt[:, 1], psL)
            if k > 1:
                P.tt("dve", PTn, PT, ps3, ALU.add)
                PT, PTn = PTn, PT
            yield
        ps3 = bankf()
        for h in range(4):
            P.mm(ps3[:, h, :], PWB[:, 1, h, :], PT[:, h, :])
        P.tt("dve", PTn, PT, ps3, ALU.add)
        PT, PTn = PTn, PT
        yield
        Pm = PWA[:, 0]; XT = DECT
        p1 = bankb()
        for h in range(4):
            P.tr(p1[:, h, :], PT[:, h, :], IDENTb)
        P.copy("act", Pm, p1)
        yield
        p2 = bankf()
        for h in range(4):
            P.mm(p2[:, h, :], Eu[:, h, :], Pm[:, h, :])
        P.copy("act", XT, p2)
        yield
        p3 = bankf()
        for h in range(4):
            P.mm(p3[:, h, :], XT[:, h, :], PT[:, h, :])
        P.tt("dve", TTb, PT, p3, ALU.subtract)
        yield
        p1 = bankf(); p2 = bankf()
        for h in range(4):
            P.mm(p1[:, h, :], TTb[:, h, :], vb[:, h, :])
        for h in range(4):
            P.mm(p2[:, h, :], gk[:, h, :], TTb[:, h, :])
        P.copy("act", so["UV"], p1)
        P.copy("dve", so["WKT"], p2)
        yield

    def SCAN(n):
        so = scanop[n % 3]
        O = Obuf[n % 2]
        for half in range(2):
            r0 = 64 * half
            rs = slice(r0, r0 + 64)
            kend = so["KLO"] if half == 0 else so["KHI"]
            psA = pbf(ringG.next()).rearrange("p (h d) -> p h d", h=4)
            for h in range(4):
                P.mm(psA[:, h, :], so["WKT"][:, h, :], Sb[:, h, :])
            P.tt("dve", ub[rs], so["UV"][rs], psA[rs], ALU.subtract)
            yield
            psS = pbf(ringG.next()).rearrange("p (h d) -> p h d", h=4)
            psO = pbf(ringG.next()).rearrange("p (h d) -> p h d", h=4)
            for h in range(4):
                P.mm(psS[:, h, :], kend[:, h, :], ub[:, h, :])
            for h in range(4):
                P.mm(psO[:, h, :], so["QD"][:, h, :], Sb[:, h, :], start=True, stop=False)
                P.mm(psO[:, h, :], so["ATT"][:, h, :], ub[:, h, :], start=False, stop=True)
            for h in range(4):
                P.stt(Sst[:, h, :], Sst[:, h, :], gend4[:, half, n, h:h + 1], psS[:, h, :], ALU.mult, ALU.add)
            P.copy("act", Sb, Sst)
            P.copy("act", O[rs], psO[rs])
            yield
        if n == 0:
            dump("O0", O.rearrange("p h d -> p (h d)"), [128, 512])
        if n == 15:
            dump("O15", O.rearrange("p h d -> p (h d)"), [128, 512])
        P.act(sqO, O, AF.Square)
        P.rsum(ssq, sqO)
        P.act(rsq, ssq, AF.Ln, bias=EPS, scale=1.0 / 128)
        P.act(rsq, rsq, AF.Exp, scale=-0.5)
        szg = szgb[(n // 4) % 2][:, n % 4, :].rearrange("p (h d) -> p h d", h=4)
        P.tt("dve", sqO, O, bc_last(rsq, 128), ALU.mult)
        P.tt("dve", oab, sqO, szg, ALU.mult)
        yield
        psT = pbb(ringG.next())[:, 0:512].rearrange("p (h d) -> p h d", h=4)
        for h in range(4):
            P.tr(psT[:, h, :], oab[:, h, :], IDENTb)
        P.copy("act", CT[:, 0:4, 128 * n:128 * (n + 1)], psT)
        yield

    def advance(must, opt):
        live = [True] * len(must)
        while any(live) or any(q[1] > 0 for q in opt):
            for gi, g in enumerate(must):
                if live[gi]:
                    try:
                        next(g)
                    except StopIteration:
                        live[gi] = False
            for q in opt:
                if q[1] > 0:
                    q[1] -= 1
                    try:
                        next(q[0])
                    except StopIteration:
                        q[1] = 0
                        q[2] = True

    P.memset("dve", Sst.rearrange("p h d -> p (h d)"), 0.0)
    P.memset("pool", Sb.rearrange("p h d -> p (h d)"), 0.0)
    P.memset("pool", ub.rearrange("p h d -> p (h d)"), 0.0)
    advance([G1(0)], [])
    g2 = {0: G2(0), 1: G2(1)}
    g1bg = [G1(1), 0, False]
    advance([g2[0]], [[g2[1], 7, False]])
    for n in range(NT):
        must = [SCAN(n)]
        if n + 1 < NT:
            must.append(g2[n + 1])
        opt = []
        if n + 2 < NT:
            g2[n + 2] = G2(n + 2)
            opt.append([g2[n + 2], 7, False])
        if n % 4 == 0 and n // 4 + 1 < 4:
            if n > 0:
                g1bg = [G1(n // 4 + 1), 0, False]
            g1bg[1] = 6
            opt.append(g1bg)
        elif n % 4 == 1 and n // 4 + 1 < 4:
            g1bg[1] = 1000
            opt.append(g1bg)
        advance(must, opt)
    dump("CTa", CT[:, 0:4, :].rearrange("p k t -> p (k t)"), [128, 4 * S])
    if stage == "G":
        return finish(nc, P, out_d, dump_tags)

    bB = Bump(A, SCR_LO, SCR_HI)
    wqkvb = bB.alloc([3, 8, 512], BF16)
    for c3 in range(3):
        P.dma("pool", wqkvb[:, c3, :, :],
              win_v[:, :, 2056 + 512 * c3:2056 + 512 * (c3 + 1)], "wqkvb%d" % c3)
    QT0 = bB.alloc([S], BF16)
    KTz = [bB.alloc([S], BF16) for _ in range(2)]
    QTb = [QT0, QT0]
    P.memset("pool", KTz[0][64:128, :], 0.0)
    P.memset("pool", KTz[1][0:64, :], 0.0)
    VAll = bB.alloc([48, 4, 192], BF16)
    PTbuf = Ring([bB.alloc([1024], BF16) for _ in range(2)])
    PT3 = bB.alloc([16, 128], BF16)
    rden0 = bB.alloc([512])
    rdenb = [rden0, rden0]
    P.memset("pool", VAll[:, :, :, 64:128], 1.0)
    ringS = Ring([2, 3, 4, 5])
    ringP = Ring([6, 7])
    accR = Ring([0, 1])

    def tok_slices():
        sl = []
        for n in range(16):
            sl.append(slice(128 * n, 128 * (n + 1), 1))
        for r in range(4):
            for c in range(4):
                sl.append(slice(512 * c + r, 512 * (c + 1), 4))
        for r in range(16):
            sl.append(slice(r, S, 16))
        return sl

    TOK = tok_slices()

    def PROJ_V():
        for t in range(48):
            bank = ringP.next()
            ps = pbf(bank)
            sl = TOK[t]
            for kc in range(8):
                P.mm(ps, hT[:, kc, sl], wqkvb[:, 2, kc, :], start=(kc == 0), stop=(kc == 7))
            ps4 = ps.rearrange("p (j e d) -> p j e d", j=4, e=2)
            P.copy("act", VAll[:, t, :, 0:64], ps4[:, :, 0, :])
            P.copy("dve", VAll[:, t, :, 128:192], ps4[:, :, 1, :])

    def PROJ_B(j):
        k = j % 2
        QT = QTb[k]
        for (dst, cbase) in ((QT, 128 * j), (None, 512 + 128 * j)):
            for tb in range(4):
                bank = ringP.next()
                ps = pbf(bank)
                for kc in range(8):
                    P.mm(ps, wqkvb[:, cbase // 512, kc, cbase % 512:cbase % 512 + 128], hT[:, kc, 512 * tb:512 * (tb + 1)],
                         start=(kc == 0), stop=(kc == 7))
                cs = slice(512 * tb, 512 * (tb + 1))
                if dst is not None:
                    P.copy("dve" if tb % 2 else "act", dst[:, cs], ps)
                else:
                    P.copy("act", KTz[0][0:64, cs], ps[0:64, :])
                    P.copy("dve", KTz[1][64:128, cs], ps[64:128, :])

    def ATTN(j):
        k = j % 2
        QT = QTb[k]

        class _VA:
            def __init__(self, h):
                self.h = h

            def __getitem__(self, idx):
                e_ = self.h % 2
                return VAll[:, idx[1], self.h // 2, 64 * e_:64 * e_ + 128]

        for e in range(2):
            hp = slice(64 * e, 64 * e + 64)
            KT = KTz[e]
            fp = slice(0, 128)
            VA = _VA(2 * j + e)
            for g in range(4):
                bank = ringS.next()
                ps = pbf(bank)
                P.mm(ps, IDENTb, MBA4b, start=True, stop=False)
                for r4 in range(4):
                    r = 4 * g + r4
                    P.mm(ps[:, 128 * r4:128 * (r4 + 1)], KT[fp, r:S:16], QT[fp, r:S:16], start=False, stop=(r4 == 3))
                P.act(PT3[:, 4 * g:4 * g + 4, :], ps.rearrange("p (a d) -> p a d", a=4), AF.Exp, scale=0.125)
            for c in range(4):
                acc = pbf(accR.next())
                first = [True]

                def pv(out, lhsT, rhs, last=False):
                    P.mm(out, lhsT, rhs, start=first[0], stop=last, skip_group_check=True)
                    first[0] = False
                pt = PTbuf.next()
                bank = ringS.next(); ps = pbf(bank)
                P.mm(ps, IDENTb, MBA4b, start=True, stop=False)
                for i in range(4):
                    n = 4 * c + i
                    P.mm(ps[:, 128 * i:128 * (i + 1)], KT[fp, 128 * n:128 * (n + 1)], QT[fp, 128 * n:128 * (n + 1)],
                         start=False, stop=(i == 3))
                P.act(pt[:, 0:512], ps, AF.Exp, scale=0.125)
                bank = ringS.next(); ps = pbf(bank)
                P.mm(ps, IDENTb, MBB4b, start=True, stop=False)
                for i in range(4):
                    n = 4 * c + i
                    if n == 0:
                        continue
                    P.mm(ps[:, 128 * i:128 * (i + 1)], KT[fp, 128 * (n - 1):128 * n], QT[fp, 128 * n:128 * (n + 1)],
                         start=False, stop=(i == 3))
                P.act(pt[:, 512:1024], ps, AF.Exp, scale=0.125)
                for i in range(4):
                    n = 4 * c + i
                    pv(acc[:, 128 * i:128 * (i + 1)], VA[:, n, e, :], pt[:, 128 * i:128 * (i + 1)])
                    if n > 0:
                        pv(acc[:, 128 * i:128 * (i + 1)], VA[:, n - 1, e, :], pt[:, 512 + 128 * i:512 + 128 * (i + 1)])
                pt = PTbuf.next()
                bank = ringS.next(); ps = pbf(bank)
                P.mm(ps, IDENTb, MBA4b, start=True, stop=False)
                for r in range(4):
                    sl = slice(512 * c + r, 512 * (c + 1), 4)
                    P.mm(ps[:, 128 * r:128 * (r + 1)], KT[fp, sl], QT[fp, sl], start=False, stop=(r == 3))
                P.act(pt[:, 0:512], ps, AF.Exp, scale=0.125)
                if c > 0:
                    bank = ringS.next(); ps = pbf(bank)
                    P.mm(ps, IDENTb, MBB4b, start=True, stop=False)
                    for r in range(4):
                        sl = slice(512 * c + r, 512 * (c + 1), 4)
                        slk = slice(512 * (c - 1) + r, 512 * c, 4)
                        P.mm(ps[:, 128 * r:128 * (r + 1)], KT[fp, slk], QT[fp, sl], start=False, stop=(r == 3))
                    P.act(pt[:, 512:1024], ps, AF.Exp, scale=0.125)
                for r in range(4):
                    pv(acc[:, r:512:4], VA[:, 16 + 4 * r + c, e, :], pt[:, 128 * r:128 * (r + 1)])
                    if c > 0:
                        pv(acc[:, r:512:4], VA[:, 16 + 4 * r + c - 1, e, :], pt[:, 512 + 128 * r:512 + 128 * (r + 1)])
                for r in range(16):
                    pv(acc[:, r:512:16], VA[:, 32 + r, e, :], PT3[:, r, 32 * c:32 * (c + 1)], last=(r == 15))
                num = slice(64 * e, 64 * e + 64)
                den = slice(64 * (1 - e), 64 * (1 - e) + 64)
                rd = rdenb[c % 2]
                P.recip(rd[den, :], acc[den, :])
                P.tt("dve", CT[num, 4 + j, 512 * c:512 * (c + 1)], acc[num, :], rd[den, :], ALU.mult)

    PROJ_V()
    for j in range(4):
        PROJ_B(j)
        ATTN(j)
    dump("CTb", CT[:, 4:8, :].rearrange("p k t -> p (k t)"), [128, 4 * S])
    if stage == "B":
        return finish(nc, P, out_d, dump_tags)

    bF = Bump(A, SCR_LO, SCR_HI)
    h2T = A.view(32768, [8, 1024], BF16)
    woutb = A.view(32768 + 16384, [2, 8, 512], BF16)
    X1 = bF.alloc([8, D])
    wu = [bF.alloc([2, 8, 512], BF16) for _ in range(2)]
    wd = [bF.alloc([4, D], BF16) for _ in range(2)]
    aTb = [bF.alloc([4, 1024], BF16) for _ in range(2)]
    rawF = [[bF.alloc([520]) for _ in range(2)] for _ in range(2)]
    accF = [[bF.alloc([512]) for _ in range(2)] for _ in range(2)]
    sgF = [bF.alloc([512]) for _ in range(2)]
    hbF = [bF.alloc([D], BF16), accF[1][1].bitcast(BF16)]
    junkF = sgF[0].bitcast(BF16)
    ssF = bF.alloc([8]); rsF = bF.alloc([8]); ssO = bF.alloc([8]); rsO = bF.alloc([8])
    for c2 in range(2):
        P.dma("pool", woutb[:, c2, :, :], wout_v[:, :, 512 * c2:512 * (c2 + 1)], "wout%d" % c2)
    P.dma("sp", NW1, n2_d, "c_nw1")
    P.dma("sp", NW2, nf_d, "c_nw2")
    groups = [list(range(g0, min(g0 + 4, 22))) for g0 in range(0, 22, 4)]
    out_tags = ["out%d" % i for i in range(8)]
    items = [(H, gi) for H in range(2) for gi in range(len(groups))]

    def load_wu(k):
        H, gi = items[k]
        grp = groups[gi]; g0 = grp[0]; npair = len(grp); slot = k % 2
        P.dma("pool", wu[slot][:, 0, :, 0:128 * npair], wup_v[:, :, 128 * g0:128 * (g0 + npair)], "wug%d" % slot)
        P.dma("pool", wu[slot][:, 1, :, 0:128 * npair],
              wup_v[:, :, 2816 + 128 * g0:2816 + 128 * (g0 + npair)], "wuu%d" % slot)

    def load_wd(k):
        H, gi = items[k]
        grp = groups[gi]; g0 = grp[0]; npair = len(grp); slot = k % 2
        P.dma("pool", wd[slot][:, 0:npair, :], wdn_v[:, g0:g0 + npair, :], "wd%d" % slot)

    def PRO(H):
        scr = [(junkF, ssF[:, i8:i8 + 1], rsF[:, i8:i8 + 1], hbF[i8 % 2]) for i8 in range(8)]

        def st_a(i8):
            i = 8 * H + i8
            P.dma("sp", X1[:, i8, :], x_d[128 * i:128 * (i + 1), :], "xf%d" % i8)
            b0 = 2 * (i8 % 2)
            for h2 in range(2):
                for kc in range(8):
                    P.mm(pbf(b0 + h2), CT[:, kc, 128 * i:128 * (i + 1)], woutb[:, h2, kc, :],
                         start=(kc == 0), stop=(kc == 7))
            for h2 in range(2):
                P.tt("dve", X1[:, i8, 512 * h2:512 * (h2 + 1)], X1[:, i8, 512 * h2:512 * (h2 + 1)], pbf(b0 + h2), ALU.add)
            if i == 0:
                dump("X1", X1[:, 0, :], [128, D])
            rmsnorm_stage1a(X1[:, i8, :], scr[i8])

        for t in range(10):
            if t < 8:
                st_a(t)
            if 1 <= t <= 8:
                rmsnorm_stage1b(X1[:, t - 1, :], NW1, scr[t - 1])
            if t >= 2:
                rmsnorm_stage2(h2T, 128 * (t - 2), scr[t - 2], 4 + (t % 2))

    upar = [0]

    def UGEN(k):
        H, gi = items[k]
        grp = groups[gi]; slot = k % 2; aT = aTb[k % 2]
        for p, g in enumerate(grp):
            for tb in range(2):
                par = upar[0]
                upar[0] ^= 1
                cols = slice(512 * tb, 512 * (tb + 1))
                banks = (4, 5) if par == 0 else (6, 7)
                accs = []
                for gu in range(2):
                    ps = pbf(banks[gu])
                    for kc in range(8):
                        P.mm(ps, wu[slot][:, gu, kc, 128 * p:128 * (p + 1)], h2T[:, kc, cols],
                             start=(kc == 0), stop=(kc == 7))
                    cc = g + 22 * gu
                    raw = rawF[par][gu]; acc = accF[par][gu]
                    P.copy("pool", raw[:, 0:2], HALOF[:, cc, :])
                    P.copy("act", raw[:, 2:514], ps)
                    P.copy("pool", HALOF[:, cc, :], raw[:, 512:514])
                    P.act(acc, ps, AF.Identity, scale=cwF[:, cc, 2:3])
                    P.stt(acc, raw[:, 1:513], cwF[:, cc, 1:2], acc, ALU.mult, ALU.add)
                    P.stt(acc, raw[:, 0:512], cwF[:, cc, 0:1], acc, ALU.mult, ALU.add)
                    accs.append(acc)
                sg = sgF[par]
                P.act(sg, accs[0], AF.Silu)
                P.tt("pool", aT[:, p, cols], sg, accs[1], ALU.mult)
                yield

    def DGEN(k):
        H, gi = items[k]
        grp = groups[gi]; slot = k % 2; aT = aTb[k % 2]; npair = len(grp)
        last = (gi == len(groups) - 1)
        for i8 in range(8):
            i = 8 * H + i8
            b0 = 2 * (i8 % 2)
            for h2 in range(2):
                for p in range(npair):
                    P.mm(pbf(b0 + h2), aT[:, p, 128 * i8:128 * (i8 + 1)], wd[slot][:, p, 512 * h2:512 * (h2 + 1)],
                         start=(p == 0), stop=(p == npair - 1))
            for h2 in range(2):
                P.tt("dve", X1[:, i8, 512 * h2:512 * (h2 + 1)], X1[:, i8, 512 * h2:512 * (h2 + 1)],
                     pbf(b0 + h2), ALU.add)
            if last:
                P.act(junkF, X1[:, i8, :], AF.Square, accum_out=ssO[:, i8:i8 + 1])
                P.act(rsO[:, i8:i8 + 1], ssO[:, i8:i8 + 1], AF.Ln, bias=EPS, scale=1.0 / D)
                P.act(rsO[:, i8:i8 + 1], rsO[:, i8:i8 + 1], AF.Exp, scale=-0.5)
                P.stt(X1[:, i8, :], X1[:, i8, :], rsO[:, i8:i8 + 1], NW2, ALU.mult, ALU.mult)
                P.dma("sp", out_d[128 * i:128 * (i + 1), :], X1[:, i8, :], "out%d" % i8)
            yield

    def rr(gens):
        gens = list(gens)
        live = [True] * len(gens)
        while any(live):
            for gi_, g_ in enumerate(gens):
                if live[gi_]:
                    try:
                        next(g_)
                    except StopIteration:
                        live[gi_] = False

    load_wu(0)
    prevD = None
    for k in range(len(items)):
        H, gi = items[k]
        load_wd(k)
        if k + 1 < len(items):
            load_wu(k + 1)
        if gi == 0:
            if prevD is not None:
                rr([prevD])
                prevD = None
            PRO(H)
        u = UGEN(k)
        rr([u] if prevD is None else [u, prevD])
        prevD = DGEN(k)
    rr([prevD])
    return finish(nc, P, out_d, dump_tags + out_tags)


def finish(nc, P, out_d, dump_tags):
    tags = list(dump_tags)
    P.emit(final_dma_tags=tags)
    return nc


def prep_inputs(inp):
    f = lambda a: np.ascontiguousarray(np.asarray(a, dtype=np.float32))
    x = f(inp["x"])
    rep = lambda v: np.ascontiguousarray(np.broadcast_to(f(v).reshape(1, -1), (128, f(v).size)))
    cwa = f(inp["conv_qkv_w"])[0]
    cwA = np.ascontiguousarray(cwa.T.reshape(12, 128, 4).transpose(1, 0, 2).reshape(128, 48))
    cwf = f(inp["ffn_conv_w"])[0]
    cwF = np.ascontiguousarray(cwf.T.reshape(44, 128, 3).transpose(1, 0, 2).reshape(128, 132))
    shared = {
        "w_in": f(inp["w_in"])[0], "w_out": f(inp["w_out"])[0], "w_up": f(inp["w_up"])[0],
        "w_down": f(inp["w_down"])[0],
        "n1rep": rep(inp["norm1_w"]), "n2rep": rep(inp["norm2_w"]), "nfrep": rep(inp["final_norm_w"]),
        "cwA": cwA, "cwF": cwF, "gnwrep": rep(inp["gdn_norm_w"]),
        "dtbrep": np.ascontiguousarray(np.tile(rep(inp["dt_bias"]), (1, 16))),
        "alogrep": np.ascontiguousarray(np.tile(rep(inp["a_log"]), (1, 16))),
        "consts": make_consts(),
    }
    maps = []
    for b in range(x.shape[0]):
        m = dict(shared)
        m["x"] = np.ascontiguousarray(x[b])
        maps.append(m)
    return maps


def kernel(**inputs):
    maps = prep_inputs(inputs)
    nc = build("full")
    res = run_bass_kernel_spmd(nc, maps, core_ids=list(range(8)))
    out = np.stack([np.asarray(r["out"], dtype=np.float32) for r in res.results], 0)
    return out
```

```python
import contextlib
import numpy as np
import concourse.bass as bass
import concourse.mybir as mybir
from concourse.bass_utils import run_bass_kernel_spmd

F32 = mybir.dt.float32
BF16 = mybir.dt.bfloat16
AF = mybir.ActivationFunctionType
ALU = mybir.AluOpType
AX = mybir.AxisListType

S = 2048
D = 1024
NT = 16
EPS = 1e-6
NEG = -30000.0
ENGS = ("pe", "act", "dve", "pool", "sp")


def _rect(ap):
    t = ap.tensor
    if str(ap.space) == "PSUM":
        return (t.name, 0, 128, 0, 2048)
    esz = mybir.dt.size(ap.dtype)
    pstride = 1
    for s in tuple(t.shape)[1:]:
        pstride *= s
    p0 = ap.start_partition()
    p1 = p0 + ap.partition_size()
    f0 = ap.offset - p0 * pstride
    ext = 0
    for (st, cnt) in tuple(ap.ap)[1:]:
        ext += abs(st) * (cnt - 1)
    return (t.name, p0, p1, f0 * esz, (f0 + ext + 1) * esz)


class _Op:
    __slots__ = ("eng", "fn", "idx", "is_dma", "tag", "waits", "signal", "sigval")


class Prog:
    def __init__(self, nc):
        self.nc = nc
        self.ops = {e: [] for e in ENGS}
        self.track = {}
        self.waited = {e: {} for e in ENGS}
        self.tagcount = {}

    def _add(self, eng, fn, reads, writes, is_dma=False, tag=None):
        op = _Op()
        op.eng = eng; op.fn = fn; op.is_dma = is_dma; op.tag = tag
        op.signal = False; op.sigval = None
        op.idx = len(self.ops[eng])
        op.waits = []
        deps = {}
        rrects = [_rect(a) for a in reads if a is not None and str(a.space) != "DRAM"]
        wrects = [_rect(a) for a in writes if a is not None and str(a.space) != "DRAM"]
        prects = [r for r in rrects if r[4] == 2048 and r[0].startswith("pb") and r not in wrects]
        for (nm, p0, p1, f0, f1) in rrects:
            for rec in self.track.get(nm, ()):
                if rec[5] == 1 and rec[0] < p1 and p0 < rec[1] and rec[2] < f1 and f0 < rec[3]:
                    deps[id(rec[4])] = (rec[4], True)
        for (nm, p0, p1, f0, f1) in wrects + prects:
            for rec in self.track.get(nm, ()):
                if rec[0] < p1 and p0 < rec[1] and rec[2] < f1 and f0 < rec[3]:
                    k = id(rec[4])
                    if k not in deps:
                        deps[k] = (rec[4], False)
        need = {}
        for (d, raw) in deps.values():
            if d.is_dma:
                key = ("dma", d.tag)
                val = self.tagcount[d.tag]
                if need.get(key, 0) < val:
                    need[key] = val
            else:
                if d.eng == eng and eng == "pe":
                    continue
                key = ("eng", d.eng)
                cur = need.get(key)
                if cur is None or cur.idx < d.idx:
                    need[key] = d
        w = self.waited[eng]
        for key, v in need.items():
            if key[0] == "dma":
                if w.get(key, 0) >= v:
                    continue
                w[key] = v
                op.waits.append((key, v))
            else:
                if w.get(key, -1) >= v.idx:
                    continue
                w[key] = v.idx
                v.signal = True
                op.waits.append((key, v))
        if is_dma:
            self.tagcount[tag] = self.tagcount.get(tag, 0) + 16
        for (nm, p0, p1, f0, f1) in wrects:
            lst = self.track.setdefault(nm, [])
            lst[:] = [r for r in lst if not (p0 <= r[0] and r[1] <= p1 and f0 <= r[2] and r[3] <= f1)]
            lst.append([p0, p1, f0, f1, op, 1])
        for (nm, p0, p1, f0, f1) in prects:
            self.track[nm] = [[p0, p1, f0, f1, op, 2]]
        for (nm, p0, p1, f0, f1) in rrects:
            if nm.startswith("pb"):
                continue
            lst = self.track.setdefault(nm, [])
            done = False
            for r in lst:
                if r[5] == 0 and r[4].eng == eng and (not r[4].is_dma) and (not is_dma) \
                        and r[0] == p0 and r[1] == p1 and r[2] == f0 and r[3] == f1:
                    r[4] = op
                    done = True
                    break
            if not done:
                lst.append([p0, p1, f0, f1, op, 0])
        self.ops[eng].append(op)
        return op

    def dma(self, q, out, in_, tag):
        return self._add(q, lambda e: e.dma_start(out=out, in_=in_), [in_], [out], is_dma=True, tag=tag)

    def mm(self, out, lhsT, rhs, start=True, stop=True, **kw):
        rd = [lhsT, rhs] + ([] if start else [out])
        return self._add("pe", lambda e: e.matmul(out, lhsT, rhs, start=start, stop=stop, **kw), rd, [out])

    def tr(self, out, in_, ident):
        return self._add("pe", lambda e: e.transpose(out, in_, ident), [in_, ident], [out])

    def act(self, out, in_, func, bias=None, scale=None, accum_out=None):
        kw = {}
        rd = [in_]
        if bias is not None:
            kw["bias"] = bias
            if not isinstance(bias, (int, float)):
                rd.append(bias)
        if scale is not None:
            kw["scale"] = scale
            if not isinstance(scale, (int, float)):
                rd.append(scale)
        wr = [out]
        if accum_out is not None:
            kw["accum_out"] = accum_out
            wr.append(accum_out)
        return self._add("act", lambda e: e.activation(out, in_, func, **kw), rd, wr)

    def tt(self, eng, out, in0, in1, op):
        return self._add(eng, lambda e: e.tensor_tensor(out, in0, in1, op), [in0, in1], [out])

    def ts(self, eng, out, in0, s1, s2, op0, op1=None):
        rd = [in0] + [s for s in (s1, s2) if s is not None and not isinstance(s, (int, float))]
        kw = {}
        if op1 is not None:
            kw["op1"] = op1
        return self._add(eng, lambda e: e.tensor_scalar(out, in0, s1, s2, op0, **kw), rd, [out])

    def stt(self, out, in0, scalar, in1, op0, op1):
        rd = [in0, in1] + ([] if isinstance(scalar, (int, float)) else [scalar])
        return self._add("dve", lambda e: e.scalar_tensor_tensor(out, in0, scalar, in1, op0, op1), rd, [out])

    def copy(self, eng, out, in_):
        if eng == "act":
            return self._add("act", lambda e: e.copy(out, in_), [in_], [out])
        return self._add(eng, lambda e: e.tensor_copy(out, in_), [in_], [out])

    def memset(self, eng, ap, val):
        return self._add(eng, lambda e: e.memset(ap, val), [], [ap])

    def recip(self, out, in_):
        return self._add("dve", lambda e: e.reciprocal(out, in_), [in_], [out])

    def rsum(self, out, in_):
        return self._add("dve", lambda e: e.tensor_reduce(out, in_, AX.X, ALU.add), [in_], [out])

    def emit(self, final_dma_tags=()):
        nc = self.nc
        for e in ENGS:
            c = 0
            for op in self.ops[e]:
                if op.signal and not op.is_dma:
                    c += 1
                    op.sigval = c
        with contextlib.ExitStack() as st:
            esem = {e: st.enter_context(nc.semaphore("s_" + e)) for e in ENGS}
            dsem = {t: st.enter_context(nc.semaphore("d_%d" % i)) for i, t in enumerate(self.tagcount)}
            block = st.enter_context(nc.Block())
            engobj = {"pe": block.tensor, "act": block.scalar, "dve": block.vector,
                      "pool": block.gpsimd, "sp": block.sync}

            def make(ename):
                def body(eng):
                    for op in self.ops[ename]:
                        for (key, v) in op.waits:
                            if key[0] == "dma":
                                eng.wait_ge(dsem[key[1]], v)
                            else:
                                eng.wait_ge(esem[key[1]], v.sigval)
                        ins = op.fn(eng)
                        if op.is_dma:
                            ins.then_inc(dsem[op.tag], 16)
                        elif op.signal:
                            ins.then_inc(esem[ename], 1)
                    if ename == "sp":
                        for t in final_dma_tags:
                            eng.wait_ge(dsem[t], self.tagcount[t])
                return body

            for e in ENGS:
                engobj[e](make(e))
        return nc


def _prod(shape):
    n = 1
    for s in shape:
        n *= s
    return n


class Arena:
    def __init__(self, nc, nbytes):
        self.t = nc.alloc_sbuf_tensor("arena", [128, nbytes // 4], F32)
        self.nbytes = nbytes

    def view(self, off, shape, dt):
        n = _prod(shape)
        esz = 4 if dt == F32 else 2
        assert off % 4 == 0 and (n * esz) % 4 == 0 and off + n * esz <= self.nbytes, (off, shape)
        ap = self.t[:, off // 4:(off + n * esz) // 4]
        if dt != F32:
            ap = ap.bitcast(dt)
        if len(shape) > 1:
            names = " ".join("d%d" % i for i in range(len(shape)))
            kw = {"d%d" % i: shape[i] for i in range(1, len(shape))}
            ap = ap.rearrange("p (%s) -> p %s" % (names, names), **kw)
        return ap


class Bump:
    def __init__(self, arena, lo, hi):
        self.a = arena; self.lo = lo; self.hi = hi; self.cur = lo

    def alloc(self, shape, dt=F32):
        esz = 4 if dt == F32 else 2
        n = (_prod(shape) * esz + 31) // 32 * 32
        off = self.cur
        self.cur += n
        assert self.cur <= self.hi, ("arena overflow", self.cur, self.hi)
        return self.a.view(off, shape, dt)


class Ring:
    def __init__(self, items):
        self.items = list(items); self.i = 0

    def next(self):
        x = self.items[self.i % len(self.items)]
        self.i += 1
        return x


_CNAMES = ["IDENT", "M1", "M2", "TRI", "SUF", "IND0", "IND1", "S32", "OFF", "ONES"]
_CB = {"MB64": 512, "MBA4": 512, "MBB4": 512}
NCF = 128 * len(_CNAMES)
NCONST = NCF + 512 * 3


def make_consts():
    i = np.arange(128)[:, None]
    j = np.arange(128)[None, :]
    same = (i // 64) == (j // 64)
    c = {}
    c["IDENT"] = (i == j)
    c["M1"] = (i > j)
    c["M2"] = (i <= j)
    c["TRI"] = (i <= j) & same
    c["SUF"] = (i > j) & same
    c["IND0"] = (i < 64) & (j >= 0)
    c["IND1"] = (i >= 64) & (j >= 0)
    c["S32"] = (i < j) & ((i // 32) == (j // 32))
    c["OFF"] = same & ((i % 64) < 32) & ((j % 64) >= 32)
    c["ONES"] = np.ones((128, 128), bool)
    cols = [c[n].astype(np.float32) for n in _CNAMES]
    mb64 = np.where((i <= j) & same, 0.0, NEG).astype(np.float32)
    mba = np.where(i <= j, 0.0, NEG).astype(np.float32)
    mbb = np.where(i >= j, 0.0, NEG).astype(np.float32)
    cols += [np.tile(mb64, (1, 4)), np.tile(mba, (1, 4)), np.tile(mbb, (1, 4))]
    return np.ascontiguousarray(np.concatenate(cols, 1))


def build(stage="full", dumps=()):
    nc = bass.Bass("TRN2", target_bir_lowering=False)
    P = Prog(nc)
    dumps = set(dumps)
    dump_tags = []

    def din(name, shape):
        return nc.dram_tensor(name, list(shape), F32, kind="ExternalInput").ap()

    x_d = din("x", [S, D])
    win_d = din("w_in", [D, 3592])
    wout_d = din("w_out", [D, D])
    wup_d = din("w_up", [D, 5632])
    wdn_d = din("w_down", [2816, D])
    n1_d = din("n1rep", [128, D])
    n2_d = din("n2rep", [128, D])
    nf_d = din("nfrep", [128, D])
    cwa_d = din("cwA", [128, 48])
    cwf_d = din("cwF", [128, 132])
    gnw_d = din("gnwrep", [128, 128])
    dtb_d = din("dtbrep", [128, 64])
    alog_d = din("alogrep", [128, 64])
    cst_d = din("consts", [128, NCONST])
    out_d = nc.dram_tensor("out", [S, D], F32, kind="ExternalOutput").ap()

    def dump(name, sb_ap, shape):
        if name not in dumps:
            return
        d = nc.dram_tensor("dbg_" + name, list(shape), sb_ap.dtype, kind="ExternalOutput").ap()
        tg = "dbg_" + name
        P.dma("sp", d, sb_ap, tg)
        dump_tags.append(tg)

    win_v = win_d.rearrange("(k p) c -> p k c", p=128)
    wout_v = wout_d.rearrange("(k p) c -> p k c", p=128)
    wup_v = wup_d.rearrange("(k p) c -> p k c", p=128)
    wdn_v = wdn_d.rearrange("(k p) c -> p k c", p=128)

    ARENA_BYTES = 207872
    A = Arena(nc, ARENA_BYTES)
    pb = [nc.alloc_psum_tensor("pb%d" % i, [128, 512], F32) for i in range(8)]

    def pbf(i):
        return pb[i][:]

    def pbb(i):
        return pb[i][:].bitcast(BF16)

    CT = A.view(0, [8, S], BF16)
    hT = A.view(32768, [8, S], BF16)
    pers = Bump(A, 65536, 86016)
    CF = pers.alloc([NCF])
    CBm = pers.alloc([1536 + 256], BF16)
    cwA = pers.alloc([12, 4])
    cwF = pers.alloc([44, 3])
    HALOA = pers.alloc([12, 3])
    HALOF = pers.alloc([44, 2])
    GNW = pers.alloc([128])
    NW1 = pers.alloc([D])
    NW2 = pers.alloc([D])
    SCR_LO = 86016
    SCR_HI = ARENA_BYTES

    def cf(name):
        k = _CNAMES.index(name)
        return CF[:, 128 * k:128 * (k + 1)]

    IDENT = cf("IDENT"); M1 = cf("M1"); M2 = cf("M2"); TRI = cf("TRI"); SUF = cf("SUF")
    IND0 = cf("IND0"); IND1 = cf("IND1"); S32 = cf("S32"); OFFM = cf("OFF"); ONES = cf("ONES")
    MB64b = CBm[:, 0:512]; MBA4b = CBm[:, 512:1024]; MBB4b = CBm[:, 1024:1536]
    IDENTb = CBm[:, 1536:1664]; ONESb = CBm[:, 1664:1792]

    P.dma("sp", CF, cst_d[:, 0:NCF], "c_cf")
    ctmp = A.view(16384 + 8192, [1536], F32)
    P.dma("sp", ctmp, cst_d[:, NCF:NCONST], "c_tmp")
    P.copy("dve", CBm[:, 0:1536], ctmp)
    P.copy("dve", IDENTb, IDENT)
    P.copy("dve", ONESb, ONES)
    P.dma("sp", cwA.rearrange("p a b -> p (a b)"), cwa_d, "c_cwa")
    P.dma("sp", cwF.rearrange("p a b -> p (a b)"), cwf_d, "c_cwf")
    P.dma("sp", GNW, gnw_d, "c_gnw")
    P.dma("sp", NW1, n1_d, "c_nw1")
    P.memset("pool", HALOA.rearrange("p a b -> p (a b)"), 0.0)
    P.memset("pool", HALOF.rearrange("p a b -> p (a b)"), 0.0)

    def bc_last(ap, n):
        return ap.unsqueeze(2).broadcast_to([128, ap.shape[1], n])

    def bc_mid(ap, n):
        return ap.unsqueeze(1).broadcast_to([128, n, ap.shape[1]])

    def rmsnorm_stage1(src_tile, wtile, scr):
        junk, ssv, rst, hb = scr
        P.act(junk, src_tile, AF.Square, accum_out=ssv)
        P.act(rst, ssv, AF.Ln, bias=EPS, scale=1.0 / D)
        P.act(rst, rst, AF.Exp, scale=-0.5)
        P.stt(hb, src_tile, rst, wtile, ALU.mult, ALU.mult)

    def rmsnorm_stage1a(src_tile, scr):
        junk, ssv, rst, hb = scr
        P.act(junk, src_tile, AF.Square, accum_out=ssv)
        P.act(rst, ssv, AF.Ln, bias=EPS, scale=1.0 / D)
        P.act(rst, rst, AF.Exp, scale=-0.5)

    def rmsnorm_stage1b(src_tile, wtile, scr):
        junk, ssv, rst, hb = scr
        P.stt(hb, src_tile, rst, wtile, ALU.mult, ALU.mult)

    def rmsnorm_stage2(dstT, col0, scr, bank):
        hb = scr[3]
        psT = pbb(bank)
        for kc in range(8):
            P.tr(psT[:, 128 * kc:128 * (kc + 1)], hb[:, 128 * kc:128 * (kc + 1)], IDENTb)
        P.copy("act", dstT[:, :, col0:col0 + 128], psT.rearrange("p (k t) -> p k t", k=8))

    bA = Bump(A, 0, 16384)
    xbuf = [bA.alloc([D]) for _ in range(3)]
    junkA = bA.alloc([D], BF16)
    hbuf = [NW2.bitcast(BF16)[:, 0:D], NW2.bitcast(BF16)[:, D:2 * D]]
    ssA = bA.alloc([NT])
    rsA = bA.alloc([NT])
    scrA = [(junkA, ssA[:, i:i + 1], rsA[:, i:i + 1], hbuf[i % 2]) for i in range(NT)]
    for i in range(NT + 2):
        if i < NT:
            xt = xbuf[i % 3]
            P.dma("sp", xt, x_d[128 * i:128 * (i + 1), :], "xa%d" % (i % 3))
            rmsnorm_stage1a(xt, scrA[i])
        if 1 <= i <= NT:
            rmsnorm_stage1b(xbuf[(i - 1) % 3], NW1, scrA[i - 1])
        if i >= 2:
            rmsnorm_stage2(hT, 128 * (i - 2), scrA[i - 2], 6 + (i % 2))
    dump("hT", hT.rearrange("p k t -> p (k t)"), [128, 8 * S])
    if stage == "A":
        return finish(nc, P, out_d, dump_tags)

    bG = Bump(A, SCR_LO, SCR_HI)
    bC = Bump(A, 16384, 32768)
    wqkva = bG.alloc([3, 8, 512], BF16)
    wba = bG.alloc([8, 8], BF16)
    wz = bC.alloc([8, 512], BF16)
    for c3 in range(3):
        P.dma("pool", wqkva[:, c3, :, :], win_v[:, :, 512 * c3:512 * (c3 + 1)], "wqkva%d" % c3)
    P.dma("pool", wba, win_v[:, :, 2048:2056], "wba")
    P.dma("pool", wz, win_v[:, :, 1536:2048], "wz")

    raw0 = bG.alloc([520]); acc0 = bG.alloc([512])
    rawb = [raw0, raw0]; accb = [acc0, acc0]
    sqb = bG.alloc([512], BF16)
    rtb = acc0
    qkvT = [dict(q=bG.alloc([4, 512], BF16), k=bG.alloc([4, 512], BF16), v=bG.alloc([4, 512], BF16))
            for _ in range(2)]
    BA = bG.alloc([NT, 8])
    sc_x = bG.alloc([64]); sc_mx = bG.alloc([64]); sc_mn = bG.alloc([64])
    dtb = bG.alloc([64]); negA = bG.alloc([64])
    gS = bG.alloc([64]); betaS = bG.alloc([64]); gamS = bG.alloc([64]); kesS = bG.alloc([64])
    gendS = bG.alloc([2, 64])
    gset = []
    for _ in range(2):
        gset.append(dict(
            G1m=bG.alloc([4, 128]),
            DECT=bG.alloc([4, 128], BF16), Uall=bG.alloc([4, 128], BF16), Eu=bG.alloc([4, 128], BF16),
            PTa=bG.alloc([4, 128], BF16), PTb=bG.alloc([4, 128], BF16),
            UL0=bG.alloc([2, 4, 128], BF16), PWA=bG.alloc([2, 4, 128], BF16), PWB=bG.alloc([2, 4, 128], BF16),
            TTb=bG.alloc([4, 128], BF16), vb=bG.alloc([4, 128], BF16), gk=bG.alloc([4, 128], BF16)))
    scanop = []
    for _ in range(3):
        scanop.append(dict(KLO=bG.alloc([4, 128], BF16), KHI=bG.alloc([4, 128], BF16), ATT=bG.alloc([4, 128], BF16),
                           QD=bG.alloc([4, 128], BF16), WKT=bG.alloc([4, 128], BF16),
                           UV=bG.alloc([4, 128], BF16)))
    Sst = bG.alloc([4, 128]); Sb = bG.alloc([4, 128], BF16); ub = bG.alloc([4, 128], BF16)
    Obuf = [bC.alloc([4, 128]) for _ in range(2)]
    sqO = bG.alloc([4, 128], BF16); oab = bG.alloc([4, 128], BF16)
    szgb = [bG.alloc([4, 512], BF16), bC.alloc([4, 512], BF16)]
    kesLo = bG.alloc([64]); kesHi = bG.alloc([64])
    ssq = bG.alloc([4]); rsq = bG.alloc([4])

    ringG = Ring([0, 1, 2, 3, 4, 5, 6, 7])

    g3 = gS.rearrange("p (n h) -> p n h", h=4)
    beta3 = betaS.rearrange("p (n h) -> p n h", h=4)
    gam3 = gamS.rearrange("p (n h) -> p n h", h=4)
    kesLo3 = kesLo.rearrange("p (n h) -> p n h", h=4)
    kesHi3 = kesHi.rearrange("p (n h) -> p n h", h=4)
    gend4 = gendS.rearrange("p a (n h) -> p a n h", h=4)

    def SCALARS():
        P.dma("sp", dtb, dtb_d, "c_dtb")
        P.dma("sp", negA, alog_d, "c_alog")
        P.act(negA, negA, AF.Exp)
        P.ts("dve", negA, negA, -1.0, None, ALU.mult)
        bk = ringG.next()
        psBA = pbf(bk)[:, 0:128].rearrange("p (n c) -> p n c", c=8)
        for i in range(NT):
            for kc in range(8):
                P.mm(psBA[:, i, :], hT[:, kc, 128 * i:128 * (i + 1)], wba[:, kc, :], start=(kc == 0), stop=(kc == 7))
        P.copy("act", BA, psBA)
        x3 = sc_x.rearrange("p (n h) -> p n h", h=4)
        P.tt("dve", x3, BA[:, :, 4:8], dtb.rearrange("p (n h) -> p n h", h=4), ALU.add)
        P.ts("dve", sc_mx, sc_x, 0.0, None, ALU.max)
        P.ts("dve", sc_mn, sc_x, 0.0, None, ALU.min)
        P.tt("dve", sc_mn, sc_mn, sc_mx, ALU.subtract)
        P.act(sc_mn, sc_mn, AF.Exp)
        P.act(sc_mn, sc_mn, AF.Ln, bias=1.0)
        P.tt("dve", sc_mx, sc_mx, sc_mn, ALU.add)
        P.tt("dve", gS, sc_mx, negA, ALU.mult)
        P.act(betaS.rearrange("p (n h) -> p n h", h=4), BA[:, :, 0:4], AF.Sigmoid)
        bk = ringG.next()
        psg = pbf(bk)
        P.mm(psg[:, 0:64], TRI, gS)
        P.mm(psg[:, 64:128], SUF, gS)
        P.mm(psg[:, 128:192], IND0, gS)
        P.mm(psg[:, 192:256], IND1, gS)
        P.act(gamS, psg[:, 0:64], AF.Exp)
        P.act(kesS, psg[:, 64:128], AF.Exp)
        P.tt("dve", kesS, kesS, betaS, ALU.mult)
        P.act(gendS.rearrange("p a b -> p (a b)"), psg[:, 128:256], AF.Exp)
        dump("gS", gS, [128, 64]); dump("betaS", betaS, [128, 64]); dump("gamS", gamS, [128, 64])
        dump("kesS", kesS, [128, 64]); dump("gendS", gendS.rearrange("p a b -> p (a b)"), [128, 128])
        P.ts("dve", kesLo, kesS, IND0[:, 0:1], None, ALU.mult)
        P.ts("dve", kesHi, kesS, IND1[:, 0:1], None, ALU.mult)


    def G1(m):
        t0 = 512 * m
        o = qkvT[m % 2]
        for c in range(12):
            ps = pbf(ringG.next())
            for kc in range(8):
                P.mm(ps, wqkva[:, c // 4, kc, 128 * (c % 4):128 * (c % 4 + 1)], hT[:, kc, t0:t0 + 512], start=(kc == 0), stop=(kc == 7))
            raw = rawb[c % 2]; acc = accb[c % 2]
            P.copy("pool", raw[:, 0:3], HALOA[:, c, :])
            P.copy("act", raw[:, 3:515], ps)
            P.copy("pool", HALOA[:, c, :], raw[:, 512:515])
            P.act(acc, ps, AF.Identity, scale=cwA[:, c, 3:4])
            P.stt(acc, raw[:, 2:514], cwA[:, c, 2:3], acc, ALU.mult, ALU.add)
            P.stt(acc, raw[:, 1:513], cwA[:, c, 1:2], acc, ALU.mult, ALU.add)
            P.stt(acc, raw[:, 0:512], cwA[:, c, 0:1], acc, ALU.mult, ALU.add)
            dst = (o["q"], o["k"], o["v"])[c // 4][:, c % 4, :]
            P.act(dst, acc, AF.Silu)
            if c % 3 == 2:
                nz = 4 * m + c // 3
                psZ = pbf(ringG.next())
                for kc in range(8):
                    P.mm(psZ, hT[:, kc, 128 * nz:128 * (nz + 1)], wz[:, kc, :], start=(kc == 0), stop=(kc == 7))
                szg = szgb[m % 2][:, c // 3, :]
                P.act(szg, psZ, AF.Silu)
                P.tt("pool", szg.rearrange("p (h d) -> p h d", h=4), szg.rearrange("p (h d) -> p h d", h=4),
                     bc_mid(GNW, 4), ALU.mult)
                yield
        for c in range(8):
            dst = (o["q"], o["k"])[c // 4][:, c % 4, :]
            sc = 128.0 if c < 4 else 1.0
            P.act(sqb, dst, AF.Square)
            psn = pbf(ringG.next())
            P.mm(psn, ONESb, sqb)
            P.act(rtb, psn, AF.Ln, bias=EPS * sc, scale=sc)
            P.act(rtb, rtb, AF.Exp, scale=-0.5)
            P.tt("dve", dst, dst, rtb, ALU.mult)
            yield
        if m == 0:
            dump("qnT0", o["q"].rearrange("p h t -> p (h t)"), [128, 2048])
            dump("knT0", o["k"].rearrange("p h t -> p (h t)"), [128, 2048])
            dump("vsT0", o["v"].rearrange("p h t -> p (h t)"), [128, 2048])

    def G2(n):
        tl = 128 * (n % 4)
        so = scanop[n % 3]
        st = gset[n % 2]
        qnT = qkvT[(n // 4) % 2]["q"]; knT = qkvT[(n // 4) % 2]["k"]; vsT = qkvT[(n // 4) % 2]["v"]
        G1m = st["G1m"]; DECT = st["DECT"]; Uall = st["Uall"]; Eu = st["Eu"]
        UL0 = st["UL0"]; PWA = st["PWA"]; PWB = st["PWB"]
        TTb = st["TTb"]; vb = st["vb"]; gk = st["gk"]
        Du, Dl = UL0[:, 0], UL0[:, 1]
        gam_b = bc_last(gam3[:, n, :], 128)
        keslo_b = bc_last(kesLo3[:, n, :], 128)
        keshi_b = bc_last(kesHi3[:, n, :], 128)
        beta_b = bc_last(beta3[:, n, :], 128)
        g_b = bc_last(g3[:, n, :], 128)

        def bankf():
            return pbf(ringG.next()).rearrange("p (h d) -> p h d", h=4)

        def bankb():
            return pbb(ringG.next())[:, 0:512].rearrange("p (h d) -> p h d", h=4)

        psKV = pbb(ringG.next()).rearrange("p (a h d) -> p a h d", a=2, h=4)
        psK = psKV[:, 0]; psV = psKV[:, 1]
        for h in range(4):
            P.tr(psK[:, h, :], knT[:, h, tl:tl + 128], IDENTb)
            P.tr(psV[:, h, :], vsT[:, h, tl:tl + 128], IDENTb)
        P.tt("dve", gk, psK, gam_b, ALU.mult)
        P.tt("dve", so["KLO"], psK, keslo_b, ALU.mult)
        P.tt("dve", so["KHI"], psK, keshi_b, ALU.mult)
        P.copy("act", vb, psV)
        yield
        P.tt("pool", G1m, bc_mid(M1, 4), g_b, ALU.mult)
        psD = bankf()
        P.mm(psD, IDENTb, MB64b, start=True, stop=False)
        for h in range(4):
            P.mm(psD[:, h, :], G1m[:, h, :], M2, start=False, stop=(h == 3))
        P.act(DECT, psD, AF.Exp)
        P.tt("pool", DECT, DECT, beta_b, ALU.mult)
        yield
        dg = Uall
        P.tt("pool", dg, bc_mid(IDENT, 4), gam_b, ALU.mult)
        psG = bankf()
        for h in range(4):
            P.mm(psG[:, h, :], ONESb, dg[:, h, :])
        P.tt("dve", so["QD"], qnT[:, :, tl:tl + 128], psG, ALU.mult)
        yield
        psKK = bankf(); psQK = bankf()
        for h in range(4):
            P.mm(psKK[:, h, :], knT[:, h, tl:tl + 128], knT[:, h, tl:tl + 128])
        for h in range(4):
            P.mm(psQK[:, h, :], knT[:, h, tl:tl + 128], qnT[:, h, tl:tl + 128])
        P.tt("dve", Uall, psKK, DECT, ALU.mult)
        P.tt("dve", so["ATT"], psQK, DECT, ALU.mult)
        P.tt("pool", Du, Uall, bc_mid(S32, 4), ALU.mult)
        P.tt("pool", Eu, Uall, bc_mid(OFFM, 4), ALU.mult)
        yield
        psT = bankb()
        for h in range(4):
            P.tr(psT[:, h, :], Du[:, h, :], IDENTb)
        P.copy("act", Dl, psT)
        P.tt("pool", st["PTa"], bc_mid(IDENT, 4), Du, ALU.subtract)
        yield
        pw = [UL0, PWA, PWB, PWA, PWB]
        PT, PTn = st["PTa"], st["PTb"]
        for k in range(1, 5):
            cur = pw[k - 1]; nxt = pw[k]
            if k < 4:
                psU = bankf()
                for h in range(4):
                    P.mm(psU[:, h, :], cur[:, 1, h, :], cur[:, 0, h, :])
            psL = bankf()
            for h in range(4):
                P.mm(psL[:, h, :], cur[:, 0, h, :], cur[:, 1, h, :])
            if k > 1:
                ps3 = bankf()
                for h in range(4):
                    P.mm(ps3[:, h, :], cur[:, 1, h, :], PT[:, h, :])
            if k < 4:
                P.copy("act", nxt[:, 0], psU)
            P.copy("act", nxt[:, 1], psL)
            if k > 1:
                P.tt("dve", PTn, PT, ps3, ALU.add)
                PT, PTn = PTn, PT
            yield
        ps3 = bankf()
        for h in range(4):
            P.mm(ps3[:, h, :], PWB[:, 1, h, :], PT[:, h, :])
        P.tt("dve", PTn, PT, ps3, ALU.add)
        PT, PTn = PTn, PT
        yield
        Pm = PWA[:, 0]; XT = DECT
        p1 = bankb()
        for h in range(4):
            P.tr(p1[:, h, :], PT[:, h, :], IDENTb)
        P.copy("act", Pm, p1)
        yield
        p2 = bankf()
        for h in range(4):
            P.mm(p2[:, h, :], Eu[:, h, :], Pm[:, h, :])
        P.copy("act", XT, p2)
        yield
        p3 = bankf()
        for h in range(4):
            P.mm(p3[:, h, :], XT[:, h, :], PT[:, h, :])
        P.tt("dve", TTb, PT, p3, ALU.subtract)
        yield
        p1 = bankf(); p2 = bankf()
        for h in range(4):
            P.mm(p1[:, h, :], TTb[:, h, :], vb[:, h, :])
        for h in range(4):
            P.mm(p2[:, h, :], gk[:, h, :], TTb[:, h, :])
        P.copy("act", so["UV"], p1)
        P.copy("dve", so["WKT"], p2)
        yield

    def SCAN(n):
        so = scanop[n % 3]
        O = Obuf[n % 2]
        for half in range(2):
            r0 = 64 * half
            rs = slice(r0, r0 + 64)
            kend = so["KLO"] if half == 0 else so["KHI"]
            psA = pbf(ringG.next()).rearrange("p (h d) -> p h d", h=4)
            for h in range(4):
                P.mm(psA[:, h, :], so["WKT"][:, h, :], Sb[:, h, :])
            P.tt("dve", ub[rs], so["UV"][rs], psA[rs], ALU.subtract)
            yield
            psS = pbf(ringG.next()).rearrange("p (h d) -> p h d", h=4)
            psO = pbf(ringG.next()).rearrange("p (h d) -> p h d", h=4)
            for h in range(4):
                P.mm(psS[:, h, :], kend[:, h, :], ub[:, h, :])
            for h in range(4):
                P.mm(psO[:, h, :], so["QD"][:, h, :], Sb[:, h, :], start=True, stop=False)
                P.mm(psO[:, h, :], so["ATT"][:, h, :], ub[:, h, :], start=False, stop=True)
            for h in range(4):
                P.stt(Sst[:, h, :], Sst[:, h, :], gend4[:, half, n, h:h + 1], psS[:, h, :], ALU.mult, ALU.add)
            P.copy("act", Sb, Sst)
            P.copy("act", O[rs], psO[rs])
            yield
        if n == 0:
            dump("O0", O.rearrange("p h d -> p (h d)"), [128, 512])
        if n == 15:
            dump("O15", O.rearrange("p h d -> p (h d)"), [128, 512])
        P.act(sqO, O, AF.Square)
        P.rsum(ssq, sqO)
        P.act(rsq, ssq, AF.Ln, bias=EPS, scale=1.0 / 128)
        P.act(rsq, rsq, AF.Exp, scale=-0.5)
        szg = szgb[(n // 4) % 2][:, n % 4, :].rearrange("p (h d) -> p h d", h=4)
        P.tt("dve", sqO, O, bc_last(rsq, 128), ALU.mult)
        P.tt("dve", oab, sqO, szg, ALU.mult)
        yield
        psT = pbb(ringG.next())[:, 0:512].rearrange("p (h d) -> p h d", h=4)
        for h in range(4):
            P.tr(psT[:, h, :], oab[:, h, :], IDENTb)
        P.copy("act", CT[:, 0:4, 128 * n:128 * (n + 1)], psT)
        yield

    def advance(must, opt):
        live = [True] * len(must)
        while any(live) or any(q[1] > 0 for q in opt):
            for gi, g in enumerate(must):
                if live[gi]:
                    try:
                        next(g)
                    except StopIteration:
                        live[gi] = False
            for q in opt:
                if q[1] > 0:
                    q[1] -= 1
                    try:
                        next(q[0])
                    except StopIteration:
                        q[1] = 0
                        q[2] = True

    P.memset("dve", Sst.rearrange("p h d -> p (h d)"), 0.0)
    P.memset("pool", Sb.rearrange("p h d -> p (h d)"), 0.0)
    P.memset("pool", ub.rearrange("p h d -> p (h d)"), 0.0)
    advance([G1(0)], [])
    SCALARS()
    g2 = {0: G2(0), 1: G2(1)}
    g1bg = [G1(1), 0, False]
    advance([g2[0]], [[g2[1], 7, False]])
    for n in range(NT):
        must = [SCAN(n)]
        if n + 1 < NT:
            must.append(g2[n + 1])
        opt = []
        if n + 2 < NT:
            g2[n + 2] = G2(n + 2)
            opt.append([g2[n + 2], 7, False])
        if n % 4 == 0 and n // 4 + 1 < 4:
            if n > 0:
                g1bg = [G1(n // 4 + 1), 0, False]
            g1bg[1] = 6
            opt.append(g1bg)
        elif n % 4 == 1 and n // 4 + 1 < 4:
            g1bg[1] = 1000
            opt.append(g1bg)
        advance(must, opt)
    dump("CTa", CT[:, 0:4, :].rearrange("p k t -> p (k t)"), [128, 4 * S])
    if stage == "G":
        return finish(nc, P, out_d, dump_tags)

    bB = Bump(A, SCR_LO, SCR_HI)
    wqkvb = bB.alloc([3, 8, 512], BF16)
    for c3 in range(3):
        P.dma("pool", wqkvb[:, c3, :, :],
              win_v[:, :, 2056 + 512 * c3:2056 + 512 * (c3 + 1)], "wqkvb%d" % c3)
    QT0 = bB.alloc([S], BF16)
    KTz = [bB.alloc([S], BF16) for _ in range(2)]
    QTb = [QT0, QT0]
    P.memset("pool", KTz[0][64:128, :], 0.0)
    P.memset("pool", KTz[1][0:64, :], 0.0)
    VAll = bB.alloc([48, 4, 192], BF16)
    PTbuf = Ring([bB.alloc([1024], BF16) for _ in range(2)])
    PT3 = bB.alloc([16, 128], BF16)
    rden0 = bB.alloc([512])
    rdenb = [rden0, rden0]
    P.memset("pool", VAll[:, :, :, 64:128], 1.0)
    ringS = Ring([2, 3, 4, 5])
    ringP = Ring([6, 7])
    accR = Ring([0, 1])

    def tok_slices():
        sl = []
        for n in range(16):
            sl.append(slice(128 * n, 128 * (n + 1), 1))
        for r in range(4):
            for c in range(4):
                sl.append(slice(512 * c + r, 512 * (c + 1), 4))
        for r in range(16):
            sl.append(slice(r, S, 16))
        return sl

    TOK = tok_slices()

    def PROJ_V():
        for t in range(48):
            bank = ringP.next()
            ps = pbf(bank)
            sl = TOK[t]
            for kc in range(8):
                P.mm(ps, hT[:, kc, sl], wqkvb[:, 2, kc, :], start=(kc == 0), stop=(kc == 7))
            ps4 = ps.rearrange("p (j e d) -> p j e d", j=4, e=2)
            P.copy("act", VAll[:, t, :, 0:64], ps4[:, :, 0, :])
            P.copy("dve", VAll[:, t, :, 128:192], ps4[:, :, 1, :])

    def PROJ_B(j):
        k = j % 2
        QT = QTb[k]
        for (dst, cbase) in ((QT, 128 * j), (None, 512 + 128 * j)):
            for tb in range(4):
                bank = ringP.next()
                ps = pbf(bank)
                for kc in range(8):
                    P.mm(ps, wqkvb[:, cbase // 512, kc, cbase % 512:cbase % 512 + 128], hT[:, kc, 512 * tb:512 * (tb + 1)],
                         start=(kc == 0), stop=(kc == 7))
                cs = slice(512 * tb, 512 * (tb + 1))
                if dst is not None:
                    P.copy("dve" if tb % 2 else "act", dst[:, cs], ps)
                else:
                    P.copy("act", KTz[0][0:64, cs], ps[0:64, :])
                    P.copy("dve", KTz[1][64:128, cs], ps[64:128, :])

    def ATTN(j):
        k = j % 2
        QT = QTb[k]

        class _VA:
            def __init__(self, h):
                self.h = h

            def __getitem__(self, idx):
                e_ = self.h % 2
                return VAll[:, idx[1], self.h // 2, 64 * e_:64 * e_ + 128]

        for e in range(2):
            hp = slice(64 * e, 64 * e + 64)
            KT = KTz[e]
            fp = slice(0, 128)
            VA = _VA(2 * j + e)
            for g in range(4):
                bank = ringS.next()
                ps = pbf(bank)
                P.mm(ps, IDENTb, MBA4b, start=True, stop=False)
                for r4 in range(4):
                    r = 4 * g + r4
                    P.mm(ps[:, 128 * r4:128 * (r4 + 1)], KT[fp, r:S:16], QT[fp, r:S:16], start=False, stop=(r4 == 3))
                P.act(PT3[:, 4 * g:4 * g + 4, :], ps.rearrange("p (a d) -> p a d", a=4), AF.Exp, scale=0.125)
            for c in range(4):
                acc = pbf(accR.next())
                first = [True]

                def pv(out, lhsT, rhs, last=False):
                    P.mm(out, lhsT, rhs, start=first[0], stop=last, skip_group_check=True)
                    first[0] = False
                pt = PTbuf.next()
                bank = ringS.next(); ps = pbf(bank)
                P.mm(ps, IDENTb, MBA4b, start=True, stop=False)
                for i in range(4):
                    n = 4 * c + i
                    P.mm(ps[:, 128 * i:128 * (i + 1)], KT[fp, 128 * n:128 * (n + 1)], QT[fp, 128 * n:128 * (n + 1)],
                         start=False, stop=(i == 3))
                P.act(pt[:, 0:512], ps, AF.Exp, scale=0.125)
                bank = ringS.next(); ps = pbf(bank)
                P.mm(ps, IDENTb, MBB4b, start=True, stop=False)
                for i in range(4):
                    n = 4 * c + i
                    if n == 0:
                        continue
                    P.mm(ps[:, 128 * i:128 * (i + 1)], KT[fp, 128 * (n - 1):128 * n], QT[fp, 128 * n:128 * (n + 1)],
                         start=False, stop=(i == 3))
                P.act(pt[:, 512:1024], ps, AF.Exp, scale=0.125)
                for i in range(4):
                    n = 4 * c + i
                    pv(acc[:, 128 * i:128 * (i + 1)], VA[:, n, e, :], pt[:, 128 * i:128 * (i + 1)])
                    if n > 0:
                        pv(acc[:, 128 * i:128 * (i + 1)], VA[:, n - 1, e, :], pt[:, 512 + 128 * i:512 + 128 * (i + 1)])
                pt = PTbuf.next()
                bank = ringS.next(); ps = pbf(bank)
                P.mm(ps, IDENTb, MBA4b, start=True, stop=False)
                for r in range(4):
                    sl = slice(512 * c + r, 512 * (c + 1), 4)
                    P.mm(ps[:, 128 * r:128 * (r + 1)], KT[fp, sl], QT[fp, sl], start=False, stop=(r == 3))
                P.act(pt[:, 0:512], ps, AF.Exp, scale=0.125)
                if c > 0:
                    bank = ringS.next(); ps = pbf(bank)
                    P.mm(ps, IDENTb, MBB4b, start=True, stop=False)
                    for r in range(4):
                        sl = slice(512 * c + r, 512 * (c + 1), 4)
                        slk = slice(512 * (c - 1) + r, 512 * c, 4)
                        P.mm(ps[:, 128 * r:128 * (r + 1)], KT[fp, slk], QT[fp, sl], start=False, stop=(r == 3))
                    P.act(pt[:, 512:1024], ps, AF.Exp, scale=0.125)
                for r in range(4):
                    pv(acc[:, r:512:4], VA[:, 16 + 4 * r + c, e, :], pt[:, 128 * r:128 * (r + 1)])
                    if c > 0:
                        pv(acc[:, r:512:4], VA[:, 16 + 4 * r + c - 1, e, :], pt[:, 512 + 128 * r:512 + 128 * (r + 1)])
                for r in range(16):
                    pv(acc[:, r:512:16], VA[:, 32 + r, e, :], PT3[:, r, 32 * c:32 * (c + 1)], last=(r == 15))
                num = slice(64 * e, 64 * e + 64)
                den = slice(64 * (1 - e), 64 * (1 - e) + 64)
                rd = rdenb[c % 2]
                P.recip(rd[den, :], acc[den, :])
                P.tt("dve", CT[num, 4 + j, 512 * c:512 * (c + 1)], acc[num, :], rd[den, :], ALU.mult)

    PROJ_V()
    for j in range(4):
        PROJ_B(j)
        ATTN(j)
    dump("CTb", CT[:, 4:8, :].rearrange("p k t -> p (k t)"), [128, 4 * S])
    if stage == "B":
        return finish(nc, P, out_d, dump_tags)

    bF = Bump(A, SCR_LO, SCR_HI)
    h2T = A.view(32768, [8, 1024], BF16)
    woutb = A.view(32768 + 16384, [2, 8, 512], BF16)
    X1 = bF.alloc([8, D])
    wu = [bF.alloc([2, 8, 512], BF16) for _ in range(2)]
    wd = [bF.alloc([4, D], BF16) for _ in range(2)]
    aTb = [bF.alloc([4, 1024], BF16) for _ in range(2)]
    rawF = [[bF.alloc([520]) for _ in range(2)] for _ in range(2)]
    accF = [[bF.alloc([512]) for _ in range(2)] for _ in range(2)]
    sgF = [bF.alloc([512]) for _ in range(2)]
    hbF = [bF.alloc([D], BF16), accF[1][1].bitcast(BF16)]
    junkF = sgF[0].bitcast(BF16)
    ssF = bF.alloc([8]); rsF = bF.alloc([8]); ssO = bF.alloc([8]); rsO = bF.alloc([8])
    for c2 in range(2):
        P.dma("pool", woutb[:, c2, :, :], wout_v[:, :, 512 * c2:512 * (c2 + 1)], "wout%d" % c2)
    P.dma("sp", NW1, n2_d, "c_nw1")
    P.dma("sp", NW2, nf_d, "c_nw2")
    groups = [list(range(g0, min(g0 + 4, 22))) for g0 in range(0, 22, 4)]
    out_tags = ["out%d" % i for i in range(8)]
    items = [(H, gi) for H in range(2) for gi in range(len(groups))]

    def load_wu(k):
        H, gi = items[k]
        grp = groups[gi]; g0 = grp[0]; npair = len(grp); slot = k % 2
        P.dma("pool", wu[slot][:, 0, :, 0:128 * npair], wup_v[:, :, 128 * g0:128 * (g0 + npair)], "wug%d" % slot)
        P.dma("pool", wu[slot][:, 1, :, 0:128 * npair],
              wup_v[:, :, 2816 + 128 * g0:2816 + 128 * (g0 + npair)], "wuu%d" % slot)

    def load_wd(k):
        H, gi = items[k]
        grp = groups[gi]; g0 = grp[0]; npair = len(grp); slot = k % 2
        P.dma("pool", wd[slot][:, 0:npair, :], wdn_v[:, g0:g0 + npair, :], "wd%d" % slot)

    def PRO(H):
        scr = [(junkF, ssF[:, i8:i8 + 1], rsF[:, i8:i8 + 1], hbF[i8 % 2]) for i8 in range(8)]

        def st_a(i8):
            i = 8 * H + i8
            P.dma("sp", X1[:, i8, :], x_d[128 * i:128 * (i + 1), :], "xf%d" % i8)
            b0 = 2 * (i8 % 2)
            for h2 in range(2):
                for kc in range(8):
                    P.mm(pbf(b0 + h2), CT[:, kc, 128 * i:128 * (i + 1)], woutb[:, h2, kc, :],
                         start=(kc == 0), stop=(kc == 7))
            for h2 in range(2):
                P.tt("dve", X1[:, i8, 512 * h2:512 * (h2 + 1)], X1[:, i8, 512 * h2:512 * (h2 + 1)], pbf(b0 + h2), ALU.add)
            if i == 0:
                dump("X1", X1[:, 0, :], [128, D])
            rmsnorm_stage1a(X1[:, i8, :], scr[i8])

        for t in range(10):
            if t < 8:
                st_a(t)
            if 1 <= t <= 8:
                rmsnorm_stage1b(X1[:, t - 1, :], NW1, scr[t - 1])
            if t >= 2:
                rmsnorm_stage2(h2T, 128 * (t - 2), scr[t - 2], 4 + (t % 2))

    upar = [0]

    def UGEN(k):
        H, gi = items[k]
        grp = groups[gi]; slot = k % 2; aT = aTb[k % 2]
        for p, g in enumerate(grp):
            for tb in range(2):
                par = upar[0]
                upar[0] ^= 1
                cols = slice(512 * tb, 512 * (tb + 1))
                banks = (4, 5) if par == 0 else (6, 7)
                accs = []
                for gu in range(2):
                    ps = pbf(banks[gu])
                    for kc in range(8):
                        P.mm(ps, wu[slot][:, gu, kc, 128 * p:128 * (p + 1)], h2T[:, kc, cols],
                             start=(kc == 0), stop=(kc == 7))
                    cc = g + 22 * gu
                    raw = rawF[par][gu]; acc = accF[par][gu]
                    P.copy("pool", raw[:, 0:2], HALOF[:, cc, :])
                    P.copy("act", raw[:, 2:514], ps)
                    P.copy("pool", HALOF[:, cc, :], raw[:, 512:514])
                    P.act(acc, ps, AF.Identity, scale=cwF[:, cc, 2:3])
                    P.stt(acc, raw[:, 1:513], cwF[:, cc, 1:2], acc, ALU.mult, ALU.add)
                    P.stt(acc, raw[:, 0:512], cwF[:, cc, 0:1], acc, ALU.mult, ALU.add)
                    accs.append(acc)
                sg = sgF[par]
                P.act(sg, accs[0], AF.Silu)
                P.tt("pool", aT[:, p, cols], sg, accs[1], ALU.mult)
                yield

    def DGEN(k):
        H, gi = items[k]
        grp = groups[gi]; slot = k % 2; aT = aTb[k % 2]; npair = len(grp)
        last = (gi == len(groups) - 1)
        for i8 in range(8):
            i = 8 * H + i8
            b0 = 2 * (i8 % 2)
            for h2 in range(2):
                for p in range(npair):
                    P.mm(pbf(b0 + h2), aT[:, p, 128 * i8:128 * (i8 + 1)], wd[slot][:, p, 512 * h2:512 * (h2 + 1)],
                         start=(p == 0), stop=(p == npair - 1))
            for h2 in range(2):
                P.tt("dve", X1[:, i8, 512 * h2:512 * (h2 + 1)], X1[:, i8, 512 * h2:512 * (h2 + 1)],
                     pbf(b0 + h2), ALU.add)
            if last:
                P.act(junkF, X1[:, i8, :], AF.Square, accum_out=ssO[:, i8:i8 + 1])
                P.act(rsO[:, i8:i8 + 1], ssO[:, i8:i8 + 1], AF.Ln, bias=EPS, scale=1.0 / D)
                P.act(rsO[:, i8:i8 + 1], rsO[:, i8:i8 + 1], AF.Exp, scale=-0.5)
                P.stt(X1[:, i8, :], X1[:, i8, :], rsO[:, i8:i8 + 1], NW2, ALU.mult, ALU.mult)
                P.dma("sp", out_d[128 * i:128 * (i + 1), :], X1[:, i8, :], "out%d" % i8)
            yield

    def rr(gens):
        gens = list(gens)
        live = [True] * len(gens)
        while any(live):
            for gi_, g_ in enumerate(gens):
                if live[gi_]:
                    try:
                        next(g_)
                    except StopIteration:
                        live[gi_] = False

    load_wu(0)
    prevD = None
    for k in range(len(items)):
        H, gi = items[k]
        load_wd(k)
        if k + 1 < len(items):
            load_wu(k + 1)
        if gi == 0:
            if prevD is not None:
                rr([prevD])
                prevD = None
            PRO(H)
        u = UGEN(k)
        rr([u] if prevD is None else [u, prevD])
        prevD = DGEN(k)
    rr([prevD])
    return finish(nc, P, out_d, dump_tags + out_tags)


def finish(nc, P, out_d, dump_tags):
    tags = list(dump_tags)
    P.emit(final_dma_tags=tags)
    return nc


def prep_inputs(inp):
    f = lambda a: np.ascontiguousarray(np.asarray(a, dtype=np.float32))
    x = f(inp["x"])
    rep = lambda v: np.ascontiguousarray(np.broadcast_to(f(v).reshape(1, -1), (128, f(v).size)))
    cwa = f(inp["conv_qkv_w"])[0]
    cwA = np.ascontiguousarray(cwa.T.reshape(12, 128, 4).transpose(1, 0, 2).reshape(128, 48))
    cwf = f(inp["ffn_conv_w"])[0]
    cwF = np.ascontiguousarray(cwf.T.reshape(44, 128, 3).transpose(1, 0, 2).reshape(128, 132))
    shared = {
        "w_in": f(inp["w_in"])[0], "w_out": f(inp["w_out"])[0], "w_up": f(inp["w_up"])[0],
        "w_down": f(inp["w_down"])[0],
        "n1rep": rep(inp["norm1_w"]), "n2rep": rep(inp["norm2_w"]), "nfrep": rep(inp["final_norm_w"]),
        "cwA": cwA, "cwF": cwF, "gnwrep": rep(inp["gdn_norm_w"]),
        "dtbrep": np.ascontiguousarray(np.tile(rep(inp["dt_bias"]), (1, 16))),
        "alogrep": np.ascontiguousarray(np.tile(rep(inp["a_log"]), (1, 16))),
        "consts": make_consts(),
    }
    maps = []
    for b in range(x.shape[0]):
        m = dict(shared)
        m["x"] = np.ascontiguousarray(x[b])
        maps.append(m)
    return maps


def kernel(**inputs):
    maps = prep_inputs(inputs)
    nc = build("full")
    res = run_bass_kernel_spmd(nc, maps, core_ids=list(range(8)))
    out = np.stack([np.asarray(r["out"], dtype=np.float32) for r in res.results], 0)
    return out
```

```python
import contextlib
import numpy as np
import concourse.bass as bass
import concourse.mybir as mybir
from concourse.bass_utils import run_bass_kernel_spmd

F32 = mybir.dt.float32
BF16 = mybir.dt.bfloat16
AF = mybir.ActivationFunctionType
ALU = mybir.AluOpType
AX = mybir.AxisListType

S = 2048
D = 1024
NT = 16
EPS = 1e-6
NEG = -30000.0
ENGS = ("pe", "act", "dve", "pool", "sp")


def _rect(ap):
    t = ap.tensor
    if str(ap.space) == "PSUM":
        return (t.name, 0, 128, 0, 2048)
    esz = mybir.dt.size(ap.dtype)
    pstride = 1
    for s in tuple(t.shape)[1:]:
        pstride *= s
    p0 = ap.start_partition()
    p1 = p0 + ap.partition_size()
    f0 = ap.offset - p0 * pstride
    ext = 0
    for (st, cnt) in tuple(ap.ap)[1:]:
        ext += abs(st) * (cnt - 1)
    return (t.name, p0, p1, f0 * esz, (f0 + ext + 1) * esz)


class _Op:
    __slots__ = ("eng", "fn", "idx", "is_dma", "tag", "waits", "signal", "sigval")


class Prog:
    def __init__(self, nc):
        self.nc = nc
        self.ops = {e: [] for e in ENGS}
        self.track = {}
        self.waited = {e: {} for e in ENGS}
        self.tagcount = {}

    def _add(self, eng, fn, reads, writes, is_dma=False, tag=None):
        op = _Op()
        op.eng = eng; op.fn = fn; op.is_dma = is_dma; op.tag = tag
        op.signal = False; op.sigval = None
        op.idx = len(self.ops[eng])
        op.waits = []
        deps = {}
        rrects = [_rect(a) for a in reads if a is not None and str(a.space) != "DRAM"]
        wrects = [_rect(a) for a in writes if a is not None and str(a.space) != "DRAM"]
        prects = [r for r in rrects if r[4] == 2048 and r[0].startswith("pb") and r not in wrects]
        for (nm, p0, p1, f0, f1) in rrects:
            for rec in self.track.get(nm, ()):
                if rec[5] == 1 and rec[0] < p1 and p0 < rec[1] and rec[2] < f1 and f0 < rec[3]:
                    deps[id(rec[4])] = (rec[4], True)
        for (nm, p0, p1, f0, f1) in wrects + prects:
            for rec in self.track.get(nm, ()):
                if rec[0] < p1 and p0 < rec[1] and rec[2] < f1 and f0 < rec[3]:
                    k = id(rec[4])
                    if k not in deps:
                        deps[k] = (rec[4], False)
        need = {}
        for (d, raw) in deps.values():
            if d.is_dma:
                key = ("dma", d.tag)
                val = self.tagcount[d.tag]
                if need.get(key, 0) < val:
                    need[key] = val
            else:
                if d.eng == eng and eng == "pe":
                    continue
                key = ("eng", d.eng)
                cur = need.get(key)
                if cur is None or cur.idx < d.idx:
                    need[key] = d
        w = self.waited[eng]
        for key, v in need.items():
            if key[0] == "dma":
                if w.get(key, 0) >= v:
                    continue
                w[key] = v
                op.waits.append((key, v))
            else:
                if w.get(key, -1) >= v.idx:
                    continue
                w[key] = v.idx
                v.signal = True
                op.waits.append((key, v))
        if is_dma:
            self.tagcount[tag] = self.tagcount.get(tag, 0) + 16
        for (nm, p0, p1, f0, f1) in wrects:
            lst = self.track.setdefault(nm, [])
            lst[:] = [r for r in lst if not (p0 <= r[0] and r[1] <= p1 and f0 <= r[2] and r[3] <= f1)]
            lst.append([p0, p1, f0, f1, op, 1])
        for (nm, p0, p1, f0, f1) in prects:
            self.track[nm] = [[p0, p1, f0, f1, op, 2]]
        for (nm, p0, p1, f0, f1) in rrects:
            if nm.startswith("pb"):
                continue
            lst = self.track.setdefault(nm, [])
            done = False
            for r in lst:
                if r[5] == 0 and r[4].eng == eng and (not r[4].is_dma) and (not is_dma) \
                        and r[0] == p0 and r[1] == p1 and r[2] == f0 and r[3] == f1:
                    r[4] = op
                    done = True
                    break
            if not done:
                lst.append([p0, p1, f0, f1, op, 0])
        self.ops[eng].append(op)
        return op

    def dma(self, q, out, in_, tag):
        return self._add(q, lambda e: e.dma_start(out=out, in_=in_), [in_], [out], is_dma=True, tag=tag)

    def mm(self, out, lhsT, rhs, start=True, stop=True, **kw):
        rd = [lhsT, rhs] + ([] if start else [out])
        return self._add("pe", lambda e: e.matmul(out, lhsT, rhs, start=start, stop=stop, **kw), rd, [out])

    def tr(self, out, in_, ident):
        return self._add("pe", lambda e: e.transpose(out, in_, ident), [in_, ident], [out])

    def act(self, out, in_, func, bias=None, scale=None, accum_out=None):
        kw = {}
        rd = [in_]
        if bias is not None:
            kw["bias"] = bias
            if not isinstance(bias, (int, float)):
                rd.append(bias)
        if scale is not None:
            kw["scale"] = scale
            if not isinstance(scale, (int, float)):
                rd.append(scale)
        wr = [out]
        if accum_out is not None:
            kw["accum_out"] = accum_out
            wr.append(accum_out)
        return self._add("act", lambda e: e.activation(out, in_, func, **kw), rd, wr)

    def tt(self, eng, out, in0, in1, op):
        return self._add(eng, lambda e: e.tensor_tensor(out, in0, in1, op), [in0, in1], [out])

    def ts(self, eng, out, in0, s1, s2, op0, op1=None):
        rd = [in0] + [s for s in (s1, s2) if s is not None and not isinstance(s, (int, float))]
        kw = {}
        if op1 is not None:
            kw["op1"] = op1
        return self._add(eng, lambda e: e.tensor_scalar(out, in0, s1, s2, op0, **kw), rd, [out])

    def stt(self, out, in0, scalar, in1, op0, op1):
        rd = [in0, in1] + ([] if isinstance(scalar, (int, float)) else [scalar])
        return self._add("dve", lambda e: e.scalar_tensor_tensor(out, in0, scalar, in1, op0, op1), rd, [out])

    def copy(self, eng, out, in_):
        if eng == "act":
            return self._add("act", lambda e: e.copy(out, in_), [in_], [out])
        return self._add(eng, lambda e: e.tensor_copy(out, in_), [in_], [out])

    def memset(self, eng, ap, val):
        return self._add(eng, lambda e: e.memset(ap, val), [], [ap])

    def recip(self, out, in_):
        return self._add("dve", lambda e: e.reciprocal(out, in_), [in_], [out])

    def rsum(self, out, in_):
        return self._add("dve", lambda e: e.tensor_reduce(out, in_, AX.X, ALU.add), [in_], [out])

    def emit(self, final_dma_tags=()):
        nc = self.nc
        for e in ENGS:
            c = 0
            for op in self.ops[e]:
                if op.signal and not op.is_dma:
                    c += 1
                    op.sigval = c
        with contextlib.ExitStack() as st:
            esem = {e: st.enter_context(nc.semaphore("s_" + e)) for e in ENGS}
            dsem = {t: st.enter_context(nc.semaphore("d_%d" % i)) for i, t in enumerate(self.tagcount)}
            block = st.enter_context(nc.Block())
            engobj = {"pe": block.tensor, "act": block.scalar, "dve": block.vector,
                      "pool": block.gpsimd, "sp": block.sync}

            def make(ename):
                def body(eng):
                    for op in self.ops[ename]:
                        for (key, v) in op.waits:
                            if key[0] == "dma":
                                eng.wait_ge(dsem[key[1]], v)
                            else:
                                eng.wait_ge(esem[key[1]], v.sigval)
                        ins = op.fn(eng)
                        if op.is_dma:
                            ins.then_inc(dsem[op.tag], 16)
                        elif op.signal:
                            ins.then_inc(esem[ename], 1)
                    if ename == "sp":
                        for t in final_dma_tags:
                            eng.wait_ge(dsem[t], self.tagcount[t])
                return body

            for e in ENGS:
                engobj[e](make(e))
        return nc


def _prod(shape):
    n = 1
    for s in shape:
        n *= s
    return n


class Arena:
    def __init__(self, nc, nbytes):
        self.t = nc.alloc_sbuf_tensor("arena", [128, nbytes // 4], F32)
        self.nbytes = nbytes

    def view(self, off, shape, dt):
        n = _prod(shape)
        esz = 4 if dt == F32 else 2
        assert off % 4 == 0 and (n * esz) % 4 == 0 and off + n * esz <= self.nbytes, (off, shape)
        ap = self.t[:, off // 4:(off + n * esz) // 4]
        if dt != F32:
            ap = ap.bitcast(dt)
        if len(shape) > 1:
            names = " ".join("d%d" % i for i in range(len(shape)))
            kw = {"d%d" % i: shape[i] for i in range(1, len(shape))}
            ap = ap.rearrange("p (%s) -> p %s" % (names, names), **kw)
        return ap


class Bump:
    def __init__(self, arena, lo, hi):
        self.a = arena; self.lo = lo; self.hi = hi; self.cur = lo

    def alloc(self, shape, dt=F32):
        esz = 4 if dt == F32 else 2
        n = (_prod(shape) * esz + 31) // 32 * 32
        off = self.cur
        self.cur += n
        assert self.cur <= self.hi, ("arena overflow", self.cur, self.hi)
        return self.a.view(off, shape, dt)


class Ring:
    def __init__(self, items):
        self.items = list(items); self.i = 0

    def next(self):
        x = self.items[self.i % len(self.items)]
        self.i += 1
        return x


_CNAMES = ["IDENT", "M1", "M2", "TRI", "SUF", "IND0", "IND1", "S32", "OFF", "ONES"]
_CB = {"MB64": 512, "MBA4": 512, "MBB4": 512}
NCF = 128 * len(_CNAMES)
NCONST = NCF + 512 * 3


def make_consts():
    i = np.arange(128)[:, None]
    j = np.arange(128)[None, :]
    same = (i // 64) == (j // 64)
    c = {}
    c["IDENT"] = (i == j)
    c["M1"] = (i > j)
    c["M2"] = (i <= j)
    c["TRI"] = (i <= j) & same
    c["SUF"] = (i > j) & same
    c["IND0"] = (i < 64) & (j >= 0)
    c["IND1"] = (i >= 64) & (j >= 0)
    c["S32"] = (i < j) & ((i // 32) == (j // 32))
    c["OFF"] = same & ((i % 64) < 32) & ((j % 64) >= 32)
    c["ONES"] = np.ones((128, 128), bool)
    cols = [c[n].astype(np.float32) for n in _CNAMES]
    mb64 = np.where((i <= j) & same, 0.0, NEG).astype(np.float32)
    mba = np.where(i <= j, 0.0, NEG).astype(np.float32)
    mbb = np.where(i >= j, 0.0, NEG).astype(np.float32)
    cols += [np.tile(mb64, (1, 4)), np.tile(mba, (1, 4)), np.tile(mbb, (1, 4))]
    return np.ascontiguousarray(np.concatenate(cols, 1))


def build(stage="full", dumps=()):
    nc = bass.Bass("TRN2", target_bir_lowering=False)
    P = Prog(nc)
    dumps = set(dumps)
    dump_tags = []

    def din(name, shape):
        return nc.dram_tensor(name, list(shape), F32, kind="ExternalInput").ap()

    x_d = din("x", [S, D])
    win_d = din("w_in", [D, 3592])
    wout_d = din("w_out", [D, D])
    wup_d = din("w_up", [D, 5632])
    wdn_d = din("w_down", [2816, D])
    n1_d = din("n1rep", [128, D])
    n2_d = din("n2rep", [128, D])
    nf_d = din("nfrep", [128, D])
    cwa_d = din("cwA", [128, 48])
    cwf_d = din("cwF", [128, 132])
    gnw_d = din("gnwrep", [128, 128])
    dtb_d = din("dtbrep", [128, 64])
    alog_d = din("alogrep", [128, 64])
    cst_d = din("consts", [128, NCONST])
    out_d = nc.dram_tensor("out", [S, D], F32, kind="ExternalOutput").ap()

    def dump(name, sb_ap, shape):
        if name not in dumps:
            return
        d = nc.dram_tensor("dbg_" + name, list(shape), sb_ap.dtype, kind="ExternalOutput").ap()
        tg = "dbg_" + name
        P.dma("sp", d, sb_ap, tg)
        dump_tags.append(tg)

    win_v = win_d.rearrange("(k p) c -> p k c", p=128)
    wout_v = wout_d.rearrange("(k p) c -> p k c", p=128)
    wup_v = wup_d.rearrange("(k p) c -> p k c", p=128)
    wdn_v = wdn_d.rearrange("(k p) c -> p k c", p=128)

    ARENA_BYTES = 207872
    A = Arena(nc, ARENA_BYTES)
    pb = [nc.alloc_psum_tensor("pb%d" % i, [128, 512], F32) for i in range(8)]

    def pbf(i):
        return pb[i][:]

    def pbb(i):
        return pb[i][:].bitcast(BF16)

    CT = A.view(0, [8, S], BF16)
    hT = A.view(32768, [8, S], BF16)
    pers = Bump(A, 65536, 86016)
    CF = pers.alloc([NCF])
    CBm = pers.alloc([1536 + 256], BF16)
    cwA = pers.alloc([12, 4])
    cwF = pers.alloc([44, 3])
    HALOA = pers.alloc([12, 3])
    HALOF = pers.alloc([44, 2])
    GNW = pers.alloc([128])
    NW1 = pers.alloc([D])
    NW2 = pers.alloc([D])
    SCR_LO = 86016
    SCR_HI = ARENA_BYTES

    def cf(name):
        k = _CNAMES.index(name)
        return CF[:, 128 * k:128 * (k + 1)]

    IDENT = cf("IDENT"); M1 = cf("M1"); M2 = cf("M2"); TRI = cf("TRI"); SUF = cf("SUF")
    IND0 = cf("IND0"); IND1 = cf("IND1"); S32 = cf("S32"); OFFM = cf("OFF"); ONES = cf("ONES")
    MB64b = CBm[:, 0:512]; MBA4b = CBm[:, 512:1024]; MBB4b = CBm[:, 1024:1536]
    IDENTb = CBm[:, 1536:1664]; ONESb = CBm[:, 1664:1792]

    P.dma("sp", CF, cst_d[:, 0:NCF], "c_cf")
    ctmp = A.view(16384 + 8192, [1536], F32)
    P.dma("sp", ctmp, cst_d[:, NCF:NCONST], "c_tmp")
    P.copy("dve", CBm[:, 0:1536], ctmp)
    P.copy("dve", IDENTb, IDENT)
    P.copy("dve", ONESb, ONES)
    P.dma("sp", cwA.rearrange("p a b -> p (a b)"), cwa_d, "c_cwa")
    P.dma("sp", cwF.rearrange("p a b -> p (a b)"), cwf_d, "c_cwf")
    P.dma("sp", GNW, gnw_d, "c_gnw")
    P.dma("sp", NW1, n1_d, "c_nw1")
    P.memset("pool", HALOA.rearrange("p a b -> p (a b)"), 0.0)
    P.memset("pool", HALOF.rearrange("p a b -> p (a b)"), 0.0)

    def bc_last(ap, n):
        return ap.unsqueeze(2).broadcast_to([128, ap.shape[1], n])

    def bc_mid(ap, n):
        return ap.unsqueeze(1).broadcast_to([128, n, ap.shape[1]])

    def rmsnorm_stage1(src_tile, wtile, scr):
        junk, ssv, rst, hb = scr
        P.act(junk, src_tile, AF.Square, accum_out=ssv)
        P.act(rst, ssv, AF.Ln, bias=EPS, scale=1.0 / D)
        P.act(rst, rst, AF.Exp, scale=-0.5)
        P.stt(hb, src_tile, rst, wtile, ALU.mult, ALU.mult)

    def rmsnorm_stage1a(src_tile, scr):
        junk, ssv, rst, hb = scr
        P.act(junk, src_tile, AF.Square, accum_out=ssv)
        P.act(rst, ssv, AF.Ln, bias=EPS, scale=1.0 / D)
        P.act(rst, rst, AF.Exp, scale=-0.5)

    def rmsnorm_stage1b(src_tile, wtile, scr):
        junk, ssv, rst, hb = scr
        P.stt(hb, src_tile, rst, wtile, ALU.mult, ALU.mult)

    def rmsnorm_stage2(dstT, col0, scr, bank):
        hb = scr[3]
        psT = pbb(bank)
        for kc in range(8):
            P.tr(psT[:, 128 * kc:128 * (kc + 1)], hb[:, 128 * kc:128 * (kc + 1)], IDENTb)
        P.copy("act", dstT[:, :, col0:col0 + 128], psT.rearrange("p (k t) -> p k t", k=8))

    bA = Bump(A, 0, 16384)
    xbuf = [bA.alloc([D]) for _ in range(3)]
    junkA = bA.alloc([D], BF16)
    hbuf = [NW2.bitcast(BF16)[:, 0:D], NW2.bitcast(BF16)[:, D:2 * D]]
    ssA = bA.alloc([NT])
    rsA = bA.alloc([NT])
    scrA = [(junkA, ssA[:, i:i + 1], rsA[:, i:i + 1], hbuf[i % 2]) for i in range(NT)]
    for i in range(NT + 2):
        if i < NT:
            xt = xbuf[i % 3]
            P.dma("sp", xt, x_d[128 * i:128 * (i + 1), :], "xa%d" % (i % 3))
            rmsnorm_stage1a(xt, scrA[i])
        if 1 <= i <= NT:
            rmsnorm_stage1b(xbuf[(i - 1) % 3], NW1, scrA[i - 1])
        if i >= 2:
            rmsnorm_stage2(hT, 128 * (i - 2), scrA[i - 2], 6 + (i % 2))
    dump("hT", hT.rearrange("p k t -> p (k t)"), [128, 8 * S])
    if stage == "A":
        return finish(nc, P, out_d, dump_tags)

    bG = Bump(A, SCR_LO, SCR_HI)
    bC = Bump(A, 16384, 32768)
    wqkva = bG.alloc([3, 8, 512], BF16)
    wba = bG.alloc([8, 8], BF16)
    wz = bC.alloc([8, 512], BF16)
    for c3 in range(3):
        P.dma("pool", wqkva[:, c3, :, :], win_v[:, :, 512 * c3:512 * (c3 + 1)], "wqkva%d" % c3)
    P.dma("pool", wba, win_v[:, :, 2048:2056], "wba")
    P.dma("pool", wz, win_v[:, :, 1536:2048], "wz")

    raw0 = bG.alloc([520]); acc0 = bG.alloc([512])
    rawb = [raw0, raw0]; accb = [acc0, acc0]
    sqb = bG.alloc([512], BF16)
    rtb = acc0
    qkvT = [dict(q=bG.alloc([4, 512], BF16), k=bG.alloc([4, 512], BF16), v=bG.alloc([4, 512], BF16))
            for _ in range(2)]
    BA = bG.alloc([NT, 8])
    sc_x = bG.alloc([64]); sc_mx = bG.alloc([64]); sc_mn = bG.alloc([64])
    dtb = bG.alloc([64]); negA = bG.alloc([64])
    gS = bG.alloc([64]); betaS = bG.alloc([64]); gamS = bG.alloc([64]); kesS = bG.alloc([64])
    gendS = bG.alloc([2, 64])
    gset = []
    for _ in range(2):
        gset.append(dict(
            G1m=bG.alloc([4, 128]),
            DECT=bG.alloc([4, 128], BF16), Uall=bG.alloc([4, 128], BF16), Eu=bG.alloc([4, 128], BF16),
            PTa=bG.alloc([4, 128], BF16), PTb=bG.alloc([4, 128], BF16),
            UL0=bG.alloc([2, 4, 128], BF16), PWA=bG.alloc([2, 4, 128], BF16), PWB=bG.alloc([2, 4, 128], BF16),
            TTb=bG.alloc([4, 128], BF16), vb=bG.alloc([4, 128], BF16), gk=bG.alloc([4, 128], BF16)))
    scanop = []
    for _ in range(3):
        scanop.append(dict(KLO=bG.alloc([4, 128], BF16), KHI=bG.alloc([4, 128], BF16), ATT=bG.alloc([4, 128], BF16),
                           QD=bG.alloc([4, 128], BF16), WKT=bG.alloc([4, 128], BF16),
                           UV=bG.alloc([4, 128], BF16)))
    Sst = bG.alloc([4, 128]); Sb = bG.alloc([4, 128], BF16); ub = bG.alloc([4, 128], BF16)
    Obuf = [bC.alloc([4, 128]) for _ in range(2)]
    sqO = bG.alloc([4, 128], BF16); oab = bG.alloc([4, 128], BF16)
    szgb = [bG.alloc([4, 512], BF16), bC.alloc([4, 512], BF16)]
    kesLo = bG.alloc([64]); kesHi = bG.alloc([64])
    ssq = bG.alloc([4]); rsq = bG.alloc([4])

    ringG = Ring([0, 1, 2, 3, 4, 5, 6, 7])

    g3 = gS.rearrange("p (n h) -> p n h", h=4)
    beta3 = betaS.rearrange("p (n h) -> p n h", h=4)
    gam3 = gamS.rearrange("p (n h) -> p n h", h=4)
    kesLo3 = kesLo.rearrange("p (n h) -> p n h", h=4)
    kesHi3 = kesHi.rearrange("p (n h) -> p n h", h=4)
    gend4 = gendS.rearrange("p a (n h) -> p a n h", h=4)

    def SCALARS():
        P.dma("sp", dtb, dtb_d, "c_dtb")
        P.dma("sp", negA, alog_d, "c_alog")
        P.act(negA, negA, AF.Exp)
        P.ts("dve", negA, negA, -1.0, None, ALU.mult)
        bk = ringG.next()
        psBA = pbf(bk)[:, 0:128].rearrange("p (n c) -> p n c", c=8)
        for i in range(NT):
            for kc in range(8):
                P.mm(psBA[:, i, :], hT[:, kc, 128 * i:128 * (i + 1)], wba[:, kc, :], start=(kc == 0), stop=(kc == 7))
        P.copy("act", BA, psBA)
        x3 = sc_x.rearrange("p (n h) -> p n h", h=4)
        P.tt("dve", x3, BA[:, :, 4:8], dtb.rearrange("p (n h) -> p n h", h=4), ALU.add)
        P.ts("dve", sc_mx, sc_x, 0.0, None, ALU.max)
        P.ts("dve", sc_mn, sc_x, 0.0, None, ALU.min)
        P.tt("dve", sc_mn, sc_mn, sc_mx, ALU.subtract)
        P.act(sc_mn, sc_mn, AF.Exp)
        P.act(sc_mn, sc_mn, AF.Ln, bias=1.0)
        P.tt("dve", sc_mx, sc_mx, sc_mn, ALU.add)
        P.tt("dve", gS, sc_mx, negA, ALU.mult)
        P.act(betaS.rearrange("p (n h) -> p n h", h=4), BA[:, :, 0:4], AF.Sigmoid)
        bk = ringG.next()
        psg = pbf(bk)
        P.mm(psg[:, 0:64], TRI, gS)
        P.mm(psg[:, 64:128], SUF, gS)
        P.mm(psg[:, 128:192], IND0, gS)
        P.mm(psg[:, 192:256], IND1, gS)
        P.act(gamS, psg[:, 0:64], AF.Exp)
        P.act(kesS, psg[:, 64:128], AF.Exp)
        P.tt("dve", kesS, kesS, betaS, ALU.mult)
        P.act(gendS.rearrange("p a b -> p (a b)"), psg[:, 128:256], AF.Exp)
        dump("gS", gS, [128, 64]); dump("betaS", betaS, [128, 64]); dump("gamS", gamS, [128, 64])
        dump("kesS", kesS, [128, 64]); dump("gendS", gendS.rearrange("p a b -> p (a b)"), [128, 128])
        P.ts("dve", kesLo, kesS, IND0[:, 0:1], None, ALU.mult)
        P.ts("dve", kesHi, kesS, IND1[:, 0:1], None, ALU.mult)


    def G1(m):
        t0 = 512 * m
        o = qkvT[m % 2]
        for c in range(12):
            ps = pbf(ringG.next())
            for kc in range(8):
                P.mm(ps, wqkva[:, c // 4, kc, 128 * (c % 4):128 * (c % 4 + 1)], hT[:, kc, t0:t0 + 512], start=(kc == 0), stop=(kc == 7))
            raw = rawb[c % 2]; acc = accb[c % 2]
            P.copy("pool", raw[:, 0:3], HALOA[:, c, :])
            P.copy("act", raw[:, 3:515], ps)
            P.copy("pool", HALOA[:, c, :], raw[:, 512:515])
            P.act(acc, ps, AF.Identity, scale=cwA[:, c, 3:4])
            P.stt(acc, raw[:, 2:514], cwA[:, c, 2:3], acc, ALU.mult, ALU.add)
            P.stt(acc, raw[:, 1:513], cwA[:, c, 1:2], acc, ALU.mult, ALU.add)
            P.stt(acc, raw[:, 0:512], cwA[:, c, 0:1], acc, ALU.mult, ALU.add)
            dst = (o["q"], o["k"], o["v"])[c // 4][:, c % 4, :]
            P.act(dst, acc, AF.Silu)
            if c % 3 == 2:
                nz = 4 * m + c // 3
                psZ = pbf(ringG.next())
                for kc in range(8):
                    P.mm(psZ, hT[:, kc, 128 * nz:128 * (nz + 1)], wz[:, kc, :], start=(kc == 0), stop=(kc == 7))
                szg = szgb[m % 2][:, c // 3, :]
                P.act(szg, psZ, AF.Silu)
                P.tt("pool", szg.rearrange("p (h d) -> p h d", h=4), szg.rearrange("p (h d) -> p h d", h=4),
                     bc_mid(GNW, 4), ALU.mult)
            yield
        for c in range(8):
            dst = (o["q"], o["k"])[c // 4][:, c % 4, :]
            sc = 128.0 if c < 4 else 1.0
            P.act(sqb, dst, AF.Square)
            psn = pbf(ringG.next())
            P.mm(psn, ONESb, sqb)
            P.act(rtb, psn, AF.Ln, bias=EPS * sc, scale=sc)
            P.act(rtb, rtb, AF.Exp, scale=-0.5)
            P.tt("dve", dst, dst, rtb, ALU.mult)
            yield
        if m == 0:
            dump("qnT0", o["q"].rearrange("p h t -> p (h t)"), [128, 2048])
            dump("knT0", o["k"].rearrange("p h t -> p (h t)"), [128, 2048])
            dump("vsT0", o["v"].rearrange("p h t -> p (h t)"), [128, 2048])

    def G2(n):
        tl = 128 * (n % 4)
        so = scanop[n % 3]
        st = gset[n % 2]
        qnT = qkvT[(n // 4) % 2]["q"]; knT = qkvT[(n // 4) % 2]["k"]; vsT = qkvT[(n // 4) % 2]["v"]
        G1m = st["G1m"]; DECT = st["DECT"]; Uall = st["Uall"]; Eu = st["Eu"]
        UL0 = st["UL0"]; PWA = st["PWA"]; PWB = st["PWB"]
        TTb = st["TTb"]; vb = st["vb"]; gk = st["gk"]
        Du, Dl = UL0[:, 0], UL0[:, 1]
        gam_b = bc_last(gam3[:, n, :], 128)
        keslo_b = bc_last(kesLo3[:, n, :], 128)
        keshi_b = bc_last(kesHi3[:, n, :], 128)
        beta_b = bc_last(beta3[:, n, :], 128)
        g_b = bc_last(g3[:, n, :], 128)

        def bankf():
            return pbf(ringG.next()).rearrange("p (h d) -> p h d", h=4)

        def bankb():
            return pbb(ringG.next())[:, 0:512].rearrange("p (h d) -> p h d", h=4)

        psKV = pbb(ringG.next()).rearrange("p (a h d) -> p a h d", a=2, h=4)
        psK = psKV[:, 0]; psV = psKV[:, 1]
        for h in range(4):
            P.tr(psK[:, h, :], knT[:, h, tl:tl + 128], IDENTb)
            P.tr(psV[:, h, :], vsT[:, h, tl:tl + 128], IDENTb)
        P.tt("dve", gk, psK, gam_b, ALU.mult)
        P.tt("dve", so["KLO"], psK, keslo_b, ALU.mult)
        P.tt("dve", so["KHI"], psK, keshi_b, ALU.mult)
        P.copy("act", vb, psV)
        yield
        P.tt("pool", G1m, bc_mid(M1, 4), g_b, ALU.mult)
        psD = bankf()
        P.mm(psD, IDENTb, MB64b, start=True, stop=False)
        for h in range(4):
            P.mm(psD[:, h, :], G1m[:, h, :], M2, start=False, stop=(h == 3))
        P.act(DECT, psD, AF.Exp)
        P.tt("pool", DECT, DECT, beta_b, ALU.mult)
        yield
        dg = Uall
        P.tt("pool", dg, bc_mid(IDENT, 4), gam_b, ALU.mult)
        psG = bankf()
        for h in range(4):
            P.mm(psG[:, h, :], ONESb, dg[:, h, :])
        P.tt("dve", so["QD"], qnT[:, :, tl:tl + 128], psG, ALU.mult)
        yield
        psKK = bankf(); psQK = bankf()
        for h in range(4):
            P.mm(psKK[:, h, :], knT[:, h, tl:tl + 128], knT[:, h, tl:tl + 128])
        for h in range(4):
            P.mm(psQK[:, h, :], knT[:, h, tl:tl + 128], qnT[:, h, tl:tl + 128])
        P.tt("dve", Uall, psKK, DECT, ALU.mult)
        P.tt("dve", so["ATT"], psQK, DECT, ALU.mult)
        P.tt("pool", Du, Uall, bc_mid(S32, 4), ALU.mult)
        P.tt("pool", Eu, Uall, bc_mid(OFFM, 4), ALU.mult)
        yield
        psT = bankb()
        for h in range(4):
            P.tr(psT[:, h, :], Du[:, h, :], IDENTb)
        P.copy("act", Dl, psT)
        P.tt("pool", st["PTa"], bc_mid(IDENT, 4), Du, ALU.subtract)
        yield
        pw = [UL0, PWA, PWB, PWA, PWB]
        PT, PTn = st["PTa"], st["PTb"]
        for k in range(1, 5):
            cur = pw[k - 1]; nxt = pw[k]
            if k < 4:
                psU = bankf()
                for h in range(4):
                    P.mm(psU[:, h, :], cur[:, 1, h, :], cur[:, 0, h, :])
            psL = bankf()
            for h in range(4):
                P.mm(psL[:, h, :], cur[:, 0, h, :], cur[:, 1, h, :])
            if k > 1:
                ps3 = bankf()
                for h in range(4):
                    P.mm(ps3[:, h, :], cur[:, 1, h, :], PT[:, h, :])
            if k < 4:
                P.copy("act", nxt[:, 0], psU)
            P.copy("act", nxt[:, 1], psL)
            if k > 1:
                P.tt("dve", PTn, PT, ps3, ALU.add)
                PT, PTn = PTn, PT
            yield
        ps3 = bankf()
        for h in range(4):
            P.mm(ps3[:, h, :], PWB[:, 1, h, :], PT[:, h, :])
        P.tt("dve", PTn, PT, ps3, ALU.add)
        PT, PTn = PTn, PT
        yield
        Pm = PWA[:, 0]; XT = DECT
        p1 = bankb()
        for h in range(4):
            P.tr(p1[:, h, :], PT[:, h, :], IDENTb)
        P.copy("act", Pm, p1)
        yield
        p2 = bankf()
        for h in range(4):
            P.mm(p2[:, h, :], Eu[:, h, :], Pm[:, h, :])
        P.copy("act", XT, p2)
        yield
        p3 = bankf()
        for h in range(4):
            P.mm(p3[:, h, :], XT[:, h, :], PT[:, h, :])
        P.tt("dve", TTb, PT, p3, ALU.subtract)
        yield
        p1 = bankf(); p2 = bankf()
        for h in range(4):
            P.mm(p1[:, h, :], TTb[:, h, :], vb[:, h, :])
        for h in range(4):
            P.mm(p2[:, h, :], gk[:, h, :], TTb[:, h, :])
        P.copy("act", so["UV"], p1)
        P.copy("dve", so["WKT"], p2)
        yield

    def SCAN(n):
        so = scanop[n % 3]
        O = Obuf[n % 2]
        for half in range(2):
            r0 = 64 * half
            rs = slice(r0, r0 + 64)
            kend = so["KLO"] if half == 0 else so["KHI"]
            psA = pbf(ringG.next()).rearrange("p (h d) -> p h d", h=4)
            for h in range(4):
                P.mm(psA[:, h, :], so["WKT"][:, h, :], Sb[:, h, :])
            P.tt("dve", ub[rs], so["UV"][rs], psA[rs], ALU.subtract)
            yield
            psS = pbf(ringG.next()).rearrange("p (h d) -> p h d", h=4)
            psO = pbf(ringG.next()).rearrange("p (h d) -> p h d", h=4)
            for h in range(4):
                P.mm(psS[:, h, :], kend[:, h, :], ub[:, h, :])
            for h in range(4):
                P.mm(psO[:, h, :], so["QD"][:, h, :], Sb[:, h, :], start=True, stop=False)
                P.mm(psO[:, h, :], so["ATT"][:, h, :], ub[:, h, :], start=False, stop=True)
            for h in range(4):
                P.stt(Sst[:, h, :], Sst[:, h, :], gend4[:, half, n, h:h + 1], psS[:, h, :], ALU.mult, ALU.add)
            P.copy("act", Sb, Sst)
            P.copy("act", O[rs], psO[rs])
            yield
        if n == 0:
            dump("O0", O.rearrange("p h d -> p (h d)"), [128, 512])
        if n == 15:
            dump("O15", O.rearrange("p h d -> p (h d)"), [128, 512])
        P.act(sqO, O, AF.Square)
        P.rsum(ssq, sqO)
        P.act(rsq, ssq, AF.Ln, bias=EPS, scale=1.0 / 128)
        P.act(rsq, rsq, AF.Exp, scale=-0.5)
        szg = szgb[(n // 4) % 2][:, n % 4, :].rearrange("p (h d) -> p h d", h=4)
        P.tt("dve", sqO, O, bc_last(rsq, 128), ALU.mult)
        P.tt("dve", oab, sqO, szg, ALU.mult)
        yield
        psT = pbb(ringG.next())[:, 0:512].rearrange("p (h d) -> p h d", h=4)
        for h in range(4):
            P.tr(psT[:, h, :], oab[:, h, :], IDENTb)
        P.copy("act", CT[:, 0:4, 128 * n:128 * (n + 1)], psT)
        yield

    def advance(must, opt):
        live = [True] * len(must)
        while any(live) or any(q[1] > 0 for q in opt):
            for gi, g in enumerate(must):
                if live[gi]:
                    try:
                        next(g)
                    except StopIteration:
                        live[gi] = False
            for q in opt:
                if q[1] > 0:
                    q[1] -= 1
                    try:
                        next(q[0])
                    except StopIteration:
                        q[1] = 0
                        q[2] = True

    P.memset("dve", Sst.rearrange("p h d -> p (h d)"), 0.0)
    P.memset("pool", Sb.rearrange("p h d -> p (h d)"), 0.0)
    P.memset("pool", ub.rearrange("p h d -> p (h d)"), 0.0)
    advance([G1(0)], [])
    SCALARS()
    g2 = {0: G2(0), 1: G2(1)}
    g1bg = [G1(1), 0, False]
    advance([g2[0]], [[g2[1], 7, False]])
    for n in range(NT):
        must = [SCAN(n)]
        if n + 1 < NT:
            must.append(g2[n + 1])
        opt = []
        if n + 2 < NT:
            g2[n + 2] = G2(n + 2)
            opt.append([g2[n + 2], 7, False])
        if n % 4 == 0 and n // 4 + 1 < 4:
            if n > 0:
                g1bg = [G1(n // 4 + 1), 0, False]
            g1bg[1] = 10
            opt.append(g1bg)
        elif n % 4 == 1 and n // 4 + 1 < 4:
            g1bg[1] = 1000
            opt.append(g1bg)
        advance(must, opt)
    dump("CTa", CT[:, 0:4, :].rearrange("p k t -> p (k t)"), [128, 4 * S])
    if stage == "G":
        return finish(nc, P, out_d, dump_tags)

    bB = Bump(A, SCR_LO, SCR_HI)
    wqkvb = bB.alloc([3, 8, 512], BF16)
    for c3 in range(3):
        P.dma("pool", wqkvb[:, c3, :, :],
              win_v[:, :, 2056 + 512 * c3:2056 + 512 * (c3 + 1)], "wqkvb%d" % c3)
    QT0 = bB.alloc([S], BF16)
    KTz = [bB.alloc([S], BF16) for _ in range(2)]
    QTb = [QT0, QT0]
    P.memset("pool", KTz[0][64:128, :], 0.0)
    P.memset("pool", KTz[1][0:64, :], 0.0)
    VAll = bB.alloc([48, 4, 192], BF16)
    PTbuf = Ring([bB.alloc([1024], BF16) for _ in range(2)])
    PT3 = bB.alloc([16, 128], BF16)
    rden0 = bB.alloc([512])
    rdenb = [rden0, rden0]
    P.memset("pool", VAll[:, :, :, 64:128], 1.0)
    ringS = Ring([2, 3, 4, 5])
    ringP = Ring([6, 7])
    accR = Ring([0, 1])

    def tok_slices():
        sl = []
        for n in range(16):
            sl.append(slice(128 * n, 128 * (n + 1), 1))
        for r in range(4):
            for c in range(4):
                sl.append(slice(512 * c + r, 512 * (c + 1), 4))
        for r in range(16):
            sl.append(slice(r, S, 16))
        return sl

    TOK = tok_slices()

    def PROJ_V():
        for t in range(48):
            bank = ringP.next()
            ps = pbf(bank)
            sl = TOK[t]
            for kc in range(8):
                P.mm(ps, hT[:, kc, sl], wqkvb[:, 2, kc, :], start=(kc == 0), stop=(kc == 7))
            ps4 = ps.rearrange("p (j e d) -> p j e d", j=4, e=2)
            P.copy("act", VAll[:, t, :, 0:64], ps4[:, :, 0, :])
            P.copy("dve", VAll[:, t, :, 128:192], ps4[:, :, 1, :])

    def PROJ_B(j):
        k = j % 2
        QT = QTb[k]
        for (dst, cbase) in ((QT, 128 * j), (None, 512 + 128 * j)):
            for tb in range(4):
                bank = ringP.next()
                ps = pbf(bank)
                for kc in range(8):
                    P.mm(ps, wqkvb[:, cbase // 512, kc, cbase % 512:cbase % 512 + 128], hT[:, kc, 512 * tb:512 * (tb + 1)],
                         start=(kc == 0), stop=(kc == 7))
                cs = slice(512 * tb, 512 * (tb + 1))
                if dst is not None:
                    P.copy("dve" if tb % 2 else "act", dst[:, cs], ps)
                else:
                    P.copy("act", KTz[0][0:64, cs], ps[0:64, :])
                    P.copy("dve", KTz[1][64:128, cs], ps[64:128, :])

    def ATTN(j):
        k = j % 2
        QT = QTb[k]

        class _VA:
            def __init__(self, h):
                self.h = h

            def __getitem__(self, idx):
                e_ = self.h % 2
                return VAll[:, idx[1], self.h // 2, 64 * e_:64 * e_ + 128]

        for e in range(2):
            hp = slice(64 * e, 64 * e + 64)
            KT = KTz[e]
            fp = slice(0, 128)
            VA = _VA(2 * j + e)
            for g in range(4):
                bank = ringS.next()
                ps = pbf(bank)
                P.mm(ps, IDENTb, MBA4b, start=True, stop=False)
                for r4 in range(4):
                    r = 4 * g + r4
                    P.mm(ps[:, 128 * r4:128 * (r4 + 1)], KT[fp, r:S:16], QT[fp, r:S:16], start=False, stop=(r4 == 3))
                P.act(PT3[:, 4 * g:4 * g + 4, :], ps.rearrange("p (a d) -> p a d", a=4), AF.Exp, scale=0.125)
            for c in range(4):
                acc = pbf(accR.next())
                first = [True]

                def pv(out, lhsT, rhs, last=False):
                    P.mm(out, lhsT, rhs, start=first[0], stop=last, skip_group_check=True)
                    first[0] = False
                pt = PTbuf.next()
                bank = ringS.next(); ps = pbf(bank)
                P.mm(ps, IDENTb, MBA4b, start=True, stop=False)
                for i in range(4):
                    n = 4 * c + i
                    P.mm(ps[:, 128 * i:128 * (i + 1)], KT[fp, 128 * n:128 * (n + 1)], QT[fp, 128 * n:128 * (n + 1)],
                         start=False, stop=(i == 3))
                P.act(pt[:, 0:512], ps, AF.Exp, scale=0.125)
                bank = ringS.next(); ps = pbf(bank)
                P.mm(ps, IDENTb, MBB4b, start=True, stop=False)
                for i in range(4):
                    n = 4 * c + i
                    if n == 0:
                        continue
                    P.mm(ps[:, 128 * i:128 * (i + 1)], KT[fp, 128 * (n - 1):128 * n], QT[fp, 128 * n:128 * (n + 1)],
                         start=False, stop=(i == 3))
                P.act(pt[:, 512:1024], ps, AF.Exp, scale=0.125)
                for i in range(4):
                    n = 4 * c + i
                    pv(acc[:, 128 * i:128 * (i + 1)], VA[:, n, e, :], pt[:, 128 * i:128 * (i + 1)])
                    if n > 0:
                        pv(acc[:, 128 * i:128 * (i + 1)], VA[:, n - 1, e, :], pt[:, 512 + 128 * i:512 + 128 * (i + 1)])
                pt = PTbuf.next()
                bank = ringS.next(); ps = pbf(bank)
                P.mm(ps, IDENTb, MBA4b, start=True, stop=False)
                for r in range(4):
                    sl = slice(512 * c + r, 512 * (c + 1), 4)
                    P.mm(ps[:, 128 * r:128 * (r + 1)], KT[fp, sl], QT[fp, sl], start=False, stop=(r == 3))
                P.act(pt[:, 0:512], ps, AF.Exp, scale=0.125)
                if c > 0:
                    bank = ringS.next(); ps = pbf(bank)
                    P.mm(ps, IDENTb, MBB4b, start=True, stop=False)
                    for r in range(4):
                        sl = slice(512 * c + r, 512 * (c + 1), 4)
                        slk = slice(512 * (c - 1) + r, 512 * c, 4)
                        P.mm(ps[:, 128 * r:128 * (r + 1)], KT[fp, slk], QT[fp, sl], start=False, stop=(r == 3))
                    P.act(pt[:, 512:1024], ps, AF.Exp, scale=0.125)
                for r in range(4):
                    pv(acc[:, r:512:4], VA[:, 16 + 4 * r + c, e, :], pt[:, 128 * r:128 * (r + 1)])
                    if c > 0:
                        pv(acc[:, r:512:4], VA[:, 16 + 4 * r + c - 1, e, :], pt[:, 512 + 128 * r:512 + 128 * (r + 1)])
                for r in range(16):
                    pv(acc[:, r:512:16], VA[:, 32 + r, e, :], PT3[:, r, 32 * c:32 * (c + 1)], last=(r == 15))
                num = slice(64 * e, 64 * e + 64)
                den = slice(64 * (1 - e), 64 * (1 - e) + 64)
                rd = rdenb[c % 2]
                P.recip(rd[den, :], acc[den, :])
                P.tt("dve", CT[num, 4 + j, 512 * c:512 * (c + 1)], acc[num, :], rd[den, :], ALU.mult)

    PROJ_V()
    for j in range(4):
        PROJ_B(j)
        ATTN(j)
    dump("CTb", CT[:, 4:8, :].rearrange("p k t -> p (k t)"), [128, 4 * S])
    if stage == "B":
        return finish(nc, P, out_d, dump_tags)

    bF = Bump(A, SCR_LO, SCR_HI)
    h2T = A.view(32768, [8, 1024], BF16)
    woutb = A.view(32768 + 16384, [2, 8, 512], BF16)
    X1 = bF.alloc([8, D])
    wu = [bF.alloc([2, 8, 512], BF16) for _ in range(2)]
    wd = [bF.alloc([4, D], BF16) for _ in range(2)]
    aTb = [bF.alloc([4, 1024], BF16) for _ in range(2)]
    rawF = [[bF.alloc([520]) for _ in range(2)] for _ in range(2)]
    accF = [[bF.alloc([512]) for _ in range(2)] for _ in range(2)]
    sgF = [bF.alloc([512]) for _ in range(2)]
    hbF = [bF.alloc([D], BF16), accF[1][1].bitcast(BF16)]
    junkF = sgF[0].bitcast(BF16)
    ssF = bF.alloc([8]); rsF = bF.alloc([8]); ssO = bF.alloc([8]); rsO = bF.alloc([8])
    for c2 in range(2):
        P.dma("pool", woutb[:, c2, :, :], wout_v[:, :, 512 * c2:512 * (c2 + 1)], "wout%d" % c2)
    P.dma("sp", NW1, n2_d, "c_nw1")
    P.dma("sp", NW2, nf_d, "c_nw2")
    groups = [list(range(g0, min(g0 + 4, 22))) for g0 in range(0, 22, 4)]
    out_tags = ["out%d" % i for i in range(8)]
    items = [(H, gi) for H in range(2) for gi in range(len(groups))]

    def load_wu(k):
        H, gi = items[k]
        grp = groups[gi]; g0 = grp[0]; npair = len(grp); slot = k % 2
        P.dma("pool", wu[slot][:, 0, :, 0:128 * npair], wup_v[:, :, 128 * g0:128 * (g0 + npair)], "wug%d" % slot)
        P.dma("pool", wu[slot][:, 1, :, 0:128 * npair],
              wup_v[:, :, 2816 + 128 * g0:2816 + 128 * (g0 + npair)], "wuu%d" % slot)

    def load_wd(k):
        H, gi = items[k]
        grp = groups[gi]; g0 = grp[0]; npair = len(grp); slot = k % 2
        P.dma("pool", wd[slot][:, 0:npair, :], wdn_v[:, g0:g0 + npair, :], "wd%d" % slot)

    def PRO(H):
        scr = [(junkF, ssF[:, i8:i8 + 1], rsF[:, i8:i8 + 1], hbF[i8 % 2]) for i8 in range(8)]

        def st_a(i8):
            i = 8 * H + i8
            P.dma("sp", X1[:, i8, :], x_d[128 * i:128 * (i + 1), :], "xf%d" % i8)
            b0 = 2 * (i8 % 2)
            for h2 in range(2):
                for kc in range(8):
                    P.mm(pbf(b0 + h2), CT[:, kc, 128 * i:128 * (i + 1)], woutb[:, h2, kc, :],
                         start=(kc == 0), stop=(kc == 7))
            for h2 in range(2):
                P.tt("dve", X1[:, i8, 512 * h2:512 * (h2 + 1)], X1[:, i8, 512 * h2:512 * (h2 + 1)], pbf(b0 + h2), ALU.add)
            if i == 0:
                dump("X1", X1[:, 0, :], [128, D])
            rmsnorm_stage1a(X1[:, i8, :], scr[i8])

        for t in range(10):
            if t < 8:
                st_a(t)
            if 1 <= t <= 8:
                rmsnorm_stage1b(X1[:, t - 1, :], NW1, scr[t - 1])
            if t >= 2:
                rmsnorm_stage2(h2T, 128 * (t - 2), scr[t - 2], 4 + (t % 2))

    upar = [0]

    def UGEN(k):
        H, gi = items[k]
        grp = groups[gi]; slot = k % 2; aT = aTb[k % 2]
        for p, g in enumerate(grp):
            for tb in range(2):
                par = upar[0]
                upar[0] ^= 1
                cols = slice(512 * tb, 512 * (tb + 1))
                banks = (4, 5) if par == 0 else (6, 7)
                accs = []
                for gu in range(2):
                    ps = pbf(banks[gu])
                    for kc in range(8):
                        P.mm(ps, wu[slot][:, gu, kc, 128 * p:128 * (p + 1)], h2T[:, kc, cols],
                             start=(kc == 0), stop=(kc == 7))
                    cc = g + 22 * gu
                    raw = rawF[par][gu]; acc = accF[par][gu]
                    P.copy("pool", raw[:, 0:2], HALOF[:, cc, :])
                    P.copy("act", raw[:, 2:514], ps)
                    P.copy("pool", HALOF[:, cc, :], raw[:, 512:514])
                    P.act(acc, ps, AF.Identity, scale=cwF[:, cc, 2:3])
                    P.stt(acc, raw[:, 1:513], cwF[:, cc, 1:2], acc, ALU.mult, ALU.add)
                    P.stt(acc, raw[:, 0:512], cwF[:, cc, 0:1], acc, ALU.mult, ALU.add)
                    accs.append(acc)
                sg = sgF[par]
                P.act(sg, accs[0], AF.Silu)
                P.tt("pool", aT[:, p, cols], sg, accs[1], ALU.mult)
                yield

    def DGEN(k):
        H, gi = items[k]
        grp = groups[gi]; slot = k % 2; aT = aTb[k % 2]; npair = len(grp)
        last = (gi == len(groups) - 1)
        for i8 in range(8):
            i = 8 * H + i8
            b0 = 2 * (i8 % 2)
            for h2 in range(2):
                for p in range(npair):
                    P.mm(pbf(b0 + h2), aT[:, p, 128 * i8:128 * (i8 + 1)], wd[slot][:, p, 512 * h2:512 * (h2 + 1)],
                         start=(p == 0), stop=(p == npair - 1))
            for h2 in range(2):
                P.tt("dve", X1[:, i8, 512 * h2:512 * (h2 + 1)], X1[:, i8, 512 * h2:512 * (h2 + 1)],
                     pbf(b0 + h2), ALU.add)
            if last:
                P.act(junkF, X1[:, i8, :], AF.Square, accum_out=ssO[:, i8:i8 + 1])
                P.act(rsO[:, i8:i8 + 1], ssO[:, i8:i8 + 1], AF.Ln, bias=EPS, scale=1.0 / D)
                P.act(rsO[:, i8:i8 + 1], rsO[:, i8:i8 + 1], AF.Exp, scale=-0.5)
                P.stt(X1[:, i8, :], X1[:, i8, :], rsO[:, i8:i8 + 1], NW2, ALU.mult, ALU.mult)
                P.dma("sp", out_d[128 * i:128 * (i + 1), :], X1[:, i8, :], "out%d" % i8)
            yield

    def rr(gens):
        gens = list(gens)
        live = [True] * len(gens)
        while any(live):
            for gi_, g_ in enumerate(gens):
                if live[gi_]:
                    try:
                        next(g_)
                    except StopIteration:
                        live[gi_] = False

    load_wu(0)
    prevD = None
    for k in range(len(items)):
        H, gi = items[k]
        load_wd(k)
        if k + 1 < len(items):
            load_wu(k + 1)
        if gi == 0:
            if prevD is not None:
                rr([prevD])
                prevD = None
            PRO(H)
        u = UGEN(k)
        rr([u] if prevD is None else [u, prevD])
        prevD = DGEN(k)
    rr([prevD])
    return finish(nc, P, out_d, dump_tags + out_tags)


def finish(nc, P, out_d, dump_tags):
    tags = list(dump_tags)
    P.emit(final_dma_tags=tags)
    return nc


def prep_inputs(inp):
    f = lambda a: np.ascontiguousarray(np.asarray(a, dtype=np.float32))
    x = f(inp["x"])
    rep = lambda v: np.ascontiguousarray(np.broadcast_to(f(v).reshape(1, -1), (128, f(v).size)))
    cwa = f(inp["conv_qkv_w"])[0]
    cwA = np.ascontiguousarray(cwa.T.reshape(12, 128, 4).transpose(1, 0, 2).reshape(128, 48))
    cwf = f(inp["ffn_conv_w"])[0]
    cwF = np.ascontiguousarray(cwf.T.reshape(44, 128, 3).transpose(1, 0, 2).reshape(128, 132))
    shared = {
        "w_in": f(inp["w_in"])[0], "w_out": f(inp["w_out"])[0], "w_up": f(inp["w_up"])[0],
        "w_down": f(inp["w_down"])[0],
        "n1rep": rep(inp["norm1_w"]), "n2rep": rep(inp["norm2_w"]), "nfrep": rep(inp["final_norm_w"]),
        "cwA": cwA, "cwF": cwF, "gnwrep": rep(inp["gdn_norm_w"]),
        "dtbrep": np.ascontiguousarray(np.tile(rep(inp["dt_bias"]), (1, 16))),
        "alogrep": np.ascontiguousarray(np.tile(rep(inp["a_log"]), (1, 16))),
        "consts": make_consts(),
    }
    maps = []
    for b in range(x.shape[0]):
        m = dict(shared)
        m["x"] = np.ascontiguousarray(x[b])
        maps.append(m)
    return maps


def kernel(**inputs):
    maps = prep_inputs(inputs)
    nc = build("full")
    res = run_bass_kernel_spmd(nc, maps, core_ids=list(range(8)))
    out = np.stack([np.asarray(r["out"], dtype=np.float32) for r in res.results], 0)
    return out
```

```python
import contextlib
import numpy as np
import concourse.bass as bass
import concourse.mybir as mybir
from concourse.bass_utils import run_bass_kernel_spmd

F32 = mybir.dt.float32
BF16 = mybir.dt.bfloat16
AF = mybir.ActivationFunctionType
ALU = mybir.AluOpType
AX = mybir.AxisListType

S = 2048
D = 1024
NT = 16
EPS = 1e-6
NEG = -30000.0
ENGS = ("pe", "act", "dve", "pool", "sp")


def _rect(ap):
    t = ap.tensor
    if str(ap.space) == "PSUM":
        return (t.name, 0, 128, 0, 2048)
    esz = mybir.dt.size(ap.dtype)
    pstride = 1
    for s in tuple(t.shape)[1:]:
        pstride *= s
    p0 = ap.start_partition()
    p1 = p0 + ap.partition_size()
    f0 = ap.offset - p0 * pstride
    ext = 0
    for (st, cnt) in tuple(ap.ap)[1:]:
        ext += abs(st) * (cnt - 1)
    return (t.name, p0, p1, f0 * esz, (f0 + ext + 1) * esz)


class _Op:
    __slots__ = ("eng", "fn", "idx", "is_dma", "tag", "waits", "signal", "sigval")


class Prog:
    def __init__(self, nc):
        self.nc = nc
        self.ops = {e: [] for e in ENGS}
        self.track = {}
        self.waited = {e: {} for e in ENGS}
        self.tagcount = {}

    def _add(self, eng, fn, reads, writes, is_dma=False, tag=None):
        op = _Op()
        op.eng = eng; op.fn = fn; op.is_dma = is_dma; op.tag = tag
        op.signal = False; op.sigval = None
        op.idx = len(self.ops[eng])
        op.waits = []
        deps = {}
        rrects = [_rect(a) for a in reads if a is not None and str(a.space) != "DRAM"]
        wrects = [_rect(a) for a in writes if a is not None and str(a.space) != "DRAM"]
        prects = [r for r in rrects if r[4] == 2048 and r[0].startswith("pb") and r not in wrects]
        for (nm, p0, p1, f0, f1) in rrects:
            for rec in self.track.get(nm, ()):
                if rec[5] == 1 and rec[0] < p1 and p0 < rec[1] and rec[2] < f1 and f0 < rec[3]:
                    deps[id(rec[4])] = (rec[4], True)
        for (nm, p0, p1, f0, f1) in wrects + prects:
            for rec in self.track.get(nm, ()):
                if rec[0] < p1 and p0 < rec[1] and rec[2] < f1 and f0 < rec[3]:
                    k = id(rec[4])
                    if k not in deps:
                        deps[k] = (rec[4], False)
        need = {}
        for (d, raw) in deps.values():
            if d.is_dma:
                key = ("dma", d.tag)
                val = self.tagcount[d.tag]
                if need.get(key, 0) < val:
                    need[key] = val
            else:
                if d.eng == eng and eng == "pe":
                    continue
                key = ("eng", d.eng)
                cur = need.get(key)
                if cur is None or cur.idx < d.idx:
                    need[key] = d
        w = self.waited[eng]
        for key, v in need.items():
            if key[0] == "dma":
                if w.get(key, 0) >= v:
                    continue
                w[key] = v
                op.waits.append((key, v))
            else:
                if w.get(key, -1) >= v.idx:
                    continue
                w[key] = v.idx
                v.signal = True
                op.waits.append((key, v))
        if is_dma:
            self.tagcount[tag] = self.tagcount.get(tag, 0) + 16
        for (nm, p0, p1, f0, f1) in wrects:
            lst = self.track.setdefault(nm, [])
            lst[:] = [r for r in lst if not (p0 <= r[0] and r[1] <= p1 and f0 <= r[2] and r[3] <= f1)]
            lst.append([p0, p1, f0, f1, op, 1])
        for (nm, p0, p1, f0, f1) in prects:
            self.track[nm] = [[p0, p1, f0, f1, op, 2]]
        for (nm, p0, p1, f0, f1) in rrects:
            if nm.startswith("pb"):
                continue
            lst = self.track.setdefault(nm, [])
            done = False
            for r in lst:
                if r[5] == 0 and r[4].eng == eng and (not r[4].is_dma) and (not is_dma) \
                        and r[0] == p0 and r[1] == p1 and r[2] == f0 and r[3] == f1:
                    r[4] = op
                    done = True
                    break
            if not done:
                lst.append([p0, p1, f0, f1, op, 0])
        self.ops[eng].append(op)
        return op

    def dma(self, q, out, in_, tag):
        return self._add(q, lambda e: e.dma_start(out=out, in_=in_), [in_], [out], is_dma=True, tag=tag)

    def mm(self, out, lhsT, rhs, start=True, stop=True, **kw):
        rd = [lhsT, rhs] + ([] if start else [out])
        return self._add("pe", lambda e: e.matmul(out, lhsT, rhs, start=start, stop=stop, **kw), rd, [out])

    def tr(self, out, in_, ident):
        return self._add("pe", lambda e: e.transpose(out, in_, ident), [in_, ident], [out])

    def act(self, out, in_, func, bias=None, scale=None, accum_out=None):
        kw = {}
        rd = [in_]
        if bias is not None:
            kw["bias"] = bias
            if not isinstance(bias, (int, float)):
                rd.append(bias)
        if scale is not None:
            kw["scale"] = scale
            if not isinstance(scale, (int, float)):
                rd.append(scale)
        wr = [out]
        if accum_out is not None:
            kw["accum_out"] = accum_out
            wr.append(accum_out)
        return self._add("act", lambda e: e.activation(out, in_, func, **kw), rd, wr)

    def tt(self, eng, out, in0, in1, op):
        return self._add(eng, lambda e: e.tensor_tensor(out, in0, in1, op), [in0, in1], [out])

    def ts(self, eng, out, in0, s1, s2, op0, op1=None):
        rd = [in0] + [s for s in (s1, s2) if s is not None and not isinstance(s, (int, float))]
        kw = {}
        if op1 is not None:
            kw["op1"] = op1
        return self._add(eng, lambda e: e.tensor_scalar(out, in0, s1, s2, op0, **kw), rd, [out])

    def stt(self, out, in0, scalar, in1, op0, op1):
        rd = [in0, in1] + ([] if isinstance(scalar, (int, float)) else [scalar])
        return self._add("dve", lambda e: e.scalar_tensor_tensor(out, in0, scalar, in1, op0, op1), rd, [out])

    def copy(self, eng, out, in_):
        if eng == "act":
            return self._add("act", lambda e: e.copy(out, in_), [in_], [out])
        return self._add(eng, lambda e: e.tensor_copy(out, in_), [in_], [out])

    def memset(self, eng, ap, val):
        return self._add(eng, lambda e: e.memset(ap, val), [], [ap])

    def recip(self, out, in_):
        return self._add("dve", lambda e: e.reciprocal(out, in_), [in_], [out])

    def rsum(self, out, in_):
        return self._add("dve", lambda e: e.tensor_reduce(out, in_, AX.X, ALU.add), [in_], [out])

    def emit(self, final_dma_tags=()):
        nc = self.nc
        for e in ENGS:
            c = 0
            for op in self.ops[e]:
                if op.signal and not op.is_dma:
                    c += 1
                    op.sigval = c
        with contextlib.ExitStack() as st:
            esem = {e: st.enter_context(nc.semaphore("s_" + e)) for e in ENGS}
            dsem = {t: st.enter_context(nc.semaphore("d_%d" % i)) for i, t in enumerate(self.tagcount)}
            block = st.enter_context(nc.Block())
            engobj = {"pe": block.tensor, "act": block.scalar, "dve": block.vector,
                      "pool": block.gpsimd, "sp": block.sync}

            def make(ename):
                def body(eng):
                    for op in self.ops[ename]:
                        for (key, v) in op.waits:
                            if key[0] == "dma":
                                eng.wait_ge(dsem[key[1]], v)
                            else:
                                eng.wait_ge(esem[key[1]], v.sigval)
                        ins = op.fn(eng)
                        if op.is_dma:
                            ins.then_inc(dsem[op.tag], 16)
                        elif op.signal:
                            ins.then_inc(esem[ename], 1)
                    if ename == "sp":
                        for t in final_dma_tags:
                            eng.wait_ge(dsem[t], self.tagcount[t])
                return body

            for e in ENGS:
                engobj[e](make(e))
        return nc


def _prod(shape):
    n = 1
    for s in shape:
        n *= s
    return n


class Arena:
    def __init__(self, nc, nbytes):
        self.t = nc.alloc_sbuf_tensor("arena", [128, nbytes // 4], F32)
        self.nbytes = nbytes

    def view(self, off, shape, dt):
        n = _prod(shape)
        esz = 4 if dt == F32 else 2
        assert off % 4 == 0 and (n * esz) % 4 == 0 and off + n * esz <= self.nbytes, (off, shape)
        ap = self.t[:, off // 4:(off + n * esz) // 4]
        if dt != F32:
            ap = ap.bitcast(dt)
        if len(shape) > 1:
            names = " ".join("d%d" % i for i in range(len(shape)))
            kw = {"d%d" % i: shape[i] for i in range(1, len(shape))}
            ap = ap.rearrange("p (%s) -> p %s" % (names, names), **kw)
        return ap


class Bump:
    def __init__(self, arena, lo, hi):
        self.a = arena; self.lo = lo; self.hi = hi; self.cur = lo

    def alloc(self, shape, dt=F32):
        esz = 4 if dt == F32 else 2
        n = (_prod(shape) * esz + 31) // 32 * 32
        off = self.cur
        self.cur += n
        assert self.cur <= self.hi, ("arena overflow", self.cur, self.hi)
        return self.a.view(off, shape, dt)


class Ring:
    def __init__(self, items):
        self.items = list(items); self.i = 0

    def next(self):
        x = self.items[self.i % len(self.items)]
        self.i += 1
        return x


_CNAMES = ["IDENT", "M1", "M2", "TRI", "SUF", "IND0", "IND1", "S32", "OFF", "ONES"]
_CB = {"MB64": 512, "MBA4": 512, "MBB4": 512}
NCF = 128 * len(_CNAMES)
NCONST = NCF + 512 * 3


def make_consts():
    i = np.arange(128)[:, None]
    j = np.arange(128)[None, :]
    same = (i // 64) == (j // 64)
    c = {}
    c["IDENT"] = (i == j)
    c["M1"] = (i > j)
    c["M2"] = (i <= j)
    c["TRI"] = (i <= j) & same
    c["SUF"] = (i > j) & same
    c["IND0"] = (i < 64) & (j >= 0)
    c["IND1"] = (i >= 64) & (j >= 0)
    c["S32"] = (i < j) & ((i // 32) == (j // 32))
    c["OFF"] = same & ((i % 64) < 32) & ((j % 64) >= 32)
    c["ONES"] = np.ones((128, 128), bool)
    cols = [c[n].astype(np.float32) for n in _CNAMES]
    mb64 = np.where((i <= j) & same, 0.0, NEG).astype(np.float32)
    mba = np.where(i <= j, 0.0, NEG).astype(np.float32)
    mbb = np.where(i >= j, 0.0, NEG).astype(np.float32)
    cols += [np.tile(mb64, (1, 4)), np.tile(mba, (1, 4)), np.tile(mbb, (1, 4))]
    return np.ascontiguousarray(np.concatenate(cols, 1))


def build(stage="full", dumps=()):
    nc = bass.Bass("TRN2", target_bir_lowering=False)
    P = Prog(nc)
    dumps = set(dumps)
    dump_tags = []

    def din(name, shape):
        return nc.dram_tensor(name, list(shape), F32, kind="ExternalInput").ap()

    x_d = din("x", [S, D])
    win_d = din("w_in", [D, 3592])
    wout_d = din("w_out", [D, D])
    wup_d = din("w_up", [D, 5632])
    wdn_d = din("w_down", [2816, D])
    n1_d = din("n1rep", [128, D])
    n2_d = din("n2rep", [128, D])
    nf_d = din("nfrep", [128, D])
    cwa_d = din("cwA", [128, 48])
    cwf_d = din("cwF", [128, 132])
    gnw_d = din("gnwrep", [128, 128])
    dtb_d = din("dtbrep", [128, 64])
    alog_d = din("alogrep", [128, 64])
    cst_d = din("consts", [128, NCONST])
    out_d = nc.dram_tensor("out", [S, D], F32, kind="ExternalOutput").ap()

    def dump(name, sb_ap, shape):
        if name not in dumps:
            return
        d = nc.dram_tensor("dbg_" + name, list(shape), sb_ap.dtype, kind="ExternalOutput").ap()
        tg = "dbg_" + name
        P.dma("sp", d, sb_ap, tg)
        dump_tags.append(tg)

    win_v = win_d.rearrange("(k p) c -> p k c", p=128)
    wout_v = wout_d.rearrange("(k p) c -> p k c", p=128)
    wup_v = wup_d.rearrange("(k p) c -> p k c", p=128)
    wdn_v = wdn_d.rearrange("(k p) c -> p k c", p=128)

    ARENA_BYTES = 207872
    A = Arena(nc, ARENA_BYTES)
    pb = [nc.alloc_psum_tensor("pb%d" % i, [128, 512], F32) for i in range(8)]

    def pbf(i):
        return pb[i][:]

    def pbb(i):
        return pb[i][:].bitcast(BF16)

    CT = A.view(0, [8, S], BF16)
    hT = A.view(32768, [8, S], BF16)
    pers = Bump(A, 65536, 86016)
    CF = pers.alloc([NCF])
    CBm = pers.alloc([1536 + 256], BF16)
    cwA = pers.alloc([12, 4])
    cwF = pers.alloc([44, 3])
    HALOA = pers.alloc([12, 3])
    HALOF = pers.alloc([44, 2])
    GNW = pers.alloc([128])
    NW1 = pers.alloc([D])
    NW2 = pers.alloc([D])
    SCR_LO = 86016
    SCR_HI = ARENA_BYTES

    def cf(name):
        k = _CNAMES.index(name)
        return CF[:, 128 * k:128 * (k + 1)]

    IDENT = cf("IDENT"); M1 = cf("M1"); M2 = cf("M2"); TRI = cf("TRI"); SUF = cf("SUF")
    IND0 = cf("IND0"); IND1 = cf("IND1"); S32 = cf("S32"); OFFM = cf("OFF"); ONES = cf("ONES")
    MB64b = CBm[:, 0:512]; MBA4b = CBm[:, 512:1024]; MBB4b = CBm[:, 1024:1536]
    IDENTb = CBm[:, 1536:1664]; ONESb = CBm[:, 1664:1792]

    P.dma("sp", CF, cst_d[:, 0:NCF], "c_cf")
    ctmp = A.view(16384 + 8192, [1536], F32)
    P.dma("sp", ctmp, cst_d[:, NCF:NCONST], "c_tmp")
    P.copy("dve", CBm[:, 0:1536], ctmp)
    P.copy("dve", IDENTb, IDENT)
    P.copy("dve", ONESb, ONES)
    P.dma("sp", cwA.rearrange("p a b -> p (a b)"), cwa_d, "c_cwa")
    P.dma("sp", cwF.rearrange("p a b -> p (a b)"), cwf_d, "c_cwf")
    P.dma("sp", GNW, gnw_d, "c_gnw")
    P.dma("sp", NW1, n1_d, "c_nw1")
    P.memset("pool", HALOA.rearrange("p a b -> p (a b)"), 0.0)
    P.memset("pool", HALOF.rearrange("p a b -> p (a b)"), 0.0)

    def bc_last(ap, n):
        return ap.unsqueeze(2).broadcast_to([128, ap.shape[1], n])

    def bc_mid(ap, n):
        return ap.unsqueeze(1).broadcast_to([128, n, ap.shape[1]])

    def rmsnorm_stage1(src_tile, wtile, scr):
        junk, ssv, rst, hb = scr
        P.act(junk, src_tile, AF.Square, accum_out=ssv)
        P.act(rst, ssv, AF.Ln, bias=EPS, scale=1.0 / D)
        P.act(rst, rst, AF.Exp, scale=-0.5)
        P.stt(hb, src_tile, rst, wtile, ALU.mult, ALU.mult)

    def rmsnorm_stage1a(src_tile, scr):
        junk, ssv, rst, hb = scr
        P.act(junk, src_tile, AF.Square, accum_out=ssv)
        P.act(rst, ssv, AF.Ln, bias=EPS, scale=1.0 / D)
        P.act(rst, rst, AF.Exp, scale=-0.5)

    def rmsnorm_stage1b(src_tile, wtile, scr):
        junk, ssv, rst, hb = scr
        P.stt(hb, src_tile, rst, wtile, ALU.mult, ALU.mult)

    def rmsnorm_stage2(dstT, col0, scr, bank):
        hb = scr[3]
        psT = pbb(bank)
        for kc in range(8):
            P.tr(psT[:, 128 * kc:128 * (kc + 1)], hb[:, 128 * kc:128 * (kc + 1)], IDENTb)
        P.copy("act", dstT[:, :, col0:col0 + 128], psT.rearrange("p (k t) -> p k t", k=8))

    bA = Bump(A, 0, 16384)
    xbuf = [bA.alloc([D]) for _ in range(3)]
    junkA = bA.alloc([D], BF16)
    hbuf = [NW2.bitcast(BF16)[:, 0:D], NW2.bitcast(BF16)[:, D:2 * D]]
    ssA = bA.alloc([NT])
    rsA = bA.alloc([NT])
    scrA = [(junkA, ssA[:, i:i + 1], rsA[:, i:i + 1], hbuf[i % 2]) for i in range(NT)]
    for i in range(NT + 2):
        if i < NT:
            xt = xbuf[i % 3]
            P.dma("sp", xt, x_d[128 * i:128 * (i + 1), :], "xa%d" % (i % 3))
            rmsnorm_stage1a(xt, scrA[i])
        if 1 <= i <= NT:
            rmsnorm_stage1b(xbuf[(i - 1) % 3], NW1, scrA[i - 1])
        if i >= 2:
            rmsnorm_stage2(hT, 128 * (i - 2), scrA[i - 2], 6 + (i % 2))
    dump("hT", hT.rearrange("p k t -> p (k t)"), [128, 8 * S])
    if stage == "A":
        return finish(nc, P, out_d, dump_tags)

    bG = Bump(A, SCR_LO, SCR_HI)
    bC = Bump(A, 16384, 32768)
    wz = bC.alloc([8, 512], BF16)
    qkvT = [dict(q=bG.alloc([4, 512], BF16), k=bG.alloc([4, 512], BF16), v=bG.alloc([4, 512], BF16))
            for _ in range(4)]
    szgb = [bG.alloc([4, 512], BF16), bG.alloc([4, 512], BF16), bG.alloc([4, 512], BF16), bC.alloc([4, 512], BF16)]
    BA = bG.alloc([NT, 8])
    sc_x = bG.alloc([64]); sc_mx = bG.alloc([64]); sc_mn = bG.alloc([64])
    dtb = bG.alloc([64]); negA = bG.alloc([64])
    gS = bG.alloc([64]); betaS = bG.alloc([64]); gamS = bG.alloc([64]); kesS = bG.alloc([64])
    gendS = bG.alloc([2, 64])
    wba = bG.alloc([8, 8], BF16)
    Sst = bG.alloc([4, 128]); Sb = bG.alloc([4, 128], BF16); ub = bG.alloc([4, 128], BF16)
    Obuf = [bC.alloc([4, 128]) for _ in range(2)]
    sqO = bG.alloc([4, 128], BF16); oab = sqO
    kesLo = bG.alloc([64]); kesHi = bG.alloc([64])
    ssq = bG.alloc([4]); rsq = bG.alloc([4])
    ov0 = bG.cur
    gset = []
    for _ in range(2):
        gset.append(dict(
            G1m=bG.alloc([4, 128]),
            DECT=bG.alloc([4, 128], BF16), Uall=bG.alloc([4, 128], BF16), Eu=bG.alloc([4, 128], BF16),
            PTa=bG.alloc([4, 128], BF16), PTb=bG.alloc([4, 128], BF16),
            UL0=bG.alloc([2, 4, 128], BF16), PWA=bG.alloc([2, 4, 128], BF16), PWB=bG.alloc([2, 4, 128], BF16),
            TTb=bG.alloc([4, 128], BF16), vb=bG.alloc([4, 128], BF16), gk=bG.alloc([4, 128], BF16)))
    scanop = []
    for _ in range(3):
        scanop.append(dict(KLO=bG.alloc([4, 128], BF16), KHI=bG.alloc([4, 128], BF16), ATT=bG.alloc([4, 128], BF16),
                           QD=bG.alloc([4, 128], BF16), WKT=bG.alloc([4, 128], BF16),
                           UV=bG.alloc([4, 128], BF16)))
    bO = Bump(A, ov0, bG.cur)
    wqkva = bO.alloc([3, 8, 512], BF16)
    rawb = [bO.alloc([520]) for _ in range(3)]
    accb = [bO.alloc([512]) for _ in range(3)]
    sqbs = [bO.alloc([512], BF16) for _ in range(2)]
    rtbs = [bO.alloc([512]) for _ in range(2)]
    for c3 in range(3):
        P.dma("pool", wqkva[:, c3, :, :], win_v[:, :, 512 * c3:512 * (c3 + 1)], "wqkva%d" % c3)
    P.dma("pool", wba, win_v[:, :, 2048:2056], "wba")
    P.dma("pool", wz, win_v[:, :, 1536:2048], "wz")

    ringG = Ring([0, 1, 2, 3, 4, 5, 6, 7])

    g3 = gS.rearrange("p (n h) -> p n h", h=4)
    beta3 = betaS.rearrange("p (n h) -> p n h", h=4)
    gam3 = gamS.rearrange("p (n h) -> p n h", h=4)
    kesLo3 = kesLo.rearrange("p (n h) -> p n h", h=4)
    kesHi3 = kesHi.rearrange("p (n h) -> p n h", h=4)
    gend4 = gendS.rearrange("p a (n h) -> p a n h", h=4)

    def SCALARS():
        P.dma("sp", dtb, dtb_d, "c_dtb")
        P.dma("sp", negA, alog_d, "c_alog")
        P.act(negA, negA, AF.Exp)
        P.ts("dve", negA, negA, -1.0, None, ALU.mult)
        bk = ringG.next()
        psBA = pbf(bk)[:, 0:128].rearrange("p (n c) -> p n c", c=8)
        for i in range(NT):
            for kc in range(8):
                P.mm(psBA[:, i, :], hT[:, kc, 128 * i:128 * (i + 1)], wba[:, kc, :], start=(kc == 0), stop=(kc == 7))
        P.copy("act", BA, psBA)
        x3 = sc_x.rearrange("p (n h) -> p n h", h=4)
        P.tt("dve", x3, BA[:, :, 4:8], dtb.rearrange("p (n h) -> p n h", h=4), ALU.add)
        P.ts("dve", sc_mx, sc_x, 0.0, None, ALU.max)
        P.ts("dve", sc_mn, sc_x, 0.0, None, ALU.min)
        P.tt("dve", sc_mn, sc_mn, sc_mx, ALU.subtract)
        P.act(sc_mn, sc_mn, AF.Exp)
        P.act(sc_mn, sc_mn, AF.Ln, bias=1.0)
        P.tt("dve", sc_mx, sc_mx, sc_mn, ALU.add)
        P.tt("dve", gS, sc_mx, negA, ALU.mult)
        P.act(betaS.rearrange("p (n h) -> p n h", h=4), BA[:, :, 0:4], AF.Sigmoid)
        bk = ringG.next()
        psg = pbf(bk)
        P.mm(psg[:, 0:64], TRI, gS)
        P.mm(psg[:, 64:128], SUF, gS)
        P.mm(psg[:, 128:192], IND0, gS)
        P.mm(psg[:, 192:256], IND1, gS)
        P.act(gamS, psg[:, 0:64], AF.Exp)
        P.act(kesS, psg[:, 64:128], AF.Exp)
        P.tt("dve", kesS, kesS, betaS, ALU.mult)
        P.act(gendS.rearrange("p a b -> p (a b)"), psg[:, 128:256], AF.Exp)
        dump("gS", gS, [128, 64]); dump("betaS", betaS, [128, 64]); dump("gamS", gamS, [128, 64])
        dump("kesS", kesS, [128, 64]); dump("gendS", gendS.rearrange("p a b -> p (a b)"), [128, 128])
        P.ts("dve", kesLo, kesS, IND0[:, 0:1], None, ALU.mult)
        P.ts("dve", kesHi, kesS, IND1[:, 0:1], None, ALU.mult)


    def G1(m):
        t0 = 512 * m
        o = qkvT[m]
        for c in range(12):
            ps = pbf(ringG.next())
            for kc in range(8):
                P.mm(ps, wqkva[:, c // 4, kc, 128 * (c % 4):128 * (c % 4 + 1)], hT[:, kc, t0:t0 + 512], start=(kc == 0), stop=(kc == 7))
            raw = rawb[(c + m) % 3]; acc = accb[(c + m) % 3]
            P.copy("pool", raw[:, 0:3], HALOA[:, c, :])
            P.copy("act", raw[:, 3:515], ps)
            P.copy("pool", HALOA[:, c, :], raw[:, 512:515])
            P.act(acc, ps, AF.Identity, scale=cwA[:, c, 3:4])
            P.stt(acc, raw[:, 2:514], cwA[:, c, 2:3], acc, ALU.mult, ALU.add)
            P.stt(acc, raw[:, 1:513], cwA[:, c, 1:2], acc, ALU.mult, ALU.add)
            P.stt(acc, raw[:, 0:512], cwA[:, c, 0:1], acc, ALU.mult, ALU.add)
            dst = (o["q"], o["k"], o["v"])[c // 4][:, c % 4, :]
            P.act(dst, acc, AF.Silu)
            if c % 3 == 2:
                nz = 4 * m + c // 3
                psZ = pbf(ringG.next())
                for kc in range(8):
                    P.mm(psZ, hT[:, kc, 128 * nz:128 * (nz + 1)], wz[:, kc, :], start=(kc == 0), stop=(kc == 7))
                szg = szgb[m][:, c // 3, :]
                P.act(szg, psZ, AF.Silu)
                P.tt("pool", szg.rearrange("p (h d) -> p h d", h=4), szg.rearrange("p (h d) -> p h d", h=4),
                     bc_mid(GNW, 4), ALU.mult)
            yield
        for c in range(8):
            dst = (o["q"], o["k"])[c // 4][:, c % 4, :]
            sc = 128.0 if c < 4 else 1.0
            sqb = sqbs[(c + m) % 2]; rtb = rtbs[(c + m) % 2]
            P.tt("pool", sqb, dst, dst, ALU.mult)
            psn = pbf(ringG.next())
            P.mm(psn, ONESb, sqb)
            P.act(rtb, psn, AF.Ln, bias=EPS * sc, scale=sc)
            P.act(rtb, rtb, AF.Exp, scale=-0.5)
            P.tt("dve", dst, dst, rtb, ALU.mult)
            yield
        if m == 0:
            dump("qnT0", o["q"].rearrange("p h t -> p (h t)"), [128, 2048])
            dump("knT0", o["k"].rearrange("p h t -> p (h t)"), [128, 2048])
            dump("vsT0", o["v"].rearrange("p h t -> p (h t)"), [128, 2048])

    def G2(n):
        tl = 128 * (n % 4)
        so = scanop[n % 3]
        st = gset[n % 2]
        qnT = qkvT[n // 4]["q"]; knT = qkvT[n // 4]["k"]; vsT = qkvT[n // 4]["v"]
        G1m = st["G1m"]; DECT = st["DECT"]; Uall = st["Uall"]; Eu = st["Eu"]
        UL0 = st["UL0"]; PWA = st["PWA"]; PWB = st["PWB"]
        TTb = st["TTb"]; vb = st["vb"]; gk = st["gk"]
        Du, Dl = UL0[:, 0], UL0[:, 1]
        gam_b = bc_last(gam3[:, n, :], 128)
        keslo_b = bc_last(kesLo3[:, n, :], 128)
        keshi_b = bc_last(kesHi3[:, n, :], 128)
        beta_b = bc_last(beta3[:, n, :], 128)
        g_b = bc_last(g3[:, n, :], 128)

        def bankf():
            return pbf(ringG.next()).rearrange("p (h d) -> p h d", h=4)

        def bankb():
            return pbb(ringG.next())[:, 0:512].rearrange("p (h d) -> p h d", h=4)

        psKV = pbb(ringG.next()).rearrange("p (a h d) -> p a h d", a=2, h=4)
        psK = psKV[:, 0]; psV = psKV[:, 1]
        for h in range(4):
            P.tr(psK[:, h, :], knT[:, h, tl:tl + 128], IDENTb)
            P.tr(psV[:, h, :], vsT[:, h, tl:tl + 128], IDENTb)
        P.tt("dve", gk, psK, gam_b, ALU.mult)
        P.tt("dve", so["KLO"], psK, keslo_b, ALU.mult)
        P.tt("dve", so["KHI"], psK, keshi_b, ALU.mult)
        P.copy("act", vb, psV)
        yield
        P.tt("pool", G1m, bc_mid(M1, 4), g_b, ALU.mult)
        psD = bankf()
        P.mm(psD, IDENTb, MB64b, start=True, stop=False)
        for h in range(4):
            P.mm(psD[:, h, :], G1m[:, h, :], M2, start=False, stop=(h == 3))
        P.act(DECT, psD, AF.Exp)
        P.tt("pool", DECT, DECT, beta_b, ALU.mult)
        yield
        dg = Uall
        P.tt("pool", dg, bc_mid(IDENT, 4), gam_b, ALU.mult)
        psG = bankf()
        for h in range(4):
            P.mm(psG[:, h, :], ONESb, dg[:, h, :])
        P.tt("dve", so["QD"], qnT[:, :, tl:tl + 128], psG, ALU.mult)
        yield
        psKK = bankf(); psQK = bankf()
        for h in range(4):
            P.mm(psKK[:, h, :], knT[:, h, tl:tl + 128], knT[:, h, tl:tl + 128])
        for h in range(4):
            P.mm(psQK[:, h, :], knT[:, h, tl:tl + 128], qnT[:, h, tl:tl + 128])
        P.tt("dve", Uall, psKK, DECT, ALU.mult)
        P.tt("dve", so["ATT"], psQK, DECT, ALU.mult)
        P.tt("pool", Du, Uall, bc_mid(S32, 4), ALU.mult)
        P.tt("pool", Eu, Uall, bc_mid(OFFM, 4), ALU.mult)
        yield
        psT = bankb()
        for h in range(4):
            P.tr(psT[:, h, :], Du[:, h, :], IDENTb)
        P.copy("act", Dl, psT)
        P.tt("pool", st["PTa"], bc_mid(IDENT, 4), Du, ALU.subtract)
        yield
        pw = [UL0, PWA, PWB, PWA, PWB]
        PT, PTn = st["PTa"], st["PTb"]
        for k in range(1, 5):
            cur = pw[k - 1]; nxt = pw[k]
            if k < 4:
                psU = bankf()
                for h in range(4):
                    P.mm(psU[:, h, :], cur[:, 1, h, :], cur[:, 0, h, :])
            psL = bankf()
            for h in range(4):
                P.mm(psL[:, h, :], cur[:, 0, h, :], cur[:, 1, h, :])
            if k > 1:
                ps3 = bankf()
                for h in range(4):
                    P.mm(ps3[:, h, :], cur[:, 1, h, :], PT[:, h, :])
            if k < 4:
                P.copy("act", nxt[:, 0], psU)
            P.copy("act", nxt[:, 1], psL)
            if k > 1:
                P.tt("dve", PTn, PT, ps3, ALU.add)
                PT, PTn = PTn, PT
            yield
        ps3 = bankf()
        for h in range(4):
            P.mm(ps3[:, h, :], PWB[:, 1, h, :], PT[:, h, :])
        P.tt("dve", PTn, PT, ps3, ALU.add)
        PT, PTn = PTn, PT
        yield
        Pm = PWA[:, 0]; XT = DECT
        p1 = bankb()
        for h in range(4):
            P.tr(p1[:, h, :], PT[:, h, :], IDENTb)
        P.copy("act", Pm, p1)
        yield
        p2 = bankf()
        for h in range(4):
            P.mm(p2[:, h, :], Eu[:, h, :], Pm[:, h, :])
        P.copy("act", XT, p2)
        yield
        p3 = bankf()
        for h in range(4):
            P.mm(p3[:, h, :], XT[:, h, :], PT[:, h, :])
        P.tt("dve", TTb, PT, p3, ALU.subtract)
        yield
        p1 = bankf(); p2 = bankf()
        for h in range(4):
            P.mm(p1[:, h, :], TTb[:, h, :], vb[:, h, :])
        for h in range(4):
            P.mm(p2[:, h, :], gk[:, h, :], TTb[:, h, :])
        P.copy("act", so["UV"], p1)
        P.copy("dve", so["WKT"], p2)
        yield

    def SCAN(n):
        so = scanop[n % 3]
        O = Obuf[n % 2]
        for half in range(2):
            r0 = 64 * half
            rs = slice(r0, r0 + 64)
            kend = so["KLO"] if half == 0 else so["KHI"]
            psA = pbf(ringG.next()).rearrange("p (h d) -> p h d", h=4)
            for h in range(4):
                P.mm(psA[:, h, :], so["WKT"][:, h, :], Sb[:, h, :])
            P.tt("dve", ub[rs], so["UV"][rs], psA[rs], ALU.subtract)
            yield
            psS = pbf(ringG.next()).rearrange("p (h d) -> p h d", h=4)
            psO = pbf(ringG.next()).rearrange("p (h d) -> p h d", h=4)
            for h in range(4):
                P.mm(psS[:, h, :], kend[:, h, :], ub[:, h, :])
            for h in range(4):
                P.mm(psO[:, h, :], so["QD"][:, h, :], Sb[:, h, :], start=True, stop=False)
                P.mm(psO[:, h, :], so["ATT"][:, h, :], ub[:, h, :], start=False, stop=True)
            for h in range(4):
                P.stt(Sst[:, h, :], Sst[:, h, :], gend4[:, half, n, h:h + 1], psS[:, h, :], ALU.mult, ALU.add)
            P.copy("act", Sb, Sst)
            P.copy("act", O[rs], psO[rs])
            yield
        if n == 0:
            dump("O0", O.rearrange("p h d -> p (h d)"), [128, 512])
        if n == 15:
            dump("O15", O.rearrange("p h d -> p (h d)"), [128, 512])
        P.act(sqO, O, AF.Square)
        P.rsum(ssq, sqO)
        P.act(rsq, ssq, AF.Ln, bias=EPS, scale=1.0 / 128)
        P.act(rsq, rsq, AF.Exp, scale=-0.5)
        szg = szgb[n // 4][:, n % 4, :].rearrange("p (h d) -> p h d", h=4)
        P.tt("dve", sqO, O, bc_last(rsq, 128), ALU.mult)
        P.tt("dve", oab, sqO, szg, ALU.mult)
        yield
        psT = pbb(ringG.next())[:, 0:512].rearrange("p (h d) -> p h d", h=4)
        for h in range(4):
            P.tr(psT[:, h, :], oab[:, h, :], IDENTb)
        P.copy("act", CT[:, 0:4, 128 * n:128 * (n + 1)], psT)
        yield

    def advance(must, opt):
        live = [True] * len(must)
        while any(live) or any(q[1] > 0 for q in opt):
            for gi, g in enumerate(must):
                if live[gi]:
                    try:
                        next(g)
                    except StopIteration:
                        live[gi] = False
            for q in opt:
                if q[1] > 0:
                    q[1] -= 1
                    try:
                        next(q[0])
                    except StopIteration:
                        q[1] = 0
                        q[2] = True

    P.memset("dve", Sst.rearrange("p h d -> p (h d)"), 0.0)
    P.memset("pool", Sb.rearrange("p h d -> p (h d)"), 0.0)
    P.memset("pool", ub.rearrange("p h d -> p (h d)"), 0.0)
    advance([G1(m_) for m_ in range(4)], [])
    SCALARS()
    g2 = {0: G2(0), 1: G2(1)}
    advance([g2[0]], [[g2[1], 7, False]])
    for n in range(NT):
        must = [SCAN(n)]
        if n + 1 < NT:
            must.append(g2[n + 1])
        opt = []
        if n + 2 < NT:
            g2[n + 2] = G2(n + 2)
            opt.append([g2[n + 2], 7, False])
        advance(must, opt)
    dump("CTa", CT[:, 0:4, :].rearrange("p k t -> p (k t)"), [128, 4 * S])
    if stage == "G":
        return finish(nc, P, out_d, dump_tags)

    bB = Bump(A, SCR_LO, SCR_HI)
    wqkvb = bB.alloc([3, 8, 512], BF16)
    for c3 in range(3):
        P.dma("pool", wqkvb[:, c3, :, :],
              win_v[:, :, 2056 + 512 * c3:2056 + 512 * (c3 + 1)], "wqkvb%d" % c3)
    QT0 = bB.alloc([S], BF16)
    KTz = [bB.alloc([S], BF16) for _ in range(2)]
    QTb = [QT0, QT0]
    P.memset("pool", KTz[0][64:128, :], 0.0)
    P.memset("pool", KTz[1][0:64, :], 0.0)
    VAll = bB.alloc([48, 4, 192], BF16)
    PTbuf = Ring([bB.alloc([1024], BF16) for _ in range(2)])
    PT3 = bB.alloc([16, 128], BF16)
    rden0 = bB.alloc([512])
    rdenb = [rden0, rden0]
    P.memset("pool", VAll[:, :, :, 64:128], 1.0)
    ringS = Ring([2, 3, 4, 5])
    ringP = Ring([6, 7])
    accR = Ring([0, 1])

    def tok_slices():
        sl = []
        for n in range(16):
            sl.append(slice(128 * n, 128 * (n + 1), 1))
        for r in range(4):
            for c in range(4):
                sl.append(slice(512 * c + r, 512 * (c + 1), 4))
        for r in range(16):
            sl.append(slice(r, S, 16))
        return sl

    TOK = tok_slices()

    def PROJ_V():
        for t in range(48):
            bank = ringP.next()
            ps = pbf(bank)
            sl = TOK[t]
            for kc in range(8):
                P.mm(ps, hT[:, kc, sl], wqkvb[:, 2, kc, :], start=(kc == 0), stop=(kc == 7))
            ps4 = ps.rearrange("p (j e d) -> p j e d", j=4, e=2)
            P.copy("act", VAll[:, t, :, 0:64], ps4[:, :, 0, :])
            P.copy("dve", VAll[:, t, :, 128:192], ps4[:, :, 1, :])

    def PROJ_B(j):
        k = j % 2
        QT = QTb[k]
        for (dst, cbase) in ((QT, 128 * j), (None, 512 + 128 * j)):
            for tb in range(4):
                bank = ringP.next()
                ps = pbf(bank)
                for kc in range(8):
                    P.mm(ps, wqkvb[:, cbase // 512, kc, cbase % 512:cbase % 512 + 128], hT[:, kc, 512 * tb:512 * (tb + 1)],
                         start=(kc == 0), stop=(kc == 7))
                cs = slice(512 * tb, 512 * (tb + 1))
                if dst is not None:
                    P.copy("dve" if tb % 2 else "act", dst[:, cs], ps)
                else:
                    P.copy("act", KTz[0][0:64, cs], ps[0:64, :])
                    P.copy("dve", KTz[1][64:128, cs], ps[64:128, :])

    def ATTN(j):
        k = j % 2
        QT = QTb[k]

        class _VA:
            def __init__(self, h):
                self.h = h

            def __getitem__(self, idx):
                e_ = self.h % 2
                return VAll[:, idx[1], self.h // 2, 64 * e_:64 * e_ + 128]

        for e in range(2):
            hp = slice(64 * e, 64 * e + 64)
            KT = KTz[e]
            fp = slice(0, 128)
            VA = _VA(2 * j + e)
            for g in range(4):
                bank = ringS.next()
                ps = pbf(bank)
                P.mm(ps, IDENTb, MBA4b, start=True, stop=False)
                for r4 in range(4):
                    r = 4 * g + r4
                    P.mm(ps[:, 128 * r4:128 * (r4 + 1)], KT[fp, r:S:16], QT[fp, r:S:16], start=False, stop=(r4 == 3))
                P.act(PT3[:, 4 * g:4 * g + 4, :], ps.rearrange("p (a d) -> p a d", a=4), AF.Exp, scale=0.125)
            for c in range(4):
                acc = pbf(accR.next())
                first = [True]

                def pv(out, lhsT, rhs, last=False):
                    P.mm(out, lhsT, rhs, start=first[0], stop=last, skip_group_check=True)
                    first[0] = False
                pt = PTbuf.next()
                bank = ringS.next(); ps = pbf(bank)
                P.mm(ps, IDENTb, MBA4b, start=True, stop=False)
                for i in range(4):
                    n = 4 * c + i
                    P.mm(ps[:, 128 * i:128 * (i + 1)], KT[fp, 128 * n:128 * (n + 1)], QT[fp, 128 * n:128 * (n + 1)],
                         start=False, stop=(i == 3))
                P.act(pt[:, 0:512], ps, AF.Exp, scale=0.125)
                bank = ringS.next(); ps = pbf(bank)
                P.mm(ps, IDENTb, MBB4b, start=True, stop=False)
                for i in range(4):
                    n = 4 * c + i
                    if n == 0:
                        continue
                    P.mm(ps[:, 128 * i:128 * (i + 1)], KT[fp, 128 * (n - 1):128 * n], QT[fp, 128 * n:128 * (n + 1)],
                         start=False, stop=(i == 3))
                P.act(pt[:, 512:1024], ps, AF.Exp, scale=0.125)
                for i in range(4):
                    n = 4 * c + i
                    pv(acc[:, 128 * i:128 * (i + 1)], VA[:, n, e, :], pt[:, 128 * i:128 * (i + 1)])
                    if n > 0:
                        pv(acc[:, 128 * i:128 * (i + 1)], VA[:, n - 1, e, :], pt[:, 512 + 128 * i:512 + 128 * (i + 1)])
                pt = PTbuf.next()
                bank = ringS.next(); ps = pbf(bank)
                P.mm(ps, IDENTb, MBA4b, start=True, stop=False)
                for r in range(4):
                    sl = slice(512 * c + r, 512 * (c + 1), 4)
                    P.mm(ps[:, 128 * r:128 * (r + 1)], KT[fp, sl], QT[fp, sl], start=False, stop=(r == 3))
                P.act(pt[:, 0:512], ps, AF.Exp, scale=0.125)
                if c > 0:
                    bank = ringS.next(); ps = pbf(bank)
                    P.mm(ps, IDENTb, MBB4b, start=True, stop=False)
                    for r in range(4):
                        sl = slice(512 * c + r, 512 * (c + 1), 4)
                        slk = slice(512 * (c - 1) + r, 512 * c, 4)
                        P.mm(ps[:, 128 * r:128 * (r + 1)], KT[fp, slk], QT[fp, sl], start=False, stop=(r == 3))
                    P.act(pt[:, 512:1024], ps, AF.Exp, scale=0.125)
                for r in range(4):
                    pv(acc[:, r:512:4], VA[:, 16 + 4 * r + c, e, :], pt[:, 128 * r:128 * (r + 1)])
                    if c > 0:
                        pv(acc[:, r:512:4], VA[:, 16 + 4 * r + c - 1, e, :], pt[:, 512 + 128 * r:512 + 128 * (r + 1)])
                for r in range(16):
                    pv(acc[:, r:512:16], VA[:, 32 + r, e, :], PT3[:, r, 32 * c:32 * (c + 1)], last=(r == 15))
                num = slice(64 * e, 64 * e + 64)
                den = slice(64 * (1 - e), 64 * (1 - e) + 64)
                rd = rdenb[c % 2]
                P.recip(rd[den, :], acc[den, :])
                P.tt("dve", CT[num, 4 + j, 512 * c:512 * (c + 1)], acc[num, :], rd[den, :], ALU.mult)

    PROJ_V()
    for j in range(4):
        PROJ_B(j)
        ATTN(j)
    dump("CTb", CT[:, 4:8, :].rearrange("p k t -> p (k t)"), [128, 4 * S])
    if stage == "B":
        return finish(nc, P, out_d, dump_tags)

    bF = Bump(A, SCR_LO, SCR_HI)
    h2T = A.view(32768, [8, 1024], BF16)
    woutb = A.view(32768 + 16384, [2, 8, 512], BF16)
    X1 = bF.alloc([8, D])
    wu = [bF.alloc([2, 8, 512], BF16) for _ in range(2)]
    wd = [bF.alloc([4, D], BF16) for _ in range(2)]
    aTb = [bF.alloc([4, 1024], BF16) for _ in range(2)]
    rawF = [[bF.alloc([520]) for _ in range(2)] for _ in range(2)]
    accF = [[bF.alloc([512]) for _ in range(2)] for _ in range(2)]
    sgF = [bF.alloc([512]) for _ in range(2)]
    hbF = [bF.alloc([D], BF16), accF[1][1].bitcast(BF16)]
    junkF = sgF[0].bitcast(BF16)
    ssF = bF.alloc([8]); rsF = bF.alloc([8]); ssO = bF.alloc([8]); rsO = bF.alloc([8])
    for c2 in range(2):
        P.dma("pool", woutb[:, c2, :, :], wout_v[:, :, 512 * c2:512 * (c2 + 1)], "wout%d" % c2)
    P.dma("sp", NW1, n2_d, "c_nw1")
    P.dma("sp", NW2, nf_d, "c_nw2")
    groups = [list(range(g0, min(g0 + 4, 22))) for g0 in range(0, 22, 4)]
    out_tags = ["out%d" % i for i in range(8)]
    items = [(H, gi) for H in range(2) for gi in range(len(groups))]

    def load_wu(k):
        H, gi = items[k]
        grp = groups[gi]; g0 = grp[0]; npair = len(grp); slot = k % 2
        P.dma("pool", wu[slot][:, 0, :, 0:128 * npair], wup_v[:, :, 128 * g0:128 * (g0 + npair)], "wug%d" % slot)
        P.dma("pool", wu[slot][:, 1, :, 0:128 * npair],
              wup_v[:, :, 2816 + 128 * g0:2816 + 128 * (g0 + npair)], "wuu%d" % slot)

    def load_wd(k):
        H, gi = items[k]
        grp = groups[gi]; g0 = grp[0]; npair = len(grp); slot = k % 2
        P.dma("pool", wd[slot][:, 0:npair, :], wdn_v[:, g0:g0 + npair, :], "wd%d" % slot)

    def PRO(H):
        scr = [(junkF, ssF[:, i8:i8 + 1], rsF[:, i8:i8 + 1], hbF[i8 % 2]) for i8 in range(8)]

        def st_a(i8):
            i = 8 * H + i8
            P.dma("sp", X1[:, i8, :], x_d[128 * i:128 * (i + 1), :], "xf%d" % i8)
            b0 = 2 * (i8 % 2)
            for h2 in range(2):
                for kc in range(8):
                    P.mm(pbf(b0 + h2), CT[:, kc, 128 * i:128 * (i + 1)], woutb[:, h2, kc, :],
                         start=(kc == 0), stop=(kc == 7))
            for h2 in range(2):
                P.tt("dve", X1[:, i8, 512 * h2:512 * (h2 + 1)], X1[:, i8, 512 * h2:512 * (h2 + 1)], pbf(b0 + h2), ALU.add)
            if i == 0:
                dump("X1", X1[:, 0, :], [128, D])
            rmsnorm_stage1a(X1[:, i8, :], scr[i8])

        for t in range(10):
            if t < 8:
                st_a(t)
            if 1 <= t <= 8:
                rmsnorm_stage1b(X1[:, t - 1, :], NW1, scr[t - 1])
            if t >= 2:
                rmsnorm_stage2(h2T, 128 * (t - 2), scr[t - 2], 4 + (t % 2))

    upar = [0]

    def UGEN(k):
        H, gi = items[k]
        grp = groups[gi]; slot = k % 2; aT = aTb[k % 2]
        for p, g in enumerate(grp):
            for tb in range(2):
                par = upar[0]
                upar[0] ^= 1
                cols = slice(512 * tb, 512 * (tb + 1))
                banks = (4, 5) if par == 0 else (6, 7)
                accs = []
                for gu in range(2):
                    ps = pbf(banks[gu])
                    for kc in range(8):
                        P.mm(ps, wu[slot][:, gu, kc, 128 * p:128 * (p + 1)], h2T[:, kc, cols],
                             start=(kc == 0), stop=(kc == 7))
                    cc = g + 22 * gu
                    raw = rawF[par][gu]; acc = accF[par][gu]
                    P.copy("pool", raw[:, 0:2], HALOF[:, cc, :])
                    P.copy("act", raw[:, 2:514], ps)
                    P.copy("pool", HALOF[:, cc, :], raw[:, 512:514])
                    P.act(acc, ps, AF.Identity, scale=cwF[:, cc, 2:3])
                    P.stt(acc, raw[:, 1:513], cwF[:, cc, 1:2], acc, ALU.mult, ALU.add)
                    P.stt(acc, raw[:, 0:512], cwF[:, cc, 0:1], acc, ALU.mult, ALU.add)
                    accs.append(acc)
                sg = sgF[par]
                P.act(sg, accs[0], AF.Silu)
                P.tt("pool", aT[:, p, cols], sg, accs[1], ALU.mult)
                yield

    def DGEN(k):
        H, gi = items[k]
        grp = groups[gi]; slot = k % 2; aT = aTb[k % 2]; npair = len(grp)
        last = (gi == len(groups) - 1)
        for i8 in range(8):
            i = 8 * H + i8
            b0 = 2 * (i8 % 2)
            for h2 in range(2):
                for p in range(npair):
                    P.mm(pbf(b0 + h2), aT[:, p, 128 * i8:128 * (i8 + 1)], wd[slot][:, p, 512 * h2:512 * (h2 + 1)],
                         start=(p == 0), stop=(p == npair - 1))
            for h2 in range(2):
                P.tt("dve", X1[:, i8, 512 * h2:512 * (h2 + 1)], X1[:, i8, 512 * h2:512 * (h2 + 1)],
                     pbf(b0 + h2), ALU.add)
            if last:
                P.act(junkF, X1[:, i8, :], AF.Square, accum_out=ssO[:, i8:i8 + 1])
                P.act(rsO[:, i8:i8 + 1], ssO[:, i8:i8 + 1], AF.Ln, bias=EPS, scale=1.0 / D)
                P.act(rsO[:, i8:i8 + 1], rsO[:, i8:i8 + 1], AF.Exp, scale=-0.5)
                P.stt(X1[:, i8, :], X1[:, i8, :], rsO[:, i8:i8 + 1], NW2, ALU.mult, ALU.mult)
                P.dma("sp", out_d[128 * i:128 * (i + 1), :], X1[:, i8, :], "out%d" % i8)
            yield

    def rr(gens):
        gens = list(gens)
        live = [True] * len(gens)
        while any(live):
            for gi_, g_ in enumerate(gens):
                if live[gi_]:
                    try:
                        next(g_)
                    except StopIteration:
                        live[gi_] = False

    load_wu(0)
    prevD = None
    for k in range(len(items)):
        H, gi = items[k]
        load_wd(k)
        if k + 1 < len(items):
            load_wu(k + 1)
        if gi == 0:
            if prevD is not None:
                rr([prevD])
                prevD = None
            PRO(H)
        u = UGEN(k)
        rr([u] if prevD is None else [u, prevD])
        prevD = DGEN(k)
    rr([prevD])
    return finish(nc, P, out_d, dump_tags + out_tags)


def finish(nc, P, out_d, dump_tags):
    tags = list(dump_tags)
    P.emit(final_dma_tags=tags)
    return nc


def prep_inputs(inp):
    f = lambda a: np.ascontiguousarray(np.asarray(a, dtype=np.float32))
    x = f(inp["x"])
    rep = lambda v: np.ascontiguousarray(np.broadcast_to(f(v).reshape(1, -1), (128, f(v).size)))
    cwa = f(inp["conv_qkv_w"])[0]
    cwA = np.ascontiguousarray(cwa.T.reshape(12, 128, 4).transpose(1, 0, 2).reshape(128, 48))
    cwf = f(inp["ffn_conv_w"])[0]
    cwF = np.ascontiguousarray(cwf.T.reshape(44, 128, 3).transpose(1, 0, 2).reshape(128, 132))
    shared = {
        "w_in": f(inp["w_in"])[0], "w_out": f(inp["w_out"])[0], "w_up": f(inp["w_up"])[0],
        "w_down": f(inp["w_down"])[0],
        "n1rep": rep(inp["norm1_w"]), "n2rep": rep(inp["norm2_w"]), "nfrep": rep(inp["final_norm_w"]),
        "cwA": cwA, "cwF": cwF, "gnwrep": rep(inp["gdn_norm_w"]),
        "dtbrep": np.ascontiguousarray(np.tile(rep(inp["dt_bias"]), (1, 16))),
        "alogrep": np.ascontiguousarray(np.tile(rep(inp["a_log"]), (1, 16))),
        "consts": make_consts(),
    }
    maps = []
    for b in range(x.shape[0]):
        m = dict(shared)
        m["x"] = np.ascontiguousarray(x[b])
        maps.append(m)
    return maps


def kernel(**inputs):
    maps = prep_inputs(inputs)
    nc = build("full")
    res = run_bass_kernel_spmd(nc, maps, core_ids=list(range(8)))
    out = np.stack([np.asarray(r["out"], dtype=np.float32) for r in res.results], 0)
    return out
```

```python
import contextlib
import numpy as np
import concourse.bass as bass
import concourse.mybir as mybir
from concourse.bass_utils import run_bass_kernel_spmd

F32 = mybir.dt.float32
BF16 = mybir.dt.bfloat16
AF = mybir.ActivationFunctionType
ALU = mybir.AluOpType
AX = mybir.AxisListType

S = 2048
D = 1024
NT = 16
EPS = 1e-6
NEG = -30000.0
ENGS = ("pe", "act", "dve", "pool", "sp")


def _rect(ap):
    t = ap.tensor
    if str(ap.space) == "PSUM":
        return (t.name, 0, 128, 0, 2048)
    esz = mybir.dt.size(ap.dtype)
    pstride = 1
    for s in tuple(t.shape)[1:]:
        pstride *= s
    p0 = ap.start_partition()
    p1 = p0 + ap.partition_size()
    f0 = ap.offset - p0 * pstride
    ext = 0
    for (st, cnt) in tuple(ap.ap)[1:]:
        ext += abs(st) * (cnt - 1)
    return (t.name, p0, p1, f0 * esz, (f0 + ext + 1) * esz)


class _Op:
    __slots__ = ("eng", "fn", "idx", "is_dma", "tag", "waits", "signal", "sigval")


class Prog:
    def __init__(self, nc):
        self.nc = nc
        self.ops = {e: [] for e in ENGS}
        self.track = {}
        self.waited = {e: {} for e in ENGS}
        self.tagcount = {}

    def _add(self, eng, fn, reads, writes, is_dma=False, tag=None):
        op = _Op()
        op.eng = eng; op.fn = fn; op.is_dma = is_dma; op.tag = tag
        op.signal = False; op.sigval = None
        op.idx = len(self.ops[eng])
        op.waits = []
        deps = {}
        rrects = [_rect(a) for a in reads if a is not None and str(a.space) != "DRAM"]
        wrects = [_rect(a) for a in writes if a is not None and str(a.space) != "DRAM"]
        prects = [r for r in rrects if r[4] == 2048 and r[0].startswith("pb") and r not in wrects]
        for (nm, p0, p1, f0, f1) in rrects:
            for rec in self.track.get(nm, ()):
                if rec[5] == 1 and rec[0] < p1 and p0 < rec[1] and rec[2] < f1 and f0 < rec[3]:
                    deps[id(rec[4])] = (rec[4], True)
        for (nm, p0, p1, f0, f1) in wrects + prects:
            for rec in self.track.get(nm, ()):
                if rec[0] < p1 and p0 < rec[1] and rec[2] < f1 and f0 < rec[3]:
                    k = id(rec[4])
                    if k not in deps:
                        deps[k] = (rec[4], False)
        need = {}
        for (d, raw) in deps.values():
            if d.is_dma:
                key = ("dma", d.tag)
                val = self.tagcount[d.tag]
                if need.get(key, 0) < val:
                    need[key] = val
            else:
                if d.eng == eng and eng == "pe":
                    continue
                key = ("eng", d.eng)
                cur = need.get(key)
                if cur is None or cur.idx < d.idx:
                    need[key] = d
        w = self.waited[eng]
        for key, v in need.items():
            if key[0] == "dma":
                if w.get(key, 0) >= v:
                    continue
                w[key] = v
                op.waits.append((key, v))
            else:
                if w.get(key, -1) >= v.idx:
                    continue
                w[key] = v.idx
                v.signal = True
                op.waits.append((key, v))
        if is_dma:
            self.tagcount[tag] = self.tagcount.get(tag, 0) + 16
        for (nm, p0, p1, f0, f1) in wrects:
            lst = self.track.setdefault(nm, [])
            lst[:] = [r for r in lst if not (p0 <= r[0] and r[1] <= p1 and f0 <= r[2] and r[3] <= f1)]
            lst.append([p0, p1, f0, f1, op, 1])
        for (nm, p0, p1, f0, f1) in prects:
            self.track[nm] = [[p0, p1, f0, f1, op, 2]]
        for (nm, p0, p1, f0, f1) in rrects:
            if nm.startswith("pb"):
                continue
            lst = self.track.setdefault(nm, [])
            done = False
            for r in lst:
                if r[5] == 0 and r[4].eng == eng and (not r[4].is_dma) and (not is_dma) \
                        and r[0] == p0 and r[1] == p1 and r[2] == f0 and r[3] == f1:
                    r[4] = op
                    done = True
                    break
            if not done:
                lst.append([p0, p1, f0, f1, op, 0])
        self.ops[eng].append(op)
        return op

    def dma(self, q, out, in_, tag):
        return self._add(q, lambda e: e.dma_start(out=out, in_=in_), [in_], [out], is_dma=True, tag=tag)

    def mm(self, out, lhsT, rhs, start=True, stop=True, **kw):
        rd = [lhsT, rhs] + ([] if start else [out])
        return self._add("pe", lambda e: e.matmul(out, lhsT, rhs, start=start, stop=stop, **kw), rd, [out])

    def tr(self, out, in_, ident):
        return self._add("pe", lambda e: e.transpose(out, in_, ident), [in_, ident], [out])

    def act(self, out, in_, func, bias=None, scale=None, accum_out=None):
        kw = {}
        rd = [in_]
        if bias is not None:
            kw["bias"] = bias
            if not isinstance(bias, (int, float)):
                rd.append(bias)
        if scale is not None:
            kw["scale"] = scale
            if not isinstance(scale, (int, float)):
                rd.append(scale)
        wr = [out]
        if accum_out is not None:
            kw["accum_out"] = accum_out
            wr.append(accum_out)
        return self._add("act", lambda e: e.activation(out, in_, func, **kw), rd, wr)

    def tt(self, eng, out, in0, in1, op):
        return self._add(eng, lambda e: e.tensor_tensor(out, in0, in1, op), [in0, in1], [out])

    def ts(self, eng, out, in0, s1, s2, op0, op1=None):
        rd = [in0] + [s for s in (s1, s2) if s is not None and not isinstance(s, (int, float))]
        kw = {}
        if op1 is not None:
            kw["op1"] = op1
        return self._add(eng, lambda e: e.tensor_scalar(out, in0, s1, s2, op0, **kw), rd, [out])

    def stt(self, out, in0, scalar, in1, op0, op1):
        rd = [in0, in1] + ([] if isinstance(scalar, (int, float)) else [scalar])
        return self._add("dve", lambda e: e.scalar_tensor_tensor(out, in0, scalar, in1, op0, op1), rd, [out])

    def copy(self, eng, out, in_):
        if eng == "act":
            return self._add("act", lambda e: e.copy(out, in_), [in_], [out])
        return self._add(eng, lambda e: e.tensor_copy(out, in_), [in_], [out])

    def memset(self, eng, ap, val):
        return self._add(eng, lambda e: e.memset(ap, val), [], [ap])

    def recip(self, out, in_):
        return self._add("dve", lambda e: e.reciprocal(out, in_), [in_], [out])

    def rsum(self, out, in_):
        return self._add("dve", lambda e: e.tensor_reduce(out, in_, AX.X, ALU.add), [in_], [out])

    def emit(self, final_dma_tags=()):
        nc = self.nc
        for e in ENGS:
            c = 0
            for op in self.ops[e]:
                if op.signal and not op.is_dma:
                    c += 1
                    op.sigval = c
        with contextlib.ExitStack() as st:
            esem = {e: st.enter_context(nc.semaphore("s_" + e)) for e in ENGS}
            dsem = {t: st.enter_context(nc.semaphore("d_%d" % i)) for i, t in enumerate(self.tagcount)}
            block = st.enter_context(nc.Block())
            engobj = {"pe": block.tensor, "act": block.scalar, "dve": block.vector,
                      "pool": block.gpsimd, "sp": block.sync}

            def make(ename):
                def body(eng):
                    for op in self.ops[ename]:
                        for (key, v) in op.waits:
                            if key[0] == "dma":
                                eng.wait_ge(dsem[key[1]], v)
                            else:
                                eng.wait_ge(esem[key[1]], v.sigval)
                        ins = op.fn(eng)
                        if op.is_dma:
                            ins.then_inc(dsem[op.tag], 16)
                        elif op.signal:
                            ins.then_inc(esem[ename], 1)
                    if ename == "sp":
                        for t in final_dma_tags:
                            eng.wait_ge(dsem[t], self.tagcount[t])
                return body

            for e in ENGS:
                engobj[e](make(e))
        return nc


def _prod(shape):
    n = 1
    for s in shape:
        n *= s
    return n


class Arena:
    def __init__(self, nc, nbytes):
        self.t = nc.alloc_sbuf_tensor("arena", [128, nbytes // 4], F32)
        self.nbytes = nbytes

    def view(self, off, shape, dt):
        n = _prod(shape)
        esz = 4 if dt == F32 else 2
        assert off % 4 == 0 and (n * esz) % 4 == 0 and off + n * esz <= self.nbytes, (off, shape)
        ap = self.t[:, off // 4:(off + n * esz) // 4]
        if dt != F32:
            ap = ap.bitcast(dt)
        if len(shape) > 1:
            names = " ".join("d%d" % i for i in range(len(shape)))
            kw = {"d%d" % i: shape[i] for i in range(1, len(shape))}
            ap = ap.rearrange("p (%s) -> p %s" % (names, names), **kw)
        return ap


class Bump:
    def __init__(self, arena, lo, hi):
        self.a = arena; self.lo = lo; self.hi = hi; self.cur = lo

    def alloc(self, shape, dt=F32):
        esz = 4 if dt == F32 else 2
        n = (_prod(shape) * esz + 31) // 32 * 32
        off = self.cur
        self.cur += n
        assert self.cur <= self.hi, ("arena overflow", self.cur, self.hi)
        return self.a.view(off, shape, dt)


class Ring:
    def __init__(self, items):
        self.items = list(items); self.i = 0

    def next(self):
        x = self.items[self.i % len(self.items)]
        self.i += 1
        return x


_CNAMES = ["IDENT", "M1", "M2", "TRI", "SUF", "IND0", "IND1", "S32", "OFF", "ONES"]
_CB = {"MB64": 512, "MBA4": 512, "MBB4": 512}
NCF = 128 * len(_CNAMES)
NCONST = NCF + 512 * 3


def make_consts():
    i = np.arange(128)[:, None]
    j = np.arange(128)[None, :]
    same = (i // 64) == (j // 64)
    c = {}
    c["IDENT"] = (i == j)
    c["M1"] = (i > j)
    c["M2"] = (i <= j)
    c["TRI"] = (i <= j) & same
    c["SUF"] = (i > j) & same
    c["IND0"] = (i < 64) & (j >= 0)
    c["IND1"] = (i >= 64) & (j >= 0)
    c["S32"] = (i < j) & ((i // 32) == (j // 32))
    c["OFF"] = same & ((i % 64) < 32) & ((j % 64) >= 32)
    c["ONES"] = np.ones((128, 128), bool)
    cols = [c[n].astype(np.float32) for n in _CNAMES]
    mb64 = np.where((i <= j) & same, 0.0, NEG).astype(np.float32)
    mba = np.where(i <= j, 0.0, NEG).astype(np.float32)
    mbb = np.where(i >= j, 0.0, NEG).astype(np.float32)
    cols += [np.tile(mb64, (1, 4)), np.tile(mba, (1, 4)), np.tile(mbb, (1, 4))]
    return np.ascontiguousarray(np.concatenate(cols, 1))


def build(stage="full", dumps=()):
    nc = bass.Bass("TRN2", target_bir_lowering=False)
    P = Prog(nc)
    dumps = set(dumps)
    dump_tags = []

    def din(name, shape):
        return nc.dram_tensor(name, list(shape), F32, kind="ExternalInput").ap()

    x_d = din("x", [S, D])
    win_d = din("w_in", [D, 3592])
    wout_d = din("w_out", [D, D])
    wup_d = din("w_up", [D, 5632])
    wdn_d = din("w_down", [2816, D])
    n1_d = din("n1rep", [128, D])
    n2_d = din("n2rep", [128, D])
    nf_d = din("nfrep", [128, D])
    cwa_d = din("cwA", [128, 48])
    cwf_d = din("cwF", [128, 132])
    gnw_d = din("gnwrep", [128, 128])
    dtb_d = din("dtbrep", [128, 64])
    alog_d = din("alogrep", [128, 64])
    cst_d = din("consts", [128, NCONST])
    out_d = nc.dram_tensor("out", [S, D], F32, kind="ExternalOutput").ap()

    def dump(name, sb_ap, shape):
        if name not in dumps:
            return
        d = nc.dram_tensor("dbg_" + name, list(shape), sb_ap.dtype, kind="ExternalOutput").ap()
        tg = "dbg_" + name
        P.dma("sp", d, sb_ap, tg)
        dump_tags.append(tg)

    win_v = win_d.rearrange("(k p) c -> p k c", p=128)
    wout_v = wout_d.rearrange("(k p) c -> p k c", p=128)
    wup_v = wup_d.rearrange("(k p) c -> p k c", p=128)
    wdn_v = wdn_d.rearrange("(k p) c -> p k c", p=128)

    ARENA_BYTES = 207872
    A = Arena(nc, ARENA_BYTES)
    pb = [nc.alloc_psum_tensor("pb%d" % i, [128, 512], F32) for i in range(8)]

    def pbf(i):
        return pb[i][:]

    def pbb(i):
        return pb[i][:].bitcast(BF16)

    CT = A.view(0, [8, S], BF16)
    hT = A.view(32768, [8, S], BF16)
    pers = Bump(A, 65536, 86016)
    CF = pers.alloc([NCF])
    CBm = pers.alloc([1536 + 256], BF16)
    cwA = pers.alloc([12, 4])
    cwF = pers.alloc([44, 3])
    HALOA = pers.alloc([12, 3])
    HALOF = pers.alloc([44, 2])
    GNW = pers.alloc([128])
    NW1 = pers.alloc([D])
    NW2 = pers.alloc([D])
    SCR_LO = 86016
    SCR_HI = ARENA_BYTES

    def cf(name):
        k = _CNAMES.index(name)
        return CF[:, 128 * k:128 * (k + 1)]

    IDENT = cf("IDENT"); M1 = cf("M1"); M2 = cf("M2"); TRI = cf("TRI"); SUF = cf("SUF")
    IND0 = cf("IND0"); IND1 = cf("IND1"); S32 = cf("S32"); OFFM = cf("OFF"); ONES = cf("ONES")
    MB64b = CBm[:, 0:512]; MBA4b = CBm[:, 512:1024]; MBB4b = CBm[:, 1024:1536]
    IDENTb = CBm[:, 1536:1664]; ONESb = CBm[:, 1664:1792]

    P.dma("sp", CF, cst_d[:, 0:NCF], "c_cf")
    ctmp = A.view(16384 + 8192, [1536], F32)
    P.dma("sp", ctmp, cst_d[:, NCF:NCONST], "c_tmp")
    P.copy("dve", CBm[:, 0:1536], ctmp)
    P.copy("dve", IDENTb, IDENT)
    P.copy("dve", ONESb, ONES)
    P.dma("sp", cwA.rearrange("p a b -> p (a b)"), cwa_d, "c_cwa")
    P.dma("sp", cwF.rearrange("p a b -> p (a b)"), cwf_d, "c_cwf")
    P.dma("sp", GNW, gnw_d, "c_gnw")
    P.dma("sp", NW1, n1_d, "c_nw1")
    P.memset("pool", HALOA.rearrange("p a b -> p (a b)"), 0.0)
    P.memset("pool", HALOF.rearrange("p a b -> p (a b)"), 0.0)

    def bc_last(ap, n):
        return ap.unsqueeze(2).broadcast_to([128, ap.shape[1], n])

    def bc_mid(ap, n):
        return ap.unsqueeze(1).broadcast_to([128, n, ap.shape[1]])

    def rmsnorm_stage1(src_tile, wtile, scr):
        junk, ssv, rst, hb = scr
        P.act(junk, src_tile, AF.Square, accum_out=ssv)
        P.act(rst, ssv, AF.Ln, bias=EPS, scale=1.0 / D)
        P.act(rst, rst, AF.Exp, scale=-0.5)
        P.stt(hb, src_tile, rst, wtile, ALU.mult, ALU.mult)

    def rmsnorm_stage1a(src_tile, scr):
        junk, ssv, rst, hb = scr
        P.act(junk, src_tile, AF.Square, accum_out=ssv)
        P.act(rst, ssv, AF.Ln, bias=EPS, scale=1.0 / D)
        P.act(rst, rst, AF.Exp, scale=-0.5)

    def rmsnorm_stage1b(src_tile, wtile, scr):
        junk, ssv, rst, hb = scr
        P.stt(hb, src_tile, rst, wtile, ALU.mult, ALU.mult)

    def rmsnorm_stage2(dstT, col0, scr, bank):
        hb = scr[3]
        psT = pbb(bank)
        for kc in range(8):
            P.tr(psT[:, 128 * kc:128 * (kc + 1)], hb[:, 128 * kc:128 * (kc + 1)], IDENTb)
        P.copy("act", dstT[:, :, col0:col0 + 128], psT.rearrange("p (k t) -> p k t", k=8))

    bA = Bump(A, 0, 16384)
    xbuf = [bA.alloc([D]) for _ in range(3)]
    junkA = bA.alloc([D], BF16)
    hbuf = [NW2.bitcast(BF16)[:, 0:D], NW2.bitcast(BF16)[:, D:2 * D]]
    ssA = bA.alloc([NT])
    rsA = bA.alloc([NT])
    scrA = [(junkA, ssA[:, i:i + 1], rsA[:, i:i + 1], hbuf[i % 2]) for i in range(NT)]
    for i in range(NT + 2):
        if i < NT:
            xt = xbuf[i % 3]
            P.dma("sp", xt, x_d[128 * i:128 * (i + 1), :], "xa%d" % (i % 3))
            rmsnorm_stage1a(xt, scrA[i])
        if 1 <= i <= NT:
            rmsnorm_stage1b(xbuf[(i - 1) % 3], NW1, scrA[i - 1])
        if i >= 2:
            rmsnorm_stage2(hT, 128 * (i - 2), scrA[i - 2], 6 + (i % 2))
    dump("hT", hT.rearrange("p k t -> p (k t)"), [128, 8 * S])
    if stage == "A":
        return finish(nc, P, out_d, dump_tags)

    bG = Bump(A, SCR_LO, SCR_HI)
    bC = Bump(A, 16384, 32768)
    wz = bC.alloc([8, 512], BF16)
    qkvT = [dict(q=bG.alloc([4, 512], BF16), k=bG.alloc([4, 512], BF16), v=bG.alloc([4, 512], BF16))
            for _ in range(4)]
    szgb = [bG.alloc([4, 512], BF16), bG.alloc([4, 512], BF16), bG.alloc([4, 512], BF16), bC.alloc([4, 512], BF16)]
    BA = bG.alloc([NT, 8])
    sc_x = bG.alloc([64]); sc_mx = bG.alloc([64]); sc_mn = bG.alloc([64])
    dtb = bG.alloc([64]); negA = bG.alloc([64])
    gS = bG.alloc([64]); betaS = bG.alloc([64]); gamS = bG.alloc([64]); kesS = bG.alloc([64])
    gendS = bG.alloc([2, 64])
    wba = bG.alloc([8, 8], BF16)
    Sst = bG.alloc([4, 128]); Sb = bG.alloc([4, 128], BF16); ub = bG.alloc([4, 128], BF16)
    Obuf = [bC.alloc([4, 128]) for _ in range(2)]
    sqO = bG.alloc([4, 128], BF16); oab = sqO
    kesLo = bG.alloc([64]); kesHi = bG.alloc([64])
    ssq = bG.alloc([4]); rsq = bG.alloc([4])
    ov0 = bG.cur
    gset = []
    for _ in range(2):
        gset.append(dict(
            G1m=bG.alloc([4, 128]),
            DECT=bG.alloc([4, 128], BF16), Uall=bG.alloc([4, 128], BF16), Eu=bG.alloc([4, 128], BF16),
            PTa=bG.alloc([4, 128], BF16), PTb=bG.alloc([4, 128], BF16),
            UL0=bG.alloc([2, 4, 128], BF16), PWA=bG.alloc([2, 4, 128], BF16), PWB=bG.alloc([2, 4, 128], BF16),
            TTb=bG.alloc([4, 128], BF16), vb=bG.alloc([4, 128], BF16), gk=bG.alloc([4, 128], BF16)))
    scanop = []
    for _ in range(3):
        scanop.append(dict(KLO=bG.alloc([4, 128], BF16), KHI=bG.alloc([4, 128], BF16), ATT=bG.alloc([4, 128], BF16),
                           QD=bG.alloc([4, 128], BF16), WKT=bG.alloc([4, 128], BF16),
                           UV=bG.alloc([4, 128], BF16)))
    bO = Bump(A, ov0, bG.cur)
    wqkva = bO.alloc([3, 8, 512], BF16)
    rawb = [bO.alloc([520], BF16) for _ in range(3)]
    dgcb = [bO.alloc([4, 128], BF16) for _ in range(3)]
    sqbs = [bO.alloc([512], BF16) for _ in range(2)]
    rtbs = [bO.alloc([512]) for _ in range(2)]
    for c3 in range(3):
        P.dma("pool", wqkva[:, c3, :, :], win_v[:, :, 512 * c3:512 * (c3 + 1)], "wqkva%d" % c3)
    P.dma("pool", wba, win_v[:, :, 2048:2056], "wba")
    P.dma("pool", wz, win_v[:, :, 1536:2048], "wz")

    ringG = Ring([0, 1, 2, 3, 4, 5, 6, 7])

    g3 = gS.rearrange("p (n h) -> p n h", h=4)
    beta3 = betaS.rearrange("p (n h) -> p n h", h=4)
    gam3 = gamS.rearrange("p (n h) -> p n h", h=4)
    kesLo3 = kesLo.rearrange("p (n h) -> p n h", h=4)
    kesHi3 = kesHi.rearrange("p (n h) -> p n h", h=4)
    gend4 = gendS.rearrange("p a (n h) -> p a n h", h=4)

    def SCALARS():
        P.dma("sp", dtb, dtb_d, "c_dtb")
        P.dma("sp", negA, alog_d, "c_alog")
        P.act(negA, negA, AF.Exp)
        P.ts("dve", negA, negA, -1.0, None, ALU.mult)
        bk = ringG.next()
        psBA = pbf(bk)[:, 0:128].rearrange("p (n c) -> p n c", c=8)
        for i in range(NT):
            for kc in range(8):
                P.mm(psBA[:, i, :], hT[:, kc, 128 * i:128 * (i + 1)], wba[:, kc, :], start=(kc == 0), stop=(kc == 7))
        P.copy("act", BA, psBA)
        x3 = sc_x.rearrange("p (n h) -> p n h", h=4)
        P.tt("dve", x3, BA[:, :, 4:8], dtb.rearrange("p (n h) -> p n h", h=4), ALU.add)
        P.ts("dve", sc_mx, sc_x, 0.0, None, ALU.max)
        P.ts("dve", sc_mn, sc_x, 0.0, None, ALU.min)
        P.tt("dve", sc_mn, sc_mn, sc_mx, ALU.subtract)
        P.act(sc_mn, sc_mn, AF.Exp)
        P.act(sc_mn, sc_mn, AF.Ln, bias=1.0)
        P.tt("dve", sc_mx, sc_mx, sc_mn, ALU.add)
        P.tt("dve", gS, sc_mx, negA, ALU.mult)
        P.act(betaS.rearrange("p (n h) -> p n h", h=4), BA[:, :, 0:4], AF.Sigmoid)
        bk = ringG.next()
        psg = pbf(bk)
        P.mm(psg[:, 0:64], TRI, gS)
        P.mm(psg[:, 64:128], SUF, gS)
        P.mm(psg[:, 128:192], IND0, gS)
        P.mm(psg[:, 192:256], IND1, gS)
        P.act(gamS, psg[:, 0:64], AF.Exp)
        P.act(kesS, psg[:, 64:128], AF.Exp)
        P.tt("dve", kesS, kesS, betaS, ALU.mult)
        P.act(gendS.rearrange("p a b -> p (a b)"), psg[:, 128:256], AF.Exp)
        dump("gS", gS, [128, 64]); dump("betaS", betaS, [128, 64]); dump("gamS", gamS, [128, 64])
        dump("kesS", kesS, [128, 64]); dump("gendS", gendS.rearrange("p a b -> p (a b)"), [128, 128])
        P.ts("dve", kesLo, kesS, IND0[:, 0:1], None, ALU.mult)
        P.ts("dve", kesHi, kesS, IND1[:, 0:1], None, ALU.mult)


    def G1(m):
        t0 = 512 * m
        o = qkvT[m]
        for c in range(12):
            ps = pbf(ringG.next())
            for kc in range(8):
                P.mm(ps, wqkva[:, c // 4, kc, 128 * (c % 4):128 * (c % 4 + 1)], hT[:, kc, t0:t0 + 512], start=(kc == 0), stop=(kc == 7))
            raw = rawb[(c + m) % 3]; dgc = dgcb[(c + m) % 3]
            for i_ in range(4):
                P.ts("dve", dgc[:, i_, :], IDENTb, cwA[:, c, i_:i_ + 1], None, ALU.mult)
            P.copy("pool", raw[:, 0:3], HALOA[:, c, :])
            P.copy("act", raw[:, 3:515], ps)
            P.copy("pool", HALOA[:, c, :], raw[:, 512:515])
            psC = pbf(ringG.next())
            for i_ in range(4):
                P.mm(psC, dgc[:, i_, :], raw[:, i_:i_ + 512], start=(i_ == 0), stop=(i_ == 3))
            dst = (o["q"], o["k"], o["v"])[c // 4][:, c % 4, :]
            P.act(dst, psC, AF.Silu)
            if c % 3 == 2:
                nz = 4 * m + c // 3
                psZ = pbf(ringG.next())
                for kc in range(8):
                    P.mm(psZ, hT[:, kc, 128 * nz:128 * (nz + 1)], wz[:, kc, :], start=(kc == 0), stop=(kc == 7))
                szg = szgb[m][:, c // 3, :]
                P.act(szg, psZ, AF.Silu)
                P.tt("pool", szg.rearrange("p (h d) -> p h d", h=4), szg.rearrange("p (h d) -> p h d", h=4),
                     bc_mid(GNW, 4), ALU.mult)
            yield
        for c in range(8):
            dst = (o["q"], o["k"])[c // 4][:, c % 4, :]
            sc = 128.0 if c < 4 else 1.0
            sqb = sqbs[(c + m) % 2]; rtb = rtbs[(c + m) % 2]
            P.tt("pool", sqb, dst, dst, ALU.mult)
            psn = pbf(ringG.next())
            P.mm(psn, ONESb, sqb)
            P.act(rtb, psn, AF.Ln, bias=EPS * sc, scale=sc)
            P.act(rtb, rtb, AF.Exp, scale=-0.5)
            P.tt("dve", dst, dst, rtb, ALU.mult)
            yield
        if m == 0:
            dump("qnT0", o["q"].rearrange("p h t -> p (h t)"), [128, 2048])
            dump("knT0", o["k"].rearrange("p h t -> p (h t)"), [128, 2048])
            dump("vsT0", o["v"].rearrange("p h t -> p (h t)"), [128, 2048])

    def G2(n):
        tl = 128 * (n % 4)
        so = scanop[n % 3]
        st = gset[n % 2]
        qnT = qkvT[n // 4]["q"]; knT = qkvT[n // 4]["k"]; vsT = qkvT[n // 4]["v"]
        G1m = st["G1m"]; DECT = st["DECT"]; Uall = st["Uall"]; Eu = st["Eu"]
        UL0 = st["UL0"]; PWA = st["PWA"]; PWB = st["PWB"]
        TTb = st["TTb"]; vb = st["vb"]; gk = st["gk"]
        Du, Dl = UL0[:, 0], UL0[:, 1]
        gam_b = bc_last(gam3[:, n, :], 128)
        keslo_b = bc_last(kesLo3[:, n, :], 128)
        keshi_b = bc_last(kesHi3[:, n, :], 128)
        beta_b = bc_last(beta3[:, n, :], 128)
        g_b = bc_last(g3[:, n, :], 128)

        def bankf():
            return pbf(ringG.next()).rearrange("p (h d) -> p h d", h=4)

        def bankb():
            return pbb(ringG.next())[:, 0:512].rearrange("p (h d) -> p h d", h=4)

        psKV = pbb(ringG.next()).rearrange("p (a h d) -> p a h d", a=2, h=4)
        psK = psKV[:, 0]; psV = psKV[:, 1]
        for h in range(4):
            P.tr(psK[:, h, :], knT[:, h, tl:tl + 128], IDENTb)
            P.tr(psV[:, h, :], vsT[:, h, tl:tl + 128], IDENTb)
        P.tt("dve", gk, psK, gam_b, ALU.mult)
        P.tt("dve", so["KLO"], psK, keslo_b, ALU.mult)
        P.tt("dve", so["KHI"], psK, keshi_b, ALU.mult)
        P.copy("act", vb, psV)
        yield
        P.tt("pool", G1m, bc_mid(M1, 4), g_b, ALU.mult)
        psD = bankf()
        P.mm(psD, IDENTb, MB64b, start=True, stop=False)
        for h in range(4):
            P.mm(psD[:, h, :], G1m[:, h, :], M2, start=False, stop=(h == 3))
        P.act(DECT, psD, AF.Exp)
        P.tt("pool", DECT, DECT, beta_b, ALU.mult)
        yield
        dg = Uall
        P.tt("pool", dg, bc_mid(IDENT, 4), gam_b, ALU.mult)
        psG = bankf()
        for h in range(4):
            P.mm(psG[:, h, :], ONESb, dg[:, h, :])
        P.tt("dve", so["QD"], qnT[:, :, tl:tl + 128], psG, ALU.mult)
        yield
        psKK = bankf(); psQK = bankf()
        for h in range(4):
            P.mm(psKK[:, h, :], knT[:, h, tl:tl + 128], knT[:, h, tl:tl + 128])
        for h in range(4):
            P.mm(psQK[:, h, :], knT[:, h, tl:tl + 128], qnT[:, h, tl:tl + 128])
        P.tt("dve", Uall, psKK, DECT, ALU.mult)
        P.tt("dve", so["ATT"], psQK, DECT, ALU.mult)
        P.tt("pool", Du, Uall, bc_mid(S32, 4), ALU.mult)
        P.tt("pool", Eu, Uall, bc_mid(OFFM, 4), ALU.mult)
        yield
        psT = bankb()
        for h in range(4):
            P.tr(psT[:, h, :], Du[:, h, :], IDENTb)
        P.copy("act", Dl, psT)
        P.tt("pool", st["PTa"], bc_mid(IDENT, 4), Du, ALU.subtract)
        yield
        pw = [UL0, PWA, PWB, PWA, PWB]
        PT, PTn = st["PTa"], st["PTb"]
        for k in range(1, 5):
            cur = pw[k - 1]; nxt = pw[k]
            if k < 4:
                psU = bankf()
                for h in range(4):
                    P.mm(psU[:, h, :], cur[:, 1, h, :], cur[:, 0, h, :])
            psL = bankf()
            for h in range(4):
                P.mm(psL[:, h, :], cur[:, 0, h, :], cur[:, 1, h, :])
            if k > 1:
                ps3 = bankf()
                for h in range(4):
                    P.mm(ps3[:, h, :], cur[:, 1, h, :], PT[:, h, :])
            if k < 4:
                P.copy("act", nxt[:, 0], psU)
            P.copy("act", nxt[:, 1], psL)
            if k > 1:
                P.tt("dve", PTn, PT, ps3, ALU.add)
                PT, PTn = PTn, PT
            yield
        ps3 = bankf()
        for h in range(4):
            P.mm(ps3[:, h, :], PWB[:, 1, h, :], PT[:, h, :])
        P.tt("dve", PTn, PT, ps3, ALU.add)
        PT, PTn = PTn, PT
        yield
        Pm = PWA[:, 0]; XT = DECT
        p1 = bankb()
        for h in range(4):
            P.tr(p1[:, h, :], PT[:, h, :], IDENTb)
        P.copy("act", Pm, p1)
        yield
        p2 = bankf()
        for h in range(4):
            P.mm(p2[:, h, :], Eu[:, h, :], Pm[:, h, :])
        P.copy("act", XT, p2)
        yield
        p3 = bankf()
        for h in range(4):
            P.mm(p3[:, h, :], XT[:, h, :], PT[:, h, :])
        P.tt("dve", TTb, PT, p3, ALU.subtract)
        yield
        p1 = bankf(); p2 = bankf()
        for h in range(4):
            P.mm(p1[:, h, :], TTb[:, h, :], vb[:, h, :])
        for h in range(4):
            P.mm(p2[:, h, :], gk[:, h, :], TTb[:, h, :])
        P.copy("act", so["UV"], p1)
        P.copy("dve", so["WKT"], p2)
        yield

    def SCAN(n):
        so = scanop[n % 3]
        O = Obuf[n % 2]
        for half in range(2):
            r0 = 64 * half
            rs = slice(r0, r0 + 64)
            kend = so["KLO"] if half == 0 else so["KHI"]
            psA = pbf(ringG.next()).rearrange("p (h d) -> p h d", h=4)
            for h in range(4):
                P.mm(psA[:, h, :], so["WKT"][:, h, :], Sb[:, h, :])
            P.tt("dve", ub[rs], so["UV"][rs], psA[rs], ALU.subtract)
            yield
            psS = pbf(ringG.next()).rearrange("p (h d) -> p h d", h=4)
            psO = pbf(ringG.next()).rearrange("p (h d) -> p h d", h=4)
            for h in range(4):
                P.mm(psS[:, h, :], kend[:, h, :], ub[:, h, :])
            for h in range(4):
                P.mm(psO[:, h, :], so["QD"][:, h, :], Sb[:, h, :], start=True, stop=False)
                P.mm(psO[:, h, :], so["ATT"][:, h, :], ub[:, h, :], start=False, stop=True)
            for h in range(4):
                P.stt(Sst[:, h, :], Sst[:, h, :], gend4[:, half, n, h:h + 1], psS[:, h, :], ALU.mult, ALU.add)
            P.copy("act", Sb, Sst)
            P.copy("act", O[rs], psO[rs])
            yield
        if n == 0:
            dump("O0", O.rearrange("p h d -> p (h d)"), [128, 512])
        if n == 15:
            dump("O15", O.rearrange("p h d -> p (h d)"), [128, 512])
        P.act(sqO, O, AF.Square)
        P.rsum(ssq, sqO)
        P.act(rsq, ssq, AF.Ln, bias=EPS, scale=1.0 / 128)
        P.act(rsq, rsq, AF.Exp, scale=-0.5)
        szg = szgb[n // 4][:, n % 4, :].rearrange("p (h d) -> p h d", h=4)
        P.tt("dve", sqO, O, bc_last(rsq, 128), ALU.mult)
        P.tt("dve", oab, sqO, szg, ALU.mult)
        yield
        psT = pbb(ringG.next())[:, 0:512].rearrange("p (h d) -> p h d", h=4)
        for h in range(4):
            P.tr(psT[:, h, :], oab[:, h, :], IDENTb)
        P.copy("act", CT[:, 0:4, 128 * n:128 * (n + 1)], psT)
        yield

    def advance(must, opt):
        live = [True] * len(must)
        while any(live) or any(q[1] > 0 for q in opt):
            for gi, g in enumerate(must):
                if live[gi]:
                    try:
                        next(g)
                    except StopIteration:
                        live[gi] = False
            for q in opt:
                if q[1] > 0:
                    q[1] -= 1
                    try:
                        next(q[0])
                    except StopIteration:
                        q[1] = 0
                        q[2] = True

    P.memset("dve", Sst.rearrange("p h d -> p (h d)"), 0.0)
    P.memset("pool", Sb.rearrange("p h d -> p (h d)"), 0.0)
    P.memset("pool", ub.rearrange("p h d -> p (h d)"), 0.0)
    advance([G1(m_) for m_ in range(4)], [])
    SCALARS()
    g2 = {0: G2(0), 1: G2(1)}
    advance([g2[0]], [[g2[1], 7, False]])
    for n in range(NT):
        must = [SCAN(n)]
        if n + 1 < NT:
            must.append(g2[n + 1])
        opt = []
        if n + 2 < NT:
            g2[n + 2] = G2(n + 2)
            opt.append([g2[n + 2], 7, False])
        advance(must, opt)
    dump("CTa", CT[:, 0:4, :].rearrange("p k t -> p (k t)"), [128, 4 * S])
    if stage == "G":
        return finish(nc, P, out_d, dump_tags)

    bB = Bump(A, SCR_LO, SCR_HI)
    wqkvb = bB.alloc([3, 8, 512], BF16)
    for c3 in range(3):
        P.dma("pool", wqkvb[:, c3, :, :],
              win_v[:, :, 2056 + 512 * c3:2056 + 512 * (c3 + 1)], "wqkvb%d" % c3)
    QT0 = bB.alloc([S], BF16)
    KTz = [bB.alloc([S], BF16) for _ in range(2)]
    QTb = [QT0, QT0]
    P.memset("pool", KTz[0][64:128, :], 0.0)
    P.memset("pool", KTz[1][0:64, :], 0.0)
    VAll = bB.alloc([48, 4, 192], BF16)
    PTbuf = Ring([bB.alloc([1024], BF16) for _ in range(2)])
    PT3 = bB.alloc([16, 128], BF16)
    rden0 = bB.alloc([512])
    rdenb = [rden0, rden0]
    P.memset("pool", VAll[:, :, :, 64:128], 1.0)
    ringS = Ring([2, 3, 4, 5])
    ringP = Ring([6, 7])
    accR = Ring([0, 1])

    def tok_slices():
        sl = []
        for n in range(16):
            sl.append(slice(128 * n, 128 * (n + 1), 1))
        for r in range(4):
            for c in range(4):
                sl.append(slice(512 * c + r, 512 * (c + 1), 4))
        for r in range(16):
            sl.append(slice(r, S, 16))
        return sl

    TOK = tok_slices()

    def PROJ_V():
        for t in range(48):
            bank = ringP.next()
            ps = pbf(bank)
            sl = TOK[t]
            for kc in range(8):
                P.mm(ps, hT[:, kc, sl], wqkvb[:, 2, kc, :], start=(kc == 0), stop=(kc == 7))
            ps4 = ps.rearrange("p (j e d) -> p j e d", j=4, e=2)
            P.copy("act", VAll[:, t, :, 0:64], ps4[:, :, 0, :])
            P.copy("dve", VAll[:, t, :, 128:192], ps4[:, :, 1, :])

    def PROJ_B(j):
        k = j % 2
        QT = QTb[k]
        for (dst, cbase) in ((QT, 128 * j), (None, 512 + 128 * j)):
            for tb in range(4):
                bank = ringP.next()
                ps = pbf(bank)
                for kc in range(8):
                    P.mm(ps, wqkvb[:, cbase // 512, kc, cbase % 512:cbase % 512 + 128], hT[:, kc, 512 * tb:512 * (tb + 1)],
                         start=(kc == 0), stop=(kc == 7))
                cs = slice(512 * tb, 512 * (tb + 1))
                if dst is not None:
                    P.copy("dve" if tb % 2 else "act", dst[:, cs], ps)
                else:
                    P.copy("act", KTz[0][0:64, cs], ps[0:64, :])
                    P.copy("dve", KTz[1][64:128, cs], ps[64:128, :])

    def ATTN(j):
        k = j % 2
        QT = QTb[k]

        class _VA:
            def __init__(self, h):
                self.h = h

            def __getitem__(self, idx):
                e_ = self.h % 2
                return VAll[:, idx[1], self.h // 2, 64 * e_:64 * e_ + 128]

        for e in range(2):
            hp = slice(64 * e, 64 * e + 64)
            KT = KTz[e]
            fp = slice(0, 128)
            VA = _VA(2 * j + e)
            for g in range(4):
                bank = ringS.next()
                ps = pbf(bank)
                P.mm(ps, IDENTb, MBA4b, start=True, stop=False)
                for r4 in range(4):
                    r = 4 * g + r4
                    P.mm(ps[:, 128 * r4:128 * (r4 + 1)], KT[fp, r:S:16], QT[fp, r:S:16], start=False, stop=(r4 == 3))
                P.act(PT3[:, 4 * g:4 * g + 4, :], ps.rearrange("p (a d) -> p a d", a=4), AF.Exp, scale=0.125)
            for c in range(4):
                acc = pbf(accR.next())
                first = [True]

                def pv(out, lhsT, rhs, last=False):
                    P.mm(out, lhsT, rhs, start=first[0], stop=last, skip_group_check=True)
                    first[0] = False
                pt = PTbuf.next()
                bank = ringS.next(); ps = pbf(bank)
                P.mm(ps, IDENTb, MBA4b, start=True, stop=False)
                for i in range(4):
                    n = 4 * c + i
                    P.mm(ps[:, 128 * i:128 * (i + 1)], KT[fp, 128 * n:128 * (n + 1)], QT[fp, 128 * n:128 * (n + 1)],
                         start=False, stop=(i == 3))
                P.act(pt[:, 0:512], ps, AF.Exp, scale=0.125)
                bank = ringS.next(); ps = pbf(bank)
                P.mm(ps, IDENTb, MBB4b, start=True, stop=False)
                for i in range(4):
                    n = 4 * c + i
                    if n == 0:
                        continue
                    P.mm(ps[:, 128 * i:128 * (i + 1)], KT[fp, 128 * (n - 1):128 * n], QT[fp, 128 * n:128 * (n + 1)],
                         start=False, stop=(i == 3))
                P.act(pt[:, 512:1024], ps, AF.Exp, scale=0.125)
                for i in range(4):
                    n = 4 * c + i
                    pv(acc[:, 128 * i:128 * (i + 1)], VA[:, n, e, :], pt[:, 128 * i:128 * (i + 1)])
                    if n > 0:
                        pv(acc[:, 128 * i:128 * (i + 1)], VA[:, n - 1, e, :], pt[:, 512 + 128 * i:512 + 128 * (i + 1)])
                pt = PTbuf.next()
                bank = ringS.next(); ps = pbf(bank)
                P.mm(ps, IDENTb, MBA4b, start=True, stop=False)
                for r in range(4):
                    sl = slice(512 * c + r, 512 * (c + 1), 4)
                    P.mm(ps[:, 128 * r:128 * (r + 1)], KT[fp, sl], QT[fp, sl], start=False, stop=(r == 3))
                P.act(pt[:, 0:512], ps, AF.Exp, scale=0.125)
                if c > 0:
                    bank = ringS.next(); ps = pbf(bank)
                    P.mm(ps, IDENTb, MBB4b, start=True, stop=False)
                    for r in range(4):
                        sl = slice(512 * c + r, 512 * (c + 1), 4)
                        slk = slice(512 * (c - 1) + r, 512 * c, 4)
                        P.mm(ps[:, 128 * r:128 * (r + 1)], KT[fp, slk], QT[fp, sl], start=False, stop=(r == 3))
                    P.act(pt[:, 512:1024], ps, AF.Exp, scale=0.125)
                for r in range(4):
                    pv(acc[:, r:512:4], VA[:, 16 + 4 * r + c, e, :], pt[:, 128 * r:128 * (r + 1)])
                    if c > 0:
                        pv(acc[:, r:512:4], VA[:, 16 + 4 * r + c - 1, e, :], pt[:, 512 + 128 * r:512 + 128 * (r + 1)])
                for r in range(16):
                    pv(acc[:, r:512:16], VA[:, 32 + r, e, :], PT3[:, r, 32 * c:32 * (c + 1)], last=(r == 15))
                num = slice(64 * e, 64 * e + 64)
                den = slice(64 * (1 - e), 64 * (1 - e) + 64)
                rd = rdenb[c % 2]
                P.recip(rd[den, :], acc[den, :])
                P.tt("dve", CT[num, 4 + j, 512 * c:512 * (c + 1)], acc[num, :], rd[den, :], ALU.mult)

    PROJ_V()
    for j in range(4):
        PROJ_B(j)
        ATTN(j)
    dump("CTb", CT[:, 4:8, :].rearrange("p k t -> p (k t)"), [128, 4 * S])
    if stage == "B":
        return finish(nc, P, out_d, dump_tags)

    bF = Bump(A, SCR_LO, SCR_HI)
    h2T = A.view(32768, [8, 1024], BF16)
    woutb = A.view(32768 + 16384, [2, 8, 512], BF16)
    X1 = bF.alloc([8, D])
    wu = [bF.alloc([2, 8, 512], BF16) for _ in range(2)]
    wd = [bF.alloc([4, D], BF16) for _ in range(2)]
    aTb = [bF.alloc([4, 1024], BF16) for _ in range(2)]
    rawF = [[bF.alloc([520]) for _ in range(2)] for _ in range(2)]
    accF = [[bF.alloc([512]) for _ in range(2)] for _ in range(2)]
    sgF = [bF.alloc([512]) for _ in range(2)]
    hbF = [bF.alloc([D], BF16), accF[1][1].bitcast(BF16)]
    junkF = sgF[0].bitcast(BF16)
    ssF = bF.alloc([8]); rsF = bF.alloc([8]); ssO = bF.alloc([8]); rsO = bF.alloc([8])
    for c2 in range(2):
        P.dma("pool", woutb[:, c2, :, :], wout_v[:, :, 512 * c2:512 * (c2 + 1)], "wout%d" % c2)
    P.dma("sp", NW1, n2_d, "c_nw1")
    P.dma("sp", NW2, nf_d, "c_nw2")
    groups = [list(range(g0, min(g0 + 4, 22))) for g0 in range(0, 22, 4)]
    out_tags = ["out%d" % i for i in range(8)]
    items = [(H, gi) for H in range(2) for gi in range(len(groups))]

    def load_wu(k):
        H, gi = items[k]
        grp = groups[gi]; g0 = grp[0]; npair = len(grp); slot = k % 2
        P.dma("pool", wu[slot][:, 0, :, 0:128 * npair], wup_v[:, :, 128 * g0:128 * (g0 + npair)], "wug%d" % slot)
        P.dma("pool", wu[slot][:, 1, :, 0:128 * npair],
              wup_v[:, :, 2816 + 128 * g0:2816 + 128 * (g0 + npair)], "wuu%d" % slot)

    def load_wd(k):
        H, gi = items[k]
        grp = groups[gi]; g0 = grp[0]; npair = len(grp); slot = k % 2
        P.dma("pool", wd[slot][:, 0:npair, :], wdn_v[:, g0:g0 + npair, :], "wd%d" % slot)

    def PRO(H):
        scr = [(junkF, ssF[:, i8:i8 + 1], rsF[:, i8:i8 + 1], hbF[i8 % 2]) for i8 in range(8)]

        def st_a(i8):
            i = 8 * H + i8
            P.dma("sp", X1[:, i8, :], x_d[128 * i:128 * (i + 1), :], "xf%d" % i8)
            b0 = 2 * (i8 % 2)
            for h2 in range(2):
                for kc in range(8):
                    P.mm(pbf(b0 + h2), CT[:, kc, 128 * i:128 * (i + 1)], woutb[:, h2, kc, :],
                         start=(kc == 0), stop=(kc == 7))
            for h2 in range(2):
                P.tt("dve", X1[:, i8, 512 * h2:512 * (h2 + 1)], X1[:, i8, 512 * h2:512 * (h2 + 1)], pbf(b0 + h2), ALU.add)
            if i == 0:
                dump("X1", X1[:, 0, :], [128, D])
            rmsnorm_stage1a(X1[:, i8, :], scr[i8])

        for t in range(10):
            if t < 8:
                st_a(t)
            if 1 <= t <= 8:
                rmsnorm_stage1b(X1[:, t - 1, :], NW1, scr[t - 1])
            if t >= 2:
                rmsnorm_stage2(h2T, 128 * (t - 2), scr[t - 2], 4 + (t % 2))

    upar = [0]

    def UGEN(k):
        H, gi = items[k]
        grp = groups[gi]; slot = k % 2; aT = aTb[k % 2]
        for p, g in enumerate(grp):
            for tb in range(2):
                par = upar[0]
                upar[0] ^= 1
                cols = slice(512 * tb, 512 * (tb + 1))
                banks = (4, 5) if par == 0 else (6, 7)
                accs = []
                for gu in range(2):
                    ps = pbf(banks[gu])
                    for kc in range(8):
                        P.mm(ps, wu[slot][:, gu, kc, 128 * p:128 * (p + 1)], h2T[:, kc, cols],
                             start=(kc == 0), stop=(kc == 7))
                    cc = g + 22 * gu
                    raw = rawF[par][gu]; acc = accF[par][gu]
                    P.copy("pool", raw[:, 0:2], HALOF[:, cc, :])
                    P.copy("act", raw[:, 2:514], ps)
                    P.copy("pool", HALOF[:, cc, :], raw[:, 512:514])
                    P.act(acc, ps, AF.Identity, scale=cwF[:, cc, 2:3])
                    P.stt(acc, raw[:, 1:513], cwF[:, cc, 1:2], acc, ALU.mult, ALU.add)
                    P.stt(acc, raw[:, 0:512], cwF[:, cc, 0:1], acc, ALU.mult, ALU.add)
                    accs.append(acc)
                sg = sgF[par]
                P.act(sg, accs[0], AF.Silu)
                P.tt("pool", aT[:, p, cols], sg, accs[1], ALU.mult)
                yield

    def DGEN(k):
        H, gi = items[k]
        grp = groups[gi]; slot = k % 2; aT = aTb[k % 2]; npair = len(grp)
        last = (gi == len(groups) - 1)
        for i8 in range(8):
            i = 8 * H + i8
            b0 = 2 * (i8 % 2)
            for h2 in range(2):
                for p in range(npair):
                    P.mm(pbf(b0 + h2), aT[:, p, 128 * i8:128 * (i8 + 1)], wd[slot][:, p, 512 * h2:512 * (h2 + 1)],
                         start=(p == 0), stop=(p == npair - 1))
            for h2 in range(2):
                P.tt("dve", X1[:, i8, 512 * h2:512 * (h2 + 1)], X1[:, i8, 512 * h2:512 * (h2 + 1)],
                     pbf(b0 + h2), ALU.add)
            if last:
                P.act(junkF, X1[:, i8, :], AF.Square, accum_out=ssO[:, i8:i8 + 1])
                P.act(rsO[:, i8:i8 + 1], ssO[:, i8:i8 + 1], AF.Ln, bias=EPS, scale=1.0 / D)
                P.act(rsO[:, i8:i8 + 1], rsO[:, i8:i8 + 1], AF.Exp, scale=-0.5)
                P.stt(X1[:, i8, :], X1[:, i8, :], rsO[:, i8:i8 + 1], NW2, ALU.mult, ALU.mult)
                P.dma("sp", out_d[128 * i:128 * (i + 1), :], X1[:, i8, :], "out%d" % i8)
            yield

    def rr(gens):
        gens = list(gens)
        live = [True] * len(gens)
        while any(live):
            for gi_, g_ in enumerate(gens):
                if live[gi_]:
                    try:
                        next(g_)
                    except StopIteration:
                        live[gi_] = False

    load_wu(0)
    prevD = None
    for k in range(len(items)):
        H, gi = items[k]
        load_wd(k)
        if k + 1 < len(items):
            load_wu(k + 1)
        if gi == 0:
            if prevD is not None:
                rr([prevD])
                prevD = None
            PRO(H)
        u = UGEN(k)
        rr([u] if prevD is None else [u, prevD])
        prevD = DGEN(k)
    rr([prevD])
    return finish(nc, P, out_d, dump_tags + out_tags)


def finish(nc, P, out_d, dump_tags):
    tags = list(dump_tags)
    P.emit(final_dma_tags=tags)
    return nc


def prep_inputs(inp):
    f = lambda a: np.ascontiguousarray(np.asarray(a, dtype=np.float32))
    x = f(inp["x"])
    rep = lambda v: np.ascontiguousarray(np.broadcast_to(f(v).reshape(1, -1), (128, f(v).size)))
    cwa = f(inp["conv_qkv_w"])[0]
    cwA = np.ascontiguousarray(cwa.T.reshape(12, 128, 4).transpose(1, 0, 2).reshape(128, 48))
    cwf = f(inp["ffn_conv_w"])[0]
    cwF = np.ascontiguousarray(cwf.T.reshape(44, 128, 3).transpose(1, 0, 2).reshape(128, 132))
    shared = {
        "w_in": f(inp["w_in"])[0], "w_out": f(inp["w_out"])[0], "w_up": f(inp["w_up"])[0],
        "w_down": f(inp["w_down"])[0],
        "n1rep": rep(inp["norm1_w"]), "n2rep": rep(inp["norm2_w"]), "nfrep": rep(inp["final_norm_w"]),
        "cwA": cwA, "cwF": cwF, "gnwrep": rep(inp["gdn_norm_w"]),
        "dtbrep": np.ascontiguousarray(np.tile(rep(inp["dt_bias"]), (1, 16))),
        "alogrep": np.ascontiguousarray(np.tile(rep(inp["a_log"]), (1, 16))),
        "consts": make_consts(),
    }
    maps = []
    for b in range(x.shape[0]):
        m = dict(shared)
        m["x"] = np.ascontiguousarray(x[b])
        maps.append(m)
    return maps


def kernel(**inputs):
    maps = prep_inputs(inputs)
    nc = build("full")
    res = run_bass_kernel_spmd(nc, maps, core_ids=list(range(8)))
    out = np.stack([np.asarray(r["out"], dtype=np.float32) for r in res.results], 0)
    return out
```

```python
import contextlib
import numpy as np
import concourse.bass as bass
import concourse.mybir as mybir
from concourse.bass_utils import run_bass_kernel_spmd

F32 = mybir.dt.float32
BF16 = mybir.dt.bfloat16
AF = mybir.ActivationFunctionType
ALU = mybir.AluOpType
AX = mybir.AxisListType

S = 2048
D = 1024
NT = 16
EPS = 1e-6
NEG = -30000.0
ENGS = ("pe", "act", "dve", "pool", "sp")


def _rect(ap):
    t = ap.tensor
    if str(ap.space) == "PSUM":
        return (t.name, 0, 128, 0, 2048)
    esz = mybir.dt.size(ap.dtype)
    pstride = 1
    for s in tuple(t.shape)[1:]:
        pstride *= s
    p0 = ap.start_partition()
    p1 = p0 + ap.partition_size()
    f0 = ap.offset - p0 * pstride
    ext = 0
    for (st, cnt) in tuple(ap.ap)[1:]:
        ext += abs(st) * (cnt - 1)
    return (t.name, p0, p1, f0 * esz, (f0 + ext + 1) * esz)


class _Op:
    __slots__ = ("eng", "fn", "idx", "is_dma", "tag", "waits", "signal", "sigval")


class Prog:
    def __init__(self, nc):
        self.nc = nc
        self.ops = {e: [] for e in ENGS}
        self.track = {}
        self.waited = {e: {} for e in ENGS}
        self.tagcount = {}

    def _add(self, eng, fn, reads, writes, is_dma=False, tag=None):
        op = _Op()
        op.eng = eng; op.fn = fn; op.is_dma = is_dma; op.tag = tag
        op.signal = False; op.sigval = None
        op.idx = len(self.ops[eng])
        op.waits = []
        deps = {}
        rrects = [_rect(a) for a in reads if a is not None and str(a.space) != "DRAM"]
        wrects = [_rect(a) for a in writes if a is not None and str(a.space) != "DRAM"]
        prects = [r for r in rrects if r[4] == 2048 and r[0].startswith("pb") and r not in wrects]
        for (nm, p0, p1, f0, f1) in rrects:
            for rec in self.track.get(nm, ()):
                if rec[5] == 1 and rec[0] < p1 and p0 < rec[1] and rec[2] < f1 and f0 < rec[3]:
                    deps[id(rec[4])] = (rec[4], True)
        for (nm, p0, p1, f0, f1) in wrects + prects:
            for rec in self.track.get(nm, ()):
                if rec[0] < p1 and p0 < rec[1] and rec[2] < f1 and f0 < rec[3]:
                    k = id(rec[4])
                    if k not in deps:
                        deps[k] = (rec[4], False)
        need = {}
        for (d, raw) in deps.values():
            if d.is_dma:
                key = ("dma", d.tag)
                val = self.tagcount[d.tag]
                if need.get(key, 0) < val:
                    need[key] = val
            else:
                if d.eng == eng and eng == "pe":
                    continue
                key = ("eng", d.eng)
                cur = need.get(key)
                if cur is None or cur.idx < d.idx:
                    need[key] = d
        w = self.waited[eng]
        for key, v in need.items():
            if key[0] == "dma":
                if w.get(key, 0) >= v:
                    continue
                w[key] = v
                op.waits.append((key, v))
            else:
                if w.get(key, -1) >= v.idx:
                    continue
                w[key] = v.idx
                v.signal = True
                op.waits.append((key, v))
        if is_dma:
            self.tagcount[tag] = self.tagcount.get(tag, 0) + 16
        for (nm, p0, p1, f0, f1) in wrects:
            lst = self.track.setdefault(nm, [])
            lst[:] = [r for r in lst if not (p0 <= r[0] and r[1] <= p1 and f0 <= r[2] and r[3] <= f1)]
            lst.append([p0, p1, f0, f1, op, 1])
        for (nm, p0, p1, f0, f1) in prects:
            self.track[nm] = [[p0, p1, f0, f1, op, 2]]
        for (nm, p0, p1, f0, f1) in rrects:
            if nm.startswith("pb"):
                continue
            lst = self.track.setdefault(nm, [])
            done = False
            for r in lst:
                if r[5] == 0 and r[4].eng == eng and (not r[4].is_dma) and (not is_dma) \
                        and r[0] == p0 and r[1] == p1 and r[2] == f0 and r[3] == f1:
                    r[4] = op
                    done = True
                    break
            if not done:
                lst.append([p0, p1, f0, f1, op, 0])
        self.ops[eng].append(op)
        return op

    def dma(self, q, out, in_, tag):
        return self._add(q, lambda e: e.dma_start(out=out, in_=in_), [in_], [out], is_dma=True, tag=tag)

    def mm(self, out, lhsT, rhs, start=True, stop=True, **kw):
        rd = [lhsT, rhs] + ([] if start else [out])
        return self._add("pe", lambda e: e.matmul(out, lhsT, rhs, start=start, stop=stop, **kw), rd, [out])

    def tr(self, out, in_, ident):
        return self._add("pe", lambda e: e.transpose(out, in_, ident), [in_, ident], [out])

    def act(self, out, in_, func, bias=None, scale=None, accum_out=None):
        kw = {}
        rd = [in_]
        if bias is not None:
            kw["bias"] = bias
            if not isinstance(bias, (int, float)):
                rd.append(bias)
        if scale is not None:
            kw["scale"] = scale
            if not isinstance(scale, (int, float)):
                rd.append(scale)
        wr = [out]
        if accum_out is not None:
            kw["accum_out"] = accum_out
            wr.append(accum_out)
        return self._add("act", lambda e: e.activation(out, in_, func, **kw), rd, wr)

    def tt(self, eng, out, in0, in1, op):
        return self._add(eng, lambda e: e.tensor_tensor(out, in0, in1, op), [in0, in1], [out])

    def ts(self, eng, out, in0, s1, s2, op0, op1=None):
        rd = [in0] + [s for s in (s1, s2) if s is not None and not isinstance(s, (int, float))]
        kw = {}
        if op1 is not None:
            kw["op1"] = op1
        return self._add(eng, lambda e: e.tensor_scalar(out, in0, s1, s2, op0, **kw), rd, [out])

    def stt(self, out, in0, scalar, in1, op0, op1):
        rd = [in0, in1] + ([] if isinstance(scalar, (int, float)) else [scalar])
        return self._add("dve", lambda e: e.scalar_tensor_tensor(out, in0, scalar, in1, op0, op1), rd, [out])

    def copy(self, eng, out, in_):
        if eng == "act":
            return self._add("act", lambda e: e.copy(out, in_), [in_], [out])
        return self._add(eng, lambda e: e.tensor_copy(out, in_), [in_], [out])

    def memset(self, eng, ap, val):
        return self._add(eng, lambda e: e.memset(ap, val), [], [ap])

    def recip(self, out, in_):
        return self._add("dve", lambda e: e.reciprocal(out, in_), [in_], [out])

    def rsum(self, out, in_):
        return self._add("dve", lambda e: e.tensor_reduce(out, in_, AX.X, ALU.add), [in_], [out])

    def emit(self, final_dma_tags=()):
        nc = self.nc
        for e in ENGS:
            c = 0
            for op in self.ops[e]:
                if op.signal and not op.is_dma:
                    c += 1
                    op.sigval = c
        with contextlib.ExitStack() as st:
            esem = {e: st.enter_context(nc.semaphore("s_" + e)) for e in ENGS}
            dsem = {t: st.enter_context(nc.semaphore("d_%d" % i)) for i, t in enumerate(self.tagcount)}
            block = st.enter_context(nc.Block())
            engobj = {"pe": block.tensor, "act": block.scalar, "dve": block.vector,
                      "pool": block.gpsimd, "sp": block.sync}

            def make(ename):
                def body(eng):
                    for op in self.ops[ename]:
                        for (key, v) in op.waits:
                            if key[0] == "dma":
                                eng.wait_ge(dsem[key[1]], v)
                            else:
                                eng.wait_ge(esem[key[1]], v.sigval)
                        ins = op.fn(eng)
                        if op.is_dma:
                            ins.then_inc(dsem[op.tag], 16)
                        elif op.signal:
                            ins.then_inc(esem[ename], 1)
                    if ename == "sp":
                        for t in final_dma_tags:
                            eng.wait_ge(dsem[t], self.tagcount[t])
                return body

            for e in ENGS:
                engobj[e](make(e))
        return nc


def _prod(shape):
    n = 1
    for s in shape:
        n *= s
    return n


class Arena:
    def __init__(self, nc, nbytes):
        self.t = nc.alloc_sbuf_tensor("arena", [128, nbytes // 4], F32)
        self.nbytes = nbytes

    def view(self, off, shape, dt):
        n = _prod(shape)
        esz = 4 if dt == F32 else 2
        assert off % 4 == 0 and (n * esz) % 4 == 0 and off + n * esz <= self.nbytes, (off, shape)
        ap = self.t[:, off // 4:(off + n * esz) // 4]
        if dt != F32:
            ap = ap.bitcast(dt)
        if len(shape) > 1:
            names = " ".join("d%d" % i for i in range(len(shape)))
            kw = {"d%d" % i: shape[i] for i in range(1, len(shape))}
            ap = ap.rearrange("p (%s) -> p %s" % (names, names), **kw)
        return ap


class Bump:
    def __init__(self, arena, lo, hi):
        self.a = arena; self.lo = lo; self.hi = hi; self.cur = lo

    def alloc(self, shape, dt=F32):
        esz = 4 if dt == F32 else 2
        n = (_prod(shape) * esz + 31) // 32 * 32
        off = self.cur
        self.cur += n
        assert self.cur <= self.hi, ("arena overflow", self.cur, self.hi)
        return self.a.view(off, shape, dt)


class Ring:
    def __init__(self, items):
        self.items = list(items); self.i = 0

    def next(self):
        x = self.items[self.i % len(self.items)]
        self.i += 1
        return x


_CNAMES = ["IDENT", "M1", "M2", "TRI", "SUF", "IND0", "IND1", "S32", "OFF", "ONES"]
_CB = {"MB64": 512, "MBA4": 512, "MBB4": 512}
NCF = 128 * len(_CNAMES)
NCONST = NCF + 512 * 3


def make_consts():
    i = np.arange(128)[:, None]
    j = np.arange(128)[None, :]
    same = (i // 64) == (j // 64)
    c = {}
    c["IDENT"] = (i == j)
    c["M1"] = (i > j)
    c["M2"] = (i <= j)
    c["TRI"] = (i <= j) & same
    c["SUF"] = (i > j) & same
    c["IND0"] = (i < 64) & (j >= 0)
    c["IND1"] = (i >= 64) & (j >= 0)
    c["S32"] = (i < j) & ((i // 32) == (j // 32))
    c["OFF"] = same & ((i % 64) < 32) & ((j % 64) >= 32)
    c["ONES"] = np.ones((128, 128), bool)
    cols = [c[n].astype(np.float32) for n in _CNAMES]
    mb64 = np.where((i <= j) & same, 0.0, NEG).astype(np.float32)
    mba = np.where(i <= j, 0.0, NEG).astype(np.float32)
    mbb = np.where(i >= j, 0.0, NEG).astype(np.float32)
    cols += [np.tile(mb64, (1, 4)), np.tile(mba, (1, 4)), np.tile(mbb, (1, 4))]
    return np.ascontiguousarray(np.concatenate(cols, 1))


def build(stage="full", dumps=()):
    nc = bass.Bass("TRN2", target_bir_lowering=False)
    P = Prog(nc)
    dumps = set(dumps)
    dump_tags = []

    def din(name, shape):
        return nc.dram_tensor(name, list(shape), F32, kind="ExternalInput").ap()

    x_d = din("x", [S, D])
    win_d = din("w_in", [D, 3592])
    wout_d = din("w_out", [D, D])
    wup_d = din("w_up", [D, 5632])
    wdn_d = din("w_down", [2816, D])
    n1_d = din("n1rep", [128, D])
    n2_d = din("n2rep", [128, D])
    nf_d = din("nfrep", [128, D])
    cwa_d = din("cwA", [128, 48])
    cwf_d = din("cwF", [128, 132])
    gnw_d = din("gnwrep", [128, 128])
    dtb_d = din("dtbrep", [128, 64])
    alog_d = din("alogrep", [128, 64])
    cst_d = din("consts", [128, NCONST])
    out_d = nc.dram_tensor("out", [S, D], F32, kind="ExternalOutput").ap()

    def dump(name, sb_ap, shape):
        if name not in dumps:
            return
        d = nc.dram_tensor("dbg_" + name, list(shape), sb_ap.dtype, kind="ExternalOutput").ap()
        tg = "dbg_" + name
        P.dma("sp", d, sb_ap, tg)
        dump_tags.append(tg)

    win_v = win_d.rearrange("(k p) c -> p k c", p=128)
    wout_v = wout_d.rearrange("(k p) c -> p k c", p=128)
    wup_v = wup_d.rearrange("(k p) c -> p k c", p=128)
    wdn_v = wdn_d.rearrange("(k p) c -> p k c", p=128)

    ARENA_BYTES = 207872
    A = Arena(nc, ARENA_BYTES)
    pb = [nc.alloc_psum_tensor("pb%d" % i, [128, 512], F32) for i in range(8)]

    def pbf(i):
        return pb[i][:]

    def pbb(i):
        return pb[i][:].bitcast(BF16)

    CT = A.view(0, [8, S], BF16)
    hT = A.view(32768, [8, S], BF16)
    pers = Bump(A, 65536, 86016)
    CF = pers.alloc([NCF])
    CBm = pers.alloc([1536 + 256], BF16)
    cwA = pers.alloc([12, 4])
    cwF = pers.alloc([44, 3])
    HALOA = pers.alloc([12, 3])
    HALOF = pers.alloc([44, 2])
    GNW = pers.alloc([128])
    NW1 = pers.alloc([D])
    NW2 = pers.alloc([D])
    SCR_LO = 86016
    SCR_HI = ARENA_BYTES

    def cf(name):
        k = _CNAMES.index(name)
        return CF[:, 128 * k:128 * (k + 1)]

    IDENT = cf("IDENT"); M1 = cf("M1"); M2 = cf("M2"); TRI = cf("TRI"); SUF = cf("SUF")
    IND0 = cf("IND0"); IND1 = cf("IND1"); S32 = cf("S32"); OFFM = cf("OFF"); ONES = cf("ONES")
    MB64b = CBm[:, 0:512]; MBA4b = CBm[:, 512:1024]; MBB4b = CBm[:, 1024:1536]
    IDENTb = CBm[:, 1536:1664]; ONESb = CBm[:, 1664:1792]

    P.dma("sp", CF, cst_d[:, 0:NCF], "c_cf")
    ctmp = A.view(16384 + 8192, [1536], F32)
    P.dma("sp", ctmp, cst_d[:, NCF:NCONST], "c_tmp")
    P.copy("dve", CBm[:, 0:1536], ctmp)
    P.copy("dve", IDENTb, IDENT)
    P.copy("dve", ONESb, ONES)
    P.dma("sp", cwA.rearrange("p a b -> p (a b)"), cwa_d, "c_cwa")
    P.dma("sp", cwF.rearrange("p a b -> p (a b)"), cwf_d, "c_cwf")
    P.dma("sp", GNW, gnw_d, "c_gnw")
    P.dma("sp", NW1, n1_d, "c_nw1")
    P.memset("pool", HALOA.rearrange("p a b -> p (a b)"), 0.0)
    P.memset("pool", HALOF.rearrange("p a b -> p (a b)"), 0.0)

    def bc_last(ap, n):
        return ap.unsqueeze(2).broadcast_to([128, ap.shape[1], n])

    def bc_mid(ap, n):
        return ap.unsqueeze(1).broadcast_to([128, n, ap.shape[1]])

    def rmsnorm_stage1(src_tile, wtile, scr):
        junk, ssv, rst, hb = scr
        P.act(junk, src_tile, AF.Square, accum_out=ssv)
        P.act(rst, ssv, AF.Ln, bias=EPS, scale=1.0 / D)
        P.act(rst, rst, AF.Exp, scale=-0.5)
        P.stt(hb, src_tile, rst, wtile, ALU.mult, ALU.mult)

    def rmsnorm_stage1a(src_tile, scr):
        junk, ssv, rst, hb = scr
        P.act(junk, src_tile, AF.Square, accum_out=ssv)
        P.act(rst, ssv, AF.Ln, bias=EPS, scale=1.0 / D)
        P.act(rst, rst, AF.Exp, scale=-0.5)

    def rmsnorm_stage1b(src_tile, wtile, scr):
        junk, ssv, rst, hb = scr
        P.stt(hb, src_tile, rst, wtile, ALU.mult, ALU.mult)

    def rmsnorm_stage2(dstT, col0, scr, bank):
        hb = scr[3]
        psT = pbb(bank)
        for kc in range(8):
            P.tr(psT[:, 128 * kc:128 * (kc + 1)], hb[:, 128 * kc:128 * (kc + 1)], IDENTb)
        P.copy("act", dstT[:, :, col0:col0 + 128], psT.rearrange("p (k t) -> p k t", k=8))

    bA = Bump(A, 0, 16384)
    xbuf = [bA.alloc([D]) for _ in range(3)]
    junkA = bA.alloc([D], BF16)
    hbuf = [NW2.bitcast(BF16)[:, 0:D], NW2.bitcast(BF16)[:, D:2 * D]]
    ssA = bA.alloc([NT])
    rsA = bA.alloc([NT])
    scrA = [(junkA, ssA[:, i:i + 1], rsA[:, i:i + 1], hbuf[i % 2]) for i in range(NT)]
    for i in range(NT + 2):
        if i < NT:
            xt = xbuf[i % 3]
            P.dma("sp", xt, x_d[128 * i:128 * (i + 1), :], "xa%d" % (i % 3))
            rmsnorm_stage1a(xt, scrA[i])
        if 1 <= i <= NT:
            rmsnorm_stage1b(xbuf[(i - 1) % 3], NW1, scrA[i - 1])
        if i >= 2:
            rmsnorm_stage2(hT, 128 * (i - 2), scrA[i - 2], 6 + (i % 2))
    dump("hT", hT.rearrange("p k t -> p (k t)"), [128, 8 * S])
    if stage == "A":
        return finish(nc, P, out_d, dump_tags)

    bG = Bump(A, SCR_LO, SCR_HI)
    bC = Bump(A, 16384, 32768)
    wz = bC.alloc([8, 512], BF16)
    qkvT = [dict(q=bG.alloc([4, 512], BF16), k=bG.alloc([4, 512], BF16), v=bG.alloc([4, 512], BF16))
            for _ in range(4)]
    szgb = [bG.alloc([4, 512], BF16), bG.alloc([4, 512], BF16), bG.alloc([4, 512], BF16), bC.alloc([4, 512], BF16)]
    BA = bG.alloc([NT, 8])
    sc_x = bG.alloc([64]); sc_mx = bG.alloc([64]); sc_mn = bG.alloc([64])
    dtb = bG.alloc([64]); negA = bG.alloc([64])
    gS = bG.alloc([64]); betaS = bG.alloc([64]); gamS = bG.alloc([64]); kesS = bG.alloc([64])
    gendS = bG.alloc([2, 64])
    wba = bG.alloc([8, 8], BF16)
    Sst = bG.alloc([4, 128]); Sb = bG.alloc([4, 128], BF16); ub = bG.alloc([4, 128], BF16)
    Obuf = [bC.alloc([4, 128]) for _ in range(2)]
    sqO = bG.alloc([4, 128], BF16); oab = sqO
    kesLo = bG.alloc([64]); kesHi = bG.alloc([64])
    ssq = bG.alloc([4]); rsq = bG.alloc([4])
    ov0 = bG.cur
    gset = []
    for _ in range(2):
        gset.append(dict(
            G1m=bG.alloc([4, 128]),
            DECT=bG.alloc([4, 128], BF16), Uall=bG.alloc([4, 128], BF16), Eu=bG.alloc([4, 128], BF16),
            PTa=bG.alloc([4, 128], BF16), PTb=bG.alloc([4, 128], BF16),
            UL0=bG.alloc([2, 4, 128], BF16), PWA=bG.alloc([2, 4, 128], BF16), PWB=bG.alloc([2, 4, 128], BF16),
            TTb=bG.alloc([4, 128], BF16), vb=bG.alloc([4, 128], BF16), gk=bG.alloc([4, 128], BF16)))
    scanop = []
    for _ in range(3):
        scanop.append(dict(KLO=bG.alloc([4, 128], BF16), KHI=bG.alloc([4, 128], BF16), ATT=bG.alloc([4, 128], BF16),
                           QD=bG.alloc([4, 128], BF16), WKT=bG.alloc([4, 128], BF16),
                           UV=bG.alloc([4, 128], BF16)))
    bO = Bump(A, ov0, bG.cur)
    wqkva = bO.alloc([3, 8, 512], BF16)
    rawb = [bO.alloc([520], BF16) for _ in range(3)]
    dgcb = [bO.alloc([4, 128], BF16) for _ in range(3)]
    sqbs = [bO.alloc([512], BF16) for _ in range(4)]
    rtbs = [bO.alloc([512]) for _ in range(4)]
    for c3 in range(3):
        P.dma("pool", wqkva[:, c3, :, :], win_v[:, :, 512 * c3:512 * (c3 + 1)], "wqkva%d" % c3)
    P.dma("pool", wba, win_v[:, :, 2048:2056], "wba")
    P.dma("pool", wz, win_v[:, :, 1536:2048], "wz")

    ringG = Ring([0, 1, 2, 3, 4, 5, 6, 7])

    g3 = gS.rearrange("p (n h) -> p n h", h=4)
    beta3 = betaS.rearrange("p (n h) -> p n h", h=4)
    gam3 = gamS.rearrange("p (n h) -> p n h", h=4)
    kesLo3 = kesLo.rearrange("p (n h) -> p n h", h=4)
    kesHi3 = kesHi.rearrange("p (n h) -> p n h", h=4)
    gend4 = gendS.rearrange("p a (n h) -> p a n h", h=4)

    def SCALARS():
        P.dma("sp", dtb, dtb_d, "c_dtb")
        P.dma("sp", negA, alog_d, "c_alog")
        P.act(negA, negA, AF.Exp)
        P.ts("dve", negA, negA, -1.0, None, ALU.mult)
        bk = ringG.next()
        psBA = pbf(bk)[:, 0:128].rearrange("p (n c) -> p n c", c=8)
        for i in range(NT):
            for kc in range(8):
                P.mm(psBA[:, i, :], hT[:, kc, 128 * i:128 * (i + 1)], wba[:, kc, :], start=(kc == 0), stop=(kc == 7))
        P.copy("act", BA, psBA)
        x3 = sc_x.rearrange("p (n h) -> p n h", h=4)
        P.tt("dve", x3, BA[:, :, 4:8], dtb.rearrange("p (n h) -> p n h", h=4), ALU.add)
        P.ts("dve", sc_mx, sc_x, 0.0, None, ALU.max)
        P.ts("dve", sc_mn, sc_x, 0.0, None, ALU.min)
        P.tt("dve", sc_mn, sc_mn, sc_mx, ALU.subtract)
        P.act(sc_mn, sc_mn, AF.Exp)
        P.act(sc_mn, sc_mn, AF.Ln, bias=1.0)
        P.tt("dve", sc_mx, sc_mx, sc_mn, ALU.add)
        P.tt("dve", gS, sc_mx, negA, ALU.mult)
        P.act(betaS.rearrange("p (n h) -> p n h", h=4), BA[:, :, 0:4], AF.Sigmoid)
        bk = ringG.next()
        psg = pbf(bk)
        P.mm(psg[:, 0:64], TRI, gS)
        P.mm(psg[:, 64:128], SUF, gS)
        P.mm(psg[:, 128:192], IND0, gS)
        P.mm(psg[:, 192:256], IND1, gS)
        P.act(gamS, psg[:, 0:64], AF.Exp)
        P.act(kesS, psg[:, 64:128], AF.Exp)
        P.tt("dve", kesS, kesS, betaS, ALU.mult)
        P.act(gendS.rearrange("p a b -> p (a b)"), psg[:, 128:256], AF.Exp)
        dump("gS", gS, [128, 64]); dump("betaS", betaS, [128, 64]); dump("gamS", gamS, [128, 64])
        dump("kesS", kesS, [128, 64]); dump("gendS", gendS.rearrange("p a b -> p (a b)"), [128, 128])
        P.ts("dve", kesLo, kesS, IND0[:, 0:1], None, ALU.mult)
        P.ts("dve", kesHi, kesS, IND1[:, 0:1], None, ALU.mult)


    def G1(m):
        t0 = 512 * m
        o = qkvT[m]
        for c in range(12):
            ps = pbf(ringG.next())
            for kc in range(8):
                P.mm(ps, wqkva[:, c // 4, kc, 128 * (c % 4):128 * (c % 4 + 1)], hT[:, kc, t0:t0 + 512], start=(kc == 0), stop=(kc == 7))
            raw = rawb[(c + m) % 3]; dgc = dgcb[(c + m) % 3]
            for i_ in range(4):
                P.ts("dve", dgc[:, i_, :], IDENTb, cwA[:, c, i_:i_ + 1], None, ALU.mult)
            P.copy("pool", raw[:, 0:3], HALOA[:, c, :])
            P.copy("act", raw[:, 3:515], ps)
            P.copy("pool", HALOA[:, c, :], raw[:, 512:515])
            psC = pbf(ringG.next())
            for i_ in range(4):
                P.mm(psC, dgc[:, i_, :], raw[:, i_:i_ + 512], start=(i_ == 0), stop=(i_ == 3))
            dst = (o["q"], o["k"], o["v"])[c // 4][:, c % 4, :]
            P.act(dst, psC, AF.Silu)
            if c % 3 == 2:
                nz = 4 * m + c // 3
                psZ = pbf(ringG.next())
                for kc in range(8):
                    P.mm(psZ, hT[:, kc, 128 * nz:128 * (nz + 1)], wz[:, kc, :], start=(kc == 0), stop=(kc == 7))
                szg = szgb[m][:, c // 3, :]
                P.act(szg, psZ, AF.Silu)
                P.tt("pool", szg.rearrange("p (h d) -> p h d", h=4), szg.rearrange("p (h d) -> p h d", h=4),
                     bc_mid(GNW, 4), ALU.mult)
            yield
        for c in range(8):
            dst = (o["q"], o["k"])[c // 4][:, c % 4, :]
            sc = 128.0 if c < 4 else 1.0
            sqb = sqbs[(c + m) % 4]; rtb = rtbs[(c + m) % 4]
            P.tt("pool", sqb, dst, dst, ALU.mult)
            psn = pbf(ringG.next())
            P.mm(psn, ONESb, sqb)
            P.act(rtb, psn, AF.Ln, bias=EPS * sc, scale=sc)
            P.act(rtb, rtb, AF.Exp, scale=-0.5)
            P.tt("dve", dst, dst, rtb, ALU.mult)
            yield
        if m == 0:
            dump("qnT0", o["q"].rearrange("p h t -> p (h t)"), [128, 2048])
            dump("knT0", o["k"].rearrange("p h t -> p (h t)"), [128, 2048])
            dump("vsT0", o["v"].rearrange("p h t -> p (h t)"), [128, 2048])

    def G2(n):
        tl = 128 * (n % 4)
        so = scanop[n % 3]
        st = gset[n % 2]
        qnT = qkvT[n // 4]["q"]; knT = qkvT[n // 4]["k"]; vsT = qkvT[n // 4]["v"]
        G1m = st["G1m"]; DECT = st["DECT"]; Uall = st["Uall"]; Eu = st["Eu"]
        UL0 = st["UL0"]; PWA = st["PWA"]; PWB = st["PWB"]
        TTb = st["TTb"]; vb = st["vb"]; gk = st["gk"]
        Du, Dl = UL0[:, 0], UL0[:, 1]
        gam_b = bc_last(gam3[:, n, :], 128)
        keslo_b = bc_last(kesLo3[:, n, :], 128)
        keshi_b = bc_last(kesHi3[:, n, :], 128)
        beta_b = bc_last(beta3[:, n, :], 128)
        g_b = bc_last(g3[:, n, :], 128)

        def bankf():
            return pbf(ringG.next()).rearrange("p (h d) -> p h d", h=4)

        def bankb():
            return pbb(ringG.next())[:, 0:512].rearrange("p (h d) -> p h d", h=4)

        psKV = pbb(ringG.next()).rearrange("p (a h d) -> p a h d", a=2, h=4)
        psK = psKV[:, 0]; psV = psKV[:, 1]
        for h in range(4):
            P.tr(psK[:, h, :], knT[:, h, tl:tl + 128], IDENTb)
            P.tr(psV[:, h, :], vsT[:, h, tl:tl + 128], IDENTb)
        P.tt("dve", gk, psK, gam_b, ALU.mult)
        P.tt("dve", so["KLO"], psK, keslo_b, ALU.mult)
        P.tt("dve", so["KHI"], psK, keshi_b, ALU.mult)
        P.copy("act", vb, psV)
        yield
        P.tt("pool", G1m, bc_mid(M1, 4), g_b, ALU.mult)
        psD = bankf()
        P.mm(psD, IDENTb, MB64b, start=True, stop=False)
        for h in range(4):
            P.mm(psD[:, h, :], G1m[:, h, :], M2, start=False, stop=(h == 3))
        P.act(DECT, psD, AF.Exp)
        P.tt("pool", DECT, DECT, beta_b, ALU.mult)
        yield
        dg = Uall
        P.tt("pool", dg, bc_mid(IDENT, 4), gam_b, ALU.mult)
        psG = bankf()
        for h in range(4):
            P.mm(psG[:, h, :], ONESb, dg[:, h, :])
        P.tt("dve", so["QD"], qnT[:, :, tl:tl + 128], psG, ALU.mult)
        yield
        psKK = bankf(); psQK = bankf()
        for h in range(4):
            P.mm(psKK[:, h, :], knT[:, h, tl:tl + 128], knT[:, h, tl:tl + 128])
        for h in range(4):
            P.mm(psQK[:, h, :], knT[:, h, tl:tl + 128], qnT[:, h, tl:tl + 128])
        P.tt("dve", Uall, psKK, DECT, ALU.mult)
        P.tt("dve", so["ATT"], psQK, DECT, ALU.mult)
        P.tt("pool", Du, Uall, bc_mid(S32, 4), ALU.mult)
        P.tt("pool", Eu, Uall, bc_mid(OFFM, 4), ALU.mult)
        yield
        psT = bankb()
        for h in range(4):
            P.tr(psT[:, h, :], Du[:, h, :], IDENTb)
        P.copy("act", Dl, psT)
        P.tt("pool", st["PTa"], bc_mid(IDENT, 4), Du, ALU.subtract)
        yield
        pw = [UL0, PWA, PWB, PWA, PWB]
        PT, PTn = st["PTa"], st["PTb"]
        for k in range(1, 5):
            cur = pw[k - 1]; nxt = pw[k]
            if k < 4:
                psU = bankf()
                for h in range(4):
                    P.mm(psU[:, h, :], cur[:, 1, h, :], cur[:, 0, h, :])
            psL = bankf()
            for h in range(4):
                P.mm(psL[:, h, :], cur[:, 0, h, :], cur[:, 1, h, :])
            if k > 1:
                ps3 = bankf()
                for h in range(4):
                    P.mm(ps3[:, h, :], cur[:, 1, h, :], PT[:, h, :])
            if k < 4:
                P.copy("act", nxt[:, 0], psU)
            P.copy("act", nxt[:, 1], psL)
            if k > 1:
                P.tt("dve", PTn, PT, ps3, ALU.add)
                PT, PTn = PTn, PT
            yield
        ps3 = bankf()
        for h in range(4):
            P.mm(ps3[:, h, :], PWB[:, 1, h, :], PT[:, h, :])
        P.tt("dve", PTn, PT, ps3, ALU.add)
        PT, PTn = PTn, PT
        yield
        Pm = PWA[:, 0]; XT = DECT
        p1 = bankb()
        for h in range(4):
            P.tr(p1[:, h, :], PT[:, h, :], IDENTb)
        P.copy("act", Pm, p1)
        yield
        p2 = bankf()
        for h in range(4):
            P.mm(p2[:, h, :], Eu[:, h, :], Pm[:, h, :])
        P.copy("act", XT, p2)
        yield
        p3 = bankf()
        for h in range(4):
            P.mm(p3[:, h, :], XT[:, h, :], PT[:, h, :])
        P.tt("dve", TTb, PT, p3, ALU.subtract)
        yield
        p1 = bankf(); p2 = bankf()
        for h in range(4):
            P.mm(p1[:, h, :], TTb[:, h, :], vb[:, h, :])
        for h in range(4):
            P.mm(p2[:, h, :], gk[:, h, :], TTb[:, h, :])
        P.copy("act", so["UV"], p1)
        P.copy("dve", so["WKT"], p2)
        yield

    def SCAN(n):
        so = scanop[n % 3]
        O = Obuf[n % 2]
        for half in range(2):
            r0 = 64 * half
            rs = slice(r0, r0 + 64)
            kend = so["KLO"] if half == 0 else so["KHI"]
            psA = pbf(ringG.next()).rearrange("p (h d) -> p h d", h=4)
            for h in range(4):
                P.mm(psA[:, h, :], so["WKT"][:, h, :], Sb[:, h, :])
            P.tt("dve", ub[rs], so["UV"][rs], psA[rs], ALU.subtract)
            yield
            psS = pbf(ringG.next()).rearrange("p (h d) -> p h d", h=4)
            psO = pbf(ringG.next()).rearrange("p (h d) -> p h d", h=4)
            for h in range(4):
                P.mm(psS[:, h, :], kend[:, h, :], ub[:, h, :])
            for h in range(4):
                P.mm(psO[:, h, :], so["QD"][:, h, :], Sb[:, h, :], start=True, stop=False)
                P.mm(psO[:, h, :], so["ATT"][:, h, :], ub[:, h, :], start=False, stop=True)
            for h in range(4):
                P.stt(Sst[:, h, :], Sst[:, h, :], gend4[:, half, n, h:h + 1], psS[:, h, :], ALU.mult, ALU.add)
            P.copy("act", Sb, Sst)
            P.copy("act", O[rs], psO[rs])
            yield
        if n == 0:
            dump("O0", O.rearrange("p h d -> p (h d)"), [128, 512])
        if n == 15:
            dump("O15", O.rearrange("p h d -> p (h d)"), [128, 512])
        P.act(sqO, O, AF.Square)
        P.rsum(ssq, sqO)
        P.act(rsq, ssq, AF.Ln, bias=EPS, scale=1.0 / 128)
        P.act(rsq, rsq, AF.Exp, scale=-0.5)
        szg = szgb[n // 4][:, n % 4, :].rearrange("p (h d) -> p h d", h=4)
        P.tt("dve", sqO, O, bc_last(rsq, 128), ALU.mult)
        P.tt("dve", oab, sqO, szg, ALU.mult)
        yield
        psT = pbb(ringG.next())[:, 0:512].rearrange("p (h d) -> p h d", h=4)
        for h in range(4):
            P.tr(psT[:, h, :], oab[:, h, :], IDENTb)
        P.copy("act", CT[:, 0:4, 128 * n:128 * (n + 1)], psT)
        yield

    def advance(must, opt):
        live = [True] * len(must)
        while any(live) or any(q[1] > 0 for q in opt):
            for gi, g in enumerate(must):
                if live[gi]:
                    try:
                        next(g)
                    except StopIteration:
                        live[gi] = False
            for q in opt:
                if q[1] > 0:
                    q[1] -= 1
                    try:
                        next(q[0])
                    except StopIteration:
                        q[1] = 0
                        q[2] = True

    P.memset("dve", Sst.rearrange("p h d -> p (h d)"), 0.0)
    P.memset("pool", Sb.rearrange("p h d -> p (h d)"), 0.0)
    P.memset("pool", ub.rearrange("p h d -> p (h d)"), 0.0)
    advance([G1(m_) for m_ in range(4)], [])
    SCALARS()
    g2 = {0: G2(0), 1: G2(1)}
    advance([g2[0]], [[g2[1], 7, False]])
    for n in range(NT):
        must = [SCAN(n)]
        if n + 1 < NT:
            must.append(g2[n + 1])
        opt = []
        if n + 2 < NT:
            g2[n + 2] = G2(n + 2)
            opt.append([g2[n + 2], 7, False])
        advance(must, opt)
    dump("CTa", CT[:, 0:4, :].rearrange("p k t -> p (k t)"), [128, 4 * S])
    if stage == "G":
        return finish(nc, P, out_d, dump_tags)

    bB = Bump(A, SCR_LO, SCR_HI)
    wqkvb = bB.alloc([3, 8, 512], BF16)
    for c3 in range(3):
        P.dma("pool", wqkvb[:, c3, :, :],
              win_v[:, :, 2056 + 512 * c3:2056 + 512 * (c3 + 1)], "wqkvb%d" % c3)
    QT0 = bB.alloc([S], BF16)
    KTz = [bB.alloc([S], BF16) for _ in range(2)]
    QTb = [QT0, QT0]
    P.memset("pool", KTz[0][64:128, :], 0.0)
    P.memset("pool", KTz[1][0:64, :], 0.0)
    VAll = bB.alloc([48, 4, 192], BF16)
    PTbuf = Ring([bB.alloc([1024], BF16) for _ in range(2)])
    PT3 = bB.alloc([16, 128], BF16)
    rden0 = bB.alloc([512])
    rdenb = [rden0, rden0]
    P.memset("pool", VAll[:, :, :, 64:128], 1.0)
    ringS = Ring([2, 3, 4, 5])
    ringP = Ring([6, 7])
    accR = Ring([0, 1])

    def tok_slices():
        sl = []
        for n in range(16):
            sl.append(slice(128 * n, 128 * (n + 1), 1))
        for r in range(4):
            for c in range(4):
                sl.append(slice(512 * c + r, 512 * (c + 1), 4))
        for r in range(16):
            sl.append(slice(r, S, 16))
        return sl

    TOK = tok_slices()

    def PROJ_V():
        for t in range(48):
            bank = ringP.next()
            ps = pbf(bank)
            sl = TOK[t]
            for kc in range(8):
                P.mm(ps, hT[:, kc, sl], wqkvb[:, 2, kc, :], start=(kc == 0), stop=(kc == 7))
            ps4 = ps.rearrange("p (j e d) -> p j e d", j=4, e=2)
            P.copy("act", VAll[:, t, :, 0:64], ps4[:, :, 0, :])
            P.copy("dve", VAll[:, t, :, 128:192], ps4[:, :, 1, :])

    def PROJ_B(j):
        k = j % 2
        QT = QTb[k]
        for (dst, cbase) in ((QT, 128 * j), (None, 512 + 128 * j)):
            for tb in range(4):
                bank = ringP.next()
                ps = pbf(bank)
                for kc in range(8):
                    P.mm(ps, wqkvb[:, cbase // 512, kc, cbase % 512:cbase % 512 + 128], hT[:, kc, 512 * tb:512 * (tb + 1)],
                         start=(kc == 0), stop=(kc == 7))
                cs = slice(512 * tb, 512 * (tb + 1))
                if dst is not None:
                    P.copy("dve" if tb % 2 else "act", dst[:, cs], ps)
                else:
                    P.copy("act", KTz[0][0:64, cs], ps[0:64, :])
                    P.copy("dve", KTz[1][64:128, cs], ps[64:128, :])

    def ATTN(j):
        k = j % 2
        QT = QTb[k]

        class _VA:
            def __init__(self, h):
                self.h = h

            def __getitem__(self, idx):
                e_ = self.h % 2
                return VAll[:, idx[1], self.h // 2, 64 * e_:64 * e_ + 128]

        for e in range(2):
            hp = slice(64 * e, 64 * e + 64)
            KT = KTz[e]
            fp = slice(0, 128)
            VA = _VA(2 * j + e)
            for g in range(4):
                bank = ringS.next()
                ps = pbf(bank)
                P.mm(ps, IDENTb, MBA4b, start=True, stop=False)
                for r4 in range(4):
                    r = 4 * g + r4
                    P.mm(ps[:, 128 * r4:128 * (r4 + 1)], KT[fp, r:S:16], QT[fp, r:S:16], start=False, stop=(r4 == 3))
                P.act(PT3[:, 4 * g:4 * g + 4, :], ps.rearrange("p (a d) -> p a d", a=4), AF.Exp, scale=0.125)
            for c in range(4):
                acc = pbf(accR.next())
                first = [True]

                def pv(out, lhsT, rhs, last=False):
                    P.mm(out, lhsT, rhs, start=first[0], stop=last, skip_group_check=True)
                    first[0] = False
                pt = PTbuf.next()
                bank = ringS.next(); ps = pbf(bank)
                P.mm(ps, IDENTb, MBA4b, start=True, stop=False)
                for i in range(4):
                    n = 4 * c + i
                    P.mm(ps[:, 128 * i:128 * (i + 1)], KT[fp, 128 * n:128 * (n + 1)], QT[fp, 128 * n:128 * (n + 1)],
                         start=False, stop=(i == 3))
                P.act(pt[:, 0:512], ps, AF.Exp, scale=0.125)
                bank = ringS.next(); ps = pbf(bank)
                P.mm(ps, IDENTb, MBB4b, start=True, stop=False)
                for i in range(4):
                    n = 4 * c + i
                    if n == 0:
                        continue
                    P.mm(ps[:, 128 * i:128 * (i + 1)], KT[fp, 128 * (n - 1):128 * n], QT[fp, 128 * n:128 * (n + 1)],
                         start=False, stop=(i == 3))
                P.act(pt[:, 512:1024], ps, AF.Exp, scale=0.125)
                for i in range(4):
                    n = 4 * c + i
                    pv(acc[:, 128 * i:128 * (i + 1)], VA[:, n, e, :], pt[:, 128 * i:128 * (i + 1)])
                    if n > 0:
                        pv(acc[:, 128 * i:128 * (i + 1)], VA[:, n - 1, e, :], pt[:, 512 + 128 * i:512 + 128 * (i + 1)])
                pt = PTbuf.next()
                bank = ringS.next(); ps = pbf(bank)
                P.mm(ps, IDENTb, MBA4b, start=True, stop=False)
                for r in range(4):
                    sl = slice(512 * c + r, 512 * (c + 1), 4)
                    P.mm(ps[:, 128 * r:128 * (r + 1)], KT[fp, sl], QT[fp, sl], start=False, stop=(r == 3))
                P.act(pt[:, 0:512], ps, AF.Exp, scale=0.125)
                if c > 0:
                    bank = ringS.next(); ps = pbf(bank)
                    P.mm(ps, IDENTb, MBB4b, start=True, stop=False)
                    for r in range(4):
                        sl = slice(512 * c + r, 512 * (c + 1), 4)
                        slk = slice(512 * (c - 1) + r, 512 * c, 4)
                        P.mm(ps[:, 128 * r:128 * (r + 1)], KT[fp, slk], QT[fp, sl], start=False, stop=(r == 3))
                    P.act(pt[:, 512:1024], ps, AF.Exp, scale=0.125)
                for r in range(4):
                    pv(acc[:, r:512:4], VA[:, 16 + 4 * r + c, e, :], pt[:, 128 * r:128 * (r + 1)])
                    if c > 0:
                        pv(acc[:, r:512:4], VA[:, 16 + 4 * r + c - 1, e, :], pt[:, 512 + 128 * r:512 + 128 * (r + 1)])
                for r in range(16):
                    pv(acc[:, r:512:16], VA[:, 32 + r, e, :], PT3[:, r, 32 * c:32 * (c + 1)], last=(r == 15))
                num = slice(64 * e, 64 * e + 64)
                den = slice(64 * (1 - e), 64 * (1 - e) + 64)
                rd = rdenb[c % 2]
                P.recip(rd[den, :], acc[den, :])
                P.tt("dve", CT[num, 4 + j, 512 * c:512 * (c + 1)], acc[num, :], rd[den, :], ALU.mult)

    PROJ_V()
    for j in range(4):
        PROJ_B(j)
        ATTN(j)
    dump("CTb", CT[:, 4:8, :].rearrange("p k t -> p (k t)"), [128, 4 * S])
    if stage == "B":
        return finish(nc, P, out_d, dump_tags)

    bF = Bump(A, SCR_LO, SCR_HI)
    h2T = A.view(32768, [8, 1024], BF16)
    woutb = A.view(32768 + 16384, [2, 8, 512], BF16)
    X1 = bF.alloc([8, D])
    wu = [bF.alloc([2, 8, 512], BF16) for _ in range(2)]
    wd = [bF.alloc([4, D], BF16) for _ in range(2)]
    aTb = [bF.alloc([4, 1024], BF16) for _ in range(2)]
    rawF = [[bF.alloc([520]) for _ in range(2)] for _ in range(2)]
    accF = [[bF.alloc([512]) for _ in range(2)] for _ in range(2)]
    sgF = [bF.alloc([512]) for _ in range(2)]
    hbF = [bF.alloc([D], BF16), accF[1][1].bitcast(BF16)]
    junkF = sgF[0].bitcast(BF16)
    ssF = bF.alloc([8]); rsF = bF.alloc([8]); ssO = bF.alloc([8]); rsO = bF.alloc([8])
    for c2 in range(2):
        P.dma("pool", woutb[:, c2, :, :], wout_v[:, :, 512 * c2:512 * (c2 + 1)], "wout%d" % c2)
    P.dma("sp", NW1, n2_d, "c_nw1")
    P.dma("sp", NW2, nf_d, "c_nw2")
    groups = [list(range(g0, min(g0 + 4, 22))) for g0 in range(0, 22, 4)]
    out_tags = ["out%d" % i for i in range(8)]
    items = [(H, gi) for H in range(2) for gi in range(len(groups))]

    def load_wu(k):
        H, gi = items[k]
        grp = groups[gi]; g0 = grp[0]; npair = len(grp); slot = k % 2
        P.dma("pool", wu[slot][:, 0, :, 0:128 * npair], wup_v[:, :, 128 * g0:128 * (g0 + npair)], "wug%d" % slot)
        P.dma("pool", wu[slot][:, 1, :, 0:128 * npair],
              wup_v[:, :, 2816 + 128 * g0:2816 + 128 * (g0 + npair)], "wuu%d" % slot)

    def load_wd(k):
        H, gi = items[k]
        grp = groups[gi]; g0 = grp[0]; npair = len(grp); slot = k % 2
        P.dma("pool", wd[slot][:, 0:npair, :], wdn_v[:, g0:g0 + npair, :], "wd%d" % slot)

    def PRO(H):
        scr = [(junkF, ssF[:, i8:i8 + 1], rsF[:, i8:i8 + 1], hbF[i8 % 2]) for i8 in range(8)]

        def st_a(i8):
            i = 8 * H + i8
            P.dma("sp", X1[:, i8, :], x_d[128 * i:128 * (i + 1), :], "xf%d" % i8)
            b0 = 2 * (i8 % 2)
            for h2 in range(2):
                for kc in range(8):
                    P.mm(pbf(b0 + h2), CT[:, kc, 128 * i:128 * (i + 1)], woutb[:, h2, kc, :],
                         start=(kc == 0), stop=(kc == 7))
            for h2 in range(2):
                P.tt("dve", X1[:, i8, 512 * h2:512 * (h2 + 1)], X1[:, i8, 512 * h2:512 * (h2 + 1)], pbf(b0 + h2), ALU.add)
            if i == 0:
                dump("X1", X1[:, 0, :], [128, D])
            rmsnorm_stage1a(X1[:, i8, :], scr[i8])

        for t in range(10):
            if t < 8:
                st_a(t)
            if 1 <= t <= 8:
                rmsnorm_stage1b(X1[:, t - 1, :], NW1, scr[t - 1])
            if t >= 2:
                rmsnorm_stage2(h2T, 128 * (t - 2), scr[t - 2], 4 + (t % 2))

    upar = [0]

    def UGEN(k):
        H, gi = items[k]
        grp = groups[gi]; slot = k % 2; aT = aTb[k % 2]
        for p, g in enumerate(grp):
            for tb in range(2):
                par = upar[0]
                upar[0] ^= 1
                cols = slice(512 * tb, 512 * (tb + 1))
                banks = (4, 5) if par == 0 else (6, 7)
                accs = []
                for gu in range(2):
                    ps = pbf(banks[gu])
                    for kc in range(8):
                        P.mm(ps, wu[slot][:, gu, kc, 128 * p:128 * (p + 1)], h2T[:, kc, cols],
                             start=(kc == 0), stop=(kc == 7))
                    cc = g + 22 * gu
                    raw = rawF[par][gu]; acc = accF[par][gu]
                    P.copy("pool", raw[:, 0:2], HALOF[:, cc, :])
                    P.copy("act", raw[:, 2:514], ps)
                    P.copy("pool", HALOF[:, cc, :], raw[:, 512:514])
                    P.act(acc, ps, AF.Identity, scale=cwF[:, cc, 2:3])
                    P.stt(acc, raw[:, 1:513], cwF[:, cc, 1:2], acc, ALU.mult, ALU.add)
                    P.stt(acc, raw[:, 0:512], cwF[:, cc, 0:1], acc, ALU.mult, ALU.add)
                    accs.append(acc)
                sg = sgF[par]
                P.act(sg, accs[0], AF.Silu)
                P.tt("pool", aT[:, p, cols], sg, accs[1], ALU.mult)
                yield

    def DGEN(k):
        H, gi = items[k]
        grp = groups[gi]; slot = k % 2; aT = aTb[k % 2]; npair = len(grp)
        last = (gi == len(groups) - 1)
        for i8 in range(8):
            i = 8 * H + i8
            b0 = 2 * (i8 % 2)
            for h2 in range(2):
                for p in range(npair):
                    P.mm(pbf(b0 + h2), aT[:, p, 128 * i8:128 * (i8 + 1)], wd[slot][:, p, 512 * h2:512 * (h2 + 1)],
                         start=(p == 0), stop=(p == npair - 1))
            for h2 in range(2):
                P.tt("dve", X1[:, i8, 512 * h2:512 * (h2 + 1)], X1[:, i8, 512 * h2:512 * (h2 + 1)],
                     pbf(b0 + h2), ALU.add)
            if last:
                P.act(junkF, X1[:, i8, :], AF.Square, accum_out=ssO[:, i8:i8 + 1])
                P.act(rsO[:, i8:i8 + 1], ssO[:, i8:i8 + 1], AF.Ln, bias=EPS, scale=1.0 / D)
                P.act(rsO[:, i8:i8 + 1], rsO[:, i8:i8 + 1], AF.Exp, scale=-0.5)
                P.stt(X1[:, i8, :], X1[:, i8, :], rsO[:, i8:i8 + 1], NW2, ALU.mult, ALU.mult)
                P.dma("sp", out_d[128 * i:128 * (i + 1), :], X1[:, i8, :], "out%d" % i8)
            yield

    def rr(gens):
        gens = list(gens)
        live = [True] * len(gens)
        while any(live):
            for gi_, g_ in enumerate(gens):
                if live[gi_]:
                    try:
                        next(g_)
                    except StopIteration:
                        live[gi_] = False

    load_wu(0)
    prevD = None
    for k in range(len(items)):
        H, gi = items[k]
        load_wd(k)
        if k + 1 < len(items):
            load_wu(k + 1)
        if gi == 0:
            if prevD is not None:
                rr([prevD])
                prevD = None
            PRO(H)
        u = UGEN(k)
        rr([u] if prevD is None else [u, prevD])
        prevD = DGEN(k)
    rr([prevD])
    return finish(nc, P, out_d, dump_tags + out_tags)


def finish(nc, P, out_d, dump_tags):
    tags = list(dump_tags)
    P.emit(final_dma_tags=tags)
    return nc


def prep_inputs(inp):
    f = lambda a: np.ascontiguousarray(np.asarray(a, dtype=np.float32))
    x = f(inp["x"])
    rep = lambda v: np.ascontiguousarray(np.broadcast_to(f(v).reshape(1, -1), (128, f(v).size)))
    cwa = f(inp["conv_qkv_w"])[0]
    cwA = np.ascontiguousarray(cwa.T.reshape(12, 128, 4).transpose(1, 0, 2).reshape(128, 48))
    cwf = f(inp["ffn_conv_w"])[0]
    cwF = np.ascontiguousarray(cwf.T.reshape(44, 128, 3).transpose(1, 0, 2).reshape(128, 132))
    shared = {
        "w_in": f(inp["w_in"])[0], "w_out": f(inp["w_out"])[0], "w_up": f(inp["w_up"])[0],
        "w_down": f(inp["w_down"])[0],
        "n1rep": rep(inp["norm1_w"]), "n2rep": rep(inp["norm2_w"]), "nfrep": rep(inp["final_norm_w"]),
        "cwA": cwA, "cwF": cwF, "gnwrep": rep(inp["gdn_norm_w"]),
        "dtbrep": np.ascontiguousarray(np.tile(rep(inp["dt_bias"]), (1, 16))),
        "alogrep": np.ascontiguousarray(np.tile(rep(inp["a_log"]), (1, 16))),
        "consts": make_consts(),
    }
    maps = []
    for b in range(x.shape[0]):
        m = dict(shared)
        m["x"] = np.ascontiguousarray(x[b])
        maps.append(m)
    return maps


def kernel(**inputs):
    maps = prep_inputs(inputs)
    nc = build("full")
    res = run_bass_kernel_spmd(nc, maps, core_ids=list(range(8)))
    out = np.stack([np.asarray(r["out"], dtype=np.float32) for r in res.results], 0)
    return out
```

```python
import contextlib
import numpy as np
import concourse.bass as bass
import concourse.mybir as mybir
from concourse.bass_utils import run_bass_kernel_spmd

F32 = mybir.dt.float32
BF16 = mybir.dt.bfloat16
AF = mybir.ActivationFunctionType
ALU = mybir.AluOpType
AX = mybir.AxisListType

S = 2048
D = 1024
NT = 16
EPS = 1e-6
NEG = -30000.0
ENGS = ("pe", "act", "dve", "pool", "sp")


def _rect(ap):
    t = ap.tensor
    if str(ap.space) == "PSUM":
        return (t.name, 0, 128, 0, 2048)
    esz = mybir.dt.size(ap.dtype)
    pstride = 1
    for s in tuple(t.shape)[1:]:
        pstride *= s
    p0 = ap.start_partition()
    p1 = p0 + ap.partition_size()
    f0 = ap.offset - p0 * pstride
    ext = 0
    for (st, cnt) in tuple(ap.ap)[1:]:
        ext += abs(st) * (cnt - 1)
    return (t.name, p0, p1, f0 * esz, (f0 + ext + 1) * esz)


class _Op:
    __slots__ = ("eng", "fn", "idx", "is_dma", "tag", "waits", "signal", "sigval")


class Prog:
    def __init__(self, nc):
        self.nc = nc
        self.ops = {e: [] for e in ENGS}
        self.track = {}
        self.waited = {e: {} for e in ENGS}
        self.tagcount = {}

    def _add(self, eng, fn, reads, writes, is_dma=False, tag=None):
        op = _Op()
        op.eng = eng; op.fn = fn; op.is_dma = is_dma; op.tag = tag
        op.signal = False; op.sigval = None
        op.idx = len(self.ops[eng])
        op.waits = []
        deps = {}
        rrects = [_rect(a) for a in reads if a is not None and str(a.space) != "DRAM"]
        wrects = [_rect(a) for a in writes if a is not None and str(a.space) != "DRAM"]
        prects = [r for r in rrects if r[4] == 2048 and r[0].startswith("pb") and r not in wrects]
        for (nm, p0, p1, f0, f1) in rrects:
            for rec in self.track.get(nm, ()):
                if rec[5] == 1 and rec[0] < p1 and p0 < rec[1] and rec[2] < f1 and f0 < rec[3]:
                    deps[id(rec[4])] = (rec[4], True)
        for (nm, p0, p1, f0, f1) in wrects + prects:
            for rec in self.track.get(nm, ()):
                if rec[0] < p1 and p0 < rec[1] and rec[2] < f1 and f0 < rec[3]:
                    k = id(rec[4])
                    if k not in deps:
                        deps[k] = (rec[4], False)
        need = {}
        for (d, raw) in deps.values():
            if d.is_dma:
                key = ("dma", d.tag)
                val = self.tagcount[d.tag]
                if need.get(key, 0) < val:
                    need[key] = val
            else:
                if d.eng == eng and eng == "pe":
                    continue
                key = ("eng", d.eng)
                cur = need.get(key)
                if cur is None or cur.idx < d.idx:
                    need[key] = d
        w = self.waited[eng]
        for key, v in need.items():
            if key[0] == "dma":
                if w.get(key, 0) >= v:
                    continue
                w[key] = v
                op.waits.append((key, v))
            else:
                if w.get(key, -1) >= v.idx:
                    continue
                w[key] = v.idx
                v.signal = True
                op.waits.append((key, v))
        if is_dma:
            self.tagcount[tag] = self.tagcount.get(tag, 0) + 16
        for (nm, p0, p1, f0, f1) in wrects:
            lst = self.track.setdefault(nm, [])
            lst[:] = [r for r in lst if not (p0 <= r[0] and r[1] <= p1 and f0 <= r[2] and r[3] <= f1)]
            lst.append([p0, p1, f0, f1, op, 1])
        for (nm, p0, p1, f0, f1) in prects:
            self.track[nm] = [[p0, p1, f0, f1, op, 2]]
        for (nm, p0, p1, f0, f1) in rrects:
            if nm.startswith("pb"):
                continue
            lst = self.track.setdefault(nm, [])
            done = False
            for r in lst:
                if r[5] == 0 and r[4].eng == eng and (not r[4].is_dma) and (not is_dma) \
                        and r[0] == p0 and r[1] == p1 and r[2] == f0 and r[3] == f1:
                    r[4] = op
                    done = True
                    break
            if not done:
                lst.append([p0, p1, f0, f1, op, 0])
        self.ops[eng].append(op)
        return op

    def dma(self, q, out, in_, tag):
        return self._add(q, lambda e: e.dma_start(out=out, in_=in_), [in_], [out], is_dma=True, tag=tag)

    def mm(self, out, lhsT, rhs, start=True, stop=True, **kw):
        rd = [lhsT, rhs] + ([] if start else [out])
        return self._add("pe", lambda e: e.matmul(out, lhsT, rhs, start=start, stop=stop, **kw), rd, [out])

    def tr(self, out, in_, ident):
        return self._add("pe", lambda e: e.transpose(out, in_, ident), [in_, ident], [out])

    def act(self, out, in_, func, bias=None, scale=None, accum_out=None):
        kw = {}
        rd = [in_]
        if bias is not None:
            kw["bias"] = bias
            if not isinstance(bias, (int, float)):
                rd.append(bias)
        if scale is not None:
            kw["scale"] = scale
            if not isinstance(scale, (int, float)):
                rd.append(scale)
        wr = [out]
        if accum_out is not None:
            kw["accum_out"] = accum_out
            wr.append(accum_out)
        return self._add("act", lambda e: e.activation(out, in_, func, **kw), rd, wr)

    def tt(self, eng, out, in0, in1, op):
        return self._add(eng, lambda e: e.tensor_tensor(out, in0, in1, op), [in0, in1], [out])

    def ts(self, eng, out, in0, s1, s2, op0, op1=None):
        rd = [in0] + [s for s in (s1, s2) if s is not None and not isinstance(s, (int, float))]
        kw = {}
        if op1 is not None:
            kw["op1"] = op1
        return self._add(eng, lambda e: e.tensor_scalar(out, in0, s1, s2, op0, **kw), rd, [out])

    def stt(self, out, in0, scalar, in1, op0, op1):
        rd = [in0, in1] + ([] if isinstance(scalar, (int, float)) else [scalar])
        return self._add("dve", lambda e: e.scalar_tensor_tensor(out, in0, scalar, in1, op0, op1), rd, [out])

    def copy(self, eng, out, in_):
        if eng == "act":
            return self._add("act", lambda e: e.copy(out, in_), [in_], [out])
        return self._add(eng, lambda e: e.tensor_copy(out, in_), [in_], [out])

    def memset(self, eng, ap, val):
        return self._add(eng, lambda e: e.memset(ap, val), [], [ap])

    def recip(self, out, in_):
        return self._add("dve", lambda e: e.reciprocal(out, in_), [in_], [out])

    def rsum(self, out, in_):
        return self._add("dve", lambda e: e.tensor_reduce(out, in_, AX.X, ALU.add), [in_], [out])

    def emit(self, final_dma_tags=()):
        nc = self.nc
        for e in ENGS:
            c = 0
            for op in self.ops[e]:
                if op.signal and not op.is_dma:
                    c += 1
                    op.sigval = c
        with contextlib.ExitStack() as st:
            esem = {e: st.enter_context(nc.semaphore("s_" + e)) for e in ENGS}
            dsem = {t: st.enter_context(nc.semaphore("d_%d" % i)) for i, t in enumerate(self.tagcount)}
            block = st.enter_context(nc.Block())
            engobj = {"pe": block.tensor, "act": block.scalar, "dve": block.vector,
                      "pool": block.gpsimd, "sp": block.sync}

            def make(ename):
                def body(eng):
                    for op in self.ops[ename]:
                        for (key, v) in op.waits:
                            if key[0] == "dma":
                                eng.wait_ge(dsem[key[1]], v)
                            else:
                                eng.wait_ge(esem[key[1]], v.sigval)
                        ins = op.fn(eng)
                        if op.is_dma:
                            ins.then_inc(dsem[op.tag], 16)
                        elif op.signal:
                            ins.then_inc(esem[ename], 1)
                    if ename == "sp":
                        for t in final_dma_tags:
                            eng.wait_ge(dsem[t], self.tagcount[t])
                return body

            for e in ENGS:
                engobj[e](make(e))
        return nc


def _prod(shape):
    n = 1
    for s in shape:
        n *= s
    return n


class Arena:
    def __init__(self, nc, nbytes):
        self.t = nc.alloc_sbuf_tensor("arena", [128, nbytes // 4], F32)
        self.nbytes = nbytes

    def view(self, off, shape, dt):
        n = _prod(shape)
        esz = 4 if dt == F32 else 2
        assert off % 4 == 0 and (n * esz) % 4 == 0 and off + n * esz <= self.nbytes, (off, shape)
        ap = self.t[:, off // 4:(off + n * esz) // 4]
        if dt != F32:
            ap = ap.bitcast(dt)
        if len(shape) > 1:
            names = " ".join("d%d" % i for i in range(len(shape)))
            kw = {"d%d" % i: shape[i] for i in range(1, len(shape))}
            ap = ap.rearrange("p (%s) -> p %s" % (names, names), **kw)
        return ap


class Bump:
    def __init__(self, arena, lo, hi):
        self.a = arena; self.lo = lo; self.hi = hi; self.cur = lo

    def alloc(self, shape, dt=F32):
        esz = 4 if dt == F32 else 2
        n = (_prod(shape) * esz + 31) // 32 * 32
        off = self.cur
        self.cur += n
        assert self.cur <= self.hi, ("arena overflow", self.cur, self.hi)
        return self.a.view(off, shape, dt)


class Ring:
    def __init__(self, items):
        self.items = list(items); self.i = 0

    def next(self):
        x = self.items[self.i % len(self.items)]
        self.i += 1
        return x


_CNAMES = ["IDENT", "M1", "M2", "TRI", "SUF", "IND0", "IND1", "S32", "OFF", "ONES"]
_CB = {"MB64": 512, "MBA4": 512, "MBB4": 512}
NCF = 128 * len(_CNAMES)
NCONST = NCF + 512 * 3


def make_consts():
    i = np.arange(128)[:, None]
    j = np.arange(128)[None, :]
    same = (i // 64) == (j // 64)
    c = {}
    c["IDENT"] = (i == j)
    c["M1"] = (i > j)
    c["M2"] = (i <= j)
    c["TRI"] = (i <= j) & same
    c["SUF"] = (i > j) & same
    c["IND0"] = (i < 64) & (j >= 0)
    c["IND1"] = (i >= 64) & (j >= 0)
    c["S32"] = (i < j) & ((i // 32) == (j // 32))
    c["OFF"] = same & ((i % 64) < 32) & ((j % 64) >= 32)
    c["ONES"] = np.ones((128, 128), bool)
    cols = [c[n].astype(np.float32) for n in _CNAMES]
    mb64 = np.where((i <= j) & same, 0.0, NEG).astype(np.float32)
    mba = np.where(i <= j, 0.0, NEG).astype(np.float32)
    mbb = np.where(i >= j, 0.0, NEG).astype(np.float32)
    cols += [np.tile(mb64, (1, 4)), np.tile(mba, (1, 4)), np.tile(mbb, (1, 4))]
    return np.ascontiguousarray(np.concatenate(cols, 1))


def build(stage="full", dumps=()):
    nc = bass.Bass("TRN2", target_bir_lowering=False)
    P = Prog(nc)
    dumps = set(dumps)
    dump_tags = []

    def din(name, shape):
        return nc.dram_tensor(name, list(shape), F32, kind="ExternalInput").ap()

    x_d = din("x", [S, D])
    win_d = din("w_in", [D, 3592])
    wout_d = din("w_out", [D, D])
    wup_d = din("w_up", [D, 5632])
    wdn_d = din("w_down", [2816, D])
    n1_d = din("n1rep", [128, D])
    n2_d = din("n2rep", [128, D])
    nf_d = din("nfrep", [128, D])
    cwa_d = din("cwA", [128, 48])
    cwf_d = din("cwF", [128, 132])
    gnw_d = din("gnwrep", [128, 128])
    dtb_d = din("dtbrep", [128, 64])
    alog_d = din("alogrep", [128, 64])
    cst_d = din("consts", [128, NCONST])
    out_d = nc.dram_tensor("out", [S, D], F32, kind="ExternalOutput").ap()

    def dump(name, sb_ap, shape):
        if name not in dumps:
            return
        d = nc.dram_tensor("dbg_" + name, list(shape), sb_ap.dtype, kind="ExternalOutput").ap()
        tg = "dbg_" + name
        P.dma("sp", d, sb_ap, tg)
        dump_tags.append(tg)

    win_v = win_d.rearrange("(k p) c -> p k c", p=128)
    wout_v = wout_d.rearrange("(k p) c -> p k c", p=128)
    wup_v = wup_d.rearrange("(k p) c -> p k c", p=128)
    wdn_v = wdn_d.rearrange("(k p) c -> p k c", p=128)

    ARENA_BYTES = 207872
    A = Arena(nc, ARENA_BYTES)
    pb = [nc.alloc_psum_tensor("pb%d" % i, [128, 512], F32) for i in range(8)]

    def pbf(i):
        return pb[i][:]

    def pbb(i):
        return pb[i][:].bitcast(BF16)

    CT = A.view(0, [8, S], BF16)
    hT = A.view(32768, [8, S], BF16)
    pers = Bump(A, 65536, 86016)
    CF = pers.alloc([NCF])
    CBm = pers.alloc([1536 + 256], BF16)
    cwA = pers.alloc([12, 4])
    cwF = pers.alloc([44, 3])
    HALOA = pers.alloc([12, 3])
    HALOF = pers.alloc([44, 2])
    GNW = pers.alloc([128])
    NW1 = pers.alloc([D])
    NW2 = pers.alloc([D])
    SCR_LO = 86016
    SCR_HI = ARENA_BYTES

    def cf(name):
        k = _CNAMES.index(name)
        return CF[:, 128 * k:128 * (k + 1)]

    IDENT = cf("IDENT"); M1 = cf("M1"); M2 = cf("M2"); TRI = cf("TRI"); SUF = cf("SUF")
    IND0 = cf("IND0"); IND1 = cf("IND1"); S32 = cf("S32"); OFFM = cf("OFF"); ONES = cf("ONES")
    MB64b = CBm[:, 0:512]; MBA4b = CBm[:, 512:1024]; MBB4b = CBm[:, 1024:1536]
    IDENTb = CBm[:, 1536:1664]; ONESb = CBm[:, 1664:1792]

    P.dma("sp", CF, cst_d[:, 0:NCF], "c_cf")
    ctmp = A.view(16384 + 8192, [1536], F32)
    P.dma("sp", ctmp, cst_d[:, NCF:NCONST], "c_tmp")
    P.copy("dve", CBm[:, 0:1536], ctmp)
    P.copy("dve", IDENTb, IDENT)
    P.copy("dve", ONESb, ONES)
    P.dma("sp", cwA.rearrange("p a b -> p (a b)"), cwa_d, "c_cwa")
    P.dma("sp", cwF.rearrange("p a b -> p (a b)"), cwf_d, "c_cwf")
    P.dma("sp", GNW, gnw_d, "c_gnw")
    P.dma("sp", NW1, n1_d, "c_nw1")
    P.memset("pool", HALOA.rearrange("p a b -> p (a b)"), 0.0)
    P.memset("pool", HALOF.rearrange("p a b -> p (a b)"), 0.0)

    def bc_last(ap, n):
        return ap.unsqueeze(2).broadcast_to([128, ap.shape[1], n])

    def bc_mid(ap, n):
        return ap.unsqueeze(1).broadcast_to([128, n, ap.shape[1]])

    def rmsnorm_stage1(src_tile, wtile, scr):
        junk, ssv, rst, hb = scr
        P.act(junk, src_tile, AF.Square, accum_out=ssv)
        P.act(rst, ssv, AF.Ln, bias=EPS, scale=1.0 / D)
        P.act(rst, rst, AF.Exp, scale=-0.5)
        P.stt(hb, src_tile, rst, wtile, ALU.mult, ALU.mult)

    def rmsnorm_stage1a(src_tile, scr):
        junk, ssv, rst, hb = scr
        P.act(junk, src_tile, AF.Square, accum_out=ssv)
        P.act(rst, ssv, AF.Ln, bias=EPS, scale=1.0 / D)
        P.act(rst, rst, AF.Exp, scale=-0.5)

    def rmsnorm_stage1b(src_tile, wtile, scr):
        junk, ssv, rst, hb = scr
        P.stt(hb, src_tile, rst, wtile, ALU.mult, ALU.mult)

    def rmsnorm_stage2(dstT, col0, scr, bank):
        hb = scr[3]
        psT = pbb(bank)
        for kc in range(8):
            P.tr(psT[:, 128 * kc:128 * (kc + 1)], hb[:, 128 * kc:128 * (kc + 1)], IDENTb)
        P.copy("act", dstT[:, :, col0:col0 + 128], psT.rearrange("p (k t) -> p k t", k=8))

    bA = Bump(A, 0, 16384)
    xbuf = [bA.alloc([D]) for _ in range(3)]
    junkA = bA.alloc([D], BF16)
    hbuf = [NW2.bitcast(BF16)[:, 0:D], NW2.bitcast(BF16)[:, D:2 * D]]
    ssA = bA.alloc([NT])
    rsA = bA.alloc([NT])
    scrA = [(junkA, ssA[:, i:i + 1], rsA[:, i:i + 1], hbuf[i % 2]) for i in range(NT)]
    for i in range(NT + 2):
        if i < NT:
            xt = xbuf[i % 3]
            P.dma("sp", xt, x_d[128 * i:128 * (i + 1), :], "xa%d" % (i % 3))
            rmsnorm_stage1a(xt, scrA[i])
        if 1 <= i <= NT:
            rmsnorm_stage1b(xbuf[(i - 1) % 3], NW1, scrA[i - 1])
        if i >= 2:
            rmsnorm_stage2(hT, 128 * (i - 2), scrA[i - 2], 6 + (i % 2))
    dump("hT", hT.rearrange("p k t -> p (k t)"), [128, 8 * S])
    if stage == "A":
        return finish(nc, P, out_d, dump_tags)

    bG = Bump(A, SCR_LO, SCR_HI)
    bC = Bump(A, 16384, 32768)
    wz = bC.alloc([8, 512], BF16)
    qkvT = [dict(q=bG.alloc([4, 512], BF16), k=bG.alloc([4, 512], BF16), v=bG.alloc([4, 512], BF16))
            for _ in range(4)]
    szgb = [bG.alloc([4, 512], BF16), bG.alloc([4, 512], BF16), bG.alloc([4, 512], BF16), bC.alloc([4, 512], BF16)]
    BA = bG.alloc([NT, 8])
    sc_x = bG.alloc([64]); sc_mx = bG.alloc([64]); sc_mn = bG.alloc([64])
    dtb = bG.alloc([64]); negA = bG.alloc([64])
    gS = bG.alloc([64]); betaS = bG.alloc([64]); gamS = bG.alloc([64]); kesS = bG.alloc([64])
    gendS = bG.alloc([2, 64])
    wba = bG.alloc([8, 8], BF16)
    Sst = bG.alloc([4, 128]); Sb = bG.alloc([4, 128], BF16); ub = bG.alloc([4, 128], BF16)
    Obuf = [bC.alloc([4, 128]) for _ in range(2)]
    sqO = bG.alloc([4, 128], BF16); oab = sqO
    kesLo = bG.alloc([64]); kesHi = bG.alloc([64])
    ssq = bG.alloc([4]); rsq = bG.alloc([4])
    ov0 = bG.cur
    gset = []
    for _ in range(2):
        gset.append(dict(
            G1m=bG.alloc([4, 128]),
            DECT=bG.alloc([4, 128], BF16), Uall=bG.alloc([4, 128], BF16), Eu=bG.alloc([4, 128], BF16),
            PTa=bG.alloc([4, 128], BF16), PTb=bG.alloc([4, 128], BF16),
            UL0=bG.alloc([2, 4, 128], BF16), PWA=bG.alloc([2, 4, 128], BF16), PWB=bG.alloc([2, 4, 128], BF16),
            TTb=bG.alloc([4, 128], BF16), vb=bG.alloc([4, 128], BF16), gk=bG.alloc([4, 128], BF16)))
    scanop = []
    for _ in range(3):
        scanop.append(dict(KLO=bG.alloc([4, 128], BF16), KHI=bG.alloc([4, 128], BF16), ATT=bG.alloc([4, 128], BF16),
                           QD=bG.alloc([4, 128], BF16), WKT=bG.alloc([4, 128], BF16),
                           UV=bG.alloc([4, 128], BF16)))
    bO = Bump(A, ov0, bG.cur)
    wqkva = bO.alloc([3, 8, 512], BF16)
    rawb = [bO.alloc([520], BF16) for _ in range(3)]
    dgcb = [bO.alloc([4, 128], BF16) for _ in range(3)]
    sqbs = [bO.alloc([512], BF16) for _ in range(2)]
    rtbs = [bO.alloc([512]) for _ in range(2)]
    for c3 in range(3):
        P.dma("pool", wqkva[:, c3, :, :], win_v[:, :, 512 * c3:512 * (c3 + 1)], "wqkva%d" % c3)
    P.dma("pool", wba, win_v[:, :, 2048:2056], "wba")
    P.dma("pool", wz, win_v[:, :, 1536:2048], "wz")

    ringG = Ring([0, 1, 2, 3, 4, 5, 6, 7])

    g3 = gS.rearrange("p (n h) -> p n h", h=4)
    beta3 = betaS.rearrange("p (n h) -> p n h", h=4)
    gam3 = gamS.rearrange("p (n h) -> p n h", h=4)
    kesLo3 = kesLo.rearrange("p (n h) -> p n h", h=4)
    kesHi3 = kesHi.rearrange("p (n h) -> p n h", h=4)
    gend4 = gendS.rearrange("p a (n h) -> p a n h", h=4)

    def SCALARS():
        P.dma("sp", dtb, dtb_d, "c_dtb")
        P.dma("sp", negA, alog_d, "c_alog")
        P.act(negA, negA, AF.Exp)
        P.ts("dve", negA, negA, -1.0, None, ALU.mult)
        bk = ringG.next()
        psBA = pbf(bk)[:, 0:128].rearrange("p (n c) -> p n c", c=8)
        for i in range(NT):
            for kc in range(8):
                P.mm(psBA[:, i, :], hT[:, kc, 128 * i:128 * (i + 1)], wba[:, kc, :], start=(kc == 0), stop=(kc == 7))
        P.copy("act", BA, psBA)
        x3 = sc_x.rearrange("p (n h) -> p n h", h=4)
        P.tt("dve", x3, BA[:, :, 4:8], dtb.rearrange("p (n h) -> p n h", h=4), ALU.add)
        P.ts("dve", sc_mx, sc_x, 0.0, None, ALU.max)
        P.ts("dve", sc_mn, sc_x, 0.0, None, ALU.min)
        P.tt("dve", sc_mn, sc_mn, sc_mx, ALU.subtract)
        P.act(sc_mn, sc_mn, AF.Exp)
        P.act(sc_mn, sc_mn, AF.Ln, bias=1.0)
        P.tt("dve", sc_mx, sc_mx, sc_mn, ALU.add)
        P.tt("dve", gS, sc_mx, negA, ALU.mult)
        P.act(betaS.rearrange("p (n h) -> p n h", h=4), BA[:, :, 0:4], AF.Sigmoid)
        bk = ringG.next()
        psg = pbf(bk)
        P.mm(psg[:, 0:64], TRI, gS)
        P.mm(psg[:, 64:128], SUF, gS)
        P.mm(psg[:, 128:192], IND0, gS)
        P.mm(psg[:, 192:256], IND1, gS)
        P.act(gamS, psg[:, 0:64], AF.Exp)
        P.act(kesS, psg[:, 64:128], AF.Exp)
        P.tt("dve", kesS, kesS, betaS, ALU.mult)
        P.act(gendS.rearrange("p a b -> p (a b)"), psg[:, 128:256], AF.Exp)
        dump("gS", gS, [128, 64]); dump("betaS", betaS, [128, 64]); dump("gamS", gamS, [128, 64])
        dump("kesS", kesS, [128, 64]); dump("gendS", gendS.rearrange("p a b -> p (a b)"), [128, 128])
        P.ts("dve", kesLo, kesS, IND0[:, 0:1], None, ALU.mult)
        P.ts("dve", kesHi, kesS, IND1[:, 0:1], None, ALU.mult)


    def G1(m):
        t0 = 512 * m
        o = qkvT[m]
        for c in range(12):
            ps = pbf(ringG.next())
            for kc in range(8):
                P.mm(ps, wqkva[:, c // 4, kc, 128 * (c % 4):128 * (c % 4 + 1)], hT[:, kc, t0:t0 + 512], start=(kc == 0), stop=(kc == 7))
            raw = rawb[(c + m) % 3]; dgc = dgcb[(c + m) % 3]
            for i_ in range(4):
                P.ts("dve", dgc[:, i_, :], IDENTb, cwA[:, c, i_:i_ + 1], None, ALU.mult)
            P.copy("pool", raw[:, 0:3], HALOA[:, c, :])
            P.copy("act", raw[:, 3:515], ps)
            P.copy("pool", HALOA[:, c, :], raw[:, 512:515])
            psC = pbf(ringG.next())
            for i_ in range(4):
                P.mm(psC, dgc[:, i_, :], raw[:, i_:i_ + 512], start=(i_ == 0), stop=(i_ == 3))
            dst = (o["q"], o["k"], o["v"])[c // 4][:, c % 4, :]
            P.act(dst, psC, AF.Silu)
            if c % 3 == 2:
                nz = 4 * m + c // 3
                psZ = pbf(ringG.next())
                for kc in range(8):
                    P.mm(psZ, hT[:, kc, 128 * nz:128 * (nz + 1)], wz[:, kc, :], start=(kc == 0), stop=(kc == 7))
                szg = szgb[m][:, c // 3, :]
                P.act(szg, psZ, AF.Silu)
                P.tt("pool", szg.rearrange("p (h d) -> p h d", h=4), szg.rearrange("p (h d) -> p h d", h=4),
                     bc_mid(GNW, 4), ALU.mult)
            yield
        for c in range(8):
            dst = (o["q"], o["k"])[c // 4][:, c % 4, :]
            sc = 128.0 if c < 4 else 1.0
            sqb = sqbs[(c + m) % 2]; rtb = rtbs[(c + m) % 2]
            P.tt("pool", sqb, dst, dst, ALU.mult)
            psn = pbf(ringG.next())
            P.mm(psn, ONESb, sqb)
            P.act(rtb, psn, AF.Ln, bias=EPS * sc, scale=sc)
            P.act(rtb, rtb, AF.Exp, scale=-0.5)
            P.tt("dve", dst, dst, rtb, ALU.mult)
            yield
        if m == 0:
            dump("qnT0", o["q"].rearrange("p h t -> p (h t)"), [128, 2048])
            dump("knT0", o["k"].rearrange("p h t -> p (h t)"), [128, 2048])
            dump("vsT0", o["v"].rearrange("p h t -> p (h t)"), [128, 2048])

    def G2(n):
        tl = 128 * (n % 4)
        so = scanop[n % 3]
        st = gset[n % 2]
        qnT = qkvT[n // 4]["q"]; knT = qkvT[n // 4]["k"]; vsT = qkvT[n // 4]["v"]
        G1m = st["G1m"]; DECT = st["DECT"]; Uall = st["Uall"]; Eu = st["Eu"]
        UL0 = st["UL0"]; PWA = st["PWA"]; PWB = st["PWB"]
        TTb = st["TTb"]; vb = st["vb"]; gk = st["gk"]
        Du, Dl = UL0[:, 0], UL0[:, 1]
        gam_b = bc_last(gam3[:, n, :], 128)
        keslo_b = bc_last(kesLo3[:, n, :], 128)
        keshi_b = bc_last(kesHi3[:, n, :], 128)
        beta_b = bc_last(beta3[:, n, :], 128)
        g_b = bc_last(g3[:, n, :], 128)

        def bankf():
            return pbf(ringG.next()).rearrange("p (h d) -> p h d", h=4)

        def bankb():
            return pbb(ringG.next())[:, 0:512].rearrange("p (h d) -> p h d", h=4)

        psKV = pbb(ringG.next()).rearrange("p (a h d) -> p a h d", a=2, h=4)
        psK = psKV[:, 0]; psV = psKV[:, 1]
        for h in range(4):
            P.tr(psK[:, h, :], knT[:, h, tl:tl + 128], IDENTb)
            P.tr(psV[:, h, :], vsT[:, h, tl:tl + 128], IDENTb)
        P.tt("dve", gk, psK, gam_b, ALU.mult)
        P.tt("dve", so["KLO"], psK, keslo_b, ALU.mult)
        P.tt("dve", so["KHI"], psK, keshi_b, ALU.mult)
        P.copy("act", vb, psV)
        yield
        P.tt("pool", G1m, bc_mid(M1, 4), g_b, ALU.mult)
        psD = bankf()
        P.mm(psD, IDENTb, MB64b, start=True, stop=False)
        for h in range(4):
            P.mm(psD[:, h, :], G1m[:, h, :], M2, start=False, stop=(h == 3))
        P.act(DECT, psD, AF.Exp)
        P.tt("pool", DECT, DECT, beta_b, ALU.mult)
        yield
        dg = Uall
        P.tt("pool", dg, bc_mid(IDENT, 4), gam_b, ALU.mult)
        psG = bankf()
        for h in range(4):
            P.mm(psG[:, h, :], ONESb, dg[:, h, :])
        P.tt("dve", so["QD"], qnT[:, :, tl:tl + 128], psG, ALU.mult)
        yield
        psKK = bankf(); psQK = bankf()
        for h in range(4):
            P.mm(psKK[:, h, :], knT[:, h, tl:tl + 128], knT[:, h, tl:tl + 128])
        for h in range(4):
            P.mm(psQK[:, h, :], knT[:, h, tl:tl + 128], qnT[:, h, tl:tl + 128])
        P.tt("dve", Uall, psKK, DECT, ALU.mult)
        P.tt("dve", so["ATT"], psQK, DECT, ALU.mult)
        P.tt("pool", Du, Uall, bc_mid(S32, 4), ALU.mult)
        P.tt("pool", Eu, Uall, bc_mid(OFFM, 4), ALU.mult)
        yield
        psT = bankb()
        for h in range(4):
            P.tr(psT[:, h, :], Du[:, h, :], IDENTb)
        P.copy("act", Dl, psT)
        P.tt("pool", st["PTa"], bc_mid(IDENT, 4), Du, ALU.subtract)
        yield
        pw = [UL0, PWA, PWB, PWA, PWB]
        PT, PTn = st["PTa"], st["PTb"]
        for k in range(1, 5):
            cur = pw[k - 1]; nxt = pw[k]
            if k < 4:
                psU = bankf()
                for h in range(4):
                    P.mm(psU[:, h, :], cur[:, 1, h, :], cur[:, 0, h, :])
            psL = bankf()
            for h in range(4):
                P.mm(psL[:, h, :], cur[:, 0, h, :], cur[:, 1, h, :])
            if k > 1:
                ps3 = bankf()
                for h in range(4):
                    P.mm(ps3[:, h, :], cur[:, 1, h, :], PT[:, h, :])
            if k < 4:
                P.copy("act", nxt[:, 0], psU)
            P.copy("act", nxt[:, 1], psL)
            if k > 1:
                P.tt("dve", PTn, PT, ps3, ALU.add)
                PT, PTn = PTn, PT
            yield
        ps3 = bankf()
        for h in range(4):
            P.mm(ps3[:, h, :], PWB[:, 1, h, :], PT[:, h, :])
        P.tt("dve", PTn, PT, ps3, ALU.add)
        PT, PTn = PTn, PT
        yield
        Pm = PWA[:, 0]; XT = DECT
        p1 = bankb()
        for h in range(4):
            P.tr(p1[:, h, :], PT[:, h, :], IDENTb)
        P.copy("act", Pm, p1)
        yield
        p2 = bankf()
        for h in range(4):
            P.mm(p2[:, h, :], Eu[:, h, :], Pm[:, h, :])
        P.copy("act", XT, p2)
        yield
        p3 = bankf()
        for h in range(4):
            P.mm(p3[:, h, :], XT[:, h, :], PT[:, h, :])
        P.tt("dve", TTb, PT, p3, ALU.subtract)
        yield
        p1 = bankf(); p2 = bankf()
        for h in range(4):
            P.mm(p1[:, h, :], TTb[:, h, :], vb[:, h, :])
        for h in range(4):
            P.mm(p2[:, h, :], gk[:, h, :], TTb[:, h, :])
        P.copy("act", so["UV"], p1)
        P.copy("dve", so["WKT"], p2)
        yield

    def SCAN(n):
        so = scanop[n % 3]
        O = Obuf[n % 2]
        for half in range(2):
            r0 = 64 * half
            rs = slice(r0, r0 + 64)
            kend = so["KLO"] if half == 0 else so["KHI"]
            psA = pbf(ringG.next()).rearrange("p (h d) -> p h d", h=4)
            for h in range(4):
                P.mm(psA[:, h, :], so["WKT"][:, h, :], Sb[:, h, :])
            P.tt("dve", ub[rs], so["UV"][rs], psA[rs], ALU.subtract)
            yield
            psS = pbf(ringG.next()).rearrange("p (h d) -> p h d", h=4)
            psO = pbf(ringG.next()).rearrange("p (h d) -> p h d", h=4)
            for h in range(4):
                P.mm(psS[:, h, :], kend[:, h, :], ub[:, h, :])
            for h in range(4):
                P.mm(psO[:, h, :], so["QD"][:, h, :], Sb[:, h, :], start=True, stop=False)
                P.mm(psO[:, h, :], so["ATT"][:, h, :], ub[:, h, :], start=False, stop=True)
            for h in range(4):
                P.stt(Sst[:, h, :], Sst[:, h, :], gend4[:, half, n, h:h + 1], psS[:, h, :], ALU.mult, ALU.add)
            P.copy("act", Sb, Sst)
            P.copy("act", O[rs], psO[rs])
            yield
        if n == 0:
            dump("O0", O.rearrange("p h d -> p (h d)"), [128, 512])
        if n == 15:
            dump("O15", O.rearrange("p h d -> p (h d)"), [128, 512])
        P.act(sqO, O, AF.Square)
        P.rsum(ssq, sqO)
        P.act(rsq, ssq, AF.Ln, bias=EPS, scale=1.0 / 128)
        P.act(rsq, rsq, AF.Exp, scale=-0.5)
        szg = szgb[n // 4][:, n % 4, :].rearrange("p (h d) -> p h d", h=4)
        P.tt("dve", sqO, O, bc_last(rsq, 128), ALU.mult)
        P.tt("dve", oab, sqO, szg, ALU.mult)
        yield
        psT = pbb(ringG.next())[:, 0:512].rearrange("p (h d) -> p h d", h=4)
        for h in range(4):
            P.tr(psT[:, h, :], oab[:, h, :], IDENTb)
        P.copy("act", CT[:, 0:4, 128 * n:128 * (n + 1)], psT)
        yield

    def advance(must, opt):
        live = [True] * len(must)
        while any(live) or any(q[1] > 0 for q in opt):
            for gi, g in enumerate(must):
                if live[gi]:
                    try:
                        next(g)
                    except StopIteration:
                        live[gi] = False
            for q in opt:
                if q[1] > 0:
                    q[1] -= 1
                    try:
                        next(q[0])
                    except StopIteration:
                        q[1] = 0
                        q[2] = True

    P.memset("dve", Sst.rearrange("p h d -> p (h d)"), 0.0)
    P.memset("pool", Sb.rearrange("p h d -> p (h d)"), 0.0)
    P.memset("pool", ub.rearrange("p h d -> p (h d)"), 0.0)
    advance([G1(m_) for m_ in range(4)], [])
    SCALARS()
    g2 = {0: G2(0), 1: G2(1)}
    advance([g2[0]], [[g2[1], 7, False]])
    for n in range(NT):
        must = [SCAN(n)]
        if n + 1 < NT:
            must.append(g2[n + 1])
        opt = []
        if n + 2 < NT:
            g2[n + 2] = G2(n + 2)
            opt.append([g2[n + 2], 7, False])
        advance(must, opt)
    dump("CTa", CT[:, 0:4, :].rearrange("p k t -> p (k t)"), [128, 4 * S])
    if stage == "G":
        return finish(nc, P, out_d, dump_tags)

    bB = Bump(A, SCR_LO, SCR_HI)
    wqkvb = bB.alloc([3, 8, 512], BF16)
    for c3 in range(3):
        P.dma("pool", wqkvb[:, c3, :, :],
              win_v[:, :, 2056 + 512 * c3:2056 + 512 * (c3 + 1)], "wqkvb%d" % c3)
    QT0 = bB.alloc([S], BF16)
    KTz = [bB.alloc([S], BF16) for _ in range(2)]
    QTb = [QT0, QT0]
    P.memset("pool", KTz[0][64:128, :], 0.0)
    P.memset("pool", KTz[1][0:64, :], 0.0)
    VAll = bB.alloc([48, 4, 192], BF16)
    PTbuf = Ring([bB.alloc([1024], BF16) for _ in range(2)])
    PT3 = bB.alloc([16, 128], BF16)
    rden0 = bB.alloc([512])
    rdenb = [rden0, rden0]
    P.memset("pool", VAll[:, :, :, 64:128], 1.0)
    ringS = Ring([2, 3, 4, 5])
    ringP = Ring([6, 7])
    accR = Ring([0, 1])

    def tok_slices():
        sl = []
        for n in range(16):
            sl.append(slice(128 * n, 128 * (n + 1), 1))
        for r in range(4):
            for c in range(4):
                sl.append(slice(512 * c + r, 512 * (c + 1), 4))
        for r in range(16):
            sl.append(slice(r, S, 16))
        return sl

    TOK = tok_slices()

    def PROJ_V():
        for t in range(48):
            bank = ringP.next()
            ps = pbf(bank)
            sl = TOK[t]
            for kc in range(8):
                P.mm(ps, hT[:, kc, sl], wqkvb[:, 2, kc, :], start=(kc == 0), stop=(kc == 7))
            ps4 = ps.rearrange("p (j e d) -> p j e d", j=4, e=2)
            P.copy("act", VAll[:, t, :, 0:64], ps4[:, :, 0, :])
            P.copy("dve", VAll[:, t, :, 128:192], ps4[:, :, 1, :])

    def PROJ_B(j):
        k = j % 2
        QT = QTb[k]
        for (dst, cbase) in ((QT, 128 * j), (None, 512 + 128 * j)):
            for tb in range(4):
                bank = ringP.next()
                ps = pbf(bank)
                for kc in range(8):
                    P.mm(ps, wqkvb[:, cbase // 512, kc, cbase % 512:cbase % 512 + 128], hT[:, kc, 512 * tb:512 * (tb + 1)],
                         start=(kc == 0), stop=(kc == 7))
                cs = slice(512 * tb, 512 * (tb + 1))
                if dst is not None:
                    P.copy("dve" if tb % 2 else "act", dst[:, cs], ps)
                else:
                    P.copy("act", KTz[0][0:64, cs], ps[0:64, :])
                    P.copy("dve", KTz[1][64:128, cs], ps[64:128, :])

    def ATTN(j):
        k = j % 2
        QT = QTb[k]

        class _VA:
            def __init__(self, h):
                self.h = h

            def __getitem__(self, idx):
                e_ = self.h % 2
                return VAll[:, idx[1], self.h // 2, 64 * e_:64 * e_ + 128]

        for e in range(2):
            hp = slice(64 * e, 64 * e + 64)
            KT = KTz[e]
            fp = slice(0, 128)
            VA = _VA(2 * j + e)
            for g in range(4):
                bank = ringS.next()
                ps = pbf(bank)
                P.mm(ps, IDENTb, MBA4b, start=True, stop=False)
                for r4 in range(4):
                    r = 4 * g + r4
                    P.mm(ps[:, 128 * r4:128 * (r4 + 1)], KT[fp, r:S:16], QT[fp, r:S:16], start=False, stop=(r4 == 3))
                P.act(PT3[:, 4 * g:4 * g + 4, :], ps.rearrange("p (a d) -> p a d", a=4), AF.Exp, scale=0.125)
            for c in range(4):
                acc = pbf(accR.next())
                first = [True]

                def pv(out, lhsT, rhs, last=False):
                    P.mm(out, lhsT, rhs, start=first[0], stop=last, skip_group_check=True)
                    first[0] = False
                pt = PTbuf.next()
                bank = ringS.next(); ps = pbf(bank)
                P.mm(ps, IDENTb, MBA4b, start=True, stop=False)
                for i in range(4):
                    n = 4 * c + i
                    P.mm(ps[:, 128 * i:128 * (i + 1)], KT[fp, 128 * n:128 * (n + 1)], QT[fp, 128 * n:128 * (n + 1)],
                         start=False, stop=(i == 3))
                P.act(pt[:, 0:512], ps, AF.Exp, scale=0.125)
                bank = ringS.next(); ps = pbf(bank)
                P.mm(ps, IDENTb, MBB4b, start=True, stop=False)
                for i in range(4):
                    n = 4 * c + i
                    if n == 0:
                        continue
                    P.mm(ps[:, 128 * i:128 * (i + 1)], KT[fp, 128 * (n - 1):128 * n], QT[fp, 128 * n:128 * (n + 1)],
                         start=False, stop=(i == 3))
                P.act(pt[:, 512:1024], ps, AF.Exp, scale=0.125)
                pt1 = pt
                pt = PTbuf.next()
                bank = ringS.next(); ps = pbf(bank)
                P.mm(ps, IDENTb, MBA4b, start=True, stop=False)
                for r in range(4):
                    sl = slice(512 * c + r, 512 * (c + 1), 4)
                    P.mm(ps[:, 128 * r:128 * (r + 1)], KT[fp, sl], QT[fp, sl], start=False, stop=(r == 3))
                P.act(pt[:, 0:512], ps, AF.Exp, scale=0.125)
                if c > 0:
                    bank = ringS.next(); ps = pbf(bank)
                    P.mm(ps, IDENTb, MBB4b, start=True, stop=False)
                    for r in range(4):
                        sl = slice(512 * c + r, 512 * (c + 1), 4)
                        slk = slice(512 * (c - 1) + r, 512 * c, 4)
                        P.mm(ps[:, 128 * r:128 * (r + 1)], KT[fp, slk], QT[fp, sl], start=False, stop=(r == 3))
                    P.act(pt[:, 512:1024], ps, AF.Exp, scale=0.125)
                for i in range(4):
                    n = 4 * c + i
                    pv(acc[:, 128 * i:128 * (i + 1)], VA[:, n, e, :], pt1[:, 128 * i:128 * (i + 1)])
                    if n > 0:
                        pv(acc[:, 128 * i:128 * (i + 1)], VA[:, n - 1, e, :], pt1[:, 512 + 128 * i:512 + 128 * (i + 1)])
                for r in range(4):
                    pv(acc[:, r:512:4], VA[:, 16 + 4 * r + c, e, :], pt[:, 128 * r:128 * (r + 1)])
                    if c > 0:
                        pv(acc[:, r:512:4], VA[:, 16 + 4 * r + c - 1, e, :], pt[:, 512 + 128 * r:512 + 128 * (r + 1)])
                for r in range(16):
                    pv(acc[:, r:512:16], VA[:, 32 + r, e, :], PT3[:, r, 32 * c:32 * (c + 1)], last=(r == 15))
                num = slice(64 * e, 64 * e + 64)
                den = slice(64 * (1 - e), 64 * (1 - e) + 64)
                rd = rdenb[c % 2]
                P.recip(rd[den, :], acc[den, :])
                P.tt("dve", CT[num, 4 + j, 512 * c:512 * (c + 1)], acc[num, :], rd[den, :], ALU.mult)

    PROJ_V()
    for j in range(4):
        PROJ_B(j)
        ATTN(j)
    dump("CTb", CT[:, 4:8, :].rearrange("p k t -> p (k t)"), [128, 4 * S])
    if stage == "B":
        return finish(nc, P, out_d, dump_tags)

    bF = Bump(A, SCR_LO, SCR_HI)
    h2T = A.view(32768, [8, 1024], BF16)
    woutb = A.view(32768 + 16384, [2, 8, 512], BF16)
    X1 = bF.alloc([8, D])
    wu = [bF.alloc([2, 8, 512], BF16) for _ in range(2)]
    wd = [bF.alloc([4, D], BF16) for _ in range(2)]
    aTb = [bF.alloc([4, 1024], BF16) for _ in range(2)]
    rawF = [[bF.alloc([520]) for _ in range(2)] for _ in range(2)]
    accF = [[bF.alloc([512]) for _ in range(2)] for _ in range(2)]
    sgF = [bF.alloc([512]) for _ in range(2)]
    hbF = [bF.alloc([D], BF16), accF[1][1].bitcast(BF16)]
    junkF = sgF[0].bitcast(BF16)
    ssF = bF.alloc([8]); rsF = bF.alloc([8]); ssO = bF.alloc([8]); rsO = bF.alloc([8])
    for c2 in range(2):
        P.dma("pool", woutb[:, c2, :, :], wout_v[:, :, 512 * c2:512 * (c2 + 1)], "wout%d" % c2)
    P.dma("sp", NW1, n2_d, "c_nw1")
    P.dma("sp", NW2, nf_d, "c_nw2")
    groups = [list(range(g0, min(g0 + 4, 22))) for g0 in range(0, 22, 4)]
    out_tags = ["out%d" % i for i in range(8)]
    items = [(H, gi) for H in range(2) for gi in range(len(groups))]

    def load_wu(k):
        H, gi = items[k]
        grp = groups[gi]; g0 = grp[0]; npair = len(grp); slot = k % 2
        P.dma("pool", wu[slot][:, 0, :, 0:128 * npair], wup_v[:, :, 128 * g0:128 * (g0 + npair)], "wug%d" % slot)
        P.dma("pool", wu[slot][:, 1, :, 0:128 * npair],
              wup_v[:, :, 2816 + 128 * g0:2816 + 128 * (g0 + npair)], "wuu%d" % slot)

    def load_wd(k):
        H, gi = items[k]
        grp = groups[gi]; g0 = grp[0]; npair = len(grp); slot = k % 2
        P.dma("pool", wd[slot][:, 0:npair, :], wdn_v[:, g0:g0 + npair, :], "wd%d" % slot)

    def PRO(H):
        scr = [(junkF, ssF[:, i8:i8 + 1], rsF[:, i8:i8 + 1], hbF[i8 % 2]) for i8 in range(8)]

        def st_a(i8):
            i = 8 * H + i8
            P.dma("sp", X1[:, i8, :], x_d[128 * i:128 * (i + 1), :], "xf%d" % i8)
            b0 = 2 * (i8 % 2)
            for h2 in range(2):
                for kc in range(8):
                    P.mm(pbf(b0 + h2), CT[:, kc, 128 * i:128 * (i + 1)], woutb[:, h2, kc, :],
                         start=(kc == 0), stop=(kc == 7))
            for h2 in range(2):
                P.tt("dve", X1[:, i8, 512 * h2:512 * (h2 + 1)], X1[:, i8, 512 * h2:512 * (h2 + 1)], pbf(b0 + h2), ALU.add)
            if i == 0:
                dump("X1", X1[:, 0, :], [128, D])
            rmsnorm_stage1a(X1[:, i8, :], scr[i8])

        for t in range(10):
            if t < 8:
                st_a(t)
            if 1 <= t <= 8:
                rmsnorm_stage1b(X1[:, t - 1, :], NW1, scr[t - 1])
            if t >= 2:
                rmsnorm_stage2(h2T, 128 * (t - 2), scr[t - 2], 4 + (t % 2))

    upar = [0]

    def UGEN(k):
        H, gi = items[k]
        grp = groups[gi]; slot = k % 2; aT = aTb[k % 2]
        for p, g in enumerate(grp):
            for tb in range(2):
                par = upar[0]
                upar[0] ^= 1
                cols = slice(512 * tb, 512 * (tb + 1))
                banks = (4, 5) if par == 0 else (6, 7)
                accs = []
                for gu in range(2):
                    ps = pbf(banks[gu])
                    for kc in range(8):
                        P.mm(ps, wu[slot][:, gu, kc, 128 * p:128 * (p + 1)], h2T[:, kc, cols],
                             start=(kc == 0), stop=(kc == 7))
                    cc = g + 22 * gu
                    raw = rawF[par][gu]; acc = accF[par][gu]
                    P.copy("pool", raw[:, 0:2], HALOF[:, cc, :])
                    P.copy("act", raw[:, 2:514], ps)
                    P.copy("pool", HALOF[:, cc, :], raw[:, 512:514])
                    P.act(acc, ps, AF.Identity, scale=cwF[:, cc, 2:3])
                    P.stt(acc, raw[:, 1:513], cwF[:, cc, 1:2], acc, ALU.mult, ALU.add)
                    P.stt(acc, raw[:, 0:512], cwF[:, cc, 0:1], acc, ALU.mult, ALU.add)
                    accs.append(acc)
                sg = sgF[par]
                P.act(sg, accs[0], AF.Silu)
                P.tt("pool", aT[:, p, cols], sg, accs[1], ALU.mult)
                yield

    def DGEN(k):
        H, gi = items[k]
        grp = groups[gi]; slot = k % 2; aT = aTb[k % 2]; npair = len(grp)
        last = (gi == len(groups) - 1)
        for i8 in range(8):
            i = 8 * H + i8
            b0 = 2 * (i8 % 2)
            for h2 in range(2):
                for p in range(npair):
                    P.mm(pbf(b0 + h2), aT[:, p, 128 * i8:128 * (i8 + 1)], wd[slot][:, p, 512 * h2:512 * (h2 + 1)],
                         start=(p == 0), stop=(p == npair - 1))
            for h2 in range(2):
                P.tt("dve", X1[:, i8, 512 * h2:512 * (h2 + 1)], X1[:, i8, 512 * h2:512 * (h2 + 1)],
                     pbf(b0 + h2), ALU.add)
            if last:
                P.act(junkF, X1[:, i8, :], AF.Square, accum_out=ssO[:, i8:i8 + 1])
                P.act(rsO[:, i8:i8 + 1], ssO[:, i8:i8 + 1], AF.Ln, bias=EPS, scale=1.0 / D)
                P.act(rsO[:, i8:i8 + 1], rsO[:, i8:i8 + 1], AF.Exp, scale=-0.5)
                P.stt(X1[:, i8, :], X1[:, i8, :], rsO[:, i8:i8 + 1], NW2, ALU.mult, ALU.mult)
                P.dma("sp", out_d[128 * i:128 * (i + 1), :], X1[:, i8, :], "out%d" % i8)
            yield

    def rr(gens):
        gens = list(gens)
        live = [True] * len(gens)
        while any(live):
            for gi_, g_ in enumerate(gens):
                if live[gi_]:
                    try:
                        next(g_)
                    except StopIteration:
                        live[gi_] = False

    load_wu(0)
    prevD = None
    for k in range(len(items)):
        H, gi = items[k]
        load_wd(k)
        if k + 1 < len(items):
            load_wu(k + 1)
        if gi == 0:
            if prevD is not None:
                rr([prevD])
                prevD = None
            PRO(H)
        u = UGEN(k)
        rr([u] if prevD is None else [u, prevD])
        prevD = DGEN(k)
    rr([prevD])
    return finish(nc, P, out_d, dump_tags + out_tags)


def finish(nc, P, out_d, dump_tags):
    tags = list(dump_tags)
    P.emit(final_dma_tags=tags)
    return nc


def prep_inputs(inp):
    f = lambda a: np.ascontiguousarray(np.asarray(a, dtype=np.float32))
    x = f(inp["x"])
    rep = lambda v: np.ascontiguousarray(np.broadcast_to(f(v).reshape(1, -1), (128, f(v).size)))
    cwa = f(inp["conv_qkv_w"])[0]
    cwA = np.ascontiguousarray(cwa.T.reshape(12, 128, 4).transpose(1, 0, 2).reshape(128, 48))
    cwf = f(inp["ffn_conv_w"])[0]
    cwF = np.ascontiguousarray(cwf.T.reshape(44, 128, 3).transpose(1, 0, 2).reshape(128, 132))
    shared = {
        "w_in": f(inp["w_in"])[0], "w_out": f(inp["w_out"])[0], "w_up": f(inp["w_up"])[0],
        "w_down": f(inp["w_down"])[0],
        "n1rep": rep(inp["norm1_w"]), "n2rep": rep(inp["norm2_w"]), "nfrep": rep(inp["final_norm_w"]),
        "cwA": cwA, "cwF": cwF, "gnwrep": rep(inp["gdn_norm_w"]),
        "dtbrep": np.ascontiguousarray(np.tile(rep(inp["dt_bias"]), (1, 16))),
        "alogrep": np.ascontiguousarray(np.tile(rep(inp["a_log"]), (1, 16))),
        "consts": make_consts(),
    }
    maps = []
    for b in range(x.shape[0]):
        m = dict(shared)
        m["x"] = np.ascontiguousarray(x[b])
        maps.append(m)
    return maps


def kernel(**inputs):
    maps = prep_inputs(inputs)
    nc = build("full")
    res = run_bass_kernel_spmd(nc, maps, core_ids=list(range(8)))
    out = np.stack([np.asarray(r["out"], dtype=np.float32) for r in res.results], 0)
    return out
```

```python
import contextlib
import numpy as np
import concourse.bass as bass
import concourse.mybir as mybir
from concourse.bass_utils import run_bass_kernel_spmd

F32 = mybir.dt.float32
BF16 = mybir.dt.bfloat16
AF = mybir.ActivationFunctionType
ALU = mybir.AluOpType
AX = mybir.AxisListType

S = 2048
D = 1024
NT = 16
EPS = 1e-6
NEG = -30000.0
ENGS = ("pe", "act", "dve", "pool", "sp")


def _rect(ap):
    t = ap.tensor
    if str(ap.space) == "PSUM":
        return (t.name, 0, 128, 0, 2048)
    esz = mybir.dt.size(ap.dtype)
    pstride = 1
    for s in tuple(t.shape)[1:]:
        pstride *= s
    p0 = ap.start_partition()
    p1 = p0 + ap.partition_size()
    f0 = ap.offset - p0 * pstride
    ext = 0
    for (st, cnt) in tuple(ap.ap)[1:]:
        ext += abs(st) * (cnt - 1)
    return (t.name, p0, p1, f0 * esz, (f0 + ext + 1) * esz)


class _Op:
    __slots__ = ("eng", "fn", "idx", "is_dma", "tag", "waits", "signal", "sigval")


class Prog:
    def __init__(self, nc):
        self.nc = nc
        self.ops = {e: [] for e in ENGS}
        self.track = {}
        self.waited = {e: {} for e in ENGS}
        self.tagcount = {}

    def _add(self, eng, fn, reads, writes, is_dma=False, tag=None):
        op = _Op()
        op.eng = eng; op.fn = fn; op.is_dma = is_dma; op.tag = tag
        op.signal = False; op.sigval = None
        op.idx = len(self.ops[eng])
        op.waits = []
        deps = {}
        rrects = [_rect(a) for a in reads if a is not None and str(a.space) != "DRAM"]
        wrects = [_rect(a) for a in writes if a is not None and str(a.space) != "DRAM"]
        prects = [r for r in rrects if r[4] == 2048 and r[0].startswith("pb") and r not in wrects]
        for (nm, p0, p1, f0, f1) in rrects:
            for rec in self.track.get(nm, ()):
                if rec[5] == 1 and rec[0] < p1 and p0 < rec[1] and rec[2] < f1 and f0 < rec[3]:
                    deps[id(rec[4])] = (rec[4], True)
        for (nm, p0, p1, f0, f1) in wrects + prects:
            for rec in self.track.get(nm, ()):
                if rec[0] < p1 and p0 < rec[1] and rec[2] < f1 and f0 < rec[3]:
                    k = id(rec[4])
                    if k not in deps:
                        deps[k] = (rec[4], False)
        need = {}
        for (d, raw) in deps.values():
            if d.is_dma:
                key = ("dma", d.tag)
                val = self.tagcount[d.tag]
                if need.get(key, 0) < val:
                    need[key] = val
            else:
                if d.eng == eng and eng == "pe":
                    continue
                key = ("eng", d.eng)
                cur = need.get(key)
                if cur is None or cur.idx < d.idx:
                    need[key] = d
        w = self.waited[eng]
        for key, v in need.items():
            if key[0] == "dma":
                if w.get(key, 0) >= v:
                    continue
                w[key] = v
                op.waits.append((key, v))
            else:
                if w.get(key, -1) >= v.idx:
                    continue
                w[key] = v.idx
                v.signal = True
                op.waits.append((key, v))
        if is_dma:
            self.tagcount[tag] = self.tagcount.get(tag, 0) + 16
        for (nm, p0, p1, f0, f1) in wrects:
            lst = self.track.setdefault(nm, [])
            lst[:] = [r for r in lst if not (p0 <= r[0] and r[1] <= p1 and f0 <= r[2] and r[3] <= f1)]
            lst.append([p0, p1, f0, f1, op, 1])
        for (nm, p0, p1, f0, f1) in prects:
            self.track[nm] = [[p0, p1, f0, f1, op, 2]]
        for (nm, p0, p1, f0, f1) in rrects:
            if nm.startswith("pb"):
                continue
            lst = self.track.setdefault(nm, [])
            done = False
            for r in lst:
                if r[5] == 0 and r[4].eng == eng and (not r[4].is_dma) and (not is_dma) \
                        and r[0] == p0 and r[1] == p1 and r[2] == f0 and r[3] == f1:
                    r[4] = op
                    done = True
                    break
            if not done:
                lst.append([p0, p1, f0, f1, op, 0])
        self.ops[eng].append(op)
        return op

    def dma(self, q, out, in_, tag):
        return self._add(q, lambda e: e.dma_start(out=out, in_=in_), [in_], [out], is_dma=True, tag=tag)

    def mm(self, out, lhsT, rhs, start=True, stop=True, **kw):
        rd = [lhsT, rhs] + ([] if start else [out])
        return self._add("pe", lambda e: e.matmul(out, lhsT, rhs, start=start, stop=stop, **kw), rd, [out])

    def tr(self, out, in_, ident):
        return self._add("pe", lambda e: e.transpose(out, in_, ident), [in_, ident], [out])

    def act(self, out, in_, func, bias=None, scale=None, accum_out=None):
        kw = {}
        rd = [in_]
        if bias is not None:
            kw["bias"] = bias
            if not isinstance(bias, (int, float)):
                rd.append(bias)
        if scale is not None:
            kw["scale"] = scale
            if not isinstance(scale, (int, float)):
                rd.append(scale)
        wr = [out]
        if accum_out is not None:
            kw["accum_out"] = accum_out
            wr.append(accum_out)
        return self._add("act", lambda e: e.activation(out, in_, func, **kw), rd, wr)

    def tt(self, eng, out, in0, in1, op):
        return self._add(eng, lambda e: e.tensor_tensor(out, in0, in1, op), [in0, in1], [out])

    def ts(self, eng, out, in0, s1, s2, op0, op1=None):
        rd = [in0] + [s for s in (s1, s2) if s is not None and not isinstance(s, (int, float))]
        kw = {}
        if op1 is not None:
            kw["op1"] = op1
        return self._add(eng, lambda e: e.tensor_scalar(out, in0, s1, s2, op0, **kw), rd, [out])

    def stt(self, out, in0, scalar, in1, op0, op1):
        rd = [in0, in1] + ([] if isinstance(scalar, (int, float)) else [scalar])
        return self._add("dve", lambda e: e.scalar_tensor_tensor(out, in0, scalar, in1, op0, op1), rd, [out])

    def copy(self, eng, out, in_):
        if eng == "act":
            return self._add("act", lambda e: e.copy(out, in_), [in_], [out])
        return self._add(eng, lambda e: e.tensor_copy(out, in_), [in_], [out])

    def memset(self, eng, ap, val):
        return self._add(eng, lambda e: e.memset(ap, val), [], [ap])

    def recip(self, out, in_):
        return self._add("dve", lambda e: e.reciprocal(out, in_), [in_], [out])

    def rsum(self, out, in_):
        return self._add("dve", lambda e: e.tensor_reduce(out, in_, AX.X, ALU.add), [in_], [out])

    def emit(self, final_dma_tags=()):
        nc = self.nc
        for e in ENGS:
            c = 0
            for op in self.ops[e]:
                if op.signal and not op.is_dma:
                    c += 1
                    op.sigval = c
        with contextlib.ExitStack() as st:
            esem = {e: st.enter_context(nc.semaphore("s_" + e)) for e in ENGS}
            dsem = {t: st.enter_context(nc.semaphore("d_%d" % i)) for i, t in enumerate(self.tagcount)}
            block = st.enter_context(nc.Block())
            engobj = {"pe": block.tensor, "act": block.scalar, "dve": block.vector,
                      "pool": block.gpsimd, "sp": block.sync}

            def make(ename):
                def body(eng):
                    for op in self.ops[ename]:
                        for (key, v) in op.waits:
                            if key[0] == "dma":
                                eng.wait_ge(dsem[key[1]], v)
                            else:
                                eng.wait_ge(esem[key[1]], v.sigval)
                        ins = op.fn(eng)
                        if op.is_dma:
                            ins.then_inc(dsem[op.tag], 16)
                        elif op.signal:
                            ins.then_inc(esem[ename], 1)
                    if ename == "sp":
                        for t in final_dma_tags:
                            eng.wait_ge(dsem[t], self.tagcount[t])
                return body

            for e in ENGS:
                engobj[e](make(e))
        return nc


def _prod(shape):
    n = 1
    for s in shape:
        n *= s
    return n


class Arena:
    def __init__(self, nc, nbytes):
        self.t = nc.alloc_sbuf_tensor("arena", [128, nbytes // 4], F32)
        self.nbytes = nbytes

    def view(self, off, shape, dt):
        n = _prod(shape)
        esz = 4 if dt == F32 else 2
        assert off % 4 == 0 and (n * esz) % 4 == 0 and off + n * esz <= self.nbytes, (off, shape)
        ap = self.t[:, off // 4:(off + n * esz) // 4]
        if dt != F32:
            ap = ap.bitcast(dt)
        if len(shape) > 1:
            names = " ".join("d%d" % i for i in range(len(shape)))
            kw = {"d%d" % i: shape[i] for i in range(1, len(shape))}
            ap = ap.rearrange("p (%s) -> p %s" % (names, names), **kw)
        return ap


class Bump:
    def __init__(self, arena, lo, hi):
        self.a = arena; self.lo = lo; self.hi = hi; self.cur = lo

    def alloc(self, shape, dt=F32):
        esz = 4 if dt == F32 else 2
        n = (_prod(shape) * esz + 31) // 32 * 32
        off = self.cur
        self.cur += n
        assert self.cur <= self.hi, ("arena overflow", self.cur, self.hi)
        return self.a.view(off, shape, dt)


class Ring:
    def __init__(self, items):
        self.items = list(items); self.i = 0

    def next(self):
        x = self.items[self.i % len(self.items)]
        self.i += 1
        return x


_CNAMES = ["IDENT", "M1", "M2", "TRI", "SUF", "IND0", "IND1", "S32", "OFF", "ONES"]
_CB = {"MB64": 512, "MBA4": 512, "MBB4": 512}
NCF = 128 * len(_CNAMES)
NCONST = NCF + 512 * 3


def make_consts():
    i = np.arange(128)[:, None]
    j = np.arange(128)[None, :]
    same = (i // 64) == (j // 64)
    c = {}
    c["IDENT"] = (i == j)
    c["M1"] = (i > j)
    c["M2"] = (i <= j)
    c["TRI"] = (i <= j) & same
    c["SUF"] = (i > j) & same
    c["IND0"] = (i < 64) & (j >= 0)
    c["IND1"] = (i >= 64) & (j >= 0)
    c["S32"] = (i < j) & ((i // 32) == (j // 32))
    c["OFF"] = same & ((i % 64) < 32) & ((j % 64) >= 32)
    c["ONES"] = np.ones((128, 128), bool)
    cols = [c[n].astype(np.float32) for n in _CNAMES]
    mb64 = np.where((i <= j) & same, 0.0, NEG).astype(np.float32)
    mba = np.where(i <= j, 0.0, NEG).astype(np.float32)
    mbb = np.where(i >= j, 0.0, NEG).astype(np.float32)
    cols += [np.tile(mb64, (1, 4)), np.tile(mba, (1, 4)), np.tile(mbb, (1, 4))]
    return np.ascontiguousarray(np.concatenate(cols, 1))


def build(stage="full", dumps=()):
    nc = bass.Bass("TRN2", target_bir_lowering=False)
    P = Prog(nc)
    dumps = set(dumps)
    dump_tags = []

    def din(name, shape):
        return nc.dram_tensor(name, list(shape), F32, kind="ExternalInput").ap()

    x_d = din("x", [S, D])
    win_d = din("w_in", [D, 3592])
    wout_d = din("w_out", [D, D])
    wup_d = din("w_up", [D, 5632])
    wdn_d = din("w_down", [2816, D])
    n1_d = din("n1rep", [128, D])
    n2_d = din("n2rep", [128, D])
    nf_d = din("nfrep", [128, D])
    cwa_d = din("cwA", [128, 48])
    cwf_d = din("cwF", [128, 132])
    gnw_d = din("gnwrep", [128, 128])
    dtb_d = din("dtbrep", [128, 64])
    alog_d = din("alogrep", [128, 64])
    cst_d = din("consts", [128, NCONST])
    out_d = nc.dram_tensor("out", [S, D], F32, kind="ExternalOutput").ap()

    def dump(name, sb_ap, shape):
        if name not in dumps:
            return
        d = nc.dram_tensor("dbg_" + name, list(shape), sb_ap.dtype, kind="ExternalOutput").ap()
        tg = "dbg_" + name
        P.dma("sp", d, sb_ap, tg)
        dump_tags.append(tg)

    win_v = win_d.rearrange("(k p) c -> p k c", p=128)
    wout_v = wout_d.rearrange("(k p) c -> p k c", p=128)
    wup_v = wup_d.rearrange("(k p) c -> p k c", p=128)
    wdn_v = wdn_d.rearrange("(k p) c -> p k c", p=128)

    ARENA_BYTES = 207872
    A = Arena(nc, ARENA_BYTES)
    pb = [nc.alloc_psum_tensor("pb%d" % i, [128, 512], F32) for i in range(8)]

    def pbf(i):
        return pb[i][:]

    def pbb(i):
        return pb[i][:].bitcast(BF16)

    CT = A.view(0, [8, S], BF16)
    hT = A.view(32768, [8, S], BF16)
    pers = Bump(A, 65536, 86016)
    CF = pers.alloc([NCF])
    CBm = pers.alloc([1536 + 256], BF16)
    cwA = pers.alloc([12, 4])
    cwF = pers.alloc([44, 3])
    HALOA = pers.alloc([12, 3])
    HALOF = pers.alloc([44, 2])
    GNW = pers.alloc([128])
    NW1 = pers.alloc([D])
    NW2 = pers.alloc([D])
    SCR_LO = 86016
    SCR_HI = ARENA_BYTES

    def cf(name):
        k = _CNAMES.index(name)
        return CF[:, 128 * k:128 * (k + 1)]

    IDENT = cf("IDENT"); M1 = cf("M1"); M2 = cf("M2"); TRI = cf("TRI"); SUF = cf("SUF")
    IND0 = cf("IND0"); IND1 = cf("IND1"); S32 = cf("S32"); OFFM = cf("OFF"); ONES = cf("ONES")
    MB64b = CBm[:, 0:512]; MBA4b = CBm[:, 512:1024]; MBB4b = CBm[:, 1024:1536]
    IDENTb = CBm[:, 1536:1664]; ONESb = CBm[:, 1664:1792]

    P.dma("sp", CF, cst_d[:, 0:NCF], "c_cf")
    ctmp = A.view(16384 + 8192, [1536], F32)
    P.dma("sp", ctmp, cst_d[:, NCF:NCONST], "c_tmp")
    P.copy("dve", CBm[:, 0:1536], ctmp)
    P.copy("dve", IDENTb, IDENT)
    P.copy("dve", ONESb, ONES)
    P.dma("sp", cwA.rearrange("p a b -> p (a b)"), cwa_d, "c_cwa")
    P.dma("sp", cwF.rearrange("p a b -> p (a b)"), cwf_d, "c_cwf")
    P.dma("sp", GNW, gnw_d, "c_gnw")
    P.dma("sp", NW1, n1_d, "c_nw1")
    P.memset("pool", HALOA.rearrange("p a b -> p (a b)"), 0.0)
    P.memset("pool", HALOF.rearrange("p a b -> p (a b)"), 0.0)

    def bc_last(ap, n):
        return ap.unsqueeze(2).broadcast_to([128, ap.shape[1], n])

    def bc_mid(ap, n):
        return ap.unsqueeze(1).broadcast_to([128, n, ap.shape[1]])

    def rmsnorm_stage1(src_tile, wtile, scr):
        junk, ssv, rst, hb = scr
        P.act(junk, src_tile, AF.Square, accum_out=ssv)
        P.act(rst, ssv, AF.Ln, bias=EPS, scale=1.0 / D)
        P.act(rst, rst, AF.Exp, scale=-0.5)
        P.stt(hb, src_tile, rst, wtile, ALU.mult, ALU.mult)

    def rmsnorm_stage1a(src_tile, scr):
        junk, ssv, rst, hb = scr
        P.act(junk, src_tile, AF.Square, accum_out=ssv)
        P.act(rst, ssv, AF.Ln, bias=EPS, scale=1.0 / D)
        P.act(rst, rst, AF.Exp, scale=-0.5)

    def rmsnorm_stage1b(src_tile, wtile, scr):
        junk, ssv, rst, hb = scr
        P.stt(hb, src_tile, rst, wtile, ALU.mult, ALU.mult)

    def rmsnorm_stage2(dstT, col0, scr, bank):
        hb = scr[3]
        psT = pbb(bank)
        for kc in range(8):
            P.tr(psT[:, 128 * kc:128 * (kc + 1)], hb[:, 128 * kc:128 * (kc + 1)], IDENTb)
        P.copy("act", dstT[:, :, col0:col0 + 128], psT.rearrange("p (k t) -> p k t", k=8))

    bA = Bump(A, 0, 16384)
    xbuf = [bA.alloc([D]) for _ in range(3)]
    junkA = bA.alloc([D], BF16)
    hbuf = [NW2.bitcast(BF16)[:, 0:D], NW2.bitcast(BF16)[:, D:2 * D]]
    ssA = bA.alloc([NT])
    rsA = bA.alloc([NT])
    scrA = [(junkA, ssA[:, i:i + 1], rsA[:, i:i + 1], hbuf[i % 2]) for i in range(NT)]
    for i in range(NT + 2):
        if i < NT:
            xt = xbuf[i % 3]
            P.dma("sp", xt, x_d[128 * i:128 * (i + 1), :], "xa%d" % (i % 3))
            rmsnorm_stage1a(xt, scrA[i])
        if 1 <= i <= NT:
            rmsnorm_stage1b(xbuf[(i - 1) % 3], NW1, scrA[i - 1])
        if i >= 2:
            rmsnorm_stage2(hT, 128 * (i - 2), scrA[i - 2], 6 + (i % 2))
    dump("hT", hT.rearrange("p k t -> p (k t)"), [128, 8 * S])
    if stage == "A":
        return finish(nc, P, out_d, dump_tags)

    bG = Bump(A, SCR_LO, SCR_HI)
    bC = Bump(A, 16384, 32768)
    wz = bC.alloc([8, 512], BF16)
    qkvT = [dict(q=bG.alloc([4, 512], BF16), k=bG.alloc([4, 512], BF16), v=bG.alloc([4, 512], BF16))
            for _ in range(4)]
    szgb = [bG.alloc([4, 512], BF16), bG.alloc([4, 512], BF16), bG.alloc([4, 512], BF16), bC.alloc([4, 512], BF16)]
    BA = bG.alloc([NT, 8])
    sc_x = bG.alloc([64]); sc_mx = bG.alloc([64]); sc_mn = bG.alloc([64])
    dtb = bG.alloc([64]); negA = bG.alloc([64])
    gS = bG.alloc([64]); betaS = bG.alloc([64]); gamS = bG.alloc([64]); kesS = bG.alloc([64])
    gendS = bG.alloc([2, 64])
    wba = bG.alloc([8, 8], BF16)
    Sst = bG.alloc([4, 128]); Sb = bG.alloc([4, 128], BF16); ub = bG.alloc([4, 128], BF16)
    Obuf = [bC.alloc([4, 128]) for _ in range(2)]
    sqO = bG.alloc([4, 128], BF16); oab = sqO
    kesLo = bG.alloc([64]); kesHi = bG.alloc([64])
    ssq = bG.alloc([4]); rsq = bG.alloc([4])
    ov0 = bG.cur
    gset = []
    for _ in range(2):
        gset.append(dict(
            G1m=bG.alloc([4, 128]),
            DECT=bG.alloc([4, 128], BF16), Uall=bG.alloc([4, 128], BF16), Eu=bG.alloc([4, 128], BF16),
            PTa=bG.alloc([4, 128], BF16), PTb=bG.alloc([4, 128], BF16),
            UL0=bG.alloc([2, 4, 128], BF16), PWA=bG.alloc([2, 4, 128], BF16), PWB=bG.alloc([2, 4, 128], BF16),
            TTb=bG.alloc([4, 128], BF16), vb=bG.alloc([4, 128], BF16), gk=bG.alloc([4, 128], BF16)))
    scanop = []
    for _ in range(3):
        scanop.append(dict(KLO=bG.alloc([4, 128], BF16), KHI=bG.alloc([4, 128], BF16), ATT=bG.alloc([4, 128], BF16),
                           QD=bG.alloc([4, 128], BF16), WKT=bG.alloc([4, 128], BF16),
                           UV=bG.alloc([4, 128], BF16)))
    bO = Bump(A, ov0, bG.cur)
    wqkva = bO.alloc([3, 8, 512], BF16)
    rawb = [bO.alloc([520], BF16) for _ in range(8)]
    dgcb = [bO.alloc([4, 128], BF16) for _ in range(8)]
    sqbs = [bO.alloc([512], BF16) for _ in range(3)]
    rtbs = [bO.alloc([512]) for _ in range(3)]
    for c3 in range(3):
        P.dma("pool", wqkva[:, c3, :, :], win_v[:, :, 512 * c3:512 * (c3 + 1)], "wqkva%d" % c3)
    P.dma("pool", wba, win_v[:, :, 2048:2056], "wba")
    P.dma("pool", wz, win_v[:, :, 1536:2048], "wz")

    ringG = Ring([0, 1, 2, 3, 4, 5, 6, 7])

    g3 = gS.rearrange("p (n h) -> p n h", h=4)
    beta3 = betaS.rearrange("p (n h) -> p n h", h=4)
    gam3 = gamS.rearrange("p (n h) -> p n h", h=4)
    kesLo3 = kesLo.rearrange("p (n h) -> p n h", h=4)
    kesHi3 = kesHi.rearrange("p (n h) -> p n h", h=4)
    gend4 = gendS.rearrange("p a (n h) -> p a n h", h=4)

    def SCALARS():
        P.dma("sp", dtb, dtb_d, "c_dtb")
        P.dma("sp", negA, alog_d, "c_alog")
        P.act(negA, negA, AF.Exp)
        P.ts("dve", negA, negA, -1.0, None, ALU.mult)
        bk = ringG.next()
        psBA = pbf(bk)[:, 0:128].rearrange("p (n c) -> p n c", c=8)
        for i in range(NT):
            for kc in range(8):
                P.mm(psBA[:, i, :], hT[:, kc, 128 * i:128 * (i + 1)], wba[:, kc, :], start=(kc == 0), stop=(kc == 7))
        P.copy("act", BA, psBA)
        x3 = sc_x.rearrange("p (n h) -> p n h", h=4)
        P.tt("dve", x3, BA[:, :, 4:8], dtb.rearrange("p (n h) -> p n h", h=4), ALU.add)
        P.ts("dve", sc_mx, sc_x, 0.0, None, ALU.max)
        P.ts("dve", sc_mn, sc_x, 0.0, None, ALU.min)
        P.tt("dve", sc_mn, sc_mn, sc_mx, ALU.subtract)
        P.act(sc_mn, sc_mn, AF.Exp)
        P.act(sc_mn, sc_mn, AF.Ln, bias=1.0)
        P.tt("dve", sc_mx, sc_mx, sc_mn, ALU.add)
        P.tt("dve", gS, sc_mx, negA, ALU.mult)
        P.act(betaS.rearrange("p (n h) -> p n h", h=4), BA[:, :, 0:4], AF.Sigmoid)
        bk = ringG.next()
        psg = pbf(bk)
        P.mm(psg[:, 0:64], TRI, gS)
        P.mm(psg[:, 64:128], SUF, gS)
        P.mm(psg[:, 128:192], IND0, gS)
        P.mm(psg[:, 192:256], IND1, gS)
        P.act(gamS, psg[:, 0:64], AF.Exp)
        P.act(kesS, psg[:, 64:128], AF.Exp)
        P.tt("dve", kesS, kesS, betaS, ALU.mult)
        P.act(gendS.rearrange("p a b -> p (a b)"), psg[:, 128:256], AF.Exp)
        dump("gS", gS, [128, 64]); dump("betaS", betaS, [128, 64]); dump("gamS", gamS, [128, 64])
        dump("kesS", kesS, [128, 64]); dump("gendS", gendS.rearrange("p a b -> p (a b)"), [128, 128])
        P.ts("dve", kesLo, kesS, IND0[:, 0:1], None, ALU.mult)
        P.ts("dve", kesHi, kesS, IND1[:, 0:1], None, ALU.mult)


    def G1(m):
        t0 = 512 * m
        o = qkvT[m]
        pend = [None]
        for c in range(12):
            ps = pbf(ringG.next())
            for kc in range(8):
                P.mm(ps, wqkva[:, c // 4, kc, 128 * (c % 4):128 * (c % 4 + 1)], hT[:, kc, t0:t0 + 512], start=(kc == 0), stop=(kc == 7))
            raw = rawb[2 * m + c % 2]; dgc = dgcb[2 * m + c % 2]
            for i_ in range(4):
                P.ts("dve", dgc[:, i_, :], IDENTb, cwA[:, c, i_:i_ + 1], None, ALU.mult)
            P.copy("pool", raw[:, 0:3], HALOA[:, c, :])
            P.copy("act", raw[:, 3:515], ps)
            P.copy("pool", HALOA[:, c, :], raw[:, 512:515])
            if pend[0] is not None:
                pend[0]()

            def _conv(raw=raw, dgc=dgc, c=c):
                psC = pbf(ringG.next())
                for i_ in range(4):
                    P.mm(psC, dgc[:, i_, :], raw[:, i_:i_ + 512], start=(i_ == 0), stop=(i_ == 3))
                dst = (o["q"], o["k"], o["v"])[c // 4][:, c % 4, :]
                P.act(dst, psC, AF.Silu)
            pend[0] = _conv
            if c % 3 == 2:
                nz = 4 * m + c // 3
                psZ = pbf(ringG.next())
                for kc in range(8):
                    P.mm(psZ, hT[:, kc, 128 * nz:128 * (nz + 1)], wz[:, kc, :], start=(kc == 0), stop=(kc == 7))
                szg = szgb[m][:, c // 3, :]
                P.act(szg, psZ, AF.Silu)
                P.tt("pool", szg.rearrange("p (h d) -> p h d", h=4), szg.rearrange("p (h d) -> p h d", h=4),
                     bc_mid(GNW, 4), ALU.mult)
            yield
        pend[0]()
        yield
        for c in range(8):
            dst = (o["q"], o["k"])[c // 4][:, c % 4, :]
            sc = 128.0 if c < 4 else 1.0
            sqb = sqbs[(c + m) % 3]; rtb = rtbs[(c + m) % 3]
            P.tt("pool", sqb, dst, dst, ALU.mult)
            psn = pbf(ringG.next())
            P.mm(psn, ONESb, sqb)
            P.act(rtb, psn, AF.Ln, bias=EPS * sc, scale=sc)
            P.act(rtb, rtb, AF.Exp, scale=-0.5)
            P.tt("dve", dst, dst, rtb, ALU.mult)
            yield
        if m == 0:
            dump("qnT0", o["q"].rearrange("p h t -> p (h t)"), [128, 2048])
            dump("knT0", o["k"].rearrange("p h t -> p (h t)"), [128, 2048])
            dump("vsT0", o["v"].rearrange("p h t -> p (h t)"), [128, 2048])

    def G2(n):
        tl = 128 * (n % 4)
        so = scanop[n % 3]
        st = gset[n % 2]
        qnT = qkvT[n // 4]["q"]; knT = qkvT[n // 4]["k"]; vsT = qkvT[n // 4]["v"]
        G1m = st["G1m"]; DECT = st["DECT"]; Uall = st["Uall"]; Eu = st["Eu"]
        UL0 = st["UL0"]; PWA = st["PWA"]; PWB = st["PWB"]
        TTb = st["TTb"]; vb = st["vb"]; gk = st["gk"]
        Du, Dl = UL0[:, 0], UL0[:, 1]
        gam_b = bc_last(gam3[:, n, :], 128)
        keslo_b = bc_last(kesLo3[:, n, :], 128)
        keshi_b = bc_last(kesHi3[:, n, :], 128)
        beta_b = bc_last(beta3[:, n, :], 128)
        g_b = bc_last(g3[:, n, :], 128)

        def bankf():
            return pbf(ringG.next()).rearrange("p (h d) -> p h d", h=4)

        def bankb():
            return pbb(ringG.next())[:, 0:512].rearrange("p (h d) -> p h d", h=4)

        psKV = pbb(ringG.next()).rearrange("p (a h d) -> p a h d", a=2, h=4)
        psK = psKV[:, 0]; psV = psKV[:, 1]
        for h in range(4):
            P.tr(psK[:, h, :], knT[:, h, tl:tl + 128], IDENTb)
            P.tr(psV[:, h, :], vsT[:, h, tl:tl + 128], IDENTb)
        P.tt("dve", gk, psK, gam_b, ALU.mult)
        P.tt("dve", so["KLO"], psK, keslo_b, ALU.mult)
        P.tt("dve", so["KHI"], psK, keshi_b, ALU.mult)
        P.copy("act", vb, psV)
        yield
        P.tt("pool", G1m, bc_mid(M1, 4), g_b, ALU.mult)
        psD = bankf()
        P.mm(psD, IDENTb, MB64b, start=True, stop=False)
        for h in range(4):
            P.mm(psD[:, h, :], G1m[:, h, :], M2, start=False, stop=(h == 3))
        P.act(DECT, psD, AF.Exp)
        P.tt("pool", DECT, DECT, beta_b, ALU.mult)
        yield
        dg = Uall
        P.tt("pool", dg, bc_mid(IDENT, 4), gam_b, ALU.mult)
        psG = bankf()
        for h in range(4):
            P.mm(psG[:, h, :], ONESb, dg[:, h, :])
        P.tt("dve", so["QD"], qnT[:, :, tl:tl + 128], psG, ALU.mult)
        yield
        psKK = bankf(); psQK = bankf()
        for h in range(4):
            P.mm(psKK[:, h, :], knT[:, h, tl:tl + 128], knT[:, h, tl:tl + 128])
        for h in range(4):
            P.mm(psQK[:, h, :], knT[:, h, tl:tl + 128], qnT[:, h, tl:tl + 128])
        P.tt("dve", Uall, psKK, DECT, ALU.mult)
        P.tt("dve", so["ATT"], psQK, DECT, ALU.mult)
        P.tt("pool", Du, Uall, bc_mid(S32, 4), ALU.mult)
        P.tt("pool", Eu, Uall, bc_mid(OFFM, 4), ALU.mult)
        yield
        psT = bankb()
        for h in range(4):
            P.tr(psT[:, h, :], Du[:, h, :], IDENTb)
        P.copy("act", Dl, psT)
        P.tt("pool", st["PTa"], bc_mid(IDENT, 4), Du, ALU.subtract)
        yield
        pw = [UL0, PWA, PWB, PWA, PWB]
        PT, PTn = st["PTa"], st["PTb"]
        for k in range(1, 5):
            cur = pw[k - 1]; nxt = pw[k]
            if k < 4:
                psU = bankf()
                for h in range(4):
                    P.mm(psU[:, h, :], cur[:, 1, h, :], cur[:, 0, h, :])
            psL = bankf()
            for h in range(4):
                P.mm(psL[:, h, :], cur[:, 0, h, :], cur[:, 1, h, :])
            if k > 1:
                ps3 = bankf()
                for h in range(4):
                    P.mm(ps3[:, h, :], cur[:, 1, h, :], PT[:, h, :])
            if k < 4:
                P.copy("act", nxt[:, 0], psU)
            P.copy("act", nxt[:, 1], psL)
            if k > 1:
                P.tt("dve", PTn, PT, ps3, ALU.add)
                PT, PTn = PTn, PT
            yield
        ps3 = bankf()
        for h in range(4):
            P.mm(ps3[:, h, :], PWB[:, 1, h, :], PT[:, h, :])
        P.tt("dve", PTn, PT, ps3, ALU.add)
        PT, PTn = PTn, PT
        yield
        Pm = PWA[:, 0]; XT = DECT
        p1 = bankb()
        for h in range(4):
            P.tr(p1[:, h, :], PT[:, h, :], IDENTb)
        P.copy("act", Pm, p1)
        yield
        p2 = bankf()
        for h in range(4):
            P.mm(p2[:, h, :], Eu[:, h, :], Pm[:, h, :])
        P.copy("act", XT, p2)
        yield
        p3 = bankf()
        for h in range(4):
            P.mm(p3[:, h, :], XT[:, h, :], PT[:, h, :])
        P.tt("dve", TTb, PT, p3, ALU.subtract)
        yield
        p1 = bankf(); p2 = bankf()
        for h in range(4):
            P.mm(p1[:, h, :], TTb[:, h, :], vb[:, h, :])
        for h in range(4):
            P.mm(p2[:, h, :], gk[:, h, :], TTb[:, h, :])
        P.copy("act", so["UV"], p1)
        P.copy("dve", so["WKT"], p2)
        yield

    def SCAN(n):
        so = scanop[n % 3]
        O = Obuf[n % 2]
        for half in range(2):
            r0 = 64 * half
            rs = slice(r0, r0 + 64)
            kend = so["KLO"] if half == 0 else so["KHI"]
            psA = pbf(ringG.next()).rearrange("p (h d) -> p h d", h=4)
            for h in range(4):
                P.mm(psA[:, h, :], so["WKT"][:, h, :], Sb[:, h, :])
            P.tt("dve", ub[rs], so["UV"][rs], psA[rs], ALU.subtract)
            yield
            psS = pbf(ringG.next()).rearrange("p (h d) -> p h d", h=4)
            psO = pbf(ringG.next()).rearrange("p (h d) -> p h d", h=4)
            for h in range(4):
                P.mm(psS[:, h, :], kend[:, h, :], ub[:, h, :])
            for h in range(4):
                P.mm(psO[:, h, :], so["QD"][:, h, :], Sb[:, h, :], start=True, stop=False)
                P.mm(psO[:, h, :], so["ATT"][:, h, :], ub[:, h, :], start=False, stop=True)
            for h in range(4):
                P.stt(Sst[:, h, :], Sst[:, h, :], gend4[:, half, n, h:h + 1], psS[:, h, :], ALU.mult, ALU.add)
            P.copy("act", Sb, Sst)
            P.copy("act", O[rs], psO[rs])
            yield
        if n == 0:
            dump("O0", O.rearrange("p h d -> p (h d)"), [128, 512])
        if n == 15:
            dump("O15", O.rearrange("p h d -> p (h d)"), [128, 512])
        P.act(sqO, O, AF.Square)
        P.rsum(ssq, sqO)
        P.act(rsq, ssq, AF.Ln, bias=EPS, scale=1.0 / 128)
        P.act(rsq, rsq, AF.Exp, scale=-0.5)
        szg = szgb[n // 4][:, n % 4, :].rearrange("p (h d) -> p h d", h=4)
        P.tt("dve", sqO, O, bc_last(rsq, 128), ALU.mult)
        P.tt("dve", oab, sqO, szg, ALU.mult)
        yield
        psT = pbb(ringG.next())[:, 0:512].rearrange("p (h d) -> p h d", h=4)
        for h in range(4):
            P.tr(psT[:, h, :], oab[:, h, :], IDENTb)
        P.copy("act", CT[:, 0:4, 128 * n:128 * (n + 1)], psT)
        yield

    def advance(must, opt):
        live = [True] * len(must)
        while any(live) or any(q[1] > 0 for q in opt):
            for gi, g in enumerate(must):
                if live[gi]:
                    try:
                        next(g)
                    except StopIteration:
                        live[gi] = False
            for q in opt:
                if q[1] > 0:
                    q[1] -= 1
                    try:
                        next(q[0])
                    except StopIteration:
                        q[1] = 0
                        q[2] = True

    P.memset("dve", Sst.rearrange("p h d -> p (h d)"), 0.0)
    P.memset("pool", Sb.rearrange("p h d -> p (h d)"), 0.0)
    P.memset("pool", ub.rearrange("p h d -> p (h d)"), 0.0)
    advance([G1(m_) for m_ in range(4)], [])
    SCALARS()
    g2 = {0: G2(0), 1: G2(1)}
    advance([g2[0]], [[g2[1], 7, False]])
    for n in range(NT):
        must = [SCAN(n)]
        if n + 1 < NT:
            must.append(g2[n + 1])
        opt = []
        if n + 2 < NT:
            g2[n + 2] = G2(n + 2)
            opt.append([g2[n + 2], 7, False])
        advance(must, opt)
    dump("CTa", CT[:, 0:4, :].rearrange("p k t -> p (k t)"), [128, 4 * S])
    if stage == "G":
        return finish(nc, P, out_d, dump_tags)

    bB = Bump(A, SCR_LO, SCR_HI)
    wqkvb = bB.alloc([3, 8, 512], BF16)
    for c3 in range(3):
        P.dma("pool", wqkvb[:, c3, :, :],
              win_v[:, :, 2056 + 512 * c3:2056 + 512 * (c3 + 1)], "wqkvb%d" % c3)
    QT0 = bB.alloc([S], BF16)
    KTz = [bB.alloc([S], BF16) for _ in range(2)]
    QTb = [QT0, QT0]
    P.memset("pool", KTz[0][64:128, :], 0.0)
    P.memset("pool", KTz[1][0:64, :], 0.0)
    VAll = bB.alloc([48, 4, 192], BF16)
    PTbuf = Ring([bB.alloc([1024], BF16) for _ in range(2)])
    PT3 = bB.alloc([16, 128], BF16)
    rden0 = bB.alloc([512])
    rdenb = [rden0, rden0]
    P.memset("pool", VAll[:, :, :, 64:128], 1.0)
    ringS = Ring([2, 3, 4, 5])
    ringP = Ring([6, 7])
    accR = Ring([0, 1])

    def tok_slices():
        sl = []
        for n in range(16):
            sl.append(slice(128 * n, 128 * (n + 1), 1))
        for r in range(4):
            for c in range(4):
                sl.append(slice(512 * c + r, 512 * (c + 1), 4))
        for r in range(16):
            sl.append(slice(r, S, 16))
        return sl

    TOK = tok_slices()

    def PROJ_V():
        for t in range(48):
            bank = ringP.next()
            ps = pbf(bank)
            sl = TOK[t]
            for kc in range(8):
                P.mm(ps, hT[:, kc, sl], wqkvb[:, 2, kc, :], start=(kc == 0), stop=(kc == 7))
            ps4 = ps.rearrange("p (j e d) -> p j e d", j=4, e=2)
            P.copy("act", VAll[:, t, :, 0:64], ps4[:, :, 0, :])
            P.copy("dve", VAll[:, t, :, 128:192], ps4[:, :, 1, :])

    def PROJ_B(j):
        k = j % 2
        QT = QTb[k]
        for (dst, cbase) in ((QT, 128 * j), (None, 512 + 128 * j)):
            for tb in range(4):
                bank = ringP.next()
                ps = pbf(bank)
                for kc in range(8):
                    P.mm(ps, wqkvb[:, cbase // 512, kc, cbase % 512:cbase % 512 + 128], hT[:, kc, 512 * tb:512 * (tb + 1)],
                         start=(kc == 0), stop=(kc == 7))
                cs = slice(512 * tb, 512 * (tb + 1))
                if dst is not None:
                    P.copy("dve" if tb % 2 else "act", dst[:, cs], ps)
                else:
                    P.copy("act", KTz[0][0:64, cs], ps[0:64, :])
                    P.copy("dve", KTz[1][64:128, cs], ps[64:128, :])

    def ATTN(j):
        k = j % 2
        QT = QTb[k]

        class _VA:
            def __init__(self, h):
                self.h = h

            def __getitem__(self, idx):
                e_ = self.h % 2
                return VAll[:, idx[1], self.h // 2, 64 * e_:64 * e_ + 128]

        for e in range(2):
            hp = slice(64 * e, 64 * e + 64)
            KT = KTz[e]
            fp = slice(0, 128)
            VA = _VA(2 * j + e)
            for g in range(4):
                bank = ringS.next()
                ps = pbf(bank)
                P.mm(ps, IDENTb, MBA4b, start=True, stop=False)
                for r4 in range(4):
                    r = 4 * g + r4
                    P.mm(ps[:, 128 * r4:128 * (r4 + 1)], KT[fp, r:S:16], QT[fp, r:S:16], start=False, stop=(r4 == 3))
                P.act(PT3[:, 4 * g:4 * g + 4, :], ps.rearrange("p (a d) -> p a d", a=4), AF.Exp, scale=0.125)
            for c in range(4):
                acc = pbf(accR.next())
                first = [True]

                def pv(out, lhsT, rhs, last=False):
                    P.mm(out, lhsT, rhs, start=first[0], stop=last, skip_group_check=True)
                    first[0] = False
                pt = PTbuf.next()
                bank = ringS.next(); ps = pbf(bank)
                P.mm(ps, IDENTb, MBA4b, start=True, stop=False)
                for i in range(4):
                    n = 4 * c + i
                    P.mm(ps[:, 128 * i:128 * (i + 1)], KT[fp, 128 * n:128 * (n + 1)], QT[fp, 128 * n:128 * (n + 1)],
                         start=False, stop=(i == 3))
                P.act(pt[:, 0:512], ps, AF.Exp, scale=0.125)
                bank = ringS.next(); ps = pbf(bank)
                P.mm(ps, IDENTb, MBB4b, start=True, stop=False)
                for i in range(4):
                    n = 4 * c + i
                    if n == 0:
                        continue
                    P.mm(ps[:, 128 * i:128 * (i + 1)], KT[fp, 128 * (n - 1):128 * n], QT[fp, 128 * n:128 * (n + 1)],
                         start=False, stop=(i == 3))
                P.act(pt[:, 512:1024], ps, AF.Exp, scale=0.125)
                pt1 = pt
                pt = PTbuf.next()
                bank = ringS.next(); ps = pbf(bank)
                P.mm(ps, IDENTb, MBA4b, start=True, stop=False)
                for r in range(4):
                    sl = slice(512 * c + r, 512 * (c + 1), 4)
                    P.mm(ps[:, 128 * r:128 * (r + 1)], KT[fp, sl], QT[fp, sl], start=False, stop=(r == 3))
                P.act(pt[:, 0:512], ps, AF.Exp, scale=0.125)
                if c > 0:
                    bank = ringS.next(); ps = pbf(bank)
                    P.mm(ps, IDENTb, MBB4b, start=True, stop=False)
                    for r in range(4):
                        sl = slice(512 * c + r, 512 * (c + 1), 4)
                        slk = slice(512 * (c - 1) + r, 512 * c, 4)
                        P.mm(ps[:, 128 * r:128 * (r + 1)], KT[fp, slk], QT[fp, sl], start=False, stop=(r == 3))
                    P.act(pt[:, 512:1024], ps, AF.Exp, scale=0.125)
                for i in range(4):
                    n = 4 * c + i
                    pv(acc[:, 128 * i:128 * (i + 1)], VA[:, n, e, :], pt1[:, 128 * i:128 * (i + 1)])
                    if n > 0:
                        pv(acc[:, 128 * i:128 * (i + 1)], VA[:, n - 1, e, :], pt1[:, 512 + 128 * i:512 + 128 * (i + 1)])
                for r in range(4):
                    pv(acc[:, r:512:4], VA[:, 16 + 4 * r + c, e, :], pt[:, 128 * r:128 * (r + 1)])
                    if c > 0:
                        pv(acc[:, r:512:4], VA[:, 16 + 4 * r + c - 1, e, :], pt[:, 512 + 128 * r:512 + 128 * (r + 1)])
                for r in range(16):
                    pv(acc[:, r:512:16], VA[:, 32 + r, e, :], PT3[:, r, 32 * c:32 * (c + 1)], last=(r == 15))
                num = slice(64 * e, 64 * e + 64)
                den = slice(64 * (1 - e), 64 * (1 - e) + 64)
                rd = rdenb[c % 2]
                P.recip(rd[den, :], acc[den, :])
                P.tt("dve", CT[num, 4 + j, 512 * c:512 * (c + 1)], acc[num, :], rd[den, :], ALU.mult)

    PROJ_V()
    for j in range(4):
        PROJ_B(j)
        ATTN(j)
    dump("CTb", CT[:, 4:8, :].rearrange("p k t -> p (k t)"), [128, 4 * S])
    if stage == "B":
        return finish(nc, P, out_d, dump_tags)

    bF = Bump(A, SCR_LO, SCR_HI)
    h2T = A.view(32768, [8, 1024], BF16)
    woutb = A.view(32768 + 16384, [2, 8, 512], BF16)
    X1 = bF.alloc([8, D])
    wu = [bF.alloc([2, 8, 512], BF16) for _ in range(2)]
    wd = [bF.alloc([4, D], BF16) for _ in range(2)]
    aTb = [bF.alloc([4, 1024], BF16) for _ in range(2)]
    rawF = [[bF.alloc([520]) for _ in range(2)] for _ in range(2)]
    accF = [[bF.alloc([512]) for _ in range(2)] for _ in range(2)]
    sgF = [bF.alloc([512]) for _ in range(2)]
    hbF = [bF.alloc([D], BF16), accF[1][1].bitcast(BF16)]
    junkF = sgF[0].bitcast(BF16)
    ssF = bF.alloc([8]); rsF = bF.alloc([8]); ssO = bF.alloc([8]); rsO = bF.alloc([8])
    for c2 in range(2):
        P.dma("pool", woutb[:, c2, :, :], wout_v[:, :, 512 * c2:512 * (c2 + 1)], "wout%d" % c2)
    P.dma("sp", NW1, n2_d, "c_nw1")
    P.dma("sp", NW2, nf_d, "c_nw2")
    groups = [list(range(g0, min(g0 + 4, 22))) for g0 in range(0, 22, 4)]
    out_tags = ["out%d" % i for i in range(8)]
    items = [(H, gi) for H in range(2) for gi in range(len(groups))]

    def load_wu(k):
        H, gi = items[k]
        grp = groups[gi]; g0 = grp[0]; npair = len(grp); slot = k % 2
        P.dma("pool", wu[slot][:, 0, :, 0:128 * npair], wup_v[:, :, 128 * g0:128 * (g0 + npair)], "wug%d" % slot)
        P.dma("pool", wu[slot][:, 1, :, 0:128 * npair],
              wup_v[:, :, 2816 + 128 * g0:2816 + 128 * (g0 + npair)], "wuu%d" % slot)

    def load_wd(k):
        H, gi = items[k]
        grp = groups[gi]; g0 = grp[0]; npair = len(grp); slot = k % 2
        P.dma("pool", wd[slot][:, 0:npair, :], wdn_v[:, g0:g0 + npair, :], "wd%d" % slot)

    def PRO(H):
        scr = [(junkF, ssF[:, i8:i8 + 1], rsF[:, i8:i8 + 1], hbF[i8 % 2]) for i8 in range(8)]

        def st_a(i8):
            i = 8 * H + i8
            P.dma("sp", X1[:, i8, :], x_d[128 * i:128 * (i + 1), :], "xf%d" % i8)
            b0 = 2 * (i8 % 2)
            for h2 in range(2):
                for kc in range(8):
                    P.mm(pbf(b0 + h2), CT[:, kc, 128 * i:128 * (i + 1)], woutb[:, h2, kc, :],
                         start=(kc == 0), stop=(kc == 7))
            for h2 in range(2):
                P.tt("dve", X1[:, i8, 512 * h2:512 * (h2 + 1)], X1[:, i8, 512 * h2:512 * (h2 + 1)], pbf(b0 + h2), ALU.add)
            if i == 0:
                dump("X1", X1[:, 0, :], [128, D])
            rmsnorm_stage1a(X1[:, i8, :], scr[i8])

        for t in range(10):
            if t < 8:
                st_a(t)
            if 1 <= t <= 8:
                rmsnorm_stage1b(X1[:, t - 1, :], NW1, scr[t - 1])
            if t >= 2:
                rmsnorm_stage2(h2T, 128 * (t - 2), scr[t - 2], 4 + (t % 2))

    upar = [0]

    def UGEN(k):
        H, gi = items[k]
        grp = groups[gi]; slot = k % 2; aT = aTb[k % 2]
        for p, g in enumerate(grp):
            for tb in range(2):
                par = upar[0]
                upar[0] ^= 1
                cols = slice(512 * tb, 512 * (tb + 1))
                banks = (4, 5) if par == 0 else (6, 7)
                accs = []
                for gu in range(2):
                    ps = pbf(banks[gu])
                    for kc in range(8):
                        P.mm(ps, wu[slot][:, gu, kc, 128 * p:128 * (p + 1)], h2T[:, kc, cols],
                             start=(kc == 0), stop=(kc == 7))
                    cc = g + 22 * gu
                    raw = rawF[par][gu]; acc = accF[par][gu]
                    P.copy("pool", raw[:, 0:2], HALOF[:, cc, :])
                    P.copy("act", raw[:, 2:514], ps)
                    P.copy("pool", HALOF[:, cc, :], raw[:, 512:514])
                    P.act(acc, ps, AF.Identity, scale=cwF[:, cc, 2:3])
                    P.stt(acc, raw[:, 1:513], cwF[:, cc, 1:2], acc, ALU.mult, ALU.add)
                    P.stt(acc, raw[:, 0:512], cwF[:, cc, 0:1], acc, ALU.mult, ALU.add)
                    accs.append(acc)
                sg = sgF[par]
                P.act(sg, accs[0], AF.Silu)
                P.tt("pool", aT[:, p, cols], sg, accs[1], ALU.mult)
                yield

    def DGEN(k):
        H, gi = items[k]
        grp = groups[gi]; slot = k % 2; aT = aTb[k % 2]; npair = len(grp)
        last = (gi == len(groups) - 1)
        for i8 in range(8):
            i = 8 * H + i8
            b0 = 2 * (i8 % 2)
            for h2 in range(2):
                for p in range(npair):
                    P.mm(pbf(b0 + h2), aT[:, p, 128 * i8:128 * (i8 + 1)], wd[slot][:, p, 512 * h2:512 * (h2 + 1)],
                         start=(p == 0), stop=(p == npair - 1))
            for h2 in range(2):
                P.tt("dve", X1[:, i8, 512 * h2:512 * (h2 + 1)], X1[:, i8, 512 * h2:512 * (h2 + 1)],
                     pbf(b0 + h2), ALU.add)
            if last:
                P.act(junkF, X1[:, i8, :], AF.Square, accum_out=ssO[:, i8:i8 + 1])
                P.act(rsO[:, i8:i8 + 1], ssO[:, i8:i8 + 1], AF.Ln, bias=EPS, scale=1.0 / D)
                P.act(rsO[:, i8:i8 + 1], rsO[:, i8:i8 + 1], AF.Exp, scale=-0.5)
                P.stt(X1[:, i8, :], X1[:, i8, :], rsO[:, i8:i8 + 1], NW2, ALU.mult, ALU.mult)
                P.dma("sp", out_d[128 * i:128 * (i + 1), :], X1[:, i8, :], "out%d" % i8)
            yield

    def rr(gens):
        gens = list(gens)
        live = [True] * len(gens)
        while any(live):
            for gi_, g_ in enumerate(gens):
                if live[gi_]:
                    try:
                        next(g_)
                    except StopIteration:
                        live[gi_] = False

    load_wu(0)
    prevD = None
    for k in range(len(items)):
        H, gi = items[k]
        load_wd(k)
        if k + 1 < len(items):
            load_wu(k + 1)
        if gi == 0:
            if prevD is not None:
                rr([prevD])
                prevD = None
            PRO(H)
        u = UGEN(k)
        rr([u] if prevD is None else [u, prevD])
        prevD = DGEN(k)
    rr([prevD])
    return finish(nc, P, out_d, dump_tags + out_tags)


def finish(nc, P, out_d, dump_tags):
    tags = list(dump_tags)
    P.emit(final_dma_tags=tags)
    return nc


def prep_inputs(inp):
    f = lambda a: np.ascontiguousarray(np.asarray(a, dtype=np.float32))
    x = f(inp["x"])
    rep = lambda v: np.ascontiguousarray(np.broadcast_to(f(v).reshape(1, -1), (128, f(v).size)))
    cwa = f(inp["conv_qkv_w"])[0]
    cwA = np.ascontiguousarray(cwa.T.reshape(12, 128, 4).transpose(1, 0, 2).reshape(128, 48))
    cwf = f(inp["ffn_conv_w"])[0]
    cwF = np.ascontiguousarray(cwf.T.reshape(44, 128, 3).transpose(1, 0, 2).reshape(128, 132))
    shared = {
        "w_in": f(inp["w_in"])[0], "w_out": f(inp["w_out"])[0], "w_up": f(inp["w_up"])[0],
        "w_down": f(inp["w_down"])[0],
        "n1rep": rep(inp["norm1_w"]), "n2rep": rep(inp["norm2_w"]), "nfrep": rep(inp["final_norm_w"]),
        "cwA": cwA, "cwF": cwF, "gnwrep": rep(inp["gdn_norm_w"]),
        "dtbrep": np.ascontiguousarray(np.tile(rep(inp["dt_bias"]), (1, 16))),
        "alogrep": np.ascontiguousarray(np.tile(rep(inp["a_log"]), (1, 16))),
        "consts": make_consts(),
    }
    maps = []
    for b in range(x.shape[0]):
        m = dict(shared)
        m["x"] = np.ascontiguousarray(x[b])
        maps.append(m)
    return maps


def kernel(**inputs):
    maps = prep_inputs(inputs)
    nc = build("full")
    res = run_bass_kernel_spmd(nc, maps, core_ids=list(range(8)))
    out = np.stack([np.asarray(r["out"], dtype=np.float32) for r in res.results], 0)
    return out
```

```python
import contextlib
import numpy as np
import concourse.bass as bass
import concourse.mybir as mybir
from concourse.bass_utils import run_bass_kernel_spmd

F32 = mybir.dt.float32
BF16 = mybir.dt.bfloat16
AF = mybir.ActivationFunctionType
ALU = mybir.AluOpType
AX = mybir.AxisListType

S = 2048
D = 1024
NT = 16
EPS = 1e-6
NEG = -30000.0
ENGS = ("pe", "act", "dve", "pool", "sp")


def _rect(ap):
    t = ap.tensor
    if str(ap.space) == "PSUM":
        return (t.name, 0, 128, 0, 2048)
    esz = mybir.dt.size(ap.dtype)
    pstride = 1
    for s in tuple(t.shape)[1:]:
        pstride *= s
    p0 = ap.start_partition()
    p1 = p0 + ap.partition_size()
    f0 = ap.offset - p0 * pstride
    ext = 0
    for (st, cnt) in tuple(ap.ap)[1:]:
        ext += abs(st) * (cnt - 1)
    return (t.name, p0, p1, f0 * esz, (f0 + ext + 1) * esz)


class _Op:
    __slots__ = ("eng", "fn", "idx", "is_dma", "tag", "waits", "signal", "sigval")


class Prog:
    def __init__(self, nc):
        self.nc = nc
        self.ops = {e: [] for e in ENGS}
        self.track = {}
        self.waited = {e: {} for e in ENGS}
        self.tagcount = {}

    def _add(self, eng, fn, reads, writes, is_dma=False, tag=None):
        op = _Op()
        op.eng = eng; op.fn = fn; op.is_dma = is_dma; op.tag = tag
        op.signal = False; op.sigval = None
        op.idx = len(self.ops[eng])
        op.waits = []
        deps = {}
        rrects = [_rect(a) for a in reads if a is not None and str(a.space) != "DRAM"]
        wrects = [_rect(a) for a in writes if a is not None and str(a.space) != "DRAM"]
        prects = [r for r in rrects if r[4] == 2048 and r[0].startswith("pb") and r not in wrects]
        for (nm, p0, p1, f0, f1) in rrects:
            for rec in self.track.get(nm, ()):
                if rec[5] == 1 and rec[0] < p1 and p0 < rec[1] and rec[2] < f1 and f0 < rec[3]:
                    deps[id(rec[4])] = (rec[4], True)
        for (nm, p0, p1, f0, f1) in wrects + prects:
            for rec in self.track.get(nm, ()):
                if rec[0] < p1 and p0 < rec[1] and rec[2] < f1 and f0 < rec[3]:
                    k = id(rec[4])
                    if k not in deps:
                        deps[k] = (rec[4], False)
        need = {}
        for (d, raw) in deps.values():
            if d.is_dma:
                key = ("dma", d.tag)
                val = self.tagcount[d.tag]
                if need.get(key, 0) < val:
                    need[key] = val
            else:
                if d.eng == eng and eng == "pe":
                    continue
                key = ("eng", d.eng)
                cur = need.get(key)
                if cur is None or cur.idx < d.idx:
                    need[key] = d
        w = self.waited[eng]
        for key, v in need.items():
            if key[0] == "dma":
                if w.get(key, 0) >= v:
                    continue
                w[key] = v
                op.waits.append((key, v))
            else:
                if w.get(key, -1) >= v.idx:
                    continue
                w[key] = v.idx
                v.signal = True
                op.waits.append((key, v))
        if is_dma:
            self.tagcount[tag] = self.tagcount.get(tag, 0) + 16
        for (nm, p0, p1, f0, f1) in wrects:
            lst = self.track.setdefault(nm, [])
            lst[:] = [r for r in lst if not (p0 <= r[0] and r[1] <= p1 and f0 <= r[2] and r[3] <= f1)]
            lst.append([p0, p1, f0, f1, op, 1])
        for (nm, p0, p1, f0, f1) in prects:
            self.track[nm] = [[p0, p1, f0, f1, op, 2]]
        for (nm, p0, p1, f0, f1) in rrects:
            if nm.startswith("pb"):
                continue
            lst = self.track.setdefault(nm, [])
            done = False
            for r in lst:
                if r[5] == 0 and r[4].eng == eng and (not r[4].is_dma) and (not is_dma) \
                        and r[0] == p0 and r[1] == p1 and r[2] == f0 and r[3] == f1:
                    r[4] = op
                    done = True
                    break
            if not done:
                lst.append([p0, p1, f0, f1, op, 0])
        self.ops[eng].append(op)
        return op

    def dma(self, q, out, in_, tag):
        return self._add(q, lambda e: e.dma_start(out=out, in_=in_), [in_], [out], is_dma=True, tag=tag)

    def mm(self, out, lhsT, rhs, start=True, stop=True, **kw):
        rd = [lhsT, rhs] + ([] if start else [out])
        return self._add("pe", lambda e: e.matmul(out, lhsT, rhs, start=start, stop=stop, **kw), rd, [out])

    def tr(self, out, in_, ident):
        return self._add("pe", lambda e: e.transpose(out, in_, ident), [in_, ident], [out])

    def act(self, out, in_, func, bias=None, scale=None, accum_out=None):
        kw = {}
        rd = [in_]
        if bias is not None:
            kw["bias"] = bias
            if not isinstance(bias, (int, float)):
                rd.append(bias)
        if scale is not None:
            kw["scale"] = scale
            if not isinstance(scale, (int, float)):
                rd.append(scale)
        wr = [out]
        if accum_out is not None:
            kw["accum_out"] = accum_out
            wr.append(accum_out)
        return self._add("act", lambda e: e.activation(out, in_, func, **kw), rd, wr)

    def tt(self, eng, out, in0, in1, op):
        return self._add(eng, lambda e: e.tensor_tensor(out, in0, in1, op), [in0, in1], [out])

    def ts(self, eng, out, in0, s1, s2, op0, op1=None):
        rd = [in0] + [s for s in (s1, s2) if s is not None and not isinstance(s, (int, float))]
        kw = {}
        if op1 is not None:
            kw["op1"] = op1
        return self._add(eng, lambda e: e.tensor_scalar(out, in0, s1, s2, op0, **kw), rd, [out])

    def stt(self, out, in0, scalar, in1, op0, op1):
        rd = [in0, in1] + ([] if isinstance(scalar, (int, float)) else [scalar])
        return self._add("dve", lambda e: e.scalar_tensor_tensor(out, in0, scalar, in1, op0, op1), rd, [out])

    def copy(self, eng, out, in_):
        if eng == "act":
            return self._add("act", lambda e: e.copy(out, in_), [in_], [out])
        return self._add(eng, lambda e: e.tensor_copy(out, in_), [in_], [out])

    def memset(self, eng, ap, val):
        return self._add(eng, lambda e: e.memset(ap, val), [], [ap])

    def recip(self, out, in_):
        return self._add("dve", lambda e: e.reciprocal(out, in_), [in_], [out])

    def rsum(self, out, in_):
        return self._add("dve", lambda e: e.tensor_reduce(out, in_, AX.X, ALU.add), [in_], [out])

    def emit(self, final_dma_tags=()):
        nc = self.nc
        for e in ENGS:
            c = 0
            for op in self.ops[e]:
                if op.signal and not op.is_dma:
                    c += 1
                    op.sigval = c
        with contextlib.ExitStack() as st:
            esem = {e: st.enter_context(nc.semaphore("s_" + e)) for e in ENGS}
            dsem = {t: st.enter_context(nc.semaphore("d_%d" % i)) for i, t in enumerate(self.tagcount)}
            block = st.enter_context(nc.Block())
            engobj = {"pe": block.tensor, "act": block.scalar, "dve": block.vector,
                      "pool": block.gpsimd, "sp": block.sync}

            def make(ename):
                def body(eng):
                    for op in self.ops[ename]:
                        for (key, v) in op.waits:
                            if key[0] == "dma":
                                eng.wait_ge(dsem[key[1]], v)
                            else:
                                eng.wait_ge(esem[key[1]], v.sigval)
                        ins = op.fn(eng)
                        if op.is_dma:
                            ins.then_inc(dsem[op.tag], 16)
                        elif op.signal:
                            ins.then_inc(esem[ename], 1)
                    if ename == "sp":
                        for t in final_dma_tags:
                            eng.wait_ge(dsem[t], self.tagcount[t])
                return body

            for e in ENGS:
                engobj[e](make(e))
        return nc


def _prod(shape):
    n = 1
    for s in shape:
        n *= s
    return n


class Arena:
    def __init__(self, nc, nbytes):
        self.t = nc.alloc_sbuf_tensor("arena", [128, nbytes // 4], F32)
        self.nbytes = nbytes

    def view(self, off, shape, dt):
        n = _prod(shape)
        esz = 4 if dt == F32 else 2
        assert off % 4 == 0 and (n * esz) % 4 == 0 and off + n * esz <= self.nbytes, (off, shape)
        ap = self.t[:, off // 4:(off + n * esz) // 4]
        if dt != F32:
            ap = ap.bitcast(dt)
        if len(shape) > 1:
            names = " ".join("d%d" % i for i in range(len(shape)))
            kw = {"d%d" % i: shape[i] for i in range(1, len(shape))}
            ap = ap.rearrange("p (%s) -> p %s" % (names, names), **kw)
        return ap


class Bump:
    def __init__(self, arena, lo, hi):
        self.a = arena; self.lo = lo; self.hi = hi; self.cur = lo

    def alloc(self, shape, dt=F32):
        esz = 4 if dt == F32 else 2
        n = (_prod(shape) * esz + 31) // 32 * 32
        off = self.cur
        self.cur += n
        assert self.cur <= self.hi, ("arena overflow", self.cur, self.hi)
        return self.a.view(off, shape, dt)


class Ring:
    def __init__(self, items):
        self.items = list(items); self.i = 0

    def next(self):
        x = self.items[self.i % len(self.items)]
        self.i += 1
        return x


_CNAMES = ["IDENT", "M1", "M2", "TRI", "SUF", "IND0", "IND1", "S32", "OFF", "ONES"]
_CB = {"MB64": 512, "MBA4": 512, "MBB4": 512}
NCF = 128 * len(_CNAMES)
NCONST = NCF + 512 * 3


def make_consts():
    i = np.arange(128)[:, None]
    j = np.arange(128)[None, :]
    same = (i // 64) == (j // 64)
    c = {}
    c["IDENT"] = (i == j)
    c["M1"] = (i > j)
    c["M2"] = (i <= j)
    c["TRI"] = (i <= j) & same
    c["SUF"] = (i > j) & same
    c["IND0"] = (i < 64) & (j >= 0)
    c["IND1"] = (i >= 64) & (j >= 0)
    c["S32"] = (i < j) & ((i // 32) == (j // 32))
    c["OFF"] = same & ((i % 64) < 32) & ((j % 64) >= 32)
    c["ONES"] = np.ones((128, 128), bool)
    cols = [c[n].astype(np.float32) for n in _CNAMES]
    mb64 = np.where((i <= j) & same, 0.0, NEG).astype(np.float32)
    mba = np.where(i <= j, 0.0, NEG).astype(np.float32)
    mbb = np.where(i >= j, 0.0, NEG).astype(np.float32)
    cols += [np.tile(mb64, (1, 4)), np.tile(mba, (1, 4)), np.tile(mbb, (1, 4))]
    return np.ascontiguousarray(np.concatenate(cols, 1))


def build(stage="full", dumps=()):
    nc = bass.Bass("TRN2", target_bir_lowering=False)
    P = Prog(nc)
    dumps = set(dumps)
    dump_tags = []

    def din(name, shape):
        return nc.dram_tensor(name, list(shape), F32, kind="ExternalInput").ap()

    x_d = din("x", [S, D])
    win_d = din("w_in", [D, 3592])
    wout_d = din("w_out", [D, D])
    wup_d = din("w_up", [D, 5632])
    wdn_d = din("w_down", [2816, D])
    n1_d = din("n1rep", [128, D])
    n2_d = din("n2rep", [128, D])
    nf_d = din("nfrep", [128, D])
    cwa_d = din("cwA", [128, 48])
    cwf_d = din("cwF", [128, 132])
    gnw_d = din("gnwrep", [128, 128])
    dtb_d = din("dtbrep", [128, 64])
    alog_d = din("alogrep", [128, 64])
    cst_d = din("consts", [128, NCONST])
    out_d = nc.dram_tensor("out", [S, D], F32, kind="ExternalOutput").ap()

    def dump(name, sb_ap, shape):
        if name not in dumps:
            return
        d = nc.dram_tensor("dbg_" + name, list(shape), sb_ap.dtype, kind="ExternalOutput").ap()
        tg = "dbg_" + name
        P.dma("sp", d, sb_ap, tg)
        dump_tags.append(tg)

    win_v = win_d.rearrange("(k p) c -> p k c", p=128)
    wout_v = wout_d.rearrange("(k p) c -> p k c", p=128)
    wup_v = wup_d.rearrange("(k p) c -> p k c", p=128)
    wdn_v = wdn_d.rearrange("(k p) c -> p k c", p=128)

    ARENA_BYTES = 207872
    A = Arena(nc, ARENA_BYTES)
    pb = [nc.alloc_psum_tensor("pb%d" % i, [128, 512], F32) for i in range(8)]

    def pbf(i):
        return pb[i][:]

    def pbb(i):
        return pb[i][:].bitcast(BF16)

    CT = A.view(0, [8, S], BF16)
    hT = A.view(32768, [8, S], BF16)
    pers = Bump(A, 65536, 86016)
    CF = pers.alloc([NCF])
    CBm = pers.alloc([1536 + 256], BF16)
    cwA = pers.alloc([12, 4])
    cwF = pers.alloc([44, 3])
    HALOA = pers.alloc([12, 3])
    HALOF = pers.alloc([44, 2])
    GNW = pers.alloc([128])
    NW1 = pers.alloc([D])
    NW2 = pers.alloc([D])
    SCR_LO = 86016
    SCR_HI = ARENA_BYTES

    def cf(name):
        k = _CNAMES.index(name)
        return CF[:, 128 * k:128 * (k + 1)]

    IDENT = cf("IDENT"); M1 = cf("M1"); M2 = cf("M2"); TRI = cf("TRI"); SUF = cf("SUF")
    IND0 = cf("IND0"); IND1 = cf("IND1"); S32 = cf("S32"); OFFM = cf("OFF"); ONES = cf("ONES")
    MB64b = CBm[:, 0:512]; MBA4b = CBm[:, 512:1024]; MBB4b = CBm[:, 1024:1536]
    IDENTb = CBm[:, 1536:1664]; ONESb = CBm[:, 1664:1792]

    P.dma("sp", CF, cst_d[:, 0:NCF], "c_cf")
    ctmp = A.view(16384 + 8192, [1536], F32)
    P.dma("sp", ctmp, cst_d[:, NCF:NCONST], "c_tmp")
    P.copy("dve", CBm[:, 0:1536], ctmp)
    P.copy("dve", IDENTb, IDENT)
    P.copy("dve", ONESb, ONES)
    P.dma("sp", cwA.rearrange("p a b -> p (a b)"), cwa_d, "c_cwa")
    P.dma("sp", cwF.rearrange("p a b -> p (a b)"), cwf_d, "c_cwf")
    P.dma("sp", GNW, gnw_d, "c_gnw")
    P.dma("sp", NW1, n1_d, "c_nw1")
    P.memset("pool", HALOA.rearrange("p a b -> p (a b)"), 0.0)
    P.memset("pool", HALOF.rearrange("p a b -> p (a b)"), 0.0)

    def bc_last(ap, n):
        return ap.unsqueeze(2).broadcast_to([128, ap.shape[1], n])

    def bc_mid(ap, n):
        return ap.unsqueeze(1).broadcast_to([128, n, ap.shape[1]])

    def rmsnorm_stage1(src_tile, wtile, scr):
        junk, ssv, rst, hb = scr
        P.act(junk, src_tile, AF.Square, accum_out=ssv)
        P.act(rst, ssv, AF.Ln, bias=EPS, scale=1.0 / D)
        P.act(rst, rst, AF.Exp, scale=-0.5)
        P.stt(hb, src_tile, rst, wtile, ALU.mult, ALU.mult)

    def rmsnorm_stage1a(src_tile, scr):
        junk, ssv, rst, hb = scr
        P.act(junk, src_tile, AF.Square, accum_out=ssv)
        P.act(rst, ssv, AF.Ln, bias=EPS, scale=1.0 / D)
        P.act(rst, rst, AF.Exp, scale=-0.5)

    def rmsnorm_stage1b(src_tile, wtile, scr):
        junk, ssv, rst, hb = scr
        P.stt(hb, src_tile, rst, wtile, ALU.mult, ALU.mult)

    def rmsnorm_stage2(dstT, col0, scr, bank):
        hb = scr[3]
        psT = pbb(bank)
        for kc in range(8):
            P.tr(psT[:, 128 * kc:128 * (kc + 1)], hb[:, 128 * kc:128 * (kc + 1)], IDENTb)
        P.copy("act", dstT[:, :, col0:col0 + 128], psT.rearrange("p (k t) -> p k t", k=8))

    bA = Bump(A, 0, 16384)
    xbuf = [bA.alloc([D]) for _ in range(3)]
    junkA = bA.alloc([D], BF16)
    hbuf = [NW2.bitcast(BF16)[:, 0:D], NW2.bitcast(BF16)[:, D:2 * D]]
    ssA = bA.alloc([NT])
    rsA = bA.alloc([NT])
    scrA = [(junkA, ssA[:, i:i + 1], rsA[:, i:i + 1], hbuf[i % 2]) for i in range(NT)]
    for i in range(NT + 2):
        if i < NT:
            xt = xbuf[i % 3]
            P.dma("sp", xt, x_d[128 * i:128 * (i + 1), :], "xa%d" % (i % 3))
            rmsnorm_stage1a(xt, scrA[i])
        if 1 <= i <= NT:
            rmsnorm_stage1b(xbuf[(i - 1) % 3], NW1, scrA[i - 1])
        if i >= 2:
            rmsnorm_stage2(hT, 128 * (i - 2), scrA[i - 2], 6 + (i % 2))
    dump("hT", hT.rearrange("p k t -> p (k t)"), [128, 8 * S])
    if stage == "A":
        return finish(nc, P, out_d, dump_tags)

    bG = Bump(A, SCR_LO, SCR_HI)
    bC = Bump(A, 16384, 32768)
    wz = bC.alloc([8, 512], BF16)
    qkvT = [dict(q=bG.alloc([4, 512], BF16), k=bG.alloc([4, 512], BF16), v=bG.alloc([4, 512], BF16))
            for _ in range(4)]
    szgb = [bG.alloc([4, 512], BF16), bG.alloc([4, 512], BF16), bG.alloc([4, 512], BF16), bC.alloc([4, 512], BF16)]
    BA = bG.alloc([NT, 8])
    sc_x = bG.alloc([64]); sc_mx = bG.alloc([64]); sc_mn = bG.alloc([64])
    dtb = bG.alloc([64]); negA = bG.alloc([64])
    gS = bG.alloc([64]); betaS = bG.alloc([64]); gamS = bG.alloc([64]); kesS = bG.alloc([64])
    gendS = bG.alloc([2, 64])
    wba = bG.alloc([8, 8], BF16)
    Sst = bG.alloc([4, 128]); Sb = bG.alloc([4, 128], BF16); ub = bG.alloc([4, 128], BF16)
    Obuf = [bC.alloc([4, 128]) for _ in range(2)]
    sqO = bG.alloc([4, 128], BF16); oab = sqO
    kesLo = bG.alloc([64]); kesHi = bG.alloc([64])
    ssq = bG.alloc([4]); rsq = bG.alloc([4])
    ov0 = bG.cur
    gset = []
    for _ in range(2):
        gset.append(dict(
            G1m=bG.alloc([4, 128]),
            DECT=bG.alloc([4, 128], BF16), Uall=bG.alloc([4, 128], BF16), Eu=bG.alloc([4, 128], BF16),
            PTa=bG.alloc([4, 128], BF16), PTb=bG.alloc([4, 128], BF16),
            UL0=bG.alloc([2, 4, 128], BF16), PWA=bG.alloc([2, 4, 128], BF16), PWB=bG.alloc([2, 4, 128], BF16),
            TTb=bG.alloc([4, 128], BF16), vb=bG.alloc([4, 128], BF16), gk=bG.alloc([4, 128], BF16)))
    scanop = []
    for _ in range(3):
        scanop.append(dict(KLO=bG.alloc([4, 128], BF16), KHI=bG.alloc([4, 128], BF16), ATT=bG.alloc([4, 128], BF16),
                           QD=bG.alloc([4, 128], BF16), WKT=bG.alloc([4, 128], BF16),
                           UV=bG.alloc([4, 128], BF16)))
    bO = Bump(A, ov0, bG.cur)
    wqkva = bO.alloc([3, 8, 512], BF16)
    rawb = [bO.alloc([520], BF16) for _ in range(8)]
    dgcb = [bO.alloc([4, 128], BF16) for _ in range(8)]
    sqbs = [bO.alloc([512], BF16) for _ in range(3)]
    rtbs = [bO.alloc([512]) for _ in range(3)]
    for c3 in range(3):
        P.dma("pool", wqkva[:, c3, :, :], win_v[:, :, 512 * c3:512 * (c3 + 1)], "wqkva%d" % c3)
    P.dma("pool", wba, win_v[:, :, 2048:2056], "wba")
    P.dma("pool", wz, win_v[:, :, 1536:2048], "wz")

    ringG = Ring([0, 1, 2, 3, 4, 5, 6, 7])

    g3 = gS.rearrange("p (n h) -> p n h", h=4)
    beta3 = betaS.rearrange("p (n h) -> p n h", h=4)
    gam3 = gamS.rearrange("p (n h) -> p n h", h=4)
    kesLo3 = kesLo.rearrange("p (n h) -> p n h", h=4)
    kesHi3 = kesHi.rearrange("p (n h) -> p n h", h=4)
    gend4 = gendS.rearrange("p a (n h) -> p a n h", h=4)

    def SCALARS():
        P.dma("sp", dtb, dtb_d, "c_dtb")
        P.dma("sp", negA, alog_d, "c_alog")
        P.act(negA, negA, AF.Exp)
        P.ts("dve", negA, negA, -1.0, None, ALU.mult)
        bk = ringG.next()
        psBA = pbf(bk)[:, 0:128].rearrange("p (n c) -> p n c", c=8)
        for i in range(NT):
            for kc in range(8):
                P.mm(psBA[:, i, :], hT[:, kc, 128 * i:128 * (i + 1)], wba[:, kc, :], start=(kc == 0), stop=(kc == 7))
        P.copy("act", BA, psBA)
        x3 = sc_x.rearrange("p (n h) -> p n h", h=4)
        P.tt("dve", x3, BA[:, :, 4:8], dtb.rearrange("p (n h) -> p n h", h=4), ALU.add)
        P.ts("dve", sc_mx, sc_x, 0.0, None, ALU.max)
        P.ts("dve", sc_mn, sc_x, 0.0, None, ALU.min)
        P.tt("dve", sc_mn, sc_mn, sc_mx, ALU.subtract)
        P.act(sc_mn, sc_mn, AF.Exp)
        P.act(sc_mn, sc_mn, AF.Ln, bias=1.0)
        P.tt("dve", sc_mx, sc_mx, sc_mn, ALU.add)
        P.tt("dve", gS, sc_mx, negA, ALU.mult)
        P.act(betaS.rearrange("p (n h) -> p n h", h=4), BA[:, :, 0:4], AF.Sigmoid)
        bk = ringG.next()
        psg = pbf(bk)
        P.mm(psg[:, 0:64], TRI, gS)
        P.mm(psg[:, 64:128], SUF, gS)
        P.mm(psg[:, 128:192], IND0, gS)
        P.mm(psg[:, 192:256], IND1, gS)
        P.act(gamS, psg[:, 0:64], AF.Exp)
        P.act(kesS, psg[:, 64:128], AF.Exp)
        P.tt("dve", kesS, kesS, betaS, ALU.mult)
        P.act(gendS.rearrange("p a b -> p (a b)"), psg[:, 128:256], AF.Exp)
        dump("gS", gS, [128, 64]); dump("betaS", betaS, [128, 64]); dump("gamS", gamS, [128, 64])
        dump("kesS", kesS, [128, 64]); dump("gendS", gendS.rearrange("p a b -> p (a b)"), [128, 128])
        P.ts("dve", kesLo, kesS, IND0[:, 0:1], None, ALU.mult)
        P.ts("dve", kesHi, kesS, IND1[:, 0:1], None, ALU.mult)


    def G1(m):
        t0 = 512 * m
        o = qkvT[m]
        pend = [None]
        for c in range(12):
            ps = pbf(ringG.next())
            for kc in range(8):
                P.mm(ps, wqkva[:, c // 4, kc, 128 * (c % 4):128 * (c % 4 + 1)], hT[:, kc, t0:t0 + 512], start=(kc == 0), stop=(kc == 7))
            raw = rawb[2 * m + c % 2]; dgc = dgcb[2 * m + c % 2]
            for i_ in range(4):
                P.ts("dve", dgc[:, i_, :], IDENTb, cwA[:, c, i_:i_ + 1], None, ALU.mult)
            P.copy("pool", raw[:, 0:3], HALOA[:, c, :])
            P.copy("act", raw[:, 3:515], ps)
            P.copy("pool", HALOA[:, c, :], raw[:, 512:515])
            if pend[0] is not None:
                pend[0]()

            def _conv(raw=raw, dgc=dgc, c=c):
                psC = pbf(ringG.next())
                for i_ in range(4):
                    P.mm(psC, dgc[:, i_, :], raw[:, i_:i_ + 512], start=(i_ == 0), stop=(i_ == 3))
                dst = (o["q"], o["k"], o["v"])[c // 4][:, c % 4, :]
                P.act(dst, psC, AF.Silu)
            pend[0] = _conv
            if c % 3 == 2:
                nz = 4 * m + c // 3
                psZ = pbf(ringG.next())
                for kc in range(8):
                    P.mm(psZ, hT[:, kc, 128 * nz:128 * (nz + 1)], wz[:, kc, :], start=(kc == 0), stop=(kc == 7))
                szg = szgb[m][:, c // 3, :]
                P.act(szg, psZ, AF.Silu)
                P.tt("pool", szg.rearrange("p (h d) -> p h d", h=4), szg.rearrange("p (h d) -> p h d", h=4),
                     bc_mid(GNW, 4), ALU.mult)
            yield
        pend[0]()
        yield
        for c in range(8):
            dst = (o["q"], o["k"])[c // 4][:, c % 4, :]
            sc = 128.0 if c < 4 else 1.0
            sqb = sqbs[(c + m) % 3]; rtb = rtbs[(c + m) % 3]
            P.tt("pool", sqb, dst, dst, ALU.mult)
            psn = pbf(ringG.next())
            P.mm(psn, ONESb, sqb)
            P.act(rtb, psn, AF.Ln, bias=EPS * sc, scale=sc)
            P.act(rtb, rtb, AF.Exp, scale=-0.5)
            P.tt("dve", dst, dst, rtb, ALU.mult)
            yield
        if m == 0:
            dump("qnT0", o["q"].rearrange("p h t -> p (h t)"), [128, 2048])
            dump("knT0", o["k"].rearrange("p h t -> p (h t)"), [128, 2048])
            dump("vsT0", o["v"].rearrange("p h t -> p (h t)"), [128, 2048])

    def G2(n):
        tl = 128 * (n % 4)
        so = scanop[n % 3]
        st = gset[n % 2]
        qnT = qkvT[n // 4]["q"]; knT = qkvT[n // 4]["k"]; vsT = qkvT[n // 4]["v"]
        G1m = st["G1m"]; DECT = st["DECT"]; Uall = st["Uall"]; Eu = st["Eu"]
        UL0 = st["UL0"]; PWA = st["PWA"]; PWB = st["PWB"]
        TTb = st["TTb"]; vb = st["vb"]; gk = st["gk"]
        Du, Dl = UL0[:, 0], UL0[:, 1]
        gam_b = bc_last(gam3[:, n, :], 128)
        keslo_b = bc_last(kesLo3[:, n, :], 128)
        keshi_b = bc_last(kesHi3[:, n, :], 128)
        beta_b = bc_last(beta3[:, n, :], 128)
        g_b = bc_last(g3[:, n, :], 128)

        def bankf():
            return pbf(ringG.next()).rearrange("p (h d) -> p h d", h=4)

        def bankb():
            return pbb(ringG.next())[:, 0:512].rearrange("p (h d) -> p h d", h=4)

        psKV = pbb(ringG.next()).rearrange("p (a h d) -> p a h d", a=2, h=4)
        psK = psKV[:, 0]; psV = psKV[:, 1]
        for h in range(4):
            P.tr(psK[:, h, :], knT[:, h, tl:tl + 128], IDENTb)
            P.tr(psV[:, h, :], vsT[:, h, tl:tl + 128], IDENTb)
        P.tt("dve", gk, psK, gam_b, ALU.mult)
        P.tt("dve", so["KLO"], psK, keslo_b, ALU.mult)
        P.tt("dve", so["KHI"], psK, keshi_b, ALU.mult)
        P.copy("act", vb, psV)
        yield
        P.tt("pool", G1m, bc_mid(M1, 4), g_b, ALU.mult)
        psD = bankf()
        P.mm(psD, IDENTb, MB64b, start=True, stop=False)
        for h in range(4):
            P.mm(psD[:, h, :], G1m[:, h, :], M2, start=False, stop=(h == 3))
        P.act(DECT, psD, AF.Exp)
        P.tt("pool", DECT, DECT, beta_b, ALU.mult)
        yield
        dg = Uall
        P.tt("pool", dg, bc_mid(IDENT, 4), gam_b, ALU.mult)
        psG = bankf()
        for h in range(4):
            P.mm(psG[:, h, :], ONESb, dg[:, h, :])
        P.tt("dve", so["QD"], qnT[:, :, tl:tl + 128], psG, ALU.mult)
        yield
        psKK = bankf(); psQK = bankf()
        for h in range(4):
            P.mm(psKK[:, h, :], knT[:, h, tl:tl + 128], knT[:, h, tl:tl + 128])
        for h in range(4):
            P.mm(psQK[:, h, :], knT[:, h, tl:tl + 128], qnT[:, h, tl:tl + 128])
        P.tt("dve", Uall, psKK, DECT, ALU.mult)
        P.tt("dve", so["ATT"], psQK, DECT, ALU.mult)
        P.tt("pool", Du, Uall, bc_mid(S32, 4), ALU.mult)
        P.tt("pool", Eu, Uall, bc_mid(OFFM, 4), ALU.mult)
        yield
        psT = bankb()
        for h in range(4):
            P.tr(psT[:, h, :], Du[:, h, :], IDENTb)
        P.copy("act", Dl, psT)
        P.tt("pool", st["PTa"], bc_mid(IDENT, 4), Du, ALU.subtract)
        yield
        pw = [UL0, PWA, PWB, PWA, PWB]
        PT, PTn = st["PTa"], st["PTb"]
        for k in range(1, 5):
            cur = pw[k - 1]; nxt = pw[k]
            if k < 4:
                psU = bankf()
                for h in range(4):
                    P.mm(psU[:, h, :], cur[:, 1, h, :], cur[:, 0, h, :])
            psL = bankf()
            for h in range(4):
                P.mm(psL[:, h, :], cur[:, 0, h, :], cur[:, 1, h, :])
            if k > 1:
                ps3 = bankf()
                for h in range(4):
                    P.mm(ps3[:, h, :], cur[:, 1, h, :], PT[:, h, :])
            if k < 4:
                P.copy("act", nxt[:, 0], psU)
            P.copy("act", nxt[:, 1], psL)
            if k > 1:
                P.tt("dve", PTn, PT, ps3, ALU.add)
                PT, PTn = PTn, PT
            yield
        ps3 = bankf()
        for h in range(4):
            P.mm(ps3[:, h, :], PWB[:, 1, h, :], PT[:, h, :])
        P.tt("dve", PTn, PT, ps3, ALU.add)
        PT, PTn = PTn, PT
        yield
        Pm = PWA[:, 0]; XT = DECT
        p1 = bankb()
        for h in range(4):
            P.tr(p1[:, h, :], PT[:, h, :], IDENTb)
        P.copy("act", Pm, p1)
        yield
        p2 = bankf()
        for h in range(4):
            P.mm(p2[:, h, :], Eu[:, h, :], Pm[:, h, :])
        P.copy("act", XT, p2)
        yield
        p3 = bankf()
        for h in range(4):
            P.mm(p3[:, h, :], XT[:, h, :], PT[:, h, :])
        P.tt("dve", TTb, PT, p3, ALU.subtract)
        yield
        p1 = bankf(); p2 = bankf()
        for h in range(4):
            P.mm(p1[:, h, :], TTb[:, h, :], vb[:, h, :])
        for h in range(4):
            P.mm(p2[:, h, :], gk[:, h, :], TTb[:, h, :])
        P.copy("act", so["UV"], p1)
        P.copy("dve", so["WKT"], p2)
        yield

    def SCAN(n):
        so = scanop[n % 3]
        O = Obuf[n % 2]
        for half in range(2):
            r0 = 64 * half
            rs = slice(r0, r0 + 64)
            kend = so["KLO"] if half == 0 else so["KHI"]
            psA = pbf(ringG.next()).rearrange("p (h d) -> p h d", h=4)
            for h in range(4):
                P.mm(psA[:, h, :], so["WKT"][:, h, :], Sb[:, h, :])
            P.tt("dve", ub[rs], so["UV"][rs], psA[rs], ALU.subtract)
            yield
            psS = pbf(ringG.next()).rearrange("p (h d) -> p h d", h=4)
            psO = pbf(ringG.next()).rearrange("p (h d) -> p h d", h=4)
            for h in range(4):
                P.mm(psS[:, h, :], kend[:, h, :], ub[:, h, :])
            for h in range(4):
                P.mm(psO[:, h, :], so["QD"][:, h, :], Sb[:, h, :], start=True, stop=False)
                P.mm(psO[:, h, :], so["ATT"][:, h, :], ub[:, h, :], start=False, stop=True)
            for h in range(4):
                P.stt(Sst[:, h, :], Sst[:, h, :], gend4[:, half, n, h:h + 1], psS[:, h, :], ALU.mult, ALU.add)
            P.copy("act", Sb, Sst)
            P.copy("act", O[rs], psO[rs])
            yield
        if n == 0:
            dump("O0", O.rearrange("p h d -> p (h d)"), [128, 512])
        if n == 15:
            dump("O15", O.rearrange("p h d -> p (h d)"), [128, 512])
        P.act(sqO, O, AF.Square)
        P.rsum(ssq, sqO)
        P.act(rsq, ssq, AF.Ln, bias=EPS, scale=1.0 / 128)
        P.act(rsq, rsq, AF.Exp, scale=-0.5)
        szg = szgb[n // 4][:, n % 4, :].rearrange("p (h d) -> p h d", h=4)
        P.tt("dve", sqO, O, bc_last(rsq, 128), ALU.mult)
        P.tt("dve", oab, sqO, szg, ALU.mult)
        yield
        psT = pbb(ringG.next())[:, 0:512].rearrange("p (h d) -> p h d", h=4)
        for h in range(4):
            P.tr(psT[:, h, :], oab[:, h, :], IDENTb)
        P.copy("act", CT[:, 0:4, 128 * n:128 * (n + 1)], psT)
        yield

    def advance(must, opt):
        live = [True] * len(must)
        while any(live) or any(q[1] > 0 for q in opt):
            for gi, g in enumerate(must):
                if live[gi]:
                    try:
                        next(g)
                    except StopIteration:
                        live[gi] = False
            for q in opt:
                if q[1] > 0:
                    q[1] -= 1
                    try:
                        next(q[0])
                    except StopIteration:
                        q[1] = 0
                        q[2] = True

    P.memset("dve", Sst.rearrange("p h d -> p (h d)"), 0.0)
    P.memset("pool", Sb.rearrange("p h d -> p (h d)"), 0.0)
    P.memset("pool", ub.rearrange("p h d -> p (h d)"), 0.0)
    advance([G1(m_) for m_ in range(4)], [])
    SCALARS()
    g2 = {0: G2(0), 1: G2(1)}
    advance([g2[0]], [[g2[1], 7, False]])
    for n in range(NT):
        must = [SCAN(n)]
        if n + 1 < NT:
            must.append(g2[n + 1])
        opt = []
        if n + 2 < NT:
            g2[n + 2] = G2(n + 2)
            opt.append([g2[n + 2], 7, False])
        advance(must, opt)
    dump("CTa", CT[:, 0:4, :].rearrange("p k t -> p (k t)"), [128, 4 * S])
    if stage == "G":
        return finish(nc, P, out_d, dump_tags)

    bB = Bump(A, SCR_LO, SCR_HI)
    wqkvb = bB.alloc([3, 8, 512], BF16)
    for c3 in range(3):
        P.dma("pool", wqkvb[:, c3, :, :],
              win_v[:, :, 2056 + 512 * c3:2056 + 512 * (c3 + 1)], "wqkvb%d" % c3)
    QT0 = bB.alloc([S], BF16)
    KTz = [bB.alloc([S], BF16) for _ in range(2)]
    QTb = [QT0, QT0]
    P.memset("pool", KTz[0][64:128, :], 0.0)
    P.memset("pool", KTz[1][0:64, :], 0.0)
    VAll = bB.alloc([48, 4, 192], BF16)
    PTbuf = Ring([bB.alloc([1024], BF16) for _ in range(2)])
    PT3 = bB.alloc([16, 128], BF16)
    rden0 = bB.alloc([512])
    rdenb = [rden0, rden0]
    P.memset("pool", VAll[:, :, :, 64:128], 1.0)
    ringS = Ring([2, 3, 4, 5])
    ringP = Ring([6, 7])
    accR = Ring([0, 1])

    def tok_slices():
        sl = []
        for n in range(16):
            sl.append(slice(128 * n, 128 * (n + 1), 1))
        for r in range(4):
            for c in range(4):
                sl.append(slice(512 * c + r, 512 * (c + 1), 4))
        for r in range(16):
            sl.append(slice(r, S, 16))
        return sl

    TOK = tok_slices()

    def PROJ_V():
        for t in range(48):
            bank = ringP.next()
            ps = pbf(bank)
            sl = TOK[t]
            for kc in range(8):
                P.mm(ps, hT[:, kc, sl], wqkvb[:, 2, kc, :], start=(kc == 0), stop=(kc == 7))
            ps4 = ps.rearrange("p (j e d) -> p j e d", j=4, e=2)
            P.copy("act", VAll[:, t, :, 0:64], ps4[:, :, 0, :])
            P.copy("dve", VAll[:, t, :, 128:192], ps4[:, :, 1, :])

    def PROJ_B(j):
        k = j % 2
        QT = QTb[k]
        for (dst, cbase) in ((QT, 128 * j), (None, 512 + 128 * j)):
            for tb in range(4):
                bank = ringP.next()
                ps = pbf(bank)
                for kc in range(8):
                    P.mm(ps, wqkvb[:, cbase // 512, kc, cbase % 512:cbase % 512 + 128], hT[:, kc, 512 * tb:512 * (tb + 1)],
                         start=(kc == 0), stop=(kc == 7))
                cs = slice(512 * tb, 512 * (tb + 1))
                if dst is not None:
                    P.copy("dve" if tb % 2 else "act", dst[:, cs], ps)
                else:
                    P.copy("act", KTz[0][0:64, cs], ps[0:64, :])
                    P.copy("dve", KTz[1][64:128, cs], ps[64:128, :])

    def ATTN(j):
        k = j % 2
        QT = QTb[k]

        class _VA:
            def __init__(self, h):
                self.h = h

            def __getitem__(self, idx):
                e_ = self.h % 2
                return VAll[:, idx[1], self.h // 2, 64 * e_:64 * e_ + 128]

        for e in range(2):
            hp = slice(64 * e, 64 * e + 64)
            KT = KTz[e]
            fp = slice(0, 128)
            VA = _VA(2 * j + e)
            for g in range(4):
                bank = ringS.next()
                ps = pbf(bank)
                P.mm(ps, IDENTb, MBA4b, start=True, stop=False)
                for r4 in range(4):
                    r = 4 * g + r4
                    P.mm(ps[:, 128 * r4:128 * (r4 + 1)], KT[fp, r:S:16], QT[fp, r:S:16], start=False, stop=(r4 == 3))
                P.act(PT3[:, 4 * g:4 * g + 4, :], ps.rearrange("p (a d) -> p a d", a=4), AF.Exp, scale=0.125)
            for c in range(4):
                acc = pbf(accR.next())
                first = [True]

                def pv(out, lhsT, rhs, last=False):
                    P.mm(out, lhsT, rhs, start=first[0], stop=last, skip_group_check=True)
                    first[0] = False
                pt = PTbuf.next()
                bank = ringS.next(); ps = pbf(bank)
                P.mm(ps, IDENTb, MBA4b, start=True, stop=False)
                for i in range(4):
                    n = 4 * c + i
                    P.mm(ps[:, 128 * i:128 * (i + 1)], KT[fp, 128 * n:128 * (n + 1)], QT[fp, 128 * n:128 * (n + 1)],
                         start=False, stop=(i == 3))
                P.act(pt[:, 0:512], ps, AF.Exp, scale=0.125)
                bank = ringS.next(); ps = pbf(bank)
                P.mm(ps, IDENTb, MBB4b, start=True, stop=False)
                for i in range(4):
                    n = 4 * c + i
                    if n == 0:
                        continue
                    P.mm(ps[:, 128 * i:128 * (i + 1)], KT[fp, 128 * (n - 1):128 * n], QT[fp, 128 * n:128 * (n + 1)],
                         start=False, stop=(i == 3))
                P.act(pt[:, 512:1024], ps, AF.Exp, scale=0.125)
                pt1 = pt
                pt = PTbuf.next()
                bank = ringS.next(); ps = pbf(bank)
                P.mm(ps, IDENTb, MBA4b, start=True, stop=False)
                for r in range(4):
                    sl = slice(512 * c + r, 512 * (c + 1), 4)
                    P.mm(ps[:, 128 * r:128 * (r + 1)], KT[fp, sl], QT[fp, sl], start=False, stop=(r == 3))
                P.act(pt[:, 0:512], ps, AF.Exp, scale=0.125)
                if c > 0:
                    bank = ringS.next(); ps = pbf(bank)
                    P.mm(ps, IDENTb, MBB4b, start=True, stop=False)
                    for r in range(4):
                        sl = slice(512 * c + r, 512 * (c + 1), 4)
                        slk = slice(512 * (c - 1) + r, 512 * c, 4)
                        P.mm(ps[:, 128 * r:128 * (r + 1)], KT[fp, slk], QT[fp, sl], start=False, stop=(r == 3))
                    P.act(pt[:, 512:1024], ps, AF.Exp, scale=0.125)
                for i in range(4):
                    n = 4 * c + i
                    pv(acc[:, 128 * i:128 * (i + 1)], VA[:, n, e, :], pt1[:, 128 * i:128 * (i + 1)])
                    if n > 0:
                        pv(acc[:, 128 * i:128 * (i + 1)], VA[:, n - 1, e, :], pt1[:, 512 + 128 * i:512 + 128 * (i + 1)])
                for r in range(4):
                    pv(acc[:, r:512:4], VA[:, 16 + 4 * r + c, e, :], pt[:, 128 * r:128 * (r + 1)])
                    if c > 0:
                        pv(acc[:, r:512:4], VA[:, 16 + 4 * r + c - 1, e, :], pt[:, 512 + 128 * r:512 + 128 * (r + 1)])
                for r in range(16):
                    pv(acc[:, r:512:16], VA[:, 32 + r, e, :], PT3[:, r, 32 * c:32 * (c + 1)], last=(r == 15))
                num = slice(64 * e, 64 * e + 64)
                den = slice(64 * (1 - e), 64 * (1 - e) + 64)
                rd = rdenb[c % 2]
                P.recip(rd[den, :], acc[den, :])
                P.tt("dve", CT[num, 4 + j, 512 * c:512 * (c + 1)], acc[num, :], rd[den, :], ALU.mult)

    PROJ_V()
    for j in range(4):
        PROJ_B(j)
        ATTN(j)
    dump("CTb", CT[:, 4:8, :].rearrange("p k t -> p (k t)"), [128, 4 * S])
    if stage == "B":
        return finish(nc, P, out_d, dump_tags)

    bF = Bump(A, SCR_LO, SCR_HI)
    h2T = A.view(32768, [8, 1024], BF16)
    woutb = A.view(32768 + 16384, [2, 8, 512], BF16)
    X1 = bF.alloc([8, D])
    wu = [bF.alloc([2, 8, 512], BF16) for _ in range(2)]
    wd = [bF.alloc([4, D], BF16) for _ in range(2)]
    aTb = [bF.alloc([4, 1024], BF16) for _ in range(2)]
    rawF = [[bF.alloc([520]) for _ in range(2)] for _ in range(2)]
    accF = [[bF.alloc([512]) for _ in range(2)] for _ in range(2)]
    sgF = [bF.alloc([512]) for _ in range(2)]
    hbF = [bF.alloc([D], BF16), accF[1][1].bitcast(BF16)]
    junkF = sgF[0].bitcast(BF16)
    ssF = bF.alloc([8]); rsF = bF.alloc([8]); ssO = bF.alloc([8]); rsO = bF.alloc([8])
    for c2 in range(2):
        P.dma("pool", woutb[:, c2, :, :], wout_v[:, :, 512 * c2:512 * (c2 + 1)], "wout%d" % c2)
    P.dma("sp", NW1, n2_d, "c_nw1")
    P.dma("sp", NW2, nf_d, "c_nw2")
    groups = [list(range(g0, min(g0 + 4, 22))) for g0 in range(0, 22, 4)]
    out_tags = ["out%d" % i for i in range(8)]
    items = [(H, gi) for H in range(2) for gi in range(len(groups))]

    def load_wu(k):
        H, gi = items[k]
        grp = groups[gi]; g0 = grp[0]; npair = len(grp); slot = k % 2
        P.dma("pool", wu[slot][:, 0, :, 0:128 * npair], wup_v[:, :, 128 * g0:128 * (g0 + npair)], "wug%d" % slot)
        P.dma("pool", wu[slot][:, 1, :, 0:128 * npair],
              wup_v[:, :, 2816 + 128 * g0:2816 + 128 * (g0 + npair)], "wuu%d" % slot)

    def load_wd(k):
        H, gi = items[k]
        grp = groups[gi]; g0 = grp[0]; npair = len(grp); slot = k % 2
        P.dma("pool", wd[slot][:, 0:npair, :], wdn_v[:, g0:g0 + npair, :], "wd%d" % slot)

    def PRO(H):
        scr = [(junkF, ssF[:, i8:i8 + 1], rsF[:, i8:i8 + 1], hbF[i8 % 2]) for i8 in range(8)]

        def st_a(i8):
            i = 8 * H + i8
            P.dma("sp", X1[:, i8, :], x_d[128 * i:128 * (i + 1), :], "xf%d" % i8)
            b0 = 2 * (i8 % 2)
            for h2 in range(2):
                for kc in range(8):
                    P.mm(pbf(b0 + h2), CT[:, kc, 128 * i:128 * (i + 1)], woutb[:, h2, kc, :],
                         start=(kc == 0), stop=(kc == 7))
            for h2 in range(2):
                P.tt("dve", X1[:, i8, 512 * h2:512 * (h2 + 1)], X1[:, i8, 512 * h2:512 * (h2 + 1)], pbf(b0 + h2), ALU.add)
            if i == 0:
                dump("X1", X1[:, 0, :], [128, D])
            rmsnorm_stage1a(X1[:, i8, :], scr[i8])

        for t in range(10):
            if t < 8:
                st_a(t)
            if 1 <= t <= 8:
                rmsnorm_stage1b(X1[:, t - 1, :], NW1, scr[t - 1])
            if t >= 2:
                rmsnorm_stage2(h2T, 128 * (t - 2), scr[t - 2], 4 + (t % 2))

    upar = [0]

    def UGEN(k):
        H, gi = items[k]
        grp = groups[gi]; slot = k % 2; aT = aTb[k % 2]
        for p, g in enumerate(grp):
            for tb in range(2):
                par = upar[0]
                upar[0] ^= 1
                cols = slice(512 * tb, 512 * (tb + 1))
                banks = (4, 5) if par == 0 else (6, 7)
                accs = []
                for gu in range(2):
                    ps = pbf(banks[gu])
                    for kc in range(8):
                        P.mm(ps, wu[slot][:, gu, kc, 128 * p:128 * (p + 1)], h2T[:, kc, cols],
                             start=(kc == 0), stop=(kc == 7))
                    cc = g + 22 * gu
                    raw = rawF[par][gu]; acc = accF[par][gu]
                    P.copy("pool", raw[:, 0:2], HALOF[:, cc, :])
                    P.copy("act", raw[:, 2:514], ps)
                    P.copy("pool", HALOF[:, cc, :], raw[:, 512:514])
                    P.act(acc, ps, AF.Identity, scale=cwF[:, cc, 2:3])
                    P.stt(acc, raw[:, 1:513], cwF[:, cc, 1:2], acc, ALU.mult, ALU.add)
                    P.stt(acc, raw[:, 0:512], cwF[:, cc, 0:1], acc, ALU.mult, ALU.add)
                    accs.append(acc)
                sg = sgF[par]
                P.act(sg, accs[0], AF.Silu)
                P.tt("pool", aT[:, p, cols], sg, accs[1], ALU.mult)
                yield

    def DGEN(k):
        H, gi = items[k]
        grp = groups[gi]; slot = k % 2; aT = aTb[k % 2]; npair = len(grp)
        last = (gi == len(groups) - 1)
        dpend = [None]
        for i8 in range(8):
            i = 8 * H + i8
            b0 = 2 * (i8 % 2)
            for h2 in range(2):
                for p in range(npair):
                    P.mm(pbf(b0 + h2), aT[:, p, 128 * i8:128 * (i8 + 1)], wd[slot][:, p, 512 * h2:512 * (h2 + 1)],
                         start=(p == 0), stop=(p == npair - 1))
            for h2 in range(2):
                P.tt("dve", X1[:, i8, 512 * h2:512 * (h2 + 1)], X1[:, i8, 512 * h2:512 * (h2 + 1)],
                     pbf(b0 + h2), ALU.add)
            if last:
                P.act(junkF, X1[:, i8, :], AF.Square, accum_out=ssO[:, i8:i8 + 1])
                P.act(rsO[:, i8:i8 + 1], ssO[:, i8:i8 + 1], AF.Ln, bias=EPS, scale=1.0 / D)
                P.act(rsO[:, i8:i8 + 1], rsO[:, i8:i8 + 1], AF.Exp, scale=-0.5)
                if dpend[0] is not None:
                    dpend[0]()

                def _fin(i8=i8, i=i):
                    P.stt(X1[:, i8, :], X1[:, i8, :], rsO[:, i8:i8 + 1], NW2, ALU.mult, ALU.mult)
                    P.dma("sp", out_d[128 * i:128 * (i + 1), :], X1[:, i8, :], "out%d" % i8)
                dpend[0] = _fin
            yield
        if last and dpend[0] is not None:
            dpend[0]()
            dpend[0] = None
            yield

    def rr(gens):
        gens = list(gens)
        live = [True] * len(gens)
        while any(live):
            for gi_, g_ in enumerate(gens):
                if live[gi_]:
                    try:
                        next(g_)
                    except StopIteration:
                        live[gi_] = False

    load_wu(0)
    prevD = None
    for k in range(len(items)):
        H, gi = items[k]
        load_wd(k)
        if k + 1 < len(items):
            load_wu(k + 1)
        if gi == 0:
            if prevD is not None:
                rr([prevD])
                prevD = None
            PRO(H)
        u = UGEN(k)
        rr([u] if prevD is None else [u, prevD])
        prevD = DGEN(k)
    rr([prevD])
    return finish(nc, P, out_d, dump_tags + out_tags)


def finish(nc, P, out_d, dump_tags):
    tags = list(dump_tags)
    P.emit(final_dma_tags=tags)
    return nc


def prep_inputs(inp):
    f = lambda a: np.ascontiguousarray(np.asarray(a, dtype=np.float32))
    x = f(inp["x"])
    rep = lambda v: np.ascontiguousarray(np.broadcast_to(f(v).reshape(1, -1), (128, f(v).size)))
    cwa = f(inp["conv_qkv_w"])[0]
    cwA = np.ascontiguousarray(cwa.T.reshape(12, 128, 4).transpose(1, 0, 2).reshape(128, 48))
    cwf = f(inp["ffn_conv_w"])[0]
    cwF = np.ascontiguousarray(cwf.T.reshape(44, 128, 3).transpose(1, 0, 2).reshape(128, 132))
    shared = {
        "w_in": f(inp["w_in"])[0], "w_out": f(inp["w_out"])[0], "w_up": f(inp["w_up"])[0],
        "w_down": f(inp["w_down"])[0],
        "n1rep": rep(inp["norm1_w"]), "n2rep": rep(inp["norm2_w"]), "nfrep": rep(inp["final_norm_w"]),
        "cwA": cwA, "cwF": cwF, "gnwrep": rep(inp["gdn_norm_w"]),
        "dtbrep": np.ascontiguousarray(np.tile(rep(inp["dt_bias"]), (1, 16))),
        "alogrep": np.ascontiguousarray(np.tile(rep(inp["a_log"]), (1, 16))),
        "consts": make_consts(),
    }
    maps = []
    for b in range(x.shape[0]):
        m = dict(shared)
        m["x"] = np.ascontiguousarray(x[b])
        maps.append(m)
    return maps


def kernel(**inputs):
    maps = prep_inputs(inputs)
    nc = build("full")
    res = run_bass_kernel_spmd(nc, maps, core_ids=list(range(8)))
    out = np.stack([np.asarray(r["out"], dtype=np.float32) for r in res.results], 0)
    return out
```

```python
import contextlib
import numpy as np
import concourse.bass as bass
import concourse.mybir as mybir
from concourse.bass_utils import run_bass_kernel_spmd

F32 = mybir.dt.float32
BF16 = mybir.dt.bfloat16
AF = mybir.ActivationFunctionType
ALU = mybir.AluOpType
AX = mybir.AxisListType

S = 2048
D = 1024
NT = 16
EPS = 1e-6
NEG = -30000.0
ENGS = ("pe", "act", "dve", "pool", "sp")


def _rect(ap):
    t = ap.tensor
    if str(ap.space) == "PSUM":
        return (t.name, 0, 128, 0, 2048)
    esz = mybir.dt.size(ap.dtype)
    pstride = 1
    for s in tuple(t.shape)[1:]:
        pstride *= s
    p0 = ap.start_partition()
    p1 = p0 + ap.partition_size()
    f0 = ap.offset - p0 * pstride
    ext = 0
    for (st, cnt) in tuple(ap.ap)[1:]:
        ext += abs(st) * (cnt - 1)
    return (t.name, p0, p1, f0 * esz, (f0 + ext + 1) * esz)


class _Op:
    __slots__ = ("eng", "fn", "idx", "is_dma", "tag", "waits", "signal", "sigval")


class Prog:
    def __init__(self, nc):
        self.nc = nc
        self.ops = {e: [] for e in ENGS}
        self.track = {}
        self.waited = {e: {} for e in ENGS}
        self.tagcount = {}

    def _add(self, eng, fn, reads, writes, is_dma=False, tag=None):
        op = _Op()
        op.eng = eng; op.fn = fn; op.is_dma = is_dma; op.tag = tag
        op.signal = False; op.sigval = None
        op.idx = len(self.ops[eng])
        op.waits = []
        deps = {}
        rrects = [_rect(a) for a in reads if a is not None and str(a.space) != "DRAM"]
        wrects = [_rect(a) for a in writes if a is not None and str(a.space) != "DRAM"]
        prects = [r for r in rrects if r[4] == 2048 and r[0].startswith("pb") and r not in wrects]
        for (nm, p0, p1, f0, f1) in rrects:
            for rec in self.track.get(nm, ()):
                if rec[5] == 1 and rec[0] < p1 and p0 < rec[1] and rec[2] < f1 and f0 < rec[3]:
                    deps[id(rec[4])] = (rec[4], True)
        for (nm, p0, p1, f0, f1) in wrects + prects:
            for rec in self.track.get(nm, ()):
                if rec[0] < p1 and p0 < rec[1] and rec[2] < f1 and f0 < rec[3]:
                    k = id(rec[4])
                    if k not in deps:
                        deps[k] = (rec[4], False)
        need = {}
        for (d, raw) in deps.values():
            if d.is_dma:
                key = ("dma", d.tag)
                val = self.tagcount[d.tag]
                if need.get(key, 0) < val:
                    need[key] = val
            else:
                if d.eng == eng and eng == "pe":
                    continue
                key = ("eng", d.eng)
                cur = need.get(key)
                if cur is None or cur.idx < d.idx:
                    need[key] = d
        w = self.waited[eng]
        for key, v in need.items():
            if key[0] == "dma":
                if w.get(key, 0) >= v:
                    continue
                w[key] = v
                op.waits.append((key, v))
            else:
                if w.get(key, -1) >= v.idx:
                    continue
                w[key] = v.idx
                v.signal = True
                op.waits.append((key, v))
        if is_dma:
            self.tagcount[tag] = self.tagcount.get(tag, 0) + 16
        for (nm, p0, p1, f0, f1) in wrects:
            lst = self.track.setdefault(nm, [])
            lst[:] = [r for r in lst if not (p0 <= r[0] and r[1] <= p1 and f0 <= r[2] and r[3] <= f1)]
            lst.append([p0, p1, f0, f1, op, 1])
        for (nm, p0, p1, f0, f1) in prects:
            self.track[nm] = [[p0, p1, f0, f1, op, 2]]
        for (nm, p0, p1, f0, f1) in rrects:
            if nm.startswith("pb"):
                continue
            lst = self.track.setdefault(nm, [])
            done = False
            for r in lst:
                if r[5] == 0 and r[4].eng == eng and (not r[4].is_dma) and (not is_dma) \
                        and r[0] == p0 and r[1] == p1 and r[2] == f0 and r[3] == f1:
                    r[4] = op
                    done = True
                    break
            if not done:
                lst.append([p0, p1, f0, f1, op, 0])
        self.ops[eng].append(op)
        return op

    def dma(self, q, out, in_, tag):
        return self._add(q, lambda e: e.dma_start(out=out, in_=in_), [in_], [out], is_dma=True, tag=tag)

    def mm(self, out, lhsT, rhs, start=True, stop=True, **kw):
        rd = [lhsT, rhs] + ([] if start else [out])
        return self._add("pe", lambda e: e.matmul(out, lhsT, rhs, start=start, stop=stop, **kw), rd, [out])

    def tr(self, out, in_, ident):
        return self._add("pe", lambda e: e.transpose(out, in_, ident), [in_, ident], [out])

    def act(self, out, in_, func, bias=None, scale=None, accum_out=None):
        kw = {}
        rd = [in_]
        if bias is not None:
            kw["bias"] = bias
            if not isinstance(bias, (int, float)):
                rd.append(bias)
        if scale is not None:
            kw["scale"] = scale
            if not isinstance(scale, (int, float)):
                rd.append(scale)
        wr = [out]
        if accum_out is not None:
            kw["accum_out"] = accum_out
            wr.append(accum_out)
        return self._add("act", lambda e: e.activation(out, in_, func, **kw), rd, wr)

    def tt(self, eng, out, in0, in1, op):
        return self._add(eng, lambda e: e.tensor_tensor(out, in0, in1, op), [in0, in1], [out])

    def ts(self, eng, out, in0, s1, s2, op0, op1=None):
        rd = [in0] + [s for s in (s1, s2) if s is not None and not isinstance(s, (int, float))]
        kw = {}
        if op1 is not None:
            kw["op1"] = op1
        return self._add(eng, lambda e: e.tensor_scalar(out, in0, s1, s2, op0, **kw), rd, [out])

    def stt(self, out, in0, scalar, in1, op0, op1):
        rd = [in0, in1] + ([] if isinstance(scalar, (int, float)) else [scalar])
        return self._add("dve", lambda e: e.scalar_tensor_tensor(out, in0, scalar, in1, op0, op1), rd, [out])

    def copy(self, eng, out, in_):
        if eng == "act":
            return self._add("act", lambda e: e.copy(out, in_), [in_], [out])
        return self._add(eng, lambda e: e.tensor_copy(out, in_), [in_], [out])

    def memset(self, eng, ap, val):
        return self._add(eng, lambda e: e.memset(ap, val), [], [ap])

    def recip(self, out, in_):
        return self._add("dve", lambda e: e.reciprocal(out, in_), [in_], [out])

    def rsum(self, out, in_):
        return self._add("dve", lambda e: e.tensor_reduce(out, in_, AX.X, ALU.add), [in_], [out])

    def emit(self, final_dma_tags=()):
        nc = self.nc
        for e in ENGS:
            c = 0
            for op in self.ops[e]:
                if op.signal and not op.is_dma:
                    c += 1
                    op.sigval = c
        with contextlib.ExitStack() as st:
            esem = {e: st.enter_context(nc.semaphore("s_" + e)) for e in ENGS}
            dsem = {t: st.enter_context(nc.semaphore("d_%d" % i)) for i, t in enumerate(self.tagcount)}
            block = st.enter_context(nc.Block())
            engobj = {"pe": block.tensor, "act": block.scalar, "dve": block.vector,
                      "pool": block.gpsimd, "sp": block.sync}

            def make(ename):
                def body(eng):
                    for op in self.ops[ename]:
                        for (key, v) in op.waits:
                            if key[0] == "dma":
                                eng.wait_ge(dsem[key[1]], v)
                            else:
                                eng.wait_ge(esem[key[1]], v.sigval)
                        ins = op.fn(eng)
                        if op.is_dma:
                            ins.then_inc(dsem[op.tag], 16)
                        elif op.signal:
                            ins.then_inc(esem[ename], 1)
                    if ename == "sp":
                        for t in final_dma_tags:
                            eng.wait_ge(dsem[t], self.tagcount[t])
                return body

            for e in ENGS:
                engobj[e](make(e))
        return nc


def _prod(shape):
    n = 1
    for s in shape:
        n *= s
    return n


class Arena:
    def __init__(self, nc, nbytes):
        self.t = nc.alloc_sbuf_tensor("arena", [128, nbytes // 4], F32)
        self.nbytes = nbytes

    def view(self, off, shape, dt):
        n = _prod(shape)
        esz = 4 if dt == F32 else 2
        assert off % 4 == 0 and (n * esz) % 4 == 0 and off + n * esz <= self.nbytes, (off, shape)
        ap = self.t[:, off // 4:(off + n * esz) // 4]
        if dt != F32:
            ap = ap.bitcast(dt)
        if len(shape) > 1:
            names = " ".join("d%d" % i for i in range(len(shape)))
            kw = {"d%d" % i: shape[i] for i in range(1, len(shape))}
            ap = ap.rearrange("p (%s) -> p %s" % (names, names), **kw)
        return ap


class Bump:
    def __init__(self, arena, lo, hi):
        self.a = arena; self.lo = lo; self.hi = hi; self.cur = lo

    def alloc(self, shape, dt=F32):
        esz = 4 if dt == F32 else 2
        n = (_prod(shape) * esz + 31) // 32 * 32
        off = self.cur
        self.cur += n
        assert self.cur <= self.hi, ("arena overflow", self.cur, self.hi)
        return self.a.view(off, shape, dt)


class Ring:
    def __init__(self, items):
        self.items = list(items); self.i = 0

    def next(self):
        x = self.items[self.i % len(self.items)]
        self.i += 1
        return x


_CNAMES = ["IDENT", "M1", "M2", "TRI", "SUF", "IND0", "IND1", "S32", "OFF", "ONES"]
_CB = {"MB64": 512, "MBA4": 512, "MBB4": 512}
NCF = 128 * len(_CNAMES)
NCONST = NCF + 512 * 3


def make_consts():
    i = np.arange(128)[:, None]
    j = np.arange(128)[None, :]
    same = (i // 64) == (j // 64)
    c = {}
    c["IDENT"] = (i == j)
    c["M1"] = (i > j)
    c["M2"] = (i <= j)
    c["TRI"] = (i <= j) & same
    c["SUF"] = (i > j) & same
    c["IND0"] = (i < 64) & (j >= 0)
    c["IND1"] = (i >= 64) & (j >= 0)
    c["S32"] = (i < j) & ((i // 32) == (j // 32))
    c["OFF"] = same & ((i % 64) < 32) & ((j % 64) >= 32)
    c["ONES"] = np.ones((128, 128), bool)
    cols = [c[n].astype(np.float32) for n in _CNAMES]
    mb64 = np.where((i <= j) & same, 0.0, NEG).astype(np.float32)
    mba = np.where(i <= j, 0.0, NEG).astype(np.float32)
    mbb = np.where(i >= j, 0.0, NEG).astype(np.float32)
    cols += [np.tile(mb64, (1, 4)), np.tile(mba, (1, 4)), np.tile(mbb, (1, 4))]
    return np.ascontiguousarray(np.concatenate(cols, 1))


def build(stage="full", dumps=()):
    nc = bass.Bass("TRN2", target_bir_lowering=False)
    P = Prog(nc)
    dumps = set(dumps)
    dump_tags = []

    def din(name, shape):
        return nc.dram_tensor(name, list(shape), F32, kind="ExternalInput").ap()

    x_d = din("x", [S, D])
    win_d = din("w_in", [D, 3592])
    wout_d = din("w_out", [D, D])
    wup_d = din("w_up", [D, 5632])
    wdn_d = din("w_down", [2816, D])
    n1_d = din("n1rep", [128, D])
    n2_d = din("n2rep", [128, D])
    nf_d = din("nfrep", [128, D])
    cwa_d = din("cwA", [128, 48])
    cwf_d = din("cwF", [128, 132])
    gnw_d = din("gnwrep", [128, 128])
    dtb_d = din("dtbrep", [128, 64])
    alog_d = din("alogrep", [128, 64])
    cst_d = din("consts", [128, NCONST])
    out_d = nc.dram_tensor("out", [S, D], F32, kind="ExternalOutput").ap()

    def dump(name, sb_ap, shape):
        if name not in dumps:
            return
        d = nc.dram_tensor("dbg_" + name, list(shape), sb_ap.dtype, kind="ExternalOutput").ap()
        tg = "dbg_" + name
        P.dma("sp", d, sb_ap, tg)
        dump_tags.append(tg)

    win_v = win_d.rearrange("(k p) c -> p k c", p=128)
    wout_v = wout_d.rearrange("(k p) c -> p k c", p=128)
    wup_v = wup_d.rearrange("(k p) c -> p k c", p=128)
    wdn_v = wdn_d.rearrange("(k p) c -> p k c", p=128)

    ARENA_BYTES = 207872
    A = Arena(nc, ARENA_BYTES)
    pb = [nc.alloc_psum_tensor("pb%d" % i, [128, 512], F32) for i in range(8)]

    def pbf(i):
        return pb[i][:]

    def pbb(i):
        return pb[i][:].bitcast(BF16)

    CT = A.view(0, [8, S], BF16)
    hT = A.view(32768, [8, S], BF16)
    pers = Bump(A, 65536, 86016)
    CF = pers.alloc([NCF])
    CBm = pers.alloc([1536 + 256], BF16)
    cwA = pers.alloc([12, 4])
    cwF = pers.alloc([44, 3])
    HALOA = pers.alloc([12, 3])
    HALOF = pers.alloc([44, 2])
    GNW = pers.alloc([128])
    NW1 = pers.alloc([D])
    NW2 = pers.alloc([D])
    SCR_LO = 86016
    SCR_HI = ARENA_BYTES

    def cf(name):
        k = _CNAMES.index(name)
        return CF[:, 128 * k:128 * (k + 1)]

    IDENT = cf("IDENT"); M1 = cf("M1"); M2 = cf("M2"); TRI = cf("TRI"); SUF = cf("SUF")
    IND0 = cf("IND0"); IND1 = cf("IND1"); S32 = cf("S32"); OFFM = cf("OFF"); ONES = cf("ONES")
    MB64b = CBm[:, 0:512]; MBA4b = CBm[:, 512:1024]; MBB4b = CBm[:, 1024:1536]
    IDENTb = CBm[:, 1536:1664]; ONESb = CBm[:, 1664:1792]

    P.dma("sp", CF, cst_d[:, 0:NCF], "c_cf")
    ctmp = A.view(16384 + 8192, [1536], F32)
    P.dma("sp", ctmp, cst_d[:, NCF:NCONST], "c_tmp")
    P.copy("dve", CBm[:, 0:1536], ctmp)
    P.copy("dve", IDENTb, IDENT)
    P.copy("dve", ONESb, ONES)
    P.dma("sp", cwA.rearrange("p a b -> p (a b)"), cwa_d, "c_cwa")
    P.dma("sp", cwF.rearrange("p a b -> p (a b)"), cwf_d, "c_cwf")
    P.dma("sp", GNW, gnw_d, "c_gnw")
    P.dma("sp", NW1, n1_d, "c_nw1")
    P.memset("pool", HALOA.rearrange("p a b -> p (a b)"), 0.0)
    P.memset("pool", HALOF.rearrange("p a b -> p (a b)"), 0.0)

    def bc_last(ap, n):
        return ap.unsqueeze(2).broadcast_to([128, ap.shape[1], n])

    def bc_mid(ap, n):
        return ap.unsqueeze(1).broadcast_to([128, n, ap.shape[1]])

    def rmsnorm_stage1(src_tile, wtile, scr):
        junk, ssv, rst, hb = scr
        P.act(junk, src_tile, AF.Square, accum_out=ssv)
        P.act(rst, ssv, AF.Ln, bias=EPS, scale=1.0 / D)
        P.act(rst, rst, AF.Exp, scale=-0.5)
        P.stt(hb, src_tile, rst, wtile, ALU.mult, ALU.mult)

    def rmsnorm_stage1a(src_tile, scr):
        junk, ssv, rst, hb = scr
        P.act(junk, src_tile, AF.Square, accum_out=ssv)
        P.act(rst, ssv, AF.Ln, bias=EPS, scale=1.0 / D)
        P.act(rst, rst, AF.Exp, scale=-0.5)

    def rmsnorm_stage1b(src_tile, wtile, scr):
        junk, ssv, rst, hb = scr
        P.stt(hb, src_tile, rst, wtile, ALU.mult, ALU.mult)

    def rmsnorm_stage2(dstT, col0, scr, bank):
        hb = scr[3]
        psT = pbb(bank)
        for kc in range(8):
            P.tr(psT[:, 128 * kc:128 * (kc + 1)], hb[:, 128 * kc:128 * (kc + 1)], IDENTb)
        P.copy("act", dstT[:, :, col0:col0 + 128], psT.rearrange("p (k t) -> p k t", k=8))

    bA = Bump(A, 0, 16384)
    xbuf = [bA.alloc([D]) for _ in range(3)]
    junkA = bA.alloc([D], BF16)
    hbuf = [NW2.bitcast(BF16)[:, 0:D], NW2.bitcast(BF16)[:, D:2 * D]]
    ssA = bA.alloc([NT])
    rsA = bA.alloc([NT])
    scrA = [(junkA, ssA[:, i:i + 1], rsA[:, i:i + 1], hbuf[i % 2]) for i in range(NT)]
    for i in range(NT + 2):
        if i < NT:
            xt = xbuf[i % 3]
            P.dma("sp", xt, x_d[128 * i:128 * (i + 1), :], "xa%d" % (i % 3))
            rmsnorm_stage1a(xt, scrA[i])
        if 1 <= i <= NT:
            rmsnorm_stage1b(xbuf[(i - 1) % 3], NW1, scrA[i - 1])
        if i >= 2:
            rmsnorm_stage2(hT, 128 * (i - 2), scrA[i - 2], 6 + (i % 2))
    dump("hT", hT.rearrange("p k t -> p (k t)"), [128, 8 * S])
    if stage == "A":
        return finish(nc, P, out_d, dump_tags)

    bG = Bump(A, SCR_LO, SCR_HI)
    bC = Bump(A, 16384, 32768)
    wz = bC.alloc([8, 512], BF16)
    qkvT = [dict(q=bG.alloc([4, 512], BF16), k=bG.alloc([4, 512], BF16), v=bG.alloc([4, 512], BF16))
            for _ in range(4)]
    szgb = [bG.alloc([4, 512], BF16), bG.alloc([4, 512], BF16), bG.alloc([4, 512], BF16), bC.alloc([4, 512], BF16)]
    BA = bG.alloc([NT, 8])
    sc_x = bG.alloc([64]); sc_mx = bG.alloc([64]); sc_mn = bG.alloc([64])
    dtb = bG.alloc([64]); negA = bG.alloc([64])
    gS = bG.alloc([64]); betaS = bG.alloc([64]); gamS = bG.alloc([64]); kesS = bG.alloc([64])
    gendS = bG.alloc([2, 64])
    wba = bG.alloc([8, 8], BF16)
    Sst = bG.alloc([4, 128]); Sb = bG.alloc([4, 128], BF16); ub = bG.alloc([4, 128], BF16)
    Obuf = [bC.alloc([4, 128]) for _ in range(2)]
    sqO = bG.alloc([4, 128], BF16); oab = sqO
    kesLo = bG.alloc([64]); kesHi = bG.alloc([64])
    ssq = bG.alloc([4]); rsq = bG.alloc([4])
    ov0 = bG.cur
    gset = []
    for _ in range(2):
        gset.append(dict(
            G1m=bG.alloc([4, 128]),
            DECT=bG.alloc([4, 128], BF16), Uall=bG.alloc([4, 128], BF16), Eu=bG.alloc([4, 128], BF16),
            PTa=bG.alloc([4, 128], BF16), PTb=bG.alloc([4, 128], BF16),
            UL0=bG.alloc([2, 4, 128], BF16), PWA=bG.alloc([2, 4, 128], BF16), PWB=bG.alloc([2, 4, 128], BF16),
            TTb=bG.alloc([4, 128], BF16), vb=bG.alloc([4, 128], BF16), gk=bG.alloc([4, 128], BF16)))
    scanop = []
    for _ in range(3):
        scanop.append(dict(KLO=bG.alloc([4, 128], BF16), KHI=bG.alloc([4, 128], BF16), ATT=bG.alloc([4, 128], BF16),
                           QD=bG.alloc([4, 128], BF16), WKT=bG.alloc([4, 128], BF16),
                           UV=bG.alloc([4, 128], BF16)))
    bO = Bump(A, ov0, bG.cur)
    wqkva = bO.alloc([3, 8, 512], BF16)
    rawb = [bO.alloc([520], BF16) for _ in range(8)]
    dgcb = [bO.alloc([4, 128], BF16) for _ in range(8)]
    sqbs = [bO.alloc([512], BF16) for _ in range(3)]
    rtbs = [bO.alloc([512]) for _ in range(3)]
    for c3 in range(3):
        P.dma("pool", wqkva[:, c3, :, :], win_v[:, :, 512 * c3:512 * (c3 + 1)], "wqkva%d" % c3)
    P.dma("pool", wba, win_v[:, :, 2048:2056], "wba")
    P.dma("pool", wz, win_v[:, :, 1536:2048], "wz")

    ringG = Ring([0, 1, 2, 3, 4, 5, 6, 7])

    g3 = gS.rearrange("p (n h) -> p n h", h=4)
    beta3 = betaS.rearrange("p (n h) -> p n h", h=4)
    gam3 = gamS.rearrange("p (n h) -> p n h", h=4)
    kesLo3 = kesLo.rearrange("p (n h) -> p n h", h=4)
    kesHi3 = kesHi.rearrange("p (n h) -> p n h", h=4)
    gend4 = gendS.rearrange("p a (n h) -> p a n h", h=4)

    def SCALARS():
        P.dma("sp", dtb, dtb_d, "c_dtb")
        P.dma("sp", negA, alog_d, "c_alog")
        P.act(negA, negA, AF.Exp)
        P.ts("dve", negA, negA, -1.0, None, ALU.mult)
        bk = ringG.next()
        psBA = pbf(bk)[:, 0:128].rearrange("p (n c) -> p n c", c=8)
        for i in range(NT):
            for kc in range(8):
                P.mm(psBA[:, i, :], hT[:, kc, 128 * i:128 * (i + 1)], wba[:, kc, :], start=(kc == 0), stop=(kc == 7))
        P.copy("act", BA, psBA)
        x3 = sc_x.rearrange("p (n h) -> p n h", h=4)
        P.tt("dve", x3, BA[:, :, 4:8], dtb.rearrange("p (n h) -> p n h", h=4), ALU.add)
        P.ts("dve", sc_mx, sc_x, 0.0, None, ALU.max)
        P.ts("dve", sc_mn, sc_x, 0.0, None, ALU.min)
        P.tt("dve", sc_mn, sc_mn, sc_mx, ALU.subtract)
        P.act(sc_mn, sc_mn, AF.Exp)
        P.act(sc_mn, sc_mn, AF.Ln, bias=1.0)
        P.tt("dve", sc_mx, sc_mx, sc_mn, ALU.add)
        P.tt("dve", gS, sc_mx, negA, ALU.mult)
        P.act(betaS.rearrange("p (n h) -> p n h", h=4), BA[:, :, 0:4], AF.Sigmoid)
        bk = ringG.next()
        psg = pbf(bk)
        P.mm(psg[:, 0:64], TRI, gS)
        P.mm(psg[:, 64:128], SUF, gS)
        P.mm(psg[:, 128:192], IND0, gS)
        P.mm(psg[:, 192:256], IND1, gS)
        P.act(gamS, psg[:, 0:64], AF.Exp)
        P.act(kesS, psg[:, 64:128], AF.Exp)
        P.tt("dve", kesS, kesS, betaS, ALU.mult)
        P.act(gendS.rearrange("p a b -> p (a b)"), psg[:, 128:256], AF.Exp)
        dump("gS", gS, [128, 64]); dump("betaS", betaS, [128, 64]); dump("gamS", gamS, [128, 64])
        dump("kesS", kesS, [128, 64]); dump("gendS", gendS.rearrange("p a b -> p (a b)"), [128, 128])
        P.ts("dve", kesLo, kesS, IND0[:, 0:1], None, ALU.mult)
        P.ts("dve", kesHi, kesS, IND1[:, 0:1], None, ALU.mult)


    def G1(m):
        t0 = 512 * m
        o = qkvT[m]
        pend = [None]
        for c in range(12):
            ps = pbf(ringG.next())
            for kc in range(8):
                P.mm(ps, wqkva[:, c // 4, kc, 128 * (c % 4):128 * (c % 4 + 1)], hT[:, kc, t0:t0 + 512], start=(kc == 0), stop=(kc == 7))
            raw = rawb[2 * m + c % 2]; dgc = dgcb[2 * m + c % 2]
            for i_ in range(4):
                P.ts("dve", dgc[:, i_, :], IDENTb, cwA[:, c, i_:i_ + 1], None, ALU.mult)
            P.copy("pool", raw[:, 0:3], HALOA[:, c, :])
            P.copy("act", raw[:, 3:515], ps)
            P.copy("pool", HALOA[:, c, :], raw[:, 512:515])
            if pend[0] is not None:
                pend[0]()

            def _conv(raw=raw, dgc=dgc, c=c):
                psC = pbf(ringG.next())
                for i_ in range(4):
                    P.mm(psC, dgc[:, i_, :], raw[:, i_:i_ + 512], start=(i_ == 0), stop=(i_ == 3))
                dst = (o["q"], o["k"], o["v"])[c // 4][:, c % 4, :]
                P.act(dst, psC, AF.Silu)
            pend[0] = _conv
            if c % 3 == 2:
                nz = 4 * m + c // 3
                psZ = pbf(ringG.next())
                for kc in range(8):
                    P.mm(psZ, hT[:, kc, 128 * nz:128 * (nz + 1)], wz[:, kc, :], start=(kc == 0), stop=(kc == 7))
                szg = szgb[m][:, c // 3, :]
                P.act(szg, psZ, AF.Silu)
                P.tt("pool", szg.rearrange("p (h d) -> p h d", h=4), szg.rearrange("p (h d) -> p h d", h=4),
                     bc_mid(GNW, 4), ALU.mult)
            yield
        pend[0]()
        yield
        for c in range(8):
            dst = (o["q"], o["k"])[c // 4][:, c % 4, :]
            sc = 128.0 if c < 4 else 1.0
            sqb = sqbs[(c + m) % 3]; rtb = rtbs[(c + m) % 3]
            P.tt("pool", sqb, dst, dst, ALU.mult)
            psn = pbf(ringG.next())
            P.mm(psn, ONESb, sqb)
            P.act(rtb, psn, AF.Ln, bias=EPS * sc, scale=sc)
            P.act(rtb, rtb, AF.Exp, scale=-0.5)
            P.tt("dve", dst, dst, rtb, ALU.mult)
            yield
        if m == 0:
            dump("qnT0", o["q"].rearrange("p h t -> p (h t)"), [128, 2048])
            dump("knT0", o["k"].rearrange("p h t -> p (h t)"), [128, 2048])
            dump("vsT0", o["v"].rearrange("p h t -> p (h t)"), [128, 2048])

    def G2(n):
        tl = 128 * (n % 4)
        so = scanop[n % 3]
        st = gset[n % 2]
        qnT = qkvT[n // 4]["q"]; knT = qkvT[n // 4]["k"]; vsT = qkvT[n // 4]["v"]
        G1m = st["G1m"]; DECT = st["DECT"]; Uall = st["Uall"]; Eu = st["Eu"]
        UL0 = st["UL0"]; PWA = st["PWA"]; PWB = st["PWB"]
        TTb = st["TTb"]; vb = st["vb"]; gk = st["gk"]
        Du, Dl = UL0[:, 0], UL0[:, 1]
        gam_b = bc_last(gam3[:, n, :], 128)
        keslo_b = bc_last(kesLo3[:, n, :], 128)
        keshi_b = bc_last(kesHi3[:, n, :], 128)
        beta_b = bc_last(beta3[:, n, :], 128)
        g_b = bc_last(g3[:, n, :], 128)

        def bankf():
            return pbf(ringG.next()).rearrange("p (h d) -> p h d", h=4)

        def bankb():
            return pbb(ringG.next())[:, 0:512].rearrange("p (h d) -> p h d", h=4)

        psKV = pbb(ringG.next()).rearrange("p (a h d) -> p a h d", a=2, h=4)
        psK = psKV[:, 0]; psV = psKV[:, 1]
        for h in range(4):
            P.tr(psK[:, h, :], knT[:, h, tl:tl + 128], IDENTb)
            P.tr(psV[:, h, :], vsT[:, h, tl:tl + 128], IDENTb)
        P.tt("dve", gk, psK, gam_b, ALU.mult)
        P.tt("dve", so["KLO"], psK, keslo_b, ALU.mult)
        P.tt("dve", so["KHI"], psK, keshi_b, ALU.mult)
        P.copy("act", vb, psV)
        yield
        P.tt("pool", G1m, bc_mid(M1, 4), g_b, ALU.mult)
        psD = bankf()
        P.mm(psD, IDENTb, MB64b, start=True, stop=False)
        for h in range(4):
            P.mm(psD[:, h, :], G1m[:, h, :], M2, start=False, stop=(h == 3))
        P.act(DECT, psD, AF.Exp)
        P.tt("pool", DECT, DECT, beta_b, ALU.mult)
        yield
        dg = Uall
        P.tt("pool", dg, bc_mid(IDENT, 4), gam_b, ALU.mult)
        psG = bankf()
        for h in range(4):
            P.mm(psG[:, h, :], ONESb, dg[:, h, :])
        P.tt("dve", so["QD"], qnT[:, :, tl:tl + 128], psG, ALU.mult)
        yield
        psKK = bankf(); psQK = bankf()
        for h in range(4):
            P.mm(psKK[:, h, :], knT[:, h, tl:tl + 128], knT[:, h, tl:tl + 128])
        for h in range(4):
            P.mm(psQK[:, h, :], knT[:, h, tl:tl + 128], qnT[:, h, tl:tl + 128])
        P.tt("dve", Uall, psKK, DECT, ALU.mult)
        P.tt("dve", so["ATT"], psQK, DECT, ALU.mult)
        P.tt("pool", Du, Uall, bc_mid(S32, 4), ALU.mult)
        P.tt("pool", Eu, Uall, bc_mid(OFFM, 4), ALU.mult)
        yield
        psT = bankb()
        for h in range(4):
            P.tr(psT[:, h, :], Du[:, h, :], IDENTb)
        P.copy("act", Dl, psT)
        P.tt("pool", st["PTa"], bc_mid(IDENT, 4), Du, ALU.subtract)
        yield
        pw = [UL0, PWA, PWB, PWA, PWB]
        PT, PTn = st["PTa"], st["PTb"]
        for k in range(1, 5):
            cur = pw[k - 1]; nxt = pw[k]
            if k < 4:
                psU = bankf()
                for h in range(4):
                    P.mm(psU[:, h, :], cur[:, 1, h, :], cur[:, 0, h, :])
            psL = bankf()
            for h in range(4):
                P.mm(psL[:, h, :], cur[:, 0, h, :], cur[:, 1, h, :])
            if k > 1:
                ps3 = bankf()
                for h in range(4):
                    P.mm(ps3[:, h, :], cur[:, 1, h, :], PT[:, h, :])
            if k < 4:
                P.copy("act", nxt[:, 0], psU)
            P.copy("act", nxt[:, 1], psL)
            if k > 1:
                P.tt("dve", PTn, PT, ps3, ALU.add)
                PT, PTn = PTn, PT
            yield
        ps3 = bankf()
        for h in range(4):
            P.mm(ps3[:, h, :], PWB[:, 1, h, :], PT[:, h, :])
        P.tt("dve", PTn, PT, ps3, ALU.add)
        PT, PTn = PTn, PT
        yield
        Pm = PWA[:, 0]; XT = DECT
        p1 = bankb()
        for h in range(4):
            P.tr(p1[:, h, :], PT[:, h, :], IDENTb)
        P.copy("act", Pm, p1)
        yield
        p2 = bankf()
        for h in range(4):
            P.mm(p2[:, h, :], Eu[:, h, :], Pm[:, h, :])
        P.copy("act", XT, p2)
        yield
        p3 = bankf()
        for h in range(4):
            P.mm(p3[:, h, :], XT[:, h, :], PT[:, h, :])
        P.tt("dve", TTb, PT, p3, ALU.subtract)
        yield
        p1 = bankf(); p2 = bankf()
        for h in range(4):
            P.mm(p1[:, h, :], TTb[:, h, :], vb[:, h, :])
        for h in range(4):
            P.mm(p2[:, h, :], gk[:, h, :], TTb[:, h, :])
        P.copy("act", so["UV"], p1)
        P.copy("dve", so["WKT"], p2)
        yield

    def SCAN(n):
        so = scanop[n % 3]
        O = Obuf[n % 2]
        for half in range(2):
            r0 = 64 * half
            rs = slice(r0, r0 + 64)
            kend = so["KLO"] if half == 0 else so["KHI"]
            psA = pbf(ringG.next()).rearrange("p (h d) -> p h d", h=4)
            for h in range(4):
                P.mm(psA[:, h, :], so["WKT"][:, h, :], Sb[:, h, :])
            P.tt("dve", ub[rs], so["UV"][rs], psA[rs], ALU.subtract)
            yield
            psS = pbf(ringG.next()).rearrange("p (h d) -> p h d", h=4)
            psO = pbf(ringG.next()).rearrange("p (h d) -> p h d", h=4)
            for h in range(4):
                P.mm(psS[:, h, :], kend[:, h, :], ub[:, h, :])
            for h in range(4):
                P.mm(psO[:, h, :], so["QD"][:, h, :], Sb[:, h, :], start=True, stop=False)
                P.mm(psO[:, h, :], so["ATT"][:, h, :], ub[:, h, :], start=False, stop=True)
            for h in range(4):
                P.stt(Sst[:, h, :], Sst[:, h, :], gend4[:, half, n, h:h + 1], psS[:, h, :], ALU.mult, ALU.add)
            P.copy("act", Sb, Sst)
            P.copy("act", O[rs], psO[rs])
            yield
        if n == 0:
            dump("O0", O.rearrange("p h d -> p (h d)"), [128, 512])
        if n == 15:
            dump("O15", O.rearrange("p h d -> p (h d)"), [128, 512])
        P.act(sqO, O, AF.Square)
        P.rsum(ssq, sqO)
        P.act(rsq, ssq, AF.Ln, bias=EPS, scale=1.0 / 128)
        P.act(rsq, rsq, AF.Exp, scale=-0.5)
        szg = szgb[n // 4][:, n % 4, :].rearrange("p (h d) -> p h d", h=4)
        P.tt("dve", sqO, O, bc_last(rsq, 128), ALU.mult)
        P.tt("dve", oab, sqO, szg, ALU.mult)
        yield
        psT = pbb(ringG.next())[:, 0:512].rearrange("p (h d) -> p h d", h=4)
        for h in range(4):
            P.tr(psT[:, h, :], oab[:, h, :], IDENTb)
        P.copy("act", CT[:, 0:4, 128 * n:128 * (n + 1)], psT)
        yield

    def advance(must, opt):
        live = [True] * len(must)
        while any(live) or any(q[1] > 0 for q in opt):
            for gi, g in enumerate(must):
                if live[gi]:
                    try:
                        next(g)
                    except StopIteration:
                        live[gi] = False
            for q in opt:
                if q[1] > 0:
                    q[1] -= 1
                    try:
                        next(q[0])
                    except StopIteration:
                        q[1] = 0
                        q[2] = True

    P.memset("dve", Sst.rearrange("p h d -> p (h d)"), 0.0)
    P.memset("pool", Sb.rearrange("p h d -> p (h d)"), 0.0)
    P.memset("pool", ub.rearrange("p h d -> p (h d)"), 0.0)
    advance([G1(m_) for m_ in range(4)], [])
    SCALARS()
    g2 = {0: G2(0), 1: G2(1)}
    advance([g2[0]], [[g2[1], 7, False]])
    for n in range(NT):
        must = [SCAN(n)]
        if n + 1 < NT:
            must.append(g2[n + 1])
        opt = []
        if n + 2 < NT:
            g2[n + 2] = G2(n + 2)
            opt.append([g2[n + 2], 7, False])
        advance(must, opt)
    dump("CTa", CT[:, 0:4, :].rearrange("p k t -> p (k t)"), [128, 4 * S])
    if stage == "G":
        return finish(nc, P, out_d, dump_tags)

    bB = Bump(A, SCR_LO, SCR_HI)
    wqkvb = bB.alloc([3, 8, 512], BF16)
    for c3 in range(3):
        P.dma("pool", wqkvb[:, c3, :, :],
              win_v[:, :, 2056 + 512 * c3:2056 + 512 * (c3 + 1)], "wqkvb%d" % c3)
    QT0 = bB.alloc([S], BF16)
    KTz = [bB.alloc([S], BF16) for _ in range(2)]
    QTb = [QT0, QT0]
    P.memset("pool", KTz[0][64:128, :], 0.0)
    P.memset("pool", KTz[1][0:64, :], 0.0)
    VAll = bB.alloc([48, 4, 192], BF16)
    PTbuf = Ring([bB.alloc([1024], BF16) for _ in range(2)])
    PT3 = bB.alloc([16, 128], BF16)
    rden0 = bB.alloc([512])
    rdenb = [rden0, rden0]
    P.memset("pool", VAll[:, :, :, 64:128], 1.0)
    ringS = Ring([2, 3, 4, 5])
    ringP = Ring([6, 7])
    accR = Ring([0, 1])

    def tok_slices():
        sl = []
        for n in range(16):
            sl.append(slice(128 * n, 128 * (n + 1), 1))
        for r in range(4):
            for c in range(4):
                sl.append(slice(512 * c + r, 512 * (c + 1), 4))
        for r in range(16):
            sl.append(slice(r, S, 16))
        return sl

    TOK = tok_slices()

    def PROJ_V():
        for t in range(48):
            bank = ringP.next()
            ps = pbf(bank)
            sl = TOK[t]
            for kc in range(8):
                P.mm(ps, hT[:, kc, sl], wqkvb[:, 2, kc, :], start=(kc == 0), stop=(kc == 7))
            ps4 = ps.rearrange("p (j e d) -> p j e d", j=4, e=2)
            P.copy("act", VAll[:, t, :, 0:64], ps4[:, :, 0, :])
            P.copy("dve", VAll[:, t, :, 128:192], ps4[:, :, 1, :])

    def PROJ_B(j):
        k = j % 2
        QT = QTb[k]
        for (dst, cbase) in ((QT, 128 * j), (None, 512 + 128 * j)):
            for tb in range(4):
                bank = ringP.next()
                ps = pbf(bank)
                for kc in range(8):
                    P.mm(ps, wqkvb[:, cbase // 512, kc, cbase % 512:cbase % 512 + 128], hT[:, kc, 512 * tb:512 * (tb + 1)],
                         start=(kc == 0), stop=(kc == 7))
                cs = slice(512 * tb, 512 * (tb + 1))
                if dst is not None:
                    P.copy("dve" if tb % 2 else "act", dst[:, cs], ps)
                else:
                    P.copy("act", KTz[0][0:64, cs], ps[0:64, :])
                    P.copy("dve", KTz[1][64:128, cs], ps[64:128, :])

    def ATTN(j):
        k = j % 2
        QT = QTb[k]

        class _VA:
            def __init__(self, h):
                self.h = h

            def __getitem__(self, idx):
                e_ = self.h % 2
                return VAll[:, idx[1], self.h // 2, 64 * e_:64 * e_ + 128]

        for e in range(2):
            hp = slice(64 * e, 64 * e + 64)
            KT = KTz[e]
            fp = slice(0, 128)
            VA = _VA(2 * j + e)
            for g in range(4):
                bank = ringS.next()
                ps = pbf(bank)
                P.mm(ps, IDENTb, MBA4b, start=True, stop=False)
                for r4 in range(4):
                    r = 4 * g + r4
                    P.mm(ps[:, 128 * r4:128 * (r4 + 1)], KT[fp, r:S:16], QT[fp, r:S:16], start=False, stop=(r4 == 3))
                P.act(PT3[:, 4 * g:4 * g + 4, :], ps.rearrange("p (a d) -> p a d", a=4), AF.Exp, scale=0.125)
            for c in range(4):
                acc = pbf(accR.next())
                first = [True]

                def pv(out, lhsT, rhs, last=False):
                    P.mm(out, lhsT, rhs, start=first[0], stop=last, skip_group_check=True)
                    first[0] = False
                pt = PTbuf.next()
                bank = ringS.next(); ps = pbf(bank)
                P.mm(ps, IDENTb, MBA4b, start=True, stop=False)
                for i in range(4):
                    n = 4 * c + i
                    P.mm(ps[:, 128 * i:128 * (i + 1)], KT[fp, 128 * n:128 * (n + 1)], QT[fp, 128 * n:128 * (n + 1)],
                         start=False, stop=(i == 3))
                P.act(pt[:, 0:512], ps, AF.Exp, scale=0.125)
                bank = ringS.next(); ps = pbf(bank)
                P.mm(ps, IDENTb, MBB4b, start=True, stop=False)
                for i in range(4):
                    n = 4 * c + i
                    if n == 0:
                        continue
                    P.mm(ps[:, 128 * i:128 * (i + 1)], KT[fp, 128 * (n - 1):128 * n], QT[fp, 128 * n:128 * (n + 1)],
                         start=False, stop=(i == 3))
                P.act(pt[:, 512:1024], ps, AF.Exp, scale=0.125)
                pt1 = pt
                pt = PTbuf.next()
                bank = ringS.next(); ps = pbf(bank)
                P.mm(ps, IDENTb, MBA4b, start=True, stop=False)
                for r in range(4):
                    sl = slice(512 * c + r, 512 * (c + 1), 4)
                    P.mm(ps[:, 128 * r:128 * (r + 1)], KT[fp, sl], QT[fp, sl], start=False, stop=(r == 3))
                P.act(pt[:, 0:512], ps, AF.Exp, scale=0.125)
                if c > 0:
                    bank = ringS.next(); ps = pbf(bank)
                    P.mm(ps, IDENTb, MBB4b, start=True, stop=False)
                    for r in range(4):
                        sl = slice(512 * c + r, 512 * (c + 1), 4)
                        slk = slice(512 * (c - 1) + r, 512 * c, 4)
                        P.mm(ps[:, 128 * r:128 * (r + 1)], KT[fp, slk], QT[fp, sl], start=False, stop=(r == 3))
                    P.act(pt[:, 512:1024], ps, AF.Exp, scale=0.125)
                for i in range(4):
                    n = 4 * c + i
                    pv(acc[:, 128 * i:128 * (i + 1)], VA[:, n, e, :], pt1[:, 128 * i:128 * (i + 1)])
                    if n > 0:
                        pv(acc[:, 128 * i:128 * (i + 1)], VA[:, n - 1, e, :], pt1[:, 512 + 128 * i:512 + 128 * (i + 1)])
                for r in range(4):
                    pv(acc[:, r:512:4], VA[:, 16 + 4 * r + c, e, :], pt[:, 128 * r:128 * (r + 1)])
                    if c > 0:
                        pv(acc[:, r:512:4], VA[:, 16 + 4 * r + c - 1, e, :], pt[:, 512 + 128 * r:512 + 128 * (r + 1)])
                for r in range(16):
                    pv(acc[:, r:512:16], VA[:, 32 + r, e, :], PT3[:, r, 32 * c:32 * (c + 1)], last=(r == 15))
                num = slice(64 * e, 64 * e + 64)
                den = slice(64 * (1 - e), 64 * (1 - e) + 64)
                rd = rdenb[c % 2]
                P.recip(rd[den, :], acc[den, :])
                P.tt("dve", CT[num, 4 + j, 512 * c:512 * (c + 1)], acc[num, :], rd[den, :], ALU.mult)

    PROJ_V()
    for j in range(4):
        PROJ_B(j)
        ATTN(j)
    dump("CTb", CT[:, 4:8, :].rearrange("p k t -> p (k t)"), [128, 4 * S])
    if stage == "B":
        return finish(nc, P, out_d, dump_tags)

    bF = Bump(A, SCR_LO, SCR_HI)
    h2T = A.view(32768, [8, 1024], BF16)
    woutb = A.view(32768 + 16384, [2, 8, 512], BF16)
    X1 = bF.alloc([8, D])
    wu = [bF.alloc([2, 8, 512], BF16) for _ in range(2)]
    wd = [bF.alloc([4, D], BF16) for _ in range(2)]
    aTb = [bF.alloc([4, 1024], BF16) for _ in range(2)]
    rawF = [[bF.alloc([520]) for _ in range(2)] for _ in range(2)]
    accF = [[bF.alloc([512]) for _ in range(2)] for _ in range(2)]
    sgF = [bF.alloc([512]) for _ in range(2)]
    hbF = [bF.alloc([D], BF16), accF[1][1].bitcast(BF16)]
    junkF = sgF[0].bitcast(BF16)
    ssF = bF.alloc([8]); rsF = bF.alloc([8]); ssO = bF.alloc([8]); rsO = bF.alloc([8])
    for c2 in range(2):
        P.dma("pool", woutb[:, c2, :, :], wout_v[:, :, 512 * c2:512 * (c2 + 1)], "wout%d" % c2)
    P.dma("sp", NW1, n2_d, "c_nw1")
    P.dma("sp", NW2, nf_d, "c_nw2")
    groups = [list(range(g0, min(g0 + 4, 22))) for g0 in range(0, 22, 4)]
    out_tags = ["out%d" % i for i in range(8)]
    items = [(H, gi) for H in range(2) for gi in range(len(groups))]

    def load_wu(k):
        H, gi = items[k]
        grp = groups[gi]; g0 = grp[0]; npair = len(grp); slot = k % 2
        P.dma("pool", wu[slot][:, 0, :, 0:128 * npair], wup_v[:, :, 128 * g0:128 * (g0 + npair)], "wug%d" % slot)
        P.dma("pool", wu[slot][:, 1, :, 0:128 * npair],
              wup_v[:, :, 2816 + 128 * g0:2816 + 128 * (g0 + npair)], "wuu%d" % slot)

    def load_wd(k):
        H, gi = items[k]
        grp = groups[gi]; g0 = grp[0]; npair = len(grp); slot = k % 2
        P.dma("pool", wd[slot][:, 0:npair, :], wdn_v[:, g0:g0 + npair, :], "wd%d" % slot)

    def PRO(H):
        scr = [(junkF, ssF[:, i8:i8 + 1], rsF[:, i8:i8 + 1], hbF[i8 % 2]) for i8 in range(8)]

        def st_a(i8):
            i = 8 * H + i8
            P.dma("sp", X1[:, i8, :], x_d[128 * i:128 * (i + 1), :], "xf%d" % i8)
            b0 = 2 * (i8 % 2)
            for h2 in range(2):
                for kc in range(8):
                    P.mm(pbf(b0 + h2), CT[:, kc, 128 * i:128 * (i + 1)], woutb[:, h2, kc, :],
                         start=(kc == 0), stop=(kc == 7))
            for h2 in range(2):
                P.tt("dve", X1[:, i8, 512 * h2:512 * (h2 + 1)], X1[:, i8, 512 * h2:512 * (h2 + 1)], pbf(b0 + h2), ALU.add)
            if i == 0:
                dump("X1", X1[:, 0, :], [128, D])
            rmsnorm_stage1a(X1[:, i8, :], scr[i8])

        for t in range(10):
            if t < 8:
                st_a(t)
            if 1 <= t <= 8:
                rmsnorm_stage1b(X1[:, t - 1, :], NW1, scr[t - 1])
            if t >= 2:
                rmsnorm_stage2(h2T, 128 * (t - 2), scr[t - 2], 4 + (t % 2))

    upar = [0]

    def UGEN(k):
        H, gi = items[k]
        grp = groups[gi]; slot = k % 2; aT = aTb[k % 2]
        upend = [None]
        for p, g in enumerate(grp):
            for tb in range(2):
                par = upar[0]
                upar[0] ^= 1
                cols = slice(512 * tb, 512 * (tb + 1))
                banks = (4, 5) if par == 0 else (6, 7)
                accs = []
                for gu in range(2):
                    ps = pbf(banks[gu])
                    for kc in range(8):
                        P.mm(ps, wu[slot][:, gu, kc, 128 * p:128 * (p + 1)], h2T[:, kc, cols],
                             start=(kc == 0), stop=(kc == 7))
                    cc = g + 22 * gu
                    raw = rawF[par][gu]; acc = accF[par][gu]
                    P.copy("pool", raw[:, 0:2], HALOF[:, cc, :])
                    P.copy("act", raw[:, 2:514], ps)
                    P.copy("pool", HALOF[:, cc, :], raw[:, 512:514])
                    P.act(acc, ps, AF.Identity, scale=cwF[:, cc, 2:3])
                    P.stt(acc, raw[:, 1:513], cwF[:, cc, 1:2], acc, ALU.mult, ALU.add)
                    P.stt(acc, raw[:, 0:512], cwF[:, cc, 0:1], acc, ALU.mult, ALU.add)
                    accs.append(acc)
                if upend[0] is not None:
                    upend[0]()

                def _gate(par=par, accs=accs, p=p, cols=cols):
                    sg = sgF[par]
                    P.act(sg, accs[0], AF.Silu)
                    P.tt("pool", aT[:, p, cols], sg, accs[1], ALU.mult)
                upend[0] = _gate
                yield
        if upend[0] is not None:
            upend[0]()
            upend[0] = None
            yield

    def DGEN(k):
        H, gi = items[k]
        grp = groups[gi]; slot = k % 2; aT = aTb[k % 2]; npair = len(grp)
        last = (gi == len(groups) - 1)
        dpend = [None]
        for i8 in range(8):
            i = 8 * H + i8
            b0 = 2 * (i8 % 2)
            for h2 in range(2):
                for p in range(npair):
                    P.mm(pbf(b0 + h2), aT[:, p, 128 * i8:128 * (i8 + 1)], wd[slot][:, p, 512 * h2:512 * (h2 + 1)],
                         start=(p == 0), stop=(p == npair - 1))
            for h2 in range(2):
                P.tt("dve", X1[:, i8, 512 * h2:512 * (h2 + 1)], X1[:, i8, 512 * h2:512 * (h2 + 1)],
                     pbf(b0 + h2), ALU.add)
            if last:
                P.act(junkF, X1[:, i8, :], AF.Square, accum_out=ssO[:, i8:i8 + 1])
                P.act(rsO[:, i8:i8 + 1], ssO[:, i8:i8 + 1], AF.Ln, bias=EPS, scale=1.0 / D)
                P.act(rsO[:, i8:i8 + 1], rsO[:, i8:i8 + 1], AF.Exp, scale=-0.5)
                if dpend[0] is not None:
                    dpend[0]()

                def _fin(i8=i8, i=i):
                    P.stt(X1[:, i8, :], X1[:, i8, :], rsO[:, i8:i8 + 1], NW2, ALU.mult, ALU.mult)
                    P.dma("sp", out_d[128 * i:128 * (i + 1), :], X1[:, i8, :], "out%d" % i8)
                dpend[0] = _fin
            yield
        if last and dpend[0] is not None:
            dpend[0]()
            dpend[0] = None
            yield

    def rr(gens):
        gens = list(gens)
        live = [True] * len(gens)
        while any(live):
            for gi_, g_ in enumerate(gens):
                if live[gi_]:
                    try:
                        next(g_)
                    except StopIteration:
                        live[gi_] = False

    load_wu(0)
    prevD = None
    for k in range(len(items)):
        H, gi = items[k]
        load_wd(k)
        if k + 1 < len(items):
            load_wu(k + 1)
        if gi == 0:
            if prevD is not None:
                rr([prevD])
                prevD = None
            PRO(H)
        u = UGEN(k)
        rr([u] if prevD is None else [u, prevD])
        prevD = DGEN(k)
    rr([prevD])
    return finish(nc, P, out_d, dump_tags + out_tags)


def finish(nc, P, out_d, dump_tags):
    tags = list(dump_tags)
    P.emit(final_dma_tags=tags)
    return nc


def prep_inputs(inp):
    f = lambda a: np.ascontiguousarray(np.asarray(a, dtype=np.float32))
    x = f(inp["x"])
    rep = lambda v: np.ascontiguousarray(np.broadcast_to(f(v).reshape(1, -1), (128, f(v).size)))
    cwa = f(inp["conv_qkv_w"])[0]
    cwA = np.ascontiguousarray(cwa.T.reshape(12, 128, 4).transpose(1, 0, 2).reshape(128, 48))
    cwf = f(inp["ffn_conv_w"])[0]
    cwF = np.ascontiguousarray(cwf.T.reshape(44, 128, 3).transpose(1, 0, 2).reshape(128, 132))
    shared = {
        "w_in": f(inp["w_in"])[0], "w_out": f(inp["w_out"])[0], "w_up": f(inp["w_up"])[0],
        "w_down": f(inp["w_down"])[0],
        "n1rep": rep(inp["norm1_w"]), "n2rep": rep(inp["norm2_w"]), "nfrep": rep(inp["final_norm_w"]),
        "cwA": cwA, "cwF": cwF, "gnwrep": rep(inp["gdn_norm_w"]),
        "dtbrep": np.ascontiguousarray(np.tile(rep(inp["dt_bias"]), (1, 16))),
        "alogrep": np.ascontiguousarray(np.tile(rep(inp["a_log"]), (1, 16))),
        "consts": make_consts(),
    }
    maps = []
    for b in range(x.shape[0]):
        m = dict(shared)
        m["x"] = np.ascontiguousarray(x[b])
        maps.append(m)
    return maps


def kernel(**inputs):
    maps = prep_inputs(inputs)
    nc = build("full")
    res = run_bass_kernel_spmd(nc, maps, core_ids=list(range(8)))
    out = np.stack([np.asarray(r["out"], dtype=np.float32) for r in res.results], 0)
    return out
```

```python
import contextlib
import numpy as np
import concourse.bass as bass
import concourse.mybir as mybir
from concourse.bass_utils import run_bass_kernel_spmd

F32 = mybir.dt.float32
BF16 = mybir.dt.bfloat16
AF = mybir.ActivationFunctionType
ALU = mybir.AluOpType
AX = mybir.AxisListType

S = 2048
D = 1024
NT = 16
EPS = 1e-6
NEG = -30000.0
ENGS = ("pe", "act", "dve", "pool", "sp")


def _rect(ap):
    t = ap.tensor
    if str(ap.space) == "PSUM":
        return (t.name, 0, 128, 0, 2048)
    esz = mybir.dt.size(ap.dtype)
    pstride = 1
    for s in tuple(t.shape)[1:]:
        pstride *= s
    p0 = ap.start_partition()
    p1 = p0 + ap.partition_size()
    f0 = ap.offset - p0 * pstride
    ext = 0
    for (st, cnt) in tuple(ap.ap)[1:]:
        ext += abs(st) * (cnt - 1)
    return (t.name, p0, p1, f0 * esz, (f0 + ext + 1) * esz)


class _Op:
    __slots__ = ("eng", "fn", "idx", "is_dma", "tag", "waits", "signal", "sigval")


class Prog:
    def __init__(self, nc):
        self.nc = nc
        self.ops = {e: [] for e in ENGS}
        self.track = {}
        self.waited = {e: {} for e in ENGS}
        self.tagcount = {}

    def _add(self, eng, fn, reads, writes, is_dma=False, tag=None):
        op = _Op()
        op.eng = eng; op.fn = fn; op.is_dma = is_dma; op.tag = tag
        op.signal = False; op.sigval = None
        op.idx = len(self.ops[eng])
        op.waits = []
        deps = {}
        rrects = [_rect(a) for a in reads if a is not None and str(a.space) != "DRAM"]
        wrects = [_rect(a) for a in writes if a is not None and str(a.space) != "DRAM"]
        prects = [r for r in rrects if r[4] == 2048 and r[0].startswith("pb") and r not in wrects]
        for (nm, p0, p1, f0, f1) in rrects:
            for rec in self.track.get(nm, ()):
                if rec[5] == 1 and rec[0] < p1 and p0 < rec[1] and rec[2] < f1 and f0 < rec[3]:
                    deps[id(rec[4])] = (rec[4], True)
        for (nm, p0, p1, f0, f1) in wrects + prects:
            for rec in self.track.get(nm, ()):
                if rec[0] < p1 and p0 < rec[1] and rec[2] < f1 and f0 < rec[3]:
                    k = id(rec[4])
                    if k not in deps:
                        deps[k] = (rec[4], False)
        need = {}
        for (d, raw) in deps.values():
            if d.is_dma:
                key = ("dma", d.tag)
                val = self.tagcount[d.tag]
                if need.get(key, 0) < val:
                    need[key] = val
            else:
                if d.eng == eng and eng == "pe":
                    continue
                key = ("eng", d.eng)
                cur = need.get(key)
                if cur is None or cur.idx < d.idx:
                    need[key] = d
        w = self.waited[eng]
        for key, v in need.items():
            if key[0] == "dma":
                if w.get(key, 0) >= v:
                    continue
                w[key] = v
                op.waits.append((key, v))
            else:
                if w.get(key, -1) >= v.idx:
                    continue
                w[key] = v.idx
                v.signal = True
                op.waits.append((key, v))
        if is_dma:
            self.tagcount[tag] = self.tagcount.get(tag, 0) + 16
        for (nm, p0, p1, f0, f1) in wrects:
            lst = self.track.setdefault(nm, [])
            lst[:] = [r for r in lst if not (p0 <= r[0] and r[1] <= p1 and f0 <= r[2] and r[3] <= f1)]
            lst.append([p0, p1, f0, f1, op, 1])
        for (nm, p0, p1, f0, f1) in prects:
            self.track[nm] = [[p0, p1, f0, f1, op, 2]]
        for (nm, p0, p1, f0, f1) in rrects:
            if nm.startswith("pb"):
                continue
            lst = self.track.setdefault(nm, [])
            done = False
            for r in lst:
                if r[5] == 0 and r[4].eng == eng and (not r[4].is_dma) and (not is_dma) \
                        and r[0] == p0 and r[1] == p1 and r[2] == f0 and r[3] == f1:
                    r[4] = op
                    done = True
                    break
            if not done:
                lst.append([p0, p1, f0, f1, op, 0])
        self.ops[eng].append(op)
        return op

    def dma(self, q, out, in_, tag):
        return self._add(q, lambda e: e.dma_start(out=out, in_=in_), [in_], [out], is_dma=True, tag=tag)

    def mm(self, out, lhsT, rhs, start=True, stop=True, **kw):
        rd = [lhsT, rhs] + ([] if start else [out])
        return self._add("pe", lambda e: e.matmul(out, lhsT, rhs, start=start, stop=stop, **kw), rd, [out])

    def tr(self, out, in_, ident):
        return self._add("pe", lambda e: e.transpose(out, in_, ident), [in_, ident], [out])

    def act(self, out, in_, func, bias=None, scale=None, accum_out=None):
        kw = {}
        rd = [in_]
        if bias is not None:
            kw["bias"] = bias
            if not isinstance(bias, (int, float)):
                rd.append(bias)
        if scale is not None:
            kw["scale"] = scale
            if not isinstance(scale, (int, float)):
                rd.append(scale)
        wr = [out]
        if accum_out is not None:
            kw["accum_out"] = accum_out
            wr.append(accum_out)
        return self._add("act", lambda e: e.activation(out, in_, func, **kw), rd, wr)

    def tt(self, eng, out, in0, in1, op):
        return self._add(eng, lambda e: e.tensor_tensor(out, in0, in1, op), [in0, in1], [out])

    def ts(self, eng, out, in0, s1, s2, op0, op1=None):
        rd = [in0] + [s for s in (s1, s2) if s is not None and not isinstance(s, (int, float))]
        kw = {}
        if op1 is not None:
            kw["op1"] = op1
        return self._add(eng, lambda e: e.tensor_scalar(out, in0, s1, s2, op0, **kw), rd, [out])

    def stt(self, out, in0, scalar, in1, op0, op1):
        rd = [in0, in1] + ([] if isinstance(scalar, (int, float)) else [scalar])
        return self._add("dve", lambda e: e.scalar_tensor_tensor(out, in0, scalar, in1, op0, op1), rd, [out])

    def copy(self, eng, out, in_):
        if eng == "act":
            return self._add("act", lambda e: e.copy(out, in_), [in_], [out])
        return self._add(eng, lambda e: e.tensor_copy(out, in_), [in_], [out])

    def memset(self, eng, ap, val):
        return self._add(eng, lambda e: e.memset(ap, val), [], [ap])

    def recip(self, out, in_):
        return self._add("dve", lambda e: e.reciprocal(out, in_), [in_], [out])

    def rsum(self, out, in_):
        return self._add("dve", lambda e: e.tensor_reduce(out, in_, AX.X, ALU.add), [in_], [out])

    def emit(self, final_dma_tags=()):
        nc = self.nc
        for e in ENGS:
            c = 0
            for op in self.ops[e]:
                if op.signal and not op.is_dma:
                    c += 1
                    op.sigval = c
        with contextlib.ExitStack() as st:
            esem = {e: st.enter_context(nc.semaphore("s_" + e)) for e in ENGS}
            dsem = {t: st.enter_context(nc.semaphore("d_%d" % i)) for i, t in enumerate(self.tagcount)}
            block = st.enter_context(nc.Block())
            engobj = {"pe": block.tensor, "act": block.scalar, "dve": block.vector,
                      "pool": block.gpsimd, "sp": block.sync}

            def make(ename):
                def body(eng):
                    for op in self.ops[ename]:
                        for (key, v) in op.waits:
                            if key[0] == "dma":
                                eng.wait_ge(dsem[key[1]], v)
                            else:
                                eng.wait_ge(esem[key[1]], v.sigval)
                        ins = op.fn(eng)
                        if op.is_dma:
                            ins.then_inc(dsem[op.tag], 16)
                        elif op.signal:
                            ins.then_inc(esem[ename], 1)
                    if ename == "sp":
                        for t in final_dma_tags:
                            eng.wait_ge(dsem[t], self.tagcount[t])
                return body

            for e in ENGS:
                engobj[e](make(e))
        return nc


def _prod(shape):
    n = 1
    for s in shape:
        n *= s
    return n


class Arena:
    def __init__(self, nc, nbytes):
        self.t = nc.alloc_sbuf_tensor("arena", [128, nbytes // 4], F32)
        self.nbytes = nbytes

    def view(self, off, shape, dt):
        n = _prod(shape)
        esz = 4 if dt == F32 else 2
        assert off % 4 == 0 and (n * esz) % 4 == 0 and off + n * esz <= self.nbytes, (off, shape)
        ap = self.t[:, off // 4:(off + n * esz) // 4]
        if dt != F32:
            ap = ap.bitcast(dt)
        if len(shape) > 1:
            names = " ".join("d%d" % i for i in range(len(shape)))
            kw = {"d%d" % i: shape[i] for i in range(1, len(shape))}
            ap = ap.rearrange("p (%s) -> p %s" % (names, names), **kw)
        return ap


class Bump:
    def __init__(self, arena, lo, hi):
        self.a = arena; self.lo = lo; self.hi = hi; self.cur = lo

    def alloc(self, shape, dt=F32):
        esz = 4 if dt == F32 else 2
        n = (_prod(shape) * esz + 31) // 32 * 32
        off = self.cur
        self.cur += n
        assert self.cur <= self.hi, ("arena overflow", self.cur, self.hi)
        return self.a.view(off, shape, dt)


class Ring:
    def __init__(self, items):
        self.items = list(items); self.i = 0

    def next(self):
        x = self.items[self.i % len(self.items)]
        self.i += 1
        return x


_CNAMES = ["IDENT", "M1", "M2", "TRI", "SUF", "IND0", "IND1", "S32", "OFF", "ONES"]
_CB = {"MB64": 512, "MBA4": 512, "MBB4": 512}
NCF = 128 * len(_CNAMES)
NCONST = NCF + 512 * 3


def make_consts():
    i = np.arange(128)[:, None]
    j = np.arange(128)[None, :]
    same = (i // 64) == (j // 64)
    c = {}
    c["IDENT"] = (i == j)
    c["M1"] = (i > j)
    c["M2"] = (i <= j)
    c["TRI"] = (i <= j) & same
    c["SUF"] = (i > j) & same
    c["IND0"] = (i < 64) & (j >= 0)
    c["IND1"] = (i >= 64) & (j >= 0)
    c["S32"] = (i < j) & ((i // 32) == (j // 32))
    c["OFF"] = same & ((i % 64) < 32) & ((j % 64) >= 32)
    c["ONES"] = np.ones((128, 128), bool)
    cols = [c[n].astype(np.float32) for n in _CNAMES]
    mb64 = np.where((i <= j) & same, 0.0, NEG).astype(np.float32)
    mba = np.where(i <= j, 0.0, NEG).astype(np.float32)
    mbb = np.where(i >= j, 0.0, NEG).astype(np.float32)
    cols += [np.tile(mb64, (1, 4)), np.tile(mba, (1, 4)), np.tile(mbb, (1, 4))]
    return np.ascontiguousarray(np.concatenate(cols, 1))


def build(stage="full", dumps=()):
    nc = bass.Bass("TRN2", target_bir_lowering=False)
    P = Prog(nc)
    dumps = set(dumps)
    dump_tags = []

    def din(name, shape):
        return nc.dram_tensor(name, list(shape), F32, kind="ExternalInput").ap()

    x_d = din("x", [S, D])
    win_d = din("w_in", [D, 3592])
    wout_d = din("w_out", [D, D])
    wup_d = din("w_up", [D, 5632])
    wdn_d = din("w_down", [2816, D])
    n1_d = din("n1rep", [128, D])
    n2_d = din("n2rep", [128, D])
    nf_d = din("nfrep", [128, D])
    cwa_d = din("cwA", [128, 48])
    cwf_d = din("cwF", [128, 132])
    gnw_d = din("gnwrep", [128, 128])
    dtb_d = din("dtbrep", [128, 64])
    alog_d = din("alogrep", [128, 64])
    cst_d = din("consts", [128, NCONST])
    out_d = nc.dram_tensor("out", [S, D], F32, kind="ExternalOutput").ap()

    def dump(name, sb_ap, shape):
        if name not in dumps:
            return
        d = nc.dram_tensor("dbg_" + name, list(shape), sb_ap.dtype, kind="ExternalOutput").ap()
        tg = "dbg_" + name
        P.dma("sp", d, sb_ap, tg)
        dump_tags.append(tg)

    win_v = win_d.rearrange("(k p) c -> p k c", p=128)
    wout_v = wout_d.rearrange("(k p) c -> p k c", p=128)
    wup_v = wup_d.rearrange("(k p) c -> p k c", p=128)
    wdn_v = wdn_d.rearrange("(k p) c -> p k c", p=128)

    ARENA_BYTES = 207872
    A = Arena(nc, ARENA_BYTES)
    pb = [nc.alloc_psum_tensor("pb%d" % i, [128, 512], F32) for i in range(8)]

    def pbf(i):
        return pb[i][:]

    def pbb(i):
        return pb[i][:].bitcast(BF16)

    CT = A.view(0, [8, S], BF16)
    hT = A.view(32768, [8, S], BF16)
    pers = Bump(A, 65536, 86016)
    CF = pers.alloc([NCF])
    CBm = pers.alloc([1536 + 256], BF16)
    cwA = pers.alloc([12, 4])
    cwF = pers.alloc([44, 3])
    HALOA = pers.alloc([12, 3])
    HALOF = pers.alloc([44, 2])
    GNW = pers.alloc([128])
    NW1 = pers.alloc([D])
    NW2 = pers.alloc([D])
    SCR_LO = 86016
    SCR_HI = ARENA_BYTES

    def cf(name):
        k = _CNAMES.index(name)
        return CF[:, 128 * k:128 * (k + 1)]

    IDENT = cf("IDENT"); M1 = cf("M1"); M2 = cf("M2"); TRI = cf("TRI"); SUF = cf("SUF")
    IND0 = cf("IND0"); IND1 = cf("IND1"); S32 = cf("S32"); OFFM = cf("OFF"); ONES = cf("ONES")
    MB64b = CBm[:, 0:512]; MBA4b = CBm[:, 512:1024]; MBB4b = CBm[:, 1024:1536]
    IDENTb = CBm[:, 1536:1664]; ONESb = CBm[:, 1664:1792]

    P.dma("sp", CF, cst_d[:, 0:NCF], "c_cf")
    ctmp = A.view(16384 + 8192, [1536], F32)
    P.dma("sp", ctmp, cst_d[:, NCF:NCONST], "c_tmp")
    P.copy("dve", CBm[:, 0:1536], ctmp)
    P.copy("dve", IDENTb, IDENT)
    P.copy("dve", ONESb, ONES)
    P.dma("sp", cwA.rearrange("p a b -> p (a b)"), cwa_d, "c_cwa")
    P.dma("sp", cwF.rearrange("p a b -> p (a b)"), cwf_d, "c_cwf")
    P.dma("sp", GNW, gnw_d, "c_gnw")
    P.dma("sp", NW1, n1_d, "c_nw1")
    P.memset("pool", HALOA.rearrange("p a b -> p (a b)"), 0.0)
    P.memset("pool", HALOF.rearrange("p a b -> p (a b)"), 0.0)

    def bc_last(ap, n):
        return ap.unsqueeze(2).broadcast_to([128, ap.shape[1], n])

    def bc_mid(ap, n):
        return ap.unsqueeze(1).broadcast_to([128, n, ap.shape[1]])

    def rmsnorm_stage1(src_tile, wtile, scr):
        junk, ssv, rst, hb = scr
        P.act(junk, src_tile, AF.Square, accum_out=ssv)
        P.act(rst, ssv, AF.Ln, bias=EPS, scale=1.0 / D)
        P.act(rst, rst, AF.Exp, scale=-0.5)
        P.stt(hb, src_tile, rst, wtile, ALU.mult, ALU.mult)

    def rmsnorm_stage1a(src_tile, scr):
        junk, ssv, rst, hb = scr
        P.act(junk, src_tile, AF.Square, accum_out=ssv)
        P.act(rst, ssv, AF.Ln, bias=EPS, scale=1.0 / D)
        P.act(rst, rst, AF.Exp, scale=-0.5)

    def rmsnorm_stage1b(src_tile, wtile, scr):
        junk, ssv, rst, hb = scr
        P.stt(hb, src_tile, rst, wtile, ALU.mult, ALU.mult)

    def rmsnorm_stage2(dstT, col0, scr, bank):
        hb = scr[3]
        psT = pbb(bank)
        for kc in range(8):
            P.tr(psT[:, 128 * kc:128 * (kc + 1)], hb[:, 128 * kc:128 * (kc + 1)], IDENTb)
        P.copy("act", dstT[:, :, col0:col0 + 128], psT.rearrange("p (k t) -> p k t", k=8))

    bA = Bump(A, 0, 16384)
    xbuf = [bA.alloc([D]) for _ in range(3)]
    junkA = bA.alloc([D], BF16)
    hbuf = [NW2.bitcast(BF16)[:, 0:D], NW2.bitcast(BF16)[:, D:2 * D]]
    ssA = bA.alloc([NT])
    rsA = bA.alloc([NT])
    scrA = [(junkA, ssA[:, i:i + 1], rsA[:, i:i + 1], hbuf[i % 2]) for i in range(NT)]
    for i in range(NT + 2):
        if i < NT:
            xt = xbuf[i % 3]
            P.dma("sp", xt, x_d[128 * i:128 * (i + 1), :], "xa%d" % (i % 3))
            rmsnorm_stage1a(xt, scrA[i])
        if 1 <= i <= NT:
            rmsnorm_stage1b(xbuf[(i - 1) % 3], NW1, scrA[i - 1])
        if i >= 2:
            rmsnorm_stage2(hT, 128 * (i - 2), scrA[i - 2], 6 + (i % 2))
    dump("hT", hT.rearrange("p k t -> p (k t)"), [128, 8 * S])
    if stage == "A":
        return finish(nc, P, out_d, dump_tags)

    bG = Bump(A, SCR_LO, SCR_HI)
    bC = Bump(A, 16384, 32768)
    wz = bC.alloc([8, 512], BF16)
    qkvT = [dict(q=bG.alloc([4, 512], BF16), k=bG.alloc([4, 512], BF16), v=bG.alloc([4, 512], BF16))
            for _ in range(4)]
    szgb = [bG.alloc([4, 512], BF16), bG.alloc([4, 512], BF16), bG.alloc([4, 512], BF16), bC.alloc([4, 512], BF16)]
    BA = bG.alloc([NT, 8])
    sc_x = bG.alloc([64]); sc_mx = bG.alloc([64]); sc_mn = bG.alloc([64])
    dtb = bG.alloc([64]); negA = bG.alloc([64])
    gS = bG.alloc([64]); betaS = bG.alloc([64]); gamS = bG.alloc([64]); kesS = bG.alloc([64])
    gendS = bG.alloc([2, 64])
    wba = bG.alloc([8, 8], BF16)
    Sst = bG.alloc([4, 128]); Sb = bG.alloc([4, 128], BF16); ub = bG.alloc([4, 128], BF16)
    Obuf = [bC.alloc([4, 128]) for _ in range(2)]
    sqO = bG.alloc([4, 128], BF16); oab = sqO
    kesLo = bG.alloc([64]); kesHi = bG.alloc([64])
    ssq = bG.alloc([4]); rsq = bG.alloc([4])
    ov0 = bG.cur
    gset = []
    for _ in range(2):
        gset.append(dict(
            G1m=bG.alloc([4, 128]),
            DECT=bG.alloc([4, 128], BF16), Uall=bG.alloc([4, 128], BF16), Eu=bG.alloc([4, 128], BF16),
            PTa=bG.alloc([4, 128], BF16), PTb=bG.alloc([4, 128], BF16),
            UL0=bG.alloc([2, 4, 128], BF16), PWA=bG.alloc([2, 4, 128], BF16), PWB=bG.alloc([2, 4, 128], BF16),
            TTb=bG.alloc([4, 128], BF16), vb=bG.alloc([4, 128], BF16), gk=bG.alloc([4, 128], BF16)))
    scanop = []
    for _ in range(3):
        scanop.append(dict(KLO=bG.alloc([4, 128], BF16), KHI=bG.alloc([4, 128], BF16), ATT=bG.alloc([4, 128], BF16),
                           QD=bG.alloc([4, 128], BF16), WKT=bG.alloc([4, 128], BF16),
                           UV=bG.alloc([4, 128], BF16)))
    bO = Bump(A, ov0, bG.cur)
    wqkva = bO.alloc([3, 8, 512], BF16)
    rawb = [bO.alloc([520], BF16) for _ in range(8)]
    dgcb = [bO.alloc([4, 128], BF16) for _ in range(8)]
    sqbs = [bO.alloc([512], BF16) for _ in range(3)]
    rtbs = [bO.alloc([512]) for _ in range(3)]
    for c3 in range(3):
        P.dma("pool", wqkva[:, c3, :, :], win_v[:, :, 512 * c3:512 * (c3 + 1)], "wqkva%d" % c3)
    P.dma("pool", wba, win_v[:, :, 2048:2056], "wba")
    P.dma("pool", wz, win_v[:, :, 1536:2048], "wz")

    ringG = Ring([0, 1, 2, 3, 4, 5, 6, 7])

    g3 = gS.rearrange("p (n h) -> p n h", h=4)
    beta3 = betaS.rearrange("p (n h) -> p n h", h=4)
    gam3 = gamS.rearrange("p (n h) -> p n h", h=4)
    kesLo3 = kesLo.rearrange("p (n h) -> p n h", h=4)
    kesHi3 = kesHi.rearrange("p (n h) -> p n h", h=4)
    gend4 = gendS.rearrange("p a (n h) -> p a n h", h=4)

    def SCALARS():
        P.dma("sp", dtb, dtb_d, "c_dtb")
        P.dma("sp", negA, alog_d, "c_alog")
        P.act(negA, negA, AF.Exp)
        P.ts("dve", negA, negA, -1.0, None, ALU.mult)
        bk = ringG.next()
        psBA = pbf(bk)[:, 0:128].rearrange("p (n c) -> p n c", c=8)
        for i in range(NT):
            for kc in range(8):
                P.mm(psBA[:, i, :], hT[:, kc, 128 * i:128 * (i + 1)], wba[:, kc, :], start=(kc == 0), stop=(kc == 7))
        P.copy("act", BA, psBA)
        x3 = sc_x.rearrange("p (n h) -> p n h", h=4)
        P.tt("dve", x3, BA[:, :, 4:8], dtb.rearrange("p (n h) -> p n h", h=4), ALU.add)
        P.ts("dve", sc_mx, sc_x, 0.0, None, ALU.max)
        P.ts("dve", sc_mn, sc_x, 0.0, None, ALU.min)
        P.tt("dve", sc_mn, sc_mn, sc_mx, ALU.subtract)
        P.act(sc_mn, sc_mn, AF.Exp)
        P.act(sc_mn, sc_mn, AF.Ln, bias=1.0)
        P.tt("dve", sc_mx, sc_mx, sc_mn, ALU.add)
        P.tt("dve", gS, sc_mx, negA, ALU.mult)
        P.act(betaS.rearrange("p (n h) -> p n h", h=4), BA[:, :, 0:4], AF.Sigmoid)
        bk = ringG.next()
        psg = pbf(bk)
        P.mm(psg[:, 0:64], TRI, gS)
        P.mm(psg[:, 64:128], SUF, gS)
        P.mm(psg[:, 128:192], IND0, gS)
        P.mm(psg[:, 192:256], IND1, gS)
        P.act(gamS, psg[:, 0:64], AF.Exp)
        P.act(kesS, psg[:, 64:128], AF.Exp)
        P.tt("dve", kesS, kesS, betaS, ALU.mult)
        P.act(gendS.rearrange("p a b -> p (a b)"), psg[:, 128:256], AF.Exp)
        dump("gS", gS, [128, 64]); dump("betaS", betaS, [128, 64]); dump("gamS", gamS, [128, 64])
        dump("kesS", kesS, [128, 64]); dump("gendS", gendS.rearrange("p a b -> p (a b)"), [128, 128])
        P.ts("dve", kesLo, kesS, IND0[:, 0:1], None, ALU.mult)
        P.ts("dve", kesHi, kesS, IND1[:, 0:1], None, ALU.mult)


    def G1(m):
        t0 = 512 * m
        o = qkvT[m]
        pend = [None]
        for c in range(12):
            ps = pbf(ringG.next())
            for kc in range(8):
                P.mm(ps, wqkva[:, c // 4, kc, 128 * (c % 4):128 * (c % 4 + 1)], hT[:, kc, t0:t0 + 512], start=(kc == 0), stop=(kc == 7))
            raw = rawb[2 * m + c % 2]; dgc = dgcb[2 * m + c % 2]
            for i_ in range(4):
                P.ts("dve", dgc[:, i_, :], IDENTb, cwA[:, c, i_:i_ + 1], None, ALU.mult)
            P.copy("pool", raw[:, 0:3], HALOA[:, c, :])
            P.copy("act", raw[:, 3:515], ps)
            P.copy("pool", HALOA[:, c, :], raw[:, 512:515])
            if pend[0] is not None:
                pend[0]()

            def _conv(raw=raw, dgc=dgc, c=c):
                psC = pbf(ringG.next())
                for i_ in range(4):
                    P.mm(psC, dgc[:, i_, :], raw[:, i_:i_ + 512], start=(i_ == 0), stop=(i_ == 3))
                dst = (o["q"], o["k"], o["v"])[c // 4][:, c % 4, :]
                P.act(dst, psC, AF.Silu)
            pend[0] = _conv
            if c % 3 == 2:
                nz = 4 * m + c // 3
                psZ = pbf(ringG.next())
                for kc in range(8):
                    P.mm(psZ, hT[:, kc, 128 * nz:128 * (nz + 1)], wz[:, kc, :], start=(kc == 0), stop=(kc == 7))
                szg = szgb[m][:, c // 3, :]
                P.act(szg, psZ, AF.Silu)
                P.tt("pool", szg.rearrange("p (h d) -> p h d", h=4), szg.rearrange("p (h d) -> p h d", h=4),
                     bc_mid(GNW, 4), ALU.mult)
            yield
        pend[0]()
        yield
        for c in range(8):
            dst = (o["q"], o["k"])[c // 4][:, c % 4, :]
            sc = 128.0 if c < 4 else 1.0
            sqb = sqbs[(c + m) % 3]; rtb = rtbs[(c + m) % 3]
            P.tt("pool", sqb, dst, dst, ALU.mult)
            psn = pbf(ringG.next())
            P.mm(psn, ONESb, sqb)
            P.act(rtb, psn, AF.Ln, bias=EPS * sc, scale=sc)
            P.act(rtb, rtb, AF.Exp, scale=-0.5)
            P.tt("dve", dst, dst, rtb, ALU.mult)
            yield
        if m == 0:
            dump("qnT0", o["q"].rearrange("p h t -> p (h t)"), [128, 2048])
            dump("knT0", o["k"].rearrange("p h t -> p (h t)"), [128, 2048])
            dump("vsT0", o["v"].rearrange("p h t -> p (h t)"), [128, 2048])

    def G2(n):
        tl = 128 * (n % 4)
        so = scanop[n % 3]
        st = gset[n % 2]
        qnT = qkvT[n // 4]["q"]; knT = qkvT[n // 4]["k"]; vsT = qkvT[n // 4]["v"]
        G1m = st["G1m"]; DECT = st["DECT"]; Uall = st["Uall"]; Eu = st["Eu"]
        UL0 = st["UL0"]; PWA = st["PWA"]; PWB = st["PWB"]
        TTb = st["TTb"]; vb = st["vb"]; gk = st["gk"]
        Du, Dl = UL0[:, 0], UL0[:, 1]
        gam_b = bc_last(gam3[:, n, :], 128)
        keslo_b = bc_last(kesLo3[:, n, :], 128)
        keshi_b = bc_last(kesHi3[:, n, :], 128)
        beta_b = bc_last(beta3[:, n, :], 128)
        g_b = bc_last(g3[:, n, :], 128)

        def bankf():
            return pbf(ringG.next()).rearrange("p (h d) -> p h d", h=4)

        def bankb():
            return pbb(ringG.next())[:, 0:512].rearrange("p (h d) -> p h d", h=4)

        psKV = pbb(ringG.next()).rearrange("p (a h d) -> p a h d", a=2, h=4)
        psK = psKV[:, 0]; psV = psKV[:, 1]
        for h in range(4):
            P.tr(psK[:, h, :], knT[:, h, tl:tl + 128], IDENTb)
            P.tr(psV[:, h, :], vsT[:, h, tl:tl + 128], IDENTb)
        P.tt("dve", gk, psK, gam_b, ALU.mult)
        P.tt("dve", so["KLO"], psK, keslo_b, ALU.mult)
        P.tt("dve", so["KHI"], psK, keshi_b, ALU.mult)
        P.copy("act", vb, psV)
        dg = Uall
        P.tt("pool", G1m, bc_mid(M1, 4), g_b, ALU.mult)
        P.tt("pool", dg, bc_mid(IDENT, 4), gam_b, ALU.mult)
        yield
        psD = bankf()
        P.mm(psD, IDENTb, MB64b, start=True, stop=False)
        for h in range(4):
            P.mm(psD[:, h, :], G1m[:, h, :], M2, start=False, stop=(h == 3))
        P.act(DECT, psD, AF.Exp)
        yield
        P.tt("pool", DECT, DECT, beta_b, ALU.mult)
        psG = bankf()
        for h in range(4):
            P.mm(psG[:, h, :], ONESb, dg[:, h, :])
        P.tt("dve", so["QD"], qnT[:, :, tl:tl + 128], psG, ALU.mult)
        yield
        psKK = bankf(); psQK = bankf()
        for h in range(4):
            P.mm(psKK[:, h, :], knT[:, h, tl:tl + 128], knT[:, h, tl:tl + 128])
        for h in range(4):
            P.mm(psQK[:, h, :], knT[:, h, tl:tl + 128], qnT[:, h, tl:tl + 128])
        P.tt("dve", Uall, psKK, DECT, ALU.mult)
        P.tt("dve", so["ATT"], psQK, DECT, ALU.mult)
        P.tt("pool", Du, Uall, bc_mid(S32, 4), ALU.mult)
        P.tt("pool", Eu, Uall, bc_mid(OFFM, 4), ALU.mult)
        yield
        psT = bankb()
        for h in range(4):
            P.tr(psT[:, h, :], Du[:, h, :], IDENTb)
        P.copy("act", Dl, psT)
        P.tt("pool", st["PTa"], bc_mid(IDENT, 4), Du, ALU.subtract)
        yield
        pw = [UL0, PWA, PWB, PWA, PWB]
        PT, PTn = st["PTa"], st["PTb"]
        for k in range(1, 5):
            cur = pw[k - 1]; nxt = pw[k]
            if k < 4:
                psU = bankf()
                for h in range(4):
                    P.mm(psU[:, h, :], cur[:, 1, h, :], cur[:, 0, h, :])
            psL = bankf()
            for h in range(4):
                P.mm(psL[:, h, :], cur[:, 0, h, :], cur[:, 1, h, :])
            if k > 1:
                ps3 = bankf()
                for h in range(4):
                    P.mm(ps3[:, h, :], cur[:, 1, h, :], PT[:, h, :])
            if k < 4:
                P.copy("act", nxt[:, 0], psU)
            P.copy("act", nxt[:, 1], psL)
            if k > 1:
                P.tt("dve", PTn, PT, ps3, ALU.add)
                PT, PTn = PTn, PT
            yield
        ps3 = bankf()
        for h in range(4):
            P.mm(ps3[:, h, :], PWB[:, 1, h, :], PT[:, h, :])
        P.tt("dve", PTn, PT, ps3, ALU.add)
        PT, PTn = PTn, PT
        yield
        Pm = PWA[:, 0]; XT = DECT
        p1 = bankb()
        for h in range(4):
            P.tr(p1[:, h, :], PT[:, h, :], IDENTb)
        P.copy("act", Pm, p1)
        yield
        p2 = bankf()
        for h in range(4):
            P.mm(p2[:, h, :], Eu[:, h, :], Pm[:, h, :])
        P.copy("act", XT, p2)
        yield
        p3 = bankf()
        for h in range(4):
            P.mm(p3[:, h, :], XT[:, h, :], PT[:, h, :])
        P.tt("dve", TTb, PT, p3, ALU.subtract)
        yield
        p1 = bankf(); p2 = bankf()
        for h in range(4):
            P.mm(p1[:, h, :], TTb[:, h, :], vb[:, h, :])
        for h in range(4):
            P.mm(p2[:, h, :], gk[:, h, :], TTb[:, h, :])
        P.copy("act", so["UV"], p1)
        P.copy("dve", so["WKT"], p2)
        yield

    def SCAN(n):
        so = scanop[n % 3]
        O = Obuf[n % 2]
        for half in range(2):
            r0 = 64 * half
            rs = slice(r0, r0 + 64)
            kend = so["KLO"] if half == 0 else so["KHI"]
            psA = pbf(ringG.next()).rearrange("p (h d) -> p h d", h=4)
            for h in range(4):
                P.mm(psA[:, h, :], so["WKT"][:, h, :], Sb[:, h, :])
            P.tt("dve", ub[rs], so["UV"][rs], psA[rs], ALU.subtract)
            yield
            psS = pbf(ringG.next()).rearrange("p (h d) -> p h d", h=4)
            psO = pbf(ringG.next()).rearrange("p (h d) -> p h d", h=4)
            for h in range(4):
                P.mm(psS[:, h, :], kend[:, h, :], ub[:, h, :])
            for h in range(4):
                P.mm(psO[:, h, :], so["QD"][:, h, :], Sb[:, h, :], start=True, stop=False)
                P.mm(psO[:, h, :], so["ATT"][:, h, :], ub[:, h, :], start=False, stop=True)
            for h in range(4):
                P.stt(Sst[:, h, :], Sst[:, h, :], gend4[:, half, n, h:h + 1], psS[:, h, :], ALU.mult, ALU.add)
            P.copy("act", O[rs], psO[rs])
            P.copy("act", Sb, Sst)
            yield
        if n == 0:
            dump("O0", O.rearrange("p h d -> p (h d)"), [128, 512])
        if n == 15:
            dump("O15", O.rearrange("p h d -> p (h d)"), [128, 512])
        P.act(sqO, O, AF.Square)
        P.rsum(ssq, sqO)
        yield
        P.act(rsq, ssq, AF.Ln, bias=EPS, scale=1.0 / 128)
        P.act(rsq, rsq, AF.Exp, scale=-0.5)
        szg = szgb[n // 4][:, n % 4, :].rearrange("p (h d) -> p h d", h=4)
        P.tt("dve", sqO, O, bc_last(rsq, 128), ALU.mult)
        P.tt("dve", oab, sqO, szg, ALU.mult)
        yield
        psT = pbb(ringG.next())[:, 0:512].rearrange("p (h d) -> p h d", h=4)
        for h in range(4):
            P.tr(psT[:, h, :], oab[:, h, :], IDENTb)
        P.copy("act", CT[:, 0:4, 128 * n:128 * (n + 1)], psT)
        yield

    def advance(must, opt):
        live = [True] * len(must)
        while any(live) or any(q[1] > 0 for q in opt):
            for gi, g in enumerate(must):
                if live[gi]:
                    try:
                        next(g)
                    except StopIteration:
                        live[gi] = False
            for q in opt:
                if q[1] > 0:
                    q[1] -= 1
                    try:
                        next(q[0])
                    except StopIteration:
                        q[1] = 0
                        q[2] = True

    P.memset("dve", Sst.rearrange("p h d -> p (h d)"), 0.0)
    P.memset("pool", Sb.rearrange("p h d -> p (h d)"), 0.0)
    P.memset("pool", ub.rearrange("p h d -> p (h d)"), 0.0)
    advance([G1(m_) for m_ in range(4)], [])
    SCALARS()
    g2 = {0: G2(0), 1: G2(1)}
    advance([g2[0]], [[g2[1], 7, False]])
    for n in range(NT):
        must = [SCAN(n)]
        if n + 1 < NT:
            must.append(g2[n + 1])
        opt = []
        if n + 2 < NT:
            g2[n + 2] = G2(n + 2)
            opt.append([g2[n + 2], 7, False])
        advance(must, opt)
    dump("CTa", CT[:, 0:4, :].rearrange("p k t -> p (k t)"), [128, 4 * S])
    if stage == "G":
        return finish(nc, P, out_d, dump_tags)

    bB = Bump(A, SCR_LO, SCR_HI)
    wqkvb = bB.alloc([3, 8, 512], BF16)
    for c3 in range(3):
        P.dma("pool", wqkvb[:, c3, :, :],
              win_v[:, :, 2056 + 512 * c3:2056 + 512 * (c3 + 1)], "wqkvb%d" % c3)
    QT0 = bB.alloc([S], BF16)
    KTz = [bB.alloc([S], BF16) for _ in range(2)]
    QTb = [QT0, QT0]
    P.memset("pool", KTz[0][64:128, :], 0.0)
    P.memset("pool", KTz[1][0:64, :], 0.0)
    VAll = bB.alloc([48, 4, 192], BF16)
    PTbuf = Ring([bB.alloc([1024], BF16) for _ in range(2)])
    PT3 = bB.alloc([16, 128], BF16)
    rden0 = bB.alloc([512])
    rdenb = [rden0, rden0]
    P.memset("pool", VAll[:, :, :, 64:128], 1.0)
    ringS = Ring([2, 3, 4, 5])
    ringP = Ring([6, 7])
    accR = Ring([0, 1])

    def tok_slices():
        sl = []
        for n in range(16):
            sl.append(slice(128 * n, 128 * (n + 1), 1))
        for r in range(4):
            for c in range(4):
                sl.append(slice(512 * c + r, 512 * (c + 1), 4))
        for r in range(16):
            sl.append(slice(r, S, 16))
        return sl

    TOK = tok_slices()

    def PROJ_V():
        for t in range(48):
            bank = ringP.next()
            ps = pbf(bank)
            sl = TOK[t]
            for kc in range(8):
                P.mm(ps, hT[:, kc, sl], wqkvb[:, 2, kc, :], start=(kc == 0), stop=(kc == 7))
            ps4 = ps.rearrange("p (j e d) -> p j e d", j=4, e=2)
            P.copy("act", VAll[:, t, :, 0:64], ps4[:, :, 0, :])
            P.copy("dve", VAll[:, t, :, 128:192], ps4[:, :, 1, :])

    def PROJ_B(j):
        k = j % 2
        QT = QTb[k]
        for (dst, cbase) in ((QT, 128 * j), (None, 512 + 128 * j)):
            for tb in range(4):
                bank = ringP.next()
                ps = pbf(bank)
                for kc in range(8):
                    P.mm(ps, wqkvb[:, cbase // 512, kc, cbase % 512:cbase % 512 + 128], hT[:, kc, 512 * tb:512 * (tb + 1)],
                         start=(kc == 0), stop=(kc == 7))
                cs = slice(512 * tb, 512 * (tb + 1))
                if dst is not None:
                    P.copy("dve" if tb % 2 else "act", dst[:, cs], ps)
                else:
                    P.copy("act", KTz[0][0:64, cs], ps[0:64, :])
                    P.copy("dve", KTz[1][64:128, cs], ps[64:128, :])

    def ATTN(j):
        k = j % 2
        QT = QTb[k]

        class _VA:
            def __init__(self, h):
                self.h = h

            def __getitem__(self, idx):
                e_ = self.h % 2
                return VAll[:, idx[1], self.h // 2, 64 * e_:64 * e_ + 128]

        for e in range(2):
            hp = slice(64 * e, 64 * e + 64)
            KT = KTz[e]
            fp = slice(0, 128)
            VA = _VA(2 * j + e)
            for g in range(4):
                bank = ringS.next()
                ps = pbf(bank)
                P.mm(ps, IDENTb, MBA4b, start=True, stop=False)
                for r4 in range(4):
                    r = 4 * g + r4
                    P.mm(ps[:, 128 * r4:128 * (r4 + 1)], KT[fp, r:S:16], QT[fp, r:S:16], start=False, stop=(r4 == 3))
                P.act(PT3[:, 4 * g:4 * g + 4, :], ps.rearrange("p (a d) -> p a d", a=4), AF.Exp, scale=0.125)
            for c in range(4):
                acc = pbf(accR.next())
                first = [True]

                def pv(out, lhsT, rhs, last=False):
                    P.mm(out, lhsT, rhs, start=first[0], stop=last, skip_group_check=True)
                    first[0] = False
                pt = PTbuf.next()
                bank = ringS.next(); ps = pbf(bank)
                P.mm(ps, IDENTb, MBA4b, start=True, stop=False)
                for i in range(4):
                    n = 4 * c + i
                    P.mm(ps[:, 128 * i:128 * (i + 1)], KT[fp, 128 * n:128 * (n + 1)], QT[fp, 128 * n:128 * (n + 1)],
                         start=False, stop=(i == 3))
                P.act(pt[:, 0:512], ps, AF.Exp, scale=0.125)
                bank = ringS.next(); ps = pbf(bank)
                P.mm(ps, IDENTb, MBB4b, start=True, stop=False)
                for i in range(4):
                    n = 4 * c + i
                    if n == 0:
                        continue
                    P.mm(ps[:, 128 * i:128 * (i + 1)], KT[fp, 128 * (n - 1):128 * n], QT[fp, 128 * n:128 * (n + 1)],
                         start=False, stop=(i == 3))
                P.act(pt[:, 512:1024], ps, AF.Exp, scale=0.125)
                pt1 = pt
                pt = PTbuf.next()
                bank = ringS.next(); ps = pbf(bank)
                P.mm(ps, IDENTb, MBA4b, start=True, stop=False)
                for r in range(4):
                    sl = slice(512 * c + r, 512 * (c + 1), 4)
                    P.mm(ps[:, 128 * r:128 * (r + 1)], KT[fp, sl], QT[fp, sl], start=False, stop=(r == 3))
                P.act(pt[:, 0:512], ps, AF.Exp, scale=0.125)
                if c > 0:
                    bank = ringS.next(); ps = pbf(bank)
                    P.mm(ps, IDENTb, MBB4b, start=True, stop=False)
                    for r in range(4):
                        sl = slice(512 * c + r, 512 * (c + 1), 4)
                        slk = slice(512 * (c - 1) + r, 512 * c, 4)
                        P.mm(ps[:, 128 * r:128 * (r + 1)], KT[fp, slk], QT[fp, sl], start=False, stop=(r == 3))
                    P.act(pt[:, 512:1024], ps, AF.Exp, scale=0.125)
                for i in range(4):
                    n = 4 * c + i
                    pv(acc[:, 128 * i:128 * (i + 1)], VA[:, n, e, :], pt1[:, 128 * i:128 * (i + 1)])
                    if n > 0:
                        pv(acc[:, 128 * i:128 * (i + 1)], VA[:, n - 1, e, :], pt1[:, 512 + 128 * i:512 + 128 * (i + 1)])
                for r in range(4):
                    pv(acc[:, r:512:4], VA[:, 16 + 4 * r + c, e, :], pt[:, 128 * r:128 * (r + 1)])
                    if c > 0:
                        pv(acc[:, r:512:4], VA[:, 16 + 4 * r + c - 1, e, :], pt[:, 512 + 128 * r:512 + 128 * (r + 1)])
                for r in range(16):
                    pv(acc[:, r:512:16], VA[:, 32 + r, e, :], PT3[:, r, 32 * c:32 * (c + 1)], last=(r == 15))
                num = slice(64 * e, 64 * e + 64)
                den = slice(64 * (1 - e), 64 * (1 - e) + 64)
                rd = rdenb[c % 2]
                P.recip(rd[den, :], acc[den, :])
                P.tt("dve", CT[num, 4 + j, 512 * c:512 * (c + 1)], acc[num, :], rd[den, :], ALU.mult)

    PROJ_V()
    for j in range(4):
        PROJ_B(j)
        ATTN(j)
    dump("CTb", CT[:, 4:8, :].rearrange("p k t -> p (k t)"), [128, 4 * S])
    if stage == "B":
        return finish(nc, P, out_d, dump_tags)

    bF = Bump(A, SCR_LO, SCR_HI)
    h2T = A.view(32768, [8, 1024], BF16)
    woutb = A.view(32768 + 16384, [2, 8, 512], BF16)
    X1 = bF.alloc([8, D])
    wu = [bF.alloc([2, 8, 512], BF16) for _ in range(2)]
    wd = [bF.alloc([4, D], BF16) for _ in range(2)]
    aTb = [bF.alloc([4, 1024], BF16) for _ in range(2)]
    rawF = [[bF.alloc([520]) for _ in range(2)] for _ in range(2)]
    accF = [[bF.alloc([512]) for _ in range(2)] for _ in range(2)]
    sgF = [bF.alloc([512]) for _ in range(2)]
    hbF = [bF.alloc([D], BF16), accF[1][1].bitcast(BF16)]
    junkF = sgF[0].bitcast(BF16)
    ssF = bF.alloc([8]); rsF = bF.alloc([8]); ssO = bF.alloc([8]); rsO = bF.alloc([8])
    for c2 in range(2):
        P.dma("pool", woutb[:, c2, :, :], wout_v[:, :, 512 * c2:512 * (c2 + 1)], "wout%d" % c2)
    P.dma("sp", NW1, n2_d, "c_nw1")
    P.dma("sp", NW2, nf_d, "c_nw2")
    groups = [list(range(g0, min(g0 + 4, 22))) for g0 in range(0, 22, 4)]
    out_tags = ["out%d" % i for i in range(8)]
    items = [(H, gi) for H in range(2) for gi in range(len(groups))]

    def load_wu(k):
        H, gi = items[k]
        grp = groups[gi]; g0 = grp[0]; npair = len(grp); slot = k % 2
        P.dma("pool", wu[slot][:, 0, :, 0:128 * npair], wup_v[:, :, 128 * g0:128 * (g0 + npair)], "wug%d" % slot)
        P.dma("pool", wu[slot][:, 1, :, 0:128 * npair],
              wup_v[:, :, 2816 + 128 * g0:2816 + 128 * (g0 + npair)], "wuu%d" % slot)

    def load_wd(k):
        H, gi = items[k]
        grp = groups[gi]; g0 = grp[0]; npair = len(grp); slot = k % 2
        P.dma("pool", wd[slot][:, 0:npair, :], wdn_v[:, g0:g0 + npair, :], "wd%d" % slot)

    def PRO(H):
        scr = [(junkF, ssF[:, i8:i8 + 1], rsF[:, i8:i8 + 1], hbF[i8 % 2]) for i8 in range(8)]

        def st_a(i8):
            i = 8 * H + i8
            P.dma("sp", X1[:, i8, :], x_d[128 * i:128 * (i + 1), :], "xf%d" % i8)
            b0 = 2 * (i8 % 2)
            for h2 in range(2):
                for kc in range(8):
                    P.mm(pbf(b0 + h2), CT[:, kc, 128 * i:128 * (i + 1)], woutb[:, h2, kc, :],
                         start=(kc == 0), stop=(kc == 7))
            for h2 in range(2):
                P.tt("dve", X1[:, i8, 512 * h2:512 * (h2 + 1)], X1[:, i8, 512 * h2:512 * (h2 + 1)], pbf(b0 + h2), ALU.add)
            if i == 0:
                dump("X1", X1[:, 0, :], [128, D])
            rmsnorm_stage1a(X1[:, i8, :], scr[i8])

        for t in range(10):
            if t < 8:
                st_a(t)
            if 1 <= t <= 8:
                rmsnorm_stage1b(X1[:, t - 1, :], NW1, scr[t - 1])
            if t >= 2:
                rmsnorm_stage2(h2T, 128 * (t - 2), scr[t - 2], 4 + (t % 2))

    upar = [0]

    def UGEN(k):
        H, gi = items[k]
        grp = groups[gi]; slot = k % 2; aT = aTb[k % 2]
        upend = [None]
        for p, g in enumerate(grp):
            for tb in range(2):
                par = upar[0]
                upar[0] ^= 1
                cols = slice(512 * tb, 512 * (tb + 1))
                banks = (4, 5) if par == 0 else (6, 7)
                accs = []
                for gu in range(2):
                    ps = pbf(banks[gu])
                    for kc in range(8):
                        P.mm(ps, wu[slot][:, gu, kc, 128 * p:128 * (p + 1)], h2T[:, kc, cols],
                             start=(kc == 0), stop=(kc == 7))
                    cc = g + 22 * gu
                    raw = rawF[par][gu]; acc = accF[par][gu]
                    P.copy("pool", raw[:, 0:2], HALOF[:, cc, :])
                    P.copy("act", raw[:, 2:514], ps)
                    P.copy("pool", HALOF[:, cc, :], raw[:, 512:514])
                    P.act(acc, ps, AF.Identity, scale=cwF[:, cc, 2:3])
                    P.stt(acc, raw[:, 1:513], cwF[:, cc, 1:2], acc, ALU.mult, ALU.add)
                    P.stt(acc, raw[:, 0:512], cwF[:, cc, 0:1], acc, ALU.mult, ALU.add)
                    accs.append(acc)
                if upend[0] is not None:
                    upend[0]()

                def _gate(par=par, accs=accs, p=p, cols=cols):
                    sg = sgF[par]
                    P.act(sg, accs[0], AF.Silu)
                    P.tt("pool", aT[:, p, cols], sg, accs[1], ALU.mult)
                upend[0] = _gate
                yield
        if upend[0] is not None:
            upend[0]()
            upend[0] = None
            yield

    def DGEN(k):
        H, gi = items[k]
        grp = groups[gi]; slot = k % 2; aT = aTb[k % 2]; npair = len(grp)
        last = (gi == len(groups) - 1)
        dpend = [None]
        for i8 in range(8):
            i = 8 * H + i8
            b0 = 2 * (i8 % 2)
            for h2 in range(2):
                for p in range(npair):
                    P.mm(pbf(b0 + h2), aT[:, p, 128 * i8:128 * (i8 + 1)], wd[slot][:, p, 512 * h2:512 * (h2 + 1)],
                         start=(p == 0), stop=(p == npair - 1))
            for h2 in range(2):
                P.tt("dve", X1[:, i8, 512 * h2:512 * (h2 + 1)], X1[:, i8, 512 * h2:512 * (h2 + 1)],
                     pbf(b0 + h2), ALU.add)
            if last:
                P.act(junkF, X1[:, i8, :], AF.Square, accum_out=ssO[:, i8:i8 + 1])
                P.act(rsO[:, i8:i8 + 1], ssO[:, i8:i8 + 1], AF.Ln, bias=EPS, scale=1.0 / D)
                P.act(rsO[:, i8:i8 + 1], rsO[:, i8:i8 + 1], AF.Exp, scale=-0.5)
                if dpend[0] is not None:
                    dpend[0]()

                def _fin(i8=i8, i=i):
                    P.stt(X1[:, i8, :], X1[:, i8, :], rsO[:, i8:i8 + 1], NW2, ALU.mult, ALU.mult)
                    P.dma("sp", out_d[128 * i:128 * (i + 1), :], X1[:, i8, :], "out%d" % i8)
                dpend[0] = _fin
            yield
        if last and dpend[0] is not None:
            dpend[0]()
            dpend[0] = None
            yield

    def rr(gens):
        gens = list(gens)
        live = [True] * len(gens)
        while any(live):
            for gi_, g_ in enumerate(gens):
                if live[gi_]:
                    try:
                        next(g_)
                    except StopIteration:
                        live[gi_] = False

    load_wu(0)
    prevD = None
    for k in range(len(items)):
        H, gi = items[k]
        load_wd(k)
        if k + 1 < len(items):
            load_wu(k + 1)
        if gi == 0:
            if prevD is not None:
                rr([prevD])
                prevD = None
            PRO(H)
        u = UGEN(k)
        rr([u] if prevD is None else [u, prevD])
        prevD = DGEN(k)
    rr([prevD])
    return finish(nc, P, out_d, dump_tags + out_tags)


def finish(nc, P, out_d, dump_tags):
    tags = list(dump_tags)
    P.emit(final_dma_tags=tags)
    return nc


def prep_inputs(inp):
    f = lambda a: np.ascontiguousarray(np.asarray(a, dtype=np.float32))
    x = f(inp["x"])
    rep = lambda v: np.ascontiguousarray(np.broadcast_to(f(v).reshape(1, -1), (128, f(v).size)))
    cwa = f(inp["conv_qkv_w"])[0]
    cwA = np.ascontiguousarray(cwa.T.reshape(12, 128, 4).transpose(1, 0, 2).reshape(128, 48))
    cwf = f(inp["ffn_conv_w"])[0]
    cwF = np.ascontiguousarray(cwf.T.reshape(44, 128, 3).transpose(1, 0, 2).reshape(128, 132))
    shared = {
        "w_in": f(inp["w_in"])[0], "w_out": f(inp["w_out"])[0], "w_up": f(inp["w_up"])[0],
        "w_down": f(inp["w_down"])[0],
        "n1rep": rep(inp["norm1_w"]), "n2rep": rep(inp["norm2_w"]), "nfrep": rep(inp["final_norm_w"]),
        "cwA": cwA, "cwF": cwF, "gnwrep": rep(inp["gdn_norm_w"]),
        "dtbrep": np.ascontiguousarray(np.tile(rep(inp["dt_bias"]), (1, 16))),
        "alogrep": np.ascontiguousarray(np.tile(rep(inp["a_log"]), (1, 16))),
        "consts": make_consts(),
    }
    maps = []
    for b in range(x.shape[0]):
        m = dict(shared)
        m["x"] = np.ascontiguousarray(x[b])
        maps.append(m)
    return maps


def kernel(**inputs):
    maps = prep_inputs(inputs)
    nc = build("full")
    res = run_bass_kernel_spmd(nc, maps, core_ids=list(range(8)))
    out = np.stack([np.asarray(r["out"], dtype=np.float32) for r in res.results], 0)
    return out
```

```python
import contextlib
import numpy as np
import concourse.bass as bass
import concourse.mybir as mybir
from concourse.bass_utils import run_bass_kernel_spmd

F32 = mybir.dt.float32
BF16 = mybir.dt.bfloat16
AF = mybir.ActivationFunctionType
ALU = mybir.AluOpType
AX = mybir.AxisListType

S = 2048
D = 1024
NT = 16
EPS = 1e-6
NEG = -30000.0
ENGS = ("pe", "act", "dve", "pool", "sp")


def _rect(ap):
    t = ap.tensor
    if str(ap.space) == "PSUM":
        return (t.name, 0, 128, 0, 2048)
    esz = mybir.dt.size(ap.dtype)
    pstride = 1
    for s in tuple(t.shape)[1:]:
        pstride *= s
    p0 = ap.start_partition()
    p1 = p0 + ap.partition_size()
    f0 = ap.offset - p0 * pstride
    ext = 0
    for (st, cnt) in tuple(ap.ap)[1:]:
        ext += abs(st) * (cnt - 1)
    return (t.name, p0, p1, f0 * esz, (f0 + ext + 1) * esz)


class _Op:
    __slots__ = ("eng", "fn", "idx", "is_dma", "tag", "waits", "signal", "sigval")


class Prog:
    def __init__(self, nc):
        self.nc = nc
        self.ops = {e: [] for e in ENGS}
        self.track = {}
        self.waited = {e: {} for e in ENGS}
        self.tagcount = {}

    def _add(self, eng, fn, reads, writes, is_dma=False, tag=None):
        op = _Op()
        op.eng = eng; op.fn = fn; op.is_dma = is_dma; op.tag = tag
        op.signal = False; op.sigval = None
        op.idx = len(self.ops[eng])
        op.waits = []
        deps = {}
        rrects = [_rect(a) for a in reads if a is not None and str(a.space) != "DRAM"]
        wrects = [_rect(a) for a in writes if a is not None and str(a.space) != "DRAM"]
        prects = [r for r in rrects if r[4] == 2048 and r[0].startswith("pb") and r not in wrects]
        for (nm, p0, p1, f0, f1) in rrects:
            for rec in self.track.get(nm, ()):
                if rec[5] == 1 and rec[0] < p1 and p0 < rec[1] and rec[2] < f1 and f0 < rec[3]:
                    deps[id(rec[4])] = (rec[4], True)
        for (nm, p0, p1, f0, f1) in wrects + prects:
            for rec in self.track.get(nm, ()):
                if rec[0] < p1 and p0 < rec[1] and rec[2] < f1 and f0 < rec[3]:
                    k = id(rec[4])
                    if k not in deps:
                        deps[k] = (rec[4], False)
        need = {}
        for (d, raw) in deps.values():
            if d.is_dma:
                key = ("dma", d.tag)
                val = self.tagcount[d.tag]
                if need.get(key, 0) < val:
                    need[key] = val
            else:
                if d.eng == eng and eng == "pe":
                    continue
                key = ("eng", d.eng)
                cur = need.get(key)
                if cur is None or cur.idx < d.idx:
                    need[key] = d
        w = self.waited[eng]
        for key, v in need.items():
            if key[0] == "dma":
                if w.get(key, 0) >= v:
                    continue
                w[key] = v
                op.waits.append((key, v))
            else:
                if w.get(key, -1) >= v.idx:
                    continue
                w[key] = v.idx
                v.signal = True
                op.waits.append((key, v))
        if is_dma:
            self.tagcount[tag] = self.tagcount.get(tag, 0) + 16
        for (nm, p0, p1, f0, f1) in wrects:
            lst = self.track.setdefault(nm, [])
            lst[:] = [r for r in lst if not (p0 <= r[0] and r[1] <= p1 and f0 <= r[2] and r[3] <= f1)]
            lst.append([p0, p1, f0, f1, op, 1])
        for (nm, p0, p1, f0, f1) in prects:
            self.track[nm] = [[p0, p1, f0, f1, op, 2]]
        for (nm, p0, p1, f0, f1) in rrects:
            if nm.startswith("pb"):
                continue
            lst = self.track.setdefault(nm, [])
            done = False
            for r in lst:
                if r[5] == 0 and r[4].eng == eng and (not r[4].is_dma) and (not is_dma) \
                        and r[0] == p0 and r[1] == p1 and r[2] == f0 and r[3] == f1:
                    r[4] = op
                    done = True
                    break
            if not done:
                lst.append([p0, p1, f0, f1, op, 0])
        self.ops[eng].append(op)
        return op

    def dma(self, q, out, in_, tag):
        return self._add(q, lambda e: e.dma_start(out=out, in_=in_), [in_], [out], is_dma=True, tag=tag)

    def mm(self, out, lhsT, rhs, start=True, stop=True, **kw):
        rd = [lhsT, rhs] + ([] if start else [out])
        return self._add("pe", lambda e: e.matmul(out, lhsT, rhs, start=start, stop=stop, **kw), rd, [out])

    def tr(self, out, in_, ident):
        return self._add("pe", lambda e: e.transpose(out, in_, ident), [in_, ident], [out])

    def act(self, out, in_, func, bias=None, scale=None, accum_out=None):
        kw = {}
        rd = [in_]
        if bias is not None:
            kw["bias"] = bias
            if not isinstance(bias, (int, float)):
                rd.append(bias)
        if scale is not None:
            kw["scale"] = scale
            if not isinstance(scale, (int, float)):
                rd.append(scale)
        wr = [out]
        if accum_out is not None:
            kw["accum_out"] = accum_out
            wr.append(accum_out)
        return self._add("act", lambda e: e.activation(out, in_, func, **kw), rd, wr)

    def tt(self, eng, out, in0, in1, op):
        return self._add(eng, lambda e: e.tensor_tensor(out, in0, in1, op), [in0, in1], [out])

    def ts(self, eng, out, in0, s1, s2, op0, op1=None):
        rd = [in0] + [s for s in (s1, s2) if s is not None and not isinstance(s, (int, float))]
        kw = {}
        if op1 is not None:
            kw["op1"] = op1
        return self._add(eng, lambda e: e.tensor_scalar(out, in0, s1, s2, op0, **kw), rd, [out])

    def stt(self, out, in0, scalar, in1, op0, op1):
        rd = [in0, in1] + ([] if isinstance(scalar, (int, float)) else [scalar])
        return self._add("dve", lambda e: e.scalar_tensor_tensor(out, in0, scalar, in1, op0, op1), rd, [out])

    def copy(self, eng, out, in_):
        if eng == "act":
            return self._add("act", lambda e: e.copy(out, in_), [in_], [out])
        return self._add(eng, lambda e: e.tensor_copy(out, in_), [in_], [out])

    def memset(self, eng, ap, val):
        return self._add(eng, lambda e: e.memset(ap, val), [], [ap])

    def recip(self, out, in_):
        return self._add("dve", lambda e: e.reciprocal(out, in_), [in_], [out])

    def rsum(self, out, in_):
        return self._add("dve", lambda e: e.tensor_reduce(out, in_, AX.X, ALU.add), [in_], [out])

    def emit(self, final_dma_tags=()):
        nc = self.nc
        for e in ENGS:
            c = 0
            for op in self.ops[e]:
                if op.signal and not op.is_dma:
                    c += 1
                    op.sigval = c
        with contextlib.ExitStack() as st:
            esem = {e: st.enter_context(nc.semaphore("s_" + e)) for e in ENGS}
            dsem = {t: st.enter_context(nc.semaphore("d_%d" % i)) for i, t in enumerate(self.tagcount)}
            block = st.enter_context(nc.Block())
            engobj = {"pe": block.tensor, "act": block.scalar, "dve": block.vector,
                      "pool": block.gpsimd, "sp": block.sync}

            def make(ename):
                def body(eng):
                    for op in self.ops[ename]:
                        for (key, v) in op.waits:
                            if key[0] == "dma":
                                eng.wait_ge(dsem[key[1]], v)
                            else:
                                eng.wait_ge(esem[key[1]], v.sigval)
                        ins = op.fn(eng)
                        if op.is_dma:
                            ins.then_inc(dsem[op.tag], 16)
                        elif op.signal:
                            ins.then_inc(esem[ename], 1)
                    if ename == "sp":
                        for t in final_dma_tags:
                            eng.wait_ge(dsem[t], self.tagcount[t])
                return body

            for e in ENGS:
                engobj[e](make(e))
        return nc


def _prod(shape):
    n = 1
    for s in shape:
        n *= s
    return n


class Arena:
    def __init__(self, nc, nbytes):
        self.t = nc.alloc_sbuf_tensor("arena", [128, nbytes // 4], F32)
        self.nbytes = nbytes

    def view(self, off, shape, dt):
        n = _prod(shape)
        esz = 4 if dt == F32 else 2
        assert off % 4 == 0 and (n * esz) % 4 == 0 and off + n * esz <= self.nbytes, (off, shape)
        ap = self.t[:, off // 4:(off + n * esz) // 4]
        if dt != F32:
            ap = ap.bitcast(dt)
        if len(shape) > 1:
            names = " ".join("d%d" % i for i in range(len(shape)))
            kw = {"d%d" % i: shape[i] for i in range(1, len(shape))}
            ap = ap.rearrange("p (%s) -> p %s" % (names, names), **kw)
        return ap


class Bump:
    def __init__(self, arena, lo, hi):
        self.a = arena; self.lo = lo; self.hi = hi; self.cur = lo

    def alloc(self, shape, dt=F32):
        esz = 4 if dt == F32 else 2
        n = (_prod(shape) * esz + 31) // 32 * 32
        off = self.cur
        self.cur += n
        assert self.cur <= self.hi, ("arena overflow", self.cur, self.hi)
        return self.a.view(off, shape, dt)


class Ring:
    def __init__(self, items):
        self.items = list(items); self.i = 0

    def next(self):
        x = self.items[self.i % len(self.items)]
        self.i += 1
        return x


_CNAMES = ["IDENT", "M1", "M2", "TRI", "SUF", "IND0", "IND1", "S32", "OFF", "ONES"]
_CB = {"MB64": 512, "MBA4": 512, "MBB4": 512}
NCF = 128 * len(_CNAMES)
NCONST = NCF + 512 * 3


def make_consts():
    i = np.arange(128)[:, None]
    j = np.arange(128)[None, :]
    same = (i // 64) == (j // 64)
    c = {}
    c["IDENT"] = (i == j)
    c["M1"] = (i > j)
    c["M2"] = (i <= j)
    c["TRI"] = (i <= j) & same
    c["SUF"] = (i > j) & same
    c["IND0"] = (i < 64) & (j >= 0)
    c["IND1"] = (i >= 64) & (j >= 0)
    c["S32"] = (i < j) & ((i // 32) == (j // 32))
    c["OFF"] = same & ((i % 64) < 32) & ((j % 64) >= 32)
    c["ONES"] = np.ones((128, 128), bool)
    cols = [c[n].astype(np.float32) for n in _CNAMES]
    mb64 = np.where((i <= j) & same, 0.0, NEG).astype(np.float32)
    mba = np.where(i <= j, 0.0, NEG).astype(np.float32)
    mbb = np.where(i >= j, 0.0, NEG).astype(np.float32)
    cols += [np.tile(mb64, (1, 4)), np.tile(mba, (1, 4)), np.tile(mbb, (1, 4))]
    return np.ascontiguousarray(np.concatenate(cols, 1))


def build(stage="full", dumps=()):
    nc = bass.Bass("TRN2", target_bir_lowering=False)
    P = Prog(nc)
    dumps = set(dumps)
    dump_tags = []

    def din(name, shape):
        return nc.dram_tensor(name, list(shape), F32, kind="ExternalInput").ap()

    x_d = din("x", [S, D])
    win_d = din("w_in", [D, 3592])
    wout_d = din("w_out", [D, D])
    wup_d = din("w_up", [D, 5632])
    wdn_d = din("w_down", [2816, D])
    n1_d = din("n1rep", [128, D])
    n2_d = din("n2rep", [128, D])
    nf_d = din("nfrep", [128, D])
    cwa_d = din("cwA", [128, 48])
    cwf_d = din("cwF", [128, 132])
    gnw_d = din("gnwrep", [128, 128])
    dtb_d = din("dtbrep", [128, 64])
    alog_d = din("alogrep", [128, 64])
    cst_d = din("consts", [128, NCONST])
    out_d = nc.dram_tensor("out", [S, D], F32, kind="ExternalOutput").ap()

    def dump(name, sb_ap, shape):
        if name not in dumps:
            return
        d = nc.dram_tensor("dbg_" + name, list(shape), sb_ap.dtype, kind="ExternalOutput").ap()
        tg = "dbg_" + name
        P.dma("sp", d, sb_ap, tg)
        dump_tags.append(tg)

    win_v = win_d.rearrange("(k p) c -> p k c", p=128)
    wout_v = wout_d.rearrange("(k p) c -> p k c", p=128)
    wup_v = wup_d.rearrange("(k p) c -> p k c", p=128)
    wdn_v = wdn_d.rearrange("(k p) c -> p k c", p=128)

    ARENA_BYTES = 207872
    A = Arena(nc, ARENA_BYTES)
    pb = [nc.alloc_psum_tensor("pb%d" % i, [128, 512], F32) for i in range(8)]

    def pbf(i):
        return pb[i][:]

    def pbb(i):
        return pb[i][:].bitcast(BF16)

    CT = A.view(0, [8, S], BF16)
    hT = A.view(32768, [8, S], BF16)
    pers = Bump(A, 65536, 86016)
    CF = pers.alloc([NCF])
    CBm = pers.alloc([1536 + 256], BF16)
    cwA = pers.alloc([12, 4])
    cwF = pers.alloc([44, 3])
    HALOA = pers.alloc([12, 3])
    HALOF = pers.alloc([44, 2])
    GNW = pers.alloc([128])
    NW1 = pers.alloc([D])
    NW2 = pers.alloc([D])
    SCR_LO = 86016
    SCR_HI = ARENA_BYTES

    def cf(name):
        k = _CNAMES.index(name)
        return CF[:, 128 * k:128 * (k + 1)]

    IDENT = cf("IDENT"); M1 = cf("M1"); M2 = cf("M2"); TRI = cf("TRI"); SUF = cf("SUF")
    IND0 = cf("IND0"); IND1 = cf("IND1"); S32 = cf("S32"); OFFM = cf("OFF"); ONES = cf("ONES")
    MB64b = CBm[:, 0:512]; MBA4b = CBm[:, 512:1024]; MBB4b = CBm[:, 1024:1536]
    IDENTb = CBm[:, 1536:1664]; ONESb = CBm[:, 1664:1792]

    P.dma("sp", CF, cst_d[:, 0:NCF], "c_cf")
    ctmp = A.view(16384 + 8192, [1536], F32)
    P.dma("sp", ctmp, cst_d[:, NCF:NCONST], "c_tmp")
    P.copy("dve", CBm[:, 0:1536], ctmp)
    P.copy("dve", IDENTb, IDENT)
    P.copy("dve", ONESb, ONES)
    P.dma("sp", cwA.rearrange("p a b -> p (a b)"), cwa_d, "c_cwa")
    P.dma("sp", cwF.rearrange("p a b -> p (a b)"), cwf_d, "c_cwf")
    P.dma("sp", GNW, gnw_d, "c_gnw")
    P.dma("sp", NW1, n1_d, "c_nw1")
    P.memset("pool", HALOA.rearrange("p a b -> p (a b)"), 0.0)
    P.memset("pool", HALOF.rearrange("p a b -> p (a b)"), 0.0)

    def bc_last(ap, n):
        return ap.unsqueeze(2).broadcast_to([128, ap.shape[1], n])

    def bc_mid(ap, n):
        return ap.unsqueeze(1).broadcast_to([128, n, ap.shape[1]])

    def rmsnorm_stage1(src_tile, wtile, scr):
        junk, ssv, rst, hb = scr
        P.act(junk, src_tile, AF.Square, accum_out=ssv)
        P.act(rst, ssv, AF.Ln, bias=EPS, scale=1.0 / D)
        P.act(rst, rst, AF.Exp, scale=-0.5)
        P.stt(hb, src_tile, rst, wtile, ALU.mult, ALU.mult)

    def rmsnorm_stage1a(src_tile, scr):
        junk, ssv, rst, hb = scr
        P.act(junk, src_tile, AF.Square, accum_out=ssv)
        P.act(rst, ssv, AF.Ln, bias=EPS, scale=1.0 / D)
        P.act(rst, rst, AF.Exp, scale=-0.5)

    def rmsnorm_stage1b(src_tile, wtile, scr):
        junk, ssv, rst, hb = scr
        P.stt(hb, src_tile, rst, wtile, ALU.mult, ALU.mult)

    def rmsnorm_stage2(dstT, col0, scr, bank):
        hb = scr[3]
        psT = pbb(bank)
        for kc in range(8):
            P.tr(psT[:, 128 * kc:128 * (kc + 1)], hb[:, 128 * kc:128 * (kc + 1)], IDENTb)
        P.copy("act", dstT[:, :, col0:col0 + 128], psT.rearrange("p (k t) -> p k t", k=8))

    bA = Bump(A, 0, 16384)
    xbuf = [bA.alloc([D]) for _ in range(3)]
    junkA = bA.alloc([D], BF16)
    hbuf = [NW2.bitcast(BF16)[:, 0:D], NW2.bitcast(BF16)[:, D:2 * D]]
    ssA = bA.alloc([NT])
    rsA = bA.alloc([NT])
    scrA = [(junkA, ssA[:, i:i + 1], rsA[:, i:i + 1], hbuf[i % 2]) for i in range(NT)]
    for i in range(NT + 2):
        if i < NT:
            xt = xbuf[i % 3]
            P.dma("sp", xt, x_d[128 * i:128 * (i + 1), :], "xa%d" % (i % 3))
            rmsnorm_stage1a(xt, scrA[i])
        if 1 <= i <= NT:
            rmsnorm_stage1b(xbuf[(i - 1) % 3], NW1, scrA[i - 1])
        if i >= 2:
            rmsnorm_stage2(hT, 128 * (i - 2), scrA[i - 2], 6 + (i % 2))
    dump("hT", hT.rearrange("p k t -> p (k t)"), [128, 8 * S])
    if stage == "A":
        return finish(nc, P, out_d, dump_tags)

    bG = Bump(A, SCR_LO, SCR_HI)
    bC = Bump(A, 16384, 32768)
    wz = bC.alloc([8, 512], BF16)
    qkvT = [dict(q=bG.alloc([4, 512], BF16), k=bG.alloc([4, 512], BF16), v=bG.alloc([4, 512], BF16))
            for _ in range(4)]
    szgb = [bG.alloc([4, 512], BF16), bG.alloc([4, 512], BF16), bG.alloc([4, 512], BF16), bC.alloc([4, 512], BF16)]
    BA = bG.alloc([NT, 8])
    sc_x = bG.alloc([64]); sc_mx = bG.alloc([64]); sc_mn = bG.alloc([64])
    dtb = bG.alloc([64]); negA = bG.alloc([64])
    gS = bG.alloc([64]); betaS = bG.alloc([64]); gamS = bG.alloc([64]); kesS = bG.alloc([64])
    gendS = bG.alloc([2, 64])
    wba = bG.alloc([8, 8], BF16)
    Sst = bG.alloc([4, 128]); Sb = bG.alloc([4, 128], BF16); ub = bG.alloc([4, 128], BF16)
    Obuf = [bC.alloc([4, 128]) for _ in range(2)]
    sqO = bG.alloc([4, 128], BF16); oab = sqO
    kesLo = bG.alloc([64]); kesHi = bG.alloc([64])
    ssq = bG.alloc([4]); rsq = bG.alloc([4])
    ov0 = bG.cur
    gset = []
    for _ in range(2):
        gset.append(dict(
            G1m=bG.alloc([4, 128]),
            DECT=bG.alloc([4, 128], BF16), Uall=bG.alloc([4, 128], BF16), Eu=bG.alloc([4, 128], BF16),
            PTa=bG.alloc([4, 128], BF16), PTb=bG.alloc([4, 128], BF16),
            UL0=bG.alloc([2, 4, 128], BF16), PWA=bG.alloc([2, 4, 128], BF16), PWB=bG.alloc([2, 4, 128], BF16),
            TTb=bG.alloc([4, 128], BF16), vb=bG.alloc([4, 128], BF16), gk=bG.alloc([4, 128], BF16)))
    scanop = []
    for _ in range(3):
        scanop.append(dict(KLO=bG.alloc([4, 128], BF16), KHI=bG.alloc([4, 128], BF16), ATT=bG.alloc([4, 128], BF16),
                           QD=bG.alloc([4, 128], BF16), WKT=bG.alloc([4, 128], BF16),
                           UV=bG.alloc([4, 128], BF16)))
    bO = Bump(A, ov0, bG.cur)
    wqkva = bO.alloc([3, 8, 512], BF16)
    rawb = [bO.alloc([520], BF16) for _ in range(8)]
    dgcb = [bO.alloc([2, 128], BF16) for _ in range(8)]
    accb2 = [bO.alloc([512]) for _ in range(2)]
    acccnt = [0]
    sqbs = [bO.alloc([512], BF16) for _ in range(3)]
    rtbs = [bO.alloc([512]) for _ in range(3)]
    for c3 in range(3):
        P.dma("pool", wqkva[:, c3, :, :], win_v[:, :, 512 * c3:512 * (c3 + 1)], "wqkva%d" % c3)
    P.dma("pool", wba, win_v[:, :, 2048:2056], "wba")
    P.dma("pool", wz, win_v[:, :, 1536:2048], "wz")

    ringG = Ring([0, 1, 2, 3, 4, 5, 6, 7])

    g3 = gS.rearrange("p (n h) -> p n h", h=4)
    beta3 = betaS.rearrange("p (n h) -> p n h", h=4)
    gam3 = gamS.rearrange("p (n h) -> p n h", h=4)
    kesLo3 = kesLo.rearrange("p (n h) -> p n h", h=4)
    kesHi3 = kesHi.rearrange("p (n h) -> p n h", h=4)
    gend4 = gendS.rearrange("p a (n h) -> p a n h", h=4)

    def SCALARS():
        P.dma("sp", dtb, dtb_d, "c_dtb")
        P.dma("sp", negA, alog_d, "c_alog")
        P.act(negA, negA, AF.Exp)
        P.ts("dve", negA, negA, -1.0, None, ALU.mult)
        bk = ringG.next()
        psBA = pbf(bk)[:, 0:128].rearrange("p (n c) -> p n c", c=8)
        for i in range(NT):
            for kc in range(8):
                P.mm(psBA[:, i, :], hT[:, kc, 128 * i:128 * (i + 1)], wba[:, kc, :], start=(kc == 0), stop=(kc == 7))
        P.copy("act", BA, psBA)
        x3 = sc_x.rearrange("p (n h) -> p n h", h=4)
        P.tt("dve", x3, BA[:, :, 4:8], dtb.rearrange("p (n h) -> p n h", h=4), ALU.add)
        P.ts("dve", sc_mx, sc_x, 0.0, None, ALU.max)
        P.ts("dve", sc_mn, sc_x, 0.0, None, ALU.min)
        P.tt("dve", sc_mn, sc_mn, sc_mx, ALU.subtract)
        P.act(sc_mn, sc_mn, AF.Exp)
        P.act(sc_mn, sc_mn, AF.Ln, bias=1.0)
        P.tt("dve", sc_mx, sc_mx, sc_mn, ALU.add)
        P.tt("dve", gS, sc_mx, negA, ALU.mult)
        P.act(betaS.rearrange("p (n h) -> p n h", h=4), BA[:, :, 0:4], AF.Sigmoid)
        bk = ringG.next()
        psg = pbf(bk)
        P.mm(psg[:, 0:64], TRI, gS)
        P.mm(psg[:, 64:128], SUF, gS)
        P.mm(psg[:, 128:192], IND0, gS)
        P.mm(psg[:, 192:256], IND1, gS)
        P.act(gamS, psg[:, 0:64], AF.Exp)
        P.act(kesS, psg[:, 64:128], AF.Exp)
        P.tt("dve", kesS, kesS, betaS, ALU.mult)
        P.act(gendS.rearrange("p a b -> p (a b)"), psg[:, 128:256], AF.Exp)
        dump("gS", gS, [128, 64]); dump("betaS", betaS, [128, 64]); dump("gamS", gamS, [128, 64])
        dump("kesS", kesS, [128, 64]); dump("gendS", gendS.rearrange("p a b -> p (a b)"), [128, 128])
        P.ts("dve", kesLo, kesS, IND0[:, 0:1], None, ALU.mult)
        P.ts("dve", kesHi, kesS, IND1[:, 0:1], None, ALU.mult)


    def G1(m):
        t0 = 512 * m
        o = qkvT[m]
        pend = [None]
        for c in range(12):
            ps = pbf(ringG.next())
            for kc in range(8):
                P.mm(ps, wqkva[:, c // 4, kc, 128 * (c % 4):128 * (c % 4 + 1)], hT[:, kc, t0:t0 + 512], start=(kc == 0), stop=(kc == 7))
            raw = rawb[2 * m + c % 2]; dgc = dgcb[2 * m + c % 2]
            for i_ in (2, 3):
                P.ts("dve", dgc[:, i_ - 2, :], IDENTb, cwA[:, c, i_:i_ + 1], None, ALU.mult)
            P.copy("pool", raw[:, 0:3], HALOA[:, c, :])
            P.copy("act", raw[:, 3:515], ps)
            P.copy("pool", HALOA[:, c, :], raw[:, 512:515])
            if pend[0] is not None:
                pend[0]()

            def _conv(raw=raw, dgc=dgc, c=c):
                psC = pbf(ringG.next())
                P.mm(psC, dgc[:, 0, :], raw[:, 2:514], start=True, stop=False)
                P.mm(psC, dgc[:, 1, :], raw[:, 3:515], start=False, stop=True)
                acc = accb2[acccnt[0] % 2]
                acccnt[0] += 1
                P.stt(acc, raw[:, 1:513], cwA[:, c, 1:2], psC, ALU.mult, ALU.add)
                P.stt(acc, raw[:, 0:512], cwA[:, c, 0:1], acc, ALU.mult, ALU.add)
                dst = (o["q"], o["k"], o["v"])[c // 4][:, c % 4, :]
                P.act(dst, acc, AF.Silu)
            pend[0] = _conv
            if c % 3 == 2:
                nz = 4 * m + c // 3
                psZ = pbf(ringG.next())
                for kc in range(8):
                    P.mm(psZ, hT[:, kc, 128 * nz:128 * (nz + 1)], wz[:, kc, :], start=(kc == 0), stop=(kc == 7))
                szg = szgb[m][:, c // 3, :]
                P.act(szg, psZ, AF.Silu)
                P.tt("pool", szg.rearrange("p (h d) -> p h d", h=4), szg.rearrange("p (h d) -> p h d", h=4),
                     bc_mid(GNW, 4), ALU.mult)
            yield
        pend[0]()
        yield
        for c in range(8):
            dst = (o["q"], o["k"])[c // 4][:, c % 4, :]
            sc = 128.0 if c < 4 else 1.0
            sqb = sqbs[(c + m) % 3]; rtb = rtbs[(c + m) % 3]
            P.tt("pool", sqb, dst, dst, ALU.mult)
            psn = pbf(ringG.next())
            P.mm(psn, ONESb, sqb)
            P.act(rtb, psn, AF.Ln, bias=EPS * sc, scale=sc)
            P.act(rtb, rtb, AF.Exp, scale=-0.5)
            P.tt("dve", dst, dst, rtb, ALU.mult)
            yield
        if m == 0:
            dump("qnT0", o["q"].rearrange("p h t -> p (h t)"), [128, 2048])
            dump("knT0", o["k"].rearrange("p h t -> p (h t)"), [128, 2048])
            dump("vsT0", o["v"].rearrange("p h t -> p (h t)"), [128, 2048])

    def G2(n):
        tl = 128 * (n % 4)
        so = scanop[n % 3]
        st = gset[n % 2]
        qnT = qkvT[n // 4]["q"]; knT = qkvT[n // 4]["k"]; vsT = qkvT[n // 4]["v"]
        G1m = st["G1m"]; DECT = st["DECT"]; Uall = st["Uall"]; Eu = st["Eu"]
        UL0 = st["UL0"]; PWA = st["PWA"]; PWB = st["PWB"]
        TTb = st["TTb"]; vb = st["vb"]; gk = st["gk"]
        Du, Dl = UL0[:, 0], UL0[:, 1]
        gam_b = bc_last(gam3[:, n, :], 128)
        keslo_b = bc_last(kesLo3[:, n, :], 128)
        keshi_b = bc_last(kesHi3[:, n, :], 128)
        beta_b = bc_last(beta3[:, n, :], 128)
        g_b = bc_last(g3[:, n, :], 128)

        def bankf():
            return pbf(ringG.next()).rearrange("p (h d) -> p h d", h=4)

        def bankb():
            return pbb(ringG.next())[:, 0:512].rearrange("p (h d) -> p h d", h=4)

        psKV = pbb(ringG.next()).rearrange("p (a h d) -> p a h d", a=2, h=4)
        psK = psKV[:, 0]; psV = psKV[:, 1]
        for h in range(4):
            P.tr(psK[:, h, :], knT[:, h, tl:tl + 128], IDENTb)
            P.tr(psV[:, h, :], vsT[:, h, tl:tl + 128], IDENTb)
        P.tt("dve", gk, psK, gam_b, ALU.mult)
        P.tt("dve", so["KLO"], psK, keslo_b, ALU.mult)
        P.tt("dve", so["KHI"], psK, keshi_b, ALU.mult)
        P.copy("act", vb, psV)
        dg = Uall
        P.tt("pool", G1m, bc_mid(M1, 4), g_b, ALU.mult)
        P.tt("pool", dg, bc_mid(IDENT, 4), gam_b, ALU.mult)
        yield
        psD = bankf()
        P.mm(psD, IDENTb, MB64b, start=True, stop=False)
        for h in range(4):
            P.mm(psD[:, h, :], G1m[:, h, :], M2, start=False, stop=(h == 3))
        P.act(DECT, psD, AF.Exp)
        yield
        P.tt("pool", DECT, DECT, beta_b, ALU.mult)
        psG = bankf()
        for h in range(4):
            P.mm(psG[:, h, :], ONESb, dg[:, h, :])
        P.tt("dve", so["QD"], qnT[:, :, tl:tl + 128], psG, ALU.mult)
        yield
        psKK = bankf(); psQK = bankf()
        for h in range(4):
            P.mm(psKK[:, h, :], knT[:, h, tl:tl + 128], knT[:, h, tl:tl + 128])
        for h in range(4):
            P.mm(psQK[:, h, :], knT[:, h, tl:tl + 128], qnT[:, h, tl:tl + 128])
        P.tt("dve", Uall, psKK, DECT, ALU.mult)
        P.tt("dve", so["ATT"], psQK, DECT, ALU.mult)
        P.tt("pool", Du, Uall, bc_mid(S32, 4), ALU.mult)
        P.tt("pool", Eu, Uall, bc_mid(OFFM, 4), ALU.mult)
        yield
        psT = bankb()
        for h in range(4):
            P.tr(psT[:, h, :], Du[:, h, :], IDENTb)
        P.copy("act", Dl, psT)
        P.tt("pool", st["PTa"], bc_mid(IDENT, 4), Du, ALU.subtract)
        yield
        pw = [UL0, PWA, PWB, PWA, PWB]
        PT, PTn = st["PTa"], st["PTb"]
        for k in range(1, 5):
            cur = pw[k - 1]; nxt = pw[k]
            if k < 4:
                psU = bankf()
                for h in range(4):
                    P.mm(psU[:, h, :], cur[:, 1, h, :], cur[:, 0, h, :])
            psL = bankf()
            for h in range(4):
                P.mm(psL[:, h, :], cur[:, 0, h, :], cur[:, 1, h, :])
            if k > 1:
                ps3 = bankf()
                for h in range(4):
                    P.mm(ps3[:, h, :], cur[:, 1, h, :], PT[:, h, :])
            if k < 4:
                P.copy("act", nxt[:, 0], psU)
            P.copy("act", nxt[:, 1], psL)
            if k > 1:
                P.tt("dve", PTn, PT, ps3, ALU.add)
                PT, PTn = PTn, PT
            yield
        ps3 = bankf()
        for h in range(4):
            P.mm(ps3[:, h, :], PWB[:, 1, h, :], PT[:, h, :])
        P.tt("dve", PTn, PT, ps3, ALU.add)
        PT, PTn = PTn, PT
        yield
        Pm = PWA[:, 0]; XT = DECT
        p1 = bankb()
        for h in range(4):
            P.tr(p1[:, h, :], PT[:, h, :], IDENTb)
        P.copy("act", Pm, p1)
        yield
        p2 = bankf()
        for h in range(4):
            P.mm(p2[:, h, :], Eu[:, h, :], Pm[:, h, :])
        P.copy("act", XT, p2)
        yield
        p3 = bankf()
        for h in range(4):
            P.mm(p3[:, h, :], XT[:, h, :], PT[:, h, :])
        P.tt("dve", TTb, PT, p3, ALU.subtract)
        yield
        p1 = bankf(); p2 = bankf()
        for h in range(4):
            P.mm(p1[:, h, :], TTb[:, h, :], vb[:, h, :])
        for h in range(4):
            P.mm(p2[:, h, :], gk[:, h, :], TTb[:, h, :])
        P.copy("act", so["UV"], p1)
        P.copy("dve", so["WKT"], p2)
        yield

    def SCAN(n):
        so = scanop[n % 3]
        O = Obuf[n % 2]
        for half in range(2):
            r0 = 64 * half
            rs = slice(r0, r0 + 64)
            kend = so["KLO"] if half == 0 else so["KHI"]
            psA = pbf(ringG.next()).rearrange("p (h d) -> p h d", h=4)
            for h in range(4):
                P.mm(psA[:, h, :], so["WKT"][:, h, :], Sb[:, h, :])
            P.tt("dve", ub[rs], so["UV"][rs], psA[rs], ALU.subtract)
            yield
            psS = pbf(ringG.next()).rearrange("p (h d) -> p h d", h=4)
            psO = pbf(ringG.next()).rearrange("p (h d) -> p h d", h=4)
            for h in range(4):
                P.mm(psS[:, h, :], kend[:, h, :], ub[:, h, :])
            for h in range(4):
                P.mm(psO[:, h, :], so["QD"][:, h, :], Sb[:, h, :], start=True, stop=False)
                P.mm(psO[:, h, :], so["ATT"][:, h, :], ub[:, h, :], start=False, stop=True)
            for h in range(4):
                P.stt(Sst[:, h, :], Sst[:, h, :], gend4[:, half, n, h:h + 1], psS[:, h, :], ALU.mult, ALU.add)
            P.copy("act", O[rs], psO[rs])
            P.copy("act", Sb, Sst)
            yield
        if n == 0:
            dump("O0", O.rearrange("p h d -> p (h d)"), [128, 512])
        if n == 15:
            dump("O15", O.rearrange("p h d -> p (h d)"), [128, 512])
        P.act(sqO, O, AF.Square)
        P.rsum(ssq, sqO)
        yield
        P.act(rsq, ssq, AF.Ln, bias=EPS, scale=1.0 / 128)
        P.act(rsq, rsq, AF.Exp, scale=-0.5)
        szg = szgb[n // 4][:, n % 4, :].rearrange("p (h d) -> p h d", h=4)
        P.tt("dve", sqO, O, bc_last(rsq, 128), ALU.mult)
        P.tt("dve", oab, sqO, szg, ALU.mult)
        yield
        psT = pbb(ringG.next())[:, 0:512].rearrange("p (h d) -> p h d", h=4)
        for h in range(4):
            P.tr(psT[:, h, :], oab[:, h, :], IDENTb)
        P.copy("act", CT[:, 0:4, 128 * n:128 * (n + 1)], psT)
        yield

    def advance(must, opt):
        live = [True] * len(must)
        while any(live) or any(q[1] > 0 for q in opt):
            for gi, g in enumerate(must):
                if live[gi]:
                    try:
                        next(g)
                    except StopIteration:
                        live[gi] = False
            for q in opt:
                if q[1] > 0:
                    q[1] -= 1
                    try:
                        next(q[0])
                    except StopIteration:
                        q[1] = 0
                        q[2] = True

    P.memset("dve", Sst.rearrange("p h d -> p (h d)"), 0.0)
    P.memset("pool", Sb.rearrange("p h d -> p (h d)"), 0.0)
    P.memset("pool", ub.rearrange("p h d -> p (h d)"), 0.0)
    advance([G1(m_) for m_ in range(4)], [])
    SCALARS()
    g2 = {0: G2(0), 1: G2(1)}
    advance([g2[0]], [[g2[1], 7, False]])
    for n in range(NT):
        must = [SCAN(n)]
        if n + 1 < NT:
            must.append(g2[n + 1])
        opt = []
        if n + 2 < NT:
            g2[n + 2] = G2(n + 2)
            opt.append([g2[n + 2], 7, False])
        advance(must, opt)
    dump("CTa", CT[:, 0:4, :].rearrange("p k t -> p (k t)"), [128, 4 * S])
    if stage == "G":
        return finish(nc, P, out_d, dump_tags)

    bB = Bump(A, SCR_LO, SCR_HI)
    wqkvb = bB.alloc([3, 8, 512], BF16)
    for c3 in range(3):
        P.dma("pool", wqkvb[:, c3, :, :],
              win_v[:, :, 2056 + 512 * c3:2056 + 512 * (c3 + 1)], "wqkvb%d" % c3)
    QT0 = bB.alloc([S], BF16)
    KTz = [bB.alloc([S], BF16) for _ in range(2)]
    QTb = [QT0, QT0]
    P.memset("pool", KTz[0][64:128, :], 0.0)
    P.memset("pool", KTz[1][0:64, :], 0.0)
    VAll = bB.alloc([48, 4, 192], BF16)
    PTbuf = Ring([bB.alloc([1024], BF16) for _ in range(2)])
    PT3 = bB.alloc([16, 128], BF16)
    rden0 = bB.alloc([512])
    rdenb = [rden0, rden0]
    P.memset("pool", VAll[:, :, :, 64:128], 1.0)
    ringS = Ring([2, 3, 4, 5])
    ringP = Ring([6, 7])
    accR = Ring([0, 1])

    def tok_slices():
        sl = []
        for n in range(16):
            sl.append(slice(128 * n, 128 * (n + 1), 1))
        for r in range(4):
            for c in range(4):
                sl.append(slice(512 * c + r, 512 * (c + 1), 4))
        for r in range(16):
            sl.append(slice(r, S, 16))
        return sl

    TOK = tok_slices()

    def PROJ_V():
        for t in range(48):
            bank = ringP.next()
            ps = pbf(bank)
            sl = TOK[t]
            for kc in range(8):
                P.mm(ps, hT[:, kc, sl], wqkvb[:, 2, kc, :], start=(kc == 0), stop=(kc == 7))
            ps4 = ps.rearrange("p (j e d) -> p j e d", j=4, e=2)
            P.copy("act", VAll[:, t, :, 0:64], ps4[:, :, 0, :])
            P.copy("dve", VAll[:, t, :, 128:192], ps4[:, :, 1, :])

    def PROJ_B(j):
        k = j % 2
        QT = QTb[k]
        for (dst, cbase) in ((QT, 128 * j), (None, 512 + 128 * j)):
            for tb in range(4):
                bank = ringP.next()
                ps = pbf(bank)
                for kc in range(8):
                    P.mm(ps, wqkvb[:, cbase // 512, kc, cbase % 512:cbase % 512 + 128], hT[:, kc, 512 * tb:512 * (tb + 1)],
                         start=(kc == 0), stop=(kc == 7))
                cs = slice(512 * tb, 512 * (tb + 1))
                if dst is not None:
                    P.copy("dve" if tb % 2 else "act", dst[:, cs], ps)
                else:
                    P.copy("act", KTz[0][0:64, cs], ps[0:64, :])
                    P.copy("dve", KTz[1][64:128, cs], ps[64:128, :])

    def ATTN(j):
        k = j % 2
        QT = QTb[k]

        class _VA:
            def __init__(self, h):
                self.h = h

            def __getitem__(self, idx):
                e_ = self.h % 2
                return VAll[:, idx[1], self.h // 2, 64 * e_:64 * e_ + 128]

        for e in range(2):
            hp = slice(64 * e, 64 * e + 64)
            KT = KTz[e]
            fp = slice(0, 128)
            VA = _VA(2 * j + e)
            for g in range(4):
                bank = ringS.next()
                ps = pbf(bank)
                P.mm(ps, IDENTb, MBA4b, start=True, stop=False)
                for r4 in range(4):
                    r = 4 * g + r4
                    P.mm(ps[:, 128 * r4:128 * (r4 + 1)], KT[fp, r:S:16], QT[fp, r:S:16], start=False, stop=(r4 == 3))
                P.act(PT3[:, 4 * g:4 * g + 4, :], ps.rearrange("p (a d) -> p a d", a=4), AF.Exp, scale=0.125)
            for c in range(4):
                acc = pbf(accR.next())
                first = [True]

                def pv(out, lhsT, rhs, last=False):
                    P.mm(out, lhsT, rhs, start=first[0], stop=last, skip_group_check=True)
                    first[0] = False
                pt = PTbuf.next()
                bank = ringS.next(); ps = pbf(bank)
                P.mm(ps, IDENTb, MBA4b, start=True, stop=False)
                for i in range(4):
                    n = 4 * c + i
                    P.mm(ps[:, 128 * i:128 * (i + 1)], KT[fp, 128 * n:128 * (n + 1)], QT[fp, 128 * n:128 * (n + 1)],
                         start=False, stop=(i == 3))
                P.act(pt[:, 0:512], ps, AF.Exp, scale=0.125)
                bank = ringS.next(); ps = pbf(bank)
                P.mm(ps, IDENTb, MBB4b, start=True, stop=False)
                for i in range(4):
                    n = 4 * c + i
                    if n == 0:
                        continue
                    P.mm(ps[:, 128 * i:128 * (i + 1)], KT[fp, 128 * (n - 1):128 * n], QT[fp, 128 * n:128 * (n + 1)],
                         start=False, stop=(i == 3))
                P.act(pt[:, 512:1024], ps, AF.Exp, scale=0.125)
                pt1 = pt
                pt = PTbuf.next()
                bank = ringS.next(); ps = pbf(bank)
                P.mm(ps, IDENTb, MBA4b, start=True, stop=False)
                for r in range(4):
                    sl = slice(512 * c + r, 512 * (c + 1), 4)
                    P.mm(ps[:, 128 * r:128 * (r + 1)], KT[fp, sl], QT[fp, sl], start=False, stop=(r == 3))
                P.act(pt[:, 0:512], ps, AF.Exp, scale=0.125)
                if c > 0:
                    bank = ringS.next(); ps = pbf(bank)
                    P.mm(ps, IDENTb, MBB4b, start=True, stop=False)
                    for r in range(4):
                        sl = slice(512 * c + r, 512 * (c + 1), 4)
                        slk = slice(512 * (c - 1) + r, 512 * c, 4)
                        P.mm(ps[:, 128 * r:128 * (r + 1)], KT[fp, slk], QT[fp, sl], start=False, stop=(r == 3))
                    P.act(pt[:, 512:1024], ps, AF.Exp, scale=0.125)
                for i in range(4):
                    n = 4 * c + i
                    pv(acc[:, 128 * i:128 * (i + 1)], VA[:, n, e, :], pt1[:, 128 * i:128 * (i + 1)])
                    if n > 0:
                        pv(acc[:, 128 * i:128 * (i + 1)], VA[:, n - 1, e, :], pt1[:, 512 + 128 * i:512 + 128 * (i + 1)])
                for r in range(4):
                    pv(acc[:, r:512:4], VA[:, 16 + 4 * r + c, e, :], pt[:, 128 * r:128 * (r + 1)])
                    if c > 0:
                        pv(acc[:, r:512:4], VA[:, 16 + 4 * r + c - 1, e, :], pt[:, 512 + 128 * r:512 + 128 * (r + 1)])
                for r in range(16):
                    pv(acc[:, r:512:16], VA[:, 32 + r, e, :], PT3[:, r, 32 * c:32 * (c + 1)], last=(r == 15))
                num = slice(64 * e, 64 * e + 64)
                den = slice(64 * (1 - e), 64 * (1 - e) + 64)
                rd = rdenb[c % 2]
                P.recip(rd[den, :], acc[den, :])
                P.tt("dve", CT[num, 4 + j, 512 * c:512 * (c + 1)], acc[num, :], rd[den, :], ALU.mult)

    PROJ_V()
    for j in range(4):
        PROJ_B(j)
        ATTN(j)
    dump("CTb", CT[:, 4:8, :].rearrange("p k t -> p (k t)"), [128, 4 * S])
    if stage == "B":
        return finish(nc, P, out_d, dump_tags)

    bF = Bump(A, SCR_LO, SCR_HI)
    h2T = A.view(32768, [8, 1024], BF16)
    woutb = A.view(32768 + 16384, [2, 8, 512], BF16)
    X1 = bF.alloc([8, D])
    wu = [bF.alloc([2, 8, 512], BF16) for _ in range(2)]
    wd = [bF.alloc([4, D], BF16) for _ in range(2)]
    aTb = [bF.alloc([4, 1024], BF16) for _ in range(2)]
    rawF = [[bF.alloc([520]) for _ in range(2)] for _ in range(2)]
    accF = [[bF.alloc([512]) for _ in range(2)] for _ in range(2)]
    sgF = [bF.alloc([512]) for _ in range(2)]
    hbF = [bF.alloc([D], BF16), accF[1][1].bitcast(BF16)]
    junkF = sgF[0].bitcast(BF16)
    ssF = bF.alloc([8]); rsF = bF.alloc([8]); ssO = bF.alloc([8]); rsO = bF.alloc([8])
    for c2 in range(2):
        P.dma("pool", woutb[:, c2, :, :], wout_v[:, :, 512 * c2:512 * (c2 + 1)], "wout%d" % c2)
    P.dma("sp", NW1, n2_d, "c_nw1")
    P.dma("sp", NW2, nf_d, "c_nw2")
    groups = [list(range(g0, min(g0 + 4, 22))) for g0 in range(0, 22, 4)]
    out_tags = ["out%d" % i for i in range(8)]
    items = [(H, gi) for H in range(2) for gi in range(len(groups))]

    def load_wu(k):
        H, gi = items[k]
        grp = groups[gi]; g0 = grp[0]; npair = len(grp); slot = k % 2
        P.dma("pool", wu[slot][:, 0, :, 0:128 * npair], wup_v[:, :, 128 * g0:128 * (g0 + npair)], "wug%d" % slot)
        P.dma("pool", wu[slot][:, 1, :, 0:128 * npair],
              wup_v[:, :, 2816 + 128 * g0:2816 + 128 * (g0 + npair)], "wuu%d" % slot)

    def load_wd(k):
        H, gi = items[k]
        grp = groups[gi]; g0 = grp[0]; npair = len(grp); slot = k % 2
        P.dma("pool", wd[slot][:, 0:npair, :], wdn_v[:, g0:g0 + npair, :], "wd%d" % slot)

    def PRO(H):
        scr = [(junkF, ssF[:, i8:i8 + 1], rsF[:, i8:i8 + 1], hbF[i8 % 2]) for i8 in range(8)]

        def st_a(i8):
            i = 8 * H + i8
            P.dma("sp", X1[:, i8, :], x_d[128 * i:128 * (i + 1), :], "xf%d" % i8)
            b0 = 2 * (i8 % 2)
            for h2 in range(2):
                for kc in range(8):
                    P.mm(pbf(b0 + h2), CT[:, kc, 128 * i:128 * (i + 1)], woutb[:, h2, kc, :],
                         start=(kc == 0), stop=(kc == 7))
            for h2 in range(2):
                P.tt("dve", X1[:, i8, 512 * h2:512 * (h2 + 1)], X1[:, i8, 512 * h2:512 * (h2 + 1)], pbf(b0 + h2), ALU.add)
            if i == 0:
                dump("X1", X1[:, 0, :], [128, D])
            rmsnorm_stage1a(X1[:, i8, :], scr[i8])

        for t in range(10):
            if t < 8:
                st_a(t)
            if 1 <= t <= 8:
                rmsnorm_stage1b(X1[:, t - 1, :], NW1, scr[t - 1])
            if t >= 2:
                rmsnorm_stage2(h2T, 128 * (t - 2), scr[t - 2], 4 + (t % 2))

    upar = [0]

    def UGEN(k):
        H, gi = items[k]
        grp = groups[gi]; slot = k % 2; aT = aTb[k % 2]
        upend = [None]
        for p, g in enumerate(grp):
            for tb in range(2):
                par = upar[0]
                upar[0] ^= 1
                cols = slice(512 * tb, 512 * (tb + 1))
                banks = (4, 5) if par == 0 else (6, 7)
                accs = []
                for gu in range(2):
                    ps = pbf(banks[gu])
                    for kc in range(8):
                        P.mm(ps, wu[slot][:, gu, kc, 128 * p:128 * (p + 1)], h2T[:, kc, cols],
                             start=(kc == 0), stop=(kc == 7))
                    cc = g + 22 * gu
                    raw = rawF[par][gu]; acc = accF[par][gu]
                    P.copy("pool", raw[:, 0:2], HALOF[:, cc, :])
                    P.copy("act", raw[:, 2:514], ps)
                    P.copy("pool", HALOF[:, cc, :], raw[:, 512:514])
                    P.act(acc, ps, AF.Identity, scale=cwF[:, cc, 2:3])
                    P.stt(acc, raw[:, 1:513], cwF[:, cc, 1:2], acc, ALU.mult, ALU.add)
                    P.stt(acc, raw[:, 0:512], cwF[:, cc, 0:1], acc, ALU.mult, ALU.add)
                    accs.append(acc)
                if upend[0] is not None:
                    upend[0]()

                def _gate(par=par, accs=accs, p=p, cols=cols):
                    sg = sgF[par]
                    P.act(sg, accs[0], AF.Silu)
                    P.tt("pool", aT[:, p, cols], sg, accs[1], ALU.mult)
                upend[0] = _gate
                yield
        if upend[0] is not None:
            upend[0]()
            upend[0] = None
            yield

    def DGEN(k):
        H, gi = items[k]
        grp = groups[gi]; slot = k % 2; aT = aTb[k % 2]; npair = len(grp)
        last = (gi == len(groups) - 1)
        dpend = [None]
        for i8 in range(8):
            i = 8 * H + i8
            b0 = 2 * (i8 % 2)
            for h2 in range(2):
                for p in range(npair):
                    P.mm(pbf(b0 + h2), aT[:, p, 128 * i8:128 * (i8 + 1)], wd[slot][:, p, 512 * h2:512 * (h2 + 1)],
                         start=(p == 0), stop=(p == npair - 1))
            for h2 in range(2):
                P.tt("dve", X1[:, i8, 512 * h2:512 * (h2 + 1)], X1[:, i8, 512 * h2:512 * (h2 + 1)],
                     pbf(b0 + h2), ALU.add)
            if last:
                P.act(junkF, X1[:, i8, :], AF.Square, accum_out=ssO[:, i8:i8 + 1])
                P.act(rsO[:, i8:i8 + 1], ssO[:, i8:i8 + 1], AF.Ln, bias=EPS, scale=1.0 / D)
                P.act(rsO[:, i8:i8 + 1], rsO[:, i8:i8 + 1], AF.Exp, scale=-0.5)
                if dpend[0] is not None:
                    dpend[0]()

                def _fin(i8=i8, i=i):
                    P.stt(X1[:, i8, :], X1[:, i8, :], rsO[:, i8:i8 + 1], NW2, ALU.mult, ALU.mult)
                    P.dma("sp", out_d[128 * i:128 * (i + 1), :], X1[:, i8, :], "out%d" % i8)
                dpend[0] = _fin
            yield
        if last and dpend[0] is not None:
            dpend[0]()
            dpend[0] = None
            yield

    def rr(gens):
        gens = list(gens)
        live = [True] * len(gens)
        while any(live):
            for gi_, g_ in enumerate(gens):
                if live[gi_]:
                    try:
                        next(g_)
                    except StopIteration:
                        live[gi_] = False

    load_wu(0)
    prevD = None
    for k in range(len(items)):
        H, gi = items[k]
        load_wd(k)
        if k + 1 < len(items):
            load_wu(k + 1)
        if gi == 0:
            if prevD is not None:
                rr([prevD])
                prevD = None
            PRO(H)
        u = UGEN(k)
        rr([u] if prevD is None else [u, prevD])
        prevD = DGEN(k)
    rr([prevD])
    return finish(nc, P, out_d, dump_tags + out_tags)


def finish(nc, P, out_d, dump_tags):
    tags = list(dump_tags)
    P.emit(final_dma_tags=tags)
    return nc


def prep_inputs(inp):
    f = lambda a: np.ascontiguousarray(np.asarray(a, dtype=np.float32))
    x = f(inp["x"])
    rep = lambda v: np.ascontiguousarray(np.broadcast_to(f(v).reshape(1, -1), (128, f(v).size)))
    cwa = f(inp["conv_qkv_w"])[0]
    cwA = np.ascontiguousarray(cwa.T.reshape(12, 128, 4).transpose(1, 0, 2).reshape(128, 48))
    cwf = f(inp["ffn_conv_w"])[0]
    cwF = np.ascontiguousarray(cwf.T.reshape(44, 128, 3).transpose(1, 0, 2).reshape(128, 132))
    shared = {
        "w_in": f(inp["w_in"])[0], "w_out": f(inp["w_out"])[0], "w_up": f(inp["w_up"])[0],
        "w_down": f(inp["w_down"])[0],
        "n1rep": rep(inp["norm1_w"]), "n2rep": rep(inp["norm2_w"]), "nfrep": rep(inp["final_norm_w"]),
        "cwA": cwA, "cwF": cwF, "gnwrep": rep(inp["gdn_norm_w"]),
        "dtbrep": np.ascontiguousarray(np.tile(rep(inp["dt_bias"]), (1, 16))),
        "alogrep": np.ascontiguousarray(np.tile(rep(inp["a_log"]), (1, 16))),
        "consts": make_consts(),
    }
    maps = []
    for b in range(x.shape[0]):
        m = dict(shared)
        m["x"] = np.ascontiguousarray(x[b])
        maps.append(m)
    return maps


def kernel(**inputs):
    maps = prep_inputs(inputs)
    nc = build("full")
    res = run_bass_kernel_spmd(nc, maps, core_ids=list(range(8)))
    out = np.stack([np.asarray(r["out"], dtype=np.float32) for r in res.results], 0)
    return out
```

```python
import contextlib
import numpy as np
import concourse.bass as bass
import concourse.mybir as mybir
from concourse.bass_utils import run_bass_kernel_spmd

F32 = mybir.dt.float32
BF16 = mybir.dt.bfloat16
AF = mybir.ActivationFunctionType
ALU = mybir.AluOpType
AX = mybir.AxisListType

S = 2048
D = 1024
NT = 16
EPS = 1e-6
NEG = -30000.0
ENGS = ("pe", "act", "dve", "pool", "sp")


def _rect(ap):
    t = ap.tensor
    if str(ap.space) == "PSUM":
        return (t.name, 0, 128, 0, 2048)
    esz = mybir.dt.size(ap.dtype)
    pstride = 1
    for s in tuple(t.shape)[1:]:
        pstride *= s
    p0 = ap.start_partition()
    p1 = p0 + ap.partition_size()
    f0 = ap.offset - p0 * pstride
    ext = 0
    for (st, cnt) in tuple(ap.ap)[1:]:
        ext += abs(st) * (cnt - 1)
    return (t.name, p0, p1, f0 * esz, (f0 + ext + 1) * esz)


class _Op:
    __slots__ = ("eng", "fn", "idx", "is_dma", "tag", "waits", "signal", "sigval")


class Prog:
    def __init__(self, nc):
        self.nc = nc
        self.ops = {e: [] for e in ENGS}
        self.track = {}
        self.waited = {e: {} for e in ENGS}
        self.tagcount = {}

    def _add(self, eng, fn, reads, writes, is_dma=False, tag=None):
        op = _Op()
        op.eng = eng; op.fn = fn; op.is_dma = is_dma; op.tag = tag
        op.signal = False; op.sigval = None
        op.idx = len(self.ops[eng])
        op.waits = []
        deps = {}
        rrects = [_rect(a) for a in reads if a is not None and str(a.space) != "DRAM"]
        wrects = [_rect(a) for a in writes if a is not None and str(a.space) != "DRAM"]
        prects = [r for r in rrects if r[4] == 2048 and r[0].startswith("pb") and r not in wrects]
        for (nm, p0, p1, f0, f1) in rrects:
            for rec in self.track.get(nm, ()):
                if rec[5] == 1 and rec[0] < p1 and p0 < rec[1] and rec[2] < f1 and f0 < rec[3]:
                    deps[id(rec[4])] = (rec[4], True)
        for (nm, p0, p1, f0, f1) in wrects + prects:
            for rec in self.track.get(nm, ()):
                if rec[0] < p1 and p0 < rec[1] and rec[2] < f1 and f0 < rec[3]:
                    k = id(rec[4])
                    if k not in deps:
                        deps[k] = (rec[4], False)
        need = {}
        for (d, raw) in deps.values():
            if d.is_dma:
                key = ("dma", d.tag)
                val = self.tagcount[d.tag]
                if need.get(key, 0) < val:
                    need[key] = val
            else:
                if d.eng == eng and eng == "pe":
                    continue
                key = ("eng", d.eng)
                cur = need.get(key)
                if cur is None or cur.idx < d.idx:
                    need[key] = d
        w = self.waited[eng]
        for key, v in need.items():
            if key[0] == "dma":
                if w.get(key, 0) >= v:
                    continue
                w[key] = v
                op.waits.append((key, v))
            else:
                if w.get(key, -1) >= v.idx:
                    continue
                w[key] = v.idx
                v.signal = True
                op.waits.append((key, v))
        if is_dma:
            self.tagcount[tag] = self.tagcount.get(tag, 0) + 16
        for (nm, p0, p1, f0, f1) in wrects:
            lst = self.track.setdefault(nm, [])
            lst[:] = [r for r in lst if not (p0 <= r[0] and r[1] <= p1 and f0 <= r[2] and r[3] <= f1)]
            lst.append([p0, p1, f0, f1, op, 1])
        for (nm, p0, p1, f0, f1) in prects:
            self.track[nm] = [[p0, p1, f0, f1, op, 2]]
        for (nm, p0, p1, f0, f1) in rrects:
            if nm.startswith("pb"):
                continue
            lst = self.track.setdefault(nm, [])
            done = False
            for r in lst:
                if r[5] == 0 and r[4].eng == eng and (not r[4].is_dma) and (not is_dma) \
                        and r[0] == p0 and r[1] == p1 and r[2] == f0 and r[3] == f1:
                    r[4] = op
                    done = True
                    break
            if not done:
                lst.append([p0, p1, f0, f1, op, 0])
        self.ops[eng].append(op)
        return op

    def dma(self, q, out, in_, tag):
        return self._add(q, lambda e: e.dma_start(out=out, in_=in_), [in_], [out], is_dma=True, tag=tag)

    def mm(self, out, lhsT, rhs, start=True, stop=True, **kw):
        rd = [lhsT, rhs] + ([] if start else [out])
        return self._add("pe", lambda e: e.matmul(out, lhsT, rhs, start=start, stop=stop, **kw), rd, [out])

    def tr(self, out, in_, ident):
        return self._add("pe", lambda e: e.transpose(out, in_, ident), [in_, ident], [out])

    def act(self, out, in_, func, bias=None, scale=None, accum_out=None):
        kw = {}
        rd = [in_]
        if bias is not None:
            kw["bias"] = bias
            if not isinstance(bias, (int, float)):
                rd.append(bias)
        if scale is not None:
            kw["scale"] = scale
            if not isinstance(scale, (int, float)):
                rd.append(scale)
        wr = [out]
        if accum_out is not None:
            kw["accum_out"] = accum_out
            wr.append(accum_out)
        return self._add("act", lambda e: e.activation(out, in_, func, **kw), rd, wr)

    def tt(self, eng, out, in0, in1, op):
        return self._add(eng, lambda e: e.tensor_tensor(out, in0, in1, op), [in0, in1], [out])

    def ts(self, eng, out, in0, s1, s2, op0, op1=None):
        rd = [in0] + [s for s in (s1, s2) if s is not None and not isinstance(s, (int, float))]
        kw = {}
        if op1 is not None:
            kw["op1"] = op1
        return self._add(eng, lambda e: e.tensor_scalar(out, in0, s1, s2, op0, **kw), rd, [out])

    def stt(self, out, in0, scalar, in1, op0, op1):
        rd = [in0, in1] + ([] if isinstance(scalar, (int, float)) else [scalar])
        return self._add("dve", lambda e: e.scalar_tensor_tensor(out, in0, scalar, in1, op0, op1), rd, [out])

    def copy(self, eng, out, in_):
        if eng == "act":
            return self._add("act", lambda e: e.copy(out, in_), [in_], [out])
        return self._add(eng, lambda e: e.tensor_copy(out, in_), [in_], [out])

    def memset(self, eng, ap, val):
        return self._add(eng, lambda e: e.memset(ap, val), [], [ap])

    def recip(self, out, in_):
        return self._add("dve", lambda e: e.reciprocal(out, in_), [in_], [out])

    def rsum(self, out, in_):
        return self._add("dve", lambda e: e.tensor_reduce(out, in_, AX.X, ALU.add), [in_], [out])

    def emit(self, final_dma_tags=()):
        nc = self.nc
        for e in ENGS:
            c = 0
            for op in self.ops[e]:
                if op.signal and not op.is_dma:
                    c += 1
                    op.sigval = c
        with contextlib.ExitStack() as st:
            esem = {e: st.enter_context(nc.semaphore("s_" + e)) for e in ENGS}
            dsem = {t: st.enter_context(nc.semaphore("d_%d" % i)) for i, t in enumerate(self.tagcount)}
            block = st.enter_context(nc.Block())
            engobj = {"pe": block.tensor, "act": block.scalar, "dve": block.vector,
                      "pool": block.gpsimd, "sp": block.sync}

            def make(ename):
                def body(eng):
                    for op in self.ops[ename]:
                        for (key, v) in op.waits:
                            if key[0] == "dma":
                                eng.wait_ge(dsem[key[1]], v)
                            else:
                                eng.wait_ge(esem[key[1]], v.sigval)
                        ins = op.fn(eng)
                        if op.is_dma:
                            ins.then_inc(dsem[op.tag], 16)
                        elif op.signal:
                            ins.then_inc(esem[ename], 1)
                    if ename == "sp":
                        for t in final_dma_tags:
                            eng.wait_ge(dsem[t], self.tagcount[t])
                return body

            for e in ENGS:
                engobj[e](make(e))
        return nc


def _prod(shape):
    n = 1
    for s in shape:
        n *= s
    return n


class Arena:
    def __init__(self, nc, nbytes):
        self.t = nc.alloc_sbuf_tensor("arena", [128, nbytes // 4], F32)
        self.nbytes = nbytes

    def view(self, off, shape, dt):
        n = _prod(shape)
        esz = 4 if dt == F32 else 2
        assert off % 4 == 0 and (n * esz) % 4 == 0 and off + n * esz <= self.nbytes, (off, shape)
        ap = self.t[:, off // 4:(off + n * esz) // 4]
        if dt != F32:
            ap = ap.bitcast(dt)
        if len(shape) > 1:
            names = " ".join("d%d" % i for i in range(len(shape)))
            kw = {"d%d" % i: shape[i] for i in range(1, len(shape))}
            ap = ap.rearrange("p (%s) -> p %s" % (names, names), **kw)
        return ap


class Bump:
    def __init__(self, arena, lo, hi):
        self.a = arena; self.lo = lo; self.hi = hi; self.cur = lo

    def alloc(self, shape, dt=F32):
        esz = 4 if dt == F32 else 2
        n = (_prod(shape) * esz + 31) // 32 * 32
        off = self.cur
        self.cur += n
        assert self.cur <= self.hi, ("arena overflow", self.cur, self.hi)
        return self.a.view(off, shape, dt)


class Ring:
    def __init__(self, items):
        self.items = list(items); self.i = 0

    def next(self):
        x = self.items[self.i % len(self.items)]
        self.i += 1
        return x


_CNAMES = ["IDENT", "M1", "M2", "TRI", "SUF", "IND0", "IND1", "S32", "OFF", "ONES"]
_CB = {"MB64": 512, "MBA4": 512, "MBB4": 512}
NCF = 128 * len(_CNAMES)
NCONST = NCF + 512 * 3


def make_consts():
    i = np.arange(128)[:, None]
    j = np.arange(128)[None, :]
    same = (i // 64) == (j // 64)
    c = {}
    c["IDENT"] = (i == j)
    c["M1"] = (i > j)
    c["M2"] = (i <= j)
    c["TRI"] = (i <= j) & same
    c["SUF"] = (i > j) & same
    c["IND0"] = (i < 64) & (j >= 0)
    c["IND1"] = (i >= 64) & (j >= 0)
    c["S32"] = (i < j) & ((i // 32) == (j // 32))
    c["OFF"] = same & ((i % 64) < 32) & ((j % 64) >= 32)
    c["ONES"] = np.ones((128, 128), bool)
    cols = [c[n].astype(np.float32) for n in _CNAMES]
    mb64 = np.where((i <= j) & same, 0.0, NEG).astype(np.float32)
    mba = np.where(i <= j, 0.0, NEG).astype(np.float32)
    mbb = np.where(i >= j, 0.0, NEG).astype(np.float32)
    cols += [np.tile(mb64, (1, 4)), np.tile(mba, (1, 4)), np.tile(mbb, (1, 4))]
    return np.ascontiguousarray(np.concatenate(cols, 1))


def build(stage="full", dumps=()):
    nc = bass.Bass("TRN2", target_bir_lowering=False)
    P = Prog(nc)
    dumps = set(dumps)
    dump_tags = []

    def din(name, shape):
        return nc.dram_tensor(name, list(shape), F32, kind="ExternalInput").ap()

    x_d = din("x", [S, D])
    win_d = din("w_in", [D, 3592])
    wout_d = din("w_out", [D, D])
    wup_d = din("w_up", [D, 5632])
    wdn_d = din("w_down", [2816, D])
    n1_d = din("n1rep", [128, D])
    n2_d = din("n2rep", [128, D])
    nf_d = din("nfrep", [128, D])
    cwa_d = din("cwA", [128, 48])
    cwf_d = din("cwF", [128, 132])
    gnw_d = din("gnwrep", [128, 128])
    dtb_d = din("dtbrep", [128, 64])
    alog_d = din("alogrep", [128, 64])
    cst_d = din("consts", [128, NCONST])
    out_d = nc.dram_tensor("out", [S, D], F32, kind="ExternalOutput").ap()

    def dump(name, sb_ap, shape):
        if name not in dumps:
            return
        d = nc.dram_tensor("dbg_" + name, list(shape), sb_ap.dtype, kind="ExternalOutput").ap()
        tg = "dbg_" + name
        P.dma("sp", d, sb_ap, tg)
        dump_tags.append(tg)

    win_v = win_d.rearrange("(k p) c -> p k c", p=128)
    wout_v = wout_d.rearrange("(k p) c -> p k c", p=128)
    wup_v = wup_d.rearrange("(k p) c -> p k c", p=128)
    wdn_v = wdn_d.rearrange("(k p) c -> p k c", p=128)

    ARENA_BYTES = 207872
    A = Arena(nc, ARENA_BYTES)
    pb = [nc.alloc_psum_tensor("pb%d" % i, [128, 512], F32) for i in range(8)]

    def pbf(i):
        return pb[i][:]

    def pbb(i):
        return pb[i][:].bitcast(BF16)

    CT = A.view(0, [8, S], BF16)
    hT = A.view(32768, [8, S], BF16)
    pers = Bump(A, 65536, 86016)
    CF = pers.alloc([NCF])
    CBm = pers.alloc([1536 + 256], BF16)
    cwA = pers.alloc([12, 4])
    cwF = pers.alloc([44, 3])
    HALOA = pers.alloc([12, 3])
    HALOF = pers.alloc([44, 2])
    GNW = pers.alloc([128])
    NW1 = pers.alloc([D])
    NW2 = pers.alloc([D])
    SCR_LO = 86016
    SCR_HI = ARENA_BYTES

    def cf(name):
        k = _CNAMES.index(name)
        return CF[:, 128 * k:128 * (k + 1)]

    IDENT = cf("IDENT"); M1 = cf("M1"); M2 = cf("M2"); TRI = cf("TRI"); SUF = cf("SUF")
    IND0 = cf("IND0"); IND1 = cf("IND1"); S32 = cf("S32"); OFFM = cf("OFF"); ONES = cf("ONES")
    MB64b = CBm[:, 0:512]; MBA4b = CBm[:, 512:1024]; MBB4b = CBm[:, 1024:1536]
    IDENTb = CBm[:, 1536:1664]; ONESb = CBm[:, 1664:1792]

    P.dma("sp", CF, cst_d[:, 0:NCF], "c_cf")
    ctmp = A.view(16384 + 8192, [1536], F32)
    P.dma("sp", ctmp, cst_d[:, NCF:NCONST], "c_tmp")
    P.copy("dve", CBm[:, 0:1536], ctmp)
    P.copy("dve", IDENTb, IDENT)
    P.copy("dve", ONESb, ONES)
    P.dma("sp", cwA.rearrange("p a b -> p (a b)"), cwa_d, "c_cwa")
    P.dma("sp", cwF.rearrange("p a b -> p (a b)"), cwf_d, "c_cwf")
    P.dma("sp", GNW, gnw_d, "c_gnw")
    P.dma("sp", NW1, n1_d, "c_nw1")
    P.memset("pool", HALOA.rearrange("p a b -> p (a b)"), 0.0)
    P.memset("pool", HALOF.rearrange("p a b -> p (a b)"), 0.0)

    def bc_last(ap, n):
        return ap.unsqueeze(2).broadcast_to([128, ap.shape[1], n])

    def bc_mid(ap, n):
        return ap.unsqueeze(1).broadcast_to([128, n, ap.shape[1]])

    def rmsnorm_stage1(src_tile, wtile, scr):
        junk, ssv, rst, hb = scr
        P.act(junk, src_tile, AF.Square, accum_out=ssv)
        P.act(rst, ssv, AF.Ln, bias=EPS, scale=1.0 / D)
        P.act(rst, rst, AF.Exp, scale=-0.5)
        P.stt(hb, src_tile, rst, wtile, ALU.mult, ALU.mult)

    def rmsnorm_stage1a(src_tile, scr):
        junk, ssv, rst, hb = scr
        P.act(junk, src_tile, AF.Square, accum_out=ssv)
        P.act(rst, ssv, AF.Ln, bias=EPS, scale=1.0 / D)
        P.act(rst, rst, AF.Exp, scale=-0.5)

    def rmsnorm_stage1b(src_tile, wtile, scr):
        junk, ssv, rst, hb = scr
        P.stt(hb, src_tile, rst, wtile, ALU.mult, ALU.mult)

    def rmsnorm_stage2(dstT, col0, scr, bank):
        hb = scr[3]
        psT = pbb(bank)
        for kc in range(8):
            P.tr(psT[:, 128 * kc:128 * (kc + 1)], hb[:, 128 * kc:128 * (kc + 1)], IDENTb)
        P.copy("act", dstT[:, :, col0:col0 + 128], psT.rearrange("p (k t) -> p k t", k=8))

    bA = Bump(A, 0, 16384)
    xbuf = [bA.alloc([D]) for _ in range(3)]
    junkA = bA.alloc([D], BF16)
    hbuf = [NW2.bitcast(BF16)[:, 0:D], NW2.bitcast(BF16)[:, D:2 * D]]
    ssA = bA.alloc([NT])
    rsA = bA.alloc([NT])
    scrA = [(junkA, ssA[:, i:i + 1], rsA[:, i:i + 1], hbuf[i % 2]) for i in range(NT)]
    for i in range(NT + 2):
        if i < NT:
            xt = xbuf[i % 3]
            P.dma("sp", xt, x_d[128 * i:128 * (i + 1), :], "xa%d" % (i % 3))
            rmsnorm_stage1a(xt, scrA[i])
        if 1 <= i <= NT:
            rmsnorm_stage1b(xbuf[(i - 1) % 3], NW1, scrA[i - 1])
        if i >= 2:
            rmsnorm_stage2(hT, 128 * (i - 2), scrA[i - 2], 6 + (i % 2))
    dump("hT", hT.rearrange("p k t -> p (k t)"), [128, 8 * S])
    if stage == "A":
        return finish(nc, P, out_d, dump_tags)

    bG = Bump(A, SCR_LO, SCR_HI)
    bC = Bump(A, 16384, 32768)
    wz = bC.alloc([8, 512], BF16)
    qkvT = [dict(q=bG.alloc([4, 512], BF16), k=bG.alloc([4, 512], BF16), v=bG.alloc([4, 512], BF16))
            for _ in range(4)]
    szgb = [bG.alloc([4, 512], BF16), bG.alloc([4, 512], BF16), bG.alloc([4, 512], BF16), bC.alloc([4, 512], BF16)]
    BA = bG.alloc([NT, 8])
    sc_x = bG.alloc([64]); sc_mx = bG.alloc([64]); sc_mn = bG.alloc([64])
    dtb = bG.alloc([64]); negA = bG.alloc([64])
    gS = bG.alloc([64]); betaS = bG.alloc([64]); gamS = bG.alloc([64]); kesS = bG.alloc([64])
    gendS = bG.alloc([2, 64])
    wba = bG.alloc([8, 8], BF16)
    Sst = bG.alloc([4, 128]); Sb = bG.alloc([4, 128], BF16); ub = bG.alloc([4, 128], BF16)
    Obuf = [bC.alloc([4, 128]) for _ in range(2)]
    sqO = bG.alloc([4, 128], BF16); oab = sqO
    kesLo = bG.alloc([64]); kesHi = bG.alloc([64])
    ssq = bG.alloc([4]); rsq = bG.alloc([4])
    ov0 = bG.cur
    gset = []
    for _ in range(2):
        gset.append(dict(
            G1m=bG.alloc([4, 128]),
            DECT=bG.alloc([4, 128], BF16), Uall=bG.alloc([4, 128], BF16), Eu=bG.alloc([4, 128], BF16),
            PTa=bG.alloc([4, 128], BF16), PTb=bG.alloc([4, 128], BF16),
            UL0=bG.alloc([2, 4, 128], BF16), PWA=bG.alloc([2, 4, 128], BF16), PWB=bG.alloc([2, 4, 128], BF16),
            TTb=bG.alloc([4, 128], BF16), vb=bG.alloc([4, 128], BF16), gk=bG.alloc([4, 128], BF16)))
    scanop = []
    for _ in range(3):
        scanop.append(dict(KLO=bG.alloc([4, 128], BF16), KHI=bG.alloc([4, 128], BF16), ATT=bG.alloc([4, 128], BF16),
                           QD=bG.alloc([4, 128], BF16), WKT=bG.alloc([4, 128], BF16),
                           UV=bG.alloc([4, 128], BF16)))
    bO = Bump(A, ov0, bG.cur)
    wqkva = bO.alloc([3, 8, 512], BF16)
    rawb = [bO.alloc([520], BF16) for _ in range(8)]
    dgcb = [bO.alloc([2, 128], BF16) for _ in range(8)]
    accb2 = [bO.alloc([512]) for _ in range(2)]
    acccnt = [0]
    sqbs = [bO.alloc([512], BF16) for _ in range(3)]
    rtbs = [bO.alloc([512]) for _ in range(3)]
    for c3 in range(3):
        P.dma("pool", wqkva[:, c3, :, :], win_v[:, :, 512 * c3:512 * (c3 + 1)], "wqkva%d" % c3)
    P.dma("pool", wba, win_v[:, :, 2048:2056], "wba")
    P.dma("pool", wz, win_v[:, :, 1536:2048], "wz")

    ringG = Ring([0, 1, 2, 3, 4, 5, 6, 7])

    g3 = gS.rearrange("p (n h) -> p n h", h=4)
    beta3 = betaS.rearrange("p (n h) -> p n h", h=4)
    gam3 = gamS.rearrange("p (n h) -> p n h", h=4)
    kesLo3 = kesLo.rearrange("p (n h) -> p n h", h=4)
    kesHi3 = kesHi.rearrange("p (n h) -> p n h", h=4)
    gend4 = gendS.rearrange("p a (n h) -> p a n h", h=4)

    def SCALARS():
        P.dma("sp", dtb, dtb_d, "c_dtb")
        P.dma("sp", negA, alog_d, "c_alog")
        P.act(negA, negA, AF.Exp)
        P.ts("dve", negA, negA, -1.0, None, ALU.mult)
        bk = ringG.next()
        psBA = pbf(bk)[:, 0:128].rearrange("p (n c) -> p n c", c=8)
        for i in range(NT):
            for kc in range(8):
                P.mm(psBA[:, i, :], hT[:, kc, 128 * i:128 * (i + 1)], wba[:, kc, :], start=(kc == 0), stop=(kc == 7))
        P.copy("act", BA, psBA)
        x3 = sc_x.rearrange("p (n h) -> p n h", h=4)
        P.tt("dve", x3, BA[:, :, 4:8], dtb.rearrange("p (n h) -> p n h", h=4), ALU.add)
        P.ts("dve", sc_mx, sc_x, 0.0, None, ALU.max)
        P.ts("dve", sc_mn, sc_x, 0.0, None, ALU.min)
        P.tt("dve", sc_mn, sc_mn, sc_mx, ALU.subtract)
        P.act(sc_mn, sc_mn, AF.Exp)
        P.act(sc_mn, sc_mn, AF.Ln, bias=1.0)
        P.tt("dve", sc_mx, sc_mx, sc_mn, ALU.add)
        P.tt("dve", gS, sc_mx, negA, ALU.mult)
        P.act(betaS.rearrange("p (n h) -> p n h", h=4), BA[:, :, 0:4], AF.Sigmoid)
        bk = ringG.next()
        psg = pbf(bk)
        P.mm(psg[:, 0:64], TRI, gS)
        P.mm(psg[:, 64:128], SUF, gS)
        P.mm(psg[:, 128:192], IND0, gS)
        P.mm(psg[:, 192:256], IND1, gS)
        P.act(gamS, psg[:, 0:64], AF.Exp)
        P.act(kesS, psg[:, 64:128], AF.Exp)
        P.tt("dve", kesS, kesS, betaS, ALU.mult)
        P.act(gendS.rearrange("p a b -> p (a b)"), psg[:, 128:256], AF.Exp)
        dump("gS", gS, [128, 64]); dump("betaS", betaS, [128, 64]); dump("gamS", gamS, [128, 64])
        dump("kesS", kesS, [128, 64]); dump("gendS", gendS.rearrange("p a b -> p (a b)"), [128, 128])
        P.ts("dve", kesLo, kesS, IND0[:, 0:1], None, ALU.mult)
        P.ts("dve", kesHi, kesS, IND1[:, 0:1], None, ALU.mult)


    def G1(m):
        t0 = 512 * m
        o = qkvT[m]
        pend = [None]
        for c in range(12):
            ps = pbf(ringG.next())
            for kc in range(8):
                P.mm(ps, wqkva[:, c // 4, kc, 128 * (c % 4):128 * (c % 4 + 1)], hT[:, kc, t0:t0 + 512], start=(kc == 0), stop=(kc == 7))
            raw = rawb[2 * m + c % 2]; dgc = dgcb[2 * m + c % 2]
            for i_ in (2, 3):
                P.ts("dve", dgc[:, i_ - 2, :], IDENTb, cwA[:, c, i_:i_ + 1], None, ALU.mult)
            P.copy("pool", raw[:, 0:3], HALOA[:, c, :])
            P.copy("act", raw[:, 3:515], ps)
            P.copy("pool", HALOA[:, c, :], raw[:, 512:515])
            if pend[0] is not None:
                pend[0]()

            def _conv(raw=raw, dgc=dgc, c=c):
                psC = pbf(ringG.next())
                P.mm(psC, dgc[:, 0, :], raw[:, 2:514], start=True, stop=False)
                P.mm(psC, dgc[:, 1, :], raw[:, 3:515], start=False, stop=True)
                acc = accb2[acccnt[0] % 2]
                acccnt[0] += 1
                P.stt(acc, raw[:, 1:513], cwA[:, c, 1:2], psC, ALU.mult, ALU.add)
                P.stt(acc, raw[:, 0:512], cwA[:, c, 0:1], acc, ALU.mult, ALU.add)
                dst = (o["q"], o["k"], o["v"])[c // 4][:, c % 4, :]
                P.act(dst, acc, AF.Silu)
            pend[0] = _conv
            if c % 3 == 2:
                nz = 4 * m + c // 3
                psZ = pbf(ringG.next())
                for kc in range(8):
                    P.mm(psZ, hT[:, kc, 128 * nz:128 * (nz + 1)], wz[:, kc, :], start=(kc == 0), stop=(kc == 7))
                szg = szgb[m][:, c // 3, :]
                P.act(szg, psZ, AF.Silu)
                P.tt("pool", szg.rearrange("p (h d) -> p h d", h=4), szg.rearrange("p (h d) -> p h d", h=4),
                     bc_mid(GNW, 4), ALU.mult)
            yield
        pend[0]()
        yield
        for c in range(8):
            dst = (o["q"], o["k"])[c // 4][:, c % 4, :]
            sc = 128.0 if c < 4 else 1.0
            sqb = sqbs[(c + m) % 3]; rtb = rtbs[(c + m) % 3]
            P.tt("pool", sqb, dst, dst, ALU.mult)
            psn = pbf(ringG.next())
            P.mm(psn, ONESb, sqb)
            P.act(rtb, psn, AF.Ln, bias=EPS * sc, scale=sc)
            P.act(rtb, rtb, AF.Exp, scale=-0.5)
            P.tt("dve", dst, dst, rtb, ALU.mult)
            yield
        if m == 0:
            dump("qnT0", o["q"].rearrange("p h t -> p (h t)"), [128, 2048])
            dump("knT0", o["k"].rearrange("p h t -> p (h t)"), [128, 2048])
            dump("vsT0", o["v"].rearrange("p h t -> p (h t)"), [128, 2048])

    def G2(n):
        tl = 128 * (n % 4)
        so = scanop[n % 3]
        st = gset[n % 2]
        qnT = qkvT[n // 4]["q"]; knT = qkvT[n // 4]["k"]; vsT = qkvT[n // 4]["v"]
        G1m = st["G1m"]; DECT = st["DECT"]; Uall = st["Uall"]; Eu = st["Eu"]
        UL0 = st["UL0"]; PWA = st["PWA"]; PWB = st["PWB"]
        TTb = st["TTb"]; vb = st["vb"]; gk = st["gk"]
        Du, Dl = UL0[:, 0], UL0[:, 1]
        gam_b = bc_last(gam3[:, n, :], 128)
        keslo_b = bc_last(kesLo3[:, n, :], 128)
        keshi_b = bc_last(kesHi3[:, n, :], 128)
        beta_b = bc_last(beta3[:, n, :], 128)
        g_b = bc_last(g3[:, n, :], 128)

        def bankf():
            return pbf(ringG.next()).rearrange("p (h d) -> p h d", h=4)

        def bankb():
            return pbb(ringG.next())[:, 0:512].rearrange("p (h d) -> p h d", h=4)

        psKV = pbb(ringG.next()).rearrange("p (a h d) -> p a h d", a=2, h=4)
        psK = psKV[:, 0]; psV = psKV[:, 1]
        for h in range(4):
            P.tr(psK[:, h, :], knT[:, h, tl:tl + 128], IDENTb)
            P.tr(psV[:, h, :], vsT[:, h, tl:tl + 128], IDENTb)
        P.tt("dve", gk, psK, gam_b, ALU.mult)
        P.tt("dve", so["KLO"], psK, keslo_b, ALU.mult)
        P.tt("dve", so["KHI"], psK, keshi_b, ALU.mult)
        P.copy("act", vb, psV)
        dg = Uall
        P.tt("pool", G1m, bc_mid(M1, 4), g_b, ALU.mult)
        P.tt("pool", dg, bc_mid(IDENT, 4), gam_b, ALU.mult)
        yield
        psD = bankf()
        P.mm(psD, IDENTb, MB64b, start=True, stop=False)
        for h in range(4):
            P.mm(psD[:, h, :], G1m[:, h, :], M2, start=False, stop=(h == 3))
        P.act(DECT, psD, AF.Exp)
        yield
        P.tt("pool", DECT, DECT, beta_b, ALU.mult)
        psG = bankf()
        for h in range(4):
            P.mm(psG[:, h, :], ONESb, dg[:, h, :])
        P.tt("dve", so["QD"], qnT[:, :, tl:tl + 128], psG, ALU.mult)
        yield
        psKK = bankf(); psQK = bankf()
        for h in range(4):
            P.mm(psKK[:, h, :], knT[:, h, tl:tl + 128], knT[:, h, tl:tl + 128])
        for h in range(4):
            P.mm(psQK[:, h, :], knT[:, h, tl:tl + 128], qnT[:, h, tl:tl + 128])
        P.tt("dve", Uall, psKK, DECT, ALU.mult)
        P.tt("dve", so["ATT"], psQK, DECT, ALU.mult)
        P.tt("pool", Du, Uall, bc_mid(S32, 4), ALU.mult)
        P.tt("pool", Eu, Uall, bc_mid(OFFM, 4), ALU.mult)
        yield
        psT = bankb()
        for h in range(4):
            P.tr(psT[:, h, :], Du[:, h, :], IDENTb)
        P.copy("act", Dl, psT)
        P.tt("pool", st["PTa"], bc_mid(IDENT, 4), Du, ALU.subtract)
        yield
        pw = [UL0, PWA, PWB, PWA, PWB]
        PT, PTn = st["PTa"], st["PTb"]
        for k in range(1, 5):
            cur = pw[k - 1]; nxt = pw[k]
            if k < 4:
                psU = bankf()
                for h in range(4):
                    P.mm(psU[:, h, :], cur[:, 1, h, :], cur[:, 0, h, :])
            psL = bankf()
            for h in range(4):
                P.mm(psL[:, h, :], cur[:, 0, h, :], cur[:, 1, h, :])
            if k > 1:
                ps3 = bankf()
                for h in range(4):
                    P.mm(ps3[:, h, :], cur[:, 1, h, :], PT[:, h, :])
            if k < 4:
                P.copy("act", nxt[:, 0], psU)
            P.copy("act", nxt[:, 1], psL)
            if k > 1:
                P.tt("dve", PTn, PT, ps3, ALU.add)
                PT, PTn = PTn, PT
            yield
        ps3 = bankf()
        for h in range(4):
            P.mm(ps3[:, h, :], PWB[:, 1, h, :], PT[:, h, :])
        P.tt("dve", PTn, PT, ps3, ALU.add)
        PT, PTn = PTn, PT
        yield
        Pm = PWA[:, 0]; XT = DECT
        p1 = bankb()
        for h in range(4):
            P.tr(p1[:, h, :], PT[:, h, :], IDENTb)
        P.copy("act", Pm, p1)
        yield
        p2 = bankf()
        for h in range(4):
            P.mm(p2[:, h, :], Eu[:, h, :], Pm[:, h, :])
        P.copy("act", XT, p2)
        yield
        p3 = bankf()
        for h in range(4):
            P.mm(p3[:, h, :], XT[:, h, :], PT[:, h, :])
        P.tt("dve", TTb, PT, p3, ALU.subtract)
        yield
        p1 = bankf(); p2 = bankf()
        for h in range(4):
            P.mm(p1[:, h, :], TTb[:, h, :], vb[:, h, :])
        for h in range(4):
            P.mm(p2[:, h, :], gk[:, h, :], TTb[:, h, :])
        P.copy("act", so["UV"], p1)
        P.copy("dve", so["WKT"], p2)
        yield

    def SCAN(n):
        so = scanop[n % 3]
        O = Obuf[n % 2]
        for half in range(2):
            r0 = 64 * half
            rs = slice(r0, r0 + 64)
            kend = so["KLO"] if half == 0 else so["KHI"]
            psA = pbf(ringG.next()).rearrange("p (h d) -> p h d", h=4)
            for h in range(4):
                P.mm(psA[:, h, :], so["WKT"][:, h, :], Sb[:, h, :])
            P.tt("dve", ub[rs], so["UV"][rs], psA[rs], ALU.subtract)
            yield
            psS = pbf(ringG.next()).rearrange("p (h d) -> p h d", h=4)
            psO = pbf(ringG.next()).rearrange("p (h d) -> p h d", h=4)
            for h in range(4):
                P.mm(psS[:, h, :], kend[:, h, :], ub[:, h, :])
            for h in range(4):
                P.mm(psO[:, h, :], so["QD"][:, h, :], Sb[:, h, :], start=True, stop=False)
                P.mm(psO[:, h, :], so["ATT"][:, h, :], ub[:, h, :], start=False, stop=True)
            for h in range(4):
                P.stt(Sst[:, h, :], Sst[:, h, :], gend4[:, half, n, h:h + 1], psS[:, h, :], ALU.mult, ALU.add)
            P.copy("act", O[rs], psO[rs])
            P.copy("act", Sb, Sst)
            yield
        if n == 0:
            dump("O0", O.rearrange("p h d -> p (h d)"), [128, 512])
        if n == 15:
            dump("O15", O.rearrange("p h d -> p (h d)"), [128, 512])
        P.act(sqO, O, AF.Square)
        P.rsum(ssq, sqO)
        yield
        P.act(rsq, ssq, AF.Ln, bias=EPS, scale=1.0 / 128)
        P.act(rsq, rsq, AF.Exp, scale=-0.5)
        szg = szgb[n // 4][:, n % 4, :].rearrange("p (h d) -> p h d", h=4)
        P.tt("dve", sqO, O, bc_last(rsq, 128), ALU.mult)
        P.tt("dve", oab, sqO, szg, ALU.mult)
        yield
        psT = pbb(ringG.next())[:, 0:512].rearrange("p (h d) -> p h d", h=4)
        for h in range(4):
            P.tr(psT[:, h, :], oab[:, h, :], IDENTb)
        P.copy("act", CT[:, 0:4, 128 * n:128 * (n + 1)], psT)
        yield

    def advance(must, opt):
        live = [True] * len(must)
        while any(live) or any(q[1] > 0 for q in opt):
            for q in opt:
                if q[1] > 0:
                    q[1] -= 1
                    try:
                        next(q[0])
                    except StopIteration:
                        q[1] = 0
                        q[2] = True
            for gi, g in enumerate(must):
                if live[gi]:
                    try:
                        next(g)
                    except StopIteration:
                        live[gi] = False

    P.memset("dve", Sst.rearrange("p h d -> p (h d)"), 0.0)
    P.memset("pool", Sb.rearrange("p h d -> p (h d)"), 0.0)
    P.memset("pool", ub.rearrange("p h d -> p (h d)"), 0.0)
    advance([G1(m_) for m_ in range(4)], [])
    SCALARS()
    g2 = {0: G2(0), 1: G2(1)}
    advance([g2[0]], [[g2[1], 7, False]])
    for n in range(NT):
        must = [SCAN(n)]
        if n + 1 < NT:
            must.append(g2[n + 1])
        opt = []
        if n + 2 < NT:
            g2[n + 2] = G2(n + 2)
            opt.append([g2[n + 2], 7, False])
        advance(must, opt)
    dump("CTa", CT[:, 0:4, :].rearrange("p k t -> p (k t)"), [128, 4 * S])
    if stage == "G":
        return finish(nc, P, out_d, dump_tags)

    bB = Bump(A, SCR_LO, SCR_HI)
    wqkvb = bB.alloc([3, 8, 512], BF16)
    for c3 in range(3):
        P.dma("pool", wqkvb[:, c3, :, :],
              win_v[:, :, 2056 + 512 * c3:2056 + 512 * (c3 + 1)], "wqkvb%d" % c3)
    QT0 = bB.alloc([S], BF16)
    KTz = [bB.alloc([S], BF16) for _ in range(2)]
    QTb = [QT0, QT0]
    P.memset("pool", KTz[0][64:128, :], 0.0)
    P.memset("pool", KTz[1][0:64, :], 0.0)
    VAll = bB.alloc([48, 4, 192], BF16)
    PTbuf = Ring([bB.alloc([1024], BF16) for _ in range(2)])
    PT3 = bB.alloc([16, 128], BF16)
    rden0 = bB.alloc([512])
    rdenb = [rden0, rden0]
    P.memset("pool", VAll[:, :, :, 64:128], 1.0)
    ringS = Ring([2, 3, 4, 5])
    ringP = Ring([6, 7])
    accR = Ring([0, 1])

    def tok_slices():
        sl = []
        for n in range(16):
            sl.append(slice(128 * n, 128 * (n + 1), 1))
        for r in range(4):
            for c in range(4):
                sl.append(slice(512 * c + r, 512 * (c + 1), 4))
        for r in range(16):
            sl.append(slice(r, S, 16))
        return sl

    TOK = tok_slices()

    def PROJ_V():
        for t in range(48):
            bank = ringP.next()
            ps = pbf(bank)
            sl = TOK[t]
            for kc in range(8):
                P.mm(ps, hT[:, kc, sl], wqkvb[:, 2, kc, :], start=(kc == 0), stop=(kc == 7))
            ps4 = ps.rearrange("p (j e d) -> p j e d", j=4, e=2)
            P.copy("act", VAll[:, t, :, 0:64], ps4[:, :, 0, :])
            P.copy("dve", VAll[:, t, :, 128:192], ps4[:, :, 1, :])

    def PROJ_B(j):
        k = j % 2
        QT = QTb[k]
        for (dst, cbase) in ((QT, 128 * j), (None, 512 + 128 * j)):
            for tb in range(4):
                bank = ringP.next()
                ps = pbf(bank)
                for kc in range(8):
                    P.mm(ps, wqkvb[:, cbase // 512, kc, cbase % 512:cbase % 512 + 128], hT[:, kc, 512 * tb:512 * (tb + 1)],
                         start=(kc == 0), stop=(kc == 7))
                cs = slice(512 * tb, 512 * (tb + 1))
                if dst is not None:
                    P.copy("dve" if tb % 2 else "act", dst[:, cs], ps)
                else:
                    P.copy("act", KTz[0][0:64, cs], ps[0:64, :])
                    P.copy("dve", KTz[1][64:128, cs], ps[64:128, :])

    def ATTN(j):
        k = j % 2
        QT = QTb[k]

        class _VA:
            def __init__(self, h):
                self.h = h

            def __getitem__(self, idx):
                e_ = self.h % 2
                return VAll[:, idx[1], self.h // 2, 64 * e_:64 * e_ + 128]

        for e in range(2):
            hp = slice(64 * e, 64 * e + 64)
            KT = KTz[e]
            fp = slice(0, 128)
            VA = _VA(2 * j + e)
            for g in range(4):
                bank = ringS.next()
                ps = pbf(bank)
                P.mm(ps, IDENTb, MBA4b, start=True, stop=False)
                for r4 in range(4):
                    r = 4 * g + r4
                    P.mm(ps[:, 128 * r4:128 * (r4 + 1)], KT[fp, r:S:16], QT[fp, r:S:16], start=False, stop=(r4 == 3))
                P.act(PT3[:, 4 * g:4 * g + 4, :], ps.rearrange("p (a d) -> p a d", a=4), AF.Exp, scale=0.125)
            for c in range(4):
                acc = pbf(accR.next())
                first = [True]

                def pv(out, lhsT, rhs, last=False):
                    P.mm(out, lhsT, rhs, start=first[0], stop=last, skip_group_check=True)
                    first[0] = False
                pt = PTbuf.next()
                bank = ringS.next(); ps = pbf(bank)
                P.mm(ps, IDENTb, MBA4b, start=True, stop=False)
                for i in range(4):
                    n = 4 * c + i
                    P.mm(ps[:, 128 * i:128 * (i + 1)], KT[fp, 128 * n:128 * (n + 1)], QT[fp, 128 * n:128 * (n + 1)],
                         start=False, stop=(i == 3))
                P.act(pt[:, 0:512], ps, AF.Exp, scale=0.125)
                bank = ringS.next(); ps = pbf(bank)
                P.mm(ps, IDENTb, MBB4b, start=True, stop=False)
                for i in range(4):
                    n = 4 * c + i
                    if n == 0:
                        continue
                    P.mm(ps[:, 128 * i:128 * (i + 1)], KT[fp, 128 * (n - 1):128 * n], QT[fp, 128 * n:128 * (n + 1)],
                         start=False, stop=(i == 3))
                P.act(pt[:, 512:1024], ps, AF.Exp, scale=0.125)
                pt1 = pt
                pt = PTbuf.next()
                bank = ringS.next(); ps = pbf(bank)
                P.mm(ps, IDENTb, MBA4b, start=True, stop=False)
                for r in range(4):
                    sl = slice(512 * c + r, 512 * (c + 1), 4)
                    P.mm(ps[:, 128 * r:128 * (r + 1)], KT[fp, sl], QT[fp, sl], start=False, stop=(r == 3))
                P.act(pt[:, 0:512], ps, AF.Exp, scale=0.125)
                if c > 0:
                    bank = ringS.next(); ps = pbf(bank)
                    P.mm(ps, IDENTb, MBB4b, start=True, stop=False)
                    for r in range(4):
                        sl = slice(512 * c + r, 512 * (c + 1), 4)
                        slk = slice(512 * (c - 1) + r, 512 * c, 4)
                        P.mm(ps[:, 128 * r:128 * (r + 1)], KT[fp, slk], QT[fp, sl], start=False, stop=(r == 3))
                    P.act(pt[:, 512:1024], ps, AF.Exp, scale=0.125)
                for i in range(4):
                    n = 4 * c + i
                    pv(acc[:, 128 * i:128 * (i + 1)], VA[:, n, e, :], pt1[:, 128 * i:128 * (i + 1)])
                    if n > 0:
                        pv(acc[:, 128 * i:128 * (i + 1)], VA[:, n - 1, e, :], pt1[:, 512 + 128 * i:512 + 128 * (i + 1)])
                for r in range(4):
                    pv(acc[:, r:512:4], VA[:, 16 + 4 * r + c, e, :], pt[:, 128 * r:128 * (r + 1)])
                    if c > 0:
                        pv(acc[:, r:512:4], VA[:, 16 + 4 * r + c - 1, e, :], pt[:, 512 + 128 * r:512 + 128 * (r + 1)])
                for r in range(16):
                    pv(acc[:, r:512:16], VA[:, 32 + r, e, :], PT3[:, r, 32 * c:32 * (c + 1)], last=(r == 15))
                num = slice(64 * e, 64 * e + 64)
                den = slice(64 * (1 - e), 64 * (1 - e) + 64)
                rd = rdenb[c % 2]
                P.recip(rd[den, :], acc[den, :])
                P.tt("dve", CT[num, 4 + j, 512 * c:512 * (c + 1)], acc[num, :], rd[den, :], ALU.mult)

    PROJ_V()
    for j in range(4):
        PROJ_B(j)
        ATTN(j)
    dump("CTb", CT[:, 4:8, :].rearrange("p k t -> p (k t)"), [128, 4 * S])
    if stage == "B":
        return finish(nc, P, out_d, dump_tags)

    bF = Bump(A, SCR_LO, SCR_HI)
    h2T = A.view(32768, [8, 1024], BF16)
    woutb = A.view(32768 + 16384, [2, 8, 512], BF16)
    X1 = bF.alloc([8, D])
    wu = [bF.alloc([2, 8, 512], BF16) for _ in range(2)]
    wd = [bF.alloc([4, D], BF16) for _ in range(2)]
    aTb = [bF.alloc([4, 1024], BF16) for _ in range(2)]
    rawF = [[bF.alloc([520]) for _ in range(2)] for _ in range(2)]
    accF = [[bF.alloc([512]) for _ in range(2)] for _ in range(2)]
    sgF = [bF.alloc([512]) for _ in range(2)]
    hbF = [bF.alloc([D], BF16), accF[1][1].bitcast(BF16)]
    junkF = sgF[0].bitcast(BF16)
    ssF = bF.alloc([8]); rsF = bF.alloc([8]); ssO = bF.alloc([8]); rsO = bF.alloc([8])
    for c2 in range(2):
        P.dma("pool", woutb[:, c2, :, :], wout_v[:, :, 512 * c2:512 * (c2 + 1)], "wout%d" % c2)
    P.dma("sp", NW1, n2_d, "c_nw1")
    P.dma("sp", NW2, nf_d, "c_nw2")
    groups = [list(range(g0, min(g0 + 4, 22))) for g0 in range(0, 22, 4)]
    out_tags = ["out%d" % i for i in range(8)]
    items = [(H, gi) for H in range(2) for gi in range(len(groups))]

    def load_wu(k):
        H, gi = items[k]
        grp = groups[gi]; g0 = grp[0]; npair = len(grp); slot = k % 2
        P.dma("pool", wu[slot][:, 0, :, 0:128 * npair], wup_v[:, :, 128 * g0:128 * (g0 + npair)], "wug%d" % slot)
        P.dma("pool", wu[slot][:, 1, :, 0:128 * npair],
              wup_v[:, :, 2816 + 128 * g0:2816 + 128 * (g0 + npair)], "wuu%d" % slot)

    def load_wd(k):
        H, gi = items[k]
        grp = groups[gi]; g0 = grp[0]; npair = len(grp); slot = k % 2
        P.dma("pool", wd[slot][:, 0:npair, :], wdn_v[:, g0:g0 + npair, :], "wd%d" % slot)

    def PRO(H):
        scr = [(junkF, ssF[:, i8:i8 + 1], rsF[:, i8:i8 + 1], hbF[i8 % 2]) for i8 in range(8)]

        def st_a(i8):
            i = 8 * H + i8
            P.dma("sp", X1[:, i8, :], x_d[128 * i:128 * (i + 1), :], "xf%d" % i8)
            b0 = 2 * (i8 % 2)
            for h2 in range(2):
                for kc in range(8):
                    P.mm(pbf(b0 + h2), CT[:, kc, 128 * i:128 * (i + 1)], woutb[:, h2, kc, :],
                         start=(kc == 0), stop=(kc == 7))
            for h2 in range(2):
                P.tt("dve", X1[:, i8, 512 * h2:512 * (h2 + 1)], X1[:, i8, 512 * h2:512 * (h2 + 1)], pbf(b0 + h2), ALU.add)
            if i == 0:
                dump("X1", X1[:, 0, :], [128, D])
            rmsnorm_stage1a(X1[:, i8, :], scr[i8])

        for t in range(10):
            if t < 8:
                st_a(t)
            if 1 <= t <= 8:
                rmsnorm_stage1b(X1[:, t - 1, :], NW1, scr[t - 1])
            if t >= 2:
                rmsnorm_stage2(h2T, 128 * (t - 2), scr[t - 2], 4 + (t % 2))

    upar = [0]

    def UGEN(k):
        H, gi = items[k]
        grp = groups[gi]; slot = k % 2; aT = aTb[k % 2]
        upend = [None]
        for p, g in enumerate(grp):
            for tb in range(2):
                par = upar[0]
                upar[0] ^= 1
                cols = slice(512 * tb, 512 * (tb + 1))
                banks = (4, 5) if par == 0 else (6, 7)
                accs = []
                for gu in range(2):
                    ps = pbf(banks[gu])
                    for kc in range(8):
                        P.mm(ps, wu[slot][:, gu, kc, 128 * p:128 * (p + 1)], h2T[:, kc, cols],
                             start=(kc == 0), stop=(kc == 7))
                    cc = g + 22 * gu
                    raw = rawF[par][gu]; acc = accF[par][gu]
                    P.copy("pool", raw[:, 0:2], HALOF[:, cc, :])
                    P.copy("act", raw[:, 2:514], ps)
                    P.copy("pool", HALOF[:, cc, :], raw[:, 512:514])
                    P.act(acc, ps, AF.Identity, scale=cwF[:, cc, 2:3])
                    P.stt(acc, raw[:, 1:513], cwF[:, cc, 1:2], acc, ALU.mult, ALU.add)
                    P.stt(acc, raw[:, 0:512], cwF[:, cc, 0:1], acc, ALU.mult, ALU.add)
                    accs.append(acc)
                if upend[0] is not None:
                    upend[0]()

                def _gate(par=par, accs=accs, p=p, cols=cols):
                    sg = sgF[par]
                    P.act(sg, accs[0], AF.Silu)
                    P.tt("pool", aT[:, p, cols], sg, accs[1], ALU.mult)
                upend[0] = _gate
                yield
        if upend[0] is not None:
            upend[0]()
            upend[0] = None
            yield

    def DGEN(k):
        H, gi = items[k]
        grp = groups[gi]; slot = k % 2; aT = aTb[k % 2]; npair = len(grp)
        last = (gi == len(groups) - 1)
        dpend = [None]
        for i8 in range(8):
            i = 8 * H + i8
            b0 = 2 * (i8 % 2)
            for h2 in range(2):
                for p in range(npair):
                    P.mm(pbf(b0 + h2), aT[:, p, 128 * i8:128 * (i8 + 1)], wd[slot][:, p, 512 * h2:512 * (h2 + 1)],
                         start=(p == 0), stop=(p == npair - 1))
            for h2 in range(2):
                P.tt("dve", X1[:, i8, 512 * h2:512 * (h2 + 1)], X1[:, i8, 512 * h2:512 * (h2 + 1)],
                     pbf(b0 + h2), ALU.add)
            if last:
                P.act(junkF, X1[:, i8, :], AF.Square, accum_out=ssO[:, i8:i8 + 1])
                P.act(rsO[:, i8:i8 + 1], ssO[:, i8:i8 + 1], AF.Ln, bias=EPS, scale=1.0 / D)
                P.act(rsO[:, i8:i8 + 1], rsO[:, i8:i8 + 1], AF.Exp, scale=-0.5)
                if dpend[0] is not None:
                    dpend[0]()

                def _fin(i8=i8, i=i):
                    P.stt(X1[:, i8, :], X1[:, i8, :], rsO[:, i8:i8 + 1], NW2, ALU.mult, ALU.mult)
                    P.dma("sp", out_d[128 * i:128 * (i + 1), :], X1[:, i8, :], "out%d" % i8)
                dpend[0] = _fin
            yield
        if last and dpend[0] is not None:
            dpend[0]()
            dpend[0] = None
            yield

    def rr(gens):
        gens = list(gens)
        live = [True] * len(gens)
        while any(live):
            for gi_, g_ in enumerate(gens):
                if live[gi_]:
                    try:
                        next(g_)
                    except StopIteration:
                        live[gi_] = False

    load_wu(0)
    prevD = None
    for k in range(len(items)):
        H, gi = items[k]
        load_wd(k)
        if k + 1 < len(items):
            load_wu(k + 1)
        if gi == 0:
            if prevD is not None:
                rr([prevD])
                prevD = None
            PRO(H)
        u = UGEN(k)
        rr([u] if prevD is None else [u, prevD])
        prevD = DGEN(k)
    rr([prevD])
    return finish(nc, P, out_d, dump_tags + out_tags)


def finish(nc, P, out_d, dump_tags):
    tags = list(dump_tags)
    P.emit(final_dma_tags=tags)
    return nc


def prep_inputs(inp):
    f = lambda a: np.ascontiguousarray(np.asarray(a, dtype=np.float32))
    x = f(inp["x"])
    rep = lambda v: np.ascontiguousarray(np.broadcast_to(f(v).reshape(1, -1), (128, f(v).size)))
    cwa = f(inp["conv_qkv_w"])[0]
    cwA = np.ascontiguousarray(cwa.T.reshape(12, 128, 4).transpose(1, 0, 2).reshape(128, 48))
    cwf = f(inp["ffn_conv_w"])[0]
    cwF = np.ascontiguousarray(cwf.T.reshape(44, 128, 3).transpose(1, 0, 2).reshape(128, 132))
    shared = {
        "w_in": f(inp["w_in"])[0], "w_out": f(inp["w_out"])[0], "w_up": f(inp["w_up"])[0],
        "w_down": f(inp["w_down"])[0],
        "n1rep": rep(inp["norm1_w"]), "n2rep": rep(inp["norm2_w"]), "nfrep": rep(inp["final_norm_w"]),
        "cwA": cwA, "cwF": cwF, "gnwrep": rep(inp["gdn_norm_w"]),
        "dtbrep": np.ascontiguousarray(np.tile(rep(inp["dt_bias"]), (1, 16))),
        "alogrep": np.ascontiguousarray(np.tile(rep(inp["a_log"]), (1, 16))),
        "consts": make_consts(),
    }
    maps = []
    for b in range(x.shape[0]):
        m = dict(shared)
        m["x"] = np.ascontiguousarray(x[b])
        maps.append(m)
    return maps


def kernel(**inputs):
    maps = prep_inputs(inputs)
    nc = build("full")
    res = run_bass_kernel_spmd(nc, maps, core_ids=list(range(8)))
    out = np.stack([np.asarray(r["out"], dtype=np.float32) for r in res.results], 0)
    return out
```

```python
import contextlib
import numpy as np
import concourse.bass as bass
import concourse.mybir as mybir
from concourse.bass_utils import run_bass_kernel_spmd

F32 = mybir.dt.float32
BF16 = mybir.dt.bfloat16
AF = mybir.ActivationFunctionType
ALU = mybir.AluOpType
AX = mybir.AxisListType

S = 2048
D = 1024
NT = 16
EPS = 1e-6
NEG = -30000.0
ENGS = ("pe", "act", "dve", "pool", "sp")


def _rect(ap):
    t = ap.tensor
    if str(ap.space) == "PSUM":
        return (t.name, 0, 128, 0, 2048)
    esz = mybir.dt.size(ap.dtype)
    pstride = 1
    for s in tuple(t.shape)[1:]:
        pstride *= s
    p0 = ap.start_partition()
    p1 = p0 + ap.partition_size()
    f0 = ap.offset - p0 * pstride
    ext = 0
    for (st, cnt) in tuple(ap.ap)[1:]:
        ext += abs(st) * (cnt - 1)
    return (t.name, p0, p1, f0 * esz, (f0 + ext + 1) * esz)


class _Op:
    __slots__ = ("eng", "fn", "idx", "is_dma", "tag", "waits", "signal", "sigval")


class Prog:
    def __init__(self, nc):
        self.nc = nc
        self.ops = {e: [] for e in ENGS}
        self.track = {}
        self.waited = {e: {} for e in ENGS}
        self.tagcount = {}

    def _add(self, eng, fn, reads, writes, is_dma=False, tag=None):
        op = _Op()
        op.eng = eng; op.fn = fn; op.is_dma = is_dma; op.tag = tag
        op.signal = False; op.sigval = None
        op.idx = len(self.ops[eng])
        op.waits = []
        deps = {}
        rrects = [_rect(a) for a in reads if a is not None and str(a.space) != "DRAM"]
        wrects = [_rect(a) for a in writes if a is not None and str(a.space) != "DRAM"]
        prects = [r for r in rrects if r[4] == 2048 and r[0].startswith("pb") and r not in wrects]
        for (nm, p0, p1, f0, f1) in rrects:
            for rec in self.track.get(nm, ()):
                if rec[5] == 1 and rec[0] < p1 and p0 < rec[1] and rec[2] < f1 and f0 < rec[3]:
                    deps[id(rec[4])] = (rec[4], True)
        for (nm, p0, p1, f0, f1) in wrects + prects:
            for rec in self.track.get(nm, ()):
                if rec[0] < p1 and p0 < rec[1] and rec[2] < f1 and f0 < rec[3]:
                    k = id(rec[4])
                    if k not in deps:
                        deps[k] = (rec[4], False)
        need = {}
        for (d, raw) in deps.values():
            if d.is_dma:
                key = ("dma", d.tag)
                val = self.tagcount[d.tag]
                if need.get(key, 0) < val:
                    need[key] = val
            else:
                if d.eng == eng and eng == "pe":
                    continue
                key = ("eng", d.eng)
                cur = need.get(key)
                if cur is None or cur.idx < d.idx:
                    need[key] = d
        w = self.waited[eng]
        for key, v in need.items():
            if key[0] == "dma":
                if w.get(key, 0) >= v:
                    continue
                w[key] = v
                op.waits.append((key, v))
            else:
                if w.get(key, -1) >= v.idx:
                    continue
                w[key] = v.idx
                v.signal = True
                op.waits.append((key, v))
        if is_dma:
            self.tagcount[tag] = self.tagcount.get(tag, 0) + 16
        for (nm, p0, p1, f0, f1) in wrects:
            lst = self.track.setdefault(nm, [])
            lst[:] = [r for r in lst if not (p0 <= r[0] and r[1] <= p1 and f0 <= r[2] and r[3] <= f1)]
            lst.append([p0, p1, f0, f1, op, 1])
        for (nm, p0, p1, f0, f1) in prects:
            self.track[nm] = [[p0, p1, f0, f1, op, 2]]
        for (nm, p0, p1, f0, f1) in rrects:
            if nm.startswith("pb"):
                continue
            lst = self.track.setdefault(nm, [])
            done = False
            for r in lst:
                if r[5] == 0 and r[4].eng == eng and (not r[4].is_dma) and (not is_dma) \
                        and r[0] == p0 and r[1] == p1 and r[2] == f0 and r[3] == f1:
                    r[4] = op
                    done = True
                    break
            if not done:
                lst.append([p0, p1, f0, f1, op, 0])
        self.ops[eng].append(op)
        return op

    def dma(self, q, out, in_, tag):
        return self._add(q, lambda e: e.dma_start(out=out, in_=in_), [in_], [out], is_dma=True, tag=tag)

    def mm(self, out, lhsT, rhs, start=True, stop=True, **kw):
        rd = [lhsT, rhs] + ([] if start else [out])
        return self._add("pe", lambda e: e.matmul(out, lhsT, rhs, start=start, stop=stop, **kw), rd, [out])

    def tr(self, out, in_, ident):
        return self._add("pe", lambda e: e.transpose(out, in_, ident), [in_, ident], [out])

    def act(self, out, in_, func, bias=None, scale=None, accum_out=None):
        kw = {}
        rd = [in_]
        if bias is not None:
            kw["bias"] = bias
            if not isinstance(bias, (int, float)):
                rd.append(bias)
        if scale is not None:
            kw["scale"] = scale
            if not isinstance(scale, (int, float)):
                rd.append(scale)
        wr = [out]
        if accum_out is not None:
            kw["accum_out"] = accum_out
            wr.append(accum_out)
        return self._add("act", lambda e: e.activation(out, in_, func, **kw), rd, wr)

    def tt(self, eng, out, in0, in1, op):
        return self._add(eng, lambda e: e.tensor_tensor(out, in0, in1, op), [in0, in1], [out])

    def ts(self, eng, out, in0, s1, s2, op0, op1=None):
        rd = [in0] + [s for s in (s1, s2) if s is not None and not isinstance(s, (int, float))]
        kw = {}
        if op1 is not None:
            kw["op1"] = op1
        return self._add(eng, lambda e: e.tensor_scalar(out, in0, s1, s2, op0, **kw), rd, [out])

    def stt(self, out, in0, scalar, in1, op0, op1):
        rd = [in0, in1] + ([] if isinstance(scalar, (int, float)) else [scalar])
        return self._add("dve", lambda e: e.scalar_tensor_tensor(out, in0, scalar, in1, op0, op1), rd, [out])

    def copy(self, eng, out, in_):
        if eng == "act":
            return self._add("act", lambda e: e.copy(out, in_), [in_], [out])
        return self._add(eng, lambda e: e.tensor_copy(out, in_), [in_], [out])

    def memset(self, eng, ap, val):
        return self._add(eng, lambda e: e.memset(ap, val), [], [ap])

    def recip(self, out, in_):
        return self._add("dve", lambda e: e.reciprocal(out, in_), [in_], [out])

    def rsum(self, out, in_):
        return self._add("dve", lambda e: e.tensor_reduce(out, in_, AX.X, ALU.add), [in_], [out])

    def emit(self, final_dma_tags=()):
        nc = self.nc
        for e in ENGS:
            c = 0
            for op in self.ops[e]:
                if op.signal and not op.is_dma:
                    c += 1
                    op.sigval = c
        with contextlib.ExitStack() as st:
            esem = {e: st.enter_context(nc.semaphore("s_" + e)) for e in ENGS}
            dsem = {t: st.enter_context(nc.semaphore("d_%d" % i)) for i, t in enumerate(self.tagcount)}
            block = st.enter_context(nc.Block())
            engobj = {"pe": block.tensor, "act": block.scalar, "dve": block.vector,
                      "pool": block.gpsimd, "sp": block.sync}

            def make(ename):
                def body(eng):
                    for op in self.ops[ename]:
                        for (key, v) in op.waits:
                            if key[0] == "dma":
                                eng.wait_ge(dsem[key[1]], v)
                            else:
                                eng.wait_ge(esem[key[1]], v.sigval)
                        ins = op.fn(eng)
                        if op.is_dma:
                            ins.then_inc(dsem[op.tag], 16)
                        elif op.signal:
                            ins.then_inc(esem[ename], 1)
                    if ename == "sp":
                        for t in final_dma_tags:
                            eng.wait_ge(dsem[t], self.tagcount[t])
                return body

            for e in ENGS:
                engobj[e](make(e))
        return nc


def _prod(shape):
    n = 1
    for s in shape:
        n *= s
    return n


class Arena:
    def __init__(self, nc, nbytes):
        self.t = nc.alloc_sbuf_tensor("arena", [128, nbytes // 4], F32)
        self.nbytes = nbytes

    def view(self, off, shape, dt):
        n = _prod(shape)
        esz = 4 if dt == F32 else 2
        assert off % 4 == 0 and (n * esz) % 4 == 0 and off + n * esz <= self.nbytes, (off, shape)
        ap = self.t[:, off // 4:(off + n * esz) // 4]
        if dt != F32:
            ap = ap.bitcast(dt)
        if len(shape) > 1:
            names = " ".join("d%d" % i for i in range(len(shape)))
            kw = {"d%d" % i: shape[i] for i in range(1, len(shape))}
            ap = ap.rearrange("p (%s) -> p %s" % (names, names), **kw)
        return ap


class Bump:
    def __init__(self, arena, lo, hi):
        self.a = arena; self.lo = lo; self.hi = hi; self.cur = lo

    def alloc(self, shape, dt=F32):
        esz = 4 if dt == F32 else 2
        n = (_prod(shape) * esz + 31) // 32 * 32
        off = self.cur
        self.cur += n
        assert self.cur <= self.hi, ("arena overflow", self.cur, self.hi)
        return self.a.view(off, shape, dt)


class Ring:
    def __init__(self, items):
        self.items = list(items); self.i = 0

    def next(self):
        x = self.items[self.i % len(self.items)]
        self.i += 1
        return x


_CNAMES = ["IDENT", "M1", "M2", "TRI", "SUF", "IND0", "IND1", "S32", "OFF", "ONES"]
_CB = {"MB64": 512, "MBA4": 512, "MBB4": 512}
NCF = 128 * len(_CNAMES)
NCONST = NCF + 512 * 3


def make_consts():
    i = np.arange(128)[:, None]
    j = np.arange(128)[None, :]
    same = (i // 64) == (j // 64)
    c = {}
    c["IDENT"] = (i == j)
    c["M1"] = (i > j)
    c["M2"] = (i <= j)
    c["TRI"] = (i <= j) & same
    c["SUF"] = (i > j) & same
    c["IND0"] = (i < 64) & (j >= 0)
    c["IND1"] = (i >= 64) & (j >= 0)
    c["S32"] = (i < j) & ((i // 32) == (j // 32))
    c["OFF"] = same & ((i % 64) < 32) & ((j % 64) >= 32)
    c["ONES"] = np.ones((128, 128), bool)
    cols = [c[n].astype(np.float32) for n in _CNAMES]
    mb64 = np.where((i <= j) & same, 0.0, NEG).astype(np.float32)
    mba = np.where(i <= j, 0.0, NEG).astype(np.float32)
    mbb = np.where(i >= j, 0.0, NEG).astype(np.float32)
    cols += [np.tile(mb64, (1, 4)), np.tile(mba, (1, 4)), np.tile(mbb, (1, 4))]
    return np.ascontiguousarray(np.concatenate(cols, 1))


def build(stage="full", dumps=()):
    nc = bass.Bass("TRN2", target_bir_lowering=False)
    P = Prog(nc)
    dumps = set(dumps)
    dump_tags = []

    def din(name, shape):
        return nc.dram_tensor(name, list(shape), F32, kind="ExternalInput").ap()

    x_d = din("x", [S, D])
    win_d = din("w_in", [D, 3592])
    wout_d = din("w_out", [D, D])
    wup_d = din("w_up", [D, 5632])
    wdn_d = din("w_down", [2816, D])
    n1_d = din("n1rep", [128, D])
    n2_d = din("n2rep", [128, D])
    nf_d = din("nfrep", [128, D])
    cwa_d = din("cwA", [128, 48])
    cwf_d = din("cwF", [128, 132])
    gnw_d = din("gnwrep", [128, 128])
    dtb_d = din("dtbrep", [128, 64])
    alog_d = din("alogrep", [128, 64])
    cst_d = din("consts", [128, NCONST])
    out_d = nc.dram_tensor("out", [S, D], F32, kind="ExternalOutput").ap()

    def dump(name, sb_ap, shape):
        if name not in dumps:
            return
        d = nc.dram_tensor("dbg_" + name, list(shape), sb_ap.dtype, kind="ExternalOutput").ap()
        tg = "dbg_" + name
        P.dma("sp", d, sb_ap, tg)
        dump_tags.append(tg)

    win_v = win_d.rearrange("(k p) c -> p k c", p=128)
    wout_v = wout_d.rearrange("(k p) c -> p k c", p=128)
    wup_v = wup_d.rearrange("(k p) c -> p k c", p=128)
    wdn_v = wdn_d.rearrange("(k p) c -> p k c", p=128)

    ARENA_BYTES = 207872
    A = Arena(nc, ARENA_BYTES)
    pb = [nc.alloc_psum_tensor("pb%d" % i, [128, 512], F32) for i in range(8)]

    def pbf(i):
        return pb[i][:]

    def pbb(i):
        return pb[i][:].bitcast(BF16)

    CT = A.view(0, [8, S], BF16)
    hT = A.view(32768, [8, S], BF16)
    pers = Bump(A, 65536, 86016)
    CF = pers.alloc([NCF])
    CBm = pers.alloc([1536 + 256], BF16)
    cwA = pers.alloc([12, 4])
    cwF = pers.alloc([44, 3])
    HALOA = pers.alloc([12, 3])
    HALOF = pers.alloc([44, 2])
    GNW = pers.alloc([128])
    NW1 = pers.alloc([D])
    NW2 = pers.alloc([D])
    SCR_LO = 86016
    SCR_HI = ARENA_BYTES

    def cf(name):
        k = _CNAMES.index(name)
        return CF[:, 128 * k:128 * (k + 1)]

    IDENT = cf("IDENT"); M1 = cf("M1"); M2 = cf("M2"); TRI = cf("TRI"); SUF = cf("SUF")
    IND0 = cf("IND0"); IND1 = cf("IND1"); S32 = cf("S32"); OFFM = cf("OFF"); ONES = cf("ONES")
    MB64b = CBm[:, 0:512]; MBA4b = CBm[:, 512:1024]; MBB4b = CBm[:, 1024:1536]
    IDENTb = CBm[:, 1536:1664]; ONESb = CBm[:, 1664:1792]

    P.dma("sp", CF, cst_d[:, 0:NCF], "c_cf")
    ctmp = A.view(16384 + 8192, [1536], F32)
    P.dma("sp", ctmp, cst_d[:, NCF:NCONST], "c_tmp")
    P.copy("dve", CBm[:, 0:1536], ctmp)
    P.copy("dve", IDENTb, IDENT)
    P.copy("dve", ONESb, ONES)
    P.dma("sp", cwA.rearrange("p a b -> p (a b)"), cwa_d, "c_cwa")
    P.dma("sp", cwF.rearrange("p a b -> p (a b)"), cwf_d, "c_cwf")
    P.dma("sp", GNW, gnw_d, "c_gnw")
    P.dma("sp", NW1, n1_d, "c_nw1")
    P.memset("pool", HALOA.rearrange("p a b -> p (a b)"), 0.0)
    P.memset("pool", HALOF.rearrange("p a b -> p (a b)"), 0.0)

    def bc_last(ap, n):
        return ap.unsqueeze(2).broadcast_to([128, ap.shape[1], n])

    def bc_mid(ap, n):
        return ap.unsqueeze(1).broadcast_to([128, n, ap.shape[1]])

    def rmsnorm_stage1(src_tile, wtile, scr):
        junk, ssv, rst, hb = scr
        P.act(junk, src_tile, AF.Square, accum_out=ssv)
        P.act(rst, ssv, AF.Ln, bias=EPS, scale=1.0 / D)
        P.act(rst, rst, AF.Exp, scale=-0.5)
        P.stt(hb, src_tile, rst, wtile, ALU.mult, ALU.mult)

    def rmsnorm_stage1a(src_tile, scr):
        junk, ssv, rst, hb = scr
        P.act(junk, src_tile, AF.Square, accum_out=ssv)
        P.act(rst, ssv, AF.Ln, bias=EPS, scale=1.0 / D)
        P.act(rst, rst, AF.Exp, scale=-0.5)

    def rmsnorm_stage1b(src_tile, wtile, scr):
        junk, ssv, rst, hb = scr
        P.stt(hb, src_tile, rst, wtile, ALU.mult, ALU.mult)

    def rmsnorm_stage2(dstT, col0, scr, bank):
        hb = scr[3]
        psT = pbb(bank)
        for kc in range(8):
            P.tr(psT[:, 128 * kc:128 * (kc + 1)], hb[:, 128 * kc:128 * (kc + 1)], IDENTb)
        P.copy("act", dstT[:, :, col0:col0 + 128], psT.rearrange("p (k t) -> p k t", k=8))

    bA = Bump(A, 0, 16384)
    xbuf = [bA.alloc([D]) for _ in range(3)]
    junkA = bA.alloc([D], BF16)
    hbuf = [NW2.bitcast(BF16)[:, 0:D], NW2.bitcast(BF16)[:, D:2 * D]]
    ssA = bA.alloc([NT])
    rsA = bA.alloc([NT])
    scrA = [(junkA, ssA[:, i:i + 1], rsA[:, i:i + 1], hbuf[i % 2]) for i in range(NT)]
    for i in range(NT + 2):
        if i < NT:
            xt = xbuf[i % 3]
            P.dma("sp", xt, x_d[128 * i:128 * (i + 1), :], "xa%d" % (i % 3))
            rmsnorm_stage1a(xt, scrA[i])
        if 1 <= i <= NT:
            rmsnorm_stage1b(xbuf[(i - 1) % 3], NW1, scrA[i - 1])
        if i >= 2:
            rmsnorm_stage2(hT, 128 * (i - 2), scrA[i - 2], 6 + (i % 2))
    dump("hT", hT.rearrange("p k t -> p (k t)"), [128, 8 * S])
    if stage == "A":
        return finish(nc, P, out_d, dump_tags)

    bG = Bump(A, SCR_LO, SCR_HI)
    bC = Bump(A, 16384, 32768)
    wz = bC.alloc([8, 512], BF16)
    qkvT = [dict(q=bG.alloc([4, 512], BF16), k=bG.alloc([4, 512], BF16), v=bG.alloc([4, 512], BF16))
            for _ in range(4)]
    szgb = [bG.alloc([4, 512], BF16), bG.alloc([4, 512], BF16), bG.alloc([4, 512], BF16), bC.alloc([4, 512], BF16)]
    BA = bG.alloc([NT, 8])
    sc_x = bG.alloc([64]); sc_mx = bG.alloc([64]); sc_mn = bG.alloc([64])
    dtb = bG.alloc([64]); negA = bG.alloc([64])
    gS = bG.alloc([64]); betaS = bG.alloc([64]); gamS = bG.alloc([64]); kesS = bG.alloc([64])
    gendS = bG.alloc([2, 64])
    wba = bG.alloc([8, 8], BF16)
    Sst = bG.alloc([4, 128]); Sb = bG.alloc([4, 128], BF16); ub = bG.alloc([4, 128], BF16)
    Obuf = [bC.alloc([4, 128]) for _ in range(2)]
    sqO = bG.alloc([4, 128], BF16); oab = sqO
    kesLo = bG.alloc([64]); kesHi = bG.alloc([64])
    ssq = bG.alloc([4]); rsq = bG.alloc([4])
    ov0 = bG.cur
    gset = []
    for _ in range(2):
        gset.append(dict(
            G1m=bG.alloc([4, 128]),
            DECT=bG.alloc([4, 128], BF16), Uall=bG.alloc([4, 128], BF16), Eu=bG.alloc([4, 128], BF16),
            PTa=bG.alloc([4, 128], BF16), PTb=bG.alloc([4, 128], BF16),
            UL0=bG.alloc([2, 4, 128], BF16), PWA=bG.alloc([2, 4, 128], BF16), PWB=bG.alloc([2, 4, 128], BF16),
            TTb=bG.alloc([4, 128], BF16), vb=bG.alloc([4, 128], BF16), gk=bG.alloc([4, 128], BF16)))
    scanop = []
    for _ in range(3):
        scanop.append(dict(KLO=bG.alloc([4, 128], BF16), KHI=bG.alloc([4, 128], BF16), ATT=bG.alloc([4, 128], BF16),
                           QD=bG.alloc([4, 128], BF16), WKT=bG.alloc([4, 128], BF16),
                           UV=bG.alloc([4, 128], BF16)))
    bO = Bump(A, ov0, bG.cur)
    wqkva = bO.alloc([3, 8, 512], BF16)
    rawb = [bO.alloc([520], BF16) for _ in range(8)]
    dgcb = [bO.alloc([2, 128], BF16) for _ in range(8)]
    accb2 = [bO.alloc([512]) for _ in range(2)]
    acccnt = [0]
    sqbs = [bO.alloc([512], BF16) for _ in range(3)]
    rtbs = [bO.alloc([512]) for _ in range(3)]
    for c3 in range(3):
        P.dma("pool", wqkva[:, c3, :, :], win_v[:, :, 512 * c3:512 * (c3 + 1)], "wqkva%d" % c3)
    P.dma("pool", wba, win_v[:, :, 2048:2056], "wba")
    P.dma("pool", wz, win_v[:, :, 1536:2048], "wz")

    ringG = Ring([0, 1, 2, 3, 4, 5, 6, 7])

    g3 = gS.rearrange("p (n h) -> p n h", h=4)
    beta3 = betaS.rearrange("p (n h) -> p n h", h=4)
    gam3 = gamS.rearrange("p (n h) -> p n h", h=4)
    kesLo3 = kesLo.rearrange("p (n h) -> p n h", h=4)
    kesHi3 = kesHi.rearrange("p (n h) -> p n h", h=4)
    gend4 = gendS.rearrange("p a (n h) -> p a n h", h=4)

    def SCALARS():
        P.dma("sp", dtb, dtb_d, "c_dtb")
        P.dma("sp", negA, alog_d, "c_alog")
        P.act(negA, negA, AF.Exp)
        P.ts("dve", negA, negA, -1.0, None, ALU.mult)
        bk = ringG.next()
        psBA = pbf(bk)[:, 0:128].rearrange("p (n c) -> p n c", c=8)
        for i in range(NT):
            for kc in range(8):
                P.mm(psBA[:, i, :], hT[:, kc, 128 * i:128 * (i + 1)], wba[:, kc, :], start=(kc == 0), stop=(kc == 7))
        P.copy("act", BA, psBA)
        x3 = sc_x.rearrange("p (n h) -> p n h", h=4)
        P.tt("dve", x3, BA[:, :, 4:8], dtb.rearrange("p (n h) -> p n h", h=4), ALU.add)
        P.ts("dve", sc_mx, sc_x, 0.0, None, ALU.max)
        P.ts("dve", sc_mn, sc_x, 0.0, None, ALU.min)
        P.tt("dve", sc_mn, sc_mn, sc_mx, ALU.subtract)
        P.act(sc_mn, sc_mn, AF.Exp)
        P.act(sc_mn, sc_mn, AF.Ln, bias=1.0)
        P.tt("dve", sc_mx, sc_mx, sc_mn, ALU.add)
        P.tt("dve", gS, sc_mx, negA, ALU.mult)
        P.act(betaS.rearrange("p (n h) -> p n h", h=4), BA[:, :, 0:4], AF.Sigmoid)
        bk = ringG.next()
        psg = pbf(bk)
        P.mm(psg[:, 0:64], TRI, gS)
        P.mm(psg[:, 64:128], SUF, gS)
        P.mm(psg[:, 128:192], IND0, gS)
        P.mm(psg[:, 192:256], IND1, gS)
        P.act(gamS, psg[:, 0:64], AF.Exp)
        P.act(kesS, psg[:, 64:128], AF.Exp)
        P.tt("dve", kesS, kesS, betaS, ALU.mult)
        P.act(gendS.rearrange("p a b -> p (a b)"), psg[:, 128:256], AF.Exp)
        dump("gS", gS, [128, 64]); dump("betaS", betaS, [128, 64]); dump("gamS", gamS, [128, 64])
        dump("kesS", kesS, [128, 64]); dump("gendS", gendS.rearrange("p a b -> p (a b)"), [128, 128])
        P.ts("dve", kesLo, kesS, IND0[:, 0:1], None, ALU.mult)
        P.ts("dve", kesHi, kesS, IND1[:, 0:1], None, ALU.mult)


    def G1(m):
        t0 = 512 * m
        o = qkvT[m]
        pend = [None]
        for c in range(12):
            ps = pbf(ringG.next())
            for kc in range(8):
                P.mm(ps, wqkva[:, c // 4, kc, 128 * (c % 4):128 * (c % 4 + 1)], hT[:, kc, t0:t0 + 512], start=(kc == 0), stop=(kc == 7))
            raw = rawb[2 * m + c % 2]; dgc = dgcb[2 * m + c % 2]
            for i_ in (2, 3):
                P.ts("dve", dgc[:, i_ - 2, :], IDENTb, cwA[:, c, i_:i_ + 1], None, ALU.mult)
            P.copy("pool", raw[:, 0:3], HALOA[:, c, :])
            P.copy("act", raw[:, 3:515], ps)
            P.copy("pool", HALOA[:, c, :], raw[:, 512:515])
            if pend[0] is not None:
                pend[0]()

            def _conv(raw=raw, dgc=dgc, c=c):
                psC = pbf(ringG.next())
                P.mm(psC, dgc[:, 0, :], raw[:, 2:514], start=True, stop=False)
                P.mm(psC, dgc[:, 1, :], raw[:, 3:515], start=False, stop=True)
                acc = accb2[acccnt[0] % 2]
                acccnt[0] += 1
                P.stt(acc, raw[:, 1:513], cwA[:, c, 1:2], psC, ALU.mult, ALU.add)
                P.stt(acc, raw[:, 0:512], cwA[:, c, 0:1], acc, ALU.mult, ALU.add)
                dst = (o["q"], o["k"], o["v"])[c // 4][:, c % 4, :]
                P.act(dst, acc, AF.Silu)
            pend[0] = _conv
            if c % 3 == 2:
                nz = 4 * m + c // 3
                psZ = pbf(ringG.next())
                for kc in range(8):
                    P.mm(psZ, hT[:, kc, 128 * nz:128 * (nz + 1)], wz[:, kc, :], start=(kc == 0), stop=(kc == 7))
                szg = szgb[m][:, c // 3, :]
                P.act(szg, psZ, AF.Silu)
                P.tt("pool", szg.rearrange("p (h d) -> p h d", h=4), szg.rearrange("p (h d) -> p h d", h=4),
                     bc_mid(GNW, 4), ALU.mult)
            yield
        pend[0]()
        yield
        for c in range(8):
            dst = (o["q"], o["k"])[c // 4][:, c % 4, :]
            sc = 128.0 if c < 4 else 1.0
            sqb = sqbs[(c + m) % 3]; rtb = rtbs[(c + m) % 3]
            P.tt("pool", sqb, dst, dst, ALU.mult)
            psn = pbf(ringG.next())
            P.mm(psn, ONESb, sqb)
            P.act(rtb, psn, AF.Ln, bias=EPS * sc, scale=sc)
            P.act(rtb, rtb, AF.Exp, scale=-0.5)
            P.tt("dve", dst, dst, rtb, ALU.mult)
            yield
        if m == 0:
            dump("qnT0", o["q"].rearrange("p h t -> p (h t)"), [128, 2048])
            dump("knT0", o["k"].rearrange("p h t -> p (h t)"), [128, 2048])
            dump("vsT0", o["v"].rearrange("p h t -> p (h t)"), [128, 2048])

    def G2(n):
        tl = 128 * (n % 4)
        so = scanop[n % 3]
        st = gset[n % 2]
        qnT = qkvT[n // 4]["q"]; knT = qkvT[n // 4]["k"]; vsT = qkvT[n // 4]["v"]
        G1m = st["G1m"]; DECT = st["DECT"]; Uall = st["Uall"]; Eu = st["Eu"]
        UL0 = st["UL0"]; PWA = st["PWA"]; PWB = st["PWB"]
        TTb = st["TTb"]; vb = st["vb"]; gk = st["gk"]
        Du, Dl = UL0[:, 0], UL0[:, 1]
        gam_b = bc_last(gam3[:, n, :], 128)
        keslo_b = bc_last(kesLo3[:, n, :], 128)
        keshi_b = bc_last(kesHi3[:, n, :], 128)
        beta_b = bc_last(beta3[:, n, :], 128)
        g_b = bc_last(g3[:, n, :], 128)

        def bankf():
            return pbf(ringG.next()).rearrange("p (h d) -> p h d", h=4)

        def bankb():
            return pbb(ringG.next())[:, 0:512].rearrange("p (h d) -> p h d", h=4)

        psKV = pbb(ringG.next()).rearrange("p (a h d) -> p a h d", a=2, h=4)
        psK = psKV[:, 0]; psV = psKV[:, 1]
        for h in range(4):
            P.tr(psK[:, h, :], knT[:, h, tl:tl + 128], IDENTb)
            P.tr(psV[:, h, :], vsT[:, h, tl:tl + 128], IDENTb)
        P.tt("dve", gk, psK, gam_b, ALU.mult)
        P.tt("dve", so["KLO"], psK, keslo_b, ALU.mult)
        P.tt("dve", so["KHI"], psK, keshi_b, ALU.mult)
        P.copy("act", vb, psV)
        dg = Uall
        P.tt("pool", G1m, bc_mid(M1, 4), g_b, ALU.mult)
        P.tt("pool", dg, bc_mid(IDENT, 4), gam_b, ALU.mult)
        yield
        psD = bankf()
        P.mm(psD, IDENTb, MB64b, start=True, stop=False)
        for h in range(4):
            P.mm(psD[:, h, :], G1m[:, h, :], M2, start=False, stop=(h == 3))
        P.act(DECT, psD, AF.Exp)
        yield
        P.tt("pool", DECT, DECT, beta_b, ALU.mult)
        psG = bankf()
        for h in range(4):
            P.mm(psG[:, h, :], ONESb, dg[:, h, :])
        P.tt("dve", so["QD"], qnT[:, :, tl:tl + 128], psG, ALU.mult)
        yield
        psKK = bankf(); psQK = bankf()
        for h in range(4):
            P.mm(psKK[:, h, :], knT[:, h, tl:tl + 128], knT[:, h, tl:tl + 128])
        for h in range(4):
            P.mm(psQK[:, h, :], knT[:, h, tl:tl + 128], qnT[:, h, tl:tl + 128])
        P.tt("dve", Uall, psKK, DECT, ALU.mult)
        P.tt("dve", so["ATT"], psQK, DECT, ALU.mult)
        P.tt("pool", Du, Uall, bc_mid(S32, 4), ALU.mult)
        P.tt("pool", Eu, Uall, bc_mid(OFFM, 4), ALU.mult)
        yield
        psT = bankb()
        for h in range(4):
            P.tr(psT[:, h, :], Du[:, h, :], IDENTb)
        P.copy("act", Dl, psT)
        P.tt("pool", st["PTa"], bc_mid(IDENT, 4), Du, ALU.subtract)
        yield
        pw = [UL0, PWA, PWB, PWA, PWB]
        PT, PTn = st["PTa"], st["PTb"]
        for k in range(1, 5):
            cur = pw[k - 1]; nxt = pw[k]
            if k < 4:
                psU = bankf()
                for h in range(4):
                    P.mm(psU[:, h, :], cur[:, 1, h, :], cur[:, 0, h, :])
            psL = bankf()
            for h in range(4):
                P.mm(psL[:, h, :], cur[:, 0, h, :], cur[:, 1, h, :])
            if k > 1:
                ps3 = bankf()
                for h in range(4):
                    P.mm(ps3[:, h, :], cur[:, 1, h, :], PT[:, h, :])
            if k < 4:
                P.copy("act", nxt[:, 0], psU)
            P.copy("act", nxt[:, 1], psL)
            if k > 1:
                P.tt("dve", PTn, PT, ps3, ALU.add)
                PT, PTn = PTn, PT
            yield
        ps3 = bankf()
        for h in range(4):
            P.mm(ps3[:, h, :], PWB[:, 1, h, :], PT[:, h, :])
        P.tt("dve", PTn, PT, ps3, ALU.add)
        PT, PTn = PTn, PT
        yield
        Pm = PWA[:, 0]; XT = DECT
        p1 = bankb()
        for h in range(4):
            P.tr(p1[:, h, :], PT[:, h, :], IDENTb)
        P.copy("act", Pm, p1)
        yield
        p2 = bankf()
        for h in range(4):
            P.mm(p2[:, h, :], Eu[:, h, :], Pm[:, h, :])
        P.copy("act", XT, p2)
        yield
        p3 = bankf()
        for h in range(4):
            P.mm(p3[:, h, :], XT[:, h, :], PT[:, h, :])
        P.tt("dve", TTb, PT, p3, ALU.subtract)
        yield
        p1 = bankf(); p2 = bankf()
        for h in range(4):
            P.mm(p1[:, h, :], TTb[:, h, :], vb[:, h, :])
        for h in range(4):
            P.mm(p2[:, h, :], gk[:, h, :], TTb[:, h, :])
        P.copy("act", so["UV"], p1)
        P.copy("dve", so["WKT"], p2)
        yield

    def SCAN(n):
        so = scanop[n % 3]
        O = Obuf[n % 2]
        for half in range(2):
            r0 = 64 * half
            rs = slice(r0, r0 + 64)
            kend = so["KLO"] if half == 0 else so["KHI"]
            psA = pbf(ringG.next()).rearrange("p (h d) -> p h d", h=4)
            for h in range(4):
                P.mm(psA[:, h, :], so["WKT"][:, h, :], Sb[:, h, :])
            P.tt("dve", ub[rs], so["UV"][rs], psA[rs], ALU.subtract)
            yield
            psS = pbf(ringG.next()).rearrange("p (h d) -> p h d", h=4)
            psO = pbf(ringG.next()).rearrange("p (h d) -> p h d", h=4)
            for h in range(4):
                P.mm(psS[:, h, :], kend[:, h, :], ub[:, h, :])
            for h in range(4):
                P.mm(psO[:, h, :], so["QD"][:, h, :], Sb[:, h, :], start=True, stop=False)
                P.mm(psO[:, h, :], so["ATT"][:, h, :], ub[:, h, :], start=False, stop=True)
            for h in range(4):
                P.stt(Sst[:, h, :], Sst[:, h, :], gend4[:, half, n, h:h + 1], psS[:, h, :], ALU.mult, ALU.add)
            P.copy("act", O[rs], psO[rs])
            P.copy("act", Sb, Sst)
            yield
        if n == 0:
            dump("O0", O.rearrange("p h d -> p (h d)"), [128, 512])
        if n == 15:
            dump("O15", O.rearrange("p h d -> p (h d)"), [128, 512])
        P.act(sqO, O, AF.Square)
        P.rsum(ssq, sqO)
        yield
        P.act(rsq, ssq, AF.Ln, bias=EPS, scale=1.0 / 128)
        P.act(rsq, rsq, AF.Exp, scale=-0.5)
        szg = szgb[n // 4][:, n % 4, :].rearrange("p (h d) -> p h d", h=4)
        P.tt("dve", sqO, O, bc_last(rsq, 128), ALU.mult)
        P.tt("dve", oab, sqO, szg, ALU.mult)
        yield
        psT = pbb(ringG.next())[:, 0:512].rearrange("p (h d) -> p h d", h=4)
        for h in range(4):
            P.tr(psT[:, h, :], oab[:, h, :], IDENTb)
        P.copy("act", CT[:, 0:4, 128 * n:128 * (n + 1)], psT)
        yield

    def advance(must, opt):
        live = [True] * len(must)
        while any(live) or any(q[1] > 0 for q in opt):
            for q in opt:
                if q[1] > 0:
                    q[1] -= 1
                    try:
                        next(q[0])
                    except StopIteration:
                        q[1] = 0
                        q[2] = True
            for gi, g in enumerate(must):
                if live[gi]:
                    try:
                        next(g)
                    except StopIteration:
                        live[gi] = False

    P.memset("dve", Sst.rearrange("p h d -> p (h d)"), 0.0)
    P.memset("pool", Sb.rearrange("p h d -> p (h d)"), 0.0)
    P.memset("pool", ub.rearrange("p h d -> p (h d)"), 0.0)
    advance([G1(m_) for m_ in range(4)], [])
    SCALARS()
    g2 = {0: G2(0), 1: G2(1)}
    advance([g2[0]], [[g2[1], 7, False]])
    for n in range(NT):
        must = [SCAN(n)]
        if n + 1 < NT:
            must.insert(0, g2[n + 1])
        opt = []
        if n + 2 < NT:
            g2[n + 2] = G2(n + 2)
            opt.append([g2[n + 2], 7, False])
        advance(must, opt)
    dump("CTa", CT[:, 0:4, :].rearrange("p k t -> p (k t)"), [128, 4 * S])
    if stage == "G":
        return finish(nc, P, out_d, dump_tags)

    bB = Bump(A, SCR_LO, SCR_HI)
    wqkvb = bB.alloc([3, 8, 512], BF16)
    for c3 in range(3):
        P.dma("pool", wqkvb[:, c3, :, :],
              win_v[:, :, 2056 + 512 * c3:2056 + 512 * (c3 + 1)], "wqkvb%d" % c3)
    QT0 = bB.alloc([S], BF16)
    KTz = [bB.alloc([S], BF16) for _ in range(2)]
    QTb = [QT0, QT0]
    P.memset("pool", KTz[0][64:128, :], 0.0)
    P.memset("pool", KTz[1][0:64, :], 0.0)
    VAll = bB.alloc([48, 4, 192], BF16)
    PTbuf = Ring([bB.alloc([1024], BF16) for _ in range(2)])
    PT3 = bB.alloc([16, 128], BF16)
    rden0 = bB.alloc([512])
    rdenb = [rden0, rden0]
    P.memset("pool", VAll[:, :, :, 64:128], 1.0)
    ringS = Ring([2, 3, 4, 5])
    ringP = Ring([6, 7])
    accR = Ring([0, 1])

    def tok_slices():
        sl = []
        for n in range(16):
            sl.append(slice(128 * n, 128 * (n + 1), 1))
        for r in range(4):
            for c in range(4):
                sl.append(slice(512 * c + r, 512 * (c + 1), 4))
        for r in range(16):
            sl.append(slice(r, S, 16))
        return sl

    TOK = tok_slices()

    def PROJ_V():
        for t in range(48):
            bank = ringP.next()
            ps = pbf(bank)
            sl = TOK[t]
            for kc in range(8):
                P.mm(ps, hT[:, kc, sl], wqkvb[:, 2, kc, :], start=(kc == 0), stop=(kc == 7))
            ps4 = ps.rearrange("p (j e d) -> p j e d", j=4, e=2)
            P.copy("act", VAll[:, t, :, 0:64], ps4[:, :, 0, :])
            P.copy("dve", VAll[:, t, :, 128:192], ps4[:, :, 1, :])

    def PROJ_B(j):
        k = j % 2
        QT = QTb[k]
        for (dst, cbase) in ((QT, 128 * j), (None, 512 + 128 * j)):
            for tb in range(4):
                bank = ringP.next()
                ps = pbf(bank)
                for kc in range(8):
                    P.mm(ps, wqkvb[:, cbase // 512, kc, cbase % 512:cbase % 512 + 128], hT[:, kc, 512 * tb:512 * (tb + 1)],
                         start=(kc == 0), stop=(kc == 7))
                cs = slice(512 * tb, 512 * (tb + 1))
                if dst is not None:
                    P.copy("dve" if tb % 2 else "act", dst[:, cs], ps)
                else:
                    P.copy("act", KTz[0][0:64, cs], ps[0:64, :])
                    P.copy("dve", KTz[1][64:128, cs], ps[64:128, :])

    def ATTN(j):
        k = j % 2
        QT = QTb[k]

        class _VA:
            def __init__(self, h):
                self.h = h

            def __getitem__(self, idx):
                e_ = self.h % 2
                return VAll[:, idx[1], self.h // 2, 64 * e_:64 * e_ + 128]

        for e in range(2):
            hp = slice(64 * e, 64 * e + 64)
            KT = KTz[e]
            fp = slice(0, 128)
            VA = _VA(2 * j + e)
            for g in range(4):
                bank = ringS.next()
                ps = pbf(bank)
                P.mm(ps, IDENTb, MBA4b, start=True, stop=False)
                for r4 in range(4):
                    r = 4 * g + r4
                    P.mm(ps[:, 128 * r4:128 * (r4 + 1)], KT[fp, r:S:16], QT[fp, r:S:16], start=False, stop=(r4 == 3))
                P.act(PT3[:, 4 * g:4 * g + 4, :], ps.rearrange("p (a d) -> p a d", a=4), AF.Exp, scale=0.125)
            for c in range(4):
                acc = pbf(accR.next())
                first = [True]

                def pv(out, lhsT, rhs, last=False):
                    P.mm(out, lhsT, rhs, start=first[0], stop=last, skip_group_check=True)
                    first[0] = False
                pt = PTbuf.next()
                bank = ringS.next(); ps = pbf(bank)
                P.mm(ps, IDENTb, MBA4b, start=True, stop=False)
                for i in range(4):
                    n = 4 * c + i
                    P.mm(ps[:, 128 * i:128 * (i + 1)], KT[fp, 128 * n:128 * (n + 1)], QT[fp, 128 * n:128 * (n + 1)],
                         start=False, stop=(i == 3))
                P.act(pt[:, 0:512], ps, AF.Exp, scale=0.125)
                bank = ringS.next(); ps = pbf(bank)
                P.mm(ps, IDENTb, MBB4b, start=True, stop=False)
                for i in range(4):
                    n = 4 * c + i
                    if n == 0:
                        continue
                    P.mm(ps[:, 128 * i:128 * (i + 1)], KT[fp, 128 * (n - 1):128 * n], QT[fp, 128 * n:128 * (n + 1)],
                         start=False, stop=(i == 3))
                P.act(pt[:, 512:1024], ps, AF.Exp, scale=0.125)
                pt1 = pt
                pt = PTbuf.next()
                bank = ringS.next(); ps = pbf(bank)
                P.mm(ps, IDENTb, MBA4b, start=True, stop=False)
                for r in range(4):
                    sl = slice(512 * c + r, 512 * (c + 1), 4)
                    P.mm(ps[:, 128 * r:128 * (r + 1)], KT[fp, sl], QT[fp, sl], start=False, stop=(r == 3))
                P.act(pt[:, 0:512], ps, AF.Exp, scale=0.125)
                if c > 0:
                    bank = ringS.next(); ps = pbf(bank)
                    P.mm(ps, IDENTb, MBB4b, start=True, stop=False)
                    for r in range(4):
                        sl = slice(512 * c + r, 512 * (c + 1), 4)
                        slk = slice(512 * (c - 1) + r, 512 * c, 4)
                        P.mm(ps[:, 128 * r:128 * (r + 1)], KT[fp, slk], QT[fp, sl], start=False, stop=(r == 3))
                    P.act(pt[:, 512:1024], ps, AF.Exp, scale=0.125)
                for i in range(4):
                    n = 4 * c + i
                    pv(acc[:, 128 * i:128 * (i + 1)], VA[:, n, e, :], pt1[:, 128 * i:128 * (i + 1)])
                    if n > 0:
                        pv(acc[:, 128 * i:128 * (i + 1)], VA[:, n - 1, e, :], pt1[:, 512 + 128 * i:512 + 128 * (i + 1)])
                for r in range(4):
                    pv(acc[:, r:512:4], VA[:, 16 + 4 * r + c, e, :], pt[:, 128 * r:128 * (r + 1)])
                    if c > 0:
                        pv(acc[:, r:512:4], VA[:, 16 + 4 * r + c - 1, e, :], pt[:, 512 + 128 * r:512 + 128 * (r + 1)])
                for r in range(16):
                    pv(acc[:, r:512:16], VA[:, 32 + r, e, :], PT3[:, r, 32 * c:32 * (c + 1)], last=(r == 15))
                num = slice(64 * e, 64 * e + 64)
                den = slice(64 * (1 - e), 64 * (1 - e) + 64)
                rd = rdenb[c % 2]
                P.recip(rd[den, :], acc[den, :])
                P.tt("dve", CT[num, 4 + j, 512 * c:512 * (c + 1)], acc[num, :], rd[den, :], ALU.mult)

    PROJ_V()
    for j in range(4):
        PROJ_B(j)
        ATTN(j)
    dump("CTb", CT[:, 4:8, :].rearrange("p k t -> p (k t)"), [128, 4 * S])
    if stage == "B":
        return finish(nc, P, out_d, dump_tags)

    bF = Bump(A, SCR_LO, SCR_HI)
    h2T = A.view(32768, [8, 1024], BF16)
    woutb = A.view(32768 + 16384, [2, 8, 512], BF16)
    X1 = bF.alloc([8, D])
    wu = [bF.alloc([2, 8, 512], BF16) for _ in range(2)]
    wd = [bF.alloc([4, D], BF16) for _ in range(2)]
    aTb = [bF.alloc([4, 1024], BF16) for _ in range(2)]
    rawF = [[bF.alloc([520]) for _ in range(2)] for _ in range(2)]
    accF = [[bF.alloc([512]) for _ in range(2)] for _ in range(2)]
    sgF = [bF.alloc([512]) for _ in range(2)]
    hbF = [bF.alloc([D], BF16), accF[1][1].bitcast(BF16)]
    junkF = sgF[0].bitcast(BF16)
    ssF = bF.alloc([8]); rsF = bF.alloc([8]); ssO = bF.alloc([8]); rsO = bF.alloc([8])
    for c2 in range(2):
        P.dma("pool", woutb[:, c2, :, :], wout_v[:, :, 512 * c2:512 * (c2 + 1)], "wout%d" % c2)
    P.dma("sp", NW1, n2_d, "c_nw1")
    P.dma("sp", NW2, nf_d, "c_nw2")
    groups = [list(range(g0, min(g0 + 4, 22))) for g0 in range(0, 22, 4)]
    out_tags = ["out%d" % i for i in range(8)]
    items = [(H, gi) for H in range(2) for gi in range(len(groups))]

    def load_wu(k):
        H, gi = items[k]
        grp = groups[gi]; g0 = grp[0]; npair = len(grp); slot = k % 2
        P.dma("pool", wu[slot][:, 0, :, 0:128 * npair], wup_v[:, :, 128 * g0:128 * (g0 + npair)], "wug%d" % slot)
        P.dma("pool", wu[slot][:, 1, :, 0:128 * npair],
              wup_v[:, :, 2816 + 128 * g0:2816 + 128 * (g0 + npair)], "wuu%d" % slot)

    def load_wd(k):
        H, gi = items[k]
        grp = groups[gi]; g0 = grp[0]; npair = len(grp); slot = k % 2
        P.dma("pool", wd[slot][:, 0:npair, :], wdn_v[:, g0:g0 + npair, :], "wd%d" % slot)

    def PRO(H):
        scr = [(junkF, ssF[:, i8:i8 + 1], rsF[:, i8:i8 + 1], hbF[i8 % 2]) for i8 in range(8)]

        def st_a(i8):
            i = 8 * H + i8
            P.dma("sp", X1[:, i8, :], x_d[128 * i:128 * (i + 1), :], "xf%d" % i8)
            b0 = 2 * (i8 % 2)
            for h2 in range(2):
                for kc in range(8):
                    P.mm(pbf(b0 + h2), CT[:, kc, 128 * i:128 * (i + 1)], woutb[:, h2, kc, :],
                         start=(kc == 0), stop=(kc == 7))
            for h2 in range(2):
                P.tt("dve", X1[:, i8, 512 * h2:512 * (h2 + 1)], X1[:, i8, 512 * h2:512 * (h2 + 1)], pbf(b0 + h2), ALU.add)
            if i == 0:
                dump("X1", X1[:, 0, :], [128, D])
            rmsnorm_stage1a(X1[:, i8, :], scr[i8])

        for t in range(10):
            if t < 8:
                st_a(t)
            if 1 <= t <= 8:
                rmsnorm_stage1b(X1[:, t - 1, :], NW1, scr[t - 1])
            if t >= 2:
                rmsnorm_stage2(h2T, 128 * (t - 2), scr[t - 2], 4 + (t % 2))

    upar = [0]

    def UGEN(k):
        H, gi = items[k]
        grp = groups[gi]; slot = k % 2; aT = aTb[k % 2]
        upend = [None]
        for p, g in enumerate(grp):
            for tb in range(2):
                par = upar[0]
                upar[0] ^= 1
                cols = slice(512 * tb, 512 * (tb + 1))
                banks = (4, 5) if par == 0 else (6, 7)
                accs = []
                for gu in range(2):
                    ps = pbf(banks[gu])
                    for kc in range(8):
                        P.mm(ps, wu[slot][:, gu, kc, 128 * p:128 * (p + 1)], h2T[:, kc, cols],
                             start=(kc == 0), stop=(kc == 7))
                    cc = g + 22 * gu
                    raw = rawF[par][gu]; acc = accF[par][gu]
                    P.copy("pool", raw[:, 0:2], HALOF[:, cc, :])
                    P.copy("act", raw[:, 2:514], ps)
                    P.copy("pool", HALOF[:, cc, :], raw[:, 512:514])
                    P.act(acc, ps, AF.Identity, scale=cwF[:, cc, 2:3])
                    P.stt(acc, raw[:, 1:513], cwF[:, cc, 1:2], acc, ALU.mult, ALU.add)
                    P.stt(acc, raw[:, 0:512], cwF[:, cc, 0:1], acc, ALU.mult, ALU.add)
                    accs.append(acc)
                if upend[0] is not None:
                    upend[0]()

                def _gate(par=par, accs=accs, p=p, cols=cols):
                    sg = sgF[par]
                    P.act(sg, accs[0], AF.Silu)
                    P.tt("pool", aT[:, p, cols], sg, accs[1], ALU.mult)
                upend[0] = _gate
                yield
        if upend[0] is not None:
            upend[0]()
            upend[0] = None
            yield

    def DGEN(k):
        H, gi = items[k]
        grp = groups[gi]; slot = k % 2; aT = aTb[k % 2]; npair = len(grp)
        last = (gi == len(groups) - 1)
        dpend = [None]
        for i8 in range(8):
            i = 8 * H + i8
            b0 = 2 * (i8 % 2)
            for h2 in range(2):
                for p in range(npair):
                    P.mm(pbf(b0 + h2), aT[:, p, 128 * i8:128 * (i8 + 1)], wd[slot][:, p, 512 * h2:512 * (h2 + 1)],
                         start=(p == 0), stop=(p == npair - 1))
            for h2 in range(2):
                P.tt("dve", X1[:, i8, 512 * h2:512 * (h2 + 1)], X1[:, i8, 512 * h2:512 * (h2 + 1)],
                     pbf(b0 + h2), ALU.add)
            if last:
                P.act(junkF, X1[:, i8, :], AF.Square, accum_out=ssO[:, i8:i8 + 1])
                P.act(rsO[:, i8:i8 + 1], ssO[:, i8:i8 + 1], AF.Ln, bias=EPS, scale=1.0 / D)
                P.act(rsO[:, i8:i8 + 1], rsO[:, i8:i8 + 1], AF.Exp, scale=-0.5)
                if dpend[0] is not None:
                    dpend[0]()

                def _fin(i8=i8, i=i):
                    P.stt(X1[:, i8, :], X1[:, i8, :], rsO[:, i8:i8 + 1], NW2, ALU.mult, ALU.mult)
                    P.dma("sp", out_d[128 * i:128 * (i + 1), :], X1[:, i8, :], "out%d" % i8)
                dpend[0] = _fin
            yield
        if last and dpend[0] is not None:
            dpend[0]()
            dpend[0] = None
            yield

    def rr(gens):
        gens = list(gens)
        live = [True] * len(gens)
        while any(live):
            for gi_, g_ in enumerate(gens):
                if live[gi_]:
                    try:
                        next(g_)
                    except StopIteration:
                        live[gi_] = False

    load_wu(0)
    prevD = None
    for k in range(len(items)):
        H, gi = items[k]
        load_wd(k)
        if k + 1 < len(items):
            load_wu(k + 1)
        if gi == 0:
            if prevD is not None:
                rr([prevD])
                prevD = None
            PRO(H)
        u = UGEN(k)
        rr([u] if prevD is None else [u, prevD])
        prevD = DGEN(k)
    rr([prevD])
    return finish(nc, P, out_d, dump_tags + out_tags)


def finish(nc, P, out_d, dump_tags):
    tags = list(dump_tags)
    P.emit(final_dma_tags=tags)
    return nc


def prep_inputs(inp):
    f = lambda a: np.ascontiguousarray(np.asarray(a, dtype=np.float32))
    x = f(inp["x"])
    rep = lambda v: np.ascontiguousarray(np.broadcast_to(f(v).reshape(1, -1), (128, f(v).size)))
    cwa = f(inp["conv_qkv_w"])[0]
    cwA = np.ascontiguousarray(cwa.T.reshape(12, 128, 4).transpose(1, 0, 2).reshape(128, 48))
    cwf = f(inp["ffn_conv_w"])[0]
    cwF = np.ascontiguousarray(cwf.T.reshape(44, 128, 3).transpose(1, 0, 2).reshape(128, 132))
    shared = {
        "w_in": f(inp["w_in"])[0], "w_out": f(inp["w_out"])[0], "w_up": f(inp["w_up"])[0],
        "w_down": f(inp["w_down"])[0],
        "n1rep": rep(inp["norm1_w"]), "n2rep": rep(inp["norm2_w"]), "nfrep": rep(inp["final_norm_w"]),
        "cwA": cwA, "cwF": cwF, "gnwrep": rep(inp["gdn_norm_w"]),
        "dtbrep": np.ascontiguousarray(np.tile(rep(inp["dt_bias"]), (1, 16))),
        "alogrep": np.ascontiguousarray(np.tile(rep(inp["a_log"]), (1, 16))),
        "consts": make_consts(),
    }
    maps = []
    for b in range(x.shape[0]):
        m = dict(shared)
        m["x"] = np.ascontiguousarray(x[b])
        maps.append(m)
    return maps


def kernel(**inputs):
    maps = prep_inputs(inputs)
    nc = build("full")
    res = run_bass_kernel_spmd(nc, maps, core_ids=list(range(8)))
    out = np.stack([np.asarray(r["out"], dtype=np.float32) for r in res.results], 0)
    return out
```
